# Optimizing a Trainium2 kernel written in Bass

```python
import math
import jax
import jax.numpy as jnp
from jax import lax
import numpy as np

D_MODEL = 1024
BATCH = 8
SEQ = 2048
DEPTH = 2

EPS = 1e-6
GRID_W = 64

SSD_HEADS = 16
SSD_HEAD_DIM = 64
SSD_D_INNER = SSD_HEADS * SSD_HEAD_DIM
SSD_GROUPS = 2
SSD_STATE = 128
SSD_CONV = 5
SSD_CONV_DIM = SSD_D_INNER + 2 * SSD_GROUPS * SSD_STATE
SSD_CHUNK = 128

GLA_HEADS = 4
GLA_DK = 128
GLA_DV = 256
GLA_KEY_W = GLA_HEADS * GLA_DK
GLA_VAL_W = GLA_HEADS * GLA_DV
GLA_GATE_RANK = 16
GLA_GATE_NORM = 16.0
GLA_CHUNK = 64

NA_HEADS = 16
NA_HEAD_DIM = 64
NA_W = NA_HEADS * NA_HEAD_DIM
NA_WIN_H = 8
NA_WIN_W = 16
NA_QB = 16
NA_KW = NA_QB + NA_WIN_W

N_BRANCH = 3
D_FF = 4 * D_MODEL

IN_SIZES = (SSD_D_INNER, SSD_CONV_DIM, SSD_HEADS, SSD_HEADS,
            GLA_KEY_W, GLA_KEY_W, GLA_VAL_W, GLA_VAL_W, GLA_GATE_RANK, GLA_GATE_RANK,
            NA_W, NA_W, NA_W, N_BRANCH * D_MODEL)
N_IN = sum(IN_SIZES)

kernel_name = 'hybrid_ssd_gla_na_encoder'


def _rms(x):
    xf = x.astype(jnp.float32)
    return (xf * lax.rsqrt(jnp.mean(jnp.square(xf), axis=-1, keepdims=True) + EPS)).astype(x.dtype)


def _rev(a):
    return jnp.flip(a, axis=1)


def dwconv_centred(x, w, b):
    pad = w.shape[0] // 2
    y = lax.conv_general_dilated(x, w[:, None, :], window_strides=(1,), padding=[(pad, pad)],
                                 dimension_numbers=('NWC', 'WIO', 'NWC'),
                                 feature_group_count=x.shape[-1])
    return y + b


def ssd_chunked(x, dt, A, Bm, Cm):
    Bsz, S, H, P = x.shape
    G, N = Bm.shape[-2:]
    HG = H // G
    L = SSD_CHUNK
    nc = S // L
    x = x.reshape(Bsz, nc, L, G, HG, P)
    dt = dt.reshape(Bsz, nc, L, G, HG)
    Bm = Bm.reshape(Bsz, nc, L, G, N)
    Cm = Cm.reshape(Bsz, nc, L, G, N)
    a_cum = jnp.cumsum(dt * A.reshape(G, HG), axis=2)
    a_last = a_cum[:, :, -1]
    xdt = x * dt[..., None]
    ac = jnp.moveaxis(a_cum, 2, -1)
    seg = ac[..., :, None] - ac[..., None, :]
    tril = jnp.tril(jnp.ones((L, L), dtype=bool))
    decay = jnp.exp(jnp.where(tril, seg, -jnp.inf))
    cb = jnp.einsum('bclgn,bcsgn->bcgls', Cm, Bm)
    y_diag = jnp.einsum('bcghls,bcsghp->bclghp', cb[:, :, :, None] * decay, xdt)
    w_state = jnp.exp(a_last[:, :, None] - a_cum)
    states = jnp.einsum('bclgn,bclghp->bcghpn', Bm, xdt * w_state[..., None])

    def step(s, inp):
        st, dec = inp
        return s * dec[..., None, None] + st, s

    s0 = jnp.zeros((Bsz, G, HG, P, N), dtype=states.dtype)
    _, s_prev = lax.scan(step, s0, (jnp.moveaxis(states, 1, 0), jnp.moveaxis(jnp.exp(a_last), 1, 0)))
    s_prev = jnp.moveaxis(s_prev, 0, 1)
    y_off = jnp.einsum('bclgn,bcghpn->bclghp', Cm, s_prev) * jnp.exp(a_cum)[..., None]
    return (y_diag + y_off).reshape(Bsz, S, H, P)


def gla_chunked(q, k, v, g):
    Bsz, S, H, Kd = q.shape
    Vd = v.shape[-1]
    L = GLA_CHUNK
    nc = S // L
    q = q.reshape(Bsz, nc, L, H, Kd) * (Kd ** -0.5)
    k = k.reshape(Bsz, nc, L, H, Kd)
    v = v.reshape(Bsz, nc, L, H, Vd)
    b = jnp.cumsum(g.reshape(Bsz, nc, L, H, Kd), axis=2)
    b_last = b[:, :, -1]
    q_dec = q * jnp.exp(b)
    att = jnp.einsum('bclhk,bcshk->bchls', q_dec, k * jnp.exp(-b))
    tril = jnp.tril(jnp.ones((L, L), dtype=bool))
    att = jnp.where(tril, att, 0.0)
    o_intra = jnp.einsum('bchls,bcshv->bclhv', att, v)
    states = jnp.einsum('bclhk,bclhv->bchkv', k * jnp.exp(b_last[:, :, None] - b), v)

    def step(s, inp):
        st, dec = inp
        return s * dec[..., None] + st, s

    s0 = jnp.zeros((Bsz, H, Kd, Vd), dtype=states.dtype)
    _, s_prev = lax.scan(step, s0, (jnp.moveaxis(states, 1, 0), jnp.moveaxis(jnp.exp(b_last), 1, 0)))
    s_prev = jnp.moveaxis(s_prev, 0, 1)
    o_inter = jnp.einsum('bclhk,bchkv->bclhv', q_dec, s_prev)
    return (o_intra + o_inter).reshape(Bsz, S, H, Vd)


def _na_col_layout():
    ncb = GRID_W // NA_QB
    cb = np.arange(ncb)
    k_start = np.clip(cb * NA_QB - NA_WIN_W // 2, 0, GRID_W - NA_KW)
    kcol = k_start[:, None] + np.arange(NA_KW)
    qcol = cb[:, None] * NA_QB + np.arange(NA_QB)
    w_start = np.clip(qcol - NA_WIN_W // 2, 0, GRID_W - NA_WIN_W)
    rel = kcol[:, None, :] - w_start[:, :, None]
    mask = (rel >= 0) & (rel < NA_WIN_W)
    off = np.clip(kcol[:, None, :] - qcol[:, :, None], -(NA_WIN_W - 1), NA_WIN_W - 1) + (NA_WIN_W - 1)
    return kcol, mask, off


def na_2d(q, k, v, rpb):
    Bsz, S, H, Dh = q.shape
    rows = S // GRID_W
    win_h = min(NA_WIN_H, rows)
    ncb = GRID_W // NA_QB
    kcol, col_mask, col_off = _na_col_layout()
    qg = q.reshape(Bsz, rows, GRID_W, H, Dh) * (Dh ** -0.5)
    kg = k.reshape(Bsz, rows, GRID_W, H, Dh)
    vg = v.reshape(Bsz, rows, GRID_W, H, Dh)
    q_rows = jnp.moveaxis(qg, 1, 0).reshape(rows, Bsz, ncb, NA_QB, H, Dh)

    def one_row(args):
        q_r, r = args
        r0 = jnp.clip(r - win_h // 2, 0, rows - win_h)
        k_blk = lax.dynamic_slice_in_dim(kg, r0, win_h, axis=1)[:, :, kcol]
        v_blk = lax.dynamic_slice_in_dim(vg, r0, win_h, axis=1)[:, :, kcol]
        s = jnp.einsum('bcqhd,bwckhd->bhcqwk', q_r, k_blk).astype(jnp.float32)
        row_off = r0 + jnp.arange(win_h) - r + (NA_WIN_H - 1)
        bias = rpb[:, row_off][:, :, col_off]
        s = s + jnp.transpose(bias, (0, 2, 3, 1, 4)).astype(jnp.float32)
        s = jnp.where(col_mask[:, :, None, :], s, -jnp.inf)
        p = jax.nn.softmax(s.reshape(s.shape[:4] + (-1,)), axis=-1).reshape(s.shape).astype(v.dtype)
        o = jnp.einsum('bhcqwk,bwckhd->bcqhd', p, v_blk)
        return o.reshape(Bsz, GRID_W, H, Dh)

    out = lax.map(one_row, (q_rows, jnp.arange(rows)))
    return jnp.moveaxis(out, 0, 1).reshape(Bsz, S, H, Dh)


def setup_inputs(seed: int = 0) -> dict:
    key = jax.random.key(seed)
    ks = jax.random.split(key, 32)
    L = DEPTH
    D = D_MODEL

    def nrm(k, shape, scale):
        return jax.random.normal(k, shape, jnp.float32) * scale

    def gain(k, shape):
        return 1.0 + 0.02 * jax.random.normal(k, shape, jnp.float32)

    def dt_bias(k):
        dt = jnp.exp(jax.random.uniform(k, (L, SSD_HEADS), jnp.float32,
                                        minval=math.log(1e-3), maxval=math.log(1e-1)))
        return dt + jnp.log(-jnp.expm1(-dt))

    def a_log(k):
        return jnp.log(jax.random.uniform(k, (L, SSD_HEADS), jnp.float32, minval=1.0, maxval=16.0))

    return {
        'x': nrm(ks[0], (BATCH, SEQ, D), 1.0),
        'norm_mix_w': gain(ks[1], (L, D)),
        'w_in': nrm(ks[2], (L, D, N_IN), D ** -0.5),
        'ssd_conv_w': nrm(ks[3], (L, SSD_CONV, SSD_CONV_DIM), SSD_CONV ** -0.5),
        'ssd_conv_b': nrm(ks[4], (L, SSD_CONV_DIM), 0.02),
        'ssd_dt_bias_f': dt_bias(ks[5]),
        'ssd_dt_bias_b': dt_bias(ks[6]),
        'ssd_a_log_f': a_log(ks[7]),
        'ssd_a_log_b': a_log(ks[8]),
        'ssd_d': gain(ks[9], (L, SSD_HEADS)),
        'ssd_norm_w': gain(ks[10], (L, SSD_D_INNER)),
        'gla_a2_f': nrm(ks[11], (L, GLA_GATE_RANK, GLA_KEY_W), GLA_GATE_RANK ** -0.5),
        'gla_a2_bias_f': nrm(ks[12], (L, GLA_KEY_W), 0.1),
        'gla_a2_b': nrm(ks[13], (L, GLA_GATE_RANK, GLA_KEY_W), GLA_GATE_RANK ** -0.5),
        'gla_a2_bias_b': nrm(ks[14], (L, GLA_KEY_W), 0.1),
        'gla_norm_w': gain(ks[15], (L, GLA_DV)),
        'na_q_norm_w': gain(ks[16], (L, NA_HEAD_DIM)),
        'na_k_norm_w': gain(ks[17], (L, NA_HEAD_DIM)),
        'na_rpb': nrm(ks[18], (L, NA_HEADS, 2 * NA_WIN_H - 1, 2 * NA_WIN_W - 1), 0.02),
        'w_branch_ssd': nrm(ks[19], (L, SSD_D_INNER, D), SSD_D_INNER ** -0.5),
        'w_branch_gla': nrm(ks[20], (L, GLA_VAL_W, D), GLA_VAL_W ** -0.5),
        'w_branch_na': nrm(ks[21], (L, NA_W, D), NA_W ** -0.5),
        'w_out': nrm(ks[22], (L, D, D), D ** -0.5),
        'norm_mlp_w': gain(ks[23], (L, D)),
        'w_ff1': nrm(ks[24], (L, D, D_FF), D ** -0.5),
        'w_ff2': nrm(ks[25], (L, D_FF, D), D_FF ** -0.5),
    }


def reference(x, norm_mix_w, w_in, ssd_conv_w, ssd_conv_b, ssd_dt_bias_f, ssd_dt_bias_b,
              ssd_a_log_f, ssd_a_log_b, ssd_d, ssd_norm_w, gla_a2_f, gla_a2_bias_f, gla_a2_b,
              gla_a2_bias_b, gla_norm_w, na_q_norm_w, na_k_norm_w, na_rpb, w_branch_ssd,
              w_branch_gla, w_branch_na, w_out, norm_mlp_w, w_ff1, w_ff2):
    Bsz, S, D = x.shape
    splits = [int(i) for i in np.cumsum(IN_SIZES)[:-1]]
    for l in range(DEPTH):
        h = _rms(x) * norm_mix_w[l]
        u = h @ w_in[l]
        (z, xbc, dt_f, dt_b, gq, gk, gv, gg, ga_f, ga_b,
         nq, nk, nv, gate_logits) = jnp.split(u, splits, axis=-1)

        xbc = jax.nn.silu(dwconv_centred(xbc, ssd_conv_w[l], ssd_conv_b[l]))
        xs, Bm, Cm = jnp.split(xbc, [SSD_D_INNER, SSD_D_INNER + SSD_GROUPS * SSD_STATE], axis=-1)
        xs = xs.reshape(Bsz, S, SSD_HEADS, SSD_HEAD_DIM)
        Bm = Bm.reshape(Bsz, S, SSD_GROUPS, SSD_STATE)
        Cm = Cm.reshape(Bsz, S, SSD_GROUPS, SSD_STATE)
        dtf = jax.nn.softplus(dt_f + ssd_dt_bias_f[l])
        dtb = jax.nn.softplus(dt_b + ssd_dt_bias_b[l])
        y_f = ssd_chunked(xs, dtf, -jnp.exp(ssd_a_log_f[l]), Bm, Cm)
        y_b = _rev(ssd_chunked(_rev(xs), _rev(dtb), -jnp.exp(ssd_a_log_b[l]), _rev(Bm), _rev(Cm)))
        y = (y_f + y_b + xs * ssd_d[l][:, None]).reshape(Bsz, S, SSD_D_INNER) * jax.nn.silu(z)
        y = _rms(y.reshape(Bsz, S, SSD_GROUPS, -1)).reshape(Bsz, S, SSD_D_INNER) * ssd_norm_w[l]
        br_ssd = y @ w_branch_ssd[l]

        q = gq.reshape(Bsz, S, GLA_HEADS, GLA_DK)
        k = gk.reshape(Bsz, S, GLA_HEADS, GLA_DK)
        v = gv.reshape(Bsz, S, GLA_HEADS, GLA_DV)
        g_f = (jax.nn.log_sigmoid(ga_f @ gla_a2_f[l] + gla_a2_bias_f[l]) / GLA_GATE_NORM
               ).reshape(Bsz, S, GLA_HEADS, GLA_DK)
        g_b = (jax.nn.log_sigmoid(ga_b @ gla_a2_b[l] + gla_a2_bias_b[l]) / GLA_GATE_NORM
               ).reshape(Bsz, S, GLA_HEADS, GLA_DK)
        o = gla_chunked(q, k, v, g_f) + _rev(gla_chunked(_rev(q), _rev(k), _rev(v), _rev(g_b)))
        o = _rms(o) * gla_norm_w[l] * jax.nn.silu(gg).reshape(Bsz, S, GLA_HEADS, GLA_DV)
        br_gla = o.reshape(Bsz, S, GLA_VAL_W) @ w_branch_gla[l]

        qn = _rms(nq.reshape(Bsz, S, NA_HEADS, NA_HEAD_DIM)) * na_q_norm_w[l]
        kn = _rms(nk.reshape(Bsz, S, NA_HEADS, NA_HEAD_DIM)) * na_k_norm_w[l]
        vn = nv.reshape(Bsz, S, NA_HEADS, NA_HEAD_DIM)
        br_na = na_2d(qn, kn, vn, na_rpb[l]).reshape(Bsz, S, NA_W) @ w_branch_na[l]

        gates = jax.nn.sigmoid(gate_logits).reshape(Bsz, S, N_BRANCH, D)
        mixed = gates[:, :, 0] * br_ssd + gates[:, :, 1] * br_gla + gates[:, :, 2] * br_na
        x = x + mixed @ w_out[l]

        hm = _rms(x) * norm_mlp_w[l]
        x = x + jnp.square(jax.nn.relu(hm @ w_ff1[l])) @ w_ff2[l]
    return x
```

```python
import contextlib
import numpy as np
import concourse.bass as bass
import concourse.mybir as mybir
from concourse.bass_utils import run_bass_kernel_spmd

F32 = mybir.dt.float32
BF16 = mybir.dt.bfloat16
ALU = mybir.AluOpType
AF = mybir.ActivationFunctionType
AX = mybir.AxisListType

D = 1024
S = 2048
NT = S // 128
DEPTH = 2
N_IN = 11840
EPS = 1e-6


class TT:
    __slots__ = ("name", "lw", "rd", "dsems", "gen")

    def __init__(self, name):
        self.name = name
        self.lw = {}
        self.rd = {}
        self.gen = {}
        self.dsems = {}


class Sched:
    ENG = ("pe", "act", "dve", "pool", "sp")
    BLK = {"pe": "tensor", "act": "scalar", "dve": "vector", "pool": "gpsimd", "sp": "sync"}

    def __init__(self, nc, stack):
        self.nc = nc
        self.stack = stack
        self.ops = {e: [] for e in self.ENG}
        self.seen = {e: {} for e in self.ENG}
        self.esem = {e: stack.enter_context(nc.semaphore("es_" + e)) for e in self.ENG if e != "sp"}
        self.tiles = []
        self.free_dsems = {"sp": [], "pool": [], "act": []}
        self.nsem = 4
        self.skip_same = {"pe"}

    def tile(self, name):
        t = TT(name)
        self.tiles.append(t)
        return t

    def tiles_n(self, name, n):
        return [self.tile("%s%d" % (name, i)) for i in range(n)]

    def _collect(self, reads, writes, part):
        evs = {}

        def add(d):
            for k, v in d.items():
                if k not in evs or evs[k][0] < v[0]:
                    evs[k] = v
        for t in reads:
            add(t.lw)
        for t in writes:
            if part and not t.rd:
                add(t.gen)
                continue
            g = dict(t.rd)
            for k, v in t.lw.items():
                if k not in g or g[k][0] < v[0]:
                    g[k] = v
            t.gen = g
            add(g)
        return evs

    def _waits(self, eng, evs):
        waits = []
        for k, (val, obj) in evs.items():
            if k == ("E", eng) and eng in self.skip_same:
                continue
            if self.seen[eng].get(k, 0) >= val:
                continue
            self.seen[eng][k] = val
            waits.append((k, val, obj))
            if k[0] == "E":
                self.ops[k[1]][val - 1]["inc"] = True
        return waits

    def _update(self, ev_key, ev_val, reads, writes, part):
        for t in reads:
            t.rd[ev_key] = ev_val
        for t in writes:
            if part and not t.rd:
                t.lw[ev_key] = ev_val
            else:
                t.lw = {ev_key: ev_val}
                t.rd = {}

    def op(self, eng, fn, reads=(), writes=(), part=False):
        waits = self._waits(eng, self._collect(reads, writes, part))
        self.ops[eng].append({"fn": fn, "waits": waits, "inc": False, "dma": None})
        idx = len(self.ops[eng])
        self._update(("E", eng), (idx, None), reads, writes, part)

    def dma(self, q, out, in_, owner, reads=(), writes=(), part=False, **kw):
        waits = self._waits(q, self._collect(reads, writes, part))
        rec = owner.dsems.get(q)
        if rec is None:
            if self.free_dsems[q]:
                rec = self.free_dsems[q].pop()
            else:
                rec = [self.stack.enter_context(self.nc.semaphore("ds%d" % self.nsem)), 0, self.nsem]
                self.nsem += 1
            owner.dsems[q] = rec
        rec[1] += 16
        self.ops[q].append({"fn": (lambda e: e.dma_start(out=out, in_=in_, **kw)), "waits": waits,
                            "inc": False, "dma": rec[0]})
        self._update(("D", rec[2]), (rec[1], rec[0]), reads, writes, part)

    def barrier(self, release=()):
        evs = {}
        for e in self.ENG:
            if e == "sp":
                continue
            idx = len(self.ops[e])
            while idx > 0 and (self.ops[e][idx - 1]["dma"] is not None or self.ops[e][idx - 1].get("nop")):
                idx -= 1
            if idx > 0:
                evs[("E", e)] = (idx, None)
        for t in self.tiles:
            for d in (t.lw, t.rd):
                for k, v in d.items():
                    if k[0] == "D" and (k not in evs or evs[k][0] < v[0]):
                        evs[k] = v
        for e in self.ENG:
            sk = self.skip_same
            self.skip_same = set()
            w = self._waits(e, dict(evs))
            self.skip_same = sk
            self.ops[e].append({"fn": (lambda en: en.nop()), "waits": w, "inc": False, "dma": None, "nop": True})
        for t in self.tiles:
            t.lw = {}
            t.rd = {}
            t.gen = {}
        rel = set(id(t) for t in release)
        for t in release:
            for q, rec in t.dsems.items():
                self.free_dsems[q].append(rec)
            t.dsems = {}
        self.tiles = [t for t in self.tiles if id(t) not in rel]

    def emit(self):
        nc = self.nc
        mile = {}
        for e in self.ENG:
            c = 0
            m = []
            for o in self.ops[e]:
                if o["inc"]:
                    c += 1
                m.append(c)
            mile[e] = m
            assert c < 60000, (e, c)
        with nc.Block() as block:
            for e in self.ENG:
                def body(engine, e=e):
                    for o in self.ops[e]:
                        for (k, val, obj) in o["waits"]:
                            if k[0] == "E":
                                engine.wait_ge(self.esem[k[1]], mile[k[1]][val - 1])
                            else:
                                engine.wait_ge(obj, val)
                        ins = o["fn"](engine)
                        if o["dma"] is not None:
                            ins.then_inc(o["dma"], 16)
                        elif o["inc"]:
                            ins.then_inc(self.esem[e], 1)
                getattr(block, self.BLK[e])(body)


class Prog:
    def __init__(self, cfg):
        self.cfg = cfg
        self.nc = bass.Bass("TRN2", target_bir_lowering=False)
        self.dbg = cfg.get("debug", ())

    def dram(self, name, shape, dt, kind="Internal"):
        if name in self.dbg:
            kind = "ExternalOutput"
        if name in self.cfg.get("ext_in", ()):
            kind = "ExternalInput"
        return self.nc.dram_tensor(name, list(shape), dt, kind=kind).ap()

    def sb(self, stack, name, shape, dt):
        self.uid = getattr(self, "uid", 0) + 1
        return stack.enter_context(self.nc.sbuf_tensor("%s_u%d" % (name, self.uid), list(shape), dt))

    def ps(self, stack, name, shape, dt):
        self.uid = getattr(self, "uid", 0) + 1
        return stack.enter_context(self.nc.psum_tensor("%s_u%d" % (name, self.uid), list(shape), dt))


IN_SIZES = (1024, 1536, 16, 16, 512, 512, 1024, 1024, 16, 16, 1024, 1024, 1024, 3072)
IN_OFF = [0]
for _s in IN_SIZES:
    IN_OFF.append(IN_OFF[-1] + _s)
(O_Z, O_XBC, O_DTF, O_DTB, O_GQ, O_GK, O_GV, O_GG, O_GAF, O_GAB, O_NQ, O_NK, O_NV, O_GATE, _) = IN_OFF

PARAM_NAMES = ["norm_mix_w", "w_in", "ssd_conv_w", "ssd_conv_b", "ssd_dt_bias_f", "ssd_dt_bias_b",
               "ssd_a_log_f", "ssd_a_log_b", "ssd_d", "ssd_norm_w", "gla_a2_f", "gla_a2_bias_f",
               "gla_a2_b", "gla_a2_bias_b", "gla_norm_w", "na_q_norm_w", "na_k_norm_w", "na_rpb",
               "w_branch_ssd", "w_branch_gla", "w_branch_na", "w_out", "norm_mlp_w", "w_ff1", "w_ff2"]
PARAM_SHAPES = {
    "norm_mix_w": (2, 1024), "w_in": (2, 1024, 11840), "ssd_conv_w": (2, 5, 1536), "ssd_conv_b": (2, 1536),
    "ssd_dt_bias_f": (2, 16), "ssd_dt_bias_b": (2, 16), "ssd_a_log_f": (2, 16), "ssd_a_log_b": (2, 16),
    "ssd_d": (2, 16), "ssd_norm_w": (2, 1024), "gla_a2_f": (2, 16, 512), "gla_a2_bias_f": (2, 512),
    "gla_a2_b": (2, 16, 512), "gla_a2_bias_b": (2, 512), "gla_norm_w": (2, 256), "na_q_norm_w": (2, 64),
    "na_k_norm_w": (2, 64), "na_rpb": (2, 16, 15, 31), "w_branch_ssd": (2, 1024, 1024),
    "w_branch_gla": (2, 1024, 1024), "w_branch_na": (2, 1024, 1024), "w_out": (2, 1024, 1024),
    "norm_mlp_w": (2, 1024), "w_ff1": (2, 1024, 4096), "w_ff2": (2, 4096, 1024),
}


def build(cfg):
    P = Prog(cfg)
    nc = P.nc
    layers = cfg.get("layers", DEPTH)
    phases = cfg.get("phases", "ABCDEF")
    x_in = nc.dram_tensor("x", [S, D], F32, kind="ExternalInput").ap()
    prm = {n: nc.dram_tensor(n, list(PARAM_SHAPES[n]), F32, kind="ExternalInput").ap() for n in PARAM_NAMES}
    y_out = nc.dram_tensor("y", [S, D], F32, kind="ExternalOutput").ap()
    natt = nc.dram_tensor("na_tt", [DEPTH, 128, 8, 17, 64], F32, kind="ExternalInput").ap()

    U = {}
    for nm, w in (("z", 1024), ("gv", 1024), ("gg", 1024), ("nv", 1024)):
        U[nm] = P.dram("u_" + nm, [S, w], BF16)
    U["dt"] = P.dram("u_dt", [S, 32], F32)
    for nm, w in (("xbc", 1536), ("gq", 512), ("gk", 512), ("nq", 1024), ("nk", 1024), ("gate", 3072)):
        U[nm] = P.dram("u_" + nm + "T", [w, S], BF16)
    U["ga"] = P.dram("u_gaT", [32, S], F32)
    YB = {nm: P.dram("yb_" + nm, [1024, S], BF16) for nm in ("ssd", "gla", "na")}
    ybw = P.dram("ybw", [S, 1024], BF16)

    with contextlib.ExitStack() as top:
        sc = Sched(nc, top)
        G = {}
        G["x"] = P.sb(top, "x_res", [128, NT, D], F32)
        G["xt"] = sc.tiles_n("x", NT)
        G["ident"] = P.sb(top, "ident", [128, 128], BF16)
        G["ident_t"] = sc.tile("ident")
        G["dram_t"] = {k: sc.tile("d_" + k) for k in list(U) + ["yb_ssd", "yb_gla", "yb_na", "ybw"]}
        G["ybw"] = ybw

        ones_f = P.sb(top, "ones_f", [128, 128], F32)
        ones_t = sc.tile("ones_f")
        sc.op("pool", lambda e: e.memset(ones_f[:], 1.0), writes=[ones_t])
        sc.op("pool", lambda e: e.affine_select(out=G["ident"][:], in_=ones_f[:], pattern=[[-1, 128]],
                                                compare_op=ALU.is_equal, fill=0.0, base=0,
                                                channel_multiplier=1),
              reads=[ones_t], writes=[G["ident_t"]])
        G["ones_f"] = ones_f
        G["eps"] = P.sb(top, "epsc", [128, 2], F32)
        G["eps_t"] = sc.tile("epsc")
        sc.op("pool", lambda e: e.memset(G["eps"][:], EPS), writes=[G["eps_t"]])
        G["one"] = P.sb(top, "onec", [128, 2], F32)
        G["one_t"] = sc.tile("onec")
        sc.op("pool", lambda e: e.memset(G["one"][:], 1.0), writes=[G["one_t"]])
        G["neghalf"] = P.sb(top, "neghalf", [128, 16], F32)
        G["neghalf_t"] = sc.tile("neghalf")
        sc.op("pool", lambda e: e.memset(G["neghalf"][:], -0.5), writes=[G["neghalf_t"]])
        G["ones_t"] = ones_t

        build_tri(P, sc, G, top)
        xv = x_in.rearrange("(i p) d -> p i d", p=128)
        for i in range(NT):
            sc.dma("sp", G["x"][:, i, :], xv[:, i, :], owner=G["xt"][i], writes=[G["xt"][i]])

        for l in range(layers):
            if "A" in phases:
                phase_A(P, sc, G, U, prm, l)
            if "B" in phases:
                phase_B(P, sc, G, U, YB, prm, l)
            if "C" in phases:
                phase_C(P, sc, G, U, YB, prm, l)
            if "D" in phases:
                phase_D(P, sc, G, U, YB, prm, natt, l)
            if "E" in phases:
                phase_E(P, sc, G, U, YB, prm, l)
            if "F" in phases:
                phase_F(P, sc, G, prm, l)

        yv = y_out.rearrange("(i p) d -> p i d", p=128)
        outt = sc.tile("yout")
        for i in range(NT):
            sc.dma("sp", yv[:, i, :], G["x"][:, i, :], owner=G["xt"][i], reads=[G["xt"][i]], writes=[outt],
                   part=True)
        sc.op("sp", lambda e: e.nop(), reads=[outt])
        sc.barrier()
        sc.emit()
    return nc


def rms_transpose(P, sc, G, ph, wrow_ap, hT, hT_t, l, tag):
    nc = P.nc
    wb = P.sb(ph, tag + "_wb", [128, D], F32)
    wb_t = sc.tile(tag + "_wb")
    sc.dma("sp", wb[:], wrow_ap.partition_broadcast(128), owner=wb_t, writes=[wb_t])
    junk = [P.sb(ph, tag + "_junk%d" % i, [128, D], BF16) for i in range(2)]
    junk_t = sc.tiles_n(tag + "_junk", 2)
    hb = [P.sb(ph, tag + "_hb%d" % i, [128, D], BF16) for i in range(2)]
    hb_t = sc.tiles_n(tag + "_hb", 2)
    ss = [P.sb(ph, tag + "_ss%d" % i, [128, 2], F32) for i in range(2)]
    ss_t = sc.tiles_n(tag + "_ss", 2)
    tp = [P.ps(ph, tag + "_tp%d" % i, [128, 4, 128], BF16) for i in range(2)]
    tp_t = sc.tiles_n(tag + "_tp", 2)
    x = G["x"]
    new_tiles = [wb_t] + junk_t + hb_t + ss_t + tp_t
    for i in range(NT):
        b = i % 2
        xt = G["xt"][i]
        sc.op("dve", lambda e, i=i, b=b: e.scalar_tensor_tensor(out=junk[b][:], in0=x[:, i, :], scalar=1.0,
                                                                in1=x[:, i, :], op0=ALU.mult, op1=ALU.mult,
                                                                accum_out=ss[b][:, 0:1]),
              reads=[xt], writes=[junk_t[b], ss_t[b]])
        sc.op("dve", lambda e, b=b: e.tensor_scalar(out=ss[b][:, 1:2], in0=ss[b][:, 0:1], scalar1=1.0 / D,
                                                    scalar2=EPS, op0=ALU.mult, op1=ALU.add),
              reads=[ss_t[b]], writes=[ss_t[b]])
        sc.op("pool", lambda e, b=b: e.tensor_tensor(out=ss[b][:, 0:1], in0=ss[b][:, 1:2],
                                                     in1=G["neghalf"][:, 0:1], op=ALU.pow),
              reads=[ss_t[b], G["neghalf_t"]], writes=[ss_t[b]])
        sc.op("dve", lambda e, i=i, b=b: e.scalar_tensor_tensor(out=hb[b][:], in0=x[:, i, :],
                                                                scalar=ss[b][:, 0:1], in1=wb[:],
                                                                op0=ALU.mult, op1=ALU.mult),
              reads=[xt, ss_t[b], wb_t], writes=[hb_t[b]])
        for half in range(2):
            pb = (2 * i + half) % 2
            for j in range(4):
                kc = half * 4 + j
                sc.op("pe", lambda e, b=b, pb=pb, j=j, kc=kc: e.transpose(
                    out=tp[pb][:, j, :], in_=hb[b][:, kc * 128:(kc + 1) * 128], identity=G["ident"][:]),
                    reads=[hb_t[b], G["ident_t"]], writes=[tp_t[pb]], part=(j > 0))
            eng = "act" if half == 0 else "dve"
            if eng == "act":
                sc.op("act", lambda e, pb=pb, half=half, i=i: e.copy(
                    out=hT[:, half * 4:half * 4 + 4, i * 128:(i + 1) * 128], in_=tp[pb][:]),
                    reads=[tp_t[pb]], writes=[hT_t[i]], part=True)
            else:
                sc.op("dve", lambda e, pb=pb, half=half, i=i: e.tensor_copy(
                    out=hT[:, half * 4:half * 4 + 4, i * 128:(i + 1) * 128], in_=tp[pb][:]),
                    reads=[tp_t[pb]], writes=[hT_t[i]], part=True)
    return new_tiles


class WStream:
    def __init__(self, P, sc, ph, tag, kdim, ncol, nf=2, nb=3):
        self.sc = sc
        self.kdim, self.ncol = kdim, ncol
        self.nf, self.nb = nf, nb
        self.f = [P.sb(ph, "%s_wf%d" % (tag, i), [128, kdim, ncol], F32) for i in range(nf)]
        self.f_t = sc.tiles_n(tag + "_wf", nf)
        self.b = [P.sb(ph, "%s_wb%d" % (tag, i), [128, kdim, ncol], BF16) for i in range(nb)]
        self.b_t = sc.tiles_n(tag + "_wbt", nb)
        self.tiles = self.f_t + self.b_t
        self.items = []

    def start(self, items):
        self.items = items
        self._load(0)
        self._load(1)
        self._cast(0)

    def _load(self, g):
        if g >= len(self.items):
            return
        ap, k, n = self.items[g]
        fs = g % self.nf
        self.sc.dma("sp", self.f[fs][:, 0:k, 0:n], ap, owner=self.f_t[fs], writes=[self.f_t[fs]])

    def _cast(self, g):
        if g >= len(self.items):
            return
        ap, k, n = self.items[g]
        fs, bs = g % self.nf, g % self.nb
        self.sc.op("pool", lambda e: e.tensor_copy(out=self.b[bs][:, 0:k, 0:n], in_=self.f[fs][:, 0:k, 0:n]),
                   reads=[self.f_t[fs]], writes=[self.b_t[bs]])

    def get(self, g):
        self._cast(g + 1)
        self._load(g + 2)
        return self.b[g % self.nb], self.b_t[g % self.nb]


def proj_groups():
    g = []

    def seg(off, n, mode, key):
        c = 0
        while c < n:
            w = min(512, n - c)
            g.append((off + c, w, mode, key, c))
            c += w
    seg(O_Z, 1024, "tok", "z")
    seg(O_XBC, 1536, "feat", "xbc")
    g.append((O_DTF, 32, "tok32", "dt", 0))
    seg(O_GQ, 512, "feat", "gq")
    seg(O_GK, 512, "feat", "gk")
    seg(O_GV, 1024, "tok", "gv")
    seg(O_GG, 1024, "tok", "gg")
    g.append((O_GAF, 32, "feat32", "ga", 0))
    seg(O_NQ, 1024, "feat", "nq")
    seg(O_NK, 1024, "feat", "nk")
    seg(O_NV, 1024, "tok", "nv")
    seg(O_GATE, 3072, "feat", "gate")
    return g


def phase_A(P, sc, G, U, prm, l):
    nc = P.nc
    with contextlib.ExitStack() as ph:
        hT = P.sb(ph, "A_hT", [128, 8, S], BF16)
        hT_t = sc.tiles_n("A_hT", NT)
        tiles = list(hT_t)
        tiles += rms_transpose(P, sc, G, ph, prm["norm_mix_w"][l], hT, hT_t, l, "A")
        wst = WStream(P, sc, ph, "A", 8, 512)
        acc = [P.ps(ph, "A_acc%d" % i, [128, 512], F32) for i in range(4)]
        acc_t = sc.tiles_n("A_acc", 4)
        NS = 3
        stg = [P.sb(ph, "A_stg%d" % i, [128, 2048], BF16) for i in range(NS)]
        stg_t = sc.tiles_n("A_stg", NS)
        stf = [P.sb(ph, "A_stf%d" % i, [128, 4, 32], F32) for i in range(2)]
        stf_t = sc.tiles_n("A_stf", 2)
        G["ga_stage"] = P.sb(ph, "A_gast", [32, 2048], F32)
        G["ga_stage_t"] = sc.tile("A_gast")
        tiles += wst.tiles + acc_t + stg_t + stf_t + [G["ga_stage_t"]]
        wv = prm["w_in"][l].rearrange("(kc p) n -> p kc n", p=128)
        groups = proj_groups()
        wst.start([(wv[:, :, c0:c0 + n], 8, n) for (c0, n, _m, _k, _d) in groups])
        ai = 0
        si = 0
        ev = 0
        for gi, (c0, n, mode, key, doff) in enumerate(groups):
            wcur, wcur_t = wst.get(gi)
            dst = U[key]
            dst_t = G["dram_t"][key]
            if mode in ("tok", "tok32"):
                for tb in range(4):
                    if mode == "tok":
                        st = si % NS
                        si += 1
                    else:
                        st = tb % 2
                    for j in range(4):
                        i = tb * 4 + j
                        a = ai % 4
                        ai += 1
                        for kc in range(8):
                            sc.op("pe", lambda e, a=a, kc=kc, i=i, wcur=wcur, n=n: e.matmul(
                                acc[a][:, 0:n], lhsT=hT[:, kc, i * 128:(i + 1) * 128], rhs=wcur[:, kc, 0:n],
                                start=(kc == 0), stop=(kc == 7)),
                                reads=[hT_t[i], wcur_t], writes=[acc_t[a]], part=(kc > 0))
                        if mode == "tok":
                            o_ap = stg[st][:, j * 512:j * 512 + n]
                            o_t = stg_t[st]
                        else:
                            o_ap = stf[st][:, j, 0:n]
                            o_t = stf_t[st]
                        ev += 1
                        if ev % 2 == 0:
                            sc.op("act", lambda e, o_ap=o_ap, a=a, n=n: e.copy(out=o_ap, in_=acc[a][:, 0:n]),
                                  reads=[acc_t[a]], writes=[o_t], part=(j > 0))
                        else:
                            sc.op("dve", lambda e, o_ap=o_ap, a=a, n=n: e.tensor_copy(out=o_ap, in_=acc[a][:, 0:n]),
                                  reads=[acc_t[a]], writes=[o_t], part=(j > 0))
                    rows = dst[tb * 512:(tb + 1) * 512, doff:doff + n].rearrange("(j p) c -> p j c", p=128)
                    if mode == "tok":
                        src = stg[st][:].rearrange("p (j c) -> p j c", j=4)[:, :, 0:n]
                        sc.dma("pool", rows, src, owner=stg_t[st], reads=[stg_t[st]], writes=[dst_t], part=True)
                    else:
                        sc.dma("pool", rows, stf[st][:, :, 0:n], owner=stf_t[st], reads=[stf_t[st]], writes=[dst_t],
                               part=True)
            else:
                nchunk = (n + 127) // 128
                for c in range(nchunk):
                    m = min(128, n - c * 128)
                    if mode == "feat":
                        st = si % NS
                        si += 1
                    else:
                        st = 0
                    for tb in range(4):
                        a = ai % 4
                        ai += 1
                        for kc in range(8):
                            sc.op("pe", lambda e, a=a, kc=kc, tb=tb, wcur=wcur, c=c, m=m: e.matmul(
                                acc[a][0:m, :], lhsT=wcur[:, kc, c * 128:c * 128 + m],
                                rhs=hT[:, kc, tb * 512:(tb + 1) * 512], start=(kc == 0), stop=(kc == 7)),
                                reads=hT_t[tb * 4:tb * 4 + 4] + [wcur_t], writes=[acc_t[a]], part=(kc > 0))
                        ev += 1
                        if mode == "feat":
                            o_ap = stg[st][0:m, tb * 512:(tb + 1) * 512]
                            o_t = stg_t[st]
                            if ev % 2 == 0:
                                sc.op("act", lambda e, o_ap=o_ap, a=a, m=m: e.copy(out=o_ap, in_=acc[a][0:m, :]),
                                      reads=[acc_t[a]], writes=[o_t], part=(tb > 0))
                            else:
                                sc.op("dve", lambda e, o_ap=o_ap, a=a, m=m: e.tensor_copy(out=o_ap, in_=acc[a][0:m, :]),
                                      reads=[acc_t[a]], writes=[o_t], part=(tb > 0))
                        else:
                            sc.op("dve", lambda e, a=a, m=m, tb=tb, gast=G["ga_stage"]: e.tensor_copy(
                                out=gast[0:m, tb * 512:(tb + 1) * 512], in_=acc[a][0:m, :]),
                                reads=[acc_t[a]], writes=[G["ga_stage_t"]], part=(tb > 0))
                    if mode == "feat":
                        sc.dma("pool", dst[doff + c * 128:doff + c * 128 + m, :], stg[st][0:m, :], owner=stg_t[st],
                               reads=[stg_t[st]], writes=[dst_t], part=True)
                    else:
                        sc.dma("pool", dst[0:m, :], G["ga_stage"][0:m, :], owner=G["ga_stage_t"],
                               reads=[G["ga_stage_t"]], writes=[dst_t], part=True)
        sc.barrier(release=tiles)


def phase_E(P, sc, G, U, YB, prm, l):
    nc = P.nc
    x = G["x"]
    with contextlib.ExitStack() as ph:
        wst = WStream(P, sc, ph, "E", 8, 256, nf=2, nb=2)
        mix = P.sb(ph, "E_mix", [128, 8, 1024], F32)
        mix_t = sc.tiles_n("E_mix", 8)
        mixb = P.sb(ph, "E_mixb", [128, 8, 1024], BF16)
        mixb_t = sc.tile("E_mixb")
        ybT = [P.sb(ph, "E_yb%d" % i, [128, 8, 1024], BF16) for i in range(2)]
        ybT_t = sc.tiles_n("E_yb", 2)
        gsl = [P.sb(ph, "E_g%d" % i, [128, 1024], BF16) for i in range(3)]
        gsl_t = sc.tiles_n("E_g", 3)
        sig = [P.sb(ph, "E_sig%d" % i, [128, 1024], F32) for i in range(2)]
        sig_t = sc.tiles_n("E_sig", 2)
        tmp = [P.sb(ph, "E_tmp%d" % i, [128, 512], F32) for i in range(2)]
        tmp_t = sc.tiles_n("E_tmp", 2)
        acc = [P.ps(ph, "E_acc%d" % i, [128, 512], F32) for i in range(4)]
        acc_t = sc.tiles_n("E_acc", 4)
        tiles = wst.tiles + mix_t + [mixb_t] + ybT_t + gsl_t + sig_t + tmp_t + acc_t
        wnames = ["w_branch_ssd", "w_branch_gla", "w_branch_na", "w_out"]
        bnames = ["ssd", "gla", "na"]
        items = []
        for half in range(2):
            for wn in wnames:
                wv = prm[wn][l].rearrange("(kc p) n -> p kc n", p=128)
                for cg in range(4):
                    items.append((wv[:, :, cg * 256:(cg + 1) * 256], 8, 256))
        wst.start(items)
        gi = 0
        ai = 0
        gcount = 0
        tcount = 0
        ybcount = 0
        for half in range(2):
            t0 = half * 1024
            for b in range(3):
                ys = ybcount % 2
                ybcount += 1
                ybv = YB[bnames[b]].rearrange("(kc p) t -> p kc t", p=128)
                sc.dma("sp", ybT[ys][:], ybv[:, :, t0:t0 + 1024], owner=ybT_t[ys],
                       reads=[G["dram_t"]["yb_" + bnames[b]]], writes=[ybT_t[ys]])
                for cg in range(4):
                    wcur, wcur_t = wst.get(gi)
                    gi += 1
                    for ecl in range(2):
                        ec = cg * 2 + ecl
                        gs = gcount % 3
                        ss_ = gcount % 2
                        gcount += 1
                        grow = b * 1024 + ec * 128
                        sc.dma("sp", gsl[gs][:], U["gate"][grow:grow + 128, t0:t0 + 1024], owner=gsl_t[gs],
                               reads=[G["dram_t"]["gate"]], writes=[gsl_t[gs]])
                        sc.op("act", lambda e, gs=gs, ss_=ss_: e.activation(out=sig[ss_][:], in_=gsl[gs][:],
                                                                            func=AF.Sigmoid),
                              reads=[gsl_t[gs]], writes=[sig_t[ss_]])
                        for tbh in range(2):
                            a = ai % 4
                            ai += 1
                            for kc in range(8):
                                sc.op("pe", lambda e, a=a, kc=kc, wcur=wcur, ecl=ecl, ys=ys, tbh=tbh: e.matmul(
                                    acc[a][:], lhsT=wcur[:, kc, ecl * 128:(ecl + 1) * 128],
                                    rhs=ybT[ys][:, kc, tbh * 512:(tbh + 1) * 512], start=(kc == 0), stop=(kc == 7)),
                                    reads=[wcur_t, ybT_t[ys]], writes=[acc_t[a]], part=(kc > 0))
                            msl = mix[:, ec, tbh * 512:(tbh + 1) * 512]
                            sgl = sig[ss_][:, tbh * 512:(tbh + 1) * 512]
                            if b == 0:
                                sc.op("dve", lambda e, msl=msl, a=a, sgl=sgl: e.tensor_tensor(
                                    out=msl, in0=acc[a][:], in1=sgl, op=ALU.mult),
                                    reads=[acc_t[a], sig_t[ss_]], writes=[mix_t[ec]], part=(tbh > 0))
                            else:
                                ts = tcount % 2
                                tcount += 1
                                sc.op("dve", lambda e, ts=ts, a=a, sgl=sgl: e.tensor_tensor(
                                    out=tmp[ts][:], in0=acc[a][:], in1=sgl, op=ALU.mult),
                                    reads=[acc_t[a], sig_t[ss_]], writes=[tmp_t[ts]])
                                if b == 1:
                                    sc.op("pool", lambda e, msl=msl, ts=ts: e.tensor_tensor(
                                        out=msl, in0=msl, in1=tmp[ts][:], op=ALU.add),
                                        reads=[tmp_t[ts], mix_t[ec]], writes=[mix_t[ec]])
                                else:
                                    sc.op("pool", lambda e, msl=msl, ts=ts, ec=ec, tbh=tbh: e.tensor_tensor(
                                        out=mixb[:, ec, tbh * 512:(tbh + 1) * 512], in0=msl, in1=tmp[ts][:],
                                        op=ALU.add),
                                        reads=[tmp_t[ts], mix_t[ec]], writes=[mixb_t], part=True)
            for cg in range(4):
                wcur, wcur_t = wst.get(gi)
                gi += 1
                for j in range(8):
                    i = half * 8 + j
                    a = ai % 4
                    ai += 1
                    for ec in range(8):
                        sc.op("pe", lambda e, a=a, ec=ec, wcur=wcur, j=j: e.matmul(
                            acc[a][:, 0:256], lhsT=mixb[:, ec, j * 128:(j + 1) * 128], rhs=wcur[:, ec, :],
                            start=(ec == 0), stop=(ec == 7)),
                            reads=[wcur_t, mixb_t], writes=[acc_t[a]], part=(ec > 0))
                    xs = x[:, i, cg * 256:(cg + 1) * 256]
                    sc.op("dve", lambda e, xs=xs, a=a: e.tensor_tensor(out=xs, in0=xs, in1=acc[a][:, 0:256], op=ALU.add),
                          reads=[acc_t[a], G["xt"][i]], writes=[G["xt"][i]])
        sc.barrier(release=tiles)


class Ring:
    def __init__(self, P, sc, stack, name, shape, dt, n, psum=False, views=None):
        if views is not None:
            self.h = views
            n = len(views)
        else:
            mk = P.ps if psum else P.sb
            self.h = [mk(stack, "%s%d" % (name, i), shape, dt) for i in range(n)]
        self.t = sc.tiles_n(name + "_", n)
        self.i = 0
        self.n = n

    def next(self):
        k = self.i % self.n
        self.i += 1
        return self.h[k], self.t[k]


def build_tri(P, sc, G, top):
    for nm in ("trif", "trib", "trif64", "trib64", "mcf64", "mcb64", "trifs", "tribs"):
        G[nm] = P.sb(top, nm, [128, 128], F32)
        G[nm + "_t"] = sc.tile(nm)
    ones_f, ones_t = G["ones_f"], G["ones_t"]
    sc.op("pool", lambda e: e.affine_select(out=G["trif"][:], in_=ones_f[:], pattern=[[1, 128]], compare_op=ALU.is_ge,
                                            fill=0.0, base=0, channel_multiplier=-1),
          reads=[ones_t], writes=[G["trif_t"]])
    sc.op("pool", lambda e: e.affine_select(out=G["trib"][:], in_=ones_f[:], pattern=[[-1, 128]], compare_op=ALU.is_ge,
                                            fill=0.0, base=0, channel_multiplier=1),
          reads=[ones_t], writes=[G["trib_t"]])
    sc.op("pool", lambda e: e.affine_select(out=G["trifs"][:], in_=ones_f[:], pattern=[[1, 128]], compare_op=ALU.is_gt,
                                            fill=0.0, base=0, channel_multiplier=-1),
          reads=[ones_t], writes=[G["trifs_t"]])
    sc.op("pool", lambda e: e.affine_select(out=G["tribs"][:], in_=ones_f[:], pattern=[[-1, 128]], compare_op=ALU.is_gt,
                                            fill=0.0, base=0, channel_multiplier=1),
          reads=[ones_t], writes=[G["tribs_t"]])
    sc.op("pool", lambda e: e.tensor_copy(out=G["trif64"][:], in_=G["trif"][:]), reads=[G["trif_t"]], writes=[G["trif64_t"]])
    sc.op("pool", lambda e: e.memset(G["trif64"][0:64, 64:128], 0.0), reads=[G["trif64_t"]], writes=[G["trif64_t"]])
    sc.op("pool", lambda e: e.tensor_copy(out=G["trib64"][:], in_=G["trib"][:]), reads=[G["trib_t"]], writes=[G["trib64_t"]])
    sc.op("pool", lambda e: e.memset(G["trib64"][64:128, 0:64], 0.0), reads=[G["trib64_t"]], writes=[G["trib64_t"]])
    sc.op("pool", lambda e: e.tensor_scalar(out=G["mcf64"][:], in0=G["trif64"][:], scalar1=-1.0 / 16.0, scalar2=None,
                                            op0=ALU.mult), reads=[G["trif64_t"]], writes=[G["mcf64_t"]])
    sc.op("pool", lambda e: e.tensor_scalar(out=G["mcb64"][:], in0=G["trib64"][:], scalar1=-1.0 / 16.0, scalar2=None,
                                            op0=ALU.mult), reads=[G["trib64_t"]], writes=[G["mcb64_t"]])


def phase_C(P, sc, G, U, YB, prm, l):
    nc = P.nc
    ident = G["ident"]
    with contextlib.ExitStack() as ph:
        qT = P.sb(ph, "C_qT", [128, 4, S], BF16)
        kT = P.sb(ph, "C_kT", [128, 4, S], BF16)
        qT_t = sc.tile("C_qT")
        kT_t = sc.tile("C_kT")
        ob = P.sb(ph, "C_ob", [128, NT, 1024], BF16)
        ob_t = sc.tiles_n("C_ob", NT)
        gaX = P.sb(ph, "C_gaX", [32, S], F32)
        gaX_t = sc.tile("C_gaX")
        a2X = [P.sb(ph, "C_a2X%d" % d, [32, 512], F32) for d in range(2)]
        a2X_t = sc.tiles_n("C_a2X", 2)
        nwb = P.sb(ph, "C_nwb", [128, 256], F32)
        nwb_t = sc.tile("C_nwb")
        Sf = P.sb(ph, "C_Sf", [128, 4, 256], F32)
        Sf_t = sc.tile("C_Sf")
        yst = P.sb(ph, "C_yst", [128, 8, 256], BF16)
        yst_t = sc.tile("C_yst")
        R = lambda name, shape, dt, n, psum=False: Ring(P, sc, ph, "C_" + name, shape, dt, n, psum)
        r_Sb = R("Sb", [128, 4, 256], BF16, 3)
        r_v = R("v", [128, 1024], BF16, 2)
        r_gg = R("gg", [128, 4, 1024], BF16, 1)
        r_e1 = R("e1", [128, 512], F32, 1)
        r_bs = R("bs", [128, 4, 128], F32, 1)
        r_eb = R("eb", [128, 4, 128], F32, 1)
        r_enb = R("enb", [128, 4, 128], F32, 1)
        r_ew = R("ew", [128, 4, 128], F32, 1)
        r_ed = R("ed", [128, 4, 2], F32, 3)
        r_qd = R("qd", [128, 4, 128], BF16, 2)
        r_kd = R("kd", [128, 4, 128], BF16, 2)
        r_kw = R("kw", [128, 4, 128], BF16, 1)
        r_kwt = R("kwt", [128, 4, 128], BF16, 2)
        r_am = R("am", [128, 4, 128], BF16, 2)
        r_oa = R("oa", [128, 1024], F32, 1)
        r_sg = R("sg", [128, 4, 1024], BF16, 1)
        r_jk = R("jk", [128, 256], BF16, 1)
        r_ss = R("ss", [128, 8], F32, 2)
        r_y = R("y", [128, 1024], BF16, 2)
        r_gp = R("gp", [128, 512], F32, 1, True)
        r_bT = R("bT", [128, 4, 128], F32, 1, True)
        r_att = R("att", [128, 4, 128], F32, 1, True)
        r_kwp = R("kwp", [128, 4, 128], BF16, 1, True)
        r_st = R("st", [128, 4, 256], F32, 1, True)
        r_o = R("o", [128, 4, 256], F32, 1, True)
        rings = [r_Sb, r_v, r_gg, r_e1, r_bs, r_eb, r_enb, r_ew, r_ed, r_qd, r_kd, r_kw, r_kwt, r_am, r_oa, r_sg,
                 r_jk, r_ss, r_y, r_gp, r_bT, r_att, r_kwp, r_st, r_o]
        tiles = [qT_t, kT_t, nwb_t, yst_t, gaX_t, Sf_t] + ob_t + a2X_t
        for r in rings:
            tiles += r.t
        sc.dma("sp", qT[:], U["gq"].rearrange("(h p) t -> p h t", p=128), owner=qT_t, reads=[G["dram_t"]["gq"]],
               writes=[qT_t])
        sc.dma("sp", kT[:], U["gk"].rearrange("(h p) t -> p h t", p=128), owner=kT_t, reads=[G["dram_t"]["gk"]],
               writes=[kT_t])
        sc.dma("sp", nwb[:], prm["gla_norm_w"][l].partition_broadcast(128), owner=nwb_t, writes=[nwb_t])
        for d in range(2):
            a2 = prm["gla_a2_f" if d == 0 else "gla_a2_b"][l]
            bi = prm["gla_a2_bias_f" if d == 0 else "gla_a2_bias_b"][l]
            sc.dma("sp", a2X[d][0:16, :], a2, owner=a2X_t[d], writes=[a2X_t[d]])
            sc.dma("sp", a2X[d][16:17, :], bi.rearrange("(o n) -> o n", o=1), owner=a2X_t[d], writes=[a2X_t[d]], part=True)

        def gla_pass(d):
            fwd = (d == 0)
            mc, mc_t = (G["mcf64"], G["mcf64_t"]) if fwd else (G["mcb64"], G["mcb64_t"])
            ma, ma_t = (G["trif64"], G["trif64_t"]) if fwd else (G["trib64"], G["trib64_t"])
            lc0 = 63 if fwd else 0
            sc.op("pool", lambda e: e.memset(gaX[:], 1.0), writes=[gaX_t])
            sc.dma("sp", gaX[0:16, :], U["ga"][16 * d:16 * d + 16, :], owner=gaX_t, reads=[G["dram_t"]["ga"]],
                   writes=[gaX_t])
            sc.op("pool", lambda e: e.memset(Sf[:], 0.0), writes=[Sf_t])
            sb0, sb0_t = r_Sb.next()
            sc.op("pool", lambda e, sb0=sb0: e.memset(sb0[:], 0.0), writes=[sb0_t])
            cur = [(sb0, sb0_t)]
            sgcur = [None]
            order = list(range(NT)) if fwd else list(range(NT - 1, -1, -1))
            chunks = (0, 1) if fwd else (1, 0)

            def stage1(i):
                tsl = slice(i * 128, (i + 1) * 128)
                v, v_t = r_v.next()
                sc.dma("sp", v[:], U["gv"][tsl, :], owner=v_t, reads=[G["dram_t"]["gv"]], writes=[v_t])
                gp, gp_t = r_gp.next()
                sc.op("pe", lambda e, gp=gp, tsl=tsl: e.matmul(gp[:], lhsT=gaX[0:17, tsl], rhs=a2X[d][0:17, :],
                                                               start=True, stop=True),
                      reads=[gaX_t, a2X_t[d]], writes=[gp_t])
                e1, e1_t = r_e1.next()
                sc.op("act", lambda e, e1=e1, gp=gp: e.activation(out=e1[:], in_=gp[:], func=AF.Exp, scale=-1.0),
                      reads=[gp_t], writes=[e1_t])
                gn, gn_t = e1, e1_t
                sc.op("act", lambda e, gn=gn, e1=e1: e.activation(out=gn[:], in_=e1[:], func=AF.Ln, bias=G["one"][:, 0:1]),
                      reads=[e1_t, G["one_t"]], writes=[gn_t])
                bT, bT_t = r_bT.next()
                for h in range(4):
                    sc.op("pe", lambda e, bT=bT, gn=gn, h=h: e.matmul(bT[:, h, :], lhsT=gn[:, h * 128:(h + 1) * 128], rhs=mc[:],
                                                                     start=True, stop=True, skip_group_check=True),
                          reads=[gn_t, mc_t], writes=[bT_t], part=(h > 0))
                bs, bs_t = r_bs.next()
                sc.op("act", lambda e, bs=bs, bT=bT: e.copy(out=bs[:], in_=bT[:]), reads=[bT_t], writes=[bs_t])
                eb, eb_t = r_eb.next()
                sc.op("act", lambda e, eb=eb, bs=bs: e.activation(out=eb[:], in_=bs[:], func=AF.Exp), reads=[bs_t], writes=[eb_t])
                enb, enb_t = r_enb.next()
                sc.op("act", lambda e, enb=enb, bs=bs: e.activation(out=enb[:], in_=bs[:], func=AF.Exp, scale=-1.0),
                      reads=[bs_t], writes=[enb_t])
                ed, ed_t = r_ed.next()
                sc.op("act", lambda e, ed=ed, bs=bs: e.activation(
                    out=ed[:], in_=bs[:].rearrange("p h (c l) -> p h c l", c=2)[:, :, :, lc0], func=AF.Exp),
                    reads=[bs_t], writes=[ed_t])
                qd, qd_t = r_qd.next()
                sc.op("dve", lambda e, qd=qd, tsl=tsl, eb=eb: e.scalar_tensor_tensor(
                    out=qd[:], in0=qT[:, :, tsl], scalar=128.0 ** -0.5, in1=eb[:], op0=ALU.mult, op1=ALU.mult),
                    reads=[qT_t, eb_t], writes=[qd_t])
                kd, kd_t = r_kd.next()
                sc.op("dve", lambda e, kd=kd, tsl=tsl, enb=enb: e.tensor_tensor(
                    out=kd[:], in0=kT[:, :, tsl], in1=enb[:], op=ALU.mult), reads=[kT_t, enb_t], writes=[kd_t])
                ew, ew_t = r_ew.next()
                sc.op("dve", lambda e, ew=ew, enb=enb, ed=ed: e.tensor_tensor(
                    out=ew[:].rearrange("p h (c l) -> p (h c) l", c=2), in0=enb[:].rearrange("p h (c l) -> p (h c) l", c=2),
                    in1=ed[:].rearrange("p h c -> p (h c)").unsqueeze(2).to_broadcast([128, 8, 64]), op=ALU.mult),
                    reads=[enb_t, ed_t], writes=[ew_t])
                kw, kw_t = r_kw.next()
                sc.op("dve", lambda e, kw=kw, tsl=tsl, ew=ew: e.tensor_tensor(
                    out=kw[:], in0=kT[:, :, tsl], in1=ew[:], op=ALU.mult), reads=[kT_t, ew_t], writes=[kw_t])
                kwp, kwp_t = r_kwp.next()
                for h in range(4):
                    sc.op("pe", lambda e, kwp=kwp, kw=kw, h=h: e.transpose(out=kwp[:, h, :], in_=kw[:, h, :], identity=ident[:]),
                          reads=[kw_t, G["ident_t"]], writes=[kwp_t], part=(h > 0))
                kwt, kwt_t = r_kwt.next()
                sc.op("act", lambda e, kwt=kwt, kwp=kwp: e.copy(out=kwt[:], in_=kwp[:]), reads=[kwp_t], writes=[kwt_t])
                att, att_t = r_att.next()
                for h in range(4):
                    sc.op("pe", lambda e, att=att, kd=kd, qd=qd, h=h: e.matmul(att[:, h, :], lhsT=kd[:, h, :], rhs=qd[:, h, :],
                                                                            start=True, stop=True, skip_group_check=True),
                          reads=[kd_t, qd_t], writes=[att_t], part=(h > 0))
                am, am_t = r_am.next()
                sc.op("dve", lambda e, am=am, att=att: e.tensor_tensor(
                    out=am[:], in0=att[:], in1=ma[:].unsqueeze(1).to_broadcast([128, 4, 128]), op=ALU.mult),
                    reads=[att_t, ma_t], writes=[am_t])
                return (i, tsl, v, v_t, qd, qd_t, kwt, kwt_t, ed, ed_t, am, am_t)

            def stage23(ctx):
                (i, tsl, v, v_t, qd, qd_t, kwt, kwt_t, ed, ed_t, am, am_t) = ctx
                sbs = [cur[0]]
                for ci, c in enumerate(chunks):
                    cs = slice(c * 64, (c + 1) * 64)
                    st, st_t = r_st.next()
                    for h in range(4):
                        sc.op("pe", lambda e, st=st, kwt=kwt, cs=cs, v=v, h=h: e.matmul(
                            st[:, h, :], lhsT=kwt[cs, h, :], rhs=v[cs, h * 256:(h + 1) * 256], start=True, stop=True,
                            skip_group_check=True),
                            reads=[kwt_t, v_t], writes=[st_t], part=(h > 0))
                    for h in range(4):
                        sc.op("dve", lambda e, st=st, h=h, ed=ed, c=c: e.scalar_tensor_tensor(
                            out=Sf[:, h, :], in0=Sf[:, h, :], scalar=ed[:, h, c:c + 1], in1=st[:, h, :], op0=ALU.mult,
                            op1=ALU.add),
                            reads=[st_t, ed_t, Sf_t], writes=[Sf_t])
                    nb, nb_t = r_Sb.next()
                    sc.op("act", lambda e, nb=nb: e.copy(out=nb[:], in_=Sf[:]), reads=[Sf_t], writes=[nb_t])
                    sbs.append((nb, nb_t))
                o, o_t = r_o.next()
                for h in range(4):
                    sc.op("pe", lambda e, o=o, am=am, v=v, h=h: e.matmul(o[:, h, :], lhsT=am[:, h, :],
                                                                       rhs=v[:, h * 256:(h + 1) * 256],
                                                                       start=True, stop=False, skip_group_check=True),
                          reads=[am_t, v_t], writes=[o_t], part=(h > 0))
                    for ci, c in enumerate(chunks):
                        cs = slice(c * 64, (c + 1) * 64)
                        sbv, sbv_t = sbs[ci]
                        sc.op("pe", lambda e, o=o, qd=qd, cs=cs, sbv=sbv, ci=ci, h=h: e.matmul(
                            o[cs, h, :], lhsT=qd[:, h, cs], rhs=sbv[:, h, :], start=False, stop=(ci == 1),
                            skip_group_check=True),
                            reads=[qd_t, sbv_t], writes=[o_t], part=True)
                cur[0] = sbs[2]
                if not fwd:
                    for hb in range(2):
                        sc.op("act", lambda e, o=o, hb=hb, i=i: e.copy(
                            out=ob[:, i, hb * 512:(hb + 1) * 512], in_=o[:, 2 * hb:2 * hb + 2, :].rearrange("p a b -> p (a b)")),
                            reads=[o_t], writes=[ob_t[i]], part=(hb > 0))
                    return
                oa, oa_t = r_oa.next()
                ss, ss_t = r_ss.next()
                for hb in range(2):
                    sc.op("dve", lambda e, oa=oa, o=o, hb=hb, i=i: e.tensor_tensor(
                        out=oa[:, hb * 512:(hb + 1) * 512], in0=o[:, 2 * hb:2 * hb + 2, :].rearrange("p a b -> p (a b)"),
                        in1=ob[:, i, hb * 512:(hb + 1) * 512], op=ALU.add),
                        reads=[o_t, ob_t[i]], writes=[oa_t], part=(hb > 0))
                for h in range(4):
                    hs = slice(h * 256, (h + 1) * 256)
                    jk, jk_t = r_jk.next()
                    sc.op("dve", lambda e, jk=jk, oa=oa, hs=hs, ss=ss, h=h: e.scalar_tensor_tensor(
                        out=jk[:], in0=oa[:, hs], scalar=1.0, in1=oa[:, hs], op0=ALU.mult, op1=ALU.mult,
                        accum_out=ss[:, h:h + 1]), reads=[oa_t], writes=[jk_t, ss_t])
                if i % 4 == 0:
                    gg, gg_t = r_gg.next()
                    sc.dma("sp", gg[:], U["gg"][i * 128:(i + 4) * 128, :].rearrange("(j p) c -> p j c", p=128), owner=gg_t,
                           reads=[G["dram_t"]["gg"]], writes=[gg_t])
                    sg, sg_t = r_sg.next()
                    sc.op("act", lambda e, sg=sg, gg=gg: e.activation(out=sg[:], in_=gg[:], func=AF.Silu),
                          reads=[gg_t], writes=[sg_t])
                    sgcur[0] = (sg, sg_t)
                sg, sg_t = sgcur[0]
                sgn = sg[:, i % 4, :]
                sgn_t = sg_t
                sc.op("pool", lambda e, sgn=sgn: e.tensor_tensor(
                    out=sgn.rearrange("p (h v) -> p h v", h=4), in0=sgn.rearrange("p (h v) -> p h v", h=4),
                    in1=nwb[:].unsqueeze(1).to_broadcast([128, 4, 256]), op=ALU.mult),
                    reads=[sg_t, nwb_t], writes=[sg_t])
                sc.op("dve", lambda e, ss=ss: e.tensor_scalar(out=ss[:, 4:8], in0=ss[:, 0:4], scalar1=1.0 / 256.0, scalar2=EPS,
                                                              op0=ALU.mult, op1=ALU.add), reads=[ss_t], writes=[ss_t])
                sc.op("pool", lambda e, ss=ss: e.tensor_tensor(out=ss[:, 0:4], in0=ss[:, 4:8], in1=G["neghalf"][:, 0:4],
                                                               op=ALU.pow), reads=[ss_t, G["neghalf_t"]], writes=[ss_t])
                sc.op("dve", lambda e, oa=oa, ss=ss: e.tensor_tensor(
                    out=oa[:].rearrange("p (h v) -> p h v", h=4), in0=oa[:].rearrange("p (h v) -> p h v", h=4),
                    in1=ss[:, 0:4].unsqueeze(2).to_broadcast([128, 4, 256]), op=ALU.mult),
                    reads=[oa_t, ss_t], writes=[oa_t])
                y, y_t = r_y.next()
                sc.op("dve", lambda e, y=y, oa=oa, sgn=sgn: e.tensor_tensor(out=y[:], in0=oa[:], in1=sgn, op=ALU.mult),
                      reads=[oa_t, sgn_t], writes=[y_t])
                return (y, y_t, i)

            def stage3(c3):
                if c3 is None:
                    return
                (y, y_t, i) = c3
                emit_yT(P, sc, G, r_kwp, y, y_t, yst, yst_t, i, YB["gla"], G["dram_t"]["yb_gla"], gsz=2)

            prev = None
            prev3 = None
            for i in order:
                ctx = stage1(i)
                if prev is not None:
                    n3 = stage23(prev)
                    stage3(prev3)
                    prev3 = n3
                prev = ctx
            n3 = stage23(prev)
            stage3(prev3)
            stage3(n3)

        gla_pass(1)
        if "dbg_ob" in P.dbg:
            dob = P.dram("dbg_ob", [S, 1024], BF16)
            dt_ = sc.tile("dbg_ob")
            sc.dma("sp", dob.rearrange("(i p) c -> p i c", p=128), ob[:], owner=ob_t[0], reads=ob_t, writes=[dt_])
        gla_pass(0)
        sc.barrier(release=tiles)


def phase_B(P, sc, G, U, YB, prm, l):
    nc = P.nc
    ident = G["ident"]
    ybw = G["ybw"]
    ybw_t = G["dram_t"]["ybw"]
    with contextlib.ExitStack() as ph:
        xtok = P.sb(ph, "B_xtok", [128, NT, 1280], BF16)
        xtok_t = sc.tiles_n("B_xtok", NT)
        BT = P.sb(ph, "B_BT", [128, 2, S], BF16)
        CT = P.sb(ph, "B_CT", [128, 2, S], BF16)
        BT_t = sc.tiles_n("B_BT", 2)
        CT_t = sc.tiles_n("B_CT", 2)
        dtv = P.sb(ph, "B_dtv", [128, NT, 32], F32)
        av = P.sb(ph, "B_av", [128, NT, 32], F32)
        dtv_t = sc.tile("B_dtv")
        av_t = sc.tile("B_av")
        rows = P.sb(ph, "B_rows", [128, 4, 32], F32)
        rows_t = sc.tile("B_rows")
        nwb = P.sb(ph, "B_nwb", [128, 1024], F32)
        nwb_t = sc.tile("B_nwb")
        tiles = xtok_t + BT_t + CT_t + [dtv_t, av_t, rows_t, nwb_t]
        with contextlib.ExitStack() as s1:
            cwr = P.sb(s1, "B_cwr", [72, 128], F32)
            cwr_t = sc.tile("B_cwr")
            cw = P.sb(s1, "B_cw", [128, 72], F32)
            cw_t = sc.tile("B_cw")
            cwp = P.ps(s1, "B_cwp", [128, 72], F32)
            cwp_t = sc.tile("B_cwp")
            identf = P.sb(s1, "B_identf", [128, 128], F32)
            identf_t = sc.tile("B_identf")
            xc = [P.sb(s1, "B_xc%d" % i, [128, S + 4], BF16) for i in range(2)]
            xc_t = sc.tiles_n("B_xc", 2)
            t1 = P.sb(s1, "B_t1", [128, S], F32)
            t1_t = sc.tile("B_t1")
            xa = [P.sb(s1, "B_xa%d" % i, [128, S], BF16) for i in range(2)]
            xa_t = sc.tiles_n("B_xa", 2)
            tp = [P.ps(s1, "B_tp%d" % i, [128, 4, 128], BF16) for i in range(2)]
            tp_t = sc.tiles_n("B_tp", 2)
            tl1 = [cwr_t, cw_t, cwp_t, identf_t, t1_t] + xc_t + xa_t + tp_t
            sc.op("pool", lambda e: e.affine_select(out=identf[:], in_=G["ones_f"][:], pattern=[[-1, 128]],
                                                    compare_op=ALU.is_equal, fill=0.0, base=0, channel_multiplier=1),
                  reads=[G["ones_t"]], writes=[identf_t])
            sc.dma("sp", cwr[0:60, :], prm["ssd_conv_w"][l].rearrange("k (c p) -> (k c) p", p=128), owner=cwr_t, writes=[cwr_t])
            sc.dma("sp", cwr[60:72, :], prm["ssd_conv_b"][l].rearrange("(c p) -> c p", p=128), owner=cwr_t, writes=[cwr_t],
                   part=True)
            sc.op("pe", lambda e: e.transpose(out=cwp[:], in_=cwr[:], identity=identf[0:72, 0:72]),
                  reads=[cwr_t, identf_t], writes=[cwp_t])
            sc.op("act", lambda e: e.copy(out=cw[:], in_=cwp[:]), reads=[cwp_t], writes=[cw_t])
            for b in range(2):
                sc.op("pool", lambda e, b=b: e.memset(xc[b][:, 0:2], 0.0), writes=[xc_t[b]])
                sc.op("pool", lambda e, b=b: e.memset(xc[b][:, S + 2:S + 4], 0.0), writes=[xc_t[b]], part=True)
            sc.dma("sp", dtv[:], U["dt"].rearrange("(i p) c -> p i c", p=128), owner=dtv_t, reads=[G["dram_t"]["dt"]],
                   writes=[dtv_t])
            for k, nm in enumerate(("ssd_dt_bias_f", "ssd_dt_bias_b")):
                sc.dma("sp", rows[:, 0, 16 * k:16 * k + 16], prm[nm][l].partition_broadcast(128), owner=rows_t,
                       writes=[rows_t], part=True)
            for k, nm in enumerate(("ssd_a_log_f", "ssd_a_log_b")):
                sc.dma("sp", rows[:, 1, 16 * k:16 * k + 16], prm[nm][l].partition_broadcast(128), owner=rows_t,
                       writes=[rows_t], part=True)
            sc.dma("sp", rows[:, 2, 0:16], prm["ssd_d"][l].partition_broadcast(128), owner=rows_t, writes=[rows_t], part=True)
            sc.dma("sp", nwb[:], prm["ssd_norm_w"][l].partition_broadcast(128), owner=nwb_t, writes=[nwb_t])
            sc.op("dve", lambda e: e.tensor_tensor(out=dtv[:], in0=dtv[:], in1=rows[:, 0:1, :].to_broadcast([128, NT, 32]),
                                                   op=ALU.add), reads=[dtv_t, rows_t], writes=[dtv_t])
            sc.op("act", lambda e: e.activation(out=dtv[:], in_=dtv[:], func=AF.Exp), reads=[dtv_t], writes=[dtv_t])
            sc.op("act", lambda e: e.activation(out=dtv[:], in_=dtv[:], func=AF.Ln, bias=G["one"][:, 0:1]),
                  reads=[dtv_t, G["one_t"]], writes=[dtv_t])
            sc.op("act", lambda e: e.activation(out=rows[:, 3, :], in_=rows[:, 1, :], func=AF.Exp), reads=[rows_t],
                  writes=[rows_t])
            sc.op("dve", lambda e: e.scalar_tensor_tensor(out=av[:], in0=dtv[:], scalar=-1.0,
                                                          in1=rows[:, 3:4, :].to_broadcast([128, NT, 32]),
                                                          op0=ALU.mult, op1=ALU.mult),
                  reads=[dtv_t, rows_t], writes=[av_t])
            tpc = 0
            for c in range(12):
                b = c % 2
                sc.dma("sp", xc[b][:, 2:S + 2], U["xbc"][c * 128:(c + 1) * 128, :], owner=xc_t[b],
                       reads=[G["dram_t"]["xbc"]], writes=[xc_t[b]], part=True)
                sc.op("dve", lambda e, b=b, c=c: e.tensor_scalar(out=t1[:], in0=xc[b][:, 0:S], scalar1=cw[:, c:c + 1],
                                                                 scalar2=cw[:, 60 + c:61 + c], op0=ALU.mult, op1=ALU.add),
                      reads=[xc_t[b], cw_t], writes=[t1_t])
                for k in range(1, 5):
                    sc.op("dve", lambda e, b=b, c=c, k=k: e.scalar_tensor_tensor(
                        out=t1[:], in0=xc[b][:, k:k + S], scalar=cw[:, k * 12 + c:k * 12 + c + 1], in1=t1[:],
                        op0=ALU.mult, op1=ALU.add), reads=[xc_t[b], cw_t, t1_t], writes=[t1_t])
                if c < 10:
                    xo, xo_t = xa[b][:], xa_t[b]
                elif c < 12 and c >= 10:
                    xo, xo_t = CT[:, c - 10, :], CT_t[c - 10]
                sc.op("act", lambda e, xo=xo: e.activation(out=xo, in_=t1[:], func=AF.Silu), reads=[t1_t], writes=[xo_t])
                if c in (8, 9):
                    sc.op("pool", lambda e, b=b, c=c: e.tensor_copy(out=BT[:, c - 8, :], in_=xa[b][:]),
                          reads=[xa_t[b]], writes=[BT_t[c - 8]])
                if c < 10:
                    for i0 in range(0, NT, 4):
                        tb_ = tpc % 2
                        tpc += 1
                        for j in range(4):
                            i = i0 + j
                            sc.op("pe", lambda e, tb_=tb_, j=j, b=b, i=i: e.transpose(
                                out=tp[tb_][:, j, :], in_=xa[b][:, i * 128:(i + 1) * 128], identity=ident[:]),
                                reads=[xa_t[b], G["ident_t"]], writes=[tp_t[tb_]], part=(j > 0))
                        eng = "act" if (tpc % 2) else "pool"
                        if eng == "act":
                            sc.op("act", lambda e, tb_=tb_, i0=i0, c=c: e.copy(
                                out=xtok[:, i0:i0 + 4, c * 128:(c + 1) * 128], in_=tp[tb_][:]),
                                reads=[tp_t[tb_]], writes=xtok_t[i0:i0 + 4], part=True)
                        else:
                            sc.op("dve", lambda e, tb_=tb_, i0=i0, c=c: e.tensor_copy(
                                out=xtok[:, i0:i0 + 4, c * 128:(c + 1) * 128], in_=tp[tb_][:]),
                                reads=[tp_t[tb_]], writes=xtok_t[i0:i0 + 4], part=True)
            sc.barrier(release=tl1)
        with contextlib.ExitStack() as s2:
            R = lambda name, shape, dt, n, psum=False: Ring(P, sc, s2, "B_" + name, shape, dt, n, psum)
            Sf = P.sb(s2, "B_Sf", [128, 2, 512], F32)
            Sf_t = sc.tiles_n("B_Sf", 2)
            Sbx = P.sb(s2, "B_Sb", [128, 2, 2, 512], BF16)
            r_Sb = [Ring(P, sc, s2, "B_Sb%d" % g, None, None, 2, views=[Sbx[:, g, k, :] for k in range(2)]) for g in range(2)]
            r_cb = R("cb", [128, 128], F32, 1, True)
            r_seg = R("seg", [128, 512], F32, 2, True)
            r_sm = R("sm", [128, 3, 16], F32, 1, True)
            r_yd = R("yd", [128, 512], F32, 1, True)
            r_stp = R("stp", [128, 512], F32, 1, True)
            r_yo = R("yo", [128, 512], F32, 1, True)
            r_tp = R("tp2", [128, 4, 128], BF16, 1, True)
            r_cbm = R("cbm", [128, 128], F32, 2)
            r_am = R("am", [128, 4, 128], F32, 4)
            r_dec = R("dec", [128, 4, 128], F32, 2)
            r_mt = R("mt", [128, 4, 128], BF16, 4)
            r_ea = R("ea", [128, 3, 16], F32, 2)
            r_xdt = R("xdt", [128, 1024], BF16, 1)
            r_xw = R("xw", [128, 1024], BF16, 1)
            r_t = R("t", [128, 512], F32, 2)
            r_ybl = R("ybl", [128, 1024], BF16, 2)
            r_yf = R("yf", [128, 1024], F32, 1)
            r_z = R("z", [128, 4, 1024], BF16, 1)
            r_jk = R("jk", [128, 512], F32, 1)
            r_ss = R("ss", [128, 4], F32, 2)
            r_y = R("y", [128, 1024], BF16, 2)
            yst = P.sb(s2, "B_yst", [128, 8, 256], BF16)
            yst_t = sc.tile("B_yst")
            rings = [r_cb, r_seg, r_sm, r_yd, r_stp, r_yo, r_tp, r_cbm, r_am, r_dec, r_mt, r_ea, r_xdt, r_xw, r_t, r_ybl,
                     r_yf, r_z, r_jk, r_ss, r_y] + r_Sb
            tl2 = Sf_t + [yst_t]
            for r in rings:
                tl2 += r.t

            def ssd_pass(d):
                fwd = (d == 0)
                tri_in, tri_in_t = (G["trif"], G["trif_t"]) if fwd else (G["trib"], G["trib_t"])
                tri_st, tri_st_t = (G["tribs"], G["tribs_t"]) if fwd else (G["trifs"], G["trifs_t"])
                cur = []
                for g in range(2):
                    sc.op("pool", lambda e, g=g: e.memset(Sf[:, g, :], 0.0), writes=[Sf_t[g]])
                    sb0, sb0_t = r_Sb[g].next()
                    sc.op("pool", lambda e, sb0=sb0: e.memset(sb0, 0.0), writes=[sb0_t])
                    cur.append((sb0, sb0_t))
                order = list(range(NT)) if fwd else list(range(NT - 1, -1, -1))
                zcur = [None]

                def tileA(i):
                    tsl = slice(i * 128, (i + 1) * 128)
                    acol = av[:, i, 16 * d:16 * d + 16]
                    ams = []
                    for u in range(4):
                        h0 = u * 4
                        am, am_t = r_am.next()
                        for hh in range(4):
                            sc.op("act", lambda e, am=am, i=i, h0=h0, hh=hh: e.activation(
                                out=am[:, hh, :], in_=tri_in[:], func=AF.Copy,
                                scale=av[:, i, 16 * d + h0 + hh:16 * d + h0 + hh + 1]),
                                reads=[tri_in_t, av_t], writes=[am_t], part=(hh > 0))
                        ams.append((am, am_t))
                    sm, sm_t = r_sm.next()
                    sc.op("pe", lambda e, sm=sm, acol=acol: e.matmul(sm[:, 0, :], lhsT=tri_in[:], rhs=acol, start=True, stop=True),
                          reads=[tri_in_t, av_t], writes=[sm_t])
                    sc.op("pe", lambda e, sm=sm, acol=acol: e.matmul(sm[:, 1, :], lhsT=tri_st[:], rhs=acol, start=True, stop=True),
                          reads=[tri_st_t, av_t], writes=[sm_t], part=True)
                    sc.op("pe", lambda e, sm=sm, acol=acol: e.matmul(sm[:, 2, :], lhsT=G["ones_f"][:], rhs=acol, start=True,
                                                                     stop=True),
                          reads=[G["ones_t"], av_t], writes=[sm_t], part=True)
                    ea, ea_t = r_ea.next()
                    sc.op("act", lambda e, ea=ea, sm=sm: e.activation(out=ea[:], in_=sm[:], func=AF.Exp), reads=[sm_t],
                          writes=[ea_t])
                    xdt, xdt_t = r_xdt.next()
                    sc.op("dve", lambda e, xdt=xdt, i=i: e.tensor_tensor(
                        out=xdt[:].rearrange("p (h q) -> p h q", q=64), in0=xtok[:, i, 0:1024].rearrange("p (h q) -> p h q", q=64),
                        in1=dtv[:, i, 16 * d:16 * d + 16].unsqueeze(2).to_broadcast([128, 16, 64]), op=ALU.mult),
                        reads=[xtok_t[i], dtv_t], writes=[xdt_t])
                    xw, xw_t = r_xw.next()
                    sc.op("dve", lambda e, xw=xw, xdt=xdt, ea=ea: e.tensor_tensor(
                        out=xw[:].rearrange("p (h q) -> p h q", q=64), in0=xdt[:].rearrange("p (h q) -> p h q", q=64),
                        in1=ea[:, 1, :].unsqueeze(2).to_broadcast([128, 16, 64]), op=ALU.mult),
                        reads=[xdt_t, ea_t], writes=[xw_t])
                    ybl, ybl_t = r_ybl.next()
                    if fwd:
                        sc.dma("sp", ybl[:], ybw[tsl, :], owner=ybl_t, reads=[ybw_t], writes=[ybl_t])
                        yf, yf_t = r_yf.next()
                    ts = []
                    for g in range(2):
                        stp, stp_t = r_stp.next()
                        sc.op("pe", lambda e, stp=stp, i=i, g=g, xw=xw: e.matmul(
                            stp[:], lhsT=xtok[:, i, 1024 + g * 128:1024 + (g + 1) * 128], rhs=xw[:, g * 512:(g + 1) * 512],
                            start=True, stop=True), reads=[xtok_t[i], xw_t], writes=[stp_t])
                        yo, yo_t = r_yo.next()
                        sbv, sbv_t = cur[g]
                        sc.op("pe", lambda e, yo=yo, g=g, tsl=tsl, sbv=sbv: e.matmul(yo[:], lhsT=CT[:, g, tsl], rhs=sbv,
                                                                                    start=True, stop=True),
                              reads=[CT_t[g], sbv_t], writes=[yo_t])
                        sc.op("pool", lambda e, g=g, ea=ea: e.tensor_tensor(
                            out=Sf[:, g, :].rearrange("p (h q) -> p h q", q=64), in0=Sf[:, g, :].rearrange("p (h q) -> p h q", q=64),
                            in1=ea[:, 2, g * 8:(g + 1) * 8].unsqueeze(2).to_broadcast([128, 8, 64]), op=ALU.mult),
                            reads=[Sf_t[g], ea_t], writes=[Sf_t[g]])
                        sc.op("dve", lambda e, g=g, stp=stp: e.tensor_tensor(out=Sf[:, g, :], in0=Sf[:, g, :], in1=stp[:],
                                                                            op=ALU.add),
                              reads=[Sf_t[g], stp_t], writes=[Sf_t[g]])
                        nb, nb_t = r_Sb[g].next()
                        sc.op("act", lambda e, nb=nb, g=g: e.copy(out=nb, in_=Sf[:, g, :]), reads=[Sf_t[g]], writes=[nb_t])
                        cur[g] = (nb, nb_t)
                        t, t_t = r_t.next()
                        sc.op("dve", lambda e, t=t, yo=yo, ea=ea, g=g: e.tensor_tensor(
                            out=t[:].rearrange("p (h q) -> p h q", q=64), in0=yo[:].rearrange("p (h q) -> p h q", q=64),
                            in1=ea[:, 0, g * 8:(g + 1) * 8].unsqueeze(2).to_broadcast([128, 8, 64]), op=ALU.mult),
                            reads=[yo_t, ea_t], writes=[t_t])
                        ts.append((t, t_t))
                    cbms = []
                    for g in range(2):
                        cb, cb_t = r_cb.next()
                        sc.op("pe", lambda e, cb=cb, g=g, tsl=tsl: e.matmul(cb[:], lhsT=BT[:, g, tsl], rhs=CT[:, g, tsl],
                                                                           start=True, stop=True),
                              reads=[BT_t[g], CT_t[g]], writes=[cb_t])
                        cbm, cbm_t = r_cbm.next()
                        sc.op("dve", lambda e, cbm=cbm, cb=cb: e.tensor_tensor(out=cbm[:], in0=cb[:], in1=tri_in[:], op=ALU.mult),
                              reads=[cb_t, tri_in_t], writes=[cbm_t])
                        cbms.append((cbm, cbm_t))
                    mts = []
                    for pair in range(2):
                        segs = []
                        for u in (2 * pair, 2 * pair + 1):
                            am, am_t = ams[u]
                            seg, seg_t = r_seg.next()
                            sc.op("pe", lambda e, seg=seg, am=am: e.matmul(seg[:], lhsT=tri_st[:],
                                                                           rhs=am[:].rearrange("p a b -> p (a b)"),
                                                                           start=True, stop=True),
                                  reads=[tri_st_t, am_t], writes=[seg_t])
                            segs.append((seg, seg_t))
                        decs = []
                        for (seg, seg_t) in segs:
                            dec, dec_t = r_dec.next()
                            sc.op("act", lambda e, dec=dec, seg=seg: e.activation(out=dec[:].rearrange("p a b -> p (a b)"),
                                                                                 in_=seg[:], func=AF.Exp),
                                  reads=[seg_t], writes=[dec_t])
                            decs.append((dec, dec_t))
                        for k, (dec, dec_t) in enumerate(decs):
                            u = 2 * pair + k
                            cbm, cbm_t = cbms[u // 2]
                            mt, mt_t = r_mt.next()
                            sc.op("dve", lambda e, mt=mt, dec=dec, cbm=cbm: e.tensor_tensor(
                                out=mt[:], in0=dec[:], in1=cbm[:].unsqueeze(1).to_broadcast([128, 4, 128]), op=ALU.mult),
                                reads=[dec_t, cbm_t], writes=[mt_t])
                            mts.append((mt, mt_t))
                    for g in range(2):
                        yd, yd_t = r_yd.next()
                        if fwd:
                            sc.op("pe", lambda e, yd=yd, ybl=ybl, g=g: e.matmul(
                                yd[:], lhsT=ident[:], rhs=ybl[:, g * 512:(g + 1) * 512], start=True, stop=False,
                                skip_group_check=True), reads=[G["ident_t"], ybl_t], writes=[yd_t])
                        for q4 in range(2):
                            mt, mt_t = mts[g * 2 + q4]
                            for hh in range(4):
                                h = g * 8 + q4 * 4 + hh
                                hl = h - g * 8
                                sc.op("pe", lambda e, yd=yd, mt=mt, hh=hh, hl=hl, h=h, xdt=xdt: e.matmul(
                                    yd[:, hl * 64:(hl + 1) * 64], lhsT=mt[:, hh, :], rhs=xdt[:, h * 64:(h + 1) * 64],
                                    start=(not fwd), stop=True, skip_group_check=True),
                                    reads=[mt_t, xdt_t], writes=[yd_t], part=(fwd or not (q4 == 0 and hh == 0)))
                        t, t_t = ts[g]
                        gs = slice(g * 512, (g + 1) * 512)
                        if not fwd:
                            sc.op("dve", lambda e, t=t, yd=yd, ybl=ybl, gs=gs: e.tensor_tensor(out=ybl[:, gs], in0=t[:], in1=yd[:],
                                                                                              op=ALU.add),
                                  reads=[t_t, yd_t], writes=[ybl_t], part=(g > 0))
                        else:
                            sc.op("dve", lambda e, t=t, yd=yd, yf=yf, gs=gs: e.tensor_tensor(out=yf[:, gs], in0=t[:], in1=yd[:],
                                                                                            op=ALU.add),
                                  reads=[t_t, yd_t], writes=[yf_t], part=(g > 0))
                    if not fwd:
                        sc.dma("pool", ybw[tsl, :], ybl[:], owner=ybl_t, reads=[ybl_t], writes=[ybw_t], part=True)
                        return None
                    if i % 4 == 0:
                        z, z_t = r_z.next()
                        sc.dma("sp", z[:], U["z"][i * 128:(i + 4) * 128, :].rearrange("(j p) c -> p j c", p=128), owner=z_t,
                               reads=[G["dram_t"]["z"]], writes=[z_t])
                        sc.op("act", lambda e, z=z: e.activation(out=z[:], in_=z[:], func=AF.Silu), reads=[z_t], writes=[z_t])
                        zcur[0] = (z, z_t)
                    z, z_t = zcur[0]
                    sz = z[:, i % 4, :]
                    sz_t = z_t
                    xd, xd_t = r_xdt.next()
                    sc.op("pool", lambda e, xd=xd, i=i: e.tensor_tensor(
                        out=xd[:].rearrange("p (h q) -> p h q", q=64), in0=xtok[:, i, 0:1024].rearrange("p (h q) -> p h q", q=64),
                        in1=rows[:, 2, 0:16].unsqueeze(2).to_broadcast([128, 16, 64]), op=ALU.mult),
                        reads=[xtok_t[i], rows_t], writes=[xd_t])
                    sc.op("dve", lambda e, yf=yf, xd=xd: e.tensor_tensor(out=yf[:], in0=yf[:], in1=xd[:], op=ALU.add),
                          reads=[yf_t, xd_t], writes=[yf_t])
                    sc.op("dve", lambda e, yf=yf, sz=sz: e.tensor_tensor(out=yf[:], in0=yf[:], in1=sz, op=ALU.mult),
                          reads=[yf_t, sz_t], writes=[yf_t])
                    ss, ss_t = r_ss.next()
                    for g in range(2):
                        gs = slice(g * 512, (g + 1) * 512)
                        jk, jk_t = r_jk.next()
                        sc.op("dve", lambda e, jk=jk, yf=yf, gs=gs, ss=ss, g=g: e.scalar_tensor_tensor(
                            out=jk[:], in0=yf[:, gs], scalar=1.0, in1=yf[:, gs], op0=ALU.mult, op1=ALU.mult,
                            accum_out=ss[:, g:g + 1]), reads=[yf_t], writes=[jk_t, ss_t])
                    sc.op("dve", lambda e, ss=ss: e.tensor_scalar(out=ss[:, 2:4], in0=ss[:, 0:2], scalar1=1.0 / 512.0, scalar2=EPS,
                                                                  op0=ALU.mult, op1=ALU.add), reads=[ss_t], writes=[ss_t])
                    sc.op("pool", lambda e, ss=ss: e.tensor_tensor(out=ss[:, 0:2], in0=ss[:, 2:4], in1=G["neghalf"][:, 0:2],
                                                                   op=ALU.pow), reads=[ss_t, G["neghalf_t"]], writes=[ss_t])
                    y, y_t = r_y.next()
                    for g in range(2):
                        gs = slice(g * 512, (g + 1) * 512)
                        sc.op("dve", lambda e, y=y, yf=yf, gs=gs, ss=ss, g=g: e.scalar_tensor_tensor(
                            out=y[:, gs], in0=yf[:, gs], scalar=ss[:, g:g + 1], in1=nwb[:, gs], op0=ALU.mult, op1=ALU.mult),
                            reads=[yf_t, ss_t, nwb_t], writes=[y_t], part=(g > 0))
                    return (y, y_t, i)

                def tileC(c3):
                    if c3 is None:
                        return
                    (y, y_t, i) = c3
                    emit_yT(P, sc, G, r_tp, y, y_t, yst, yst_t, i, YB["ssd"], G["dram_t"]["yb_ssd"], gsz=2)

                prev3 = None
                for i in order:
                    n3 = tileA(i)
                    tileC(prev3)
                    prev3 = n3
                tileC(prev3)

            ssd_pass(1)
            ssd_pass(0)
            sc.barrier(release=tl2)
        sc.barrier(release=tiles)


def emit_yT(P, sc, G, r_tp, y, y_t, yst, yst_t, i, dst, dst_t, gsz=4):
    ident = G["ident"]
    for half in range(2):
        tp, tp_t = r_tp.next()
        for jq in range(4):
            c = half * 4 + jq
            sc.op("pe", lambda e, tp=tp, jq=jq, c=c: e.transpose(out=tp[:, jq, :], in_=y[:, c * 128:(c + 1) * 128],
                                                               identity=ident[:]),
                  reads=[y_t, G["ident_t"]], writes=[tp_t], part=(jq > 0))
        sc.op("act", lambda e, tp=tp, half=half: e.copy(
            out=yst[:, half * 4:half * 4 + 4, (i % gsz) * 128:(i % gsz + 1) * 128], in_=tp[:]),
            reads=[tp_t], writes=[yst_t], part=not (i % gsz == 0 and half == 0))
    if i % gsz == gsz - 1:
        yv = dst.rearrange("(c p) t -> p c t", p=128)
        sc.dma("pool", yv[:, :, (i - gsz + 1) * 128:(i + 1) * 128], yst[:], owner=yst_t, reads=[yst_t], writes=[dst_t],
               part=True)


NEG = -30000.0


def na_r0(r):
    return min(max(r - 4, 0), 24)


def na_valid(kr, qr):
    return na_r0(qr) <= kr < na_r0(qr) + 8


def phase_D(P, sc, G, U, YB, prm, natt, l):
    nc = P.nc
    with contextlib.ExitStack() as ph:
        qnT = P.sb(ph, "D_qnT", [128, 8, S], BF16)
        knT = P.sb(ph, "D_knT", [128, 8, S], BF16)
        qn_t = sc.tiles_n("D_qn", 8)
        kn_t = sc.tiles_n("D_kn", 8)
        TT = P.sb(ph, "D_TT", [128, 8, 17, 64], BF16)
        TT_t = sc.tiles_n("D_TT", 4)
        wcol = P.sb(ph, "D_wcol", [128, 4], F32)
        wcol_t = sc.tile("D_wcol")
        tiles = qn_t + kn_t + TT_t + [wcol_t]
        with contextlib.ExitStack() as s1:
            TTf = [P.sb(s1, "D_TTf%d" % i, [128, 2, 17, 64], F32) for i in range(2)]
            TTf_t = sc.tiles_n("D_TTf", 2)
            qc_ = [P.sb(s1, "D_qc%d" % i, [128, S], BF16) for i in range(2)]
            qc_t = sc.tiles_n("D_qc", 2)
            sq = [P.sb(s1, "D_sq%d" % i, [128, 512], F32) for i in range(2)]
            sq_t = sc.tiles_n("D_sq", 2)
            lnv = [P.sb(s1, "D_ln%d" % i, [128, 512], F32) for i in range(2)]
            lnv_t = sc.tiles_n("D_ln", 2)
            bones = P.sb(s1, "D_bones", [128, 128], F32)
            bones_t = sc.tile("D_bones")
            ssp = [P.ps(s1, "D_ssp%d" % i, [128, 512], F32) for i in range(2)]
            ssp_t = sc.tiles_n("D_ssp", 2)
            t1 = TTf_t + qc_t + sq_t + lnv_t + [bones_t] + ssp_t
            for g in range(4):
                b = g % 2
                sc.dma("sp", TTf[b][:], natt[l][:, 2 * g:2 * g + 2, :, :], owner=TTf_t[b], writes=[TTf_t[b]])
                sc.op("pool", lambda e, b=b, g=g: e.tensor_copy(out=TT[:, 2 * g:2 * g + 2, :, :], in_=TTf[b][:]),
                      reads=[TTf_t[b]], writes=[TT_t[g]])
            for hh in range(2):
                sc.dma("sp", wcol[hh * 64:(hh + 1) * 64, 2:3], prm["na_q_norm_w"][l].rearrange("(d o) -> d o", o=1),
                       owner=wcol_t, writes=[wcol_t], part=True)
                sc.dma("sp", wcol[hh * 64:(hh + 1) * 64, 1:2], prm["na_k_norm_w"][l].rearrange("(d o) -> d o", o=1),
                       owner=wcol_t, writes=[wcol_t], part=True)
            sc.op("dve", lambda e: e.tensor_scalar(out=wcol[:, 0:1], in0=wcol[:, 2:3], scalar1=0.125, scalar2=None,
                                                   op0=ALU.mult), reads=[wcol_t], writes=[wcol_t])
            sc.op("pool", lambda e: e.memset(bones[:], 0.0), writes=[bones_t])
            sc.op("pool", lambda e: e.memset(bones[0:64, 0:64], 1.0), reads=[bones_t], writes=[bones_t])
            sc.op("pool", lambda e: e.memset(bones[64:128, 64:128], 1.0), reads=[bones_t], writes=[bones_t])
            cnt = 0
            for which, (src, dstT, dst_t, wc) in enumerate(((U["nq"], qnT, qn_t, 0), (U["nk"], knT, kn_t, 1))):
                src_t = G["dram_t"]["nq" if which == 0 else "nk"]
                for c in range(8):
                    cb = cnt % 2
                    cnt += 1
                    sc.dma("sp", qc_[cb][:], src[c * 128:(c + 1) * 128, :], owner=qc_t[cb], reads=[src_t],
                           writes=[qc_t[cb]])
                    for tb in range(4):
                        b = tb % 2
                        sl = slice(tb * 512, (tb + 1) * 512)
                        sc.op("dve", lambda e, cb=cb, b=b, sl=sl: e.tensor_tensor(out=sq[b][:], in0=qc_[cb][:, sl],
                                                                                  in1=qc_[cb][:, sl], op=ALU.mult),
                              reads=[qc_t[cb]], writes=[sq_t[b]])
                        sc.op("pe", lambda e, b=b: e.matmul(ssp[b][:], lhsT=bones[:], rhs=sq[b][:], start=True, stop=True),
                              reads=[bones_t, sq_t[b]], writes=[ssp_t[b]])
                        sc.op("act", lambda e, b=b: e.activation(out=lnv[b][:], in_=ssp[b][:], func=AF.Ln,
                                                                 bias=G["eps"][:, 0:1], scale=1.0 / 64.0),
                              reads=[ssp_t[b], G["eps_t"]], writes=[lnv_t[b]])
                        sc.op("act", lambda e, b=b: e.activation(out=lnv[b][:], in_=lnv[b][:], func=AF.Exp, scale=-0.5),
                              reads=[lnv_t[b]], writes=[lnv_t[b]])
                        sc.op("dve", lambda e, cb=cb, b=b, sl=sl, dstT=dstT, c=c, wc=wc: e.scalar_tensor_tensor(
                            out=dstT[:, c, sl], in0=qc_[cb][:, sl], scalar=wcol[:, wc:wc + 1], in1=lnv[b][:],
                            op0=ALU.mult, op1=ALU.mult),
                            reads=[qc_t[cb], lnv_t[b], wcol_t], writes=[dst_t[c]], part=(tb > 0))
            sc.barrier(release=t1)
        with contextlib.ExitStack() as s2:
            vx = P.sb(s2, "D_vx", [128, NT, 16, 65], BF16)
            vx_t = sc.tiles_n("D_vx", NT)
            sps = [P.ps(s2, "D_sps%d" % i, [128, 8, 128], F32) for i in range(2)]
            sps_t = sc.tiles_n("D_sps", 2)
            pT = [P.sb(s2, "D_pT%d" % i, [128, 5, 128], BF16) for i in range(3)]
            pT_t = sc.tiles_n("D_pT", 3)
            po = [P.ps(s2, "D_po%d" % i, [128, 2, 66], F32) for i in range(2)]
            po_t = sc.tiles_n("D_po", 2)
            rc = [P.sb(s2, "D_rc%d" % i, [128, 2], F32) for i in range(2)]
            rc_t = sc.tiles_n("D_rc", 2)
            ot = [P.sb(s2, "D_ot%d" % i, [128, 1024], BF16) for i in range(2)]
            ot_t = sc.tiles_n("D_ot", 2)
            tp = [P.ps(s2, "D_tp%d" % i, [128, 4, 128], BF16) for i in range(2)]
            tp_t = sc.tiles_n("D_tp", 2)
            yst = P.sb(s2, "D_yst", [128, 8, 512], BF16)
            yst_t = sc.tile("D_yst")
            t2 = vx_t + sps_t + pT_t + po_t + rc_t + ot_t + tp_t + [yst_t]
            nvv = U["nv"].rearrange("(i p) (h d) -> p i h d", p=128, d=64)
            for i in range(NT):
                sc.op("pool", lambda e, i=i: e.memset(vx[:, i, :, 64:65], 1.0), writes=[vx_t[i]])
                sc.dma("sp", vx[:, i, :, 0:64], nvv[:, i, :, :], owner=vx_t[i], reads=[G["dram_t"]["nv"]],
                       writes=[vx_t[i]], part=True)
            ident = G["ident"]
            tpc = [0]
            units = []
            for i in range(NT):
                jlo = na_r0(2 * i) // 2
                jhi = (na_r0(2 * i + 1) + 7) // 2
                js = list(range(jlo, jhi + 1))
                for hp in range(8):
                    for hh in range(2):
                        units.append((i, hp, hh, js))

            def emit_S(u):
                i, hp, hh, js = units[u]
                h = 2 * hp + hh
                p0 = 64 * hh
                sb_ = u % 2
                for jj, j in enumerate(js):
                    sc.op("pe", lambda e, sb_=sb_, jj=jj, j=j, p0=p0, hp=hp, i=i: e.matmul(
                        sps[sb_][:, jj, :], lhsT=knT[p0:p0 + 64, hp, j * 128:(j + 1) * 128],
                        rhs=qnT[p0:p0 + 64, hp, i * 128:(i + 1) * 128], start=True, stop=False,
                        skip_group_check=True),
                        reads=[kn_t[hp], qn_t[hp]], writes=[sps_t[sb_]], part=(jj > 0))
                    mms = []
                    for b0 in range(2):
                        qr = 2 * i + b0
                        va = [na_valid(2 * j + a, qr) for a in range(2)]
                        dr0 = 2 * j - qr + 7
                        cs = slice(b0 * 64, (b0 + 1) * 64)
                        if va[0] and va[1]:
                            mms.append((slice(0, 128), cs, TT[p0:p0 + 64, hp, dr0:dr0 + 2, :]))
                        elif not va[0] and not va[1]:
                            mms.append((slice(0, 128), cs, TT[p0:p0 + 64, hp, 15:17, :]))
                        else:
                            d0 = dr0 if va[0] else 15
                            d1 = dr0 + 1 if va[1] else 16
                            mms.append((slice(0, 64), cs, TT[p0:p0 + 64, hp, d0, :]))
                            mms.append((slice(64, 128), cs, TT[p0:p0 + 64, hp, d1, :]))
                    for mi, (ps_, cs, lhs) in enumerate(mms):
                        sc.op("pe", lambda e, sb_=sb_, jj=jj, ps_=ps_, cs=cs, lhs=lhs, p0=p0, last=(mi == len(mms) - 1):
                              e.matmul(sps[sb_][ps_, jj, cs], lhsT=lhs, rhs=ident[p0:p0 + 64, p0:p0 + 64],
                                       start=False, stop=last, skip_group_check=True),
                              reads=[TT_t[hp // 2], G["ident_t"]], writes=[sps_t[sb_]], part=True)

            def emit_rest(u):
                i, hp, hh, js = units[u]
                h = 2 * hp + hh
                sb_ = u % 2
                pt = u % 3
                pb_ = (u // 2) % 2
                ob = i % 2
                n = len(js)
                n1 = min(n, 4)
                sc.op("act", lambda e, pt=pt, sb_=sb_, n1=n1: e.activation(out=pT[pt][:, 0:n1, :],
                                                                         in_=sps[sb_][:, 0:n1, :], func=AF.Exp),
                      reads=[sps_t[sb_]], writes=[pT_t[pt]])
                if n > 4:
                    sc.op("act", lambda e, pt=pt, sb_=sb_, n=n: e.activation(out=pT[pt][:, 4:n, :],
                                                                           in_=sps[sb_][:, 4:n, :], func=AF.Exp),
                          reads=[sps_t[sb_]], writes=[pT_t[pt]], part=True)
                for jj, j in enumerate(js):
                    sc.op("pe", lambda e, pb_=pb_, hh=hh, pt=pt, jj=jj, j=j, h=h, n=n: e.matmul(
                        po[pb_][:, hh, 0:65], lhsT=pT[pt][:, jj, :], rhs=vx[:, j, h, :],
                        start=(jj == 0), stop=(jj == n - 1)),
                        reads=[pT_t[pt], vx_t[j]], writes=[po_t[pb_]], part=(hh > 0 or jj > 0))
                if hh == 1:
                    sc.op("dve", lambda e, pb_=pb_: e.reciprocal(out=rc[pb_][:, 0:2], in_=po[pb_][:, :, 64]),
                          reads=[po_t[pb_]], writes=[rc_t[pb_]])
                    for h2 in range(2):
                        hx = 2 * hp + h2
                        sc.op("dve", lambda e, pb_=pb_, h2=h2, hx=hx, ob=ob: e.tensor_scalar(
                            out=ot[ob][:, hx * 64:(hx + 1) * 64], in0=po[pb_][:, h2, 0:64], scalar1=rc[pb_][:, h2:h2 + 1],
                            scalar2=None, op0=ALU.mult),
                            reads=[po_t[pb_], rc_t[pb_]], writes=[ot_t[ob]], part=(hx > 0))
                if hp == 7 and hh == 1:
                    for half in range(2):
                        tb_ = tpc[0] % 2
                        tpc[0] += 1
                        for jq in range(4):
                            c = half * 4 + jq
                            sc.op("pe", lambda e, tb_=tb_, jq=jq, c=c, ob=ob: e.transpose(
                                out=tp[tb_][:, jq, :], in_=ot[ob][:, c * 128:(c + 1) * 128], identity=ident[:]),
                                reads=[ot_t[ob], G["ident_t"]], writes=[tp_t[tb_]], part=(jq > 0))
                        sc.op("act", lambda e, tb_=tb_, half=half, i=i: e.copy(
                            out=yst[:, half * 4:half * 4 + 4, (i % 4) * 128:(i % 4 + 1) * 128], in_=tp[tb_][:]),
                            reads=[tp_t[tb_]], writes=[yst_t], part=not (i % 4 == 0 and half == 0))
                    if i % 4 == 3:
                        yv = YB["na"].rearrange("(c p) t -> p c t", p=128)
                        sc.dma("pool", yv[:, :, (i - 3) * 128:(i + 1) * 128], yst[:], owner=yst_t, reads=[yst_t],
                               writes=[G["dram_t"]["yb_na"]], part=True)

            emit_S(0)
            for u in range(len(units)):
                if u + 1 < len(units):
                    emit_S(u + 1)
                emit_rest(u)
            sc.barrier(release=t2)
        sc.barrier(release=tiles)


def phase_F(P, sc, G, prm, l):
    nc = P.nc
    x = G["x"]
    with contextlib.ExitStack() as ph:
        hT = P.sb(ph, "F_hT", [128, 8, S], BF16)
        hT_t = sc.tiles_n("F_hT", NT)
        tiles = list(hT_t)
        tiles += rms_transpose(P, sc, G, ph, prm["norm_mlp_w"][l], hT, hT_t, l, "F")
        wst = WStream(P, sc, ph, "F", 1, 4096, nf=2, nb=3)
        fT = [P.sb(ph, "F_fT%d" % i, [128, 4, S], BF16) for i in range(2)]
        fT_t = [sc.tiles_n("F_fT%d_" % i, 4) for i in range(2)]
        rl = [P.sb(ph, "F_rl%d" % i, [128, 512], F32) for i in range(2)]
        rl_t = sc.tiles_n("F_rl", 2)
        acc = [P.ps(ph, "F_acc%d" % i, [128, 512], F32) for i in range(4)]
        acc_t = sc.tiles_n("F_acc", 4)
        tiles += wst.tiles + fT_t[0] + fT_t[1] + rl_t + acc_t
        w1v = prm["w_ff1"][l].rearrange("(kc p) n -> p kc n", p=128)
        w2v = prm["w_ff2"][l].rearrange("(c p) n -> p c n", p=128)
        items = []
        for g in range(8):
            items.append((w1v[:, :, g * 512:(g + 1) * 512], 8, 512))
            items.append((w2v[:, g * 4:(g + 1) * 4, :], 4, 1024))
        wst.items = items
        wst_views = {}

        def view(slot, k, n):
            return slot[:, 0, :].rearrange("p (k n) -> p k n", k=k)
        def _load(g):
            if g >= len(items):
                return
            ap, k, n = items[g]
            fs = g % wst.nf
            sc.dma("sp", view(wst.f[fs], k, n), ap, owner=wst.f_t[fs], writes=[wst.f_t[fs]])

        def _cast(g):
            if g >= len(items):
                return
            fs, bs = g % wst.nf, g % wst.nb
            sc.op("pool", lambda e: e.tensor_copy(out=wst.b[bs][:, 0, :], in_=wst.f[fs][:, 0, :]),
                  reads=[wst.f_t[fs]], writes=[wst.b_t[bs]])
        wst._load = _load
        wst._cast = _cast
        _load(0)
        _load(1)
        _cast(0)
        ai = 0
        ri = 0
        for g in range(8):
            fb = g % 2
            w1s, w1_t = wst.get(2 * g)
            w1b = view(w1s, 8, 512)
            for c in range(4):
                for tb in range(4):
                    a = ai % 4
                    ai += 1
                    for kc in range(8):
                        sc.op("pe", lambda e, a=a, kc=kc, w1b=w1b, c=c, tb=tb: e.matmul(
                            acc[a][:], lhsT=w1b[:, kc, c * 128:(c + 1) * 128],
                            rhs=hT[:, kc, tb * 512:(tb + 1) * 512], start=(kc == 0), stop=(kc == 7)),
                            reads=[w1_t] + hT_t[tb * 4:tb * 4 + 4], writes=[acc_t[a]], part=(kc > 0))
                    r = ri % 2
                    ri += 1
                    sc.op("act", lambda e, r=r, a=a: e.activation(out=rl[r][:], in_=acc[a][:], func=AF.Relu),
                          reads=[acc_t[a]], writes=[rl_t[r]])
                    sc.op("pool", lambda e, r=r, fb=fb, c=c, tb=tb: e.tensor_tensor(
                        out=fT[fb][:, c, tb * 512:(tb + 1) * 512], in0=rl[r][:], in1=rl[r][:], op=ALU.mult),
                        reads=[rl_t[r]], writes=[fT_t[fb][c]], part=(tb > 0))
            w2s, w2_t = wst.get(2 * g + 1)
            w2b = view(w2s, 4, 1024)
            for i in range(NT):
                for hh in range(2):
                    a = ai % 4
                    ai += 1
                    for c in range(4):
                        sc.op("pe", lambda e, a=a, c=c, w2b=w2b, i=i, hh=hh, fb=fb: e.matmul(
                            acc[a][:], lhsT=fT[fb][:, c, i * 128:(i + 1) * 128],
                            rhs=w2b[:, c, hh * 512:(hh + 1) * 512], start=(c == 0), stop=(c == 3)),
                            reads=[w2_t, fT_t[fb][c]], writes=[acc_t[a]], part=(c > 0))
                    xs = x[:, i, hh * 512:(hh + 1) * 512]
                    sc.op("dve", lambda e, xs=xs, a=a: e.tensor_tensor(out=xs, in0=xs, in1=acc[a][:], op=ALU.add),
                          reads=[acc_t[a], G["xt"][i]], writes=[G["xt"][i]])
        sc.barrier(release=tiles)


_NC_CACHE = {}


def make_na_tt(rpb):
    rpb = np.asarray(rpb, dtype=np.float32)
    L = rpb.shape[0]
    out = np.full((L, 128, 8, 17, 64), NEG, dtype=np.float32)
    qc = np.arange(64)
    ws = np.clip(qc - 8, 0, 48)
    for q in range(64):
        kc = np.arange(ws[q], ws[q] + 16)
        idx = kc - q + 15
        for hh in range(2):
            out[:, hh * 64 + q, :, 0:15, ws[q]:ws[q] + 16] = rpb[:, hh::2][:, :, :, idx]
    return out


def kernel(**inputs):
    cfg = {}
    key = "full"
    if key not in _NC_CACHE:
        _NC_CACHE[key] = build(cfg)
    nc = _NC_CACHE[key]
    x = np.ascontiguousarray(inputs["x"], dtype=np.float32)
    base = {n: np.ascontiguousarray(inputs[n], dtype=np.float32) for n in PARAM_NAMES}
    base["na_tt"] = make_na_tt(inputs["na_rpb"])
    in_maps = []
    for c in range(8):
        m = dict(base)
        m["x"] = x[c]
        in_maps.append(m)
    res = run_bass_kernel_spmd(nc, in_maps, core_ids=list(range(8)))
    return np.stack([r["y"] for r in res.results], axis=0).astype(np.float32)
```

```python
import contextlib
import numpy as np
import concourse.bass as bass
import concourse.mybir as mybir
from concourse.bass_utils import run_bass_kernel_spmd

F32 = mybir.dt.float32
BF16 = mybir.dt.bfloat16
ALU = mybir.AluOpType
AF = mybir.ActivationFunctionType
AX = mybir.AxisListType

D = 1024
S = 2048
NT = S // 128
DEPTH = 2
N_IN = 11840
EPS = 1e-6


class TT:
    __slots__ = ("name", "lw", "rd", "dsems", "gen")

    def __init__(self, name):
        self.name = name
        self.lw = {}
        self.rd = {}
        self.gen = {}
        self.dsems = {}


class Sched:
    ENG = ("pe", "act", "dve", "pool", "sp")
    BLK = {"pe": "tensor", "act": "scalar", "dve": "vector", "pool": "gpsimd", "sp": "sync"}

    def __init__(self, nc, stack):
        self.nc = nc
        self.stack = stack
        self.ops = {e: [] for e in self.ENG}
        self.seen = {e: {} for e in self.ENG}
        self.esem = {e: stack.enter_context(nc.semaphore("es_" + e)) for e in self.ENG if e != "sp"}
        self.tiles = []
        self.free_dsems = {"sp": [], "pool": [], "act": []}
        self.nsem = 4
        self.skip_same = {"pe"}

    def tile(self, name):
        t = TT(name)
        self.tiles.append(t)
        return t

    def tiles_n(self, name, n):
        return [self.tile("%s%d" % (name, i)) for i in range(n)]

    def _collect(self, reads, writes, part):
        evs = {}

        def add(d):
            for k, v in d.items():
                if k not in evs or evs[k][0] < v[0]:
                    evs[k] = v
        for t in reads:
            add(t.lw)
        for t in writes:
            if part and not t.rd:
                add(t.gen)
                continue
            g = dict(t.rd)
            for k, v in t.lw.items():
                if k not in g or g[k][0] < v[0]:
                    g[k] = v
            t.gen = g
            add(g)
        return evs

    def _waits(self, eng, evs):
        waits = []
        for k, (val, obj) in evs.items():
            if k == ("E", eng) and eng in self.skip_same:
                continue
            if self.seen[eng].get(k, 0) >= val:
                continue
            self.seen[eng][k] = val
            waits.append((k, val, obj))
            if k[0] == "E":
                self.ops[k[1]][val - 1]["inc"] = True
        return waits

    def _update(self, ev_key, ev_val, reads, writes, part):
        for t in reads:
            t.rd[ev_key] = ev_val
        for t in writes:
            if part and not t.rd:
                t.lw[ev_key] = ev_val
            else:
                t.lw = {ev_key: ev_val}
                t.rd = {}

    def op(self, eng, fn, reads=(), writes=(), part=False):
        waits = self._waits(eng, self._collect(reads, writes, part))
        self.ops[eng].append({"fn": fn, "waits": waits, "inc": False, "dma": None})
        idx = len(self.ops[eng])
        self._update(("E", eng), (idx, None), reads, writes, part)

    def dma(self, q, out, in_, owner, reads=(), writes=(), part=False, **kw):
        waits = self._waits(q, self._collect(reads, writes, part))
        rec = owner.dsems.get(q)
        if rec is None:
            if self.free_dsems[q]:
                rec = self.free_dsems[q].pop()
            else:
                rec = [self.stack.enter_context(self.nc.semaphore("ds%d" % self.nsem)), 0, self.nsem]
                self.nsem += 1
            owner.dsems[q] = rec
        rec[1] += 16
        self.ops[q].append({"fn": (lambda e: e.dma_start(out=out, in_=in_, **kw)), "waits": waits,
                            "inc": False, "dma": rec[0]})
        self._update(("D", rec[2]), (rec[1], rec[0]), reads, writes, part)

    def barrier(self, release=()):
        evs = {}
        for e in self.ENG:
            if e == "sp":
                continue
            idx = len(self.ops[e])
            while idx > 0 and (self.ops[e][idx - 1]["dma"] is not None or self.ops[e][idx - 1].get("nop")):
                idx -= 1
            if idx > 0:
                evs[("E", e)] = (idx, None)
        for t in self.tiles:
            for d in (t.lw, t.rd):
                for k, v in d.items():
                    if k[0] == "D" and (k not in evs or evs[k][0] < v[0]):
                        evs[k] = v
        for e in self.ENG:
            sk = self.skip_same
            self.skip_same = set()
            w = self._waits(e, dict(evs))
            self.skip_same = sk
            self.ops[e].append({"fn": (lambda en: en.nop()), "waits": w, "inc": False, "dma": None, "nop": True})
        for t in self.tiles:
            t.lw = {}
            t.rd = {}
            t.gen = {}
        rel = set(id(t) for t in release)
        for t in release:
            for q, rec in t.dsems.items():
                self.free_dsems[q].append(rec)
            t.dsems = {}
        self.tiles = [t for t in self.tiles if id(t) not in rel]

    def emit(self):
        nc = self.nc
        mile = {}
        for e in self.ENG:
            c = 0
            m = []
            for o in self.ops[e]:
                if o["inc"]:
                    c += 1
                m.append(c)
            mile[e] = m
            assert c < 60000, (e, c)
        with nc.Block() as block:
            for e in self.ENG:
                def body(engine, e=e):
                    for o in self.ops[e]:
                        for (k, val, obj) in o["waits"]:
                            if k[0] == "E":
                                engine.wait_ge(self.esem[k[1]], mile[k[1]][val - 1])
                            else:
                                engine.wait_ge(obj, val)
                        ins = o["fn"](engine)
                        if o["dma"] is not None:
                            ins.then_inc(o["dma"], 16)
                        elif o["inc"]:
                            ins.then_inc(self.esem[e], 1)
                getattr(block, self.BLK[e])(body)


class Prog:
    def __init__(self, cfg):
        self.cfg = cfg
        self.nc = bass.Bass("TRN2", target_bir_lowering=False)
        self.dbg = cfg.get("debug", ())

    def dram(self, name, shape, dt, kind="Internal"):
        if name in self.dbg:
            kind = "ExternalOutput"
        if name in self.cfg.get("ext_in", ()):
            kind = "ExternalInput"
        return self.nc.dram_tensor(name, list(shape), dt, kind=kind).ap()

    def sb(self, stack, name, shape, dt):
        self.uid = getattr(self, "uid", 0) + 1
        return stack.enter_context(self.nc.sbuf_tensor("%s_u%d" % (name, self.uid), list(shape), dt))

    def ps(self, stack, name, shape, dt):
        self.uid = getattr(self, "uid", 0) + 1
        return stack.enter_context(self.nc.psum_tensor("%s_u%d" % (name, self.uid), list(shape), dt))


IN_SIZES = (1024, 1536, 16, 16, 512, 512, 1024, 1024, 16, 16, 1024, 1024, 1024, 3072)
IN_OFF = [0]
for _s in IN_SIZES:
    IN_OFF.append(IN_OFF[-1] + _s)
(O_Z, O_XBC, O_DTF, O_DTB, O_GQ, O_GK, O_GV, O_GG, O_GAF, O_GAB, O_NQ, O_NK, O_NV, O_GATE, _) = IN_OFF

PARAM_NAMES = ["norm_mix_w", "w_in", "ssd_conv_w", "ssd_conv_b", "ssd_dt_bias_f", "ssd_dt_bias_b",
               "ssd_a_log_f", "ssd_a_log_b", "ssd_d", "ssd_norm_w", "gla_a2_f", "gla_a2_bias_f",
               "gla_a2_b", "gla_a2_bias_b", "gla_norm_w", "na_q_norm_w", "na_k_norm_w", "na_rpb",
               "w_branch_ssd", "w_branch_gla", "w_branch_na", "w_out", "norm_mlp_w", "w_ff1", "w_ff2"]
PARAM_SHAPES = {
    "norm_mix_w": (2, 1024), "w_in": (2, 1024, 11840), "ssd_conv_w": (2, 5, 1536), "ssd_conv_b": (2, 1536),
    "ssd_dt_bias_f": (2, 16), "ssd_dt_bias_b": (2, 16), "ssd_a_log_f": (2, 16), "ssd_a_log_b": (2, 16),
    "ssd_d": (2, 16), "ssd_norm_w": (2, 1024), "gla_a2_f": (2, 16, 512), "gla_a2_bias_f": (2, 512),
    "gla_a2_b": (2, 16, 512), "gla_a2_bias_b": (2, 512), "gla_norm_w": (2, 256), "na_q_norm_w": (2, 64),
    "na_k_norm_w": (2, 64), "na_rpb": (2, 16, 15, 31), "w_branch_ssd": (2, 1024, 1024),
    "w_branch_gla": (2, 1024, 1024), "w_branch_na": (2, 1024, 1024), "w_out": (2, 1024, 1024),
    "norm_mlp_w": (2, 1024), "w_ff1": (2, 1024, 4096), "w_ff2": (2, 4096, 1024),
}


def build(cfg):
    P = Prog(cfg)
    nc = P.nc
    layers = cfg.get("layers", DEPTH)
    phases = cfg.get("phases", "ABCDEF")
    x_in = nc.dram_tensor("x", [S, D], F32, kind="ExternalInput").ap()
    prm = {n: nc.dram_tensor(n, list(PARAM_SHAPES[n]), F32, kind="ExternalInput").ap() for n in PARAM_NAMES}
    y_out = nc.dram_tensor("y", [S, D], F32, kind="ExternalOutput").ap()
    natt = nc.dram_tensor("na_tt", [DEPTH, 128, 8, 17, 64], F32, kind="ExternalInput").ap()

    U = {}
    for nm, w in (("z", 1024), ("gv", 1024), ("gg", 1024), ("nv", 1024)):
        U[nm] = P.dram("u_" + nm, [S, w], BF16)
    U["dt"] = P.dram("u_dt", [S, 32], F32)
    for nm, w in (("xbc", 1536), ("gq", 512), ("gk", 512), ("nq", 1024), ("nk", 1024), ("gate", 3072)):
        U[nm] = P.dram("u_" + nm + "T", [w, S], BF16)
    U["ga"] = P.dram("u_gaT", [32, S], F32)
    YB = {nm: P.dram("yb_" + nm, [1024, S], BF16) for nm in ("ssd", "gla", "na")}
    ybw = P.dram("ybw", [S, 1024], BF16)

    with contextlib.ExitStack() as top:
        sc = Sched(nc, top)
        G = {}
        G["x"] = P.sb(top, "x_res", [128, NT, D], F32)
        G["xt"] = sc.tiles_n("x", NT)
        G["ident"] = P.sb(top, "ident", [128, 128], BF16)
        G["ident_t"] = sc.tile("ident")
        G["dram_t"] = {k: sc.tile("d_" + k) for k in list(U) + ["yb_ssd", "yb_gla", "yb_na", "ybw"]}
        G["ybw"] = ybw

        ones_f = P.sb(top, "ones_f", [128, 128], F32)
        ones_t = sc.tile("ones_f")
        sc.op("pool", lambda e: e.memset(ones_f[:], 1.0), writes=[ones_t])
        sc.op("pool", lambda e: e.affine_select(out=G["ident"][:], in_=ones_f[:], pattern=[[-1, 128]],
                                                compare_op=ALU.is_equal, fill=0.0, base=0,
                                                channel_multiplier=1),
              reads=[ones_t], writes=[G["ident_t"]])
        G["ones_f"] = ones_f
        G["eps"] = P.sb(top, "epsc", [128, 2], F32)
        G["eps_t"] = sc.tile("epsc")
        sc.op("pool", lambda e: e.memset(G["eps"][:], EPS), writes=[G["eps_t"]])
        G["one"] = P.sb(top, "onec", [128, 2], F32)
        G["one_t"] = sc.tile("onec")
        sc.op("pool", lambda e: e.memset(G["one"][:], 1.0), writes=[G["one_t"]])
        G["neghalf"] = P.sb(top, "neghalf", [128, 16], F32)
        G["neghalf_t"] = sc.tile("neghalf")
        sc.op("pool", lambda e: e.memset(G["neghalf"][:], -0.5), writes=[G["neghalf_t"]])
        G["ones_t"] = ones_t

        build_tri(P, sc, G, top)
        xv = x_in.rearrange("(i p) d -> p i d", p=128)
        for i in range(NT):
            sc.dma("sp", G["x"][:, i, :], xv[:, i, :], owner=G["xt"][i], writes=[G["xt"][i]])

        for l in range(layers):
            if "A" in phases:
                phase_A(P, sc, G, U, prm, l)
            if "B" in phases:
                phase_B(P, sc, G, U, YB, prm, l)
            if "C" in phases:
                phase_C(P, sc, G, U, YB, prm, l)
            if "D" in phases:
                phase_D(P, sc, G, U, YB, prm, natt, l)
            if "E" in phases:
                phase_E(P, sc, G, U, YB, prm, l)
            if "F" in phases:
                phase_F(P, sc, G, prm, l)

        yv = y_out.rearrange("(i p) d -> p i d", p=128)
        outt = sc.tile("yout")
        for i in range(NT):
            sc.dma("sp", yv[:, i, :], G["x"][:, i, :], owner=G["xt"][i], reads=[G["xt"][i]], writes=[outt],
                   part=True)
        sc.op("sp", lambda e: e.nop(), reads=[outt])
        sc.barrier()
        sc.emit()
    return nc


def rms_transpose(P, sc, G, ph, wrow_ap, hT, hT_t, l, tag):
    nc = P.nc
    wb = P.sb(ph, tag + "_wb", [128, D], F32)
    wb_t = sc.tile(tag + "_wb")
    sc.dma("sp", wb[:], wrow_ap.partition_broadcast(128), owner=wb_t, writes=[wb_t])
    junk = [P.sb(ph, tag + "_junk%d" % i, [128, D], BF16) for i in range(2)]
    junk_t = sc.tiles_n(tag + "_junk", 2)
    hb = [P.sb(ph, tag + "_hb%d" % i, [128, D], BF16) for i in range(2)]
    hb_t = sc.tiles_n(tag + "_hb", 2)
    ss = [P.sb(ph, tag + "_ss%d" % i, [128, 2], F32) for i in range(2)]
    ss_t = sc.tiles_n(tag + "_ss", 2)
    tp = [P.ps(ph, tag + "_tp%d" % i, [128, 4, 128], BF16) for i in range(2)]
    tp_t = sc.tiles_n(tag + "_tp", 2)
    x = G["x"]
    new_tiles = [wb_t] + junk_t + hb_t + ss_t + tp_t
    for i in range(NT):
        b = i % 2
        xt = G["xt"][i]
        sc.op("dve", lambda e, i=i, b=b: e.scalar_tensor_tensor(out=junk[b][:], in0=x[:, i, :], scalar=1.0,
                                                                in1=x[:, i, :], op0=ALU.mult, op1=ALU.mult,
                                                                accum_out=ss[b][:, 0:1]),
              reads=[xt], writes=[junk_t[b], ss_t[b]])
        sc.op("dve", lambda e, b=b: e.tensor_scalar(out=ss[b][:, 1:2], in0=ss[b][:, 0:1], scalar1=1.0 / D,
                                                    scalar2=EPS, op0=ALU.mult, op1=ALU.add),
              reads=[ss_t[b]], writes=[ss_t[b]])
        sc.op("pool", lambda e, b=b: e.tensor_tensor(out=ss[b][:, 0:1], in0=ss[b][:, 1:2],
                                                     in1=G["neghalf"][:, 0:1], op=ALU.pow),
              reads=[ss_t[b], G["neghalf_t"]], writes=[ss_t[b]])
        sc.op("dve", lambda e, i=i, b=b: e.scalar_tensor_tensor(out=hb[b][:], in0=x[:, i, :],
                                                                scalar=ss[b][:, 0:1], in1=wb[:],
                                                                op0=ALU.mult, op1=ALU.mult),
              reads=[xt, ss_t[b], wb_t], writes=[hb_t[b]])
        for half in range(2):
            pb = (2 * i + half) % 2
            for j in range(4):
                kc = half * 4 + j
                sc.op("pe", lambda e, b=b, pb=pb, j=j, kc=kc: e.transpose(
                    out=tp[pb][:, j, :], in_=hb[b][:, kc * 128:(kc + 1) * 128], identity=G["ident"][:]),
                    reads=[hb_t[b], G["ident_t"]], writes=[tp_t[pb]], part=(j > 0))
            eng = "act" if half == 0 else "dve"
            if eng == "act":
                sc.op("act", lambda e, pb=pb, half=half, i=i: e.copy(
                    out=hT[:, half * 4:half * 4 + 4, i * 128:(i + 1) * 128], in_=tp[pb][:]),
                    reads=[tp_t[pb]], writes=[hT_t[i]], part=True)
            else:
                sc.op("dve", lambda e, pb=pb, half=half, i=i: e.tensor_copy(
                    out=hT[:, half * 4:half * 4 + 4, i * 128:(i + 1) * 128], in_=tp[pb][:]),
                    reads=[tp_t[pb]], writes=[hT_t[i]], part=True)
    return new_tiles


class WStream:
    def __init__(self, P, sc, ph, tag, kdim, ncol, nf=2, nb=3):
        self.sc = sc
        self.kdim, self.ncol = kdim, ncol
        self.nf, self.nb = nf, nb
        self.f = [P.sb(ph, "%s_wf%d" % (tag, i), [128, kdim, ncol], F32) for i in range(nf)]
        self.f_t = sc.tiles_n(tag + "_wf", nf)
        self.b = [P.sb(ph, "%s_wb%d" % (tag, i), [128, kdim, ncol], BF16) for i in range(nb)]
        self.b_t = sc.tiles_n(tag + "_wbt", nb)
        self.tiles = self.f_t + self.b_t
        self.items = []

    def start(self, items):
        self.items = items
        self._load(0)
        self._load(1)
        self._cast(0)

    def _load(self, g):
        if g >= len(self.items):
            return
        ap, k, n = self.items[g]
        fs = g % self.nf
        self.sc.dma("sp", self.f[fs][:, 0:k, 0:n], ap, owner=self.f_t[fs], writes=[self.f_t[fs]])

    def _cast(self, g):
        if g >= len(self.items):
            return
        ap, k, n = self.items[g]
        fs, bs = g % self.nf, g % self.nb
        self.sc.op("pool", lambda e: e.tensor_copy(out=self.b[bs][:, 0:k, 0:n], in_=self.f[fs][:, 0:k, 0:n]),
                   reads=[self.f_t[fs]], writes=[self.b_t[bs]])

    def get(self, g):
        self._cast(g + 1)
        self._load(g + 2)
        return self.b[g % self.nb], self.b_t[g % self.nb]


def proj_groups():
    g = []

    def seg(off, n, mode, key):
        c = 0
        while c < n:
            w = min(512, n - c)
            g.append((off + c, w, mode, key, c))
            c += w
    seg(O_Z, 1024, "tok", "z")
    seg(O_XBC, 1536, "feat", "xbc")
    g.append((O_DTF, 32, "tok32", "dt", 0))
    seg(O_GQ, 512, "feat", "gq")
    seg(O_GK, 512, "feat", "gk")
    seg(O_GV, 1024, "tok", "gv")
    seg(O_GG, 1024, "tok", "gg")
    g.append((O_GAF, 32, "feat32", "ga", 0))
    seg(O_NQ, 1024, "feat", "nq")
    seg(O_NK, 1024, "feat", "nk")
    seg(O_NV, 1024, "tok", "nv")
    seg(O_GATE, 3072, "feat", "gate")
    return g


def phase_A(P, sc, G, U, prm, l):
    nc = P.nc
    with contextlib.ExitStack() as ph:
        hT = P.sb(ph, "A_hT", [128, 8, S], BF16)
        hT_t = sc.tiles_n("A_hT", NT)
        tiles = list(hT_t)
        tiles += rms_transpose(P, sc, G, ph, prm["norm_mix_w"][l], hT, hT_t, l, "A")
        wst = WStream(P, sc, ph, "A", 8, 512)
        acc = [P.ps(ph, "A_acc%d" % i, [128, 512], F32) for i in range(4)]
        acc_t = sc.tiles_n("A_acc", 4)
        NS = 3
        stg = [P.sb(ph, "A_stg%d" % i, [128, 2048], BF16) for i in range(NS)]
        stg_t = sc.tiles_n("A_stg", NS)
        stf = [P.sb(ph, "A_stf%d" % i, [128, 4, 32], F32) for i in range(2)]
        stf_t = sc.tiles_n("A_stf", 2)
        G["ga_stage"] = P.sb(ph, "A_gast", [32, 2048], F32)
        G["ga_stage_t"] = sc.tile("A_gast")
        tiles += wst.tiles + acc_t + stg_t + stf_t + [G["ga_stage_t"]]
        wv = prm["w_in"][l].rearrange("(kc p) n -> p kc n", p=128)
        groups = proj_groups()
        wst.start([(wv[:, :, c0:c0 + n], 8, n) for (c0, n, _m, _k, _d) in groups])
        ai = 0
        si = 0
        ev = 0
        for gi, (c0, n, mode, key, doff) in enumerate(groups):
            wcur, wcur_t = wst.get(gi)
            dst = U[key]
            dst_t = G["dram_t"][key]
            if mode in ("tok", "tok32"):
                for tb in range(4):
                    if mode == "tok":
                        st = si % NS
                        si += 1
                    else:
                        st = tb % 2
                    for j in range(4):
                        i = tb * 4 + j
                        a = ai % 4
                        ai += 1
                        for kc in range(8):
                            sc.op("pe", lambda e, a=a, kc=kc, i=i, wcur=wcur, n=n: e.matmul(
                                acc[a][:, 0:n], lhsT=hT[:, kc, i * 128:(i + 1) * 128], rhs=wcur[:, kc, 0:n],
                                start=(kc == 0), stop=(kc == 7)),
                                reads=[hT_t[i], wcur_t], writes=[acc_t[a]], part=(kc > 0))
                        if mode == "tok":
                            o_ap = stg[st][:, j * 512:j * 512 + n]
                            o_t = stg_t[st]
                        else:
                            o_ap = stf[st][:, j, 0:n]
                            o_t = stf_t[st]
                        ev += 1
                        if ev % 2 == 0:
                            sc.op("act", lambda e, o_ap=o_ap, a=a, n=n: e.copy(out=o_ap, in_=acc[a][:, 0:n]),
                                  reads=[acc_t[a]], writes=[o_t], part=(j > 0))
                        else:
                            sc.op("dve", lambda e, o_ap=o_ap, a=a, n=n: e.tensor_copy(out=o_ap, in_=acc[a][:, 0:n]),
                                  reads=[acc_t[a]], writes=[o_t], part=(j > 0))
                    rows = dst[tb * 512:(tb + 1) * 512, doff:doff + n].rearrange("(j p) c -> p j c", p=128)
                    if mode == "tok":
                        src = stg[st][:].rearrange("p (j c) -> p j c", j=4)[:, :, 0:n]
                        sc.dma("pool", rows, src, owner=stg_t[st], reads=[stg_t[st]], writes=[dst_t], part=True)
                    else:
                        sc.dma("pool", rows, stf[st][:, :, 0:n], owner=stf_t[st], reads=[stf_t[st]], writes=[dst_t],
                               part=True)
            else:
                nchunk = (n + 127) // 128
                for c in range(nchunk):
                    m = min(128, n - c * 128)
                    if mode == "feat":
                        st = si % NS
                        si += 1
                    else:
                        st = 0
                    for tb in range(4):
                        a = ai % 4
                        ai += 1
                        for kc in range(8):
                            sc.op("pe", lambda e, a=a, kc=kc, tb=tb, wcur=wcur, c=c, m=m: e.matmul(
                                acc[a][0:m, :], lhsT=wcur[:, kc, c * 128:c * 128 + m],
                                rhs=hT[:, kc, tb * 512:(tb + 1) * 512], start=(kc == 0), stop=(kc == 7)),
                                reads=hT_t[tb * 4:tb * 4 + 4] + [wcur_t], writes=[acc_t[a]], part=(kc > 0))
                        ev += 1
                        if mode == "feat":
                            o_ap = stg[st][0:m, tb * 512:(tb + 1) * 512]
                            o_t = stg_t[st]
                            if ev % 2 == 0:
                                sc.op("act", lambda e, o_ap=o_ap, a=a, m=m: e.copy(out=o_ap, in_=acc[a][0:m, :]),
                                      reads=[acc_t[a]], writes=[o_t], part=(tb > 0))
                            else:
                                sc.op("dve", lambda e, o_ap=o_ap, a=a, m=m: e.tensor_copy(out=o_ap, in_=acc[a][0:m, :]),
                                      reads=[acc_t[a]], writes=[o_t], part=(tb > 0))
                        else:
                            sc.op("dve", lambda e, a=a, m=m, tb=tb, gast=G["ga_stage"]: e.tensor_copy(
                                out=gast[0:m, tb * 512:(tb + 1) * 512], in_=acc[a][0:m, :]),
                                reads=[acc_t[a]], writes=[G["ga_stage_t"]], part=(tb > 0))
                    if mode == "feat":
                        sc.dma("pool", dst[doff + c * 128:doff + c * 128 + m, :], stg[st][0:m, :], owner=stg_t[st],
                               reads=[stg_t[st]], writes=[dst_t], part=True)
                    else:
                        sc.dma("pool", dst[0:m, :], G["ga_stage"][0:m, :], owner=G["ga_stage_t"],
                               reads=[G["ga_stage_t"]], writes=[dst_t], part=True)
        sc.barrier(release=tiles)


def phase_E(P, sc, G, U, YB, prm, l):
    nc = P.nc
    x = G["x"]
    with contextlib.ExitStack() as ph:
        wst = WStream(P, sc, ph, "E", 8, 256, nf=2, nb=2)
        mix = P.sb(ph, "E_mix", [128, 8, 1024], F32)
        mix_t = sc.tiles_n("E_mix", 8)
        mixb = P.sb(ph, "E_mixb", [128, 8, 1024], BF16)
        mixb_t = sc.tile("E_mixb")
        ybT = [P.sb(ph, "E_yb%d" % i, [128, 8, 1024], BF16) for i in range(2)]
        ybT_t = sc.tiles_n("E_yb", 2)
        gsl = [P.sb(ph, "E_g%d" % i, [128, 1024], BF16) for i in range(3)]
        gsl_t = sc.tiles_n("E_g", 3)
        sig = [P.sb(ph, "E_sig%d" % i, [128, 1024], F32) for i in range(2)]
        sig_t = sc.tiles_n("E_sig", 2)
        tmp = [P.sb(ph, "E_tmp%d" % i, [128, 512], F32) for i in range(2)]
        tmp_t = sc.tiles_n("E_tmp", 2)
        acc = [P.ps(ph, "E_acc%d" % i, [128, 512], F32) for i in range(4)]
        acc_t = sc.tiles_n("E_acc", 4)
        tiles = wst.tiles + mix_t + [mixb_t] + ybT_t + gsl_t + sig_t + tmp_t + acc_t
        wnames = ["w_branch_ssd", "w_branch_gla", "w_branch_na", "w_out"]
        bnames = ["ssd", "gla", "na"]
        items = []
        for half in range(2):
            for wn in wnames:
                wv = prm[wn][l].rearrange("(kc p) n -> p kc n", p=128)
                for cg in range(4):
                    items.append((wv[:, :, cg * 256:(cg + 1) * 256], 8, 256))
        wst.start(items)
        gi = 0
        ai = 0
        gcount = 0
        tcount = 0
        ybcount = 0
        for half in range(2):
            t0 = half * 1024
            for b in range(3):
                ys = ybcount % 2
                ybcount += 1
                ybv = YB[bnames[b]].rearrange("(kc p) t -> p kc t", p=128)
                sc.dma("sp", ybT[ys][:], ybv[:, :, t0:t0 + 1024], owner=ybT_t[ys],
                       reads=[G["dram_t"]["yb_" + bnames[b]]], writes=[ybT_t[ys]])
                for cg in range(4):
                    wcur, wcur_t = wst.get(gi)
                    gi += 1
                    for ecl in range(2):
                        ec = cg * 2 + ecl
                        gs = gcount % 3
                        ss_ = gcount % 2
                        gcount += 1
                        grow = b * 1024 + ec * 128
                        sc.dma("sp", gsl[gs][:], U["gate"][grow:grow + 128, t0:t0 + 1024], owner=gsl_t[gs],
                               reads=[G["dram_t"]["gate"]], writes=[gsl_t[gs]])
                        sc.op("act", lambda e, gs=gs, ss_=ss_: e.activation(out=sig[ss_][:], in_=gsl[gs][:],
                                                                            func=AF.Sigmoid),
                              reads=[gsl_t[gs]], writes=[sig_t[ss_]])
                        for tbh in range(2):
                            a = ai % 4
                            ai += 1
                            for kc in range(8):
                                sc.op("pe", lambda e, a=a, kc=kc, wcur=wcur, ecl=ecl, ys=ys, tbh=tbh: e.matmul(
                                    acc[a][:], lhsT=wcur[:, kc, ecl * 128:(ecl + 1) * 128],
                                    rhs=ybT[ys][:, kc, tbh * 512:(tbh + 1) * 512], start=(kc == 0), stop=(kc == 7)),
                                    reads=[wcur_t, ybT_t[ys]], writes=[acc_t[a]], part=(kc > 0))
                            msl = mix[:, ec, tbh * 512:(tbh + 1) * 512]
                            sgl = sig[ss_][:, tbh * 512:(tbh + 1) * 512]
                            if b == 0:
                                sc.op("dve", lambda e, msl=msl, a=a, sgl=sgl: e.tensor_tensor(
                                    out=msl, in0=acc[a][:], in1=sgl, op=ALU.mult),
                                    reads=[acc_t[a], sig_t[ss_]], writes=[mix_t[ec]], part=(tbh > 0))
                            else:
                                ts = tcount % 2
                                tcount += 1
                                sc.op("dve", lambda e, ts=ts, a=a, sgl=sgl: e.tensor_tensor(
                                    out=tmp[ts][:], in0=acc[a][:], in1=sgl, op=ALU.mult),
                                    reads=[acc_t[a], sig_t[ss_]], writes=[tmp_t[ts]])
                                if b == 1:
                                    sc.op("pool", lambda e, msl=msl, ts=ts: e.tensor_tensor(
                                        out=msl, in0=msl, in1=tmp[ts][:], op=ALU.add),
                                        reads=[tmp_t[ts], mix_t[ec]], writes=[mix_t[ec]])
                                else:
                                    sc.op("pool", lambda e, msl=msl, ts=ts, ec=ec, tbh=tbh: e.tensor_tensor(
                                        out=mixb[:, ec, tbh * 512:(tbh + 1) * 512], in0=msl, in1=tmp[ts][:],
                                        op=ALU.add),
                                        reads=[tmp_t[ts], mix_t[ec]], writes=[mixb_t], part=True)
            for cg in range(4):
                wcur, wcur_t = wst.get(gi)
                gi += 1
                for j in range(8):
                    i = half * 8 + j
                    a = ai % 4
                    ai += 1
                    for ec in range(8):
                        sc.op("pe", lambda e, a=a, ec=ec, wcur=wcur, j=j: e.matmul(
                            acc[a][:, 0:256], lhsT=mixb[:, ec, j * 128:(j + 1) * 128], rhs=wcur[:, ec, :],
                            start=(ec == 0), stop=(ec == 7)),
                            reads=[wcur_t, mixb_t], writes=[acc_t[a]], part=(ec > 0))
                    xs = x[:, i, cg * 256:(cg + 1) * 256]
                    sc.op("dve", lambda e, xs=xs, a=a: e.tensor_tensor(out=xs, in0=xs, in1=acc[a][:, 0:256], op=ALU.add),
                          reads=[acc_t[a], G["xt"][i]], writes=[G["xt"][i]])
        sc.barrier(release=tiles)


class Ring:
    def __init__(self, P, sc, stack, name, shape, dt, n, psum=False, views=None):
        if views is not None:
            self.h = views
            n = len(views)
        else:
            mk = P.ps if psum else P.sb
            self.h = [mk(stack, "%s%d" % (name, i), shape, dt) for i in range(n)]
        self.t = sc.tiles_n(name + "_", n)
        self.i = 0
        self.n = n

    def next(self):
        k = self.i % self.n
        self.i += 1
        return self.h[k], self.t[k]


def build_tri(P, sc, G, top):
    for nm in ("trif", "trib", "trif64", "trib64", "mcf64", "mcb64", "trifs", "tribs"):
        G[nm] = P.sb(top, nm, [128, 128], F32)
        G[nm + "_t"] = sc.tile(nm)
    ones_f, ones_t = G["ones_f"], G["ones_t"]
    sc.op("pool", lambda e: e.affine_select(out=G["trif"][:], in_=ones_f[:], pattern=[[1, 128]], compare_op=ALU.is_ge,
                                            fill=0.0, base=0, channel_multiplier=-1),
          reads=[ones_t], writes=[G["trif_t"]])
    sc.op("pool", lambda e: e.affine_select(out=G["trib"][:], in_=ones_f[:], pattern=[[-1, 128]], compare_op=ALU.is_ge,
                                            fill=0.0, base=0, channel_multiplier=1),
          reads=[ones_t], writes=[G["trib_t"]])
    sc.op("pool", lambda e: e.affine_select(out=G["trifs"][:], in_=ones_f[:], pattern=[[1, 128]], compare_op=ALU.is_gt,
                                            fill=0.0, base=0, channel_multiplier=-1),
          reads=[ones_t], writes=[G["trifs_t"]])
    sc.op("pool", lambda e: e.affine_select(out=G["tribs"][:], in_=ones_f[:], pattern=[[-1, 128]], compare_op=ALU.is_gt,
                                            fill=0.0, base=0, channel_multiplier=1),
          reads=[ones_t], writes=[G["tribs_t"]])
    sc.op("pool", lambda e: e.tensor_copy(out=G["trif64"][:], in_=G["trif"][:]), reads=[G["trif_t"]], writes=[G["trif64_t"]])
    sc.op("pool", lambda e: e.memset(G["trif64"][0:64, 64:128], 0.0), reads=[G["trif64_t"]], writes=[G["trif64_t"]])
    sc.op("pool", lambda e: e.tensor_copy(out=G["trib64"][:], in_=G["trib"][:]), reads=[G["trib_t"]], writes=[G["trib64_t"]])
    sc.op("pool", lambda e: e.memset(G["trib64"][64:128, 0:64], 0.0), reads=[G["trib64_t"]], writes=[G["trib64_t"]])
    sc.op("pool", lambda e: e.tensor_scalar(out=G["mcf64"][:], in0=G["trif64"][:], scalar1=-1.0 / 16.0, scalar2=None,
                                            op0=ALU.mult), reads=[G["trif64_t"]], writes=[G["mcf64_t"]])
    sc.op("pool", lambda e: e.tensor_scalar(out=G["mcb64"][:], in0=G["trib64"][:], scalar1=-1.0 / 16.0, scalar2=None,
                                            op0=ALU.mult), reads=[G["trib64_t"]], writes=[G["mcb64_t"]])


def phase_C(P, sc, G, U, YB, prm, l):
    nc = P.nc
    ident = G["ident"]
    with contextlib.ExitStack() as ph:
        qT = P.sb(ph, "C_qT", [128, 4, S], BF16)
        kT = P.sb(ph, "C_kT", [128, 4, S], BF16)
        qT_t = sc.tile("C_qT")
        kT_t = sc.tile("C_kT")
        ob = P.sb(ph, "C_ob", [128, NT, 1024], BF16)
        ob_t = sc.tiles_n("C_ob", NT)
        gaX = P.sb(ph, "C_gaX", [32, S], F32)
        gaX_t = sc.tile("C_gaX")
        a2X = [P.sb(ph, "C_a2X%d" % d, [32, 512], F32) for d in range(2)]
        a2X_t = sc.tiles_n("C_a2X", 2)
        nwb = P.sb(ph, "C_nwb", [128, 256], F32)
        nwb_t = sc.tile("C_nwb")
        Sf = P.sb(ph, "C_Sf", [128, 4, 256], F32)
        Sf_t = sc.tile("C_Sf")
        yst = P.sb(ph, "C_yst", [128, 8, 256], BF16)
        yst_t = sc.tile("C_yst")
        R = lambda name, shape, dt, n, psum=False: Ring(P, sc, ph, "C_" + name, shape, dt, n, psum)
        r_Sb = R("Sb", [128, 4, 256], BF16, 3)
        r_v = R("v", [128, 1024], BF16, 2)
        r_gg = R("gg", [128, 4, 1024], BF16, 1)
        r_e1 = R("e1", [128, 512], F32, 1)
        r_bs = R("bs", [128, 4, 128], F32, 1)
        r_eb = R("eb", [128, 4, 128], F32, 1)
        r_enb = R("enb", [128, 4, 128], F32, 1)
        r_ew = R("ew", [128, 4, 128], F32, 1)
        r_ed = R("ed", [128, 4, 2], F32, 3)
        r_qd = R("qd", [128, 4, 128], BF16, 2)
        r_kd = R("kd", [128, 4, 128], BF16, 2)
        r_kw = R("kw", [128, 4, 128], BF16, 1)
        r_kwt = R("kwt", [128, 4, 128], BF16, 2)
        r_am = R("am", [128, 4, 128], BF16, 2)
        r_oa = R("oa", [128, 1024], F32, 1)
        r_sg = R("sg", [128, 4, 1024], BF16, 1)
        r_jk = R("jk", [128, 256], BF16, 1)
        r_ss = R("ss", [128, 8], F32, 2)
        r_y = R("y", [128, 1024], BF16, 2)
        r_gp = R("gp", [128, 512], F32, 1, True)
        r_bT = R("bT", [128, 4, 128], F32, 1, True)
        r_att = R("att", [128, 4, 128], F32, 1, True)
        r_kwp = R("kwp", [128, 4, 128], BF16, 1, True)
        r_st = R("st", [128, 4, 256], F32, 1, True)
        r_o = R("o", [128, 4, 256], F32, 1, True)
        rings = [r_Sb, r_v, r_gg, r_e1, r_bs, r_eb, r_enb, r_ew, r_ed, r_qd, r_kd, r_kw, r_kwt, r_am, r_oa, r_sg,
                 r_jk, r_ss, r_y, r_gp, r_bT, r_att, r_kwp, r_st, r_o]
        tiles = [qT_t, kT_t, nwb_t, yst_t, gaX_t, Sf_t] + ob_t + a2X_t
        for r in rings:
            tiles += r.t
        sc.dma("sp", qT[:], U["gq"].rearrange("(h p) t -> p h t", p=128), owner=qT_t, reads=[G["dram_t"]["gq"]],
               writes=[qT_t])
        sc.dma("sp", kT[:], U["gk"].rearrange("(h p) t -> p h t", p=128), owner=kT_t, reads=[G["dram_t"]["gk"]],
               writes=[kT_t])
        sc.dma("sp", nwb[:], prm["gla_norm_w"][l].partition_broadcast(128), owner=nwb_t, writes=[nwb_t])
        for d in range(2):
            a2 = prm["gla_a2_f" if d == 0 else "gla_a2_b"][l]
            bi = prm["gla_a2_bias_f" if d == 0 else "gla_a2_bias_b"][l]
            sc.dma("sp", a2X[d][0:16, :], a2, owner=a2X_t[d], writes=[a2X_t[d]])
            sc.dma("sp", a2X[d][16:17, :], bi.rearrange("(o n) -> o n", o=1), owner=a2X_t[d], writes=[a2X_t[d]], part=True)

        def gla_pass(d):
            fwd = (d == 0)
            mc, mc_t = (G["mcf64"], G["mcf64_t"]) if fwd else (G["mcb64"], G["mcb64_t"])
            ma, ma_t = (G["trif64"], G["trif64_t"]) if fwd else (G["trib64"], G["trib64_t"])
            lc0 = 63 if fwd else 0
            sc.op("pool", lambda e: e.memset(gaX[:], 1.0), writes=[gaX_t])
            sc.dma("sp", gaX[0:16, :], U["ga"][16 * d:16 * d + 16, :], owner=gaX_t, reads=[G["dram_t"]["ga"]],
                   writes=[gaX_t])
            sc.op("pool", lambda e: e.memset(Sf[:], 0.0), writes=[Sf_t])
            sb0, sb0_t = r_Sb.next()
            sc.op("pool", lambda e, sb0=sb0: e.memset(sb0[:], 0.0), writes=[sb0_t])
            cur = [(sb0, sb0_t)]
            sgcur = [None]
            order = list(range(NT)) if fwd else list(range(NT - 1, -1, -1))
            chunks = (0, 1) if fwd else (1, 0)

            def stage1(i):
                tsl = slice(i * 128, (i + 1) * 128)
                v, v_t = r_v.next()
                sc.dma("sp", v[:], U["gv"][tsl, :], owner=v_t, reads=[G["dram_t"]["gv"]], writes=[v_t])
                gp, gp_t = r_gp.next()
                sc.op("pe", lambda e, gp=gp, tsl=tsl: e.matmul(gp[:], lhsT=gaX[0:17, tsl], rhs=a2X[d][0:17, :],
                                                               start=True, stop=True),
                      reads=[gaX_t, a2X_t[d]], writes=[gp_t])
                e1, e1_t = r_e1.next()
                sc.op("act", lambda e, e1=e1, gp=gp: e.activation(out=e1[:], in_=gp[:], func=AF.Exp, scale=-1.0),
                      reads=[gp_t], writes=[e1_t])
                gn, gn_t = e1, e1_t
                sc.op("act", lambda e, gn=gn, e1=e1: e.activation(out=gn[:], in_=e1[:], func=AF.Ln, bias=G["one"][:, 0:1]),
                      reads=[e1_t, G["one_t"]], writes=[gn_t])
                bT, bT_t = r_bT.next()
                for h in range(4):
                    sc.op("pe", lambda e, bT=bT, gn=gn, h=h: e.matmul(bT[:, h, :], lhsT=gn[:, h * 128:(h + 1) * 128], rhs=mc[:],
                                                                     start=True, stop=True, skip_group_check=True),
                          reads=[gn_t, mc_t], writes=[bT_t], part=(h > 0))
                bs, bs_t = r_bs.next()
                sc.op("act", lambda e, bs=bs, bT=bT: e.copy(out=bs[:], in_=bT[:]), reads=[bT_t], writes=[bs_t])
                eb, eb_t = r_eb.next()
                sc.op("act", lambda e, eb=eb, bs=bs: e.activation(out=eb[:], in_=bs[:], func=AF.Exp), reads=[bs_t], writes=[eb_t])
                enb, enb_t = r_enb.next()
                sc.op("act", lambda e, enb=enb, bs=bs: e.activation(out=enb[:], in_=bs[:], func=AF.Exp, scale=-1.0),
                      reads=[bs_t], writes=[enb_t])
                ed, ed_t = r_ed.next()
                sc.op("act", lambda e, ed=ed, bs=bs: e.activation(
                    out=ed[:], in_=bs[:].rearrange("p h (c l) -> p h c l", c=2)[:, :, :, lc0], func=AF.Exp),
                    reads=[bs_t], writes=[ed_t])
                qd, qd_t = r_qd.next()
                sc.op("dve", lambda e, qd=qd, tsl=tsl, eb=eb: e.scalar_tensor_tensor(
                    out=qd[:], in0=qT[:, :, tsl], scalar=128.0 ** -0.5, in1=eb[:], op0=ALU.mult, op1=ALU.mult),
                    reads=[qT_t, eb_t], writes=[qd_t])
                kd, kd_t = r_kd.next()
                sc.op("dve", lambda e, kd=kd, tsl=tsl, enb=enb: e.tensor_tensor(
                    out=kd[:], in0=kT[:, :, tsl], in1=enb[:], op=ALU.mult), reads=[kT_t, enb_t], writes=[kd_t])
                ew, ew_t = r_ew.next()
                sc.op("dve", lambda e, ew=ew, enb=enb, ed=ed: e.tensor_tensor(
                    out=ew[:].rearrange("p h (c l) -> p (h c) l", c=2), in0=enb[:].rearrange("p h (c l) -> p (h c) l", c=2),
                    in1=ed[:].rearrange("p h c -> p (h c)").unsqueeze(2).to_broadcast([128, 8, 64]), op=ALU.mult),
                    reads=[enb_t, ed_t], writes=[ew_t])
                kw, kw_t = r_kw.next()
                sc.op("dve", lambda e, kw=kw, tsl=tsl, ew=ew: e.tensor_tensor(
                    out=kw[:], in0=kT[:, :, tsl], in1=ew[:], op=ALU.mult), reads=[kT_t, ew_t], writes=[kw_t])
                kwp, kwp_t = r_kwp.next()
                for h in range(4):
                    sc.op("pe", lambda e, kwp=kwp, kw=kw, h=h: e.transpose(out=kwp[:, h, :], in_=kw[:, h, :], identity=ident[:]),
                          reads=[kw_t, G["ident_t"]], writes=[kwp_t], part=(h > 0))
                kwt, kwt_t = r_kwt.next()
                sc.op("act", lambda e, kwt=kwt, kwp=kwp: e.copy(out=kwt[:], in_=kwp[:]), reads=[kwp_t], writes=[kwt_t])
                att, att_t = r_att.next()
                for h in range(4):
                    sc.op("pe", lambda e, att=att, kd=kd, qd=qd, h=h: e.matmul(att[:, h, :], lhsT=kd[:, h, :], rhs=qd[:, h, :],
                                                                            start=True, stop=True, skip_group_check=True),
                          reads=[kd_t, qd_t], writes=[att_t], part=(h > 0))
                am, am_t = r_am.next()
                sc.op("dve", lambda e, am=am, att=att: e.tensor_tensor(
                    out=am[:], in0=att[:], in1=ma[:].unsqueeze(1).to_broadcast([128, 4, 128]), op=ALU.mult),
                    reads=[att_t, ma_t], writes=[am_t])
                return (i, tsl, v, v_t, qd, qd_t, kwt, kwt_t, ed, ed_t, am, am_t)

            def stage23(ctx):
                (i, tsl, v, v_t, qd, qd_t, kwt, kwt_t, ed, ed_t, am, am_t) = ctx
                sbs = [cur[0]]
                for ci, c in enumerate(chunks):
                    cs = slice(c * 64, (c + 1) * 64)
                    st, st_t = r_st.next()
                    for h in range(4):
                        sc.op("pe", lambda e, st=st, kwt=kwt, cs=cs, v=v, h=h: e.matmul(
                            st[:, h, :], lhsT=kwt[cs, h, :], rhs=v[cs, h * 256:(h + 1) * 256], start=True, stop=True,
                            skip_group_check=True),
                            reads=[kwt_t, v_t], writes=[st_t], part=(h > 0))
                    for h in range(4):
                        sc.op("dve", lambda e, st=st, h=h, ed=ed, c=c: e.scalar_tensor_tensor(
                            out=Sf[:, h, :], in0=Sf[:, h, :], scalar=ed[:, h, c:c + 1], in1=st[:, h, :], op0=ALU.mult,
                            op1=ALU.add),
                            reads=[st_t, ed_t, Sf_t], writes=[Sf_t])
                    nb, nb_t = r_Sb.next()
                    sc.op("act", lambda e, nb=nb: e.copy(out=nb[:], in_=Sf[:]), reads=[Sf_t], writes=[nb_t])
                    sbs.append((nb, nb_t))
                o, o_t = r_o.next()
                for h in range(4):
                    sc.op("pe", lambda e, o=o, am=am, v=v, h=h: e.matmul(o[:, h, :], lhsT=am[:, h, :],
                                                                       rhs=v[:, h * 256:(h + 1) * 256],
                                                                       start=True, stop=False, skip_group_check=True),
                          reads=[am_t, v_t], writes=[o_t], part=(h > 0))
                    for ci, c in enumerate(chunks):
                        cs = slice(c * 64, (c + 1) * 64)
                        sbv, sbv_t = sbs[ci]
                        sc.op("pe", lambda e, o=o, qd=qd, cs=cs, sbv=sbv, ci=ci, h=h: e.matmul(
                            o[cs, h, :], lhsT=qd[:, h, cs], rhs=sbv[:, h, :], start=False, stop=(ci == 1),
                            skip_group_check=True),
                            reads=[qd_t, sbv_t], writes=[o_t], part=True)
                cur[0] = sbs[2]
                if not fwd:
                    for hb in range(2):
                        sc.op("act", lambda e, o=o, hb=hb, i=i: e.copy(
                            out=ob[:, i, hb * 512:(hb + 1) * 512], in_=o[:, 2 * hb:2 * hb + 2, :].rearrange("p a b -> p (a b)")),
                            reads=[o_t], writes=[ob_t[i]], part=(hb > 0))
                    return
                oa, oa_t = r_oa.next()
                ss, ss_t = r_ss.next()
                for hb in range(2):
                    sc.op("dve", lambda e, oa=oa, o=o, hb=hb, i=i: e.tensor_tensor(
                        out=oa[:, hb * 512:(hb + 1) * 512], in0=o[:, 2 * hb:2 * hb + 2, :].rearrange("p a b -> p (a b)"),
                        in1=ob[:, i, hb * 512:(hb + 1) * 512], op=ALU.add),
                        reads=[o_t, ob_t[i]], writes=[oa_t], part=(hb > 0))
                for h in range(4):
                    hs = slice(h * 256, (h + 1) * 256)
                    jk, jk_t = r_jk.next()
                    sc.op("dve", lambda e, jk=jk, oa=oa, hs=hs, ss=ss, h=h: e.scalar_tensor_tensor(
                        out=jk[:], in0=oa[:, hs], scalar=1.0, in1=oa[:, hs], op0=ALU.mult, op1=ALU.mult,
                        accum_out=ss[:, h:h + 1]), reads=[oa_t], writes=[jk_t, ss_t])
                if i % 4 == 0:
                    gg, gg_t = r_gg.next()
                    sc.dma("sp", gg[:], U["gg"][i * 128:(i + 4) * 128, :].rearrange("(j p) c -> p j c", p=128), owner=gg_t,
                           reads=[G["dram_t"]["gg"]], writes=[gg_t])
                    sg, sg_t = r_sg.next()
                    sc.op("act", lambda e, sg=sg, gg=gg: e.activation(out=sg[:], in_=gg[:], func=AF.Silu),
                          reads=[gg_t], writes=[sg_t])
                    sgcur[0] = (sg, sg_t)
                sg, sg_t = sgcur[0]
                sgn = sg[:, i % 4, :]
                sgn_t = sg_t
                sc.op("pool", lambda e, sgn=sgn: e.tensor_tensor(
                    out=sgn.rearrange("p (h v) -> p h v", h=4), in0=sgn.rearrange("p (h v) -> p h v", h=4),
                    in1=nwb[:].unsqueeze(1).to_broadcast([128, 4, 256]), op=ALU.mult),
                    reads=[sg_t, nwb_t], writes=[sg_t])
                sc.op("dve", lambda e, ss=ss: e.tensor_scalar(out=ss[:, 4:8], in0=ss[:, 0:4], scalar1=1.0 / 256.0, scalar2=EPS,
                                                              op0=ALU.mult, op1=ALU.add), reads=[ss_t], writes=[ss_t])
                sc.op("pool", lambda e, ss=ss: e.tensor_tensor(out=ss[:, 0:4], in0=ss[:, 4:8], in1=G["neghalf"][:, 0:4],
                                                               op=ALU.pow), reads=[ss_t, G["neghalf_t"]], writes=[ss_t])
                sc.op("dve", lambda e, oa=oa, ss=ss: e.tensor_tensor(
                    out=oa[:].rearrange("p (h v) -> p h v", h=4), in0=oa[:].rearrange("p (h v) -> p h v", h=4),
                    in1=ss[:, 0:4].unsqueeze(2).to_broadcast([128, 4, 256]), op=ALU.mult),
                    reads=[oa_t, ss_t], writes=[oa_t])
                y, y_t = r_y.next()
                sc.op("dve", lambda e, y=y, oa=oa, sgn=sgn: e.tensor_tensor(out=y[:], in0=oa[:], in1=sgn, op=ALU.mult),
                      reads=[oa_t, sgn_t], writes=[y_t])
                return (y, y_t, i)

            def stage3(c3):
                if c3 is None:
                    return
                (y, y_t, i) = c3
                emit_yT(P, sc, G, r_kwp, y, y_t, yst, yst_t, i, YB["gla"], G["dram_t"]["yb_gla"], gsz=2)

            prev = None
            prev3 = None
            for i in order:
                ctx = stage1(i)
                if prev is not None:
                    n3 = stage23(prev)
                    stage3(prev3)
                    prev3 = n3
                prev = ctx
            n3 = stage23(prev)
            stage3(prev3)
            stage3(n3)

        gla_pass(1)
        if "dbg_ob" in P.dbg:
            dob = P.dram("dbg_ob", [S, 1024], BF16)
            dt_ = sc.tile("dbg_ob")
            sc.dma("sp", dob.rearrange("(i p) c -> p i c", p=128), ob[:], owner=ob_t[0], reads=ob_t, writes=[dt_])
        gla_pass(0)
        sc.barrier(release=tiles)


def phase_B(P, sc, G, U, YB, prm, l):
    nc = P.nc
    ident = G["ident"]
    ybw = G["ybw"]
    ybw_t = G["dram_t"]["ybw"]
    with contextlib.ExitStack() as ph:
        xtok = P.sb(ph, "B_xtok", [128, NT, 1280], BF16)
        xtok_t = sc.tiles_n("B_xtok", NT)
        BT = P.sb(ph, "B_BT", [128, 2, S], BF16)
        CT = P.sb(ph, "B_CT", [128, 2, S], BF16)
        BT_t = sc.tiles_n("B_BT", 2)
        CT_t = sc.tiles_n("B_CT", 2)
        dtv = P.sb(ph, "B_dtv", [128, NT, 32], F32)
        av = P.sb(ph, "B_av", [128, NT, 32], F32)
        dtv_t = sc.tile("B_dtv")
        av_t = sc.tile("B_av")
        rows = P.sb(ph, "B_rows", [128, 4, 32], F32)
        rows_t = sc.tile("B_rows")
        nwb = P.sb(ph, "B_nwb", [128, 1024], F32)
        nwb_t = sc.tile("B_nwb")
        tiles = xtok_t + BT_t + CT_t + [dtv_t, av_t, rows_t, nwb_t]
        with contextlib.ExitStack() as s1:
            cwr = P.sb(s1, "B_cwr", [72, 128], F32)
            cwr_t = sc.tile("B_cwr")
            cw = P.sb(s1, "B_cw", [128, 72], F32)
            cw_t = sc.tile("B_cw")
            cwp = P.ps(s1, "B_cwp", [128, 72], F32)
            cwp_t = sc.tile("B_cwp")
            identf = P.sb(s1, "B_identf", [128, 128], F32)
            identf_t = sc.tile("B_identf")
            xc = [P.sb(s1, "B_xc%d" % i, [128, S + 4], BF16) for i in range(2)]
            xc_t = sc.tiles_n("B_xc", 2)
            dg = [P.sb(s1, "B_dg%d" % i, [128, 5, 128], BF16) for i in range(2)]
            dg_t = sc.tiles_n("B_dg", 2)
            cacc = [P.ps(s1, "B_cacc%d" % i, [128, 512], F32) for i in range(2)]
            cacc_t = sc.tiles_n("B_cacc", 2)
            xa = [P.sb(s1, "B_xa%d" % i, [128, S], BF16) for i in range(2)]
            xa_t = sc.tiles_n("B_xa", 2)
            tp = [P.ps(s1, "B_tp%d" % i, [128, 4, 128], BF16) for i in range(2)]
            tp_t = sc.tiles_n("B_tp", 2)
            tl1 = [cwr_t, cw_t, cwp_t, identf_t] + dg_t + cacc_t + xc_t + xa_t + tp_t
            sc.op("pool", lambda e: e.affine_select(out=identf[:], in_=G["ones_f"][:], pattern=[[-1, 128]],
                                                    compare_op=ALU.is_equal, fill=0.0, base=0, channel_multiplier=1),
                  reads=[G["ones_t"]], writes=[identf_t])
            sc.dma("sp", cwr[0:60, :], prm["ssd_conv_w"][l].rearrange("k (c p) -> (k c) p", p=128), owner=cwr_t, writes=[cwr_t])
            sc.dma("sp", cwr[60:72, :], prm["ssd_conv_b"][l].rearrange("(c p) -> c p", p=128), owner=cwr_t, writes=[cwr_t],
                   part=True)
            sc.op("pe", lambda e: e.transpose(out=cwp[:], in_=cwr[:], identity=identf[0:72, 0:72]),
                  reads=[cwr_t, identf_t], writes=[cwp_t])
            sc.op("act", lambda e: e.copy(out=cw[:], in_=cwp[:]), reads=[cwp_t], writes=[cw_t])
            for b in range(2):
                sc.op("pool", lambda e, b=b: e.memset(xc[b][:, 0:2], 0.0), writes=[xc_t[b]])
                sc.op("pool", lambda e, b=b: e.memset(xc[b][:, S + 2:S + 4], 0.0), writes=[xc_t[b]], part=True)
            sc.dma("sp", dtv[:], U["dt"].rearrange("(i p) c -> p i c", p=128), owner=dtv_t, reads=[G["dram_t"]["dt"]],
                   writes=[dtv_t])
            for k, nm in enumerate(("ssd_dt_bias_f", "ssd_dt_bias_b")):
                sc.dma("sp", rows[:, 0, 16 * k:16 * k + 16], prm[nm][l].partition_broadcast(128), owner=rows_t,
                       writes=[rows_t], part=True)
            for k, nm in enumerate(("ssd_a_log_f", "ssd_a_log_b")):
                sc.dma("sp", rows[:, 1, 16 * k:16 * k + 16], prm[nm][l].partition_broadcast(128), owner=rows_t,
                       writes=[rows_t], part=True)
            sc.dma("sp", rows[:, 2, 0:16], prm["ssd_d"][l].partition_broadcast(128), owner=rows_t, writes=[rows_t], part=True)
            sc.dma("sp", nwb[:], prm["ssd_norm_w"][l].partition_broadcast(128), owner=nwb_t, writes=[nwb_t])
            sc.op("dve", lambda e: e.tensor_tensor(out=dtv[:], in0=dtv[:], in1=rows[:, 0:1, :].to_broadcast([128, NT, 32]),
                                                   op=ALU.add), reads=[dtv_t, rows_t], writes=[dtv_t])
            sc.op("act", lambda e: e.activation(out=dtv[:], in_=dtv[:], func=AF.Exp), reads=[dtv_t], writes=[dtv_t])
            sc.op("act", lambda e: e.activation(out=dtv[:], in_=dtv[:], func=AF.Ln, bias=G["one"][:, 0:1]),
                  reads=[dtv_t, G["one_t"]], writes=[dtv_t])
            sc.op("act", lambda e: e.activation(out=rows[:, 3, :], in_=rows[:, 1, :], func=AF.Exp), reads=[rows_t],
                  writes=[rows_t])
            sc.op("dve", lambda e: e.scalar_tensor_tensor(out=av[:], in0=dtv[:], scalar=-1.0,
                                                          in1=rows[:, 3:4, :].to_broadcast([128, NT, 32]),
                                                          op0=ALU.mult, op1=ALU.mult),
                  reads=[dtv_t, rows_t], writes=[av_t])
            tpc = 0
            for c in range(12):
                b = c % 2
                sc.dma("sp", xc[b][:, 2:S + 2], U["xbc"][c * 128:(c + 1) * 128, :], owner=xc_t[b],
                       reads=[G["dram_t"]["xbc"]], writes=[xc_t[b]], part=True)
                dgb = c % 2
                for k in range(5):
                    sc.op("dve", lambda e, dgb=dgb, k=k, c=c: e.tensor_scalar(
                        out=dg[dgb][:, k, :], in0=identf[:], scalar1=cw[:, k * 12 + c:k * 12 + c + 1], scalar2=None,
                        op0=ALU.mult), reads=[identf_t, cw_t], writes=[dg_t[dgb]], part=(k > 0))
                if c < 10:
                    xo, xo_t = xa[b], xa_t[b]
                    xsl = lambda tb: xa[b][:, tb * 512:(tb + 1) * 512]
                else:
                    xo_t = CT_t[c - 10]
                    xsl = lambda tb, c=c: CT[:, c - 10, tb * 512:(tb + 1) * 512]
                for tb in range(4):
                    ca, ca_t = cacc[(4 * c + tb) % 2], cacc_t[(4 * c + tb) % 2]
                    for k in range(5):
                        sc.op("pe", lambda e, ca=ca, dgb=dgb, k=k, b=b, tb=tb: e.matmul(
                            ca[:], lhsT=dg[dgb][:, k, :], rhs=xc[b][:, k + tb * 512:k + tb * 512 + 512],
                            start=(k == 0), stop=(k == 4)),
                            reads=[dg_t[dgb], xc_t[b]], writes=[ca_t], part=(k > 0))
                    sc.op("act", lambda e, ca=ca, o_ap=xsl(tb), c=c: e.activation(
                        out=o_ap, in_=ca[:], func=AF.Silu, bias=cw[:, 60 + c:61 + c]),
                        reads=[ca_t, cw_t], writes=[xo_t], part=(tb > 0))
                if c in (8, 9):
                    sc.op("pool", lambda e, b=b, c=c: e.tensor_copy(out=BT[:, c - 8, :], in_=xa[b][:]),
                          reads=[xa_t[b]], writes=[BT_t[c - 8]])
                if c < 10:
                    for i0 in range(0, NT, 4):
                        tb_ = tpc % 2
                        tpc += 1
                        for j in range(4):
                            i = i0 + j
                            sc.op("pe", lambda e, tb_=tb_, j=j, b=b, i=i: e.transpose(
                                out=tp[tb_][:, j, :], in_=xa[b][:, i * 128:(i + 1) * 128], identity=ident[:]),
                                reads=[xa_t[b], G["ident_t"]], writes=[tp_t[tb_]], part=(j > 0))
                        eng = "act" if (tpc % 2) else "pool"
                        if eng == "act":
                            sc.op("act", lambda e, tb_=tb_, i0=i0, c=c: e.copy(
                                out=xtok[:, i0:i0 + 4, c * 128:(c + 1) * 128], in_=tp[tb_][:]),
                                reads=[tp_t[tb_]], writes=xtok_t[i0:i0 + 4], part=True)
                        else:
                            sc.op("dve", lambda e, tb_=tb_, i0=i0, c=c: e.tensor_copy(
                                out=xtok[:, i0:i0 + 4, c * 128:(c + 1) * 128], in_=tp[tb_][:]),
                                reads=[tp_t[tb_]], writes=xtok_t[i0:i0 + 4], part=True)
            sc.barrier(release=tl1)
        with contextlib.ExitStack() as s2:
            R = lambda name, shape, dt, n, psum=False: Ring(P, sc, s2, "B_" + name, shape, dt, n, psum)
            Sf = P.sb(s2, "B_Sf", [128, 2, 512], F32)
            Sf_t = sc.tiles_n("B_Sf", 2)
            Sbx = P.sb(s2, "B_Sb", [128, 2, 2, 512], BF16)
            r_Sb = [Ring(P, sc, s2, "B_Sb%d" % g, None, None, 2, views=[Sbx[:, g, k, :] for k in range(2)]) for g in range(2)]
            r_cb = R("cb", [128, 128], F32, 1, True)
            r_seg = R("seg", [128, 512], F32, 2, True)
            r_sm = R("sm", [128, 3, 16], F32, 1, True)
            r_yd = R("yd", [128, 512], F32, 1, True)
            r_stp = R("stp", [128, 512], F32, 1, True)
            r_yo = R("yo", [128, 512], F32, 1, True)
            r_tp = R("tp2", [128, 4, 128], BF16, 1, True)
            r_cbm = R("cbm", [128, 128], F32, 2)
            r_am = R("am", [128, 4, 128], F32, 4)
            r_dec = R("dec", [128, 4, 128], F32, 2)
            r_mt = R("mt", [128, 4, 128], BF16, 4)
            r_ea = R("ea", [128, 3, 16], F32, 2)
            r_xdt = R("xdt", [128, 1024], BF16, 1)
            r_xw = R("xw", [128, 1024], BF16, 1)
            r_t = R("t", [128, 512], F32, 2)
            r_ybl = R("ybl", [128, 1024], BF16, 2)
            r_yf = R("yf", [128, 1024], F32, 1)
            r_z = R("z", [128, 4, 1024], BF16, 1)
            r_jk = R("jk", [128, 512], F32, 1)
            r_ss = R("ss", [128, 4], F32, 2)
            r_y = R("y", [128, 1024], BF16, 2)
            yst = P.sb(s2, "B_yst", [128, 8, 256], BF16)
            yst_t = sc.tile("B_yst")
            rings = [r_cb, r_seg, r_sm, r_yd, r_stp, r_yo, r_tp, r_cbm, r_am, r_dec, r_mt, r_ea, r_xdt, r_xw, r_t, r_ybl,
                     r_yf, r_z, r_jk, r_ss, r_y] + r_Sb
            tl2 = Sf_t + [yst_t]
            for r in rings:
                tl2 += r.t

            def ssd_pass(d):
                fwd = (d == 0)
                tri_in, tri_in_t = (G["trif"], G["trif_t"]) if fwd else (G["trib"], G["trib_t"])
                tri_st, tri_st_t = (G["tribs"], G["tribs_t"]) if fwd else (G["trifs"], G["trifs_t"])
                cur = []
                for g in range(2):
                    sc.op("pool", lambda e, g=g: e.memset(Sf[:, g, :], 0.0), writes=[Sf_t[g]])
                    sb0, sb0_t = r_Sb[g].next()
                    sc.op("pool", lambda e, sb0=sb0: e.memset(sb0, 0.0), writes=[sb0_t])
                    cur.append((sb0, sb0_t))
                order = list(range(NT)) if fwd else list(range(NT - 1, -1, -1))
                zcur = [None]

                def tileA(i):
                    tsl = slice(i * 128, (i + 1) * 128)
                    acol = av[:, i, 16 * d:16 * d + 16]
                    ams = []
                    for u in range(4):
                        h0 = u * 4
                        am, am_t = r_am.next()
                        for hh in range(4):
                            sc.op("act", lambda e, am=am, i=i, h0=h0, hh=hh: e.activation(
                                out=am[:, hh, :], in_=tri_in[:], func=AF.Copy,
                                scale=av[:, i, 16 * d + h0 + hh:16 * d + h0 + hh + 1]),
                                reads=[tri_in_t, av_t], writes=[am_t], part=(hh > 0))
                        ams.append((am, am_t))
                    sm, sm_t = r_sm.next()
                    sc.op("pe", lambda e, sm=sm, acol=acol: e.matmul(sm[:, 0, :], lhsT=tri_in[:], rhs=acol, start=True, stop=True),
                          reads=[tri_in_t, av_t], writes=[sm_t])
                    sc.op("pe", lambda e, sm=sm, acol=acol: e.matmul(sm[:, 1, :], lhsT=tri_st[:], rhs=acol, start=True, stop=True),
                          reads=[tri_st_t, av_t], writes=[sm_t], part=True)
                    sc.op("pe", lambda e, sm=sm, acol=acol: e.matmul(sm[:, 2, :], lhsT=G["ones_f"][:], rhs=acol, start=True,
                                                                     stop=True),
                          reads=[G["ones_t"], av_t], writes=[sm_t], part=True)
                    ea, ea_t = r_ea.next()
                    sc.op("act", lambda e, ea=ea, sm=sm: e.activation(out=ea[:], in_=sm[:], func=AF.Exp), reads=[sm_t],
                          writes=[ea_t])
                    xdt, xdt_t = r_xdt.next()
                    sc.op("dve", lambda e, xdt=xdt, i=i: e.tensor_tensor(
                        out=xdt[:].rearrange("p (h q) -> p h q", q=64), in0=xtok[:, i, 0:1024].rearrange("p (h q) -> p h q", q=64),
                        in1=dtv[:, i, 16 * d:16 * d + 16].unsqueeze(2).to_broadcast([128, 16, 64]), op=ALU.mult),
                        reads=[xtok_t[i], dtv_t], writes=[xdt_t])
                    xw, xw_t = r_xw.next()
                    sc.op("dve", lambda e, xw=xw, xdt=xdt, ea=ea: e.tensor_tensor(
                        out=xw[:].rearrange("p (h q) -> p h q", q=64), in0=xdt[:].rearrange("p (h q) -> p h q", q=64),
                        in1=ea[:, 1, :].unsqueeze(2).to_broadcast([128, 16, 64]), op=ALU.mult),
                        reads=[xdt_t, ea_t], writes=[xw_t])
                    ybl, ybl_t = r_ybl.next()
                    if fwd:
                        sc.dma("sp", ybl[:], ybw[tsl, :], owner=ybl_t, reads=[ybw_t], writes=[ybl_t])
                        yf, yf_t = r_yf.next()
                    ts = []
                    for g in range(2):
                        stp, stp_t = r_stp.next()
                        sc.op("pe", lambda e, stp=stp, i=i, g=g, xw=xw: e.matmul(
                            stp[:], lhsT=xtok[:, i, 1024 + g * 128:1024 + (g + 1) * 128], rhs=xw[:, g * 512:(g + 1) * 512],
                            start=True, stop=True), reads=[xtok_t[i], xw_t], writes=[stp_t])
                        yo, yo_t = r_yo.next()
                        sbv, sbv_t = cur[g]
                        sc.op("pe", lambda e, yo=yo, g=g, tsl=tsl, sbv=sbv: e.matmul(yo[:], lhsT=CT[:, g, tsl], rhs=sbv,
                                                                                    start=True, stop=True),
                              reads=[CT_t[g], sbv_t], writes=[yo_t])
                        sc.op("pool", lambda e, g=g, ea=ea: e.tensor_tensor(
                            out=Sf[:, g, :].rearrange("p (h q) -> p h q", q=64), in0=Sf[:, g, :].rearrange("p (h q) -> p h q", q=64),
                            in1=ea[:, 2, g * 8:(g + 1) * 8].unsqueeze(2).to_broadcast([128, 8, 64]), op=ALU.mult),
                            reads=[Sf_t[g], ea_t], writes=[Sf_t[g]])
                        sc.op("dve", lambda e, g=g, stp=stp: e.tensor_tensor(out=Sf[:, g, :], in0=Sf[:, g, :], in1=stp[:],
                                                                            op=ALU.add),
                              reads=[Sf_t[g], stp_t], writes=[Sf_t[g]])
                        nb, nb_t = r_Sb[g].next()
                        sc.op("act", lambda e, nb=nb, g=g: e.copy(out=nb, in_=Sf[:, g, :]), reads=[Sf_t[g]], writes=[nb_t])
                        cur[g] = (nb, nb_t)
                        t, t_t = r_t.next()
                        sc.op("dve", lambda e, t=t, yo=yo, ea=ea, g=g: e.tensor_tensor(
                            out=t[:].rearrange("p (h q) -> p h q", q=64), in0=yo[:].rearrange("p (h q) -> p h q", q=64),
                            in1=ea[:, 0, g * 8:(g + 1) * 8].unsqueeze(2).to_broadcast([128, 8, 64]), op=ALU.mult),
                            reads=[yo_t, ea_t], writes=[t_t])
                        ts.append((t, t_t))
                    cbms = []
                    for g in range(2):
                        cb, cb_t = r_cb.next()
                        sc.op("pe", lambda e, cb=cb, g=g, tsl=tsl: e.matmul(cb[:], lhsT=BT[:, g, tsl], rhs=CT[:, g, tsl],
                                                                           start=True, stop=True),
                              reads=[BT_t[g], CT_t[g]], writes=[cb_t])
                        cbm, cbm_t = r_cbm.next()
                        sc.op("dve", lambda e, cbm=cbm, cb=cb: e.tensor_tensor(out=cbm[:], in0=cb[:], in1=tri_in[:], op=ALU.mult),
                              reads=[cb_t, tri_in_t], writes=[cbm_t])
                        cbms.append((cbm, cbm_t))
                    mts = []
                    for pair in range(2):
                        segs = []
                        for u in (2 * pair, 2 * pair + 1):
                            am, am_t = ams[u]
                            seg, seg_t = r_seg.next()
                            sc.op("pe", lambda e, seg=seg, am=am: e.matmul(seg[:], lhsT=tri_st[:],
                                                                           rhs=am[:].rearrange("p a b -> p (a b)"),
                                                                           start=True, stop=True),
                                  reads=[tri_st_t, am_t], writes=[seg_t])
                            segs.append((seg, seg_t))
                        decs = []
                        for (seg, seg_t) in segs:
                            dec, dec_t = r_dec.next()
                            sc.op("act", lambda e, dec=dec, seg=seg: e.activation(out=dec[:].rearrange("p a b -> p (a b)"),
                                                                                 in_=seg[:], func=AF.Exp),
                                  reads=[seg_t], writes=[dec_t])
                            decs.append((dec, dec_t))
                        for k, (dec, dec_t) in enumerate(decs):
                            u = 2 * pair + k
                            cbm, cbm_t = cbms[u // 2]
                            mt, mt_t = r_mt.next()
                            sc.op("dve", lambda e, mt=mt, dec=dec, cbm=cbm: e.tensor_tensor(
                                out=mt[:], in0=dec[:], in1=cbm[:].unsqueeze(1).to_broadcast([128, 4, 128]), op=ALU.mult),
                                reads=[dec_t, cbm_t], writes=[mt_t])
                            mts.append((mt, mt_t))
                    for g in range(2):
                        yd, yd_t = r_yd.next()
                        if fwd:
                            sc.op("pe", lambda e, yd=yd, ybl=ybl, g=g: e.matmul(
                                yd[:], lhsT=ident[:], rhs=ybl[:, g * 512:(g + 1) * 512], start=True, stop=False,
                                skip_group_check=True), reads=[G["ident_t"], ybl_t], writes=[yd_t])
                        for q4 in range(2):
                            mt, mt_t = mts[g * 2 + q4]
                            for hh in range(4):
                                h = g * 8 + q4 * 4 + hh
                                hl = h - g * 8
                                sc.op("pe", lambda e, yd=yd, mt=mt, hh=hh, hl=hl, h=h, xdt=xdt: e.matmul(
                                    yd[:, hl * 64:(hl + 1) * 64], lhsT=mt[:, hh, :], rhs=xdt[:, h * 64:(h + 1) * 64],
                                    start=(not fwd), stop=True, skip_group_check=True),
                                    reads=[mt_t, xdt_t], writes=[yd_t], part=(fwd or not (q4 == 0 and hh == 0)))
                        t, t_t = ts[g]
                        gs = slice(g * 512, (g + 1) * 512)
                        if not fwd:
                            sc.op("dve", lambda e, t=t, yd=yd, ybl=ybl, gs=gs: e.tensor_tensor(out=ybl[:, gs], in0=t[:], in1=yd[:],
                                                                                              op=ALU.add),
                                  reads=[t_t, yd_t], writes=[ybl_t], part=(g > 0))
                        else:
                            sc.op("dve", lambda e, t=t, yd=yd, yf=yf, gs=gs: e.tensor_tensor(out=yf[:, gs], in0=t[:], in1=yd[:],
                                                                                            op=ALU.add),
                                  reads=[t_t, yd_t], writes=[yf_t], part=(g > 0))
                    if not fwd:
                        sc.dma("pool", ybw[tsl, :], ybl[:], owner=ybl_t, reads=[ybl_t], writes=[ybw_t], part=True)
                        return None
                    if i % 4 == 0:
                        z, z_t = r_z.next()
                        sc.dma("sp", z[:], U["z"][i * 128:(i + 4) * 128, :].rearrange("(j p) c -> p j c", p=128), owner=z_t,
                               reads=[G["dram_t"]["z"]], writes=[z_t])
                        sc.op("act", lambda e, z=z: e.activation(out=z[:], in_=z[:], func=AF.Silu), reads=[z_t], writes=[z_t])
                        zcur[0] = (z, z_t)
                    z, z_t = zcur[0]
                    sz = z[:, i % 4, :]
                    sz_t = z_t
                    xd, xd_t = r_xdt.next()
                    sc.op("pool", lambda e, xd=xd, i=i: e.tensor_tensor(
                        out=xd[:].rearrange("p (h q) -> p h q", q=64), in0=xtok[:, i, 0:1024].rearrange("p (h q) -> p h q", q=64),
                        in1=rows[:, 2, 0:16].unsqueeze(2).to_broadcast([128, 16, 64]), op=ALU.mult),
                        reads=[xtok_t[i], rows_t], writes=[xd_t])
                    sc.op("dve", lambda e, yf=yf, xd=xd: e.tensor_tensor(out=yf[:], in0=yf[:], in1=xd[:], op=ALU.add),
                          reads=[yf_t, xd_t], writes=[yf_t])
                    sc.op("dve", lambda e, yf=yf, sz=sz: e.tensor_tensor(out=yf[:], in0=yf[:], in1=sz, op=ALU.mult),
                          reads=[yf_t, sz_t], writes=[yf_t])
                    ss, ss_t = r_ss.next()
                    for g in range(2):
                        gs = slice(g * 512, (g + 1) * 512)
                        jk, jk_t = r_jk.next()
                        sc.op("dve", lambda e, jk=jk, yf=yf, gs=gs, ss=ss, g=g: e.scalar_tensor_tensor(
                            out=jk[:], in0=yf[:, gs], scalar=1.0, in1=yf[:, gs], op0=ALU.mult, op1=ALU.mult,
                            accum_out=ss[:, g:g + 1]), reads=[yf_t], writes=[jk_t, ss_t])
                    sc.op("dve", lambda e, ss=ss: e.tensor_scalar(out=ss[:, 2:4], in0=ss[:, 0:2], scalar1=1.0 / 512.0, scalar2=EPS,
                                                                  op0=ALU.mult, op1=ALU.add), reads=[ss_t], writes=[ss_t])
                    sc.op("pool", lambda e, ss=ss: e.tensor_tensor(out=ss[:, 0:2], in0=ss[:, 2:4], in1=G["neghalf"][:, 0:2],
                                                                   op=ALU.pow), reads=[ss_t, G["neghalf_t"]], writes=[ss_t])
                    y, y_t = r_y.next()
                    for g in range(2):
                        gs = slice(g * 512, (g + 1) * 512)
                        sc.op("dve", lambda e, y=y, yf=yf, gs=gs, ss=ss, g=g: e.scalar_tensor_tensor(
                            out=y[:, gs], in0=yf[:, gs], scalar=ss[:, g:g + 1], in1=nwb[:, gs], op0=ALU.mult, op1=ALU.mult),
                            reads=[yf_t, ss_t, nwb_t], writes=[y_t], part=(g > 0))
                    return (y, y_t, i)

                def tileC(c3):
                    if c3 is None:
                        return
                    (y, y_t, i) = c3
                    emit_yT(P, sc, G, r_tp, y, y_t, yst, yst_t, i, YB["ssd"], G["dram_t"]["yb_ssd"], gsz=2)

                prev3 = None
                for i in order:
                    n3 = tileA(i)
                    tileC(prev3)
                    prev3 = n3
                tileC(prev3)

            ssd_pass(1)
            ssd_pass(0)
            sc.barrier(release=tl2)
        sc.barrier(release=tiles)


def emit_yT(P, sc, G, r_tp, y, y_t, yst, yst_t, i, dst, dst_t, gsz=4):
    ident = G["ident"]
    for half in range(2):
        tp, tp_t = r_tp.next()
        for jq in range(4):
            c = half * 4 + jq
            sc.op("pe", lambda e, tp=tp, jq=jq, c=c: e.transpose(out=tp[:, jq, :], in_=y[:, c * 128:(c + 1) * 128],
                                                               identity=ident[:]),
                  reads=[y_t, G["ident_t"]], writes=[tp_t], part=(jq > 0))
        sc.op("act", lambda e, tp=tp, half=half: e.copy(
            out=yst[:, half * 4:half * 4 + 4, (i % gsz) * 128:(i % gsz + 1) * 128], in_=tp[:]),
            reads=[tp_t], writes=[yst_t], part=not (i % gsz == 0 and half == 0))
    if i % gsz == gsz - 1:
        yv = dst.rearrange("(c p) t -> p c t", p=128)
        sc.dma("pool", yv[:, :, (i - gsz + 1) * 128:(i + 1) * 128], yst[:], owner=yst_t, reads=[yst_t], writes=[dst_t],
               part=True)


NEG = -30000.0


def na_r0(r):
    return min(max(r - 4, 0), 24)


def na_valid(kr, qr):
    return na_r0(qr) <= kr < na_r0(qr) + 8


def phase_D(P, sc, G, U, YB, prm, natt, l):
    nc = P.nc
    with contextlib.ExitStack() as ph:
        qnT = P.sb(ph, "D_qnT", [128, 8, S], BF16)
        knT = P.sb(ph, "D_knT", [128, 8, S], BF16)
        qn_t = sc.tiles_n("D_qn", 8)
        kn_t = sc.tiles_n("D_kn", 8)
        TT = P.sb(ph, "D_TT", [128, 8, 17, 64], BF16)
        TT_t = sc.tiles_n("D_TT", 4)
        wcol = P.sb(ph, "D_wcol", [128, 4], F32)
        wcol_t = sc.tile("D_wcol")
        tiles = qn_t + kn_t + TT_t + [wcol_t]
        with contextlib.ExitStack() as s1:
            TTf = [P.sb(s1, "D_TTf%d" % i, [128, 2, 17, 64], F32) for i in range(2)]
            TTf_t = sc.tiles_n("D_TTf", 2)
            qc_ = [P.sb(s1, "D_qc%d" % i, [128, S], BF16) for i in range(2)]
            qc_t = sc.tiles_n("D_qc", 2)
            sq = [P.sb(s1, "D_sq%d" % i, [128, 512], F32) for i in range(2)]
            sq_t = sc.tiles_n("D_sq", 2)
            lnv = [P.sb(s1, "D_ln%d" % i, [128, 512], F32) for i in range(2)]
            lnv_t = sc.tiles_n("D_ln", 2)
            bones = P.sb(s1, "D_bones", [128, 128], F32)
            bones_t = sc.tile("D_bones")
            ssp = [P.ps(s1, "D_ssp%d" % i, [128, 512], F32) for i in range(2)]
            ssp_t = sc.tiles_n("D_ssp", 2)
            t1 = TTf_t + qc_t + sq_t + lnv_t + [bones_t] + ssp_t
            for g in range(4):
                b = g % 2
                sc.dma("sp", TTf[b][:], natt[l][:, 2 * g:2 * g + 2, :, :], owner=TTf_t[b], writes=[TTf_t[b]])
                sc.op("pool", lambda e, b=b, g=g: e.tensor_copy(out=TT[:, 2 * g:2 * g + 2, :, :], in_=TTf[b][:]),
                      reads=[TTf_t[b]], writes=[TT_t[g]])
            for hh in range(2):
                sc.dma("sp", wcol[hh * 64:(hh + 1) * 64, 2:3], prm["na_q_norm_w"][l].rearrange("(d o) -> d o", o=1),
                       owner=wcol_t, writes=[wcol_t], part=True)
                sc.dma("sp", wcol[hh * 64:(hh + 1) * 64, 1:2], prm["na_k_norm_w"][l].rearrange("(d o) -> d o", o=1),
                       owner=wcol_t, writes=[wcol_t], part=True)
            sc.op("dve", lambda e: e.tensor_scalar(out=wcol[:, 0:1], in0=wcol[:, 2:3], scalar1=0.125, scalar2=None,
                                                   op0=ALU.mult), reads=[wcol_t], writes=[wcol_t])
            sc.op("pool", lambda e: e.memset(bones[:], 0.0), writes=[bones_t])
            sc.op("pool", lambda e: e.memset(bones[0:64, 0:64], 1.0), reads=[bones_t], writes=[bones_t])
            sc.op("pool", lambda e: e.memset(bones[64:128, 64:128], 1.0), reads=[bones_t], writes=[bones_t])
            cnt = 0
            for which, (src, dstT, dst_t, wc) in enumerate(((U["nq"], qnT, qn_t, 0), (U["nk"], knT, kn_t, 1))):
                src_t = G["dram_t"]["nq" if which == 0 else "nk"]
                for c in range(8):
                    cb = cnt % 2
                    cnt += 1
                    sc.dma("sp", qc_[cb][:], src[c * 128:(c + 1) * 128, :], owner=qc_t[cb], reads=[src_t],
                           writes=[qc_t[cb]])
                    for tb in range(4):
                        b = tb % 2
                        sl = slice(tb * 512, (tb + 1) * 512)
                        sc.op("dve", lambda e, cb=cb, b=b, sl=sl: e.tensor_tensor(out=sq[b][:], in0=qc_[cb][:, sl],
                                                                                  in1=qc_[cb][:, sl], op=ALU.mult),
                              reads=[qc_t[cb]], writes=[sq_t[b]])
                        sc.op("pe", lambda e, b=b: e.matmul(ssp[b][:], lhsT=bones[:], rhs=sq[b][:], start=True, stop=True),
                              reads=[bones_t, sq_t[b]], writes=[ssp_t[b]])
                        sc.op("act", lambda e, b=b: e.activation(out=lnv[b][:], in_=ssp[b][:], func=AF.Ln,
                                                                 bias=G["eps"][:, 0:1], scale=1.0 / 64.0),
                              reads=[ssp_t[b], G["eps_t"]], writes=[lnv_t[b]])
                        sc.op("act", lambda e, b=b: e.activation(out=lnv[b][:], in_=lnv[b][:], func=AF.Exp, scale=-0.5),
                              reads=[lnv_t[b]], writes=[lnv_t[b]])
                        sc.op("dve", lambda e, cb=cb, b=b, sl=sl, dstT=dstT, c=c, wc=wc: e.scalar_tensor_tensor(
                            out=dstT[:, c, sl], in0=qc_[cb][:, sl], scalar=wcol[:, wc:wc + 1], in1=lnv[b][:],
                            op0=ALU.mult, op1=ALU.mult),
                            reads=[qc_t[cb], lnv_t[b], wcol_t], writes=[dst_t[c]], part=(tb > 0))
            sc.barrier(release=t1)
        with contextlib.ExitStack() as s2:
            vx = P.sb(s2, "D_vx", [128, NT, 16, 65], BF16)
            vx_t = sc.tiles_n("D_vx", NT)
            sps = [P.ps(s2, "D_sps%d" % i, [128, 8, 128], F32) for i in range(2)]
            sps_t = sc.tiles_n("D_sps", 2)
            pT = [P.sb(s2, "D_pT%d" % i, [128, 5, 128], BF16) for i in range(3)]
            pT_t = sc.tiles_n("D_pT", 3)
            po = [P.ps(s2, "D_po%d" % i, [128, 2, 66], F32) for i in range(2)]
            po_t = sc.tiles_n("D_po", 2)
            rc = [P.sb(s2, "D_rc%d" % i, [128, 2], F32) for i in range(2)]
            rc_t = sc.tiles_n("D_rc", 2)
            ot = [P.sb(s2, "D_ot%d" % i, [128, 1024], BF16) for i in range(2)]
            ot_t = sc.tiles_n("D_ot", 2)
            tp = [P.ps(s2, "D_tp%d" % i, [128, 4, 128], BF16) for i in range(2)]
            tp_t = sc.tiles_n("D_tp", 2)
            yst = P.sb(s2, "D_yst", [128, 8, 512], BF16)
            yst_t = sc.tile("D_yst")
            t2 = vx_t + sps_t + pT_t + po_t + rc_t + ot_t + tp_t + [yst_t]
            nvv = U["nv"].rearrange("(i p) (h d) -> p i h d", p=128, d=64)
            for i in range(NT):
                sc.op("pool", lambda e, i=i: e.memset(vx[:, i, :, 64:65], 1.0), writes=[vx_t[i]])
                sc.dma("sp", vx[:, i, :, 0:64], nvv[:, i, :, :], owner=vx_t[i], reads=[G["dram_t"]["nv"]],
                       writes=[vx_t[i]], part=True)
            ident = G["ident"]
            tpc = [0]
            units = []
            for i in range(NT):
                jlo = na_r0(2 * i) // 2
                jhi = (na_r0(2 * i + 1) + 7) // 2
                js = list(range(jlo, jhi + 1))
                for hp in range(8):
                    for hh in range(2):
                        units.append((i, hp, hh, js))

            def emit_S(u):
                i, hp, hh, js = units[u]
                h = 2 * hp + hh
                p0 = 64 * hh
                sb_ = u % 2
                for jj, j in enumerate(js):
                    sc.op("pe", lambda e, sb_=sb_, jj=jj, j=j, p0=p0, hp=hp, i=i: e.matmul(
                        sps[sb_][:, jj, :], lhsT=knT[p0:p0 + 64, hp, j * 128:(j + 1) * 128],
                        rhs=qnT[p0:p0 + 64, hp, i * 128:(i + 1) * 128], start=True, stop=False,
                        skip_group_check=True),
                        reads=[kn_t[hp], qn_t[hp]], writes=[sps_t[sb_]], part=(jj > 0))
                    mms = []
                    for b0 in range(2):
                        qr = 2 * i + b0
                        va = [na_valid(2 * j + a, qr) for a in range(2)]
                        dr0 = 2 * j - qr + 7
                        cs = slice(b0 * 64, (b0 + 1) * 64)
                        if va[0] and va[1]:
                            mms.append((slice(0, 128), cs, TT[p0:p0 + 64, hp, dr0:dr0 + 2, :]))
                        elif not va[0] and not va[1]:
                            mms.append((slice(0, 128), cs, TT[p0:p0 + 64, hp, 15:17, :]))
                        else:
                            d0 = dr0 if va[0] else 15
                            d1 = dr0 + 1 if va[1] else 16
                            mms.append((slice(0, 64), cs, TT[p0:p0 + 64, hp, d0, :]))
                            mms.append((slice(64, 128), cs, TT[p0:p0 + 64, hp, d1, :]))
                    for mi, (ps_, cs, lhs) in enumerate(mms):
                        sc.op("pe", lambda e, sb_=sb_, jj=jj, ps_=ps_, cs=cs, lhs=lhs, p0=p0, last=(mi == len(mms) - 1):
                              e.matmul(sps[sb_][ps_, jj, cs], lhsT=lhs, rhs=ident[p0:p0 + 64, p0:p0 + 64],
                                       start=False, stop=last, skip_group_check=True),
                              reads=[TT_t[hp // 2], G["ident_t"]], writes=[sps_t[sb_]], part=True)

            def emit_rest(u):
                i, hp, hh, js = units[u]
                h = 2 * hp + hh
                sb_ = u % 2
                pt = u % 3
                pb_ = (u // 2) % 2
                ob = i % 2
                n = len(js)
                n1 = min(n, 4)
                sc.op("act", lambda e, pt=pt, sb_=sb_, n1=n1: e.activation(out=pT[pt][:, 0:n1, :],
                                                                         in_=sps[sb_][:, 0:n1, :], func=AF.Exp),
                      reads=[sps_t[sb_]], writes=[pT_t[pt]])
                if n > 4:
                    sc.op("act", lambda e, pt=pt, sb_=sb_, n=n: e.activation(out=pT[pt][:, 4:n, :],
                                                                           in_=sps[sb_][:, 4:n, :], func=AF.Exp),
                          reads=[sps_t[sb_]], writes=[pT_t[pt]], part=True)
                for jj, j in enumerate(js):
                    sc.op("pe", lambda e, pb_=pb_, hh=hh, pt=pt, jj=jj, j=j, h=h, n=n: e.matmul(
                        po[pb_][:, hh, 0:65], lhsT=pT[pt][:, jj, :], rhs=vx[:, j, h, :],
                        start=(jj == 0), stop=(jj == n - 1)),
                        reads=[pT_t[pt], vx_t[j]], writes=[po_t[pb_]], part=(hh > 0 or jj > 0))
                if hh == 1:
                    sc.op("dve", lambda e, pb_=pb_: e.reciprocal(out=rc[pb_][:, 0:2], in_=po[pb_][:, :, 64]),
                          reads=[po_t[pb_]], writes=[rc_t[pb_]])
                    for h2 in range(2):
                        hx = 2 * hp + h2
                        sc.op("dve", lambda e, pb_=pb_, h2=h2, hx=hx, ob=ob: e.tensor_scalar(
                            out=ot[ob][:, hx * 64:(hx + 1) * 64], in0=po[pb_][:, h2, 0:64], scalar1=rc[pb_][:, h2:h2 + 1],
                            scalar2=None, op0=ALU.mult),
                            reads=[po_t[pb_], rc_t[pb_]], writes=[ot_t[ob]], part=(hx > 0))
                if hp == 7 and hh == 1:
                    for half in range(2):
                        tb_ = tpc[0] % 2
                        tpc[0] += 1
                        for jq in range(4):
                            c = half * 4 + jq
                            sc.op("pe", lambda e, tb_=tb_, jq=jq, c=c, ob=ob: e.transpose(
                                out=tp[tb_][:, jq, :], in_=ot[ob][:, c * 128:(c + 1) * 128], identity=ident[:]),
                                reads=[ot_t[ob], G["ident_t"]], writes=[tp_t[tb_]], part=(jq > 0))
                        sc.op("act", lambda e, tb_=tb_, half=half, i=i: e.copy(
                            out=yst[:, half * 4:half * 4 + 4, (i % 4) * 128:(i % 4 + 1) * 128], in_=tp[tb_][:]),
                            reads=[tp_t[tb_]], writes=[yst_t], part=not (i % 4 == 0 and half == 0))
                    if i % 4 == 3:
                        yv = YB["na"].rearrange("(c p) t -> p c t", p=128)
                        sc.dma("pool", yv[:, :, (i - 3) * 128:(i + 1) * 128], yst[:], owner=yst_t, reads=[yst_t],
                               writes=[G["dram_t"]["yb_na"]], part=True)

            emit_S(0)
            for u in range(len(units)):
                if u + 1 < len(units):
                    emit_S(u + 1)
                emit_rest(u)
            sc.barrier(release=t2)
        sc.barrier(release=tiles)


def phase_F(P, sc, G, prm, l):
    nc = P.nc
    x = G["x"]
    with contextlib.ExitStack() as ph:
        hT = P.sb(ph, "F_hT", [128, 8, S], BF16)
        hT_t = sc.tiles_n("F_hT", NT)
        tiles = list(hT_t)
        tiles += rms_transpose(P, sc, G, ph, prm["norm_mlp_w"][l], hT, hT_t, l, "F")
        wst = WStream(P, sc, ph, "F", 1, 4096, nf=2, nb=3)
        fT = [P.sb(ph, "F_fT%d" % i, [128, 4, S], BF16) for i in range(2)]
        fT_t = [sc.tiles_n("F_fT%d_" % i, 4) for i in range(2)]
        rl = [P.sb(ph, "F_rl%d" % i, [128, 512], F32) for i in range(2)]
        rl_t = sc.tiles_n("F_rl", 2)
        acc = [P.ps(ph, "F_acc%d" % i, [128, 512], F32) for i in range(4)]
        acc_t = sc.tiles_n("F_acc", 4)
        tiles += wst.tiles + fT_t[0] + fT_t[1] + rl_t + acc_t
        w1v = prm["w_ff1"][l].rearrange("(kc p) n -> p kc n", p=128)
        w2v = prm["w_ff2"][l].rearrange("(c p) n -> p c n", p=128)
        items = []
        for g in range(8):
            items.append((w1v[:, :, g * 512:(g + 1) * 512], 8, 512))
            items.append((w2v[:, g * 4:(g + 1) * 4, :], 4, 1024))
        wst.items = items
        wst_views = {}

        def view(slot, k, n):
            return slot[:, 0, :].rearrange("p (k n) -> p k n", k=k)
        def _load(g):
            if g >= len(items):
                return
            ap, k, n = items[g]
            fs = g % wst.nf
            sc.dma("sp", view(wst.f[fs], k, n), ap, owner=wst.f_t[fs], writes=[wst.f_t[fs]])

        def _cast(g):
            if g >= len(items):
                return
            fs, bs = g % wst.nf, g % wst.nb
            sc.op("pool", lambda e: e.tensor_copy(out=wst.b[bs][:, 0, :], in_=wst.f[fs][:, 0, :]),
                  reads=[wst.f_t[fs]], writes=[wst.b_t[bs]])
        wst._load = _load
        wst._cast = _cast
        _load(0)
        _load(1)
        _cast(0)
        ai = 0
        ri = 0
        for g in range(8):
            fb = g % 2
            w1s, w1_t = wst.get(2 * g)
            w1b = view(w1s, 8, 512)
            for c in range(4):
                for tb in range(4):
                    a = ai % 4
                    ai += 1
                    for kc in range(8):
                        sc.op("pe", lambda e, a=a, kc=kc, w1b=w1b, c=c, tb=tb: e.matmul(
                            acc[a][:], lhsT=w1b[:, kc, c * 128:(c + 1) * 128],
                            rhs=hT[:, kc, tb * 512:(tb + 1) * 512], start=(kc == 0), stop=(kc == 7)),
                            reads=[w1_t] + hT_t[tb * 4:tb * 4 + 4], writes=[acc_t[a]], part=(kc > 0))
                    r = ri % 2
                    ri += 1
                    sc.op("act", lambda e, r=r, a=a: e.activation(out=rl[r][:], in_=acc[a][:], func=AF.Relu),
                          reads=[acc_t[a]], writes=[rl_t[r]])
                    sc.op("pool", lambda e, r=r, fb=fb, c=c, tb=tb: e.tensor_tensor(
                        out=fT[fb][:, c, tb * 512:(tb + 1) * 512], in0=rl[r][:], in1=rl[r][:], op=ALU.mult),
                        reads=[rl_t[r]], writes=[fT_t[fb][c]], part=(tb > 0))
            w2s, w2_t = wst.get(2 * g + 1)
            w2b = view(w2s, 4, 1024)
            for i in range(NT):
                for hh in range(2):
                    a = ai % 4
                    ai += 1
                    for c in range(4):
                        sc.op("pe", lambda e, a=a, c=c, w2b=w2b, i=i, hh=hh, fb=fb: e.matmul(
                            acc[a][:], lhsT=fT[fb][:, c, i * 128:(i + 1) * 128],
                            rhs=w2b[:, c, hh * 512:(hh + 1) * 512], start=(c == 0), stop=(c == 3)),
                            reads=[w2_t, fT_t[fb][c]], writes=[acc_t[a]], part=(c > 0))
                    xs = x[:, i, hh * 512:(hh + 1) * 512]
                    sc.op("dve", lambda e, xs=xs, a=a: e.tensor_tensor(out=xs, in0=xs, in1=acc[a][:], op=ALU.add),
                          reads=[acc_t[a], G["xt"][i]], writes=[G["xt"][i]])
        sc.barrier(release=tiles)


_NC_CACHE = {}


def make_na_tt(rpb):
    rpb = np.asarray(rpb, dtype=np.float32)
    L = rpb.shape[0]
    out = np.full((L, 128, 8, 17, 64), NEG, dtype=np.float32)
    qc = np.arange(64)
    ws = np.clip(qc - 8, 0, 48)
    for q in range(64):
        kc = np.arange(ws[q], ws[q] + 16)
        idx = kc - q + 15
        for hh in range(2):
            out[:, hh * 64 + q, :, 0:15, ws[q]:ws[q] + 16] = rpb[:, hh::2][:, :, :, idx]
    return out


def kernel(**inputs):
    cfg = {}
    key = "full"
    if key not in _NC_CACHE:
        _NC_CACHE[key] = build(cfg)
    nc = _NC_CACHE[key]
    x = np.ascontiguousarray(inputs["x"], dtype=np.float32)
    base = {n: np.ascontiguousarray(inputs[n], dtype=np.float32) for n in PARAM_NAMES}
    base["na_tt"] = make_na_tt(inputs["na_rpb"])
    in_maps = []
    for c in range(8):
        m = dict(base)
        m["x"] = x[c]
        in_maps.append(m)
    res = run_bass_kernel_spmd(nc, in_maps, core_ids=list(range(8)))
    return np.stack([r["y"] for r in res.results], axis=0).astype(np.float32)
```

```python
import contextlib
import numpy as np
import concourse.bass as bass
import concourse.mybir as mybir
from concourse.bass_utils import run_bass_kernel_spmd

F32 = mybir.dt.float32
BF16 = mybir.dt.bfloat16
ALU = mybir.AluOpType
AF = mybir.ActivationFunctionType
AX = mybir.AxisListType

D = 1024
S = 2048
NT = S // 128
DEPTH = 2
N_IN = 11840
EPS = 1e-6


class TT:
    __slots__ = ("name", "lw", "rd", "dsems", "gen")

    def __init__(self, name):
        self.name = name
        self.lw = {}
        self.rd = {}
        self.gen = {}
        self.dsems = {}


class Sched:
    ENG = ("pe", "act", "dve", "pool", "sp")
    BLK = {"pe": "tensor", "act": "scalar", "dve": "vector", "pool": "gpsimd", "sp": "sync"}

    def __init__(self, nc, stack):
        self.nc = nc
        self.stack = stack
        self.ops = {e: [] for e in self.ENG}
        self.seen = {e: {} for e in self.ENG}
        self.esem = {e: stack.enter_context(nc.semaphore("es_" + e)) for e in self.ENG if e != "sp"}
        self.tiles = []
        self.free_dsems = {"sp": [], "pool": [], "act": []}
        self.nsem = 4
        self.skip_same = {"pe"}

    def tile(self, name):
        t = TT(name)
        self.tiles.append(t)
        return t

    def tiles_n(self, name, n):
        return [self.tile("%s%d" % (name, i)) for i in range(n)]

    def _collect(self, reads, writes, part):
        evs = {}

        def add(d):
            for k, v in d.items():
                if k not in evs or evs[k][0] < v[0]:
                    evs[k] = v
        for t in reads:
            add(t.lw)
        for t in writes:
            if part and not t.rd:
                add(t.gen)
                continue
            g = dict(t.rd)
            for k, v in t.lw.items():
                if k not in g or g[k][0] < v[0]:
                    g[k] = v
            t.gen = g
            add(g)
        return evs

    def _waits(self, eng, evs):
        waits = []
        for k, (val, obj) in evs.items():
            if k == ("E", eng) and eng in self.skip_same:
                continue
            if self.seen[eng].get(k, 0) >= val:
                continue
            self.seen[eng][k] = val
            waits.append((k, val, obj))
            if k[0] == "E":
                self.ops[k[1]][val - 1]["inc"] = True
        return waits

    def _update(self, ev_key, ev_val, reads, writes, part):
        for t in reads:
            t.rd[ev_key] = ev_val
        for t in writes:
            if part and not t.rd:
                t.lw[ev_key] = ev_val
            else:
                t.lw = {ev_key: ev_val}
                t.rd = {}

    def op(self, eng, fn, reads=(), writes=(), part=False):
        waits = self._waits(eng, self._collect(reads, writes, part))
        self.ops[eng].append({"fn": fn, "waits": waits, "inc": False, "dma": None})
        idx = len(self.ops[eng])
        self._update(("E", eng), (idx, None), reads, writes, part)

    def dma(self, q, out, in_, owner, reads=(), writes=(), part=False, **kw):
        waits = self._waits(q, self._collect(reads, writes, part))
        rec = owner.dsems.get(q)
        if rec is None:
            if self.free_dsems[q]:
                rec = self.free_dsems[q].pop()
            else:
                rec = [self.stack.enter_context(self.nc.semaphore("ds%d" % self.nsem)), 0, self.nsem]
                self.nsem += 1
            owner.dsems[q] = rec
        rec[1] += 16
        self.ops[q].append({"fn": (lambda e: e.dma_start(out=out, in_=in_, **kw)), "waits": waits,
                            "inc": False, "dma": rec[0]})
        self._update(("D", rec[2]), (rec[1], rec[0]), reads, writes, part)

    def barrier(self, release=()):
        evs = {}
        for e in self.ENG:
            if e == "sp":
                continue
            idx = len(self.ops[e])
            while idx > 0 and (self.ops[e][idx - 1]["dma"] is not None or self.ops[e][idx - 1].get("nop")):
                idx -= 1
            if idx > 0:
                evs[("E", e)] = (idx, None)
        for t in self.tiles:
            for d in (t.lw, t.rd):
                for k, v in d.items():
                    if k[0] == "D" and (k not in evs or evs[k][0] < v[0]):
                        evs[k] = v
        for e in self.ENG:
            sk = self.skip_same
            self.skip_same = set()
            w = self._waits(e, dict(evs))
            self.skip_same = sk
            self.ops[e].append({"fn": (lambda en: en.nop()), "waits": w, "inc": False, "dma": None, "nop": True})
        for t in self.tiles:
            t.lw = {}
            t.rd = {}
            t.gen = {}
        rel = set(id(t) for t in release)
        for t in release:
            for q, rec in t.dsems.items():
                self.free_dsems[q].append(rec)
            t.dsems = {}
        self.tiles = [t for t in self.tiles if id(t) not in rel]

    def emit(self):
        nc = self.nc
        mile = {}
        for e in self.ENG:
            c = 0
            m = []
            for o in self.ops[e]:
                if o["inc"]:
                    c += 1
                m.append(c)
            mile[e] = m
            assert c < 60000, (e, c)
        with nc.Block() as block:
            for e in self.ENG:
                def body(engine, e=e):
                    for o in self.ops[e]:
                        for (k, val, obj) in o["waits"]:
                            if k[0] == "E":
                                engine.wait_ge(self.esem[k[1]], mile[k[1]][val - 1])
                            else:
                                engine.wait_ge(obj, val)
                        ins = o["fn"](engine)
                        if o["dma"] is not None:
                            ins.then_inc(o["dma"], 16)
                        elif o["inc"]:
                            ins.then_inc(self.esem[e], 1)
                getattr(block, self.BLK[e])(body)


class Prog:
    def __init__(self, cfg):
        self.cfg = cfg
        self.nc = bass.Bass("TRN2", target_bir_lowering=False)
        self.dbg = cfg.get("debug", ())

    def dram(self, name, shape, dt, kind="Internal"):
        if name in self.dbg:
            kind = "ExternalOutput"
        if name in self.cfg.get("ext_in", ()):
            kind = "ExternalInput"
        return self.nc.dram_tensor(name, list(shape), dt, kind=kind).ap()

    def sb(self, stack, name, shape, dt):
        self.uid = getattr(self, "uid", 0) + 1
        return stack.enter_context(self.nc.sbuf_tensor("%s_u%d" % (name, self.uid), list(shape), dt))

    def ps(self, stack, name, shape, dt):
        self.uid = getattr(self, "uid", 0) + 1
        return stack.enter_context(self.nc.psum_tensor("%s_u%d" % (name, self.uid), list(shape), dt))


IN_SIZES = (1024, 1536, 16, 16, 512, 512, 1024, 1024, 16, 16, 1024, 1024, 1024, 3072)
IN_OFF = [0]
for _s in IN_SIZES:
    IN_OFF.append(IN_OFF[-1] + _s)
(O_Z, O_XBC, O_DTF, O_DTB, O_GQ, O_GK, O_GV, O_GG, O_GAF, O_GAB, O_NQ, O_NK, O_NV, O_GATE, _) = IN_OFF

PARAM_NAMES = ["norm_mix_w", "w_in", "ssd_conv_w", "ssd_conv_b", "ssd_dt_bias_f", "ssd_dt_bias_b",
               "ssd_a_log_f", "ssd_a_log_b", "ssd_d", "ssd_norm_w", "gla_a2_f", "gla_a2_bias_f",
               "gla_a2_b", "gla_a2_bias_b", "gla_norm_w", "na_q_norm_w", "na_k_norm_w", "na_rpb",
               "w_branch_ssd", "w_branch_gla", "w_branch_na", "w_out", "norm_mlp_w", "w_ff1", "w_ff2"]
PARAM_SHAPES = {
    "norm_mix_w": (2, 1024), "w_in": (2, 1024, 11840), "ssd_conv_w": (2, 5, 1536), "ssd_conv_b": (2, 1536),
    "ssd_dt_bias_f": (2, 16), "ssd_dt_bias_b": (2, 16), "ssd_a_log_f": (2, 16), "ssd_a_log_b": (2, 16),
    "ssd_d": (2, 16), "ssd_norm_w": (2, 1024), "gla_a2_f": (2, 16, 512), "gla_a2_bias_f": (2, 512),
    "gla_a2_b": (2, 16, 512), "gla_a2_bias_b": (2, 512), "gla_norm_w": (2, 256), "na_q_norm_w": (2, 64),
    "na_k_norm_w": (2, 64), "na_rpb": (2, 16, 15, 31), "w_branch_ssd": (2, 1024, 1024),
    "w_branch_gla": (2, 1024, 1024), "w_branch_na": (2, 1024, 1024), "w_out": (2, 1024, 1024),
    "norm_mlp_w": (2, 1024), "w_ff1": (2, 1024, 4096), "w_ff2": (2, 4096, 1024),
}


def build(cfg):
    P = Prog(cfg)
    nc = P.nc
    layers = cfg.get("layers", DEPTH)
    phases = cfg.get("phases", "ABCDEF")
    x_in = nc.dram_tensor("x", [S, D], F32, kind="ExternalInput").ap()
    prm = {n: nc.dram_tensor(n, list(PARAM_SHAPES[n]), F32, kind="ExternalInput").ap() for n in PARAM_NAMES}
    y_out = nc.dram_tensor("y", [S, D], F32, kind="ExternalOutput").ap()
    natt = nc.dram_tensor("na_tt", [DEPTH, 128, 8, 17, 64], F32, kind="ExternalInput").ap()

    U = {}
    for nm, w in (("z", 1024), ("gv", 1024), ("gg", 1024), ("nv", 1024)):
        U[nm] = P.dram("u_" + nm, [S, w], BF16)
    U["dt"] = P.dram("u_dt", [S, 32], F32)
    for nm, w in (("xbc", 1536), ("gq", 512), ("gk", 512), ("nq", 1024), ("nk", 1024), ("gate", 3072)):
        U[nm] = P.dram("u_" + nm + "T", [w, S], BF16)
    U["ga"] = P.dram("u_gaT", [32, S], F32)
    YB = {nm: P.dram("yb_" + nm, [1024, S], BF16) for nm in ("ssd", "gla", "na")}
    ybw = P.dram("ybw", [S, 1024], BF16)

    with contextlib.ExitStack() as top:
        sc = Sched(nc, top)
        G = {}
        G["x"] = P.sb(top, "x_res", [128, NT, D], F32)
        G["xt"] = sc.tiles_n("x", NT)
        G["ident"] = P.sb(top, "ident", [128, 128], BF16)
        G["ident_t"] = sc.tile("ident")
        G["dram_t"] = {k: sc.tile("d_" + k) for k in list(U) + ["yb_ssd", "yb_gla", "yb_na", "ybw"]}
        G["ybw"] = ybw

        ones_f = P.sb(top, "ones_f", [128, 128], F32)
        ones_t = sc.tile("ones_f")
        sc.op("pool", lambda e: e.memset(ones_f[:], 1.0), writes=[ones_t])
        sc.op("pool", lambda e: e.affine_select(out=G["ident"][:], in_=ones_f[:], pattern=[[-1, 128]],
                                                compare_op=ALU.is_equal, fill=0.0, base=0,
                                                channel_multiplier=1),
              reads=[ones_t], writes=[G["ident_t"]])
        G["ones_f"] = ones_f
        G["eps"] = P.sb(top, "epsc", [128, 2], F32)
        G["eps_t"] = sc.tile("epsc")
        sc.op("pool", lambda e: e.memset(G["eps"][:], EPS), writes=[G["eps_t"]])
        G["one"] = P.sb(top, "onec", [128, 2], F32)
        G["one_t"] = sc.tile("onec")
        sc.op("pool", lambda e: e.memset(G["one"][:], 1.0), writes=[G["one_t"]])
        G["neghalf"] = P.sb(top, "neghalf", [128, 16], F32)
        G["neghalf_t"] = sc.tile("neghalf")
        sc.op("pool", lambda e: e.memset(G["neghalf"][:], -0.5), writes=[G["neghalf_t"]])
        G["ones_t"] = ones_t

        build_tri(P, sc, G, top)
        xv = x_in.rearrange("(i p) d -> p i d", p=128)
        for i in range(NT):
            sc.dma("sp", G["x"][:, i, :], xv[:, i, :], owner=G["xt"][i], writes=[G["xt"][i]])

        for l in range(layers):
            if "A" in phases:
                phase_A(P, sc, G, U, prm, l)
            if "B" in phases:
                phase_B(P, sc, G, U, YB, prm, l)
            if "C" in phases:
                phase_C(P, sc, G, U, YB, prm, l)
            if "D" in phases:
                phase_D(P, sc, G, U, YB, prm, natt, l)
            if "E" in phases:
                phase_E(P, sc, G, U, YB, prm, l)
            if "F" in phases:
                phase_F(P, sc, G, prm, l)

        yv = y_out.rearrange("(i p) d -> p i d", p=128)
        outt = sc.tile("yout")
        for i in range(NT):
            sc.dma("sp", yv[:, i, :], G["x"][:, i, :], owner=G["xt"][i], reads=[G["xt"][i]], writes=[outt],
                   part=True)
        sc.op("sp", lambda e: e.nop(), reads=[outt])
        sc.barrier()
        sc.emit()
    return nc


def rms_transpose(P, sc, G, ph, wrow_ap, hT, hT_t, l, tag):
    nc = P.nc
    wb = P.sb(ph, tag + "_wb", [128, D], F32)
    wb_t = sc.tile(tag + "_wb")
    sc.dma("sp", wb[:], wrow_ap.partition_broadcast(128), owner=wb_t, writes=[wb_t])
    junk = [P.sb(ph, tag + "_junk%d" % i, [128, D], BF16) for i in range(2)]
    junk_t = sc.tiles_n(tag + "_junk", 2)
    hb = [P.sb(ph, tag + "_hb%d" % i, [128, D], BF16) for i in range(2)]
    hb_t = sc.tiles_n(tag + "_hb", 2)
    ss = [P.sb(ph, tag + "_ss%d" % i, [128, 2], F32) for i in range(2)]
    ss_t = sc.tiles_n(tag + "_ss", 2)
    tp = [P.ps(ph, tag + "_tp%d" % i, [128, 4, 128], BF16) for i in range(2)]
    tp_t = sc.tiles_n(tag + "_tp", 2)
    x = G["x"]
    new_tiles = [wb_t] + junk_t + hb_t + ss_t + tp_t
    for i in range(NT):
        b = i % 2
        xt = G["xt"][i]
        sc.op("dve", lambda e, i=i, b=b: e.scalar_tensor_tensor(out=junk[b][:], in0=x[:, i, :], scalar=1.0,
                                                                in1=x[:, i, :], op0=ALU.mult, op1=ALU.mult,
                                                                accum_out=ss[b][:, 0:1]),
              reads=[xt], writes=[junk_t[b], ss_t[b]])
        sc.op("dve", lambda e, b=b: e.tensor_scalar(out=ss[b][:, 1:2], in0=ss[b][:, 0:1], scalar1=1.0 / D,
                                                    scalar2=EPS, op0=ALU.mult, op1=ALU.add),
              reads=[ss_t[b]], writes=[ss_t[b]])
        sc.op("pool", lambda e, b=b: e.tensor_tensor(out=ss[b][:, 0:1], in0=ss[b][:, 1:2],
                                                     in1=G["neghalf"][:, 0:1], op=ALU.pow),
              reads=[ss_t[b], G["neghalf_t"]], writes=[ss_t[b]])
        sc.op("dve", lambda e, i=i, b=b: e.scalar_tensor_tensor(out=hb[b][:], in0=x[:, i, :],
                                                                scalar=ss[b][:, 0:1], in1=wb[:],
                                                                op0=ALU.mult, op1=ALU.mult),
              reads=[xt, ss_t[b], wb_t], writes=[hb_t[b]])
        for half in range(2):
            pb = (2 * i + half) % 2
            for j in range(4):
                kc = half * 4 + j
                sc.op("pe", lambda e, b=b, pb=pb, j=j, kc=kc: e.transpose(
                    out=tp[pb][:, j, :], in_=hb[b][:, kc * 128:(kc + 1) * 128], identity=G["ident"][:]),
                    reads=[hb_t[b], G["ident_t"]], writes=[tp_t[pb]], part=(j > 0))
            eng = "act" if half == 0 else "dve"
            if eng == "act":
                sc.op("act", lambda e, pb=pb, half=half, i=i: e.copy(
                    out=hT[:, half * 4:half * 4 + 4, i * 128:(i + 1) * 128], in_=tp[pb][:]),
                    reads=[tp_t[pb]], writes=[hT_t[i]], part=True)
            else:
                sc.op("dve", lambda e, pb=pb, half=half, i=i: e.tensor_copy(
                    out=hT[:, half * 4:half * 4 + 4, i * 128:(i + 1) * 128], in_=tp[pb][:]),
                    reads=[tp_t[pb]], writes=[hT_t[i]], part=True)
    return new_tiles


class WStream:
    def __init__(self, P, sc, ph, tag, kdim, ncol, nf=2, nb=3):
        self.sc = sc
        self.kdim, self.ncol = kdim, ncol
        self.nf, self.nb = nf, nb
        self.f = [P.sb(ph, "%s_wf%d" % (tag, i), [128, kdim, ncol], F32) for i in range(nf)]
        self.f_t = sc.tiles_n(tag + "_wf", nf)
        self.b = [P.sb(ph, "%s_wb%d" % (tag, i), [128, kdim, ncol], BF16) for i in range(nb)]
        self.b_t = sc.tiles_n(tag + "_wbt", nb)
        self.tiles = self.f_t + self.b_t
        self.items = []

    def start(self, items):
        self.items = items
        self._load(0)
        self._load(1)
        self._cast(0)

    def _load(self, g):
        if g >= len(self.items):
            return
        ap, k, n = self.items[g]
        fs = g % self.nf
        self.sc.dma("sp", self.f[fs][:, 0:k, 0:n], ap, owner=self.f_t[fs], writes=[self.f_t[fs]])

    def _cast(self, g):
        if g >= len(self.items):
            return
        ap, k, n = self.items[g]
        fs, bs = g % self.nf, g % self.nb
        self.sc.op("pool", lambda e: e.tensor_copy(out=self.b[bs][:, 0:k, 0:n], in_=self.f[fs][:, 0:k, 0:n]),
                   reads=[self.f_t[fs]], writes=[self.b_t[bs]])

    def get(self, g):
        self._cast(g + 1)
        self._load(g + 2)
        return self.b[g % self.nb], self.b_t[g % self.nb]


def proj_groups():
    g = []

    def seg(off, n, mode, key):
        c = 0
        while c < n:
            w = min(512, n - c)
            g.append((off + c, w, mode, key, c))
            c += w
    seg(O_Z, 1024, "tok", "z")
    seg(O_XBC, 1536, "feat", "xbc")
    g.append((O_DTF, 32, "tok32", "dt", 0))
    seg(O_GQ, 512, "feat", "gq")
    seg(O_GK, 512, "feat", "gk")
    seg(O_GV, 1024, "tok", "gv")
    seg(O_GG, 1024, "tok", "gg")
    g.append((O_GAF, 32, "feat32", "ga", 0))
    seg(O_NQ, 1024, "feat", "nq")
    seg(O_NK, 1024, "feat", "nk")
    seg(O_NV, 1024, "tok", "nv")
    seg(O_GATE, 3072, "feat", "gate")
    return g


def phase_A(P, sc, G, U, prm, l):
    nc = P.nc
    with contextlib.ExitStack() as ph:
        hT = P.sb(ph, "A_hT", [128, 8, S], BF16)
        hT_t = sc.tiles_n("A_hT", NT)
        tiles = list(hT_t)
        tiles += rms_transpose(P, sc, G, ph, prm["norm_mix_w"][l], hT, hT_t, l, "A")
        wst = WStream(P, sc, ph, "A", 8, 512)
        acc = [P.ps(ph, "A_acc%d" % i, [128, 512], F32) for i in range(4)]
        acc_t = sc.tiles_n("A_acc", 4)
        NS = 3
        stg = [P.sb(ph, "A_stg%d" % i, [128, 2048], BF16) for i in range(NS)]
        stg_t = sc.tiles_n("A_stg", NS)
        stf = [P.sb(ph, "A_stf%d" % i, [128, 4, 32], F32) for i in range(2)]
        stf_t = sc.tiles_n("A_stf", 2)
        G["ga_stage"] = P.sb(ph, "A_gast", [32, 2048], F32)
        G["ga_stage_t"] = sc.tile("A_gast")
        tiles += wst.tiles + acc_t + stg_t + stf_t + [G["ga_stage_t"]]
        wv = prm["w_in"][l].rearrange("(kc p) n -> p kc n", p=128)
        groups = proj_groups()
        wst.start([(wv[:, :, c0:c0 + n], 8, n) for (c0, n, _m, _k, _d) in groups])
        ai = 0
        si = 0
        ev = 0
        for gi, (c0, n, mode, key, doff) in enumerate(groups):
            wcur, wcur_t = wst.get(gi)
            dst = U[key]
            dst_t = G["dram_t"][key]
            if mode in ("tok", "tok32"):
                for tb in range(4):
                    if mode == "tok":
                        st = si % NS
                        si += 1
                    else:
                        st = tb % 2
                    for j in range(4):
                        i = tb * 4 + j
                        a = ai % 4
                        ai += 1
                        for kc in range(8):
                            sc.op("pe", lambda e, a=a, kc=kc, i=i, wcur=wcur, n=n: e.matmul(
                                acc[a][:, 0:n], lhsT=hT[:, kc, i * 128:(i + 1) * 128], rhs=wcur[:, kc, 0:n],
                                start=(kc == 0), stop=(kc == 7)),
                                reads=[hT_t[i], wcur_t], writes=[acc_t[a]], part=(kc > 0))
                        if mode == "tok":
                            o_ap = stg[st][:, j * 512:j * 512 + n]
                            o_t = stg_t[st]
                        else:
                            o_ap = stf[st][:, j, 0:n]
                            o_t = stf_t[st]
                        ev += 1
                        if ev % 2 == 0:
                            sc.op("act", lambda e, o_ap=o_ap, a=a, n=n: e.copy(out=o_ap, in_=acc[a][:, 0:n]),
                                  reads=[acc_t[a]], writes=[o_t], part=(j > 0))
                        else:
                            sc.op("dve", lambda e, o_ap=o_ap, a=a, n=n: e.tensor_copy(out=o_ap, in_=acc[a][:, 0:n]),
                                  reads=[acc_t[a]], writes=[o_t], part=(j > 0))
                    rows = dst[tb * 512:(tb + 1) * 512, doff:doff + n].rearrange("(j p) c -> p j c", p=128)
                    if mode == "tok":
                        src = stg[st][:].rearrange("p (j c) -> p j c", j=4)[:, :, 0:n]
                        sc.dma("pool", rows, src, owner=stg_t[st], reads=[stg_t[st]], writes=[dst_t], part=True)
                    else:
                        sc.dma("pool", rows, stf[st][:, :, 0:n], owner=stf_t[st], reads=[stf_t[st]], writes=[dst_t],
                               part=True)
            else:
                nchunk = (n + 127) // 128
                for c in range(nchunk):
                    m = min(128, n - c * 128)
                    if mode == "feat":
                        st = si % NS
                        si += 1
                    else:
                        st = 0
                    for tb in range(4):
                        a = ai % 4
                        ai += 1
                        for kc in range(8):
                            sc.op("pe", lambda e, a=a, kc=kc, tb=tb, wcur=wcur, c=c, m=m: e.matmul(
                                acc[a][0:m, :], lhsT=wcur[:, kc, c * 128:c * 128 + m],
                                rhs=hT[:, kc, tb * 512:(tb + 1) * 512], start=(kc == 0), stop=(kc == 7)),
                                reads=hT_t[tb * 4:tb * 4 + 4] + [wcur_t], writes=[acc_t[a]], part=(kc > 0))
                        ev += 1
                        if mode == "feat":
                            o_ap = stg[st][0:m, tb * 512:(tb + 1) * 512]
                            o_t = stg_t[st]
                            if ev % 2 == 0:
                                sc.op("act", lambda e, o_ap=o_ap, a=a, m=m: e.copy(out=o_ap, in_=acc[a][0:m, :]),
                                      reads=[acc_t[a]], writes=[o_t], part=(tb > 0))
                            else:
                                sc.op("dve", lambda e, o_ap=o_ap, a=a, m=m: e.tensor_copy(out=o_ap, in_=acc[a][0:m, :]),
                                      reads=[acc_t[a]], writes=[o_t], part=(tb > 0))
                        else:
                            sc.op("dve", lambda e, a=a, m=m, tb=tb, gast=G["ga_stage"]: e.tensor_copy(
                                out=gast[0:m, tb * 512:(tb + 1) * 512], in_=acc[a][0:m, :]),
                                reads=[acc_t[a]], writes=[G["ga_stage_t"]], part=(tb > 0))
                    if mode == "feat":
                        sc.dma("pool", dst[doff + c * 128:doff + c * 128 + m, :], stg[st][0:m, :], owner=stg_t[st],
                               reads=[stg_t[st]], writes=[dst_t], part=True)
                    else:
                        sc.dma("pool", dst[0:m, :], G["ga_stage"][0:m, :], owner=G["ga_stage_t"],
                               reads=[G["ga_stage_t"]], writes=[dst_t], part=True)
        sc.barrier(release=tiles)


def phase_E(P, sc, G, U, YB, prm, l):
    nc = P.nc
    x = G["x"]
    with contextlib.ExitStack() as ph:
        wst = WStream(P, sc, ph, "E", 8, 256, nf=2, nb=2)
        mix = P.sb(ph, "E_mix", [128, 8, 1024], F32)
        mix_t = sc.tiles_n("E_mix", 8)
        mixb = P.sb(ph, "E_mixb", [128, 8, 1024], BF16)
        mixb_t = sc.tile("E_mixb")
        ybT = [P.sb(ph, "E_yb%d" % i, [128, 8, 1024], BF16) for i in range(2)]
        ybT_t = sc.tiles_n("E_yb", 2)
        gsl = [P.sb(ph, "E_g%d" % i, [128, 1024], BF16) for i in range(3)]
        gsl_t = sc.tiles_n("E_g", 3)
        sig = [P.sb(ph, "E_sig%d" % i, [128, 1024], F32) for i in range(2)]
        sig_t = sc.tiles_n("E_sig", 2)
        tmp = [P.sb(ph, "E_tmp%d" % i, [128, 512], F32) for i in range(2)]
        tmp_t = sc.tiles_n("E_tmp", 2)
        acc = [P.ps(ph, "E_acc%d" % i, [128, 512], F32) for i in range(4)]
        acc_t = sc.tiles_n("E_acc", 4)
        tiles = wst.tiles + mix_t + [mixb_t] + ybT_t + gsl_t + sig_t + tmp_t + acc_t
        wnames = ["w_branch_ssd", "w_branch_gla", "w_branch_na", "w_out"]
        bnames = ["ssd", "gla", "na"]
        items = []
        for half in range(2):
            for wn in wnames:
                wv = prm[wn][l].rearrange("(kc p) n -> p kc n", p=128)
                for cg in range(4):
                    items.append((wv[:, :, cg * 256:(cg + 1) * 256], 8, 256))
        wst.start(items)
        gi = 0
        ai = 0
        gcount = 0
        tcount = 0
        ybcount = 0
        for half in range(2):
            t0 = half * 1024
            for b in range(3):
                ys = ybcount % 2
                ybcount += 1
                ybv = YB[bnames[b]].rearrange("(kc p) t -> p kc t", p=128)
                sc.dma("sp", ybT[ys][:], ybv[:, :, t0:t0 + 1024], owner=ybT_t[ys],
                       reads=[G["dram_t"]["yb_" + bnames[b]]], writes=[ybT_t[ys]])
                for cg in range(4):
                    wcur, wcur_t = wst.get(gi)
                    gi += 1
                    for ecl in range(2):
                        ec = cg * 2 + ecl
                        gs = gcount % 3
                        ss_ = gcount % 2
                        gcount += 1
                        grow = b * 1024 + ec * 128
                        sc.dma("sp", gsl[gs][:], U["gate"][grow:grow + 128, t0:t0 + 1024], owner=gsl_t[gs],
                               reads=[G["dram_t"]["gate"]], writes=[gsl_t[gs]])
                        sc.op("act", lambda e, gs=gs, ss_=ss_: e.activation(out=sig[ss_][:], in_=gsl[gs][:],
                                                                            func=AF.Sigmoid),
                              reads=[gsl_t[gs]], writes=[sig_t[ss_]])
                        for tbh in range(2):
                            a = ai % 4
                            ai += 1
                            for kc in range(8):
                                sc.op("pe", lambda e, a=a, kc=kc, wcur=wcur, ecl=ecl, ys=ys, tbh=tbh: e.matmul(
                                    acc[a][:], lhsT=wcur[:, kc, ecl * 128:(ecl + 1) * 128],
                                    rhs=ybT[ys][:, kc, tbh * 512:(tbh + 1) * 512], start=(kc == 0), stop=(kc == 7)),
                                    reads=[wcur_t, ybT_t[ys]], writes=[acc_t[a]], part=(kc > 0))
                            msl = mix[:, ec, tbh * 512:(tbh + 1) * 512]
                            sgl = sig[ss_][:, tbh * 512:(tbh + 1) * 512]
                            if b == 0:
                                sc.op("dve", lambda e, msl=msl, a=a, sgl=sgl: e.tensor_tensor(
                                    out=msl, in0=acc[a][:], in1=sgl, op=ALU.mult),
                                    reads=[acc_t[a], sig_t[ss_]], writes=[mix_t[ec]], part=(tbh > 0))
                            else:
                                ts = tcount % 2
                                tcount += 1
                                sc.op("dve", lambda e, ts=ts, a=a, sgl=sgl: e.tensor_tensor(
                                    out=tmp[ts][:], in0=acc[a][:], in1=sgl, op=ALU.mult),
                                    reads=[acc_t[a], sig_t[ss_]], writes=[tmp_t[ts]])
                                if b == 1:
                                    sc.op("pool", lambda e, msl=msl, ts=ts: e.tensor_tensor(
                                        out=msl, in0=msl, in1=tmp[ts][:], op=ALU.add),
                                        reads=[tmp_t[ts], mix_t[ec]], writes=[mix_t[ec]])
                                else:
                                    sc.op("pool", lambda e, msl=msl, ts=ts, ec=ec, tbh=tbh: e.tensor_tensor(
                                        out=mixb[:, ec, tbh * 512:(tbh + 1) * 512], in0=msl, in1=tmp[ts][:],
                                        op=ALU.add),
                                        reads=[tmp_t[ts], mix_t[ec]], writes=[mixb_t], part=True)
            for cg in range(4):
                wcur, wcur_t = wst.get(gi)
                gi += 1
                for j in range(8):
                    i = half * 8 + j
                    a = ai % 4
                    ai += 1
                    for ec in range(8):
                        sc.op("pe", lambda e, a=a, ec=ec, wcur=wcur, j=j: e.matmul(
                            acc[a][:, 0:256], lhsT=mixb[:, ec, j * 128:(j + 1) * 128], rhs=wcur[:, ec, :],
                            start=(ec == 0), stop=(ec == 7)),
                            reads=[wcur_t, mixb_t], writes=[acc_t[a]], part=(ec > 0))
                    xs = x[:, i, cg * 256:(cg + 1) * 256]
                    sc.op("dve", lambda e, xs=xs, a=a: e.tensor_tensor(out=xs, in0=xs, in1=acc[a][:, 0:256], op=ALU.add),
                          reads=[acc_t[a], G["xt"][i]], writes=[G["xt"][i]])
        sc.barrier(release=tiles)


class Ring:
    def __init__(self, P, sc, stack, name, shape, dt, n, psum=False, views=None):
        if views is not None:
            self.h = views
            n = len(views)
        else:
            mk = P.ps if psum else P.sb
            self.h = [mk(stack, "%s%d" % (name, i), shape, dt) for i in range(n)]
        self.t = sc.tiles_n(name + "_", n)
        self.i = 0
        self.n = n

    def next(self):
        k = self.i % self.n
        self.i += 1
        return self.h[k], self.t[k]


def build_tri(P, sc, G, top):
    for nm in ("trif", "trib", "trif64", "trib64", "mcf64", "mcb64", "trifs", "tribs"):
        G[nm] = P.sb(top, nm, [128, 128], F32)
        G[nm + "_t"] = sc.tile(nm)
    ones_f, ones_t = G["ones_f"], G["ones_t"]
    sc.op("pool", lambda e: e.affine_select(out=G["trif"][:], in_=ones_f[:], pattern=[[1, 128]], compare_op=ALU.is_ge,
                                            fill=0.0, base=0, channel_multiplier=-1),
          reads=[ones_t], writes=[G["trif_t"]])
    sc.op("pool", lambda e: e.affine_select(out=G["trib"][:], in_=ones_f[:], pattern=[[-1, 128]], compare_op=ALU.is_ge,
                                            fill=0.0, base=0, channel_multiplier=1),
          reads=[ones_t], writes=[G["trib_t"]])
    sc.op("pool", lambda e: e.affine_select(out=G["trifs"][:], in_=ones_f[:], pattern=[[1, 128]], compare_op=ALU.is_gt,
                                            fill=0.0, base=0, channel_multiplier=-1),
          reads=[ones_t], writes=[G["trifs_t"]])
    sc.op("pool", lambda e: e.affine_select(out=G["tribs"][:], in_=ones_f[:], pattern=[[-1, 128]], compare_op=ALU.is_gt,
                                            fill=0.0, base=0, channel_multiplier=1),
          reads=[ones_t], writes=[G["tribs_t"]])
    sc.op("pool", lambda e: e.tensor_copy(out=G["trif64"][:], in_=G["trif"][:]), reads=[G["trif_t"]], writes=[G["trif64_t"]])
    sc.op("pool", lambda e: e.memset(G["trif64"][0:64, 64:128], 0.0), reads=[G["trif64_t"]], writes=[G["trif64_t"]])
    sc.op("pool", lambda e: e.tensor_copy(out=G["trib64"][:], in_=G["trib"][:]), reads=[G["trib_t"]], writes=[G["trib64_t"]])
    sc.op("pool", lambda e: e.memset(G["trib64"][64:128, 0:64], 0.0), reads=[G["trib64_t"]], writes=[G["trib64_t"]])
    sc.op("pool", lambda e: e.tensor_scalar(out=G["mcf64"][:], in0=G["trif64"][:], scalar1=-1.0 / 16.0, scalar2=None,
                                            op0=ALU.mult), reads=[G["trif64_t"]], writes=[G["mcf64_t"]])
    sc.op("pool", lambda e: e.tensor_scalar(out=G["mcb64"][:], in0=G["trib64"][:], scalar1=-1.0 / 16.0, scalar2=None,
                                            op0=ALU.mult), reads=[G["trib64_t"]], writes=[G["mcb64_t"]])


def phase_C(P, sc, G, U, YB, prm, l):
    nc = P.nc
    ident = G["ident"]
    with contextlib.ExitStack() as ph:
        qT = P.sb(ph, "C_qT", [128, 4, S], BF16)
        kT = P.sb(ph, "C_kT", [128, 4, S], BF16)
        qT_t = sc.tile("C_qT")
        kT_t = sc.tile("C_kT")
        ob = P.sb(ph, "C_ob", [128, NT, 1024], BF16)
        ob_t = sc.tiles_n("C_ob", NT)
        gaX = P.sb(ph, "C_gaX", [32, S], F32)
        gaX_t = sc.tile("C_gaX")
        a2X = [P.sb(ph, "C_a2X%d" % d, [32, 512], F32) for d in range(2)]
        a2X_t = sc.tiles_n("C_a2X", 2)
        nwb = P.sb(ph, "C_nwb", [128, 256], F32)
        nwb_t = sc.tile("C_nwb")
        Sf = P.sb(ph, "C_Sf", [128, 4, 256], F32)
        Sf_t = sc.tile("C_Sf")
        yst = P.sb(ph, "C_yst", [128, 8, 256], BF16)
        yst_t = sc.tile("C_yst")
        R = lambda name, shape, dt, n, psum=False: Ring(P, sc, ph, "C_" + name, shape, dt, n, psum)
        r_Sb = R("Sb", [128, 4, 256], BF16, 3)
        r_v = R("v", [128, 1024], BF16, 2)
        r_gg = R("gg", [128, 4, 1024], BF16, 1)
        r_e1 = R("e1", [128, 512], F32, 1)
        r_bs = R("bs", [128, 4, 128], F32, 1)
        r_eb = R("eb", [128, 4, 128], F32, 1)
        r_enb = R("enb", [128, 4, 128], F32, 1)
        r_ew = R("ew", [128, 4, 128], F32, 1)
        r_ed = R("ed", [128, 4, 2], F32, 3)
        r_qd = R("qd", [128, 4, 128], BF16, 2)
        r_kd = R("kd", [128, 4, 128], BF16, 2)
        r_kw = R("kw", [128, 4, 128], BF16, 1)
        r_kwt = R("kwt", [128, 4, 128], BF16, 2)
        r_am = R("am", [128, 4, 128], BF16, 2)
        r_oa = R("oa", [128, 1024], F32, 1)
        r_sg = R("sg", [128, 4, 1024], BF16, 1)
        r_jk = R("jk", [128, 256], BF16, 1)
        r_ss = R("ss", [128, 8], F32, 2)
        r_y = R("y", [128, 1024], BF16, 2)
        r_gp = R("gp", [128, 512], F32, 1, True)
        r_bT = R("bT", [128, 4, 128], F32, 1, True)
        r_att = R("att", [128, 4, 128], F32, 1, True)
        r_kwp = R("kwp", [128, 4, 128], BF16, 1, True)
        r_st = R("st", [128, 4, 256], F32, 1, True)
        r_o = R("o", [128, 4, 256], F32, 1, True)
        rings = [r_Sb, r_v, r_gg, r_e1, r_bs, r_eb, r_enb, r_ew, r_ed, r_qd, r_kd, r_kw, r_kwt, r_am, r_oa, r_sg,
                 r_jk, r_ss, r_y, r_gp, r_bT, r_att, r_kwp, r_st, r_o]
        tiles = [qT_t, kT_t, nwb_t, yst_t, gaX_t, Sf_t] + ob_t + a2X_t
        for r in rings:
            tiles += r.t
        sc.dma("sp", qT[:], U["gq"].rearrange("(h p) t -> p h t", p=128), owner=qT_t, reads=[G["dram_t"]["gq"]],
               writes=[qT_t])
        sc.dma("sp", kT[:], U["gk"].rearrange("(h p) t -> p h t", p=128), owner=kT_t, reads=[G["dram_t"]["gk"]],
               writes=[kT_t])
        sc.dma("sp", nwb[:], prm["gla_norm_w"][l].partition_broadcast(128), owner=nwb_t, writes=[nwb_t])
        for d in range(2):
            a2 = prm["gla_a2_f" if d == 0 else "gla_a2_b"][l]
            bi = prm["gla_a2_bias_f" if d == 0 else "gla_a2_bias_b"][l]
            sc.dma("sp", a2X[d][0:16, :], a2, owner=a2X_t[d], writes=[a2X_t[d]])
            sc.dma("sp", a2X[d][16:17, :], bi.rearrange("(o n) -> o n", o=1), owner=a2X_t[d], writes=[a2X_t[d]], part=True)

        def gla_pass(d):
            fwd = (d == 0)
            mc, mc_t = (G["mcf64"], G["mcf64_t"]) if fwd else (G["mcb64"], G["mcb64_t"])
            ma, ma_t = (G["trif64"], G["trif64_t"]) if fwd else (G["trib64"], G["trib64_t"])
            lc0 = 63 if fwd else 0
            sc.op("pool", lambda e: e.memset(gaX[:], 1.0), writes=[gaX_t])
            sc.dma("sp", gaX[0:16, :], U["ga"][16 * d:16 * d + 16, :], owner=gaX_t, reads=[G["dram_t"]["ga"]],
                   writes=[gaX_t])
            sc.op("pool", lambda e: e.memset(Sf[:], 0.0), writes=[Sf_t])
            sb0, sb0_t = r_Sb.next()
            sc.op("pool", lambda e, sb0=sb0: e.memset(sb0[:], 0.0), writes=[sb0_t])
            cur = [(sb0, sb0_t)]
            sgcur = [None]
            order = list(range(NT)) if fwd else list(range(NT - 1, -1, -1))
            chunks = (0, 1) if fwd else (1, 0)

            def stage1(i):
                tsl = slice(i * 128, (i + 1) * 128)
                v, v_t = r_v.next()
                sc.dma("sp", v[:], U["gv"][tsl, :], owner=v_t, reads=[G["dram_t"]["gv"]], writes=[v_t])
                gp, gp_t = r_gp.next()
                sc.op("pe", lambda e, gp=gp, tsl=tsl: e.matmul(gp[:], lhsT=gaX[0:17, tsl], rhs=a2X[d][0:17, :],
                                                               start=True, stop=True),
                      reads=[gaX_t, a2X_t[d]], writes=[gp_t])
                e1, e1_t = r_e1.next()
                sc.op("act", lambda e, e1=e1, gp=gp: e.activation(out=e1[:], in_=gp[:], func=AF.Exp, scale=-1.0),
                      reads=[gp_t], writes=[e1_t])
                gn, gn_t = e1, e1_t
                sc.op("act", lambda e, gn=gn, e1=e1: e.activation(out=gn[:], in_=e1[:], func=AF.Ln, bias=G["one"][:, 0:1]),
                      reads=[e1_t, G["one_t"]], writes=[gn_t])
                bT, bT_t = r_bT.next()
                for h in range(4):
                    sc.op("pe", lambda e, bT=bT, gn=gn, h=h: e.matmul(bT[:, h, :], lhsT=gn[:, h * 128:(h + 1) * 128], rhs=mc[:],
                                                                     start=True, stop=True, skip_group_check=True),
                          reads=[gn_t, mc_t], writes=[bT_t], part=(h > 0))
                bs, bs_t = r_bs.next()
                sc.op("act", lambda e, bs=bs, bT=bT: e.copy(out=bs[:], in_=bT[:]), reads=[bT_t], writes=[bs_t])
                eb, eb_t = r_eb.next()
                sc.op("act", lambda e, eb=eb, bs=bs: e.activation(out=eb[:], in_=bs[:], func=AF.Exp), reads=[bs_t], writes=[eb_t])
                enb, enb_t = r_enb.next()
                sc.op("act", lambda e, enb=enb, bs=bs: e.activation(out=enb[:], in_=bs[:], func=AF.Exp, scale=-1.0),
                      reads=[bs_t], writes=[enb_t])
                ed, ed_t = r_ed.next()
                sc.op("act", lambda e, ed=ed, bs=bs: e.activation(
                    out=ed[:], in_=bs[:].rearrange("p h (c l) -> p h c l", c=2)[:, :, :, lc0], func=AF.Exp),
                    reads=[bs_t], writes=[ed_t])
                qd, qd_t = r_qd.next()
                sc.op("dve", lambda e, qd=qd, tsl=tsl, eb=eb: e.scalar_tensor_tensor(
                    out=qd[:], in0=qT[:, :, tsl], scalar=128.0 ** -0.5, in1=eb[:], op0=ALU.mult, op1=ALU.mult),
                    reads=[qT_t, eb_t], writes=[qd_t])
                kd, kd_t = r_kd.next()
                sc.op("dve", lambda e, kd=kd, tsl=tsl, enb=enb: e.tensor_tensor(
                    out=kd[:], in0=kT[:, :, tsl], in1=enb[:], op=ALU.mult), reads=[kT_t, enb_t], writes=[kd_t])
                ew, ew_t = r_ew.next()
                sc.op("dve", lambda e, ew=ew, enb=enb, ed=ed: e.tensor_tensor(
                    out=ew[:].rearrange("p h (c l) -> p (h c) l", c=2), in0=enb[:].rearrange("p h (c l) -> p (h c) l", c=2),
                    in1=ed[:].rearrange("p h c -> p (h c)").unsqueeze(2).to_broadcast([128, 8, 64]), op=ALU.mult),
                    reads=[enb_t, ed_t], writes=[ew_t])
                kw, kw_t = r_kw.next()
                sc.op("dve", lambda e, kw=kw, tsl=tsl, ew=ew: e.tensor_tensor(
                    out=kw[:], in0=kT[:, :, tsl], in1=ew[:], op=ALU.mult), reads=[kT_t, ew_t], writes=[kw_t])
                kwp, kwp_t = r_kwp.next()
                for h in range(4):
                    sc.op("pe", lambda e, kwp=kwp, kw=kw, h=h: e.transpose(out=kwp[:, h, :], in_=kw[:, h, :], identity=ident[:]),
                          reads=[kw_t, G["ident_t"]], writes=[kwp_t], part=(h > 0))
                kwt, kwt_t = r_kwt.next()
                sc.op("act", lambda e, kwt=kwt, kwp=kwp: e.copy(out=kwt[:], in_=kwp[:]), reads=[kwp_t], writes=[kwt_t])
                att, att_t = r_att.next()
                for h in range(4):
                    sc.op("pe", lambda e, att=att, kd=kd, qd=qd, h=h: e.matmul(att[:, h, :], lhsT=kd[:, h, :], rhs=qd[:, h, :],
                                                                            start=True, stop=True, skip_group_check=True),
                          reads=[kd_t, qd_t], writes=[att_t], part=(h > 0))
                am, am_t = r_am.next()
                sc.op("dve", lambda e, am=am, att=att: e.tensor_tensor(
                    out=am[:], in0=att[:], in1=ma[:].unsqueeze(1).to_broadcast([128, 4, 128]), op=ALU.mult),
                    reads=[att_t, ma_t], writes=[am_t])
                return (i, tsl, v, v_t, qd, qd_t, kwt, kwt_t, ed, ed_t, am, am_t)

            def stage23(ctx):
                (i, tsl, v, v_t, qd, qd_t, kwt, kwt_t, ed, ed_t, am, am_t) = ctx
                sbs = [cur[0]]
                for ci, c in enumerate(chunks):
                    cs = slice(c * 64, (c + 1) * 64)
                    st, st_t = r_st.next()
                    for h in range(4):
                        sc.op("pe", lambda e, st=st, kwt=kwt, cs=cs, v=v, h=h: e.matmul(
                            st[:, h, :], lhsT=kwt[cs, h, :], rhs=v[cs, h * 256:(h + 1) * 256], start=True, stop=True,
                            skip_group_check=True),
                            reads=[kwt_t, v_t], writes=[st_t], part=(h > 0))
                    for h in range(4):
                        sc.op("dve", lambda e, st=st, h=h, ed=ed, c=c: e.scalar_tensor_tensor(
                            out=Sf[:, h, :], in0=Sf[:, h, :], scalar=ed[:, h, c:c + 1], in1=st[:, h, :], op0=ALU.mult,
                            op1=ALU.add),
                            reads=[st_t, ed_t, Sf_t], writes=[Sf_t])
                    nb, nb_t = r_Sb.next()
                    sc.op("act", lambda e, nb=nb: e.copy(out=nb[:], in_=Sf[:]), reads=[Sf_t], writes=[nb_t])
                    sbs.append((nb, nb_t))
                o, o_t = r_o.next()
                for h in range(4):
                    sc.op("pe", lambda e, o=o, am=am, v=v, h=h: e.matmul(o[:, h, :], lhsT=am[:, h, :],
                                                                       rhs=v[:, h * 256:(h + 1) * 256],
                                                                       start=True, stop=False, skip_group_check=True),
                          reads=[am_t, v_t], writes=[o_t], part=(h > 0))
                    for ci, c in enumerate(chunks):
                        cs = slice(c * 64, (c + 1) * 64)
                        sbv, sbv_t = sbs[ci]
                        sc.op("pe", lambda e, o=o, qd=qd, cs=cs, sbv=sbv, ci=ci, h=h: e.matmul(
                            o[cs, h, :], lhsT=qd[:, h, cs], rhs=sbv[:, h, :], start=False, stop=(ci == 1),
                            skip_group_check=True),
                            reads=[qd_t, sbv_t], writes=[o_t], part=True)
                cur[0] = sbs[2]
                if not fwd:
                    for hb in range(2):
                        sc.op("act", lambda e, o=o, hb=hb, i=i: e.copy(
                            out=ob[:, i, hb * 512:(hb + 1) * 512], in_=o[:, 2 * hb:2 * hb + 2, :].rearrange("p a b -> p (a b)")),
                            reads=[o_t], writes=[ob_t[i]], part=(hb > 0))
                    return
                oa, oa_t = r_oa.next()
                ss, ss_t = r_ss.next()
                for hb in range(2):
                    sc.op("dve", lambda e, oa=oa, o=o, hb=hb, i=i: e.tensor_tensor(
                        out=oa[:, hb * 512:(hb + 1) * 512], in0=o[:, 2 * hb:2 * hb + 2, :].rearrange("p a b -> p (a b)"),
                        in1=ob[:, i, hb * 512:(hb + 1) * 512], op=ALU.add),
                        reads=[o_t, ob_t[i]], writes=[oa_t], part=(hb > 0))
                for h in range(4):
                    hs = slice(h * 256, (h + 1) * 256)
                    jk, jk_t = r_jk.next()
                    sc.op("dve", lambda e, jk=jk, oa=oa, hs=hs, ss=ss, h=h: e.scalar_tensor_tensor(
                        out=jk[:], in0=oa[:, hs], scalar=1.0, in1=oa[:, hs], op0=ALU.mult, op1=ALU.mult,
                        accum_out=ss[:, h:h + 1]), reads=[oa_t], writes=[jk_t, ss_t])
                if i % 4 == 0:
                    gg, gg_t = r_gg.next()
                    sc.dma("sp", gg[:], U["gg"][i * 128:(i + 4) * 128, :].rearrange("(j p) c -> p j c", p=128), owner=gg_t,
                           reads=[G["dram_t"]["gg"]], writes=[gg_t])
                    sg, sg_t = r_sg.next()
                    sc.op("act", lambda e, sg=sg, gg=gg: e.activation(out=sg[:], in_=gg[:], func=AF.Silu),
                          reads=[gg_t], writes=[sg_t])
                    sgcur[0] = (sg, sg_t)
                sg, sg_t = sgcur[0]
                sgn = sg[:, i % 4, :]
                sgn_t = sg_t
                sc.op("pool", lambda e, sgn=sgn: e.tensor_tensor(
                    out=sgn.rearrange("p (h v) -> p h v", h=4), in0=sgn.rearrange("p (h v) -> p h v", h=4),
                    in1=nwb[:].unsqueeze(1).to_broadcast([128, 4, 256]), op=ALU.mult),
                    reads=[sg_t, nwb_t], writes=[sg_t])
                sc.op("dve", lambda e, ss=ss: e.tensor_scalar(out=ss[:, 4:8], in0=ss[:, 0:4], scalar1=1.0 / 256.0, scalar2=EPS,
                                                              op0=ALU.mult, op1=ALU.add), reads=[ss_t], writes=[ss_t])
                sc.op("pool", lambda e, ss=ss: e.tensor_tensor(out=ss[:, 0:4], in0=ss[:, 4:8], in1=G["neghalf"][:, 0:4],
                                                               op=ALU.pow), reads=[ss_t, G["neghalf_t"]], writes=[ss_t])
                sc.op("dve", lambda e, oa=oa, ss=ss: e.tensor_tensor(
                    out=oa[:].rearrange("p (h v) -> p h v", h=4), in0=oa[:].rearrange("p (h v) -> p h v", h=4),
                    in1=ss[:, 0:4].unsqueeze(2).to_broadcast([128, 4, 256]), op=ALU.mult),
                    reads=[oa_t, ss_t], writes=[oa_t])
                y, y_t = r_y.next()
                sc.op("dve", lambda e, y=y, oa=oa, sgn=sgn: e.tensor_tensor(out=y[:], in0=oa[:], in1=sgn, op=ALU.mult),
                      reads=[oa_t, sgn_t], writes=[y_t])
                return (y, y_t, i)

            def stage3(c3):
                if c3 is None:
                    return
                (y, y_t, i) = c3
                emit_yT(P, sc, G, r_kwp, y, y_t, yst, yst_t, i, YB["gla"], G["dram_t"]["yb_gla"], gsz=2)

            prev = None
            prev3 = None
            for i in order:
                ctx = stage1(i)
                if prev is not None:
                    n3 = stage23(prev)
                    stage3(prev3)
                    prev3 = n3
                prev = ctx
            n3 = stage23(prev)
            stage3(prev3)
            stage3(n3)

        gla_pass(1)
        if "dbg_ob" in P.dbg:
            dob = P.dram("dbg_ob", [S, 1024], BF16)
            dt_ = sc.tile("dbg_ob")
            sc.dma("sp", dob.rearrange("(i p) c -> p i c", p=128), ob[:], owner=ob_t[0], reads=ob_t, writes=[dt_])
        gla_pass(0)
        sc.barrier(release=tiles)


def phase_B(P, sc, G, U, YB, prm, l):
    nc = P.nc
    ident = G["ident"]
    ybw = G["ybw"]
    ybw_t = G["dram_t"]["ybw"]
    with contextlib.ExitStack() as ph:
        xtok = P.sb(ph, "B_xtok", [128, NT, 1280], BF16)
        xtok_t = sc.tiles_n("B_xtok", NT)
        BT = P.sb(ph, "B_BT", [128, 2, S], BF16)
        CT = P.sb(ph, "B_CT", [128, 2, S], BF16)
        BT_t = sc.tiles_n("B_BT", 2)
        CT_t = sc.tiles_n("B_CT", 2)
        dtv = P.sb(ph, "B_dtv", [128, NT, 32], F32)
        av = P.sb(ph, "B_av", [128, NT, 32], F32)
        dtv_t = sc.tile("B_dtv")
        av_t = sc.tile("B_av")
        rows = P.sb(ph, "B_rows", [128, 4, 32], F32)
        rows_t = sc.tile("B_rows")
        nwb = P.sb(ph, "B_nwb", [128, 1024], F32)
        nwb_t = sc.tile("B_nwb")
        tiles = xtok_t + BT_t + CT_t + [dtv_t, av_t, rows_t, nwb_t]
        with contextlib.ExitStack() as s1:
            cwr = P.sb(s1, "B_cwr", [72, 128], F32)
            cwr_t = sc.tile("B_cwr")
            cw = P.sb(s1, "B_cw", [128, 72], F32)
            cw_t = sc.tile("B_cw")
            cwp = P.ps(s1, "B_cwp", [128, 72], F32)
            cwp_t = sc.tile("B_cwp")
            identf = P.sb(s1, "B_identf", [128, 128], F32)
            identf_t = sc.tile("B_identf")
            xc = [P.sb(s1, "B_xc%d" % i, [128, S + 4], BF16) for i in range(2)]
            xc_t = sc.tiles_n("B_xc", 2)
            dg = [P.sb(s1, "B_dg%d" % i, [128, 5, 128], BF16) for i in range(2)]
            dg_t = sc.tiles_n("B_dg", 2)
            cacc = [P.ps(s1, "B_cacc%d" % i, [128, 512], F32) for i in range(2)]
            cacc_t = sc.tiles_n("B_cacc", 2)
            xa = [P.sb(s1, "B_xa%d" % i, [128, S], BF16) for i in range(2)]
            xa_t = sc.tiles_n("B_xa", 2)
            tp = [P.ps(s1, "B_tp%d" % i, [128, 4, 128], BF16) for i in range(2)]
            tp_t = sc.tiles_n("B_tp", 2)
            tl1 = [cwr_t, cw_t, cwp_t, identf_t] + dg_t + cacc_t + xc_t + xa_t + tp_t
            sc.op("pool", lambda e: e.affine_select(out=identf[:], in_=G["ones_f"][:], pattern=[[-1, 128]],
                                                    compare_op=ALU.is_equal, fill=0.0, base=0, channel_multiplier=1),
                  reads=[G["ones_t"]], writes=[identf_t])
            sc.dma("sp", cwr[0:60, :], prm["ssd_conv_w"][l].rearrange("k (c p) -> (k c) p", p=128), owner=cwr_t, writes=[cwr_t])
            sc.dma("sp", cwr[60:72, :], prm["ssd_conv_b"][l].rearrange("(c p) -> c p", p=128), owner=cwr_t, writes=[cwr_t],
                   part=True)
            sc.op("pe", lambda e: e.transpose(out=cwp[:], in_=cwr[:], identity=identf[0:72, 0:72]),
                  reads=[cwr_t, identf_t], writes=[cwp_t])
            sc.op("act", lambda e: e.copy(out=cw[:], in_=cwp[:]), reads=[cwp_t], writes=[cw_t])
            for b in range(2):
                sc.op("pool", lambda e, b=b: e.memset(xc[b][:, 0:2], 0.0), writes=[xc_t[b]])
                sc.op("pool", lambda e, b=b: e.memset(xc[b][:, S + 2:S + 4], 0.0), writes=[xc_t[b]], part=True)
            sc.dma("sp", dtv[:], U["dt"].rearrange("(i p) c -> p i c", p=128), owner=dtv_t, reads=[G["dram_t"]["dt"]],
                   writes=[dtv_t])
            for k, nm in enumerate(("ssd_dt_bias_f", "ssd_dt_bias_b")):
                sc.dma("sp", rows[:, 0, 16 * k:16 * k + 16], prm[nm][l].partition_broadcast(128), owner=rows_t,
                       writes=[rows_t], part=True)
            for k, nm in enumerate(("ssd_a_log_f", "ssd_a_log_b")):
                sc.dma("sp", rows[:, 1, 16 * k:16 * k + 16], prm[nm][l].partition_broadcast(128), owner=rows_t,
                       writes=[rows_t], part=True)
            sc.dma("sp", rows[:, 2, 0:16], prm["ssd_d"][l].partition_broadcast(128), owner=rows_t, writes=[rows_t], part=True)
            sc.dma("sp", nwb[:], prm["ssd_norm_w"][l].partition_broadcast(128), owner=nwb_t, writes=[nwb_t])
            sc.op("dve", lambda e: e.tensor_tensor(out=dtv[:], in0=dtv[:], in1=rows[:, 0:1, :].to_broadcast([128, NT, 32]),
                                                   op=ALU.add), reads=[dtv_t, rows_t], writes=[dtv_t])
            sc.op("act", lambda e: e.activation(out=dtv[:], in_=dtv[:], func=AF.Exp), reads=[dtv_t], writes=[dtv_t])
            sc.op("act", lambda e: e.activation(out=dtv[:], in_=dtv[:], func=AF.Ln, bias=G["one"][:, 0:1]),
                  reads=[dtv_t, G["one_t"]], writes=[dtv_t])
            sc.op("act", lambda e: e.activation(out=rows[:, 3, :], in_=rows[:, 1, :], func=AF.Exp), reads=[rows_t],
                  writes=[rows_t])
            sc.op("dve", lambda e: e.scalar_tensor_tensor(out=av[:], in0=dtv[:], scalar=-1.0,
                                                          in1=rows[:, 3:4, :].to_broadcast([128, NT, 32]),
                                                          op0=ALU.mult, op1=ALU.mult),
                  reads=[dtv_t, rows_t], writes=[av_t])
            tpc = 0
            for c in range(12):
                b = c % 2
                sc.dma("sp", xc[b][:, 2:S + 2], U["xbc"][c * 128:(c + 1) * 128, :], owner=xc_t[b],
                       reads=[G["dram_t"]["xbc"]], writes=[xc_t[b]], part=True)
                dgb = c % 2
                for k in range(5):
                    sc.op("dve", lambda e, dgb=dgb, k=k, c=c: e.tensor_scalar(
                        out=dg[dgb][:, k, :], in0=identf[:], scalar1=cw[:, k * 12 + c:k * 12 + c + 1], scalar2=None,
                        op0=ALU.mult), reads=[identf_t, cw_t], writes=[dg_t[dgb]], part=(k > 0))
                if c < 10:
                    xo, xo_t = xa[b], xa_t[b]
                    xsl = lambda tb: xa[b][:, tb * 512:(tb + 1) * 512]
                else:
                    xo_t = CT_t[c - 10]
                    xsl = lambda tb, c=c: CT[:, c - 10, tb * 512:(tb + 1) * 512]
                for tb in range(4):
                    ca, ca_t = cacc[(4 * c + tb) % 2], cacc_t[(4 * c + tb) % 2]
                    for k in range(5):
                        sc.op("pe", lambda e, ca=ca, dgb=dgb, k=k, b=b, tb=tb: e.matmul(
                            ca[:], lhsT=dg[dgb][:, k, :], rhs=xc[b][:, k + tb * 512:k + tb * 512 + 512],
                            start=(k == 0), stop=(k == 4)),
                            reads=[dg_t[dgb], xc_t[b]], writes=[ca_t], part=(k > 0))
                    sc.op("act", lambda e, ca=ca, o_ap=xsl(tb), c=c: e.activation(
                        out=o_ap, in_=ca[:], func=AF.Silu, bias=cw[:, 60 + c:61 + c]),
                        reads=[ca_t, cw_t], writes=[xo_t], part=(tb > 0))
                if c in (8, 9):
                    sc.op("pool", lambda e, b=b, c=c: e.tensor_copy(out=BT[:, c - 8, :], in_=xa[b][:]),
                          reads=[xa_t[b]], writes=[BT_t[c - 8]])
                if c < 10:
                    for i0 in range(0, NT, 4):
                        tb_ = tpc % 2
                        tpc += 1
                        for j in range(4):
                            i = i0 + j
                            sc.op("pe", lambda e, tb_=tb_, j=j, b=b, i=i: e.transpose(
                                out=tp[tb_][:, j, :], in_=xa[b][:, i * 128:(i + 1) * 128], identity=ident[:]),
                                reads=[xa_t[b], G["ident_t"]], writes=[tp_t[tb_]], part=(j > 0))
                        eng = "act" if (tpc % 2) else "pool"
                        if eng == "act":
                            sc.op("act", lambda e, tb_=tb_, i0=i0, c=c: e.copy(
                                out=xtok[:, i0:i0 + 4, c * 128:(c + 1) * 128], in_=tp[tb_][:]),
                                reads=[tp_t[tb_]], writes=xtok_t[i0:i0 + 4], part=True)
                        else:
                            sc.op("dve", lambda e, tb_=tb_, i0=i0, c=c: e.tensor_copy(
                                out=xtok[:, i0:i0 + 4, c * 128:(c + 1) * 128], in_=tp[tb_][:]),
                                reads=[tp_t[tb_]], writes=xtok_t[i0:i0 + 4], part=True)
            sc.barrier(release=tl1)
        with contextlib.ExitStack() as s2:
            R = lambda name, shape, dt, n, psum=False: Ring(P, sc, s2, "B_" + name, shape, dt, n, psum)
            Sf = P.sb(s2, "B_Sf", [128, 2, 512], F32)
            Sf_t = sc.tiles_n("B_Sf", 2)
            Sbx = P.sb(s2, "B_Sb", [128, 2, 2, 512], BF16)
            r_Sb = [Ring(P, sc, s2, "B_Sb%d" % g, None, None, 2, views=[Sbx[:, g, k, :] for k in range(2)]) for g in range(2)]
            r_cb = R("cb", [128, 128], F32, 1, True)
            r_seg = R("seg", [128, 512], F32, 2, True)
            r_sm = R("sm", [128, 3, 16], F32, 1, True)
            r_yd = R("yd", [128, 512], F32, 1, True)
            r_stp = R("stp", [128, 512], F32, 1, True)
            r_yo = R("yo", [128, 512], F32, 1, True)
            r_tp = R("tp2", [128, 4, 128], BF16, 1, True)
            r_cbm = R("cbm", [128, 128], F32, 2)
            r_am = R("am", [128, 4, 128], F32, 4)
            r_dec = R("dec", [128, 4, 128], F32, 2)
            r_mt = R("mt", [128, 4, 128], BF16, 4)
            r_ea = R("ea", [128, 3, 16], F32, 2)
            r_xdt = R("xdt", [128, 1024], BF16, 1)
            r_xw = R("xw", [128, 1024], BF16, 1)
            r_t = R("t", [128, 512], F32, 2)
            r_ybl = R("ybl", [128, 1024], BF16, 2)
            r_yf = R("yf", [128, 1024], F32, 1)
            r_z = R("z", [128, 4, 1024], BF16, 1)
            r_jk = R("jk", [128, 512], F32, 1)
            r_ss = R("ss", [128, 4], F32, 2)
            r_y = R("y", [128, 1024], BF16, 2)
            yst = P.sb(s2, "B_yst", [128, 8, 256], BF16)
            yst_t = sc.tile("B_yst")
            rings = [r_cb, r_seg, r_sm, r_yd, r_stp, r_yo, r_tp, r_cbm, r_am, r_dec, r_mt, r_ea, r_xdt, r_xw, r_t, r_ybl,
                     r_yf, r_z, r_jk, r_ss, r_y] + r_Sb
            tl2 = Sf_t + [yst_t]
            for r in rings:
                tl2 += r.t

            def ssd_pass(d):
                fwd = (d == 0)
                tri_in, tri_in_t = (G["trif"], G["trif_t"]) if fwd else (G["trib"], G["trib_t"])
                tri_st, tri_st_t = (G["tribs"], G["tribs_t"]) if fwd else (G["trifs"], G["trifs_t"])
                cur = []
                for g in range(2):
                    sc.op("pool", lambda e, g=g: e.memset(Sf[:, g, :], 0.0), writes=[Sf_t[g]])
                    sb0, sb0_t = r_Sb[g].next()
                    sc.op("pool", lambda e, sb0=sb0: e.memset(sb0, 0.0), writes=[sb0_t])
                    cur.append((sb0, sb0_t))
                order = list(range(NT)) if fwd else list(range(NT - 1, -1, -1))
                zcur = [None]

                def tileA(i):
                    tsl = slice(i * 128, (i + 1) * 128)
                    acol = av[:, i, 16 * d:16 * d + 16]
                    ams = []
                    for u in range(4):
                        h0 = u * 4
                        am, am_t = r_am.next()
                        for hh in range(4):
                            sc.op("act", lambda e, am=am, i=i, h0=h0, hh=hh: e.activation(
                                out=am[:, hh, :], in_=tri_in[:], func=AF.Copy,
                                scale=av[:, i, 16 * d + h0 + hh:16 * d + h0 + hh + 1]),
                                reads=[tri_in_t, av_t], writes=[am_t], part=(hh > 0))
                        ams.append((am, am_t))
                    sm, sm_t = r_sm.next()
                    sc.op("pe", lambda e, sm=sm, acol=acol: e.matmul(sm[:, 0, :], lhsT=tri_in[:], rhs=acol, start=True, stop=True),
                          reads=[tri_in_t, av_t], writes=[sm_t])
                    sc.op("pe", lambda e, sm=sm, acol=acol: e.matmul(sm[:, 1, :], lhsT=tri_st[:], rhs=acol, start=True, stop=True),
                          reads=[tri_st_t, av_t], writes=[sm_t], part=True)
                    sc.op("pe", lambda e, sm=sm, acol=acol: e.matmul(sm[:, 2, :], lhsT=G["ones_f"][:], rhs=acol, start=True,
                                                                     stop=True),
                          reads=[G["ones_t"], av_t], writes=[sm_t], part=True)
                    ea, ea_t = r_ea.next()
                    sc.op("act", lambda e, ea=ea, sm=sm: e.activation(out=ea[:], in_=sm[:], func=AF.Exp), reads=[sm_t],
                          writes=[ea_t])
                    xdt, xdt_t = r_xdt.next()
                    sc.op("dve", lambda e, xdt=xdt, i=i: e.tensor_tensor(
                        out=xdt[:].rearrange("p (h q) -> p h q", q=64), in0=xtok[:, i, 0:1024].rearrange("p (h q) -> p h q", q=64),
                        in1=dtv[:, i, 16 * d:16 * d + 16].unsqueeze(2).to_broadcast([128, 16, 64]), op=ALU.mult),
                        reads=[xtok_t[i], dtv_t], writes=[xdt_t])
                    xw, xw_t = r_xw.next()
                    sc.op("dve", lambda e, xw=xw, xdt=xdt, ea=ea: e.tensor_tensor(
                        out=xw[:].rearrange("p (h q) -> p h q", q=64), in0=xdt[:].rearrange("p (h q) -> p h q", q=64),
                        in1=ea[:, 1, :].unsqueeze(2).to_broadcast([128, 16, 64]), op=ALU.mult),
                        reads=[xdt_t, ea_t], writes=[xw_t])
                    ybl, ybl_t = r_ybl.next()
                    if fwd:
                        sc.dma("sp", ybl[:], ybw[tsl, :], owner=ybl_t, reads=[ybw_t], writes=[ybl_t])
                        yf, yf_t = r_yf.next()
                    ts = []
                    for g in range(2):
                        stp, stp_t = r_stp.next()
                        sc.op("pe", lambda e, stp=stp, i=i, g=g, xw=xw: e.matmul(
                            stp[:], lhsT=xtok[:, i, 1024 + g * 128:1024 + (g + 1) * 128], rhs=xw[:, g * 512:(g + 1) * 512],
                            start=True, stop=True), reads=[xtok_t[i], xw_t], writes=[stp_t])
                        yo, yo_t = r_yo.next()
                        sbv, sbv_t = cur[g]
                        sc.op("pe", lambda e, yo=yo, g=g, tsl=tsl, sbv=sbv: e.matmul(yo[:], lhsT=CT[:, g, tsl], rhs=sbv,
                                                                                    start=True, stop=True),
                              reads=[CT_t[g], sbv_t], writes=[yo_t])
                        sc.op("pool", lambda e, g=g, ea=ea: e.tensor_tensor(
                            out=Sf[:, g, :].rearrange("p (h q) -> p h q", q=64), in0=Sf[:, g, :].rearrange("p (h q) -> p h q", q=64),
                            in1=ea[:, 2, g * 8:(g + 1) * 8].unsqueeze(2).to_broadcast([128, 8, 64]), op=ALU.mult),
                            reads=[Sf_t[g], ea_t], writes=[Sf_t[g]])
                        sc.op("dve", lambda e, g=g, stp=stp: e.tensor_tensor(out=Sf[:, g, :], in0=Sf[:, g, :], in1=stp[:],
                                                                            op=ALU.add),
                              reads=[Sf_t[g], stp_t], writes=[Sf_t[g]])
                        nb, nb_t = r_Sb[g].next()
                        sc.op("act", lambda e, nb=nb, g=g: e.copy(out=nb, in_=Sf[:, g, :]), reads=[Sf_t[g]], writes=[nb_t])
                        cur[g] = (nb, nb_t)
                        t, t_t = r_t.next()
                        sc.op("dve", lambda e, t=t, yo=yo, ea=ea, g=g: e.tensor_tensor(
                            out=t[:].rearrange("p (h q) -> p h q", q=64), in0=yo[:].rearrange("p (h q) -> p h q", q=64),
                            in1=ea[:, 0, g * 8:(g + 1) * 8].unsqueeze(2).to_broadcast([128, 8, 64]), op=ALU.mult),
                            reads=[yo_t, ea_t], writes=[t_t])
                        ts.append((t, t_t))
                    cbms = []
                    for g in range(2):
                        cb, cb_t = r_cb.next()
                        sc.op("pe", lambda e, cb=cb, g=g, tsl=tsl: e.matmul(cb[:], lhsT=BT[:, g, tsl], rhs=CT[:, g, tsl],
                                                                           start=True, stop=True),
                              reads=[BT_t[g], CT_t[g]], writes=[cb_t])
                        cbm, cbm_t = r_cbm.next()
                        sc.op("dve", lambda e, cbm=cbm, cb=cb: e.tensor_tensor(out=cbm[:], in0=cb[:], in1=tri_in[:], op=ALU.mult),
                              reads=[cb_t, tri_in_t], writes=[cbm_t])
                        cbms.append((cbm, cbm_t))
                    mts = []
                    for pair in range(2):
                        segs = []
                        for u in (2 * pair, 2 * pair + 1):
                            am, am_t = ams[u]
                            seg, seg_t = r_seg.next()
                            sc.op("pe", lambda e, seg=seg, am=am: e.matmul(seg[:], lhsT=tri_st[:],
                                                                           rhs=am[:].rearrange("p a b -> p (a b)"),
                                                                           start=True, stop=True),
                                  reads=[tri_st_t, am_t], writes=[seg_t])
                            segs.append((seg, seg_t))
                        decs = []
                        for (seg, seg_t) in segs:
                            dec, dec_t = r_dec.next()
                            sc.op("act", lambda e, dec=dec, seg=seg: e.activation(out=dec[:].rearrange("p a b -> p (a b)"),
                                                                                 in_=seg[:], func=AF.Exp),
                                  reads=[seg_t], writes=[dec_t])
                            decs.append((dec, dec_t))
                        for k, (dec, dec_t) in enumerate(decs):
                            u = 2 * pair + k
                            cbm, cbm_t = cbms[u // 2]
                            mt, mt_t = r_mt.next()
                            sc.op("dve", lambda e, mt=mt, dec=dec, cbm=cbm: e.tensor_tensor(
                                out=mt[:], in0=dec[:], in1=cbm[:].unsqueeze(1).to_broadcast([128, 4, 128]), op=ALU.mult),
                                reads=[dec_t, cbm_t], writes=[mt_t])
                            mts.append((mt, mt_t))
                    for g in range(2):
                        yd, yd_t = r_yd.next()
                        if fwd:
                            sc.op("pe", lambda e, yd=yd, ybl=ybl, g=g: e.matmul(
                                yd[:], lhsT=ident[:], rhs=ybl[:, g * 512:(g + 1) * 512], start=True, stop=False,
                                skip_group_check=True), reads=[G["ident_t"], ybl_t], writes=[yd_t])
                        for q4 in range(2):
                            mt, mt_t = mts[g * 2 + q4]
                            for hh in range(4):
                                h = g * 8 + q4 * 4 + hh
                                hl = h - g * 8
                                sc.op("pe", lambda e, yd=yd, mt=mt, hh=hh, hl=hl, h=h, xdt=xdt: e.matmul(
                                    yd[:, hl * 64:(hl + 1) * 64], lhsT=mt[:, hh, :], rhs=xdt[:, h * 64:(h + 1) * 64],
                                    start=(not fwd), stop=True, skip_group_check=True),
                                    reads=[mt_t, xdt_t], writes=[yd_t], part=(fwd or not (q4 == 0 and hh == 0)))
                        t, t_t = ts[g]
                        gs = slice(g * 512, (g + 1) * 512)
                        if not fwd:
                            sc.op("dve", lambda e, t=t, yd=yd, ybl=ybl, gs=gs: e.tensor_tensor(out=ybl[:, gs], in0=t[:], in1=yd[:],
                                                                                              op=ALU.add),
                                  reads=[t_t, yd_t], writes=[ybl_t], part=(g > 0))
                        else:
                            sc.op("dve", lambda e, t=t, yd=yd, yf=yf, gs=gs: e.tensor_tensor(out=yf[:, gs], in0=t[:], in1=yd[:],
                                                                                            op=ALU.add),
                                  reads=[t_t, yd_t], writes=[yf_t], part=(g > 0))
                    if not fwd:
                        sc.dma("pool", ybw[tsl, :], ybl[:], owner=ybl_t, reads=[ybl_t], writes=[ybw_t], part=True)
                        return None
                    if i % 4 == 0:
                        z, z_t = r_z.next()
                        sc.dma("sp", z[:], U["z"][i * 128:(i + 4) * 128, :].rearrange("(j p) c -> p j c", p=128), owner=z_t,
                               reads=[G["dram_t"]["z"]], writes=[z_t])
                        sc.op("act", lambda e, z=z: e.activation(out=z[:], in_=z[:], func=AF.Silu), reads=[z_t], writes=[z_t])
                        zcur[0] = (z, z_t)
                    z, z_t = zcur[0]
                    sz = z[:, i % 4, :]
                    sz_t = z_t
                    xd, xd_t = r_xdt.next()
                    sc.op("pool", lambda e, xd=xd, i=i: e.tensor_tensor(
                        out=xd[:].rearrange("p (h q) -> p h q", q=64), in0=xtok[:, i, 0:1024].rearrange("p (h q) -> p h q", q=64),
                        in1=rows[:, 2, 0:16].unsqueeze(2).to_broadcast([128, 16, 64]), op=ALU.mult),
                        reads=[xtok_t[i], rows_t], writes=[xd_t])
                    sc.op("dve", lambda e, yf=yf, xd=xd: e.tensor_tensor(out=yf[:], in0=yf[:], in1=xd[:], op=ALU.add),
                          reads=[yf_t, xd_t], writes=[yf_t])
                    sc.op("dve", lambda e, yf=yf, sz=sz: e.tensor_tensor(out=yf[:], in0=yf[:], in1=sz, op=ALU.mult),
                          reads=[yf_t, sz_t], writes=[yf_t])
                    ss, ss_t = r_ss.next()
                    for g in range(2):
                        gs = slice(g * 512, (g + 1) * 512)
                        jk, jk_t = r_jk.next()
                        sc.op("dve", lambda e, jk=jk, yf=yf, gs=gs, ss=ss, g=g: e.scalar_tensor_tensor(
                            out=jk[:], in0=yf[:, gs], scalar=1.0, in1=yf[:, gs], op0=ALU.mult, op1=ALU.mult,
                            accum_out=ss[:, g:g + 1]), reads=[yf_t], writes=[jk_t, ss_t])
                    sc.op("dve", lambda e, ss=ss: e.tensor_scalar(out=ss[:, 2:4], in0=ss[:, 0:2], scalar1=1.0 / 512.0, scalar2=EPS,
                                                                  op0=ALU.mult, op1=ALU.add), reads=[ss_t], writes=[ss_t])
                    sc.op("pool", lambda e, ss=ss: e.tensor_tensor(out=ss[:, 0:2], in0=ss[:, 2:4], in1=G["neghalf"][:, 0:2],
                                                                   op=ALU.pow), reads=[ss_t, G["neghalf_t"]], writes=[ss_t])
                    y, y_t = r_y.next()
                    for g in range(2):
                        gs = slice(g * 512, (g + 1) * 512)
                        sc.op("dve", lambda e, y=y, yf=yf, gs=gs, ss=ss, g=g: e.scalar_tensor_tensor(
                            out=y[:, gs], in0=yf[:, gs], scalar=ss[:, g:g + 1], in1=nwb[:, gs], op0=ALU.mult, op1=ALU.mult),
                            reads=[yf_t, ss_t, nwb_t], writes=[y_t], part=(g > 0))
                    return (y, y_t, i)

                def tileC(c3):
                    if c3 is None:
                        return
                    (y, y_t, i) = c3
                    emit_yT(P, sc, G, r_tp, y, y_t, yst, yst_t, i, YB["ssd"], G["dram_t"]["yb_ssd"], gsz=2)

                prev3 = None
                for i in order:
                    n3 = tileA(i)
                    tileC(prev3)
                    prev3 = n3
                tileC(prev3)

            ssd_pass(1)
            ssd_pass(0)
            sc.barrier(release=tl2)
        sc.barrier(release=tiles)


def emit_yT(P, sc, G, r_tp, y, y_t, yst, yst_t, i, dst, dst_t, gsz=4):
    ident = G["ident"]
    for half in range(2):
        tp, tp_t = r_tp.next()
        for jq in range(4):
            c = half * 4 + jq
            sc.op("pe", lambda e, tp=tp, jq=jq, c=c: e.transpose(out=tp[:, jq, :], in_=y[:, c * 128:(c + 1) * 128],
                                                               identity=ident[:]),
                  reads=[y_t, G["ident_t"]], writes=[tp_t], part=(jq > 0))
        sc.op("act", lambda e, tp=tp, half=half: e.copy(
            out=yst[:, half * 4:half * 4 + 4, (i % gsz) * 128:(i % gsz + 1) * 128], in_=tp[:]),
            reads=[tp_t], writes=[yst_t], part=not (i % gsz == 0 and half == 0))
    if i % gsz == gsz - 1:
        yv = dst.rearrange("(c p) t -> p c t", p=128)
        sc.dma("pool", yv[:, :, (i - gsz + 1) * 128:(i + 1) * 128], yst[:], owner=yst_t, reads=[yst_t], writes=[dst_t],
               part=True)


NEG = -30000.0


def na_r0(r):
    return min(max(r - 4, 0), 24)


def na_valid(kr, qr):
    return na_r0(qr) <= kr < na_r0(qr) + 8


def phase_D(P, sc, G, U, YB, prm, natt, l):
    nc = P.nc
    with contextlib.ExitStack() as ph:
        qnT = P.sb(ph, "D_qnT", [128, 8, S], BF16)
        knT = P.sb(ph, "D_knT", [128, 8, S], BF16)
        qn_t = sc.tiles_n("D_qn", 8)
        kn_t = sc.tiles_n("D_kn", 8)
        TT = P.sb(ph, "D_TT", [128, 8, 17, 64], BF16)
        TT_t = sc.tiles_n("D_TT", 4)
        wcol = P.sb(ph, "D_wcol", [128, 4], F32)
        wcol_t = sc.tile("D_wcol")
        tiles = qn_t + kn_t + TT_t + [wcol_t]
        with contextlib.ExitStack() as s1:
            TTf = [P.sb(s1, "D_TTf%d" % i, [128, 2, 17, 64], F32) for i in range(2)]
            TTf_t = sc.tiles_n("D_TTf", 2)
            qc_ = [P.sb(s1, "D_qc%d" % i, [128, S], BF16) for i in range(2)]
            qc_t = sc.tiles_n("D_qc", 2)
            sq = [P.sb(s1, "D_sq%d" % i, [128, 512], BF16) for i in range(3)]
            sq_t = sc.tiles_n("D_sq", 3)
            lnv = [P.sb(s1, "D_ln%d" % i, [128, 512], F32) for i in range(2)]
            lnv_t = sc.tiles_n("D_ln", 2)
            bones = P.sb(s1, "D_bones", [128, 128], BF16)
            bones_t = sc.tile("D_bones")
            ssp = [P.ps(s1, "D_ssp%d" % i, [128, 512], F32) for i in range(2)]
            ssp_t = sc.tiles_n("D_ssp", 2)
            t1 = TTf_t + qc_t + sq_t + lnv_t + [bones_t] + ssp_t
            for g in range(4):
                b = g % 2
                sc.dma("sp", TTf[b][:], natt[l][:, 2 * g:2 * g + 2, :, :], owner=TTf_t[b], writes=[TTf_t[b]])
                sc.op("pool", lambda e, b=b, g=g: e.tensor_copy(out=TT[:, 2 * g:2 * g + 2, :, :], in_=TTf[b][:]),
                      reads=[TTf_t[b]], writes=[TT_t[g]])
            for hh in range(2):
                sc.dma("sp", wcol[hh * 64:(hh + 1) * 64, 2:3], prm["na_q_norm_w"][l].rearrange("(d o) -> d o", o=1),
                       owner=wcol_t, writes=[wcol_t], part=True)
                sc.dma("sp", wcol[hh * 64:(hh + 1) * 64, 1:2], prm["na_k_norm_w"][l].rearrange("(d o) -> d o", o=1),
                       owner=wcol_t, writes=[wcol_t], part=True)
            sc.op("dve", lambda e: e.tensor_scalar(out=wcol[:, 0:1], in0=wcol[:, 2:3], scalar1=0.125, scalar2=None,
                                                   op0=ALU.mult), reads=[wcol_t], writes=[wcol_t])
            sc.op("pool", lambda e: e.memset(bones[:], 0.0), writes=[bones_t])
            sc.op("pool", lambda e: e.memset(bones[0:64, 0:64], 1.0), reads=[bones_t], writes=[bones_t])
            sc.op("pool", lambda e: e.memset(bones[64:128, 64:128], 1.0), reads=[bones_t], writes=[bones_t])
            cnt = 0
            for which, (src, dstT, dst_t, wc) in enumerate(((U["nq"], qnT, qn_t, 0), (U["nk"], knT, kn_t, 1))):
                src_t = G["dram_t"]["nq" if which == 0 else "nk"]
                for c in range(8):
                    cb = cnt % 2
                    cnt += 1
                    sc.dma("sp", qc_[cb][:], src[c * 128:(c + 1) * 128, :], owner=qc_t[cb], reads=[src_t],
                           writes=[qc_t[cb]])
                    for tb in range(4):
                        b = tb % 2
                        sb3 = (cnt * 4 + tb) % 3
                        sl = slice(tb * 512, (tb + 1) * 512)
                        sc.op("dve", lambda e, cb=cb, sb3=sb3, sl=sl: e.tensor_tensor(out=sq[sb3][:], in0=qc_[cb][:, sl],
                                                                                      in1=qc_[cb][:, sl], op=ALU.mult),
                              reads=[qc_t[cb]], writes=[sq_t[sb3]])
                        sc.op("pe", lambda e, b=b, sb3=sb3: e.matmul(ssp[b][:], lhsT=bones[:], rhs=sq[sb3][:], start=True,
                                                                     stop=True),
                              reads=[bones_t, sq_t[sb3]], writes=[ssp_t[b]])
                        sc.op("act", lambda e, b=b: e.activation(out=lnv[b][:], in_=ssp[b][:], func=AF.Ln,
                                                                 bias=G["eps"][:, 0:1], scale=1.0 / 64.0),
                              reads=[ssp_t[b], G["eps_t"]], writes=[lnv_t[b]])
                        sc.op("act", lambda e, b=b: e.activation(out=lnv[b][:], in_=lnv[b][:], func=AF.Exp, scale=-0.5),
                              reads=[lnv_t[b]], writes=[lnv_t[b]])
                        sc.op("dve", lambda e, cb=cb, b=b, sl=sl, dstT=dstT, c=c, wc=wc: e.scalar_tensor_tensor(
                            out=dstT[:, c, sl], in0=qc_[cb][:, sl], scalar=wcol[:, wc:wc + 1], in1=lnv[b][:],
                            op0=ALU.mult, op1=ALU.mult),
                            reads=[qc_t[cb], lnv_t[b], wcol_t], writes=[dst_t[c]], part=(tb > 0))
            sc.barrier(release=t1)
        with contextlib.ExitStack() as s2:
            vx = P.sb(s2, "D_vx", [128, NT, 16, 65], BF16)
            vx_t = sc.tiles_n("D_vx", NT)
            sps = [P.ps(s2, "D_sps%d" % i, [128, 8, 128], F32) for i in range(2)]
            sps_t = sc.tiles_n("D_sps", 2)
            pT = [P.sb(s2, "D_pT%d" % i, [128, 5, 128], BF16) for i in range(3)]
            pT_t = sc.tiles_n("D_pT", 3)
            po = [P.ps(s2, "D_po%d" % i, [128, 2, 66], F32) for i in range(2)]
            po_t = sc.tiles_n("D_po", 2)
            rc = [P.sb(s2, "D_rc%d" % i, [128, 2], F32) for i in range(2)]
            rc_t = sc.tiles_n("D_rc", 2)
            ot = [P.sb(s2, "D_ot%d" % i, [128, 1024], BF16) for i in range(2)]
            ot_t = sc.tiles_n("D_ot", 2)
            tp = [P.ps(s2, "D_tp%d" % i, [128, 4, 128], BF16) for i in range(2)]
            tp_t = sc.tiles_n("D_tp", 2)
            yst = P.sb(s2, "D_yst", [128, 8, 512], BF16)
            yst_t = sc.tile("D_yst")
            t2 = vx_t + sps_t + pT_t + po_t + rc_t + ot_t + tp_t + [yst_t]
            nvv = U["nv"].rearrange("(i p) (h d) -> p i h d", p=128, d=64)
            for i in range(NT):
                sc.op("pool", lambda e, i=i: e.memset(vx[:, i, :, 64:65], 1.0), writes=[vx_t[i]])
                sc.dma("sp", vx[:, i, :, 0:64], nvv[:, i, :, :], owner=vx_t[i], reads=[G["dram_t"]["nv"]],
                       writes=[vx_t[i]], part=True)
            ident = G["ident"]
            tpc = [0]
            units = []
            for i in range(NT):
                jlo = na_r0(2 * i) // 2
                jhi = (na_r0(2 * i + 1) + 7) // 2
                js = list(range(jlo, jhi + 1))
                for hp in range(8):
                    for hh in range(2):
                        units.append((i, hp, hh, js))

            def emit_S(u):
                i, hp, hh, js = units[u]
                h = 2 * hp + hh
                p0 = 64 * hh
                sb_ = u % 2
                for jj, j in enumerate(js):
                    sc.op("pe", lambda e, sb_=sb_, jj=jj, j=j, p0=p0, hp=hp, i=i: e.matmul(
                        sps[sb_][:, jj, :], lhsT=knT[p0:p0 + 64, hp, j * 128:(j + 1) * 128],
                        rhs=qnT[p0:p0 + 64, hp, i * 128:(i + 1) * 128], start=True, stop=False,
                        skip_group_check=True),
                        reads=[kn_t[hp], qn_t[hp]], writes=[sps_t[sb_]], part=(jj > 0))
                    mms = []
                    for b0 in range(2):
                        qr = 2 * i + b0
                        va = [na_valid(2 * j + a, qr) for a in range(2)]
                        dr0 = 2 * j - qr + 7
                        cs = slice(b0 * 64, (b0 + 1) * 64)
                        if va[0] and va[1]:
                            mms.append((slice(0, 128), cs, TT[p0:p0 + 64, hp, dr0:dr0 + 2, :]))
                        elif not va[0] and not va[1]:
                            mms.append((slice(0, 128), cs, TT[p0:p0 + 64, hp, 15:17, :]))
                        else:
                            d0 = dr0 if va[0] else 15
                            d1 = dr0 + 1 if va[1] else 16
                            mms.append((slice(0, 64), cs, TT[p0:p0 + 64, hp, d0, :]))
                            mms.append((slice(64, 128), cs, TT[p0:p0 + 64, hp, d1, :]))
                    for mi, (ps_, cs, lhs) in enumerate(mms):
                        sc.op("pe", lambda e, sb_=sb_, jj=jj, ps_=ps_, cs=cs, lhs=lhs, p0=p0, last=(mi == len(mms) - 1):
                              e.matmul(sps[sb_][ps_, jj, cs], lhsT=lhs, rhs=ident[p0:p0 + 64, p0:p0 + 64],
                                       start=False, stop=last, skip_group_check=True),
                              reads=[TT_t[hp // 2], G["ident_t"]], writes=[sps_t[sb_]], part=True)

            def emit_rest(u):
                i, hp, hh, js = units[u]
                h = 2 * hp + hh
                sb_ = u % 2
                pt = u % 3
                pb_ = (u // 2) % 2
                ob = i % 2
                n = len(js)
                n1 = min(n, 4)
                sc.op("act", lambda e, pt=pt, sb_=sb_, n1=n1: e.activation(out=pT[pt][:, 0:n1, :],
                                                                         in_=sps[sb_][:, 0:n1, :], func=AF.Exp),
                      reads=[sps_t[sb_]], writes=[pT_t[pt]])
                if n > 4:
                    sc.op("act", lambda e, pt=pt, sb_=sb_, n=n: e.activation(out=pT[pt][:, 4:n, :],
                                                                           in_=sps[sb_][:, 4:n, :], func=AF.Exp),
                          reads=[sps_t[sb_]], writes=[pT_t[pt]], part=True)
                for jj, j in enumerate(js):
                    sc.op("pe", lambda e, pb_=pb_, hh=hh, pt=pt, jj=jj, j=j, h=h, n=n: e.matmul(
                        po[pb_][:, hh, 0:65], lhsT=pT[pt][:, jj, :], rhs=vx[:, j, h, :],
                        start=(jj == 0), stop=(jj == n - 1)),
                        reads=[pT_t[pt], vx_t[j]], writes=[po_t[pb_]], part=(hh > 0 or jj > 0))
                if hh == 1:
                    sc.op("dve", lambda e, pb_=pb_: e.reciprocal(out=rc[pb_][:, 0:2], in_=po[pb_][:, :, 64]),
                          reads=[po_t[pb_]], writes=[rc_t[pb_]])
                    for h2 in range(2):
                        hx = 2 * hp + h2
                        sc.op("dve", lambda e, pb_=pb_, h2=h2, hx=hx, ob=ob: e.tensor_scalar(
                            out=ot[ob][:, hx * 64:(hx + 1) * 64], in0=po[pb_][:, h2, 0:64], scalar1=rc[pb_][:, h2:h2 + 1],
                            scalar2=None, op0=ALU.mult),
                            reads=[po_t[pb_], rc_t[pb_]], writes=[ot_t[ob]], part=(hx > 0))
                if hp == 7 and hh == 1:
                    for half in range(2):
                        tb_ = tpc[0] % 2
                        tpc[0] += 1
                        for jq in range(4):
                            c = half * 4 + jq
                            sc.op("pe", lambda e, tb_=tb_, jq=jq, c=c, ob=ob: e.transpose(
                                out=tp[tb_][:, jq, :], in_=ot[ob][:, c * 128:(c + 1) * 128], identity=ident[:]),
                                reads=[ot_t[ob], G["ident_t"]], writes=[tp_t[tb_]], part=(jq > 0))
                        sc.op("act", lambda e, tb_=tb_, half=half, i=i: e.copy(
                            out=yst[:, half * 4:half * 4 + 4, (i % 4) * 128:(i % 4 + 1) * 128], in_=tp[tb_][:]),
                            reads=[tp_t[tb_]], writes=[yst_t], part=not (i % 4 == 0 and half == 0))
                    if i % 4 == 3:
                        yv = YB["na"].rearrange("(c p) t -> p c t", p=128)
                        sc.dma("pool", yv[:, :, (i - 3) * 128:(i + 1) * 128], yst[:], owner=yst_t, reads=[yst_t],
                               writes=[G["dram_t"]["yb_na"]], part=True)

            emit_S(0)
            for u in range(len(units)):
                if u + 1 < len(units):
                    emit_S(u + 1)
                emit_rest(u)
            sc.barrier(release=t2)
        sc.barrier(release=tiles)


def phase_F(P, sc, G, prm, l):
    nc = P.nc
    x = G["x"]
    with contextlib.ExitStack() as ph:
        hT = P.sb(ph, "F_hT", [128, 8, S], BF16)
        hT_t = sc.tiles_n("F_hT", NT)
        tiles = list(hT_t)
        tiles += rms_transpose(P, sc, G, ph, prm["norm_mlp_w"][l], hT, hT_t, l, "F")
        wst = WStream(P, sc, ph, "F", 1, 4096, nf=2, nb=3)
        fT = [P.sb(ph, "F_fT%d" % i, [128, 4, S], BF16) for i in range(2)]
        fT_t = [sc.tiles_n("F_fT%d_" % i, 4) for i in range(2)]
        rl = [P.sb(ph, "F_rl%d" % i, [128, 512], F32) for i in range(2)]
        rl_t = sc.tiles_n("F_rl", 2)
        acc = [P.ps(ph, "F_acc%d" % i, [128, 512], F32) for i in range(4)]
        acc_t = sc.tiles_n("F_acc", 4)
        tiles += wst.tiles + fT_t[0] + fT_t[1] + rl_t + acc_t
        w1v = prm["w_ff1"][l].rearrange("(kc p) n -> p kc n", p=128)
        w2v = prm["w_ff2"][l].rearrange("(c p) n -> p c n", p=128)
        items = []
        for g in range(8):
            items.append((w1v[:, :, g * 512:(g + 1) * 512], 8, 512))
            items.append((w2v[:, g * 4:(g + 1) * 4, :], 4, 1024))
        wst.items = items
        wst_views = {}

        def view(slot, k, n):
            return slot[:, 0, :].rearrange("p (k n) -> p k n", k=k)
        def _load(g):
            if g >= len(items):
                return
            ap, k, n = items[g]
            fs = g % wst.nf
            sc.dma("sp", view(wst.f[fs], k, n), ap, owner=wst.f_t[fs], writes=[wst.f_t[fs]])

        def _cast(g):
            if g >= len(items):
                return
            fs, bs = g % wst.nf, g % wst.nb
            sc.op("pool", lambda e: e.tensor_copy(out=wst.b[bs][:, 0, :], in_=wst.f[fs][:, 0, :]),
                  reads=[wst.f_t[fs]], writes=[wst.b_t[bs]])
        wst._load = _load
        wst._cast = _cast
        _load(0)
        _load(1)
        _cast(0)
        ai = 0
        ri = 0
        for g in range(8):
            fb = g % 2
            w1s, w1_t = wst.get(2 * g)
            w1b = view(w1s, 8, 512)
            for c in range(4):
                for tb in range(4):
                    a = ai % 4
                    ai += 1
                    for kc in range(8):
                        sc.op("pe", lambda e, a=a, kc=kc, w1b=w1b, c=c, tb=tb: e.matmul(
                            acc[a][:], lhsT=w1b[:, kc, c * 128:(c + 1) * 128],
                            rhs=hT[:, kc, tb * 512:(tb + 1) * 512], start=(kc == 0), stop=(kc == 7)),
                            reads=[w1_t] + hT_t[tb * 4:tb * 4 + 4], writes=[acc_t[a]], part=(kc > 0))
                    r = ri % 2
                    ri += 1
                    sc.op("act", lambda e, r=r, a=a: e.activation(out=rl[r][:], in_=acc[a][:], func=AF.Relu),
                          reads=[acc_t[a]], writes=[rl_t[r]])
                    sc.op("pool", lambda e, r=r, fb=fb, c=c, tb=tb: e.tensor_tensor(
                        out=fT[fb][:, c, tb * 512:(tb + 1) * 512], in0=rl[r][:], in1=rl[r][:], op=ALU.mult),
                        reads=[rl_t[r]], writes=[fT_t[fb][c]], part=(tb > 0))
            w2s, w2_t = wst.get(2 * g + 1)
            w2b = view(w2s, 4, 1024)
            for i in range(NT):
                for hh in range(2):
                    a = ai % 4
                    ai += 1
                    for c in range(4):
                        sc.op("pe", lambda e, a=a, c=c, w2b=w2b, i=i, hh=hh, fb=fb: e.matmul(
                            acc[a][:], lhsT=fT[fb][:, c, i * 128:(i + 1) * 128],
                            rhs=w2b[:, c, hh * 512:(hh + 1) * 512], start=(c == 0), stop=(c == 3)),
                            reads=[w2_t, fT_t[fb][c]], writes=[acc_t[a]], part=(c > 0))
                    xs = x[:, i, hh * 512:(hh + 1) * 512]
                    sc.op("dve", lambda e, xs=xs, a=a: e.tensor_tensor(out=xs, in0=xs, in1=acc[a][:], op=ALU.add),
                          reads=[acc_t[a], G["xt"][i]], writes=[G["xt"][i]])
        sc.barrier(release=tiles)


_NC_CACHE = {}


def make_na_tt(rpb):
    rpb = np.asarray(rpb, dtype=np.float32)
    L = rpb.shape[0]
    out = np.full((L, 128, 8, 17, 64), NEG, dtype=np.float32)
    qc = np.arange(64)
    ws = np.clip(qc - 8, 0, 48)
    for q in range(64):
        kc = np.arange(ws[q], ws[q] + 16)
        idx = kc - q + 15
        for hh in range(2):
            out[:, hh * 64 + q, :, 0:15, ws[q]:ws[q] + 16] = rpb[:, hh::2][:, :, :, idx]
    return out


def kernel(**inputs):
    cfg = {}
    key = "full"
    if key not in _NC_CACHE:
        _NC_CACHE[key] = build(cfg)
    nc = _NC_CACHE[key]
    x = np.ascontiguousarray(inputs["x"], dtype=np.float32)
    base = {n: np.ascontiguousarray(inputs[n], dtype=np.float32) for n in PARAM_NAMES}
    base["na_tt"] = make_na_tt(inputs["na_rpb"])
    in_maps = []
    for c in range(8):
        m = dict(base)
        m["x"] = x[c]
        in_maps.append(m)
    res = run_bass_kernel_spmd(nc, in_maps, core_ids=list(range(8)))
    return np.stack([r["y"] for r in res.results], axis=0).astype(np.float32)
```

```python
import contextlib
import numpy as np
import concourse.bass as bass
import concourse.mybir as mybir
from concourse.bass_utils import run_bass_kernel_spmd

F32 = mybir.dt.float32
BF16 = mybir.dt.bfloat16
ALU = mybir.AluOpType
AF = mybir.ActivationFunctionType
AX = mybir.AxisListType

D = 1024
S = 2048
NT = S // 128
DEPTH = 2
N_IN = 11840
EPS = 1e-6


class TT:
    __slots__ = ("name", "lw", "rd", "dsems", "gen")

    def __init__(self, name):
        self.name = name
        self.lw = {}
        self.rd = {}
        self.gen = {}
        self.dsems = {}


class Sched:
    ENG = ("pe", "act", "dve", "pool", "sp")
    BLK = {"pe": "tensor", "act": "scalar", "dve": "vector", "pool": "gpsimd", "sp": "sync"}

    def __init__(self, nc, stack):
        self.nc = nc
        self.stack = stack
        self.ops = {e: [] for e in self.ENG}
        self.seen = {e: {} for e in self.ENG}
        self.esem = {e: stack.enter_context(nc.semaphore("es_" + e)) for e in self.ENG if e != "sp"}
        self.tiles = []
        self.free_dsems = {"sp": [], "pool": [], "act": []}
        self.nsem = 4
        self.skip_same = {"pe"}

    def tile(self, name):
        t = TT(name)
        self.tiles.append(t)
        return t

    def tiles_n(self, name, n):
        return [self.tile("%s%d" % (name, i)) for i in range(n)]

    def _collect(self, reads, writes, part):
        evs = {}

        def add(d):
            for k, v in d.items():
                if k not in evs or evs[k][0] < v[0]:
                    evs[k] = v
        for t in reads:
            add(t.lw)
        for t in writes:
            if part and not t.rd:
                add(t.gen)
                continue
            g = dict(t.rd)
            for k, v in t.lw.items():
                if k not in g or g[k][0] < v[0]:
                    g[k] = v
            t.gen = g
            add(g)
        return evs

    def _waits(self, eng, evs):
        waits = []
        for k, (val, obj) in evs.items():
            if k == ("E", eng) and eng in self.skip_same:
                continue
            if self.seen[eng].get(k, 0) >= val:
                continue
            self.seen[eng][k] = val
            waits.append((k, val, obj))
            if k[0] == "E":
                self.ops[k[1]][val - 1]["inc"] = True
        return waits

    def _update(self, ev_key, ev_val, reads, writes, part):
        for t in reads:
            t.rd[ev_key] = ev_val
        for t in writes:
            if part and not t.rd:
                t.lw[ev_key] = ev_val
            else:
                t.lw = {ev_key: ev_val}
                t.rd = {}

    def op(self, eng, fn, reads=(), writes=(), part=False):
        waits = self._waits(eng, self._collect(reads, writes, part))
        self.ops[eng].append({"fn": fn, "waits": waits, "inc": False, "dma": None})
        idx = len(self.ops[eng])
        self._update(("E", eng), (idx, None), reads, writes, part)

    def dma(self, q, out, in_, owner, reads=(), writes=(), part=False, **kw):
        waits = self._waits(q, self._collect(reads, writes, part))
        rec = owner.dsems.get(q)
        if rec is None:
            if self.free_dsems[q]:
                rec = self.free_dsems[q].pop()
            else:
                rec = [self.stack.enter_context(self.nc.semaphore("ds%d" % self.nsem)), 0, self.nsem]
                self.nsem += 1
            owner.dsems[q] = rec
        rec[1] += 16
        self.ops[q].append({"fn": (lambda e: e.dma_start(out=out, in_=in_, **kw)), "waits": waits,
                            "inc": False, "dma": rec[0]})
        self._update(("D", rec[2]), (rec[1], rec[0]), reads, writes, part)

    def barrier(self, release=()):
        evs = {}
        for e in self.ENG:
            if e == "sp":
                continue
            idx = len(self.ops[e])
            while idx > 0 and (self.ops[e][idx - 1]["dma"] is not None or self.ops[e][idx - 1].get("nop")):
                idx -= 1
            if idx > 0:
                evs[("E", e)] = (idx, None)
        for t in self.tiles:
            for d in (t.lw, t.rd):
                for k, v in d.items():
                    if k[0] == "D" and (k not in evs or evs[k][0] < v[0]):
                        evs[k] = v
        for e in self.ENG:
            sk = self.skip_same
            self.skip_same = set()
            w = self._waits(e, dict(evs))
            self.skip_same = sk
            self.ops[e].append({"fn": (lambda en: en.nop()), "waits": w, "inc": False, "dma": None, "nop": True})
        for t in self.tiles:
            t.lw = {}
            t.rd = {}
            t.gen = {}
        rel = set(id(t) for t in release)
        for t in release:
            for q, rec in t.dsems.items():
                self.free_dsems[q].append(rec)
            t.dsems = {}
        self.tiles = [t for t in self.tiles if id(t) not in rel]

    def emit(self):
        nc = self.nc
        mile = {}
        for e in self.ENG:
            c = 0
            m = []
            for o in self.ops[e]:
                if o["inc"]:
                    c += 1
                m.append(c)
            mile[e] = m
            assert c < 60000, (e, c)
        with nc.Block() as block:
            for e in self.ENG:
                def body(engine, e=e):
                    for o in self.ops[e]:
                        for (k, val, obj) in o["waits"]:
                            if k[0] == "E":
                                engine.wait_ge(self.esem[k[1]], mile[k[1]][val - 1])
                            else:
                                engine.wait_ge(obj, val)
                        ins = o["fn"](engine)
                        if o["dma"] is not None:
                            ins.then_inc(o["dma"], 16)
                        elif o["inc"]:
                            ins.then_inc(self.esem[e], 1)
                getattr(block, self.BLK[e])(body)


class Prog:
    def __init__(self, cfg):
        self.cfg = cfg
        self.nc = bass.Bass("TRN2", target_bir_lowering=False)
        self.dbg = cfg.get("debug", ())

    def dram(self, name, shape, dt, kind="Internal"):
        if name in self.dbg:
            kind = "ExternalOutput"
        if name in self.cfg.get("ext_in", ()):
            kind = "ExternalInput"
        return self.nc.dram_tensor(name, list(shape), dt, kind=kind).ap()

    def sb(self, stack, name, shape, dt):
        self.uid = getattr(self, "uid", 0) + 1
        return stack.enter_context(self.nc.sbuf_tensor("%s_u%d" % (name, self.uid), list(shape), dt))

    def ps(self, stack, name, shape, dt):
        self.uid = getattr(self, "uid", 0) + 1
        return stack.enter_context(self.nc.psum_tensor("%s_u%d" % (name, self.uid), list(shape), dt))


IN_SIZES = (1024, 1536, 16, 16, 512, 512, 1024, 1024, 16, 16, 1024, 1024, 1024, 3072)
IN_OFF = [0]
for _s in IN_SIZES:
    IN_OFF.append(IN_OFF[-1] + _s)
(O_Z, O_XBC, O_DTF, O_DTB, O_GQ, O_GK, O_GV, O_GG, O_GAF, O_GAB, O_NQ, O_NK, O_NV, O_GATE, _) = IN_OFF

PARAM_NAMES = ["norm_mix_w", "w_in", "ssd_conv_w", "ssd_conv_b", "ssd_dt_bias_f", "ssd_dt_bias_b",
               "ssd_a_log_f", "ssd_a_log_b", "ssd_d", "ssd_norm_w", "gla_a2_f", "gla_a2_bias_f",
               "gla_a2_b", "gla_a2_bias_b", "gla_norm_w", "na_q_norm_w", "na_k_norm_w", "na_rpb",
               "w_branch_ssd", "w_branch_gla", "w_branch_na", "w_out", "norm_mlp_w", "w_ff1", "w_ff2"]
PARAM_SHAPES = {
    "norm_mix_w": (2, 1024), "w_in": (2, 1024, 11840), "ssd_conv_w": (2, 5, 1536), "ssd_conv_b": (2, 1536),
    "ssd_dt_bias_f": (2, 16), "ssd_dt_bias_b": (2, 16), "ssd_a_log_f": (2, 16), "ssd_a_log_b": (2, 16),
    "ssd_d": (2, 16), "ssd_norm_w": (2, 1024), "gla_a2_f": (2, 16, 512), "gla_a2_bias_f": (2, 512),
    "gla_a2_b": (2, 16, 512), "gla_a2_bias_b": (2, 512), "gla_norm_w": (2, 256), "na_q_norm_w": (2, 64),
    "na_k_norm_w": (2, 64), "na_rpb": (2, 16, 15, 31), "w_branch_ssd": (2, 1024, 1024),
    "w_branch_gla": (2, 1024, 1024), "w_branch_na": (2, 1024, 1024), "w_out": (2, 1024, 1024),
    "norm_mlp_w": (2, 1024), "w_ff1": (2, 1024, 4096), "w_ff2": (2, 4096, 1024),
}


def build(cfg):
    P = Prog(cfg)
    nc = P.nc
    layers = cfg.get("layers", DEPTH)
    phases = cfg.get("phases", "ABCDEF")
    x_in = nc.dram_tensor("x", [S, D], F32, kind="ExternalInput").ap()
    prm = {n: nc.dram_tensor(n, list(PARAM_SHAPES[n]), F32, kind="ExternalInput").ap() for n in PARAM_NAMES}
    y_out = nc.dram_tensor("y", [S, D], F32, kind="ExternalOutput").ap()
    natt = nc.dram_tensor("na_tt", [DEPTH, 128, 8, 17, 64], F32, kind="ExternalInput").ap()

    U = {}
    for nm, w in (("z", 1024), ("gv", 1024), ("gg", 1024), ("nv", 1024)):
        U[nm] = P.dram("u_" + nm, [S, w], BF16)
    U["dt"] = P.dram("u_dt", [S, 32], F32)
    for nm, w in (("xbc", 1536), ("gq", 512), ("gk", 512), ("nq", 1024), ("nk", 1024), ("gate", 3072)):
        U[nm] = P.dram("u_" + nm + "T", [w, S], BF16)
    U["ga"] = P.dram("u_gaT", [32, S], F32)
    YB = {nm: P.dram("yb_" + nm, [1024, S], BF16) for nm in ("ssd", "gla", "na")}
    ybw = P.dram("ybw", [S, 1024], BF16)

    with contextlib.ExitStack() as top:
        sc = Sched(nc, top)
        G = {}
        G["x"] = P.sb(top, "x_res", [128, NT, D], F32)
        G["xt"] = sc.tiles_n("x", NT)
        G["ident"] = P.sb(top, "ident", [128, 128], BF16)
        G["ident_t"] = sc.tile("ident")
        G["dram_t"] = {k: sc.tile("d_" + k) for k in list(U) + ["yb_ssd", "yb_gla", "yb_na", "ybw"]}
        G["ybw"] = ybw

        ones_f = P.sb(top, "ones_f", [128, 128], F32)
        ones_t = sc.tile("ones_f")
        sc.op("pool", lambda e: e.memset(ones_f[:], 1.0), writes=[ones_t])
        sc.op("pool", lambda e: e.affine_select(out=G["ident"][:], in_=ones_f[:], pattern=[[-1, 128]],
                                                compare_op=ALU.is_equal, fill=0.0, base=0,
                                                channel_multiplier=1),
              reads=[ones_t], writes=[G["ident_t"]])
        G["ones_f"] = ones_f
        G["eps"] = P.sb(top, "epsc", [128, 2], F32)
        G["eps_t"] = sc.tile("epsc")
        sc.op("pool", lambda e: e.memset(G["eps"][:], EPS), writes=[G["eps_t"]])
        G["one"] = P.sb(top, "onec", [128, 2], F32)
        G["one_t"] = sc.tile("onec")
        sc.op("pool", lambda e: e.memset(G["one"][:], 1.0), writes=[G["one_t"]])
        G["neghalf"] = P.sb(top, "neghalf", [128, 16], F32)
        G["neghalf_t"] = sc.tile("neghalf")
        sc.op("pool", lambda e: e.memset(G["neghalf"][:], -0.5), writes=[G["neghalf_t"]])
        G["ones_t"] = ones_t

        build_tri(P, sc, G, top)
        xv = x_in.rearrange("(i p) d -> p i d", p=128)
        for i in range(NT):
            sc.dma("sp", G["x"][:, i, :], xv[:, i, :], owner=G["xt"][i], writes=[G["xt"][i]])

        for l in range(layers):
            if "A" in phases:
                phase_A(P, sc, G, U, prm, l)
            if "B" in phases:
                phase_B(P, sc, G, U, YB, prm, l)
            if "C" in phases:
                phase_C(P, sc, G, U, YB, prm, l)
            if "D" in phases:
                phase_D(P, sc, G, U, YB, prm, natt, l)
            if "E" in phases:
                phase_E(P, sc, G, U, YB, prm, l)
            if "F" in phases:
                phase_F(P, sc, G, prm, l)

        yv = y_out.rearrange("(i p) d -> p i d", p=128)
        outt = sc.tile("yout")
        for i in range(NT):
            sc.dma("sp", yv[:, i, :], G["x"][:, i, :], owner=G["xt"][i], reads=[G["xt"][i]], writes=[outt],
                   part=True)
        sc.op("sp", lambda e: e.nop(), reads=[outt])
        sc.barrier()
        sc.emit()
    return nc


def rms_transpose(P, sc, G, ph, wrow_ap, hT, hT_t, l, tag):
    nc = P.nc
    wb = P.sb(ph, tag + "_wb", [128, D], F32)
    wb_t = sc.tile(tag + "_wb")
    sc.dma("sp", wb[:], wrow_ap.partition_broadcast(128), owner=wb_t, writes=[wb_t])
    junk = [P.sb(ph, tag + "_junk%d" % i, [128, D], BF16) for i in range(2)]
    junk_t = sc.tiles_n(tag + "_junk", 2)
    hb = [P.sb(ph, tag + "_hb%d" % i, [128, D], BF16) for i in range(2)]
    hb_t = sc.tiles_n(tag + "_hb", 2)
    ss = [P.sb(ph, tag + "_ss%d" % i, [128, 2], F32) for i in range(2)]
    ss_t = sc.tiles_n(tag + "_ss", 2)
    tp = [P.ps(ph, tag + "_tp%d" % i, [128, 4, 128], BF16) for i in range(2)]
    tp_t = sc.tiles_n(tag + "_tp", 2)
    x = G["x"]
    new_tiles = [wb_t] + junk_t + hb_t + ss_t + tp_t
    def _s1(i):
        b = i % 2
        xt = G["xt"][i]
        sc.op("dve", lambda e, i=i, b=b: e.scalar_tensor_tensor(out=junk[b][:], in0=x[:, i, :], scalar=1.0,
                                                                in1=x[:, i, :], op0=ALU.mult, op1=ALU.mult,
                                                                accum_out=ss[b][:, 0:1]),
              reads=[xt], writes=[junk_t[b], ss_t[b]])
        sc.op("dve", lambda e, b=b: e.tensor_scalar(out=ss[b][:, 1:2], in0=ss[b][:, 0:1], scalar1=1.0 / D,
                                                    scalar2=EPS, op0=ALU.mult, op1=ALU.add),
              reads=[ss_t[b]], writes=[ss_t[b]])
        sc.op("pool", lambda e, b=b: e.tensor_tensor(out=ss[b][:, 0:1], in0=ss[b][:, 1:2],
                                                     in1=G["neghalf"][:, 0:1], op=ALU.pow),
              reads=[ss_t[b], G["neghalf_t"]], writes=[ss_t[b]])

    def _s2(i):
        b = i % 2
        xt = G["xt"][i]
        sc.op("dve", lambda e, i=i, b=b: e.scalar_tensor_tensor(out=hb[b][:], in0=x[:, i, :],
                                                                scalar=ss[b][:, 0:1], in1=wb[:],
                                                                op0=ALU.mult, op1=ALU.mult),
              reads=[xt, ss_t[b], wb_t], writes=[hb_t[b]])
        for half in range(2):
            pb = (2 * i + half) % 2
            for j in range(4):
                kc = half * 4 + j
                sc.op("pe", lambda e, b=b, pb=pb, j=j, kc=kc: e.transpose(
                    out=tp[pb][:, j, :], in_=hb[b][:, kc * 128:(kc + 1) * 128], identity=G["ident"][:]),
                    reads=[hb_t[b], G["ident_t"]], writes=[tp_t[pb]], part=(j > 0))
            eng = "act"
            if eng == "act":
                sc.op("act", lambda e, pb=pb, half=half, i=i: e.copy(
                    out=hT[:, half * 4:half * 4 + 4, i * 128:(i + 1) * 128], in_=tp[pb][:]),
                    reads=[tp_t[pb]], writes=[hT_t[i]], part=True)
            else:
                sc.op("dve", lambda e, pb=pb, half=half, i=i: e.tensor_copy(
                    out=hT[:, half * 4:half * 4 + 4, i * 128:(i + 1) * 128], in_=tp[pb][:]),
                    reads=[tp_t[pb]], writes=[hT_t[i]], part=True)

    _s1(0)
    for i in range(NT):
        if i + 1 < NT:
            _s1(i + 1)
        _s2(i)
    return new_tiles


class WStream:
    def __init__(self, P, sc, ph, tag, kdim, ncol, nf=2, nb=3, cast_eng="pool"):
        self.sc = sc
        self.cast_eng = cast_eng
        self.kdim, self.ncol = kdim, ncol
        self.nf, self.nb = nf, nb
        self.f = [P.sb(ph, "%s_wf%d" % (tag, i), [128, kdim, ncol], F32) for i in range(nf)]
        self.f_t = sc.tiles_n(tag + "_wf", nf)
        self.b = [P.sb(ph, "%s_wb%d" % (tag, i), [128, kdim, ncol], BF16) for i in range(nb)]
        self.b_t = sc.tiles_n(tag + "_wbt", nb)
        self.tiles = self.f_t + self.b_t
        self.items = []

    def start(self, items):
        self.items = items
        self._load(0)
        self._load(1)
        self._cast(0)

    def _load(self, g):
        if g >= len(self.items):
            return
        ap, k, n = self.items[g]
        fs = g % self.nf
        self.sc.dma("sp", self.f[fs][:, 0:k, 0:n], ap, owner=self.f_t[fs], writes=[self.f_t[fs]])

    def _cast(self, g):
        if g >= len(self.items):
            return
        ap, k, n = self.items[g]
        fs, bs = g % self.nf, g % self.nb
        if self.cast_eng == "act":
            self.sc.op("act", lambda e: e.copy(out=self.b[bs][:, 0:k, 0:n], in_=self.f[fs][:, 0:k, 0:n]),
                       reads=[self.f_t[fs]], writes=[self.b_t[bs]])
        else:
            self.sc.op("pool", lambda e: e.tensor_copy(out=self.b[bs][:, 0:k, 0:n], in_=self.f[fs][:, 0:k, 0:n]),
                       reads=[self.f_t[fs]], writes=[self.b_t[bs]])

    def get(self, g):
        self._cast(g + 1)
        self._load(g + 2)
        return self.b[g % self.nb], self.b_t[g % self.nb]


def proj_groups():
    g = []

    def seg(off, n, mode, key):
        c = 0
        while c < n:
            w = min(512, n - c)
            g.append((off + c, w, mode, key, c))
            c += w
    seg(O_Z, 1024, "tok", "z")
    seg(O_XBC, 1536, "feat", "xbc")
    g.append((O_DTF, 32, "tok32", "dt", 0))
    seg(O_GQ, 512, "feat", "gq")
    seg(O_GK, 512, "feat", "gk")
    seg(O_GV, 1024, "tok", "gv")
    seg(O_GG, 1024, "tok", "gg")
    g.append((O_GAF, 32, "feat32", "ga", 0))
    seg(O_NQ, 1024, "feat", "nq")
    seg(O_NK, 1024, "feat", "nk")
    seg(O_NV, 1024, "tok", "nv")
    seg(O_GATE, 3072, "feat", "gate")
    return g


def phase_A(P, sc, G, U, prm, l):
    nc = P.nc
    with contextlib.ExitStack() as ph:
        hT = P.sb(ph, "A_hT", [128, 8, S], BF16)
        hT_t = sc.tiles_n("A_hT", NT)
        tiles = list(hT_t)
        tiles += rms_transpose(P, sc, G, ph, prm["norm_mix_w"][l], hT, hT_t, l, "A")
        wst = WStream(P, sc, ph, "A", 8, 512)
        acc = [P.ps(ph, "A_acc%d" % i, [128, 512], F32) for i in range(4)]
        acc_t = sc.tiles_n("A_acc", 4)
        NS = 3
        stg = [P.sb(ph, "A_stg%d" % i, [128, 2048], BF16) for i in range(NS)]
        stg_t = sc.tiles_n("A_stg", NS)
        stf = [P.sb(ph, "A_stf%d" % i, [128, 4, 32], F32) for i in range(2)]
        stf_t = sc.tiles_n("A_stf", 2)
        G["ga_stage"] = P.sb(ph, "A_gast", [32, 2048], F32)
        G["ga_stage_t"] = sc.tile("A_gast")
        tiles += wst.tiles + acc_t + stg_t + stf_t + [G["ga_stage_t"]]
        wv = prm["w_in"][l].rearrange("(kc p) n -> p kc n", p=128)
        groups = proj_groups()
        wst.start([(wv[:, :, c0:c0 + n], 8, n) for (c0, n, _m, _k, _d) in groups])
        ai = 0
        si = 0
        ev = 0
        for gi, (c0, n, mode, key, doff) in enumerate(groups):
            wcur, wcur_t = wst.get(gi)
            dst = U[key]
            dst_t = G["dram_t"][key]
            if mode in ("tok", "tok32"):
                for tb in range(4):
                    if mode == "tok":
                        st = si % NS
                        si += 1
                    else:
                        st = tb % 2
                    for j in range(4):
                        i = tb * 4 + j
                        a = ai % 4
                        ai += 1
                        for kc in range(8):
                            sc.op("pe", lambda e, a=a, kc=kc, i=i, wcur=wcur, n=n: e.matmul(
                                acc[a][:, 0:n], lhsT=hT[:, kc, i * 128:(i + 1) * 128], rhs=wcur[:, kc, 0:n],
                                start=(kc == 0), stop=(kc == 7)),
                                reads=[hT_t[i], wcur_t], writes=[acc_t[a]], part=(kc > 0))
                        if mode == "tok":
                            o_ap = stg[st][:, j * 512:j * 512 + n]
                            o_t = stg_t[st]
                        else:
                            o_ap = stf[st][:, j, 0:n]
                            o_t = stf_t[st]
                        ev += 1
                        if ev % 2 == 0:
                            sc.op("act", lambda e, o_ap=o_ap, a=a, n=n: e.copy(out=o_ap, in_=acc[a][:, 0:n]),
                                  reads=[acc_t[a]], writes=[o_t], part=(j > 0))
                        else:
                            sc.op("dve", lambda e, o_ap=o_ap, a=a, n=n: e.tensor_copy(out=o_ap, in_=acc[a][:, 0:n]),
                                  reads=[acc_t[a]], writes=[o_t], part=(j > 0))
                    rows = dst[tb * 512:(tb + 1) * 512, doff:doff + n].rearrange("(j p) c -> p j c", p=128)
                    if mode == "tok":
                        src = stg[st][:].rearrange("p (j c) -> p j c", j=4)[:, :, 0:n]
                        sc.dma("pool", rows, src, owner=stg_t[st], reads=[stg_t[st]], writes=[dst_t], part=True)
                    else:
                        sc.dma("pool", rows, stf[st][:, :, 0:n], owner=stf_t[st], reads=[stf_t[st]], writes=[dst_t],
                               part=True)
            else:
                nchunk = (n + 127) // 128
                for c in range(nchunk):
                    m = min(128, n - c * 128)
                    if mode == "feat":
                        st = si % NS
                        si += 1
                    else:
                        st = 0
                    for tb in range(4):
                        a = ai % 4
                        ai += 1
                        for kc in range(8):
                            sc.op("pe", lambda e, a=a, kc=kc, tb=tb, wcur=wcur, c=c, m=m: e.matmul(
                                acc[a][0:m, :], lhsT=wcur[:, kc, c * 128:c * 128 + m],
                                rhs=hT[:, kc, tb * 512:(tb + 1) * 512], start=(kc == 0), stop=(kc == 7)),
                                reads=hT_t[tb * 4:tb * 4 + 4] + [wcur_t], writes=[acc_t[a]], part=(kc > 0))
                        ev += 1
                        if mode == "feat":
                            o_ap = stg[st][0:m, tb * 512:(tb + 1) * 512]
                            o_t = stg_t[st]
                            if ev % 2 == 0:
                                sc.op("act", lambda e, o_ap=o_ap, a=a, m=m: e.copy(out=o_ap, in_=acc[a][0:m, :]),
                                      reads=[acc_t[a]], writes=[o_t], part=(tb > 0))
                            else:
                                sc.op("dve", lambda e, o_ap=o_ap, a=a, m=m: e.tensor_copy(out=o_ap, in_=acc[a][0:m, :]),
                                      reads=[acc_t[a]], writes=[o_t], part=(tb > 0))
                        else:
                            sc.op("dve", lambda e, a=a, m=m, tb=tb, gast=G["ga_stage"]: e.tensor_copy(
                                out=gast[0:m, tb * 512:(tb + 1) * 512], in_=acc[a][0:m, :]),
                                reads=[acc_t[a]], writes=[G["ga_stage_t"]], part=(tb > 0))
                    if mode == "feat":
                        sc.dma("pool", dst[doff + c * 128:doff + c * 128 + m, :], stg[st][0:m, :], owner=stg_t[st],
                               reads=[stg_t[st]], writes=[dst_t], part=True)
                    else:
                        sc.dma("pool", dst[0:m, :], G["ga_stage"][0:m, :], owner=G["ga_stage_t"],
                               reads=[G["ga_stage_t"]], writes=[dst_t], part=True)
        sc.barrier(release=tiles)


def phase_E(P, sc, G, U, YB, prm, l):
    nc = P.nc
    x = G["x"]
    with contextlib.ExitStack() as ph:
        wst = WStream(P, sc, ph, "E", 8, 256, nf=2, nb=2, cast_eng="act")
        mix = P.sb(ph, "E_mix", [128, 8, 1024], F32)
        mix_t = sc.tiles_n("E_mix", 8)
        mixb = P.sb(ph, "E_mixb", [128, 8, 1024], BF16)
        mixb_t = sc.tile("E_mixb")
        ybT = [P.sb(ph, "E_yb%d" % i, [128, 8, 1024], BF16) for i in range(2)]
        ybT_t = sc.tiles_n("E_yb", 2)
        gsl = [P.sb(ph, "E_g%d" % i, [128, 1024], BF16) for i in range(3)]
        gsl_t = sc.tiles_n("E_g", 3)
        sig = [P.sb(ph, "E_sig%d" % i, [128, 1024], F32) for i in range(2)]
        sig_t = sc.tiles_n("E_sig", 2)
        tmp = [P.sb(ph, "E_tmp%d" % i, [128, 512], F32) for i in range(2)]
        tmp_t = sc.tiles_n("E_tmp", 2)
        acc = [P.ps(ph, "E_acc%d" % i, [128, 512], F32) for i in range(4)]
        acc_t = sc.tiles_n("E_acc", 4)
        tiles = wst.tiles + mix_t + [mixb_t] + ybT_t + gsl_t + sig_t + tmp_t + acc_t
        wnames = ["w_branch_ssd", "w_branch_gla", "w_branch_na", "w_out"]
        bnames = ["ssd", "gla", "na"]
        items = []
        for half in range(2):
            for wn in wnames:
                wv = prm[wn][l].rearrange("(kc p) n -> p kc n", p=128)
                for cg in range(4):
                    items.append((wv[:, :, cg * 256:(cg + 1) * 256], 8, 256))
        wst.start(items)
        gi = 0
        ai = 0
        gcount = 0
        tcount = 0
        ybcount = 0
        for half in range(2):
            t0 = half * 1024
            for b in range(3):
                ys = ybcount % 2
                ybcount += 1
                ybv = YB[bnames[b]].rearrange("(kc p) t -> p kc t", p=128)
                sc.dma("sp", ybT[ys][:], ybv[:, :, t0:t0 + 1024], owner=ybT_t[ys],
                       reads=[G["dram_t"]["yb_" + bnames[b]]], writes=[ybT_t[ys]])
                for cg in range(4):
                    wcur, wcur_t = wst.get(gi)
                    gi += 1
                    for ecl in range(2):
                        ec = cg * 2 + ecl
                        gs = gcount % 3
                        ss_ = gcount % 2
                        gcount += 1
                        grow = b * 1024 + ec * 128
                        sc.dma("sp", gsl[gs][:], U["gate"][grow:grow + 128, t0:t0 + 1024], owner=gsl_t[gs],
                               reads=[G["dram_t"]["gate"]], writes=[gsl_t[gs]])
                        sc.op("act", lambda e, gs=gs, ss_=ss_: e.activation(out=sig[ss_][:], in_=gsl[gs][:],
                                                                            func=AF.Sigmoid),
                              reads=[gsl_t[gs]], writes=[sig_t[ss_]])
                        for tbh in range(2):
                            a = ai % 4
                            ai += 1
                            for kc in range(8):
                                sc.op("pe", lambda e, a=a, kc=kc, wcur=wcur, ecl=ecl, ys=ys, tbh=tbh: e.matmul(
                                    acc[a][:], lhsT=wcur[:, kc, ecl * 128:(ecl + 1) * 128],
                                    rhs=ybT[ys][:, kc, tbh * 512:(tbh + 1) * 512], start=(kc == 0), stop=(kc == 7)),
                                    reads=[wcur_t, ybT_t[ys]], writes=[acc_t[a]], part=(kc > 0))
                            msl = mix[:, ec, tbh * 512:(tbh + 1) * 512]
                            sgl = sig[ss_][:, tbh * 512:(tbh + 1) * 512]
                            if b == 0:
                                sc.op("dve", lambda e, msl=msl, a=a, sgl=sgl: e.tensor_tensor(
                                    out=msl, in0=acc[a][:], in1=sgl, op=ALU.mult),
                                    reads=[acc_t[a], sig_t[ss_]], writes=[mix_t[ec]], part=(tbh > 0))
                            else:
                                ts = tcount % 2
                                tcount += 1
                                sc.op("dve", lambda e, ts=ts, a=a, sgl=sgl: e.tensor_tensor(
                                    out=tmp[ts][:], in0=acc[a][:], in1=sgl, op=ALU.mult),
                                    reads=[acc_t[a], sig_t[ss_]], writes=[tmp_t[ts]])
                                if b == 1:
                                    sc.op("pool", lambda e, msl=msl, ts=ts: e.tensor_tensor(
                                        out=msl, in0=msl, in1=tmp[ts][:], op=ALU.add),
                                        reads=[tmp_t[ts], mix_t[ec]], writes=[mix_t[ec]])
                                else:
                                    sc.op("pool", lambda e, msl=msl, ts=ts, ec=ec, tbh=tbh: e.tensor_tensor(
                                        out=mixb[:, ec, tbh * 512:(tbh + 1) * 512], in0=msl, in1=tmp[ts][:],
                                        op=ALU.add),
                                        reads=[tmp_t[ts], mix_t[ec]], writes=[mixb_t], part=True)
            for cg in range(4):
                wcur, wcur_t = wst.get(gi)
                gi += 1
                for j in range(8):
                    i = half * 8 + j
                    a = ai % 4
                    ai += 1
                    for ec in range(8):
                        sc.op("pe", lambda e, a=a, ec=ec, wcur=wcur, j=j: e.matmul(
                            acc[a][:, 0:256], lhsT=mixb[:, ec, j * 128:(j + 1) * 128], rhs=wcur[:, ec, :],
                            start=(ec == 0), stop=(ec == 7)),
                            reads=[wcur_t, mixb_t], writes=[acc_t[a]], part=(ec > 0))
                    xs = x[:, i, cg * 256:(cg + 1) * 256]
                    sc.op("dve", lambda e, xs=xs, a=a: e.tensor_tensor(out=xs, in0=xs, in1=acc[a][:, 0:256], op=ALU.add),
                          reads=[acc_t[a], G["xt"][i]], writes=[G["xt"][i]])
        sc.barrier(release=tiles)


class Ring:
    def __init__(self, P, sc, stack, name, shape, dt, n, psum=False, views=None):
        if views is not None:
            self.h = views
            n = len(views)
        else:
            mk = P.ps if psum else P.sb
            self.h = [mk(stack, "%s%d" % (name, i), shape, dt) for i in range(n)]
        self.t = sc.tiles_n(name + "_", n)
        self.i = 0
        self.n = n

    def next(self):
        k = self.i % self.n
        self.i += 1
        return self.h[k], self.t[k]


def build_tri(P, sc, G, top):
    for nm in ("trif", "trib", "trif64", "trib64", "mcf64", "mcb64", "trifs", "tribs"):
        G[nm] = P.sb(top, nm, [128, 128], F32)
        G[nm + "_t"] = sc.tile(nm)
    ones_f, ones_t = G["ones_f"], G["ones_t"]
    sc.op("pool", lambda e: e.affine_select(out=G["trif"][:], in_=ones_f[:], pattern=[[1, 128]], compare_op=ALU.is_ge,
                                            fill=0.0, base=0, channel_multiplier=-1),
          reads=[ones_t], writes=[G["trif_t"]])
    sc.op("pool", lambda e: e.affine_select(out=G["trib"][:], in_=ones_f[:], pattern=[[-1, 128]], compare_op=ALU.is_ge,
                                            fill=0.0, base=0, channel_multiplier=1),
          reads=[ones_t], writes=[G["trib_t"]])
    sc.op("pool", lambda e: e.affine_select(out=G["trifs"][:], in_=ones_f[:], pattern=[[1, 128]], compare_op=ALU.is_gt,
                                            fill=0.0, base=0, channel_multiplier=-1),
          reads=[ones_t], writes=[G["trifs_t"]])
    sc.op("pool", lambda e: e.affine_select(out=G["tribs"][:], in_=ones_f[:], pattern=[[-1, 128]], compare_op=ALU.is_gt,
                                            fill=0.0, base=0, channel_multiplier=1),
          reads=[ones_t], writes=[G["tribs_t"]])
    sc.op("pool", lambda e: e.tensor_copy(out=G["trif64"][:], in_=G["trif"][:]), reads=[G["trif_t"]], writes=[G["trif64_t"]])
    sc.op("pool", lambda e: e.memset(G["trif64"][0:64, 64:128], 0.0), reads=[G["trif64_t"]], writes=[G["trif64_t"]])
    sc.op("pool", lambda e: e.tensor_copy(out=G["trib64"][:], in_=G["trib"][:]), reads=[G["trib_t"]], writes=[G["trib64_t"]])
    sc.op("pool", lambda e: e.memset(G["trib64"][64:128, 0:64], 0.0), reads=[G["trib64_t"]], writes=[G["trib64_t"]])
    sc.op("pool", lambda e: e.tensor_scalar(out=G["mcf64"][:], in0=G["trif64"][:], scalar1=-1.0 / 16.0, scalar2=None,
                                            op0=ALU.mult), reads=[G["trif64_t"]], writes=[G["mcf64_t"]])
    sc.op("pool", lambda e: e.tensor_scalar(out=G["mcb64"][:], in0=G["trib64"][:], scalar1=-1.0 / 16.0, scalar2=None,
                                            op0=ALU.mult), reads=[G["trib64_t"]], writes=[G["mcb64_t"]])


def phase_C(P, sc, G, U, YB, prm, l):
    nc = P.nc
    ident = G["ident"]
    with contextlib.ExitStack() as ph:
        qT = P.sb(ph, "C_qT", [128, 4, S], BF16)
        kT = P.sb(ph, "C_kT", [128, 4, S], BF16)
        qT_t = sc.tile("C_qT")
        kT_t = sc.tile("C_kT")
        ob = P.sb(ph, "C_ob", [128, NT, 1024], BF16)
        ob_t = sc.tiles_n("C_ob", NT)
        gaX = P.sb(ph, "C_gaX", [32, S], F32)
        gaX_t = sc.tile("C_gaX")
        a2X = [P.sb(ph, "C_a2X%d" % d, [32, 512], F32) for d in range(2)]
        a2X_t = sc.tiles_n("C_a2X", 2)
        nwb = P.sb(ph, "C_nwb", [128, 256], F32)
        nwb_t = sc.tile("C_nwb")
        Sf = P.sb(ph, "C_Sf", [128, 4, 256], F32)
        Sf_t = sc.tile("C_Sf")
        yst = P.sb(ph, "C_yst", [128, 8, 256], BF16)
        yst_t = sc.tile("C_yst")
        R = lambda name, shape, dt, n, psum=False: Ring(P, sc, ph, "C_" + name, shape, dt, n, psum)
        r_Sb = R("Sb", [128, 4, 256], BF16, 3)
        r_v = R("v", [128, 1024], BF16, 2)
        r_gg = R("gg", [128, 4, 1024], BF16, 1)
        r_e1 = R("e1", [128, 512], F32, 1)
        r_bs = R("bs", [128, 4, 128], F32, 1)
        r_eb = R("eb", [128, 4, 128], F32, 1)
        r_enb = R("enb", [128, 4, 128], F32, 1)
        r_ew = R("ew", [128, 4, 128], F32, 1)
        r_ed = R("ed", [128, 4, 2], F32, 3)
        r_qd = R("qd", [128, 4, 128], BF16, 2)
        r_kd = R("kd", [128, 4, 128], BF16, 2)
        r_kw = R("kw", [128, 4, 128], BF16, 1)
        r_kwt = R("kwt", [128, 4, 128], BF16, 2)
        r_am = R("am", [128, 4, 128], BF16, 2)
        r_oa = R("oa", [128, 1024], F32, 1)
        r_sg = R("sg", [128, 4, 1024], BF16, 1)
        r_jk = R("jk", [128, 256], BF16, 1)
        r_ss = R("ss", [128, 8], F32, 2)
        r_y = R("y", [128, 1024], BF16, 2)
        r_gp = R("gp", [128, 512], F32, 1, True)
        r_bT = R("bT", [128, 4, 128], F32, 1, True)
        r_att = R("att", [128, 4, 128], F32, 1, True)
        r_kwp = R("kwp", [128, 4, 128], BF16, 1, True)
        r_st = R("st", [128, 4, 256], F32, 1, True)
        r_o = R("o", [128, 4, 256], F32, 1, True)
        rings = [r_Sb, r_v, r_gg, r_e1, r_bs, r_eb, r_enb, r_ew, r_ed, r_qd, r_kd, r_kw, r_kwt, r_am, r_oa, r_sg,
                 r_jk, r_ss, r_y, r_gp, r_bT, r_att, r_kwp, r_st, r_o]
        tiles = [qT_t, kT_t, nwb_t, yst_t, gaX_t, Sf_t] + ob_t + a2X_t
        for r in rings:
            tiles += r.t
        sc.dma("sp", qT[:], U["gq"].rearrange("(h p) t -> p h t", p=128), owner=qT_t, reads=[G["dram_t"]["gq"]],
               writes=[qT_t])
        sc.dma("sp", kT[:], U["gk"].rearrange("(h p) t -> p h t", p=128), owner=kT_t, reads=[G["dram_t"]["gk"]],
               writes=[kT_t])
        sc.dma("sp", nwb[:], prm["gla_norm_w"][l].partition_broadcast(128), owner=nwb_t, writes=[nwb_t])
        for d in range(2):
            a2 = prm["gla_a2_f" if d == 0 else "gla_a2_b"][l]
            bi = prm["gla_a2_bias_f" if d == 0 else "gla_a2_bias_b"][l]
            sc.dma("sp", a2X[d][0:16, :], a2, owner=a2X_t[d], writes=[a2X_t[d]])
            sc.dma("sp", a2X[d][16:17, :], bi.rearrange("(o n) -> o n", o=1), owner=a2X_t[d], writes=[a2X_t[d]], part=True)

        def gla_pass(d):
            fwd = (d == 0)
            mc, mc_t = (G["mcf64"], G["mcf64_t"]) if fwd else (G["mcb64"], G["mcb64_t"])
            ma, ma_t = (G["trif64"], G["trif64_t"]) if fwd else (G["trib64"], G["trib64_t"])
            lc0 = 63 if fwd else 0
            sc.op("pool", lambda e: e.memset(gaX[:], 1.0), writes=[gaX_t])
            sc.dma("sp", gaX[0:16, :], U["ga"][16 * d:16 * d + 16, :], owner=gaX_t, reads=[G["dram_t"]["ga"]],
                   writes=[gaX_t])
            sc.op("pool", lambda e: e.memset(Sf[:], 0.0), writes=[Sf_t])
            sb0, sb0_t = r_Sb.next()
            sc.op("pool", lambda e, sb0=sb0: e.memset(sb0[:], 0.0), writes=[sb0_t])
            cur = [(sb0, sb0_t)]
            sgcur = [None]
            order = list(range(NT)) if fwd else list(range(NT - 1, -1, -1))
            chunks = (0, 1) if fwd else (1, 0)

            def stage1(i):
                tsl = slice(i * 128, (i + 1) * 128)
                v, v_t = r_v.next()
                sc.dma("sp", v[:], U["gv"][tsl, :], owner=v_t, reads=[G["dram_t"]["gv"]], writes=[v_t])
                gp, gp_t = r_gp.next()
                sc.op("pe", lambda e, gp=gp, tsl=tsl: e.matmul(gp[:], lhsT=gaX[0:17, tsl], rhs=a2X[d][0:17, :],
                                                               start=True, stop=True),
                      reads=[gaX_t, a2X_t[d]], writes=[gp_t])
                e1, e1_t = r_e1.next()
                sc.op("act", lambda e, e1=e1, gp=gp: e.activation(out=e1[:], in_=gp[:], func=AF.Exp, scale=-1.0),
                      reads=[gp_t], writes=[e1_t])
                gn, gn_t = e1, e1_t
                sc.op("act", lambda e, gn=gn, e1=e1: e.activation(out=gn[:], in_=e1[:], func=AF.Ln, bias=G["one"][:, 0:1]),
                      reads=[e1_t, G["one_t"]], writes=[gn_t])
                bT, bT_t = r_bT.next()
                for h in range(4):
                    sc.op("pe", lambda e, bT=bT, gn=gn, h=h: e.matmul(bT[:, h, :], lhsT=gn[:, h * 128:(h + 1) * 128], rhs=mc[:],
                                                                     start=True, stop=True, skip_group_check=True),
                          reads=[gn_t, mc_t], writes=[bT_t], part=(h > 0))
                bs, bs_t = r_bs.next()
                sc.op("act", lambda e, bs=bs, bT=bT: e.copy(out=bs[:], in_=bT[:]), reads=[bT_t], writes=[bs_t])
                eb, eb_t = r_eb.next()
                sc.op("act", lambda e, eb=eb, bs=bs: e.activation(out=eb[:], in_=bs[:], func=AF.Exp), reads=[bs_t], writes=[eb_t])
                enb, enb_t = r_enb.next()
                sc.op("act", lambda e, enb=enb, bs=bs: e.activation(out=enb[:], in_=bs[:], func=AF.Exp, scale=-1.0),
                      reads=[bs_t], writes=[enb_t])
                ed, ed_t = r_ed.next()
                sc.op("act", lambda e, ed=ed, bs=bs: e.activation(
                    out=ed[:], in_=bs[:].rearrange("p h (c l) -> p h c l", c=2)[:, :, :, lc0], func=AF.Exp),
                    reads=[bs_t], writes=[ed_t])
                qd, qd_t = r_qd.next()
                sc.op("dve", lambda e, qd=qd, tsl=tsl, eb=eb: e.scalar_tensor_tensor(
                    out=qd[:], in0=qT[:, :, tsl], scalar=128.0 ** -0.5, in1=eb[:], op0=ALU.mult, op1=ALU.mult),
                    reads=[qT_t, eb_t], writes=[qd_t])
                kd, kd_t = r_kd.next()
                sc.op("dve", lambda e, kd=kd, tsl=tsl, enb=enb: e.tensor_tensor(
                    out=kd[:], in0=kT[:, :, tsl], in1=enb[:], op=ALU.mult), reads=[kT_t, enb_t], writes=[kd_t])
                ew, ew_t = r_ew.next()
                sc.op("dve", lambda e, ew=ew, enb=enb, ed=ed: e.tensor_tensor(
                    out=ew[:].rearrange("p h (c l) -> p (h c) l", c=2), in0=enb[:].rearrange("p h (c l) -> p (h c) l", c=2),
                    in1=ed[:].rearrange("p h c -> p (h c)").unsqueeze(2).to_broadcast([128, 8, 64]), op=ALU.mult),
                    reads=[enb_t, ed_t], writes=[ew_t])
                kw, kw_t = r_kw.next()
                sc.op("dve", lambda e, kw=kw, tsl=tsl, ew=ew: e.tensor_tensor(
                    out=kw[:], in0=kT[:, :, tsl], in1=ew[:], op=ALU.mult), reads=[kT_t, ew_t], writes=[kw_t])
                kwp, kwp_t = r_kwp.next()
                for h in range(4):
                    sc.op("pe", lambda e, kwp=kwp, kw=kw, h=h: e.transpose(out=kwp[:, h, :], in_=kw[:, h, :], identity=ident[:]),
                          reads=[kw_t, G["ident_t"]], writes=[kwp_t], part=(h > 0))
                kwt, kwt_t = r_kwt.next()
                sc.op("act", lambda e, kwt=kwt, kwp=kwp: e.copy(out=kwt[:], in_=kwp[:]), reads=[kwp_t], writes=[kwt_t])
                att, att_t = r_att.next()
                for h in range(4):
                    sc.op("pe", lambda e, att=att, kd=kd, qd=qd, h=h: e.matmul(att[:, h, :], lhsT=kd[:, h, :], rhs=qd[:, h, :],
                                                                            start=True, stop=True, skip_group_check=True),
                          reads=[kd_t, qd_t], writes=[att_t], part=(h > 0))
                am, am_t = r_am.next()
                sc.op("dve", lambda e, am=am, att=att: e.tensor_tensor(
                    out=am[:], in0=att[:], in1=ma[:].unsqueeze(1).to_broadcast([128, 4, 128]), op=ALU.mult),
                    reads=[att_t, ma_t], writes=[am_t])
                return (i, tsl, v, v_t, qd, qd_t, kwt, kwt_t, ed, ed_t, am, am_t)

            def stage23(ctx):
                (i, tsl, v, v_t, qd, qd_t, kwt, kwt_t, ed, ed_t, am, am_t) = ctx
                sbs = [cur[0]]
                for ci, c in enumerate(chunks):
                    cs = slice(c * 64, (c + 1) * 64)
                    st, st_t = r_st.next()
                    for h in range(4):
                        sc.op("pe", lambda e, st=st, kwt=kwt, cs=cs, v=v, h=h: e.matmul(
                            st[:, h, :], lhsT=kwt[cs, h, :], rhs=v[cs, h * 256:(h + 1) * 256], start=True, stop=True,
                            skip_group_check=True),
                            reads=[kwt_t, v_t], writes=[st_t], part=(h > 0))
                    for h in range(4):
                        sc.op("dve", lambda e, st=st, h=h, ed=ed, c=c: e.scalar_tensor_tensor(
                            out=Sf[:, h, :], in0=Sf[:, h, :], scalar=ed[:, h, c:c + 1], in1=st[:, h, :], op0=ALU.mult,
                            op1=ALU.add),
                            reads=[st_t, ed_t, Sf_t], writes=[Sf_t])
                    nb, nb_t = r_Sb.next()
                    sc.op("act", lambda e, nb=nb: e.copy(out=nb[:], in_=Sf[:]), reads=[Sf_t], writes=[nb_t])
                    sbs.append((nb, nb_t))
                o, o_t = r_o.next()
                for h in range(4):
                    sc.op("pe", lambda e, o=o, am=am, v=v, h=h: e.matmul(o[:, h, :], lhsT=am[:, h, :],
                                                                       rhs=v[:, h * 256:(h + 1) * 256],
                                                                       start=True, stop=False, skip_group_check=True),
                          reads=[am_t, v_t], writes=[o_t], part=(h > 0))
                    for ci, c in enumerate(chunks):
                        cs = slice(c * 64, (c + 1) * 64)
                        sbv, sbv_t = sbs[ci]
                        sc.op("pe", lambda e, o=o, qd=qd, cs=cs, sbv=sbv, ci=ci, h=h: e.matmul(
                            o[cs, h, :], lhsT=qd[:, h, cs], rhs=sbv[:, h, :], start=False, stop=(ci == 1),
                            skip_group_check=True),
                            reads=[qd_t, sbv_t], writes=[o_t], part=True)
                cur[0] = sbs[2]
                if not fwd:
                    for hb in range(2):
                        sc.op("act", lambda e, o=o, hb=hb, i=i: e.copy(
                            out=ob[:, i, hb * 512:(hb + 1) * 512], in_=o[:, 2 * hb:2 * hb + 2, :].rearrange("p a b -> p (a b)")),
                            reads=[o_t], writes=[ob_t[i]], part=(hb > 0))
                    return
                oa, oa_t = r_oa.next()
                ss, ss_t = r_ss.next()
                for hb in range(2):
                    sc.op("dve", lambda e, oa=oa, o=o, hb=hb, i=i: e.tensor_tensor(
                        out=oa[:, hb * 512:(hb + 1) * 512], in0=o[:, 2 * hb:2 * hb + 2, :].rearrange("p a b -> p (a b)"),
                        in1=ob[:, i, hb * 512:(hb + 1) * 512], op=ALU.add),
                        reads=[o_t, ob_t[i]], writes=[oa_t], part=(hb > 0))
                for h in range(4):
                    hs = slice(h * 256, (h + 1) * 256)
                    jk, jk_t = r_jk.next()
                    sc.op("dve", lambda e, jk=jk, oa=oa, hs=hs, ss=ss, h=h: e.scalar_tensor_tensor(
                        out=jk[:], in0=oa[:, hs], scalar=1.0, in1=oa[:, hs], op0=ALU.mult, op1=ALU.mult,
                        accum_out=ss[:, h:h + 1]), reads=[oa_t], writes=[jk_t, ss_t])
                if i % 4 == 0:
                    gg, gg_t = r_gg.next()
                    sc.dma("sp", gg[:], U["gg"][i * 128:(i + 4) * 128, :].rearrange("(j p) c -> p j c", p=128), owner=gg_t,
                           reads=[G["dram_t"]["gg"]], writes=[gg_t])
                    sg, sg_t = r_sg.next()
                    sc.op("act", lambda e, sg=sg, gg=gg: e.activation(out=sg[:], in_=gg[:], func=AF.Silu),
                          reads=[gg_t], writes=[sg_t])
                    sgcur[0] = (sg, sg_t)
                sg, sg_t = sgcur[0]
                sgn = sg[:, i % 4, :]
                sgn_t = sg_t
                sc.op("pool", lambda e, sgn=sgn: e.tensor_tensor(
                    out=sgn.rearrange("p (h v) -> p h v", h=4), in0=sgn.rearrange("p (h v) -> p h v", h=4),
                    in1=nwb[:].unsqueeze(1).to_broadcast([128, 4, 256]), op=ALU.mult),
                    reads=[sg_t, nwb_t], writes=[sg_t])
                sc.op("dve", lambda e, ss=ss: e.tensor_scalar(out=ss[:, 4:8], in0=ss[:, 0:4], scalar1=1.0 / 256.0, scalar2=EPS,
                                                              op0=ALU.mult, op1=ALU.add), reads=[ss_t], writes=[ss_t])
                sc.op("pool", lambda e, ss=ss: e.tensor_tensor(out=ss[:, 0:4], in0=ss[:, 4:8], in1=G["neghalf"][:, 0:4],
                                                               op=ALU.pow), reads=[ss_t, G["neghalf_t"]], writes=[ss_t])
                sc.op("dve", lambda e, oa=oa, ss=ss: e.tensor_tensor(
                    out=oa[:].rearrange("p (h v) -> p h v", h=4), in0=oa[:].rearrange("p (h v) -> p h v", h=4),
                    in1=ss[:, 0:4].unsqueeze(2).to_broadcast([128, 4, 256]), op=ALU.mult),
                    reads=[oa_t, ss_t], writes=[oa_t])
                y, y_t = r_y.next()
                sc.op("dve", lambda e, y=y, oa=oa, sgn=sgn: e.tensor_tensor(out=y[:], in0=oa[:], in1=sgn, op=ALU.mult),
                      reads=[oa_t, sgn_t], writes=[y_t])
                return (y, y_t, i)

            def stage3(c3):
                if c3 is None:
                    return
                (y, y_t, i) = c3
                emit_yT(P, sc, G, r_kwp, y, y_t, yst, yst_t, i, YB["gla"], G["dram_t"]["yb_gla"], gsz=2)

            prev = None
            prev3 = None
            for i in order:
                ctx = stage1(i)
                if prev is not None:
                    n3 = stage23(prev)
                    stage3(prev3)
                    prev3 = n3
                prev = ctx
            n3 = stage23(prev)
            stage3(prev3)
            stage3(n3)

        gla_pass(1)
        if "dbg_ob" in P.dbg:
            dob = P.dram("dbg_ob", [S, 1024], BF16)
            dt_ = sc.tile("dbg_ob")
            sc.dma("sp", dob.rearrange("(i p) c -> p i c", p=128), ob[:], owner=ob_t[0], reads=ob_t, writes=[dt_])
        gla_pass(0)
        sc.barrier(release=tiles)


def phase_B(P, sc, G, U, YB, prm, l):
    nc = P.nc
    ident = G["ident"]
    ybw = G["ybw"]
    ybw_t = G["dram_t"]["ybw"]
    with contextlib.ExitStack() as ph:
        xtok = P.sb(ph, "B_xtok", [128, NT, 1280], BF16)
        xtok_t = sc.tiles_n("B_xtok", NT)
        BT = P.sb(ph, "B_BT", [128, 2, S], BF16)
        CT = P.sb(ph, "B_CT", [128, 2, S], BF16)
        BT_t = sc.tiles_n("B_BT", 2)
        CT_t = sc.tiles_n("B_CT", 2)
        dtv = P.sb(ph, "B_dtv", [128, NT, 32], F32)
        av = P.sb(ph, "B_av", [128, NT, 32], F32)
        dtv_t = sc.tile("B_dtv")
        av_t = sc.tile("B_av")
        rows = P.sb(ph, "B_rows", [128, 4, 32], F32)
        rows_t = sc.tile("B_rows")
        nwb = P.sb(ph, "B_nwb", [128, 1024], F32)
        nwb_t = sc.tile("B_nwb")
        tiles = xtok_t + BT_t + CT_t + [dtv_t, av_t, rows_t, nwb_t]
        with contextlib.ExitStack() as s1:
            cwr = P.sb(s1, "B_cwr", [72, 128], F32)
            cwr_t = sc.tile("B_cwr")
            cw = P.sb(s1, "B_cw", [128, 72], F32)
            cw_t = sc.tile("B_cw")
            cwp = P.ps(s1, "B_cwp", [128, 72], F32)
            cwp_t = sc.tile("B_cwp")
            identf = P.sb(s1, "B_identf", [128, 128], F32)
            identf_t = sc.tile("B_identf")
            xc = [P.sb(s1, "B_xc%d" % i, [128, S + 4], BF16) for i in range(2)]
            xc_t = sc.tiles_n("B_xc", 2)
            dg = [P.sb(s1, "B_dg%d" % i, [128, 5, 128], BF16) for i in range(2)]
            dg_t = sc.tiles_n("B_dg", 2)
            cacc = [P.ps(s1, "B_cacc%d" % i, [128, 512], F32) for i in range(2)]
            cacc_t = sc.tiles_n("B_cacc", 2)
            xa = [P.sb(s1, "B_xa%d" % i, [128, S], BF16) for i in range(2)]
            xa_t = sc.tiles_n("B_xa", 2)
            tp = [P.ps(s1, "B_tp%d" % i, [128, 4, 128], BF16) for i in range(2)]
            tp_t = sc.tiles_n("B_tp", 2)
            tl1 = [cwr_t, cw_t, cwp_t, identf_t] + dg_t + cacc_t + xc_t + xa_t + tp_t
            sc.op("pool", lambda e: e.affine_select(out=identf[:], in_=G["ones_f"][:], pattern=[[-1, 128]],
                                                    compare_op=ALU.is_equal, fill=0.0, base=0, channel_multiplier=1),
                  reads=[G["ones_t"]], writes=[identf_t])
            sc.dma("sp", cwr[0:60, :], prm["ssd_conv_w"][l].rearrange("k (c p) -> (k c) p", p=128), owner=cwr_t, writes=[cwr_t])
            sc.dma("sp", cwr[60:72, :], prm["ssd_conv_b"][l].rearrange("(c p) -> c p", p=128), owner=cwr_t, writes=[cwr_t],
                   part=True)
            sc.op("pe", lambda e: e.transpose(out=cwp[:], in_=cwr[:], identity=identf[0:72, 0:72]),
                  reads=[cwr_t, identf_t], writes=[cwp_t])
            sc.op("act", lambda e: e.copy(out=cw[:], in_=cwp[:]), reads=[cwp_t], writes=[cw_t])
            for b in range(2):
                sc.op("pool", lambda e, b=b: e.memset(xc[b][:, 0:2], 0.0), writes=[xc_t[b]])
                sc.op("pool", lambda e, b=b: e.memset(xc[b][:, S + 2:S + 4], 0.0), writes=[xc_t[b]], part=True)
            sc.dma("sp", dtv[:], U["dt"].rearrange("(i p) c -> p i c", p=128), owner=dtv_t, reads=[G["dram_t"]["dt"]],
                   writes=[dtv_t])
            for k, nm in enumerate(("ssd_dt_bias_f", "ssd_dt_bias_b")):
                sc.dma("sp", rows[:, 0, 16 * k:16 * k + 16], prm[nm][l].partition_broadcast(128), owner=rows_t,
                       writes=[rows_t], part=True)
            for k, nm in enumerate(("ssd_a_log_f", "ssd_a_log_b")):
                sc.dma("sp", rows[:, 1, 16 * k:16 * k + 16], prm[nm][l].partition_broadcast(128), owner=rows_t,
                       writes=[rows_t], part=True)
            sc.dma("sp", rows[:, 2, 0:16], prm["ssd_d"][l].partition_broadcast(128), owner=rows_t, writes=[rows_t], part=True)
            sc.dma("sp", nwb[:], prm["ssd_norm_w"][l].partition_broadcast(128), owner=nwb_t, writes=[nwb_t])
            sc.op("dve", lambda e: e.tensor_tensor(out=dtv[:], in0=dtv[:], in1=rows[:, 0:1, :].to_broadcast([128, NT, 32]),
                                                   op=ALU.add), reads=[dtv_t, rows_t], writes=[dtv_t])
            sc.op("act", lambda e: e.activation(out=dtv[:], in_=dtv[:], func=AF.Exp), reads=[dtv_t], writes=[dtv_t])
            sc.op("act", lambda e: e.activation(out=dtv[:], in_=dtv[:], func=AF.Ln, bias=G["one"][:, 0:1]),
                  reads=[dtv_t, G["one_t"]], writes=[dtv_t])
            sc.op("act", lambda e: e.activation(out=rows[:, 3, :], in_=rows[:, 1, :], func=AF.Exp), reads=[rows_t],
                  writes=[rows_t])
            sc.op("dve", lambda e: e.scalar_tensor_tensor(out=av[:], in0=dtv[:], scalar=-1.0,
                                                          in1=rows[:, 3:4, :].to_broadcast([128, NT, 32]),
                                                          op0=ALU.mult, op1=ALU.mult),
                  reads=[dtv_t, rows_t], writes=[av_t])
            tpc = 0
            for c in range(12):
                b = c % 2
                sc.dma("sp", xc[b][:, 2:S + 2], U["xbc"][c * 128:(c + 1) * 128, :], owner=xc_t[b],
                       reads=[G["dram_t"]["xbc"]], writes=[xc_t[b]], part=True)
                dgb = c % 2
                for k in range(5):
                    sc.op("dve", lambda e, dgb=dgb, k=k, c=c: e.tensor_scalar(
                        out=dg[dgb][:, k, :], in0=identf[:], scalar1=cw[:, k * 12 + c:k * 12 + c + 1], scalar2=None,
                        op0=ALU.mult), reads=[identf_t, cw_t], writes=[dg_t[dgb]], part=(k > 0))
                if c < 10:
                    xo, xo_t = xa[b], xa_t[b]
                    xsl = lambda tb: xa[b][:, tb * 512:(tb + 1) * 512]
                else:
                    xo_t = CT_t[c - 10]
                    xsl = lambda tb, c=c: CT[:, c - 10, tb * 512:(tb + 1) * 512]
                for tb in range(4):
                    ca, ca_t = cacc[(4 * c + tb) % 2], cacc_t[(4 * c + tb) % 2]
                    for k in range(5):
                        sc.op("pe", lambda e, ca=ca, dgb=dgb, k=k, b=b, tb=tb: e.matmul(
                            ca[:], lhsT=dg[dgb][:, k, :], rhs=xc[b][:, k + tb * 512:k + tb * 512 + 512],
                            start=(k == 0), stop=(k == 4)),
                            reads=[dg_t[dgb], xc_t[b]], writes=[ca_t], part=(k > 0))
                    sc.op("act", lambda e, ca=ca, o_ap=xsl(tb), c=c: e.activation(
                        out=o_ap, in_=ca[:], func=AF.Silu, bias=cw[:, 60 + c:61 + c]),
                        reads=[ca_t, cw_t], writes=[xo_t], part=(tb > 0))
                if c in (8, 9):
                    sc.op("pool", lambda e, b=b, c=c: e.tensor_copy(out=BT[:, c - 8, :], in_=xa[b][:]),
                          reads=[xa_t[b]], writes=[BT_t[c - 8]])
                if c < 10:
                    for i0 in range(0, NT, 4):
                        tb_ = tpc % 2
                        tpc += 1
                        for j in range(4):
                            i = i0 + j
                            sc.op("pe", lambda e, tb_=tb_, j=j, b=b, i=i: e.transpose(
                                out=tp[tb_][:, j, :], in_=xa[b][:, i * 128:(i + 1) * 128], identity=ident[:]),
                                reads=[xa_t[b], G["ident_t"]], writes=[tp_t[tb_]], part=(j > 0))
                        eng = "act" if (tpc % 2) else "pool"
                        if eng == "act":
                            sc.op("act", lambda e, tb_=tb_, i0=i0, c=c: e.copy(
                                out=xtok[:, i0:i0 + 4, c * 128:(c + 1) * 128], in_=tp[tb_][:]),
                                reads=[tp_t[tb_]], writes=xtok_t[i0:i0 + 4], part=True)
                        else:
                            sc.op("dve", lambda e, tb_=tb_, i0=i0, c=c: e.tensor_copy(
                                out=xtok[:, i0:i0 + 4, c * 128:(c + 1) * 128], in_=tp[tb_][:]),
                                reads=[tp_t[tb_]], writes=xtok_t[i0:i0 + 4], part=True)
            sc.barrier(release=tl1)
        with contextlib.ExitStack() as s2:
            R = lambda name, shape, dt, n, psum=False: Ring(P, sc, s2, "B_" + name, shape, dt, n, psum)
            Sf = P.sb(s2, "B_Sf", [128, 2, 512], F32)
            Sf_t = sc.tiles_n("B_Sf", 2)
            Sbx = P.sb(s2, "B_Sb", [128, 2, 2, 512], BF16)
            r_Sb = [Ring(P, sc, s2, "B_Sb%d" % g, None, None, 2, views=[Sbx[:, g, k, :] for k in range(2)]) for g in range(2)]
            r_cb = R("cb", [128, 128], F32, 1, True)
            r_seg = R("seg", [128, 512], F32, 2, True)
            r_sm = R("sm", [128, 3, 16], F32, 1, True)
            r_yd = R("yd", [128, 512], F32, 1, True)
            r_stp = R("stp", [128, 512], F32, 1, True)
            r_yo = R("yo", [128, 512], F32, 1, True)
            r_tp = R("tp2", [128, 4, 128], BF16, 1, True)
            r_cbm = R("cbm", [128, 128], F32, 2)
            r_am = R("am", [128, 4, 128], F32, 4)
            r_dec = R("dec", [128, 4, 128], F32, 2)
            r_mt = R("mt", [128, 4, 128], BF16, 4)
            r_ea = R("ea", [128, 3, 16], F32, 2)
            r_xdt = R("xdt", [128, 1024], BF16, 1)
            r_xw = R("xw", [128, 1024], BF16, 1)
            r_t = R("t", [128, 512], F32, 2)
            r_ybl = R("ybl", [128, 1024], BF16, 2)
            r_yf = R("yf", [128, 1024], F32, 1)
            r_z = R("z", [128, 4, 1024], BF16, 1)
            r_jk = R("jk", [128, 512], F32, 1)
            r_ss = R("ss", [128, 4], F32, 2)
            r_y = R("y", [128, 1024], BF16, 2)
            yst = P.sb(s2, "B_yst", [128, 8, 256], BF16)
            yst_t = sc.tile("B_yst")
            rings = [r_cb, r_seg, r_sm, r_yd, r_stp, r_yo, r_tp, r_cbm, r_am, r_dec, r_mt, r_ea, r_xdt, r_xw, r_t, r_ybl,
                     r_yf, r_z, r_jk, r_ss, r_y] + r_Sb
            tl2 = Sf_t + [yst_t]
            for r in rings:
                tl2 += r.t

            def ssd_pass(d):
                fwd = (d == 0)
                tri_in, tri_in_t = (G["trif"], G["trif_t"]) if fwd else (G["trib"], G["trib_t"])
                tri_st, tri_st_t = (G["tribs"], G["tribs_t"]) if fwd else (G["trifs"], G["trifs_t"])
                cur = []
                for g in range(2):
                    sc.op("pool", lambda e, g=g: e.memset(Sf[:, g, :], 0.0), writes=[Sf_t[g]])
                    sb0, sb0_t = r_Sb[g].next()
                    sc.op("pool", lambda e, sb0=sb0: e.memset(sb0, 0.0), writes=[sb0_t])
                    cur.append((sb0, sb0_t))
                order = list(range(NT)) if fwd else list(range(NT - 1, -1, -1))
                zcur = [None]

                def tileA(i):
                    tsl = slice(i * 128, (i + 1) * 128)
                    acol = av[:, i, 16 * d:16 * d + 16]
                    ams = []
                    for u in range(4):
                        h0 = u * 4
                        am, am_t = r_am.next()
                        for hh in range(4):
                            sc.op("act", lambda e, am=am, i=i, h0=h0, hh=hh: e.activation(
                                out=am[:, hh, :], in_=tri_in[:], func=AF.Copy,
                                scale=av[:, i, 16 * d + h0 + hh:16 * d + h0 + hh + 1]),
                                reads=[tri_in_t, av_t], writes=[am_t], part=(hh > 0))
                        ams.append((am, am_t))
                    sm, sm_t = r_sm.next()
                    sc.op("pe", lambda e, sm=sm, acol=acol: e.matmul(sm[:, 0, :], lhsT=tri_in[:], rhs=acol, start=True, stop=True),
                          reads=[tri_in_t, av_t], writes=[sm_t])
                    sc.op("pe", lambda e, sm=sm, acol=acol: e.matmul(sm[:, 1, :], lhsT=tri_st[:], rhs=acol, start=True, stop=True),
                          reads=[tri_st_t, av_t], writes=[sm_t], part=True)
                    sc.op("pe", lambda e, sm=sm, acol=acol: e.matmul(sm[:, 2, :], lhsT=G["ones_f"][:], rhs=acol, start=True,
                                                                     stop=True),
                          reads=[G["ones_t"], av_t], writes=[sm_t], part=True)
                    ea, ea_t = r_ea.next()
                    sc.op("act", lambda e, ea=ea, sm=sm: e.activation(out=ea[:], in_=sm[:], func=AF.Exp), reads=[sm_t],
                          writes=[ea_t])
                    xdt, xdt_t = r_xdt.next()
                    sc.op("dve", lambda e, xdt=xdt, i=i: e.tensor_tensor(
                        out=xdt[:].rearrange("p (h q) -> p h q", q=64), in0=xtok[:, i, 0:1024].rearrange("p (h q) -> p h q", q=64),
                        in1=dtv[:, i, 16 * d:16 * d + 16].unsqueeze(2).to_broadcast([128, 16, 64]), op=ALU.mult),
                        reads=[xtok_t[i], dtv_t], writes=[xdt_t])
                    xw, xw_t = r_xw.next()
                    sc.op("dve", lambda e, xw=xw, xdt=xdt, ea=ea: e.tensor_tensor(
                        out=xw[:].rearrange("p (h q) -> p h q", q=64), in0=xdt[:].rearrange("p (h q) -> p h q", q=64),
                        in1=ea[:, 1, :].unsqueeze(2).to_broadcast([128, 16, 64]), op=ALU.mult),
                        reads=[xdt_t, ea_t], writes=[xw_t])
                    ybl, ybl_t = r_ybl.next()
                    if fwd:
                        sc.dma("sp", ybl[:], ybw[tsl, :], owner=ybl_t, reads=[ybw_t], writes=[ybl_t])
                        yf, yf_t = r_yf.next()
                    ts = []
                    for g in range(2):
                        stp, stp_t = r_stp.next()
                        sc.op("pe", lambda e, stp=stp, i=i, g=g, xw=xw: e.matmul(
                            stp[:], lhsT=xtok[:, i, 1024 + g * 128:1024 + (g + 1) * 128], rhs=xw[:, g * 512:(g + 1) * 512],
                            start=True, stop=True), reads=[xtok_t[i], xw_t], writes=[stp_t])
                        yo, yo_t = r_yo.next()
                        sbv, sbv_t = cur[g]
                        sc.op("pe", lambda e, yo=yo, g=g, tsl=tsl, sbv=sbv: e.matmul(yo[:], lhsT=CT[:, g, tsl], rhs=sbv,
                                                                                    start=True, stop=True),
                              reads=[CT_t[g], sbv_t], writes=[yo_t])
                        sc.op("pool", lambda e, g=g, ea=ea: e.tensor_tensor(
                            out=Sf[:, g, :].rearrange("p (h q) -> p h q", q=64), in0=Sf[:, g, :].rearrange("p (h q) -> p h q", q=64),
                            in1=ea[:, 2, g * 8:(g + 1) * 8].unsqueeze(2).to_broadcast([128, 8, 64]), op=ALU.mult),
                            reads=[Sf_t[g], ea_t], writes=[Sf_t[g]])
                        sc.op("dve", lambda e, g=g, stp=stp: e.tensor_tensor(out=Sf[:, g, :], in0=Sf[:, g, :], in1=stp[:],
                                                                            op=ALU.add),
                              reads=[Sf_t[g], stp_t], writes=[Sf_t[g]])
                        nb, nb_t = r_Sb[g].next()
                        sc.op("act", lambda e, nb=nb, g=g: e.copy(out=nb, in_=Sf[:, g, :]), reads=[Sf_t[g]], writes=[nb_t])
                        cur[g] = (nb, nb_t)
                        t, t_t = r_t.next()
                        sc.op("dve", lambda e, t=t, yo=yo, ea=ea, g=g: e.tensor_tensor(
                            out=t[:].rearrange("p (h q) -> p h q", q=64), in0=yo[:].rearrange("p (h q) -> p h q", q=64),
                            in1=ea[:, 0, g * 8:(g + 1) * 8].unsqueeze(2).to_broadcast([128, 8, 64]), op=ALU.mult),
                            reads=[yo_t, ea_t], writes=[t_t])
                        ts.append((t, t_t))
                    cbms = []
                    for g in range(2):
                        cb, cb_t = r_cb.next()
                        sc.op("pe", lambda e, cb=cb, g=g, tsl=tsl: e.matmul(cb[:], lhsT=BT[:, g, tsl], rhs=CT[:, g, tsl],
                                                                           start=True, stop=True),
                              reads=[BT_t[g], CT_t[g]], writes=[cb_t])
                        cbm, cbm_t = r_cbm.next()
                        sc.op("dve", lambda e, cbm=cbm, cb=cb: e.tensor_tensor(out=cbm[:], in0=cb[:], in1=tri_in[:], op=ALU.mult),
                              reads=[cb_t, tri_in_t], writes=[cbm_t])
                        cbms.append((cbm, cbm_t))
                    mts = []
                    for pair in range(2):
                        segs = []
                        for u in (2 * pair, 2 * pair + 1):
                            am, am_t = ams[u]
                            seg, seg_t = r_seg.next()
                            sc.op("pe", lambda e, seg=seg, am=am: e.matmul(seg[:], lhsT=tri_st[:],
                                                                           rhs=am[:].rearrange("p a b -> p (a b)"),
                                                                           start=True, stop=True),
                                  reads=[tri_st_t, am_t], writes=[seg_t])
                            segs.append((seg, seg_t))
                        decs = []
                        for (seg, seg_t) in segs:
                            dec, dec_t = r_dec.next()
                            sc.op("act", lambda e, dec=dec, seg=seg: e.activation(out=dec[:].rearrange("p a b -> p (a b)"),
                                                                                 in_=seg[:], func=AF.Exp),
                                  reads=[seg_t], writes=[dec_t])
                            decs.append((dec, dec_t))
                        for k, (dec, dec_t) in enumerate(decs):
                            u = 2 * pair + k
                            cbm, cbm_t = cbms[u // 2]
                            mt, mt_t = r_mt.next()
                            sc.op("dve", lambda e, mt=mt, dec=dec, cbm=cbm: e.tensor_tensor(
                                out=mt[:], in0=dec[:], in1=cbm[:].unsqueeze(1).to_broadcast([128, 4, 128]), op=ALU.mult),
                                reads=[dec_t, cbm_t], writes=[mt_t])
                            mts.append((mt, mt_t))
                    for g in range(2):
                        yd, yd_t = r_yd.next()
                        if fwd:
                            sc.op("pe", lambda e, yd=yd, ybl=ybl, g=g: e.matmul(
                                yd[:], lhsT=ident[:], rhs=ybl[:, g * 512:(g + 1) * 512], start=True, stop=False,
                                skip_group_check=True), reads=[G["ident_t"], ybl_t], writes=[yd_t])
                        for q4 in range(2):
                            mt, mt_t = mts[g * 2 + q4]
                            for hh in range(4):
                                h = g * 8 + q4 * 4 + hh
                                hl = h - g * 8
                                sc.op("pe", lambda e, yd=yd, mt=mt, hh=hh, hl=hl, h=h, xdt=xdt: e.matmul(
                                    yd[:, hl * 64:(hl + 1) * 64], lhsT=mt[:, hh, :], rhs=xdt[:, h * 64:(h + 1) * 64],
                                    start=(not fwd), stop=True, skip_group_check=True),
                                    reads=[mt_t, xdt_t], writes=[yd_t], part=(fwd or not (q4 == 0 and hh == 0)))
                        t, t_t = ts[g]
                        gs = slice(g * 512, (g + 1) * 512)
                        if not fwd:
                            sc.op("dve", lambda e, t=t, yd=yd, ybl=ybl, gs=gs: e.tensor_tensor(out=ybl[:, gs], in0=t[:], in1=yd[:],
                                                                                              op=ALU.add),
                                  reads=[t_t, yd_t], writes=[ybl_t], part=(g > 0))
                        else:
                            sc.op("dve", lambda e, t=t, yd=yd, yf=yf, gs=gs: e.tensor_tensor(out=yf[:, gs], in0=t[:], in1=yd[:],
                                                                                            op=ALU.add),
                                  reads=[t_t, yd_t], writes=[yf_t], part=(g > 0))
                    if not fwd:
                        sc.dma("pool", ybw[tsl, :], ybl[:], owner=ybl_t, reads=[ybl_t], writes=[ybw_t], part=True)
                        return None
                    if i % 4 == 0:
                        z, z_t = r_z.next()
                        sc.dma("sp", z[:], U["z"][i * 128:(i + 4) * 128, :].rearrange("(j p) c -> p j c", p=128), owner=z_t,
                               reads=[G["dram_t"]["z"]], writes=[z_t])
                        sc.op("act", lambda e, z=z: e.activation(out=z[:], in_=z[:], func=AF.Silu), reads=[z_t], writes=[z_t])
                        zcur[0] = (z, z_t)
                    z, z_t = zcur[0]
                    sz = z[:, i % 4, :]
                    sz_t = z_t
                    xd, xd_t = r_xdt.next()
                    sc.op("pool", lambda e, xd=xd, i=i: e.tensor_tensor(
                        out=xd[:].rearrange("p (h q) -> p h q", q=64), in0=xtok[:, i, 0:1024].rearrange("p (h q) -> p h q", q=64),
                        in1=rows[:, 2, 0:16].unsqueeze(2).to_broadcast([128, 16, 64]), op=ALU.mult),
                        reads=[xtok_t[i], rows_t], writes=[xd_t])
                    sc.op("dve", lambda e, yf=yf, xd=xd: e.tensor_tensor(out=yf[:], in0=yf[:], in1=xd[:], op=ALU.add),
                          reads=[yf_t, xd_t], writes=[yf_t])
                    sc.op("dve", lambda e, yf=yf, sz=sz: e.tensor_tensor(out=yf[:], in0=yf[:], in1=sz, op=ALU.mult),
                          reads=[yf_t, sz_t], writes=[yf_t])
                    ss, ss_t = r_ss.next()
                    for g in range(2):
                        gs = slice(g * 512, (g + 1) * 512)
                        jk, jk_t = r_jk.next()
                        sc.op("dve", lambda e, jk=jk, yf=yf, gs=gs, ss=ss, g=g: e.scalar_tensor_tensor(
                            out=jk[:], in0=yf[:, gs], scalar=1.0, in1=yf[:, gs], op0=ALU.mult, op1=ALU.mult,
                            accum_out=ss[:, g:g + 1]), reads=[yf_t], writes=[jk_t, ss_t])
                    sc.op("dve", lambda e, ss=ss: e.tensor_scalar(out=ss[:, 2:4], in0=ss[:, 0:2], scalar1=1.0 / 512.0, scalar2=EPS,
                                                                  op0=ALU.mult, op1=ALU.add), reads=[ss_t], writes=[ss_t])
                    sc.op("pool", lambda e, ss=ss: e.tensor_tensor(out=ss[:, 0:2], in0=ss[:, 2:4], in1=G["neghalf"][:, 0:2],
                                                                   op=ALU.pow), reads=[ss_t, G["neghalf_t"]], writes=[ss_t])
                    y, y_t = r_y.next()
                    for g in range(2):
                        gs = slice(g * 512, (g + 1) * 512)
                        sc.op("dve", lambda e, y=y, yf=yf, gs=gs, ss=ss, g=g: e.scalar_tensor_tensor(
                            out=y[:, gs], in0=yf[:, gs], scalar=ss[:, g:g + 1], in1=nwb[:, gs], op0=ALU.mult, op1=ALU.mult),
                            reads=[yf_t, ss_t, nwb_t], writes=[y_t], part=(g > 0))
                    return (y, y_t, i)

                def tileC(c3):
                    if c3 is None:
                        return
                    (y, y_t, i) = c3
                    emit_yT(P, sc, G, r_tp, y, y_t, yst, yst_t, i, YB["ssd"], G["dram_t"]["yb_ssd"], gsz=2)

                prev3 = None
                for i in order:
                    n3 = tileA(i)
                    tileC(prev3)
                    prev3 = n3
                tileC(prev3)

            ssd_pass(1)
            ssd_pass(0)
            sc.barrier(release=tl2)
        sc.barrier(release=tiles)


def emit_yT(P, sc, G, r_tp, y, y_t, yst, yst_t, i, dst, dst_t, gsz=4):
    ident = G["ident"]
    for half in range(2):
        tp, tp_t = r_tp.next()
        for jq in range(4):
            c = half * 4 + jq
            sc.op("pe", lambda e, tp=tp, jq=jq, c=c: e.transpose(out=tp[:, jq, :], in_=y[:, c * 128:(c + 1) * 128],
                                                               identity=ident[:]),
                  reads=[y_t, G["ident_t"]], writes=[tp_t], part=(jq > 0))
        sc.op("act", lambda e, tp=tp, half=half: e.copy(
            out=yst[:, half * 4:half * 4 + 4, (i % gsz) * 128:(i % gsz + 1) * 128], in_=tp[:]),
            reads=[tp_t], writes=[yst_t], part=not (i % gsz == 0 and half == 0))
    if i % gsz == gsz - 1:
        yv = dst.rearrange("(c p) t -> p c t", p=128)
        sc.dma("pool", yv[:, :, (i - gsz + 1) * 128:(i + 1) * 128], yst[:], owner=yst_t, reads=[yst_t], writes=[dst_t],
               part=True)


NEG = -30000.0


def na_r0(r):
    return min(max(r - 4, 0), 24)


def na_valid(kr, qr):
    return na_r0(qr) <= kr < na_r0(qr) + 8


def phase_D(P, sc, G, U, YB, prm, natt, l):
    nc = P.nc
    with contextlib.ExitStack() as ph:
        qnT = P.sb(ph, "D_qnT", [128, 8, S], BF16)
        knT = P.sb(ph, "D_knT", [128, 8, S], BF16)
        qn_t = sc.tiles_n("D_qn", 8)
        kn_t = sc.tiles_n("D_kn", 8)
        TT = P.sb(ph, "D_TT", [128, 8, 17, 64], BF16)
        TT_t = sc.tiles_n("D_TT", 4)
        wcol = P.sb(ph, "D_wcol", [128, 4], F32)
        wcol_t = sc.tile("D_wcol")
        tiles = qn_t + kn_t + TT_t + [wcol_t]
        with contextlib.ExitStack() as s1:
            TTf = [P.sb(s1, "D_TTf%d" % i, [128, 2, 17, 64], F32) for i in range(2)]
            TTf_t = sc.tiles_n("D_TTf", 2)
            qc_ = [P.sb(s1, "D_qc%d" % i, [128, S], BF16) for i in range(2)]
            qc_t = sc.tiles_n("D_qc", 2)
            sq = [P.sb(s1, "D_sq%d" % i, [128, S], BF16) for i in range(2)]
            sq_t = sc.tiles_n("D_sq", 2)
            lnv = [P.sb(s1, "D_ln%d" % i, [128, S], F32) for i in range(2)]
            lnv_t = sc.tiles_n("D_ln", 2)
            bones = P.sb(s1, "D_bones", [128, 128], BF16)
            bones_t = sc.tile("D_bones")
            ssp = [P.ps(s1, "D_ssp%d" % i, [128, S], F32) for i in range(2)]
            ssp_t = sc.tiles_n("D_ssp", 2)
            t1 = TTf_t + qc_t + sq_t + lnv_t + [bones_t] + ssp_t
            for g in range(4):
                b = g % 2
                sc.dma("sp", TTf[b][:], natt[l][:, 2 * g:2 * g + 2, :, :], owner=TTf_t[b], writes=[TTf_t[b]])
                sc.op("pool", lambda e, b=b, g=g: e.tensor_copy(out=TT[:, 2 * g:2 * g + 2, :, :], in_=TTf[b][:]),
                      reads=[TTf_t[b]], writes=[TT_t[g]])
            for hh in range(2):
                sc.dma("sp", wcol[hh * 64:(hh + 1) * 64, 2:3], prm["na_q_norm_w"][l].rearrange("(d o) -> d o", o=1),
                       owner=wcol_t, writes=[wcol_t], part=True)
                sc.dma("sp", wcol[hh * 64:(hh + 1) * 64, 1:2], prm["na_k_norm_w"][l].rearrange("(d o) -> d o", o=1),
                       owner=wcol_t, writes=[wcol_t], part=True)
            sc.op("dve", lambda e: e.tensor_scalar(out=wcol[:, 0:1], in0=wcol[:, 2:3], scalar1=0.125, scalar2=None,
                                                   op0=ALU.mult), reads=[wcol_t], writes=[wcol_t])
            sc.op("pool", lambda e: e.memset(bones[:], 0.0), writes=[bones_t])
            sc.op("pool", lambda e: e.memset(bones[0:64, 0:64], 1.0), reads=[bones_t], writes=[bones_t])
            sc.op("pool", lambda e: e.memset(bones[64:128, 64:128], 1.0), reads=[bones_t], writes=[bones_t])
            jobs = []
            for which, (src, dstT, dst_t, wc) in enumerate(((U["nq"], qnT, qn_t, 0), (U["nk"], knT, kn_t, 1))):
                src_t = G["dram_t"]["nq" if which == 0 else "nk"]
                for c in range(8):
                    jobs.append((src, src_t, dstT, dst_t, wc, c))

            def n_s1(k):
                (src, src_t, dstT, dst_t, wc, c) = jobs[k]
                cb = k % 2
                sc.dma("sp", qc_[cb][:], src[c * 128:(c + 1) * 128, :], owner=qc_t[cb], reads=[src_t], writes=[qc_t[cb]])
                sc.op("dve", lambda e, cb=cb: e.tensor_tensor(out=sq[cb][:], in0=qc_[cb][:], in1=qc_[cb][:], op=ALU.mult),
                      reads=[qc_t[cb]], writes=[sq_t[cb]])
                for tb in range(4):
                    sl = slice(tb * 512, (tb + 1) * 512)
                    sc.op("pe", lambda e, cb=cb, sl=sl: e.matmul(ssp[cb][:, sl], lhsT=bones[:], rhs=sq[cb][:, sl], start=True,
                                                                 stop=True),
                          reads=[bones_t, sq_t[cb]], writes=[ssp_t[cb]], part=(tb > 0))
                sc.op("act", lambda e, cb=cb: e.activation(out=lnv[cb][:], in_=ssp[cb][:], func=AF.Ln,
                                                           bias=G["eps"][:, 0:1], scale=1.0 / 64.0),
                      reads=[ssp_t[cb], G["eps_t"]], writes=[lnv_t[cb]])
                sc.op("act", lambda e, cb=cb: e.activation(out=lnv[cb][:], in_=lnv[cb][:], func=AF.Exp, scale=-0.5),
                      reads=[lnv_t[cb]], writes=[lnv_t[cb]])

            def n_s2(k):
                (src, src_t, dstT, dst_t, wc, c) = jobs[k]
                cb = k % 2
                sc.op("dve", lambda e, cb=cb, dstT=dstT, c=c, wc=wc: e.scalar_tensor_tensor(
                    out=dstT[:, c, :], in0=qc_[cb][:], scalar=wcol[:, wc:wc + 1], in1=lnv[cb][:],
                    op0=ALU.mult, op1=ALU.mult),
                    reads=[qc_t[cb], lnv_t[cb], wcol_t], writes=[dst_t[c]])

            n_s1(0)
            for k in range(len(jobs)):
                if k + 1 < len(jobs):
                    n_s1(k + 1)
                n_s2(k)
            sc.barrier(release=t1)
        with contextlib.ExitStack() as s2:
            vx = P.sb(s2, "D_vx", [128, NT, 16, 65], BF16)
            vx_t = sc.tiles_n("D_vx", NT)
            sps = [P.ps(s2, "D_sps%d" % i, [128, 8, 128], F32) for i in range(2)]
            sps_t = sc.tiles_n("D_sps", 2)
            pT = [P.sb(s2, "D_pT%d" % i, [128, 5, 128], BF16) for i in range(3)]
            pT_t = sc.tiles_n("D_pT", 3)
            po = [P.ps(s2, "D_po%d" % i, [128, 2, 66], F32) for i in range(2)]
            po_t = sc.tiles_n("D_po", 2)
            rc = [P.sb(s2, "D_rc%d" % i, [128, 2], F32) for i in range(2)]
            rc_t = sc.tiles_n("D_rc", 2)
            ot = [P.sb(s2, "D_ot%d" % i, [128, 1024], BF16) for i in range(2)]
            ot_t = sc.tiles_n("D_ot", 2)
            tp = [P.ps(s2, "D_tp%d" % i, [128, 4, 128], BF16) for i in range(2)]
            tp_t = sc.tiles_n("D_tp", 2)
            yst = P.sb(s2, "D_yst", [128, 8, 512], BF16)
            yst_t = sc.tile("D_yst")
            t2 = vx_t + sps_t + pT_t + po_t + rc_t + ot_t + tp_t + [yst_t]
            nvv = U["nv"].rearrange("(i p) (h d) -> p i h d", p=128, d=64)
            for i in range(NT):
                sc.op("pool", lambda e, i=i: e.memset(vx[:, i, :, 64:65], 1.0), writes=[vx_t[i]])
                sc.dma("sp", vx[:, i, :, 0:64], nvv[:, i, :, :], owner=vx_t[i], reads=[G["dram_t"]["nv"]],
                       writes=[vx_t[i]], part=True)
            ident = G["ident"]
            tpc = [0]
            units = []
            for i in range(NT):
                jlo = na_r0(2 * i) // 2
                jhi = (na_r0(2 * i + 1) + 7) // 2
                js = list(range(jlo, jhi + 1))
                for hp in range(8):
                    for hh in range(2):
                        units.append((i, hp, hh, js))

            def emit_S(u):
                i, hp, hh, js = units[u]
                h = 2 * hp + hh
                p0 = 64 * hh
                sb_ = u % 2
                for jj, j in enumerate(js):
                    sc.op("pe", lambda e, sb_=sb_, jj=jj, j=j, p0=p0, hp=hp, i=i: e.matmul(
                        sps[sb_][:, jj, :], lhsT=knT[p0:p0 + 64, hp, j * 128:(j + 1) * 128],
                        rhs=qnT[p0:p0 + 64, hp, i * 128:(i + 1) * 128], start=True, stop=False,
                        skip_group_check=True),
                        reads=[kn_t[hp], qn_t[hp]], writes=[sps_t[sb_]], part=(jj > 0))
                    mms = []
                    for b0 in range(2):
                        qr = 2 * i + b0
                        va = [na_valid(2 * j + a, qr) for a in range(2)]
                        dr0 = 2 * j - qr + 7
                        cs = slice(b0 * 64, (b0 + 1) * 64)
                        if va[0] and va[1]:
                            mms.append((slice(0, 128), cs, TT[p0:p0 + 64, hp, dr0:dr0 + 2, :]))
                        elif not va[0] and not va[1]:
                            mms.append((slice(0, 128), cs, TT[p0:p0 + 64, hp, 15:17, :]))
                        else:
                            d0 = dr0 if va[0] else 15
                            d1 = dr0 + 1 if va[1] else 16
                            mms.append((slice(0, 64), cs, TT[p0:p0 + 64, hp, d0, :]))
                            mms.append((slice(64, 128), cs, TT[p0:p0 + 64, hp, d1, :]))
                    for mi, (ps_, cs, lhs) in enumerate(mms):
                        sc.op("pe", lambda e, sb_=sb_, jj=jj, ps_=ps_, cs=cs, lhs=lhs, p0=p0, last=(mi == len(mms) - 1):
                              e.matmul(sps[sb_][ps_, jj, cs], lhsT=lhs, rhs=ident[p0:p0 + 64, p0:p0 + 64],
                                       start=False, stop=last, skip_group_check=True),
                              reads=[TT_t[hp // 2], G["ident_t"]], writes=[sps_t[sb_]], part=True)

            def emit_rest(u):
                i, hp, hh, js = units[u]
                h = 2 * hp + hh
                sb_ = u % 2
                pt = u % 3
                pb_ = (u // 2) % 2
                ob = i % 2
                n = len(js)
                n1 = min(n, 4)
                sc.op("act", lambda e, pt=pt, sb_=sb_, n1=n1: e.activation(out=pT[pt][:, 0:n1, :],
                                                                         in_=sps[sb_][:, 0:n1, :], func=AF.Exp),
                      reads=[sps_t[sb_]], writes=[pT_t[pt]])
                if n > 4:
                    sc.op("act", lambda e, pt=pt, sb_=sb_, n=n: e.activation(out=pT[pt][:, 4:n, :],
                                                                           in_=sps[sb_][:, 4:n, :], func=AF.Exp),
                          reads=[sps_t[sb_]], writes=[pT_t[pt]], part=True)
                for jj, j in enumerate(js):
                    sc.op("pe", lambda e, pb_=pb_, hh=hh, pt=pt, jj=jj, j=j, h=h, n=n: e.matmul(
                        po[pb_][:, hh, 0:65], lhsT=pT[pt][:, jj, :], rhs=vx[:, j, h, :],
                        start=(jj == 0), stop=(jj == n - 1)),
                        reads=[pT_t[pt], vx_t[j]], writes=[po_t[pb_]], part=(hh > 0 or jj > 0))
                if hh == 1:
                    sc.op("dve", lambda e, pb_=pb_: e.reciprocal(out=rc[pb_][:, 0:2], in_=po[pb_][:, :, 64]),
                          reads=[po_t[pb_]], writes=[rc_t[pb_]])
                    for h2 in range(2):
                        hx = 2 * hp + h2
                        sc.op("dve", lambda e, pb_=pb_, h2=h2, hx=hx, ob=ob: e.tensor_scalar(
                            out=ot[ob][:, hx * 64:(hx + 1) * 64], in0=po[pb_][:, h2, 0:64], scalar1=rc[pb_][:, h2:h2 + 1],
                            scalar2=None, op0=ALU.mult),
                            reads=[po_t[pb_], rc_t[pb_]], writes=[ot_t[ob]], part=(hx > 0))
                if hp == 7 and hh == 1:
                    for half in range(2):
                        tb_ = tpc[0] % 2
                        tpc[0] += 1
                        for jq in range(4):
                            c = half * 4 + jq
                            sc.op("pe", lambda e, tb_=tb_, jq=jq, c=c, ob=ob: e.transpose(
                                out=tp[tb_][:, jq, :], in_=ot[ob][:, c * 128:(c + 1) * 128], identity=ident[:]),
                                reads=[ot_t[ob], G["ident_t"]], writes=[tp_t[tb_]], part=(jq > 0))
                        sc.op("act", lambda e, tb_=tb_, half=half, i=i: e.copy(
                            out=yst[:, half * 4:half * 4 + 4, (i % 4) * 128:(i % 4 + 1) * 128], in_=tp[tb_][:]),
                            reads=[tp_t[tb_]], writes=[yst_t], part=not (i % 4 == 0 and half == 0))
                    if i % 4 == 3:
                        yv = YB["na"].rearrange("(c p) t -> p c t", p=128)
                        sc.dma("pool", yv[:, :, (i - 3) * 128:(i + 1) * 128], yst[:], owner=yst_t, reads=[yst_t],
                               writes=[G["dram_t"]["yb_na"]], part=True)

            emit_S(0)
            for u in range(len(units)):
                if u + 1 < len(units):
                    emit_S(u + 1)
                emit_rest(u)
            sc.barrier(release=t2)
        sc.barrier(release=tiles)


def phase_F(P, sc, G, prm, l):
    nc = P.nc
    x = G["x"]
    with contextlib.ExitStack() as ph:
        hT = P.sb(ph, "F_hT", [128, 8, S], BF16)
        hT_t = sc.tiles_n("F_hT", NT)
        tiles = list(hT_t)
        tiles += rms_transpose(P, sc, G, ph, prm["norm_mlp_w"][l], hT, hT_t, l, "F")
        wst = WStream(P, sc, ph, "F", 1, 4096, nf=2, nb=3)
        fT = [P.sb(ph, "F_fT%d" % i, [128, 4, S], BF16) for i in range(2)]
        fT_t = [sc.tiles_n("F_fT%d_" % i, 4) for i in range(2)]
        rl = [P.sb(ph, "F_rl%d" % i, [128, 512], F32) for i in range(2)]
        rl_t = sc.tiles_n("F_rl", 2)
        acc = [P.ps(ph, "F_acc%d" % i, [128, 512], F32) for i in range(4)]
        acc_t = sc.tiles_n("F_acc", 4)
        tiles += wst.tiles + fT_t[0] + fT_t[1] + rl_t + acc_t
        w1v = prm["w_ff1"][l].rearrange("(kc p) n -> p kc n", p=128)
        w2v = prm["w_ff2"][l].rearrange("(c p) n -> p c n", p=128)
        items = []
        for g in range(8):
            items.append((w1v[:, :, g * 512:(g + 1) * 512], 8, 512))
            items.append((w2v[:, g * 4:(g + 1) * 4, :], 4, 1024))
        wst.items = items
        wst_views = {}

        def view(slot, k, n):
            return slot[:, 0, :].rearrange("p (k n) -> p k n", k=k)
        def _load(g):
            if g >= len(items):
                return
            ap, k, n = items[g]
            fs = g % wst.nf
            sc.dma("sp", view(wst.f[fs], k, n), ap, owner=wst.f_t[fs], writes=[wst.f_t[fs]])

        def _cast(g):
            if g >= len(items):
                return
            fs, bs = g % wst.nf, g % wst.nb
            sc.op("pool", lambda e: e.tensor_copy(out=wst.b[bs][:, 0, :], in_=wst.f[fs][:, 0, :]),
                  reads=[wst.f_t[fs]], writes=[wst.b_t[bs]])
        wst._load = _load
        wst._cast = _cast
        _load(0)
        _load(1)
        _cast(0)
        ai = 0
        ri = 0
        for g in range(8):
            fb = g % 2
            w1s, w1_t = wst.get(2 * g)
            w1b = view(w1s, 8, 512)
            for c in range(4):
                for tb in range(4):
                    a = ai % 4
                    ai += 1
                    for kc in range(8):
                        sc.op("pe", lambda e, a=a, kc=kc, w1b=w1b, c=c, tb=tb: e.matmul(
                            acc[a][:], lhsT=w1b[:, kc, c * 128:(c + 1) * 128],
                            rhs=hT[:, kc, tb * 512:(tb + 1) * 512], start=(kc == 0), stop=(kc == 7)),
                            reads=[w1_t] + hT_t[tb * 4:tb * 4 + 4], writes=[acc_t[a]], part=(kc > 0))
                    r = ri % 2
                    ri += 1
                    sc.op("act", lambda e, r=r, a=a: e.activation(out=rl[r][:], in_=acc[a][:], func=AF.Relu),
                          reads=[acc_t[a]], writes=[rl_t[r]])
                    sc.op("pool", lambda e, r=r, fb=fb, c=c, tb=tb: e.tensor_tensor(
                        out=fT[fb][:, c, tb * 512:(tb + 1) * 512], in0=rl[r][:], in1=rl[r][:], op=ALU.mult),
                        reads=[rl_t[r]], writes=[fT_t[fb][c]], part=(tb > 0))
            w2s, w2_t = wst.get(2 * g + 1)
            w2b = view(w2s, 4, 1024)
            for i in range(NT):
                for hh in range(2):
                    a = ai % 4
                    ai += 1
                    for c in range(4):
                        sc.op("pe", lambda e, a=a, c=c, w2b=w2b, i=i, hh=hh, fb=fb: e.matmul(
                            acc[a][:], lhsT=fT[fb][:, c, i * 128:(i + 1) * 128],
                            rhs=w2b[:, c, hh * 512:(hh + 1) * 512], start=(c == 0), stop=(c == 3)),
                            reads=[w2_t, fT_t[fb][c]], writes=[acc_t[a]], part=(c > 0))
                    xs = x[:, i, hh * 512:(hh + 1) * 512]
                    sc.op("dve", lambda e, xs=xs, a=a: e.tensor_tensor(out=xs, in0=xs, in1=acc[a][:], op=ALU.add),
                          reads=[acc_t[a], G["xt"][i]], writes=[G["xt"][i]])
        sc.barrier(release=tiles)


_NC_CACHE = {}


def make_na_tt(rpb):
    rpb = np.asarray(rpb, dtype=np.float32)
    L = rpb.shape[0]
    out = np.full((L, 128, 8, 17, 64), NEG, dtype=np.float32)
    qc = np.arange(64)
    ws = np.clip(qc - 8, 0, 48)
    for q in range(64):
        kc = np.arange(ws[q], ws[q] + 16)
        idx = kc - q + 15
        for hh in range(2):
            out[:, hh * 64 + q, :, 0:15, ws[q]:ws[q] + 16] = rpb[:, hh::2][:, :, :, idx]
    return out


def kernel(**inputs):
    cfg = {}
    key = "full"
    if key not in _NC_CACHE:
        _NC_CACHE[key] = build(cfg)
    nc = _NC_CACHE[key]
    x = np.ascontiguousarray(inputs["x"], dtype=np.float32)
    base = {n: np.ascontiguousarray(inputs[n], dtype=np.float32) for n in PARAM_NAMES}
    base["na_tt"] = make_na_tt(inputs["na_rpb"])
    in_maps = []
    for c in range(8):
        m = dict(base)
        m["x"] = x[c]
        in_maps.append(m)
    res = run_bass_kernel_spmd(nc, in_maps, core_ids=list(range(8)))
    return np.stack([r["y"] for r in res.results], axis=0).astype(np.float32)
```

```python
import contextlib
import numpy as np
import concourse.bass as bass
import concourse.mybir as mybir
from concourse.bass_utils import run_bass_kernel_spmd

F32 = mybir.dt.float32
BF16 = mybir.dt.bfloat16
ALU = mybir.AluOpType
AF = mybir.ActivationFunctionType
AX = mybir.AxisListType

D = 1024
S = 2048
NT = S // 128
DEPTH = 2
N_IN = 11840
EPS = 1e-6


class TT:
    __slots__ = ("name", "lw", "rd", "dsems", "gen")

    def __init__(self, name):
        self.name = name
        self.lw = {}
        self.rd = {}
        self.gen = {}
        self.dsems = {}


class Sched:
    ENG = ("pe", "act", "dve", "pool", "sp")
    BLK = {"pe": "tensor", "act": "scalar", "dve": "vector", "pool": "gpsimd", "sp": "sync"}

    def __init__(self, nc, stack):
        self.nc = nc
        self.stack = stack
        self.ops = {e: [] for e in self.ENG}
        self.seen = {e: {} for e in self.ENG}
        self.esem = {e: stack.enter_context(nc.semaphore("es_" + e)) for e in self.ENG if e != "sp"}
        self.tiles = []
        self.free_dsems = {"sp": [], "pool": [], "act": []}
        self.nsem = 4
        self.skip_same = {"pe"}

    def tile(self, name):
        t = TT(name)
        self.tiles.append(t)
        return t

    def tiles_n(self, name, n):
        return [self.tile("%s%d" % (name, i)) for i in range(n)]

    def _collect(self, reads, writes, part):
        evs = {}

        def add(d):
            for k, v in d.items():
                if k not in evs or evs[k][0] < v[0]:
                    evs[k] = v
        for t in reads:
            add(t.lw)
        for t in writes:
            if part and not t.rd:
                add(t.gen)
                continue
            g = dict(t.rd)
            for k, v in t.lw.items():
                if k not in g or g[k][0] < v[0]:
                    g[k] = v
            t.gen = g
            add(g)
        return evs

    def _waits(self, eng, evs):
        waits = []
        for k, (val, obj) in evs.items():
            if k == ("E", eng) and eng in self.skip_same:
                continue
            if self.seen[eng].get(k, 0) >= val:
                continue
            self.seen[eng][k] = val
            waits.append((k, val, obj))
            if k[0] == "E":
                self.ops[k[1]][val - 1]["inc"] = True
        return waits

    def _update(self, ev_key, ev_val, reads, writes, part):
        for t in reads:
            t.rd[ev_key] = ev_val
        for t in writes:
            if part and not t.rd:
                t.lw[ev_key] = ev_val
            else:
                t.lw = {ev_key: ev_val}
                t.rd = {}

    def op(self, eng, fn, reads=(), writes=(), part=False):
        waits = self._waits(eng, self._collect(reads, writes, part))
        self.ops[eng].append({"fn": fn, "waits": waits, "inc": False, "dma": None})
        idx = len(self.ops[eng])
        self._update(("E", eng), (idx, None), reads, writes, part)

    def dma(self, q, out, in_, owner, reads=(), writes=(), part=False, **kw):
        waits = self._waits(q, self._collect(reads, writes, part))
        rec = owner.dsems.get(q)
        if rec is None:
            if self.free_dsems[q]:
                rec = self.free_dsems[q].pop()
            else:
                rec = [self.stack.enter_context(self.nc.semaphore("ds%d" % self.nsem)), 0, self.nsem]
                self.nsem += 1
            owner.dsems[q] = rec
        rec[1] += 16
        self.ops[q].append({"fn": (lambda e: e.dma_start(out=out, in_=in_, **kw)), "waits": waits,
                            "inc": False, "dma": rec[0]})
        self._update(("D", rec[2]), (rec[1], rec[0]), reads, writes, part)

    def barrier(self, release=()):
        evs = {}
        for e in self.ENG:
            if e == "sp":
                continue
            idx = len(self.ops[e])
            while idx > 0 and (self.ops[e][idx - 1]["dma"] is not None or self.ops[e][idx - 1].get("nop")):
                idx -= 1
            if idx > 0:
                evs[("E", e)] = (idx, None)
        for t in self.tiles:
            for d in (t.lw, t.rd):
                for k, v in d.items():
                    if k[0] == "D" and (k not in evs or evs[k][0] < v[0]):
                        evs[k] = v
        for e in self.ENG:
            sk = self.skip_same
            self.skip_same = set()
            w = self._waits(e, dict(evs))
            self.skip_same = sk
            self.ops[e].append({"fn": (lambda en: en.nop()), "waits": w, "inc": False, "dma": None, "nop": True})
        for t in self.tiles:
            t.lw = {}
            t.rd = {}
            t.gen = {}
        rel = set(id(t) for t in release)
        for t in release:
            for q, rec in t.dsems.items():
                self.free_dsems[q].append(rec)
            t.dsems = {}
        self.tiles = [t for t in self.tiles if id(t) not in rel]

    def emit(self):
        nc = self.nc
        mile = {}
        for e in self.ENG:
            c = 0
            m = []
            for o in self.ops[e]:
                if o["inc"]:
                    c += 1
                m.append(c)
            mile[e] = m
            assert c < 60000, (e, c)
        with nc.Block() as block:
            for e in self.ENG:
                def body(engine, e=e):
                    for o in self.ops[e]:
                        for (k, val, obj) in o["waits"]:
                            if k[0] == "E":
                                engine.wait_ge(self.esem[k[1]], mile[k[1]][val - 1])
                            else:
                                engine.wait_ge(obj, val)
                        ins = o["fn"](engine)
                        if o["dma"] is not None:
                            ins.then_inc(o["dma"], 16)
                        elif o["inc"]:
                            ins.then_inc(self.esem[e], 1)
                getattr(block, self.BLK[e])(body)


class Prog:
    def __init__(self, cfg):
        self.cfg = cfg
        self.nc = bass.Bass("TRN2", target_bir_lowering=False)
        self.dbg = cfg.get("debug", ())

    def dram(self, name, shape, dt, kind="Internal"):
        if name in self.dbg:
            kind = "ExternalOutput"
        if name in self.cfg.get("ext_in", ()):
            kind = "ExternalInput"
        return self.nc.dram_tensor(name, list(shape), dt, kind=kind).ap()

    def sb(self, stack, name, shape, dt):
        self.uid = getattr(self, "uid", 0) + 1
        return stack.enter_context(self.nc.sbuf_tensor("%s_u%d" % (name, self.uid), list(shape), dt))

    def ps(self, stack, name, shape, dt):
        self.uid = getattr(self, "uid", 0) + 1
        return stack.enter_context(self.nc.psum_tensor("%s_u%d" % (name, self.uid), list(shape), dt))


IN_SIZES = (1024, 1536, 16, 16, 512, 512, 1024, 1024, 16, 16, 1024, 1024, 1024, 3072)
IN_OFF = [0]
for _s in IN_SIZES:
    IN_OFF.append(IN_OFF[-1] + _s)
(O_Z, O_XBC, O_DTF, O_DTB, O_GQ, O_GK, O_GV, O_GG, O_GAF, O_GAB, O_NQ, O_NK, O_NV, O_GATE, _) = IN_OFF

PARAM_NAMES = ["norm_mix_w", "w_in", "ssd_conv_w", "ssd_conv_b", "ssd_dt_bias_f", "ssd_dt_bias_b",
               "ssd_a_log_f", "ssd_a_log_b", "ssd_d", "ssd_norm_w", "gla_a2_f", "gla_a2_bias_f",
               "gla_a2_b", "gla_a2_bias_b", "gla_norm_w", "na_q_norm_w", "na_k_norm_w", "na_rpb",
               "w_branch_ssd", "w_branch_gla", "w_branch_na", "w_out", "norm_mlp_w", "w_ff1", "w_ff2"]
PARAM_SHAPES = {
    "norm_mix_w": (2, 1024), "w_in": (2, 1024, 11840), "ssd_conv_w": (2, 5, 1536), "ssd_conv_b": (2, 1536),
    "ssd_dt_bias_f": (2, 16), "ssd_dt_bias_b": (2, 16), "ssd_a_log_f": (2, 16), "ssd_a_log_b": (2, 16),
    "ssd_d": (2, 16), "ssd_norm_w": (2, 1024), "gla_a2_f": (2, 16, 512), "gla_a2_bias_f": (2, 512),
    "gla_a2_b": (2, 16, 512), "gla_a2_bias_b": (2, 512), "gla_norm_w": (2, 256), "na_q_norm_w": (2, 64),
    "na_k_norm_w": (2, 64), "na_rpb": (2, 16, 15, 31), "w_branch_ssd": (2, 1024, 1024),
    "w_branch_gla": (2, 1024, 1024), "w_branch_na": (2, 1024, 1024), "w_out": (2, 1024, 1024),
    "norm_mlp_w": (2, 1024), "w_ff1": (2, 1024, 4096), "w_ff2": (2, 4096, 1024),
}


def build(cfg):
    P = Prog(cfg)
    nc = P.nc
    layers = cfg.get("layers", DEPTH)
    phases = cfg.get("phases", "ABCDEF")
    x_in = nc.dram_tensor("x", [S, D], F32, kind="ExternalInput").ap()
    prm = {n: nc.dram_tensor(n, list(PARAM_SHAPES[n]), F32, kind="ExternalInput").ap() for n in PARAM_NAMES}
    y_out = nc.dram_tensor("y", [S, D], F32, kind="ExternalOutput").ap()
    natt = nc.dram_tensor("na_tt", [DEPTH, 128, 8, 17, 64], F32, kind="ExternalInput").ap()

    U = {}
    for nm, w in (("z", 1024), ("gv", 1024), ("gg", 1024), ("nv", 1024)):
        U[nm] = P.dram("u_" + nm, [S, w], BF16)
    U["dt"] = P.dram("u_dt", [S, 32], F32)
    for nm, w in (("xbc", 1536), ("gq", 512), ("gk", 512), ("nq", 1024), ("nk", 1024), ("gate", 3072)):
        U[nm] = P.dram("u_" + nm + "T", [w, S], BF16)
    U["ga"] = P.dram("u_gaT", [32, S], BF16)
    YB = {nm: P.dram("yb_" + nm, [1024, S], BF16) for nm in ("ssd", "gla", "na")}
    ybw = P.dram("ybw", [S, 1024], BF16)

    with contextlib.ExitStack() as top:
        sc = Sched(nc, top)
        G = {}
        G["x"] = P.sb(top, "x_res", [128, NT, D], F32)
        G["xt"] = sc.tiles_n("x", NT)
        G["ident"] = P.sb(top, "ident", [128, 128], BF16)
        G["ident_t"] = sc.tile("ident")
        G["dram_t"] = {k: sc.tile("d_" + k) for k in list(U) + ["yb_ssd", "yb_gla", "yb_na", "ybw"]}
        G["ybw"] = ybw

        ones_f = P.sb(top, "ones_f", [128, 128], F32)
        ones_t = sc.tile("ones_f")
        sc.op("pool", lambda e: e.memset(ones_f[:], 1.0), writes=[ones_t])
        sc.op("pool", lambda e: e.affine_select(out=G["ident"][:], in_=ones_f[:], pattern=[[-1, 128]],
                                                compare_op=ALU.is_equal, fill=0.0, base=0,
                                                channel_multiplier=1),
              reads=[ones_t], writes=[G["ident_t"]])
        G["ones_f"] = ones_f
        G["eps"] = P.sb(top, "epsc", [128, 2], F32)
        G["eps_t"] = sc.tile("epsc")
        sc.op("pool", lambda e: e.memset(G["eps"][:], EPS), writes=[G["eps_t"]])
        G["one"] = P.sb(top, "onec", [128, 2], F32)
        G["one_t"] = sc.tile("onec")
        sc.op("pool", lambda e: e.memset(G["one"][:], 1.0), writes=[G["one_t"]])
        G["neghalf"] = P.sb(top, "neghalf", [128, 16], F32)
        G["neghalf_t"] = sc.tile("neghalf")
        sc.op("pool", lambda e: e.memset(G["neghalf"][:], -0.5), writes=[G["neghalf_t"]])
        G["ones_t"] = ones_t

        build_tri(P, sc, G, top)
        xv = x_in.rearrange("(i p) d -> p i d", p=128)
        for i in range(NT):
            sc.dma("sp", G["x"][:, i, :], xv[:, i, :], owner=G["xt"][i], writes=[G["xt"][i]])

        for l in range(layers):
            if "A" in phases:
                phase_A(P, sc, G, U, prm, l)
            if "B" in phases:
                phase_B(P, sc, G, U, YB, prm, l)
            if "C" in phases:
                phase_C(P, sc, G, U, YB, prm, l)
            if "D" in phases:
                phase_D(P, sc, G, U, YB, prm, natt, l)
            if "E" in phases:
                phase_E(P, sc, G, U, YB, prm, l)
            if "F" in phases:
                phase_F(P, sc, G, prm, l)

        yv = y_out.rearrange("(i p) d -> p i d", p=128)
        outt = sc.tile("yout")
        for i in range(NT):
            sc.dma("sp", yv[:, i, :], G["x"][:, i, :], owner=G["xt"][i], reads=[G["xt"][i]], writes=[outt],
                   part=True)
        sc.op("sp", lambda e: e.nop(), reads=[outt])
        sc.barrier()
        sc.emit()
    return nc


def rms_transpose(P, sc, G, ph, wrow_ap, hT, hT_t, l, tag):
    nc = P.nc
    wb = P.sb(ph, tag + "_wb", [128, D], F32)
    wb_t = sc.tile(tag + "_wb")
    sc.dma("sp", wb[:], wrow_ap.partition_broadcast(128), owner=wb_t, writes=[wb_t])
    junk = [P.sb(ph, tag + "_junk%d" % i, [128, D], BF16) for i in range(2)]
    junk_t = sc.tiles_n(tag + "_junk", 2)
    hb = [P.sb(ph, tag + "_hb%d" % i, [128, D], BF16) for i in range(2)]
    hb_t = sc.tiles_n(tag + "_hb", 2)
    ss = [P.sb(ph, tag + "_ss%d" % i, [128, 2], F32) for i in range(2)]
    ss_t = sc.tiles_n(tag + "_ss", 2)
    tp = [P.ps(ph, tag + "_tp%d" % i, [128, 4, 128], BF16) for i in range(2)]
    tp_t = sc.tiles_n(tag + "_tp", 2)
    x = G["x"]
    new_tiles = [wb_t] + junk_t + hb_t + ss_t + tp_t
    def _s1(i):
        b = i % 2
        xt = G["xt"][i]
        sc.op("dve", lambda e, i=i, b=b: e.scalar_tensor_tensor(out=junk[b][:], in0=x[:, i, :], scalar=1.0,
                                                                in1=x[:, i, :], op0=ALU.mult, op1=ALU.mult,
                                                                accum_out=ss[b][:, 0:1]),
              reads=[xt], writes=[junk_t[b], ss_t[b]])
        sc.op("dve", lambda e, b=b: e.tensor_scalar(out=ss[b][:, 1:2], in0=ss[b][:, 0:1], scalar1=1.0 / D,
                                                    scalar2=EPS, op0=ALU.mult, op1=ALU.add),
              reads=[ss_t[b]], writes=[ss_t[b]])
        sc.op("pool", lambda e, b=b: e.tensor_tensor(out=ss[b][:, 0:1], in0=ss[b][:, 1:2],
                                                     in1=G["neghalf"][:, 0:1], op=ALU.pow),
              reads=[ss_t[b], G["neghalf_t"]], writes=[ss_t[b]])

    def _s2(i):
        b = i % 2
        xt = G["xt"][i]
        sc.op("dve", lambda e, i=i, b=b: e.scalar_tensor_tensor(out=hb[b][:], in0=x[:, i, :],
                                                                scalar=ss[b][:, 0:1], in1=wb[:],
                                                                op0=ALU.mult, op1=ALU.mult),
              reads=[xt, ss_t[b], wb_t], writes=[hb_t[b]])
        for half in range(2):
            pb = (2 * i + half) % 2
            for j in range(4):
                kc = half * 4 + j
                sc.op("pe", lambda e, b=b, pb=pb, j=j, kc=kc: e.transpose(
                    out=tp[pb][:, j, :], in_=hb[b][:, kc * 128:(kc + 1) * 128], identity=G["ident"][:]),
                    reads=[hb_t[b], G["ident_t"]], writes=[tp_t[pb]], part=(j > 0))
            eng = "act"
            if eng == "act":
                sc.op("act", lambda e, pb=pb, half=half, i=i: e.copy(
                    out=hT[:, half * 4:half * 4 + 4, i * 128:(i + 1) * 128], in_=tp[pb][:]),
                    reads=[tp_t[pb]], writes=[hT_t[i]], part=True)
            else:
                sc.op("dve", lambda e, pb=pb, half=half, i=i: e.tensor_copy(
                    out=hT[:, half * 4:half * 4 + 4, i * 128:(i + 1) * 128], in_=tp[pb][:]),
                    reads=[tp_t[pb]], writes=[hT_t[i]], part=True)

    _s1(0)
    for i in range(NT):
        if i + 1 < NT:
            _s1(i + 1)
        _s2(i)
    return new_tiles


class WStream:
    def __init__(self, P, sc, ph, tag, kdim, ncol, nf=2, nb=3, cast_eng="pool"):
        self.sc = sc
        self.cast_eng = cast_eng
        self.kdim, self.ncol = kdim, ncol
        self.nf, self.nb = nf, nb
        self.f = [P.sb(ph, "%s_wf%d" % (tag, i), [128, kdim, ncol], F32) for i in range(nf)]
        self.f_t = sc.tiles_n(tag + "_wf", nf)
        self.b = [P.sb(ph, "%s_wb%d" % (tag, i), [128, kdim, ncol], BF16) for i in range(nb)]
        self.b_t = sc.tiles_n(tag + "_wbt", nb)
        self.tiles = self.f_t + self.b_t
        self.items = []

    def start(self, items):
        self.items = items
        self._load(0)
        self._load(1)
        self._cast(0)

    def _load(self, g):
        if g >= len(self.items):
            return
        ap, k, n = self.items[g]
        fs = g % self.nf
        self.sc.dma("sp", self.f[fs][:, 0:k, 0:n], ap, owner=self.f_t[fs], writes=[self.f_t[fs]])

    def _cast(self, g):
        if g >= len(self.items):
            return
        ap, k, n = self.items[g]
        fs, bs = g % self.nf, g % self.nb
        if self.cast_eng == "act":
            self.sc.op("act", lambda e: e.copy(out=self.b[bs][:, 0:k, 0:n], in_=self.f[fs][:, 0:k, 0:n]),
                       reads=[self.f_t[fs]], writes=[self.b_t[bs]])
        else:
            self.sc.op("pool", lambda e: e.tensor_copy(out=self.b[bs][:, 0:k, 0:n], in_=self.f[fs][:, 0:k, 0:n]),
                       reads=[self.f_t[fs]], writes=[self.b_t[bs]])

    def get(self, g):
        self._cast(g + 1)
        self._load(g + 2)
        return self.b[g % self.nb], self.b_t[g % self.nb]


def proj_groups():
    g = []

    def seg(off, n, mode, key):
        c = 0
        while c < n:
            w = min(512, n - c)
            g.append((off + c, w, mode, key, c))
            c += w
    seg(O_Z, 1024, "tok", "z")
    seg(O_XBC, 1536, "feat", "xbc")
    g.append((O_DTF, 32, "tok32", "dt", 0))
    seg(O_GQ, 512, "feat", "gq")
    seg(O_GK, 512, "feat", "gk")
    seg(O_GV, 1024, "tok", "gv")
    seg(O_GG, 1024, "tok", "gg")
    g.append((O_GAF, 32, "feat32", "ga", 0))
    seg(O_NQ, 1024, "feat", "nq")
    seg(O_NK, 1024, "feat", "nk")
    seg(O_NV, 1024, "tok", "nv")
    seg(O_GATE, 3072, "feat", "gate")
    return g


def phase_A(P, sc, G, U, prm, l):
    nc = P.nc
    with contextlib.ExitStack() as ph:
        hT = P.sb(ph, "A_hT", [128, 8, S], BF16)
        hT_t = sc.tiles_n("A_hT", NT)
        tiles = list(hT_t)
        tiles += rms_transpose(P, sc, G, ph, prm["norm_mix_w"][l], hT, hT_t, l, "A")
        wst = WStream(P, sc, ph, "A", 8, 512)
        acc = [P.ps(ph, "A_acc%d" % i, [128, 512], F32) for i in range(4)]
        acc_t = sc.tiles_n("A_acc", 4)
        NS = 3
        stg = [P.sb(ph, "A_stg%d" % i, [128, 2048], BF16) for i in range(NS)]
        stg_t = sc.tiles_n("A_stg", NS)
        stf = [P.sb(ph, "A_stf%d" % i, [128, 4, 32], F32) for i in range(2)]
        stf_t = sc.tiles_n("A_stf", 2)
        G["ga_stage"] = P.sb(ph, "A_gast", [32, 2048], BF16)
        G["ga_stage_t"] = sc.tile("A_gast")
        tiles += wst.tiles + acc_t + stg_t + stf_t + [G["ga_stage_t"]]
        wv = prm["w_in"][l].rearrange("(kc p) n -> p kc n", p=128)
        groups = proj_groups()
        wst.start([(wv[:, :, c0:c0 + n], 8, n) for (c0, n, _m, _k, _d) in groups])
        ai = 0
        si = 0
        ev = 0
        for gi, (c0, n, mode, key, doff) in enumerate(groups):
            wcur, wcur_t = wst.get(gi)
            dst = U[key]
            dst_t = G["dram_t"][key]
            if mode in ("tok", "tok32"):
                for tb in range(4):
                    if mode == "tok":
                        st = si % NS
                        si += 1
                    else:
                        st = tb % 2
                    for j in range(4):
                        i = tb * 4 + j
                        a = ai % 4
                        ai += 1
                        for kc in range(8):
                            sc.op("pe", lambda e, a=a, kc=kc, i=i, wcur=wcur, n=n: e.matmul(
                                acc[a][:, 0:n], lhsT=hT[:, kc, i * 128:(i + 1) * 128], rhs=wcur[:, kc, 0:n],
                                start=(kc == 0), stop=(kc == 7)),
                                reads=[hT_t[i], wcur_t], writes=[acc_t[a]], part=(kc > 0))
                        if mode == "tok":
                            o_ap = stg[st][:, j * 512:j * 512 + n]
                            o_t = stg_t[st]
                        else:
                            o_ap = stf[st][:, j, 0:n]
                            o_t = stf_t[st]
                        ev += 1
                        if ev % 2 == 0:
                            sc.op("act", lambda e, o_ap=o_ap, a=a, n=n: e.copy(out=o_ap, in_=acc[a][:, 0:n]),
                                  reads=[acc_t[a]], writes=[o_t], part=(j > 0))
                        else:
                            sc.op("dve", lambda e, o_ap=o_ap, a=a, n=n: e.tensor_copy(out=o_ap, in_=acc[a][:, 0:n]),
                                  reads=[acc_t[a]], writes=[o_t], part=(j > 0))
                    rows = dst[tb * 512:(tb + 1) * 512, doff:doff + n].rearrange("(j p) c -> p j c", p=128)
                    if mode == "tok":
                        src = stg[st][:].rearrange("p (j c) -> p j c", j=4)[:, :, 0:n]
                        sc.dma("pool", rows, src, owner=stg_t[st], reads=[stg_t[st]], writes=[dst_t], part=True)
                    else:
                        sc.dma("pool", rows, stf[st][:, :, 0:n], owner=stf_t[st], reads=[stf_t[st]], writes=[dst_t],
                               part=True)
            else:
                nchunk = (n + 127) // 128
                for c in range(nchunk):
                    m = min(128, n - c * 128)
                    if mode == "feat":
                        st = si % NS
                        si += 1
                    else:
                        st = 0
                    for tb in range(4):
                        a = ai % 4
                        ai += 1
                        for kc in range(8):
                            sc.op("pe", lambda e, a=a, kc=kc, tb=tb, wcur=wcur, c=c, m=m: e.matmul(
                                acc[a][0:m, :], lhsT=wcur[:, kc, c * 128:c * 128 + m],
                                rhs=hT[:, kc, tb * 512:(tb + 1) * 512], start=(kc == 0), stop=(kc == 7)),
                                reads=hT_t[tb * 4:tb * 4 + 4] + [wcur_t], writes=[acc_t[a]], part=(kc > 0))
                        ev += 1
                        if mode == "feat":
                            o_ap = stg[st][0:m, tb * 512:(tb + 1) * 512]
                            o_t = stg_t[st]
                            if ev % 2 == 0:
                                sc.op("act", lambda e, o_ap=o_ap, a=a, m=m: e.copy(out=o_ap, in_=acc[a][0:m, :]),
                                      reads=[acc_t[a]], writes=[o_t], part=(tb > 0))
                            else:
                                sc.op("dve", lambda e, o_ap=o_ap, a=a, m=m: e.tensor_copy(out=o_ap, in_=acc[a][0:m, :]),
                                      reads=[acc_t[a]], writes=[o_t], part=(tb > 0))
                        else:
                            sc.op("dve", lambda e, a=a, m=m, tb=tb, gast=G["ga_stage"]: e.tensor_copy(
                                out=gast[0:m, tb * 512:(tb + 1) * 512], in_=acc[a][0:m, :]),
                                reads=[acc_t[a]], writes=[G["ga_stage_t"]], part=(tb > 0))
                    if mode == "feat":
                        sc.dma("pool", dst[doff + c * 128:doff + c * 128 + m, :], stg[st][0:m, :], owner=stg_t[st],
                               reads=[stg_t[st]], writes=[dst_t], part=True)
                    else:
                        sc.dma("pool", dst[0:m, :], G["ga_stage"][0:m, :], owner=G["ga_stage_t"],
                               reads=[G["ga_stage_t"]], writes=[dst_t], part=True)
        sc.barrier(release=tiles)


def phase_E(P, sc, G, U, YB, prm, l):
    nc = P.nc
    x = G["x"]
    with contextlib.ExitStack() as ph:
        wst = WStream(P, sc, ph, "E", 8, 256, nf=2, nb=2, cast_eng="act")
        mix = P.sb(ph, "E_mix", [128, 8, 1024], F32)
        mix_t = sc.tiles_n("E_mix", 8)
        mixb = P.sb(ph, "E_mixb", [128, 8, 1024], BF16)
        mixb_t = sc.tile("E_mixb")
        ybT = [P.sb(ph, "E_yb%d" % i, [128, 8, 1024], BF16) for i in range(2)]
        ybT_t = sc.tiles_n("E_yb", 2)
        gsl = [P.sb(ph, "E_g%d" % i, [128, 1024], BF16) for i in range(3)]
        gsl_t = sc.tiles_n("E_g", 3)
        sig = [P.sb(ph, "E_sig%d" % i, [128, 1024], F32) for i in range(2)]
        sig_t = sc.tiles_n("E_sig", 2)
        tmp = [P.sb(ph, "E_tmp%d" % i, [128, 512], F32) for i in range(2)]
        tmp_t = sc.tiles_n("E_tmp", 2)
        acc = [P.ps(ph, "E_acc%d" % i, [128, 512], F32) for i in range(4)]
        acc_t = sc.tiles_n("E_acc", 4)
        tiles = wst.tiles + mix_t + [mixb_t] + ybT_t + gsl_t + sig_t + tmp_t + acc_t
        wnames = ["w_branch_ssd", "w_branch_gla", "w_branch_na", "w_out"]
        bnames = ["ssd", "gla", "na"]
        items = []
        for half in range(2):
            for wn in wnames:
                wv = prm[wn][l].rearrange("(kc p) n -> p kc n", p=128)
                for cg in range(4):
                    items.append((wv[:, :, cg * 256:(cg + 1) * 256], 8, 256))
        wst.start(items)
        gi = 0
        ai = 0
        gcount = 0
        tcount = 0
        ybcount = 0
        for half in range(2):
            t0 = half * 1024
            for b in range(3):
                ys = ybcount % 2
                ybcount += 1
                ybv = YB[bnames[b]].rearrange("(kc p) t -> p kc t", p=128)
                sc.dma("sp", ybT[ys][:], ybv[:, :, t0:t0 + 1024], owner=ybT_t[ys],
                       reads=[G["dram_t"]["yb_" + bnames[b]]], writes=[ybT_t[ys]])
                for cg in range(4):
                    wcur, wcur_t = wst.get(gi)
                    gi += 1
                    for ecl in range(2):
                        ec = cg * 2 + ecl
                        gs = gcount % 3
                        ss_ = gcount % 2
                        gcount += 1
                        grow = b * 1024 + ec * 128
                        sc.dma("sp", gsl[gs][:], U["gate"][grow:grow + 128, t0:t0 + 1024], owner=gsl_t[gs],
                               reads=[G["dram_t"]["gate"]], writes=[gsl_t[gs]])
                        sc.op("act", lambda e, gs=gs, ss_=ss_: e.activation(out=sig[ss_][:], in_=gsl[gs][:],
                                                                            func=AF.Sigmoid),
                              reads=[gsl_t[gs]], writes=[sig_t[ss_]])
                        for tbh in range(2):
                            a = ai % 4
                            ai += 1
                            for kc in range(8):
                                sc.op("pe", lambda e, a=a, kc=kc, wcur=wcur, ecl=ecl, ys=ys, tbh=tbh: e.matmul(
                                    acc[a][:], lhsT=wcur[:, kc, ecl * 128:(ecl + 1) * 128],
                                    rhs=ybT[ys][:, kc, tbh * 512:(tbh + 1) * 512], start=(kc == 0), stop=(kc == 7)),
                                    reads=[wcur_t, ybT_t[ys]], writes=[acc_t[a]], part=(kc > 0))
                            msl = mix[:, ec, tbh * 512:(tbh + 1) * 512]
                            sgl = sig[ss_][:, tbh * 512:(tbh + 1) * 512]
                            if b == 0:
                                sc.op("dve", lambda e, msl=msl, a=a, sgl=sgl: e.tensor_tensor(
                                    out=msl, in0=acc[a][:], in1=sgl, op=ALU.mult),
                                    reads=[acc_t[a], sig_t[ss_]], writes=[mix_t[ec]], part=(tbh > 0))
                            else:
                                ts = tcount % 2
                                tcount += 1
                                sc.op("dve", lambda e, ts=ts, a=a, sgl=sgl: e.tensor_tensor(
                                    out=tmp[ts][:], in0=acc[a][:], in1=sgl, op=ALU.mult),
                                    reads=[acc_t[a], sig_t[ss_]], writes=[tmp_t[ts]])
                                if b == 1:
                                    sc.op("pool", lambda e, msl=msl, ts=ts: e.tensor_tensor(
                                        out=msl, in0=msl, in1=tmp[ts][:], op=ALU.add),
                                        reads=[tmp_t[ts], mix_t[ec]], writes=[mix_t[ec]])
                                else:
                                    sc.op("pool", lambda e, msl=msl, ts=ts, ec=ec, tbh=tbh: e.tensor_tensor(
                                        out=mixb[:, ec, tbh * 512:(tbh + 1) * 512], in0=msl, in1=tmp[ts][:],
                                        op=ALU.add),
                                        reads=[tmp_t[ts], mix_t[ec]], writes=[mixb_t], part=True)
            for cg in range(4):
                wcur, wcur_t = wst.get(gi)
                gi += 1
                for j in range(8):
                    i = half * 8 + j
                    a = ai % 4
                    ai += 1
                    for ec in range(8):
                        sc.op("pe", lambda e, a=a, ec=ec, wcur=wcur, j=j: e.matmul(
                            acc[a][:, 0:256], lhsT=mixb[:, ec, j * 128:(j + 1) * 128], rhs=wcur[:, ec, :],
                            start=(ec == 0), stop=(ec == 7)),
                            reads=[wcur_t, mixb_t], writes=[acc_t[a]], part=(ec > 0))
                    xs = x[:, i, cg * 256:(cg + 1) * 256]
                    sc.op("dve", lambda e, xs=xs, a=a: e.tensor_tensor(out=xs, in0=xs, in1=acc[a][:, 0:256], op=ALU.add),
                          reads=[acc_t[a], G["xt"][i]], writes=[G["xt"][i]])
        sc.barrier(release=tiles)


class Ring:
    def __init__(self, P, sc, stack, name, shape, dt, n, psum=False, views=None):
        if views is not None:
            self.h = views
            n = len(views)
        else:
            mk = P.ps if psum else P.sb
            self.h = [mk(stack, "%s%d" % (name, i), shape, dt) for i in range(n)]
        self.t = sc.tiles_n(name + "_", n)
        self.i = 0
        self.n = n

    def next(self):
        k = self.i % self.n
        self.i += 1
        return self.h[k], self.t[k]


def build_tri(P, sc, G, top):
    for nm in ("trif", "trib", "trif64", "trib64", "mcf64", "mcb64", "trifs", "tribs"):
        G[nm] = P.sb(top, nm, [128, 128], F32)
        G[nm + "_t"] = sc.tile(nm)
    ones_f, ones_t = G["ones_f"], G["ones_t"]
    sc.op("pool", lambda e: e.affine_select(out=G["trif"][:], in_=ones_f[:], pattern=[[1, 128]], compare_op=ALU.is_ge,
                                            fill=0.0, base=0, channel_multiplier=-1),
          reads=[ones_t], writes=[G["trif_t"]])
    sc.op("pool", lambda e: e.affine_select(out=G["trib"][:], in_=ones_f[:], pattern=[[-1, 128]], compare_op=ALU.is_ge,
                                            fill=0.0, base=0, channel_multiplier=1),
          reads=[ones_t], writes=[G["trib_t"]])
    sc.op("pool", lambda e: e.affine_select(out=G["trifs"][:], in_=ones_f[:], pattern=[[1, 128]], compare_op=ALU.is_gt,
                                            fill=0.0, base=0, channel_multiplier=-1),
          reads=[ones_t], writes=[G["trifs_t"]])
    sc.op("pool", lambda e: e.affine_select(out=G["tribs"][:], in_=ones_f[:], pattern=[[-1, 128]], compare_op=ALU.is_gt,
                                            fill=0.0, base=0, channel_multiplier=1),
          reads=[ones_t], writes=[G["tribs_t"]])
    sc.op("pool", lambda e: e.tensor_copy(out=G["trif64"][:], in_=G["trif"][:]), reads=[G["trif_t"]], writes=[G["trif64_t"]])
    sc.op("pool", lambda e: e.memset(G["trif64"][0:64, 64:128], 0.0), reads=[G["trif64_t"]], writes=[G["trif64_t"]])
    sc.op("pool", lambda e: e.tensor_copy(out=G["trib64"][:], in_=G["trib"][:]), reads=[G["trib_t"]], writes=[G["trib64_t"]])
    sc.op("pool", lambda e: e.memset(G["trib64"][64:128, 0:64], 0.0), reads=[G["trib64_t"]], writes=[G["trib64_t"]])
    sc.op("pool", lambda e: e.tensor_scalar(out=G["mcf64"][:], in0=G["trif64"][:], scalar1=-1.0 / 16.0, scalar2=None,
                                            op0=ALU.mult), reads=[G["trif64_t"]], writes=[G["mcf64_t"]])
    sc.op("pool", lambda e: e.tensor_scalar(out=G["mcb64"][:], in0=G["trib64"][:], scalar1=-1.0 / 16.0, scalar2=None,
                                            op0=ALU.mult), reads=[G["trib64_t"]], writes=[G["mcb64_t"]])
    for nm in ("mcf64", "mcb64", "trifs", "tribs"):
        G[nm + "b"] = P.sb(top, nm + "b", [128, 128], BF16)
        G[nm + "b_t"] = sc.tile(nm + "b")
        sc.op("pool", lambda e, nm=nm: e.tensor_copy(out=G[nm + "b"][:], in_=G[nm][:]), reads=[G[nm + "_t"]],
              writes=[G[nm + "b_t"]])


def phase_C(P, sc, G, U, YB, prm, l):
    nc = P.nc
    ident = G["ident"]
    with contextlib.ExitStack() as ph:
        qT = P.sb(ph, "C_qT", [128, 4, S], BF16)
        kT = P.sb(ph, "C_kT", [128, 4, S], BF16)
        qT_t = sc.tile("C_qT")
        kT_t = sc.tile("C_kT")
        ob = P.sb(ph, "C_ob", [128, NT, 1024], BF16)
        ob_t = sc.tiles_n("C_ob", NT)
        gaX = P.sb(ph, "C_gaX", [32, S], BF16)
        gaX_t = sc.tile("C_gaX")
        a2X = [P.sb(ph, "C_a2X%d" % d, [32, 512], BF16) for d in range(2)]
        a2X_t = sc.tiles_n("C_a2X", 2)
        a2f = P.sb(ph, "C_a2f", [32, 512], F32)
        a2f_t = sc.tile("C_a2f")
        nwb = P.sb(ph, "C_nwb", [128, 256], F32)
        nwb_t = sc.tile("C_nwb")
        Sf = P.sb(ph, "C_Sf", [128, 4, 256], F32)
        Sf_t = sc.tile("C_Sf")
        yst = P.sb(ph, "C_yst", [128, 8, 256], BF16)
        yst_t = sc.tile("C_yst")
        R = lambda name, shape, dt, n, psum=False: Ring(P, sc, ph, "C_" + name, shape, dt, n, psum)
        r_Sb = R("Sb", [128, 4, 256], BF16, 3)
        r_v = R("v", [128, 1024], BF16, 2)
        r_gg = R("gg", [128, 4, 1024], BF16, 1)
        r_e1 = R("e1", [128, 512], F32, 1)
        r_gn = R("gn", [128, 512], BF16, 1)
        r_bs = R("bs", [128, 4, 128], F32, 1)
        r_eb = R("eb", [128, 4, 128], F32, 1)
        r_enb = R("enb", [128, 4, 128], F32, 1)
        r_ew = R("ew", [128, 4, 128], F32, 1)
        r_ed = R("ed", [128, 4, 2], F32, 3)
        r_qd = R("qd", [128, 4, 128], BF16, 2)
        r_kd = R("kd", [128, 4, 128], BF16, 2)
        r_kw = R("kw", [128, 4, 128], BF16, 1)
        r_kwt = R("kwt", [128, 4, 128], BF16, 2)
        r_am = R("am", [128, 4, 128], BF16, 2)
        r_oa = R("oa", [128, 1024], F32, 1)
        r_sg = R("sg", [128, 4, 1024], BF16, 1)
        r_jk = R("jk", [128, 256], BF16, 1)
        r_ss = R("ss", [128, 8], F32, 2)
        r_y = R("y", [128, 1024], BF16, 2)
        r_gp = R("gp", [128, 512], F32, 1, True)
        r_bT = R("bT", [128, 4, 128], F32, 1, True)
        r_att = R("att", [128, 4, 128], F32, 1, True)
        r_kwp = R("kwp", [128, 4, 128], BF16, 1, True)
        r_st = R("st", [128, 4, 256], F32, 1, True)
        r_o = R("o", [128, 4, 256], F32, 1, True)
        rings = [r_Sb, r_v, r_gg, r_e1, r_gn, r_bs, r_eb, r_enb, r_ew, r_ed, r_qd, r_kd, r_kw, r_kwt, r_am, r_oa, r_sg,
                 r_jk, r_ss, r_y, r_gp, r_bT, r_att, r_kwp, r_st, r_o]
        tiles = [qT_t, kT_t, nwb_t, yst_t, gaX_t, Sf_t, a2f_t] + ob_t + a2X_t
        for r in rings:
            tiles += r.t
        sc.dma("sp", qT[:], U["gq"].rearrange("(h p) t -> p h t", p=128), owner=qT_t, reads=[G["dram_t"]["gq"]],
               writes=[qT_t])
        sc.dma("sp", kT[:], U["gk"].rearrange("(h p) t -> p h t", p=128), owner=kT_t, reads=[G["dram_t"]["gk"]],
               writes=[kT_t])
        sc.dma("sp", nwb[:], prm["gla_norm_w"][l].partition_broadcast(128), owner=nwb_t, writes=[nwb_t])
        for d in range(2):
            a2 = prm["gla_a2_f" if d == 0 else "gla_a2_b"][l]
            bi = prm["gla_a2_bias_f" if d == 0 else "gla_a2_bias_b"][l]
            sc.dma("sp", a2f[0:16, :], a2, owner=a2f_t, writes=[a2f_t])
            sc.dma("sp", a2f[16:17, :], bi.rearrange("(o n) -> o n", o=1), owner=a2f_t, writes=[a2f_t], part=True)
            sc.op("act", lambda e, d=d: e.copy(out=a2X[d][0:17, :], in_=a2f[0:17, :]), reads=[a2f_t], writes=[a2X_t[d]])

        def gla_pass(d):
            fwd = (d == 0)
            mc, mc_t = (G["mcf64b"], G["mcf64b_t"]) if fwd else (G["mcb64b"], G["mcb64b_t"])
            ma, ma_t = (G["trif64"], G["trif64_t"]) if fwd else (G["trib64"], G["trib64_t"])
            lc0 = 63 if fwd else 0
            sc.op("pool", lambda e: e.memset(gaX[:], 1.0), writes=[gaX_t])
            sc.dma("sp", gaX[0:16, :], U["ga"][16 * d:16 * d + 16, :], owner=gaX_t, reads=[G["dram_t"]["ga"]],
                   writes=[gaX_t])
            sc.op("pool", lambda e: e.memset(Sf[:], 0.0), writes=[Sf_t])
            sb0, sb0_t = r_Sb.next()
            sc.op("pool", lambda e, sb0=sb0: e.memset(sb0[:], 0.0), writes=[sb0_t])
            cur = [(sb0, sb0_t)]
            sgcur = [None]
            order = list(range(NT)) if fwd else list(range(NT - 1, -1, -1))
            chunks = (0, 1) if fwd else (1, 0)

            def stage1(i):
                tsl = slice(i * 128, (i + 1) * 128)
                v, v_t = r_v.next()
                sc.dma("sp", v[:], U["gv"][tsl, :], owner=v_t, reads=[G["dram_t"]["gv"]], writes=[v_t])
                gp, gp_t = r_gp.next()
                sc.op("pe", lambda e, gp=gp, tsl=tsl: e.matmul(gp[:], lhsT=gaX[0:17, tsl], rhs=a2X[d][0:17, :],
                                                               start=True, stop=True),
                      reads=[gaX_t, a2X_t[d]], writes=[gp_t])
                e1, e1_t = r_e1.next()
                sc.op("act", lambda e, e1=e1, gp=gp: e.activation(out=e1[:], in_=gp[:], func=AF.Exp, scale=-1.0),
                      reads=[gp_t], writes=[e1_t])
                gn, gn_t = r_gn.next()
                sc.op("act", lambda e, gn=gn, e1=e1: e.activation(out=gn[:], in_=e1[:], func=AF.Ln, bias=G["one"][:, 0:1]),
                      reads=[e1_t, G["one_t"]], writes=[gn_t])
                bT, bT_t = r_bT.next()
                for h in range(4):
                    sc.op("pe", lambda e, bT=bT, gn=gn, h=h: e.matmul(bT[:, h, :], lhsT=gn[:, h * 128:(h + 1) * 128], rhs=mc[:],
                                                                     start=True, stop=True, skip_group_check=True),
                          reads=[gn_t, mc_t], writes=[bT_t], part=(h > 0))
                bs, bs_t = r_bs.next()
                sc.op("act", lambda e, bs=bs, bT=bT: e.copy(out=bs[:], in_=bT[:]), reads=[bT_t], writes=[bs_t])
                eb, eb_t = r_eb.next()
                sc.op("act", lambda e, eb=eb, bs=bs: e.activation(out=eb[:], in_=bs[:], func=AF.Exp), reads=[bs_t], writes=[eb_t])
                enb, enb_t = r_enb.next()
                sc.op("act", lambda e, enb=enb, bs=bs: e.activation(out=enb[:], in_=bs[:], func=AF.Exp, scale=-1.0),
                      reads=[bs_t], writes=[enb_t])
                ed, ed_t = r_ed.next()
                sc.op("act", lambda e, ed=ed, bs=bs: e.activation(
                    out=ed[:], in_=bs[:].rearrange("p h (c l) -> p h c l", c=2)[:, :, :, lc0], func=AF.Exp),
                    reads=[bs_t], writes=[ed_t])
                qd, qd_t = r_qd.next()
                sc.op("dve", lambda e, qd=qd, tsl=tsl, eb=eb: e.scalar_tensor_tensor(
                    out=qd[:], in0=qT[:, :, tsl], scalar=128.0 ** -0.5, in1=eb[:], op0=ALU.mult, op1=ALU.mult),
                    reads=[qT_t, eb_t], writes=[qd_t])
                kd, kd_t = r_kd.next()
                sc.op("dve", lambda e, kd=kd, tsl=tsl, enb=enb: e.tensor_tensor(
                    out=kd[:], in0=kT[:, :, tsl], in1=enb[:], op=ALU.mult), reads=[kT_t, enb_t], writes=[kd_t])
                ew, ew_t = r_ew.next()
                sc.op("dve", lambda e, ew=ew, enb=enb, ed=ed: e.tensor_tensor(
                    out=ew[:].rearrange("p h (c l) -> p (h c) l", c=2), in0=enb[:].rearrange("p h (c l) -> p (h c) l", c=2),
                    in1=ed[:].rearrange("p h c -> p (h c)").unsqueeze(2).to_broadcast([128, 8, 64]), op=ALU.mult),
                    reads=[enb_t, ed_t], writes=[ew_t])
                kw, kw_t = r_kw.next()
                sc.op("dve", lambda e, kw=kw, tsl=tsl, ew=ew: e.tensor_tensor(
                    out=kw[:], in0=kT[:, :, tsl], in1=ew[:], op=ALU.mult), reads=[kT_t, ew_t], writes=[kw_t])
                kwp, kwp_t = r_kwp.next()
                for h in range(4):
                    sc.op("pe", lambda e, kwp=kwp, kw=kw, h=h: e.transpose(out=kwp[:, h, :], in_=kw[:, h, :], identity=ident[:]),
                          reads=[kw_t, G["ident_t"]], writes=[kwp_t], part=(h > 0))
                kwt, kwt_t = r_kwt.next()
                sc.op("act", lambda e, kwt=kwt, kwp=kwp: e.copy(out=kwt[:], in_=kwp[:]), reads=[kwp_t], writes=[kwt_t])
                att, att_t = r_att.next()
                for h in range(4):
                    sc.op("pe", lambda e, att=att, kd=kd, qd=qd, h=h: e.matmul(att[:, h, :], lhsT=kd[:, h, :], rhs=qd[:, h, :],
                                                                            start=True, stop=True, skip_group_check=True),
                          reads=[kd_t, qd_t], writes=[att_t], part=(h > 0))
                am, am_t = r_am.next()
                sc.op("dve", lambda e, am=am, att=att: e.tensor_tensor(
                    out=am[:], in0=att[:], in1=ma[:].unsqueeze(1).to_broadcast([128, 4, 128]), op=ALU.mult),
                    reads=[att_t, ma_t], writes=[am_t])
                return (i, tsl, v, v_t, qd, qd_t, kwt, kwt_t, ed, ed_t, am, am_t)

            def stage23(ctx):
                (i, tsl, v, v_t, qd, qd_t, kwt, kwt_t, ed, ed_t, am, am_t) = ctx
                sbs = [cur[0]]
                for ci, c in enumerate(chunks):
                    cs = slice(c * 64, (c + 1) * 64)
                    st, st_t = r_st.next()
                    for h in range(4):
                        sc.op("pe", lambda e, st=st, kwt=kwt, cs=cs, v=v, h=h: e.matmul(
                            st[:, h, :], lhsT=kwt[cs, h, :], rhs=v[cs, h * 256:(h + 1) * 256], start=True, stop=True,
                            skip_group_check=True),
                            reads=[kwt_t, v_t], writes=[st_t], part=(h > 0))
                    for h in range(4):
                        sc.op("dve", lambda e, st=st, h=h, ed=ed, c=c: e.scalar_tensor_tensor(
                            out=Sf[:, h, :], in0=Sf[:, h, :], scalar=ed[:, h, c:c + 1], in1=st[:, h, :], op0=ALU.mult,
                            op1=ALU.add),
                            reads=[st_t, ed_t, Sf_t], writes=[Sf_t])
                    nb, nb_t = r_Sb.next()
                    sc.op("act", lambda e, nb=nb: e.copy(out=nb[:], in_=Sf[:]), reads=[Sf_t], writes=[nb_t])
                    sbs.append((nb, nb_t))
                o, o_t = r_o.next()
                for h in range(4):
                    sc.op("pe", lambda e, o=o, am=am, v=v, h=h: e.matmul(o[:, h, :], lhsT=am[:, h, :],
                                                                       rhs=v[:, h * 256:(h + 1) * 256],
                                                                       start=True, stop=False, skip_group_check=True),
                          reads=[am_t, v_t], writes=[o_t], part=(h > 0))
                    for ci, c in enumerate(chunks):
                        cs = slice(c * 64, (c + 1) * 64)
                        sbv, sbv_t = sbs[ci]
                        sc.op("pe", lambda e, o=o, qd=qd, cs=cs, sbv=sbv, ci=ci, h=h: e.matmul(
                            o[cs, h, :], lhsT=qd[:, h, cs], rhs=sbv[:, h, :], start=False, stop=(ci == 1),
                            skip_group_check=True),
                            reads=[qd_t, sbv_t], writes=[o_t], part=True)
                cur[0] = sbs[2]
                if not fwd:
                    for hb in range(2):
                        sc.op("act", lambda e, o=o, hb=hb, i=i: e.copy(
                            out=ob[:, i, hb * 512:(hb + 1) * 512], in_=o[:, 2 * hb:2 * hb + 2, :].rearrange("p a b -> p (a b)")),
                            reads=[o_t], writes=[ob_t[i]], part=(hb > 0))
                    return
                oa, oa_t = r_oa.next()
                ss, ss_t = r_ss.next()
                for hb in range(2):
                    sc.op("dve", lambda e, oa=oa, o=o, hb=hb, i=i: e.tensor_tensor(
                        out=oa[:, hb * 512:(hb + 1) * 512], in0=o[:, 2 * hb:2 * hb + 2, :].rearrange("p a b -> p (a b)"),
                        in1=ob[:, i, hb * 512:(hb + 1) * 512], op=ALU.add),
                        reads=[o_t, ob_t[i]], writes=[oa_t], part=(hb > 0))
                for h in range(4):
                    hs = slice(h * 256, (h + 1) * 256)
                    jk, jk_t = r_jk.next()
                    sc.op("dve", lambda e, jk=jk, oa=oa, hs=hs, ss=ss, h=h: e.scalar_tensor_tensor(
                        out=jk[:], in0=oa[:, hs], scalar=1.0, in1=oa[:, hs], op0=ALU.mult, op1=ALU.mult,
                        accum_out=ss[:, h:h + 1]), reads=[oa_t], writes=[jk_t, ss_t])
                if i % 4 == 0:
                    gg, gg_t = r_gg.next()
                    sc.dma("sp", gg[:], U["gg"][i * 128:(i + 4) * 128, :].rearrange("(j p) c -> p j c", p=128), owner=gg_t,
                           reads=[G["dram_t"]["gg"]], writes=[gg_t])
                    sg, sg_t = r_sg.next()
                    sc.op("act", lambda e, sg=sg, gg=gg: e.activation(out=sg[:], in_=gg[:], func=AF.Silu),
                          reads=[gg_t], writes=[sg_t])
                    sgcur[0] = (sg, sg_t)
                sg, sg_t = sgcur[0]
                sgn = sg[:, i % 4, :]
                sgn_t = sg_t
                sc.op("pool", lambda e, sgn=sgn: e.tensor_tensor(
                    out=sgn.rearrange("p (h v) -> p h v", h=4), in0=sgn.rearrange("p (h v) -> p h v", h=4),
                    in1=nwb[:].unsqueeze(1).to_broadcast([128, 4, 256]), op=ALU.mult),
                    reads=[sg_t, nwb_t], writes=[sg_t])
                sc.op("dve", lambda e, ss=ss: e.tensor_scalar(out=ss[:, 4:8], in0=ss[:, 0:4], scalar1=1.0 / 256.0, scalar2=EPS,
                                                              op0=ALU.mult, op1=ALU.add), reads=[ss_t], writes=[ss_t])
                sc.op("pool", lambda e, ss=ss: e.tensor_tensor(out=ss[:, 0:4], in0=ss[:, 4:8], in1=G["neghalf"][:, 0:4],
                                                               op=ALU.pow), reads=[ss_t, G["neghalf_t"]], writes=[ss_t])
                sc.op("dve", lambda e, oa=oa, ss=ss: e.tensor_tensor(
                    out=oa[:].rearrange("p (h v) -> p h v", h=4), in0=oa[:].rearrange("p (h v) -> p h v", h=4),
                    in1=ss[:, 0:4].unsqueeze(2).to_broadcast([128, 4, 256]), op=ALU.mult),
                    reads=[oa_t, ss_t], writes=[oa_t])
                y, y_t = r_y.next()
                sc.op("dve", lambda e, y=y, oa=oa, sgn=sgn: e.tensor_tensor(out=y[:], in0=oa[:], in1=sgn, op=ALU.mult),
                      reads=[oa_t, sgn_t], writes=[y_t])
                return (y, y_t, i)

            def stage3(c3):
                if c3 is None:
                    return
                (y, y_t, i) = c3
                emit_yT(P, sc, G, r_kwp, y, y_t, yst, yst_t, i, YB["gla"], G["dram_t"]["yb_gla"], gsz=2)

            prev = None
            prev3 = None
            for i in order:
                ctx = stage1(i)
                if prev is not None:
                    n3 = stage23(prev)
                    stage3(prev3)
                    prev3 = n3
                prev = ctx
            n3 = stage23(prev)
            stage3(prev3)
            stage3(n3)

        gla_pass(1)
        if "dbg_ob" in P.dbg:
            dob = P.dram("dbg_ob", [S, 1024], BF16)
            dt_ = sc.tile("dbg_ob")
            sc.dma("sp", dob.rearrange("(i p) c -> p i c", p=128), ob[:], owner=ob_t[0], reads=ob_t, writes=[dt_])
        gla_pass(0)
        sc.barrier(release=tiles)


def phase_B(P, sc, G, U, YB, prm, l):
    nc = P.nc
    ident = G["ident"]
    ybw = G["ybw"]
    ybw_t = G["dram_t"]["ybw"]
    with contextlib.ExitStack() as ph:
        xtok = P.sb(ph, "B_xtok", [128, NT, 1280], BF16)
        xtok_t = sc.tiles_n("B_xtok", NT)
        BT = P.sb(ph, "B_BT", [128, 2, S], BF16)
        CT = P.sb(ph, "B_CT", [128, 2, S], BF16)
        BT_t = sc.tiles_n("B_BT", 2)
        CT_t = sc.tiles_n("B_CT", 2)
        dtv = P.sb(ph, "B_dtv", [128, NT, 32], F32)
        av = P.sb(ph, "B_av", [128, NT, 32], F32)
        dtv_t = sc.tile("B_dtv")
        av_t = sc.tile("B_av")
        rows = P.sb(ph, "B_rows", [128, 4, 32], F32)
        rows_t = sc.tile("B_rows")
        nwb = P.sb(ph, "B_nwb", [128, 1024], F32)
        nwb_t = sc.tile("B_nwb")
        tiles = xtok_t + BT_t + CT_t + [dtv_t, av_t, rows_t, nwb_t]
        with contextlib.ExitStack() as s1:
            cwr = P.sb(s1, "B_cwr", [72, 128], F32)
            cwr_t = sc.tile("B_cwr")
            cw = P.sb(s1, "B_cw", [128, 72], F32)
            cw_t = sc.tile("B_cw")
            cwp = P.ps(s1, "B_cwp", [128, 72], F32)
            cwp_t = sc.tile("B_cwp")
            identf = P.sb(s1, "B_identf", [128, 128], F32)
            identf_t = sc.tile("B_identf")
            xc = [P.sb(s1, "B_xc%d" % i, [128, S + 4], BF16) for i in range(2)]
            xc_t = sc.tiles_n("B_xc", 2)
            dg = [P.sb(s1, "B_dg%d" % i, [128, 5, 128], BF16) for i in range(2)]
            dg_t = sc.tiles_n("B_dg", 2)
            cacc = [P.ps(s1, "B_cacc%d" % i, [128, 512], F32) for i in range(2)]
            cacc_t = sc.tiles_n("B_cacc", 2)
            xa = [P.sb(s1, "B_xa%d" % i, [128, S], BF16) for i in range(2)]
            xa_t = sc.tiles_n("B_xa", 2)
            tp = [P.ps(s1, "B_tp%d" % i, [128, 4, 128], BF16) for i in range(2)]
            tp_t = sc.tiles_n("B_tp", 2)
            tl1 = [cwr_t, cw_t, cwp_t, identf_t] + dg_t + cacc_t + xc_t + xa_t + tp_t
            sc.op("pool", lambda e: e.affine_select(out=identf[:], in_=G["ones_f"][:], pattern=[[-1, 128]],
                                                    compare_op=ALU.is_equal, fill=0.0, base=0, channel_multiplier=1),
                  reads=[G["ones_t"]], writes=[identf_t])
            sc.dma("sp", cwr[0:60, :], prm["ssd_conv_w"][l].rearrange("k (c p) -> (k c) p", p=128), owner=cwr_t, writes=[cwr_t])
            sc.dma("sp", cwr[60:72, :], prm["ssd_conv_b"][l].rearrange("(c p) -> c p", p=128), owner=cwr_t, writes=[cwr_t],
                   part=True)
            sc.op("pe", lambda e: e.transpose(out=cwp[:], in_=cwr[:], identity=identf[0:72, 0:72]),
                  reads=[cwr_t, identf_t], writes=[cwp_t])
            sc.op("act", lambda e: e.copy(out=cw[:], in_=cwp[:]), reads=[cwp_t], writes=[cw_t])
            for b in range(2):
                sc.op("pool", lambda e, b=b: e.memset(xc[b][:, 0:2], 0.0), writes=[xc_t[b]])
                sc.op("pool", lambda e, b=b: e.memset(xc[b][:, S + 2:S + 4], 0.0), writes=[xc_t[b]], part=True)
            sc.dma("sp", dtv[:], U["dt"].rearrange("(i p) c -> p i c", p=128), owner=dtv_t, reads=[G["dram_t"]["dt"]],
                   writes=[dtv_t])
            for k, nm in enumerate(("ssd_dt_bias_f", "ssd_dt_bias_b")):
                sc.dma("sp", rows[:, 0, 16 * k:16 * k + 16], prm[nm][l].partition_broadcast(128), owner=rows_t,
                       writes=[rows_t], part=True)
            for k, nm in enumerate(("ssd_a_log_f", "ssd_a_log_b")):
                sc.dma("sp", rows[:, 1, 16 * k:16 * k + 16], prm[nm][l].partition_broadcast(128), owner=rows_t,
                       writes=[rows_t], part=True)
            sc.dma("sp", rows[:, 2, 0:16], prm["ssd_d"][l].partition_broadcast(128), owner=rows_t, writes=[rows_t], part=True)
            sc.dma("sp", nwb[:], prm["ssd_norm_w"][l].partition_broadcast(128), owner=nwb_t, writes=[nwb_t])
            sc.op("dve", lambda e: e.tensor_tensor(out=dtv[:], in0=dtv[:], in1=rows[:, 0:1, :].to_broadcast([128, NT, 32]),
                                                   op=ALU.add), reads=[dtv_t, rows_t], writes=[dtv_t])
            sc.op("act", lambda e: e.activation(out=dtv[:], in_=dtv[:], func=AF.Exp), reads=[dtv_t], writes=[dtv_t])
            sc.op("act", lambda e: e.activation(out=dtv[:], in_=dtv[:], func=AF.Ln, bias=G["one"][:, 0:1]),
                  reads=[dtv_t, G["one_t"]], writes=[dtv_t])
            sc.op("act", lambda e: e.activation(out=rows[:, 3, :], in_=rows[:, 1, :], func=AF.Exp), reads=[rows_t],
                  writes=[rows_t])
            sc.op("dve", lambda e: e.scalar_tensor_tensor(out=av[:], in0=dtv[:], scalar=-1.0,
                                                          in1=rows[:, 3:4, :].to_broadcast([128, NT, 32]),
                                                          op0=ALU.mult, op1=ALU.mult),
                  reads=[dtv_t, rows_t], writes=[av_t])
            tpc = 0
            for c in range(12):
                b = c % 2
                sc.dma("sp", xc[b][:, 2:S + 2], U["xbc"][c * 128:(c + 1) * 128, :], owner=xc_t[b],
                       reads=[G["dram_t"]["xbc"]], writes=[xc_t[b]], part=True)
                dgb = c % 2
                for k in range(5):
                    sc.op("dve", lambda e, dgb=dgb, k=k, c=c: e.tensor_scalar(
                        out=dg[dgb][:, k, :], in0=identf[:], scalar1=cw[:, k * 12 + c:k * 12 + c + 1], scalar2=None,
                        op0=ALU.mult), reads=[identf_t, cw_t], writes=[dg_t[dgb]], part=(k > 0))
                if c < 10:
                    xo, xo_t = xa[b], xa_t[b]
                    xsl = lambda tb: xa[b][:, tb * 512:(tb + 1) * 512]
                else:
                    xo_t = CT_t[c - 10]
                    xsl = lambda tb, c=c: CT[:, c - 10, tb * 512:(tb + 1) * 512]
                for tb in range(4):
                    ca, ca_t = cacc[(4 * c + tb) % 2], cacc_t[(4 * c + tb) % 2]
                    for k in range(5):
                        sc.op("pe", lambda e, ca=ca, dgb=dgb, k=k, b=b, tb=tb: e.matmul(
                            ca[:], lhsT=dg[dgb][:, k, :], rhs=xc[b][:, k + tb * 512:k + tb * 512 + 512],
                            start=(k == 0), stop=(k == 4)),
                            reads=[dg_t[dgb], xc_t[b]], writes=[ca_t], part=(k > 0))
                    sc.op("act", lambda e, ca=ca, o_ap=xsl(tb), c=c: e.activation(
                        out=o_ap, in_=ca[:], func=AF.Silu, bias=cw[:, 60 + c:61 + c]),
                        reads=[ca_t, cw_t], writes=[xo_t], part=(tb > 0))
                if c in (8, 9):
                    sc.op("pool", lambda e, b=b, c=c: e.tensor_copy(out=BT[:, c - 8, :], in_=xa[b][:]),
                          reads=[xa_t[b]], writes=[BT_t[c - 8]])
                if c < 10:
                    for i0 in range(0, NT, 4):
                        tb_ = tpc % 2
                        tpc += 1
                        for j in range(4):
                            i = i0 + j
                            sc.op("pe", lambda e, tb_=tb_, j=j, b=b, i=i: e.transpose(
                                out=tp[tb_][:, j, :], in_=xa[b][:, i * 128:(i + 1) * 128], identity=ident[:]),
                                reads=[xa_t[b], G["ident_t"]], writes=[tp_t[tb_]], part=(j > 0))
                        eng = "act" if (tpc % 2) else "pool"
                        if eng == "act":
                            sc.op("act", lambda e, tb_=tb_, i0=i0, c=c: e.copy(
                                out=xtok[:, i0:i0 + 4, c * 128:(c + 1) * 128], in_=tp[tb_][:]),
                                reads=[tp_t[tb_]], writes=xtok_t[i0:i0 + 4], part=True)
                        else:
                            sc.op("dve", lambda e, tb_=tb_, i0=i0, c=c: e.tensor_copy(
                                out=xtok[:, i0:i0 + 4, c * 128:(c + 1) * 128], in_=tp[tb_][:]),
                                reads=[tp_t[tb_]], writes=xtok_t[i0:i0 + 4], part=True)
            sc.barrier(release=tl1)
        with contextlib.ExitStack() as s2:
            R = lambda name, shape, dt, n, psum=False: Ring(P, sc, s2, "B_" + name, shape, dt, n, psum)
            Sf = P.sb(s2, "B_Sf", [128, 2, 512], F32)
            Sf_t = sc.tiles_n("B_Sf", 2)
            Sbx = P.sb(s2, "B_Sb", [128, 2, 2, 512], BF16)
            r_Sb = [Ring(P, sc, s2, "B_Sb%d" % g, None, None, 2, views=[Sbx[:, g, k, :] for k in range(2)]) for g in range(2)]
            r_cb = R("cb", [128, 128], F32, 1, True)
            r_seg = R("seg", [128, 512], F32, 2, True)
            r_sm = R("sm", [128, 3, 16], F32, 1, True)
            r_yd = R("yd", [128, 512], F32, 1, True)
            r_stp = R("stp", [128, 512], F32, 1, True)
            r_yo = R("yo", [128, 512], F32, 1, True)
            r_tp = R("tp2", [128, 4, 128], BF16, 1, True)
            r_cbm = R("cbm", [128, 128], F32, 2)
            r_am = R("am", [128, 4, 128], BF16, 4)
            r_dec = R("dec", [128, 4, 128], F32, 2)
            r_mt = R("mt", [128, 4, 128], BF16, 4)
            r_ea = R("ea", [128, 3, 16], F32, 2)
            r_xdt = R("xdt", [128, 1024], BF16, 1)
            r_xw = R("xw", [128, 1024], BF16, 1)
            r_t = R("t", [128, 512], F32, 2)
            r_ybl = R("ybl", [128, 1024], BF16, 2)
            r_yf = R("yf", [128, 1024], F32, 1)
            r_z = R("z", [128, 4, 1024], BF16, 1)
            r_jk = R("jk", [128, 512], F32, 1)
            r_ss = R("ss", [128, 4], F32, 2)
            r_y = R("y", [128, 1024], BF16, 2)
            yst = P.sb(s2, "B_yst", [128, 8, 256], BF16)
            yst_t = sc.tile("B_yst")
            rings = [r_cb, r_seg, r_sm, r_yd, r_stp, r_yo, r_tp, r_cbm, r_am, r_dec, r_mt, r_ea, r_xdt, r_xw, r_t, r_ybl,
                     r_yf, r_z, r_jk, r_ss, r_y] + r_Sb
            tl2 = Sf_t + [yst_t]
            for r in rings:
                tl2 += r.t

            def ssd_pass(d):
                fwd = (d == 0)
                tri_in, tri_in_t = (G["trif"], G["trif_t"]) if fwd else (G["trib"], G["trib_t"])
                tri_st, tri_st_t = (G["tribs"], G["tribs_t"]) if fwd else (G["trifs"], G["trifs_t"])
                tri_sb, tri_sb_t = (G["tribsb"], G["tribsb_t"]) if fwd else (G["trifsb"], G["trifsb_t"])
                cur = []
                for g in range(2):
                    sc.op("pool", lambda e, g=g: e.memset(Sf[:, g, :], 0.0), writes=[Sf_t[g]])
                    sb0, sb0_t = r_Sb[g].next()
                    sc.op("pool", lambda e, sb0=sb0: e.memset(sb0, 0.0), writes=[sb0_t])
                    cur.append((sb0, sb0_t))
                order = list(range(NT)) if fwd else list(range(NT - 1, -1, -1))
                zcur = [None]

                def tileA(i):
                    tsl = slice(i * 128, (i + 1) * 128)
                    acol = av[:, i, 16 * d:16 * d + 16]
                    ams = []
                    for u in range(4):
                        h0 = u * 4
                        am, am_t = r_am.next()
                        for hh in range(4):
                            sc.op("act", lambda e, am=am, i=i, h0=h0, hh=hh: e.activation(
                                out=am[:, hh, :], in_=tri_in[:], func=AF.Copy,
                                scale=av[:, i, 16 * d + h0 + hh:16 * d + h0 + hh + 1]),
                                reads=[tri_in_t, av_t], writes=[am_t], part=(hh > 0))
                        ams.append((am, am_t))
                    sm, sm_t = r_sm.next()
                    sc.op("pe", lambda e, sm=sm, acol=acol: e.matmul(sm[:, 0, :], lhsT=tri_in[:], rhs=acol, start=True, stop=True),
                          reads=[tri_in_t, av_t], writes=[sm_t])
                    sc.op("pe", lambda e, sm=sm, acol=acol: e.matmul(sm[:, 1, :], lhsT=tri_st[:], rhs=acol, start=True, stop=True),
                          reads=[tri_st_t, av_t], writes=[sm_t], part=True)
                    sc.op("pe", lambda e, sm=sm, acol=acol: e.matmul(sm[:, 2, :], lhsT=G["ones_f"][:], rhs=acol, start=True,
                                                                     stop=True),
                          reads=[G["ones_t"], av_t], writes=[sm_t], part=True)
                    ea, ea_t = r_ea.next()
                    sc.op("act", lambda e, ea=ea, sm=sm: e.activation(out=ea[:], in_=sm[:], func=AF.Exp), reads=[sm_t],
                          writes=[ea_t])
                    xdt, xdt_t = r_xdt.next()
                    sc.op("dve", lambda e, xdt=xdt, i=i: e.tensor_tensor(
                        out=xdt[:].rearrange("p (h q) -> p h q", q=64), in0=xtok[:, i, 0:1024].rearrange("p (h q) -> p h q", q=64),
                        in1=dtv[:, i, 16 * d:16 * d + 16].unsqueeze(2).to_broadcast([128, 16, 64]), op=ALU.mult),
                        reads=[xtok_t[i], dtv_t], writes=[xdt_t])
                    xw, xw_t = r_xw.next()
                    sc.op("dve", lambda e, xw=xw, xdt=xdt, ea=ea: e.tensor_tensor(
                        out=xw[:].rearrange("p (h q) -> p h q", q=64), in0=xdt[:].rearrange("p (h q) -> p h q", q=64),
                        in1=ea[:, 1, :].unsqueeze(2).to_broadcast([128, 16, 64]), op=ALU.mult),
                        reads=[xdt_t, ea_t], writes=[xw_t])
                    ybl, ybl_t = r_ybl.next()
                    if fwd:
                        sc.dma("sp", ybl[:], ybw[tsl, :], owner=ybl_t, reads=[ybw_t], writes=[ybl_t])
                        yf, yf_t = r_yf.next()
                    ts = []
                    for g in range(2):
                        stp, stp_t = r_stp.next()
                        sc.op("pe", lambda e, stp=stp, i=i, g=g, xw=xw: e.matmul(
                            stp[:], lhsT=xtok[:, i, 1024 + g * 128:1024 + (g + 1) * 128], rhs=xw[:, g * 512:(g + 1) * 512],
                            start=True, stop=True), reads=[xtok_t[i], xw_t], writes=[stp_t])
                        yo, yo_t = r_yo.next()
                        sbv, sbv_t = cur[g]
                        sc.op("pe", lambda e, yo=yo, g=g, tsl=tsl, sbv=sbv: e.matmul(yo[:], lhsT=CT[:, g, tsl], rhs=sbv,
                                                                                    start=True, stop=True),
                              reads=[CT_t[g], sbv_t], writes=[yo_t])
                        sc.op("pool", lambda e, g=g, ea=ea: e.tensor_tensor(
                            out=Sf[:, g, :].rearrange("p (h q) -> p h q", q=64), in0=Sf[:, g, :].rearrange("p (h q) -> p h q", q=64),
                            in1=ea[:, 2, g * 8:(g + 1) * 8].unsqueeze(2).to_broadcast([128, 8, 64]), op=ALU.mult),
                            reads=[Sf_t[g], ea_t], writes=[Sf_t[g]])
                        sc.op("dve", lambda e, g=g, stp=stp: e.tensor_tensor(out=Sf[:, g, :], in0=Sf[:, g, :], in1=stp[:],
                                                                            op=ALU.add),
                              reads=[Sf_t[g], stp_t], writes=[Sf_t[g]])
                        nb, nb_t = r_Sb[g].next()
                        sc.op("act", lambda e, nb=nb, g=g: e.copy(out=nb, in_=Sf[:, g, :]), reads=[Sf_t[g]], writes=[nb_t])
                        cur[g] = (nb, nb_t)
                        t, t_t = r_t.next()
                        sc.op("dve", lambda e, t=t, yo=yo, ea=ea, g=g: e.tensor_tensor(
                            out=t[:].rearrange("p (h q) -> p h q", q=64), in0=yo[:].rearrange("p (h q) -> p h q", q=64),
                            in1=ea[:, 0, g * 8:(g + 1) * 8].unsqueeze(2).to_broadcast([128, 8, 64]), op=ALU.mult),
                            reads=[yo_t, ea_t], writes=[t_t])
                        ts.append((t, t_t))
                    cbms = []
                    for g in range(2):
                        cb, cb_t = r_cb.next()
                        sc.op("pe", lambda e, cb=cb, g=g, tsl=tsl: e.matmul(cb[:], lhsT=BT[:, g, tsl], rhs=CT[:, g, tsl],
                                                                           start=True, stop=True),
                              reads=[BT_t[g], CT_t[g]], writes=[cb_t])
                        cbm, cbm_t = r_cbm.next()
                        sc.op("dve", lambda e, cbm=cbm, cb=cb: e.tensor_tensor(out=cbm[:], in0=cb[:], in1=tri_in[:], op=ALU.mult),
                              reads=[cb_t, tri_in_t], writes=[cbm_t])
                        cbms.append((cbm, cbm_t))
                    mts = []
                    for pair in range(2):
                        segs = []
                        for u in (2 * pair, 2 * pair + 1):
                            am, am_t = ams[u]
                            seg, seg_t = r_seg.next()
                            sc.op("pe", lambda e, seg=seg, am=am: e.matmul(seg[:], lhsT=tri_sb[:],
                                                                           rhs=am[:].rearrange("p a b -> p (a b)"),
                                                                           start=True, stop=True),
                                  reads=[tri_sb_t, am_t], writes=[seg_t])
                            segs.append((seg, seg_t))
                        decs = []
                        for (seg, seg_t) in segs:
                            dec, dec_t = r_dec.next()
                            sc.op("act", lambda e, dec=dec, seg=seg: e.activation(out=dec[:].rearrange("p a b -> p (a b)"),
                                                                                 in_=seg[:], func=AF.Exp),
                                  reads=[seg_t], writes=[dec_t])
                            decs.append((dec, dec_t))
                        for k, (dec, dec_t) in enumerate(decs):
                            u = 2 * pair + k
                            cbm, cbm_t = cbms[u // 2]
                            mt, mt_t = r_mt.next()
                            sc.op("dve", lambda e, mt=mt, dec=dec, cbm=cbm: e.tensor_tensor(
                                out=mt[:], in0=dec[:], in1=cbm[:].unsqueeze(1).to_broadcast([128, 4, 128]), op=ALU.mult),
                                reads=[dec_t, cbm_t], writes=[mt_t])
                            mts.append((mt, mt_t))
                    for g in range(2):
                        yd, yd_t = r_yd.next()
                        if fwd:
                            sc.op("pe", lambda e, yd=yd, ybl=ybl, g=g: e.matmul(
                                yd[:], lhsT=ident[:], rhs=ybl[:, g * 512:(g + 1) * 512], start=True, stop=False,
                                skip_group_check=True), reads=[G["ident_t"], ybl_t], writes=[yd_t])
                        for q4 in range(2):
                            mt, mt_t = mts[g * 2 + q4]
                            for hh in range(4):
                                h = g * 8 + q4 * 4 + hh
                                hl = h - g * 8
                                sc.op("pe", lambda e, yd=yd, mt=mt, hh=hh, hl=hl, h=h, xdt=xdt: e.matmul(
                                    yd[:, hl * 64:(hl + 1) * 64], lhsT=mt[:, hh, :], rhs=xdt[:, h * 64:(h + 1) * 64],
                                    start=(not fwd), stop=True, skip_group_check=True),
                                    reads=[mt_t, xdt_t], writes=[yd_t], part=(fwd or not (q4 == 0 and hh == 0)))
                        t, t_t = ts[g]
                        gs = slice(g * 512, (g + 1) * 512)
                        if not fwd:
                            sc.op("dve", lambda e, t=t, yd=yd, ybl=ybl, gs=gs: e.tensor_tensor(out=ybl[:, gs], in0=t[:], in1=yd[:],
                                                                                              op=ALU.add),
                                  reads=[t_t, yd_t], writes=[ybl_t], part=(g > 0))
                        else:
                            sc.op("dve", lambda e, t=t, yd=yd, yf=yf, gs=gs: e.tensor_tensor(out=yf[:, gs], in0=t[:], in1=yd[:],
                                                                                            op=ALU.add),
                                  reads=[t_t, yd_t], writes=[yf_t], part=(g > 0))
                    if not fwd:
                        sc.dma("pool", ybw[tsl, :], ybl[:], owner=ybl_t, reads=[ybl_t], writes=[ybw_t], part=True)
                        return None
                    if i % 4 == 0:
                        z, z_t = r_z.next()
                        sc.dma("sp", z[:], U["z"][i * 128:(i + 4) * 128, :].rearrange("(j p) c -> p j c", p=128), owner=z_t,
                               reads=[G["dram_t"]["z"]], writes=[z_t])
                        sc.op("act", lambda e, z=z: e.activation(out=z[:], in_=z[:], func=AF.Silu), reads=[z_t], writes=[z_t])
                        zcur[0] = (z, z_t)
                    z, z_t = zcur[0]
                    sz = z[:, i % 4, :]
                    sz_t = z_t
                    xd, xd_t = r_xdt.next()
                    sc.op("pool", lambda e, xd=xd, i=i: e.tensor_tensor(
                        out=xd[:].rearrange("p (h q) -> p h q", q=64), in0=xtok[:, i, 0:1024].rearrange("p (h q) -> p h q", q=64),
                        in1=rows[:, 2, 0:16].unsqueeze(2).to_broadcast([128, 16, 64]), op=ALU.mult),
                        reads=[xtok_t[i], rows_t], writes=[xd_t])
                    sc.op("dve", lambda e, yf=yf, xd=xd: e.tensor_tensor(out=yf[:], in0=yf[:], in1=xd[:], op=ALU.add),
                          reads=[yf_t, xd_t], writes=[yf_t])
                    sc.op("dve", lambda e, yf=yf, sz=sz: e.tensor_tensor(out=yf[:], in0=yf[:], in1=sz, op=ALU.mult),
                          reads=[yf_t, sz_t], writes=[yf_t])
                    ss, ss_t = r_ss.next()
                    for g in range(2):
                        gs = slice(g * 512, (g + 1) * 512)
                        jk, jk_t = r_jk.next()
                        sc.op("dve", lambda e, jk=jk, yf=yf, gs=gs, ss=ss, g=g: e.scalar_tensor_tensor(
                            out=jk[:], in0=yf[:, gs], scalar=1.0, in1=yf[:, gs], op0=ALU.mult, op1=ALU.mult,
                            accum_out=ss[:, g:g + 1]), reads=[yf_t], writes=[jk_t, ss_t])
                    sc.op("dve", lambda e, ss=ss: e.tensor_scalar(out=ss[:, 2:4], in0=ss[:, 0:2], scalar1=1.0 / 512.0, scalar2=EPS,
                                                                  op0=ALU.mult, op1=ALU.add), reads=[ss_t], writes=[ss_t])
                    sc.op("pool", lambda e, ss=ss: e.tensor_tensor(out=ss[:, 0:2], in0=ss[:, 2:4], in1=G["neghalf"][:, 0:2],
                                                                   op=ALU.pow), reads=[ss_t, G["neghalf_t"]], writes=[ss_t])
                    y, y_t = r_y.next()
                    for g in range(2):
                        gs = slice(g * 512, (g + 1) * 512)
                        sc.op("dve", lambda e, y=y, yf=yf, gs=gs, ss=ss, g=g: e.scalar_tensor_tensor(
                            out=y[:, gs], in0=yf[:, gs], scalar=ss[:, g:g + 1], in1=nwb[:, gs], op0=ALU.mult, op1=ALU.mult),
                            reads=[yf_t, ss_t, nwb_t], writes=[y_t], part=(g > 0))
                    return (y, y_t, i)

                def tileC(c3):
                    if c3 is None:
                        return
                    (y, y_t, i) = c3
                    emit_yT(P, sc, G, r_tp, y, y_t, yst, yst_t, i, YB["ssd"], G["dram_t"]["yb_ssd"], gsz=2)

                prev3 = None
                for i in order:
                    n3 = tileA(i)
                    tileC(prev3)
                    prev3 = n3
                tileC(prev3)

            ssd_pass(1)
            ssd_pass(0)
            sc.barrier(release=tl2)
        sc.barrier(release=tiles)


def emit_yT(P, sc, G, r_tp, y, y_t, yst, yst_t, i, dst, dst_t, gsz=4):
    ident = G["ident"]
    for half in range(2):
        tp, tp_t = r_tp.next()
        for jq in range(4):
            c = half * 4 + jq
            sc.op("pe", lambda e, tp=tp, jq=jq, c=c: e.transpose(out=tp[:, jq, :], in_=y[:, c * 128:(c + 1) * 128],
                                                               identity=ident[:]),
                  reads=[y_t, G["ident_t"]], writes=[tp_t], part=(jq > 0))
        sc.op("act", lambda e, tp=tp, half=half: e.copy(
            out=yst[:, half * 4:half * 4 + 4, (i % gsz) * 128:(i % gsz + 1) * 128], in_=tp[:]),
            reads=[tp_t], writes=[yst_t], part=not (i % gsz == 0 and half == 0))
    if i % gsz == gsz - 1:
        yv = dst.rearrange("(c p) t -> p c t", p=128)
        sc.dma("pool", yv[:, :, (i - gsz + 1) * 128:(i + 1) * 128], yst[:], owner=yst_t, reads=[yst_t], writes=[dst_t],
               part=True)


NEG = -30000.0


def na_r0(r):
    return min(max(r - 4, 0), 24)


def na_valid(kr, qr):
    return na_r0(qr) <= kr < na_r0(qr) + 8


def phase_D(P, sc, G, U, YB, prm, natt, l):
    nc = P.nc
    with contextlib.ExitStack() as ph:
        qnT = P.sb(ph, "D_qnT", [128, 8, S], BF16)
        knT = P.sb(ph, "D_knT", [128, 8, S], BF16)
        qn_t = sc.tiles_n("D_qn", 8)
        kn_t = sc.tiles_n("D_kn", 8)
        TT = P.sb(ph, "D_TT", [128, 8, 17, 64], BF16)
        TT_t = sc.tiles_n("D_TT", 4)
        wcol = P.sb(ph, "D_wcol", [128, 4], F32)
        wcol_t = sc.tile("D_wcol")
        tiles = qn_t + kn_t + TT_t + [wcol_t]
        with contextlib.ExitStack() as s1:
            TTf = [P.sb(s1, "D_TTf%d" % i, [128, 2, 17, 64], F32) for i in range(2)]
            TTf_t = sc.tiles_n("D_TTf", 2)
            qc_ = [P.sb(s1, "D_qc%d" % i, [128, S], BF16) for i in range(2)]
            qc_t = sc.tiles_n("D_qc", 2)
            sq = [P.sb(s1, "D_sq%d" % i, [128, S], BF16) for i in range(2)]
            sq_t = sc.tiles_n("D_sq", 2)
            lnv = [P.sb(s1, "D_ln%d" % i, [128, S], F32) for i in range(2)]
            lnv_t = sc.tiles_n("D_ln", 2)
            bones = P.sb(s1, "D_bones", [128, 128], BF16)
            bones_t = sc.tile("D_bones")
            ssp = [P.ps(s1, "D_ssp%d" % i, [128, S], F32) for i in range(2)]
            ssp_t = sc.tiles_n("D_ssp", 2)
            t1 = TTf_t + qc_t + sq_t + lnv_t + [bones_t] + ssp_t
            for g in range(4):
                b = g % 2
                sc.dma("sp", TTf[b][:], natt[l][:, 2 * g:2 * g + 2, :, :], owner=TTf_t[b], writes=[TTf_t[b]])
                sc.op("pool", lambda e, b=b, g=g: e.tensor_copy(out=TT[:, 2 * g:2 * g + 2, :, :], in_=TTf[b][:]),
                      reads=[TTf_t[b]], writes=[TT_t[g]])
            for hh in range(2):
                sc.dma("sp", wcol[hh * 64:(hh + 1) * 64, 2:3], prm["na_q_norm_w"][l].rearrange("(d o) -> d o", o=1),
                       owner=wcol_t, writes=[wcol_t], part=True)
                sc.dma("sp", wcol[hh * 64:(hh + 1) * 64, 1:2], prm["na_k_norm_w"][l].rearrange("(d o) -> d o", o=1),
                       owner=wcol_t, writes=[wcol_t], part=True)
            sc.op("dve", lambda e: e.tensor_scalar(out=wcol[:, 0:1], in0=wcol[:, 2:3], scalar1=0.125, scalar2=None,
                                                   op0=ALU.mult), reads=[wcol_t], writes=[wcol_t])
            sc.op("pool", lambda e: e.memset(bones[:], 0.0), writes=[bones_t])
            sc.op("pool", lambda e: e.memset(bones[0:64, 0:64], 1.0), reads=[bones_t], writes=[bones_t])
            sc.op("pool", lambda e: e.memset(bones[64:128, 64:128], 1.0), reads=[bones_t], writes=[bones_t])
            jobs = []
            for which, (src, dstT, dst_t, wc) in enumerate(((U["nq"], qnT, qn_t, 0), (U["nk"], knT, kn_t, 1))):
                src_t = G["dram_t"]["nq" if which == 0 else "nk"]
                for c in range(8):
                    jobs.append((src, src_t, dstT, dst_t, wc, c))

            def n_s1(k):
                (src, src_t, dstT, dst_t, wc, c) = jobs[k]
                cb = k % 2
                sc.dma("sp", qc_[cb][:], src[c * 128:(c + 1) * 128, :], owner=qc_t[cb], reads=[src_t], writes=[qc_t[cb]])
                sc.op("dve", lambda e, cb=cb: e.tensor_tensor(out=sq[cb][:], in0=qc_[cb][:], in1=qc_[cb][:], op=ALU.mult),
                      reads=[qc_t[cb]], writes=[sq_t[cb]])
                for tb in range(4):
                    sl = slice(tb * 512, (tb + 1) * 512)
                    sc.op("pe", lambda e, cb=cb, sl=sl: e.matmul(ssp[cb][:, sl], lhsT=bones[:], rhs=sq[cb][:, sl], start=True,
                                                                 stop=True),
                          reads=[bones_t, sq_t[cb]], writes=[ssp_t[cb]], part=(tb > 0))
                sc.op("act", lambda e, cb=cb: e.activation(out=lnv[cb][:], in_=ssp[cb][:], func=AF.Ln,
                                                           bias=G["eps"][:, 0:1], scale=1.0 / 64.0),
                      reads=[ssp_t[cb], G["eps_t"]], writes=[lnv_t[cb]])
                sc.op("act", lambda e, cb=cb: e.activation(out=lnv[cb][:], in_=lnv[cb][:], func=AF.Exp, scale=-0.5),
                      reads=[lnv_t[cb]], writes=[lnv_t[cb]])

            def n_s2(k):
                (src, src_t, dstT, dst_t, wc, c) = jobs[k]
                cb = k % 2
                sc.op("dve", lambda e, cb=cb, dstT=dstT, c=c, wc=wc: e.scalar_tensor_tensor(
                    out=dstT[:, c, :], in0=qc_[cb][:], scalar=wcol[:, wc:wc + 1], in1=lnv[cb][:],
                    op0=ALU.mult, op1=ALU.mult),
                    reads=[qc_t[cb], lnv_t[cb], wcol_t], writes=[dst_t[c]])

            n_s1(0)
            for k in range(len(jobs)):
                if k + 1 < len(jobs):
                    n_s1(k + 1)
                n_s2(k)
            sc.barrier(release=t1)
        with contextlib.ExitStack() as s2:
            vx = P.sb(s2, "D_vx", [128, NT, 16, 65], BF16)
            vx_t = sc.tiles_n("D_vx", NT)
            sps = [P.ps(s2, "D_sps%d" % i, [128, 8, 128], F32) for i in range(2)]
            sps_t = sc.tiles_n("D_sps", 2)
            pT = [P.sb(s2, "D_pT%d" % i, [128, 5, 128], BF16) for i in range(3)]
            pT_t = sc.tiles_n("D_pT", 3)
            po = [P.ps(s2, "D_po%d" % i, [128, 2, 66], F32) for i in range(2)]
            po_t = sc.tiles_n("D_po", 2)
            rc = [P.sb(s2, "D_rc%d" % i, [128, 2], F32) for i in range(2)]
            rc_t = sc.tiles_n("D_rc", 2)
            ot = [P.sb(s2, "D_ot%d" % i, [128, 1024], BF16) for i in range(2)]
            ot_t = sc.tiles_n("D_ot", 2)
            tp = [P.ps(s2, "D_tp%d" % i, [128, 4, 128], BF16) for i in range(2)]
            tp_t = sc.tiles_n("D_tp", 2)
            yst = P.sb(s2, "D_yst", [128, 8, 512], BF16)
            yst_t = sc.tile("D_yst")
            t2 = vx_t + sps_t + pT_t + po_t + rc_t + ot_t + tp_t + [yst_t]
            nvv = U["nv"].rearrange("(i p) (h d) -> p i h d", p=128, d=64)
            for i in range(NT):
                sc.op("pool", lambda e, i=i: e.memset(vx[:, i, :, 64:65], 1.0), writes=[vx_t[i]])
                sc.dma("sp", vx[:, i, :, 0:64], nvv[:, i, :, :], owner=vx_t[i], reads=[G["dram_t"]["nv"]],
                       writes=[vx_t[i]], part=True)
            ident = G["ident"]
            tpc = [0]
            units = []
            for i in range(NT):
                jlo = na_r0(2 * i) // 2
                jhi = (na_r0(2 * i + 1) + 7) // 2
                js = list(range(jlo, jhi + 1))
                for hp in range(8):
                    for hh in range(2):
                        units.append((i, hp, hh, js))

            def emit_S(u):
                i, hp, hh, js = units[u]
                h = 2 * hp + hh
                p0 = 64 * hh
                sb_ = u % 2
                for jj, j in enumerate(js):
                    sc.op("pe", lambda e, sb_=sb_, jj=jj, j=j, p0=p0, hp=hp, i=i: e.matmul(
                        sps[sb_][:, jj, :], lhsT=knT[p0:p0 + 64, hp, j * 128:(j + 1) * 128],
                        rhs=qnT[p0:p0 + 64, hp, i * 128:(i + 1) * 128], start=True, stop=False,
                        skip_group_check=True),
                        reads=[kn_t[hp], qn_t[hp]], writes=[sps_t[sb_]], part=(jj > 0))
                    mms = []
                    for b0 in range(2):
                        qr = 2 * i + b0
                        va = [na_valid(2 * j + a, qr) for a in range(2)]
                        dr0 = 2 * j - qr + 7
                        cs = slice(b0 * 64, (b0 + 1) * 64)
                        if va[0] and va[1]:
                            mms.append((slice(0, 128), cs, TT[p0:p0 + 64, hp, dr0:dr0 + 2, :]))
                        elif not va[0] and not va[1]:
                            mms.append((slice(0, 128), cs, TT[p0:p0 + 64, hp, 15:17, :]))
                        else:
                            d0 = dr0 if va[0] else 15
                            d1 = dr0 + 1 if va[1] else 16
                            mms.append((slice(0, 64), cs, TT[p0:p0 + 64, hp, d0, :]))
                            mms.append((slice(64, 128), cs, TT[p0:p0 + 64, hp, d1, :]))
                    for mi, (ps_, cs, lhs) in enumerate(mms):
                        sc.op("pe", lambda e, sb_=sb_, jj=jj, ps_=ps_, cs=cs, lhs=lhs, p0=p0, last=(mi == len(mms) - 1):
                              e.matmul(sps[sb_][ps_, jj, cs], lhsT=lhs, rhs=ident[p0:p0 + 64, p0:p0 + 64],
                                       start=False, stop=last, skip_group_check=True),
                              reads=[TT_t[hp // 2], G["ident_t"]], writes=[sps_t[sb_]], part=True)

            def emit_rest(u):
                i, hp, hh, js = units[u]
                h = 2 * hp + hh
                sb_ = u % 2
                pt = u % 3
                pb_ = (u // 2) % 2
                ob = i % 2
                n = len(js)
                n1 = min(n, 4)
                sc.op("act", lambda e, pt=pt, sb_=sb_, n1=n1: e.activation(out=pT[pt][:, 0:n1, :],
                                                                         in_=sps[sb_][:, 0:n1, :], func=AF.Exp),
                      reads=[sps_t[sb_]], writes=[pT_t[pt]])
                if n > 4:
                    sc.op("act", lambda e, pt=pt, sb_=sb_, n=n: e.activation(out=pT[pt][:, 4:n, :],
                                                                           in_=sps[sb_][:, 4:n, :], func=AF.Exp),
                          reads=[sps_t[sb_]], writes=[pT_t[pt]], part=True)
                for jj, j in enumerate(js):
                    sc.op("pe", lambda e, pb_=pb_, hh=hh, pt=pt, jj=jj, j=j, h=h, n=n: e.matmul(
                        po[pb_][:, hh, 0:65], lhsT=pT[pt][:, jj, :], rhs=vx[:, j, h, :],
                        start=(jj == 0), stop=(jj == n - 1)),
                        reads=[pT_t[pt], vx_t[j]], writes=[po_t[pb_]], part=(hh > 0 or jj > 0))
                if hh == 1:
                    sc.op("dve", lambda e, pb_=pb_: e.reciprocal(out=rc[pb_][:, 0:2], in_=po[pb_][:, :, 64]),
                          reads=[po_t[pb_]], writes=[rc_t[pb_]])
                    for h2 in range(2):
                        hx = 2 * hp + h2
                        sc.op("dve", lambda e, pb_=pb_, h2=h2, hx=hx, ob=ob: e.tensor_scalar(
                            out=ot[ob][:, hx * 64:(hx + 1) * 64], in0=po[pb_][:, h2, 0:64], scalar1=rc[pb_][:, h2:h2 + 1],
                            scalar2=None, op0=ALU.mult),
                            reads=[po_t[pb_], rc_t[pb_]], writes=[ot_t[ob]], part=(hx > 0))
                if hp == 7 and hh == 1:
                    for half in range(2):
                        tb_ = tpc[0] % 2
                        tpc[0] += 1
                        for jq in range(4):
                            c = half * 4 + jq
                            sc.op("pe", lambda e, tb_=tb_, jq=jq, c=c, ob=ob: e.transpose(
                                out=tp[tb_][:, jq, :], in_=ot[ob][:, c * 128:(c + 1) * 128], identity=ident[:]),
                                reads=[ot_t[ob], G["ident_t"]], writes=[tp_t[tb_]], part=(jq > 0))
                        sc.op("act", lambda e, tb_=tb_, half=half, i=i: e.copy(
                            out=yst[:, half * 4:half * 4 + 4, (i % 4) * 128:(i % 4 + 1) * 128], in_=tp[tb_][:]),
                            reads=[tp_t[tb_]], writes=[yst_t], part=not (i % 4 == 0 and half == 0))
                    if i % 4 == 3:
                        yv = YB["na"].rearrange("(c p) t -> p c t", p=128)
                        sc.dma("pool", yv[:, :, (i - 3) * 128:(i + 1) * 128], yst[:], owner=yst_t, reads=[yst_t],
                               writes=[G["dram_t"]["yb_na"]], part=True)

            emit_S(0)
            for u in range(len(units)):
                if u + 1 < len(units):
                    emit_S(u + 1)
                emit_rest(u)
            sc.barrier(release=t2)
        sc.barrier(release=tiles)


def phase_F(P, sc, G, prm, l):
    nc = P.nc
    x = G["x"]
    with contextlib.ExitStack() as ph:
        hT = P.sb(ph, "F_hT", [128, 8, S], BF16)
        hT_t = sc.tiles_n("F_hT", NT)
        tiles = list(hT_t)
        tiles += rms_transpose(P, sc, G, ph, prm["norm_mlp_w"][l], hT, hT_t, l, "F")
        wst = WStream(P, sc, ph, "F", 1, 4096, nf=2, nb=3)
        fT = [P.sb(ph, "F_fT%d" % i, [128, 4, S], BF16) for i in range(2)]
        fT_t = [sc.tiles_n("F_fT%d_" % i, 4) for i in range(2)]
        rl = [P.sb(ph, "F_rl%d" % i, [128, 512], F32) for i in range(2)]
        rl_t = sc.tiles_n("F_rl", 2)
        acc = [P.ps(ph, "F_acc%d" % i, [128, 512], F32) for i in range(4)]
        acc_t = sc.tiles_n("F_acc", 4)
        tiles += wst.tiles + fT_t[0] + fT_t[1] + rl_t + acc_t
        w1v = prm["w_ff1"][l].rearrange("(kc p) n -> p kc n", p=128)
        w2v = prm["w_ff2"][l].rearrange("(c p) n -> p c n", p=128)
        items = []
        for g in range(8):
            items.append((w1v[:, :, g * 512:(g + 1) * 512], 8, 512))
            items.append((w2v[:, g * 4:(g + 1) * 4, :], 4, 1024))
        wst.items = items
        wst_views = {}

        def view(slot, k, n):
            return slot[:, 0, :].rearrange("p (k n) -> p k n", k=k)
        def _load(g):
            if g >= len(items):
                return
            ap, k, n = items[g]
            fs = g % wst.nf
            sc.dma("sp", view(wst.f[fs], k, n), ap, owner=wst.f_t[fs], writes=[wst.f_t[fs]])

        def _cast(g):
            if g >= len(items):
                return
            fs, bs = g % wst.nf, g % wst.nb
            sc.op("pool", lambda e: e.tensor_copy(out=wst.b[bs][:, 0, :], in_=wst.f[fs][:, 0, :]),
                  reads=[wst.f_t[fs]], writes=[wst.b_t[bs]])
        wst._load = _load
        wst._cast = _cast
        _load(0)
        _load(1)
        _cast(0)
        ai = 0
        ri = 0
        for g in range(8):
            fb = g % 2
            w1s, w1_t = wst.get(2 * g)
            w1b = view(w1s, 8, 512)
            for c in range(4):
                for tb in range(4):
                    a = ai % 4
                    ai += 1
                    for kc in range(8):
                        sc.op("pe", lambda e, a=a, kc=kc, w1b=w1b, c=c, tb=tb: e.matmul(
                            acc[a][:], lhsT=w1b[:, kc, c * 128:(c + 1) * 128],
                            rhs=hT[:, kc, tb * 512:(tb + 1) * 512], start=(kc == 0), stop=(kc == 7)),
                            reads=[w1_t] + hT_t[tb * 4:tb * 4 + 4], writes=[acc_t[a]], part=(kc > 0))
                    r = ri % 2
                    ri += 1
                    sc.op("act", lambda e, r=r, a=a: e.activation(out=rl[r][:], in_=acc[a][:], func=AF.Relu),
                          reads=[acc_t[a]], writes=[rl_t[r]])
                    sc.op("pool", lambda e, r=r, fb=fb, c=c, tb=tb: e.tensor_tensor(
                        out=fT[fb][:, c, tb * 512:(tb + 1) * 512], in0=rl[r][:], in1=rl[r][:], op=ALU.mult),
                        reads=[rl_t[r]], writes=[fT_t[fb][c]], part=(tb > 0))
            w2s, w2_t = wst.get(2 * g + 1)
            w2b = view(w2s, 4, 1024)
            for i in range(NT):
                for hh in range(2):
                    a = ai % 4
                    ai += 1
                    for c in range(4):
                        sc.op("pe", lambda e, a=a, c=c, w2b=w2b, i=i, hh=hh, fb=fb: e.matmul(
                            acc[a][:], lhsT=fT[fb][:, c, i * 128:(i + 1) * 128],
                            rhs=w2b[:, c, hh * 512:(hh + 1) * 512], start=(c == 0), stop=(c == 3)),
                            reads=[w2_t, fT_t[fb][c]], writes=[acc_t[a]], part=(c > 0))
                    xs = x[:, i, hh * 512:(hh + 1) * 512]
                    sc.op("dve", lambda e, xs=xs, a=a: e.tensor_tensor(out=xs, in0=xs, in1=acc[a][:], op=ALU.add),
                          reads=[acc_t[a], G["xt"][i]], writes=[G["xt"][i]])
        sc.barrier(release=tiles)


_NC_CACHE = {}


def make_na_tt(rpb):
    rpb = np.asarray(rpb, dtype=np.float32)
    L = rpb.shape[0]
    out = np.full((L, 128, 8, 17, 64), NEG, dtype=np.float32)
    qc = np.arange(64)
    ws = np.clip(qc - 8, 0, 48)
    for q in range(64):
        kc = np.arange(ws[q], ws[q] + 16)
        idx = kc - q + 15
        for hh in range(2):
            out[:, hh * 64 + q, :, 0:15, ws[q]:ws[q] + 16] = rpb[:, hh::2][:, :, :, idx]
    return out


def kernel(**inputs):
    cfg = {}
    key = "full"
    if key not in _NC_CACHE:
        _NC_CACHE[key] = build(cfg)
    nc = _NC_CACHE[key]
    x = np.ascontiguousarray(inputs["x"], dtype=np.float32)
    base = {n: np.ascontiguousarray(inputs[n], dtype=np.float32) for n in PARAM_NAMES}
    base["na_tt"] = make_na_tt(inputs["na_rpb"])
    in_maps = []
    for c in range(8):
        m = dict(base)
        m["x"] = x[c]
        in_maps.append(m)
    res = run_bass_kernel_spmd(nc, in_maps, core_ids=list(range(8)))
    return np.stack([r["y"] for r in res.results], axis=0).astype(np.float32)
```

```python
import contextlib
import numpy as np
import concourse.bass as bass
import concourse.mybir as mybir
from concourse.bass_utils import run_bass_kernel_spmd

F32 = mybir.dt.float32
BF16 = mybir.dt.bfloat16
ALU = mybir.AluOpType
AF = mybir.ActivationFunctionType
AX = mybir.AxisListType

D = 1024
S = 2048
NT = S // 128
DEPTH = 2
N_IN = 11840
EPS = 1e-6


class TT:
    __slots__ = ("name", "lw", "rd", "dsems", "gen")

    def __init__(self, name):
        self.name = name
        self.lw = {}
        self.rd = {}
        self.gen = {}
        self.dsems = {}


class Sched:
    ENG = ("pe", "act", "dve", "pool", "sp")
    BLK = {"pe": "tensor", "act": "scalar", "dve": "vector", "pool": "gpsimd", "sp": "sync"}

    def __init__(self, nc, stack):
        self.nc = nc
        self.stack = stack
        self.ops = {e: [] for e in self.ENG}
        self.seen = {e: {} for e in self.ENG}
        self.esem = {e: stack.enter_context(nc.semaphore("es_" + e)) for e in self.ENG if e != "sp"}
        self.tiles = []
        self.free_dsems = {"sp": [], "pool": [], "act": []}
        self.nsem = 4
        self.skip_same = {"pe"}

    def tile(self, name):
        t = TT(name)
        self.tiles.append(t)
        return t

    def tiles_n(self, name, n):
        return [self.tile("%s%d" % (name, i)) for i in range(n)]

    def _collect(self, reads, writes, part):
        evs = {}

        def add(d):
            for k, v in d.items():
                if k not in evs or evs[k][0] < v[0]:
                    evs[k] = v
        for t in reads:
            add(t.lw)
        for t in writes:
            if part and not t.rd:
                add(t.gen)
                continue
            g = dict(t.rd)
            for k, v in t.lw.items():
                if k not in g or g[k][0] < v[0]:
                    g[k] = v
            t.gen = g
            add(g)
        return evs

    def _waits(self, eng, evs):
        waits = []
        for k, (val, obj) in evs.items():
            if k == ("E", eng) and eng in self.skip_same:
                continue
            if self.seen[eng].get(k, 0) >= val:
                continue
            self.seen[eng][k] = val
            waits.append((k, val, obj))
            if k[0] == "E":
                self.ops[k[1]][val - 1]["inc"] = True
        return waits

    def _update(self, ev_key, ev_val, reads, writes, part):
        for t in reads:
            t.rd[ev_key] = ev_val
        for t in writes:
            if part and not t.rd:
                t.lw[ev_key] = ev_val
            else:
                t.lw = {ev_key: ev_val}
                t.rd = {}

    def op(self, eng, fn, reads=(), writes=(), part=False):
        waits = self._waits(eng, self._collect(reads, writes, part))
        self.ops[eng].append({"fn": fn, "waits": waits, "inc": False, "dma": None})
        idx = len(self.ops[eng])
        self._update(("E", eng), (idx, None), reads, writes, part)

    def dma(self, q, out, in_, owner, reads=(), writes=(), part=False, **kw):
        waits = self._waits(q, self._collect(reads, writes, part))
        rec = owner.dsems.get(q)
        if rec is None:
            if self.free_dsems[q]:
                rec = self.free_dsems[q].pop()
            else:
                rec = [self.stack.enter_context(self.nc.semaphore("ds%d" % self.nsem)), 0, self.nsem]
                self.nsem += 1
            owner.dsems[q] = rec
        rec[1] += 16
        self.ops[q].append({"fn": (lambda e: e.dma_start(out=out, in_=in_, **kw)), "waits": waits,
                            "inc": False, "dma": rec[0]})
        self._update(("D", rec[2]), (rec[1], rec[0]), reads, writes, part)

    def barrier(self, release=()):
        evs = {}
        for e in self.ENG:
            if e == "sp":
                continue
            idx = len(self.ops[e])
            while idx > 0 and (self.ops[e][idx - 1]["dma"] is not None or self.ops[e][idx - 1].get("nop")):
                idx -= 1
            if idx > 0:
                evs[("E", e)] = (idx, None)
        for t in self.tiles:
            for d in (t.lw, t.rd):
                for k, v in d.items():
                    if k[0] == "D" and (k not in evs or evs[k][0] < v[0]):
                        evs[k] = v
        for e in self.ENG:
            sk = self.skip_same
            self.skip_same = set()
            w = self._waits(e, dict(evs))
            self.skip_same = sk
            self.ops[e].append({"fn": (lambda en: en.nop()), "waits": w, "inc": False, "dma": None, "nop": True})
        for t in self.tiles:
            t.lw = {}
            t.rd = {}
            t.gen = {}
        rel = set(id(t) for t in release)
        for t in release:
            for q, rec in t.dsems.items():
                self.free_dsems[q].append(rec)
            t.dsems = {}
        self.tiles = [t for t in self.tiles if id(t) not in rel]

    def emit(self):
        nc = self.nc
        mile = {}
        for e in self.ENG:
            c = 0
            m = []
            for o in self.ops[e]:
                if o["inc"]:
                    c += 1
                m.append(c)
            mile[e] = m
            assert c < 60000, (e, c)
        with nc.Block() as block:
            for e in self.ENG:
                def body(engine, e=e):
                    for o in self.ops[e]:
                        for (k, val, obj) in o["waits"]:
                            if k[0] == "E":
                                engine.wait_ge(self.esem[k[1]], mile[k[1]][val - 1])
                            else:
                                engine.wait_ge(obj, val)
                        ins = o["fn"](engine)
                        if o["dma"] is not None:
                            ins.then_inc(o["dma"], 16)
                        elif o["inc"]:
                            ins.then_inc(self.esem[e], 1)
                getattr(block, self.BLK[e])(body)


class Prog:
    def __init__(self, cfg):
        self.cfg = cfg
        self.nc = bass.Bass("TRN2", target_bir_lowering=False)
        self.dbg = cfg.get("debug", ())

    def dram(self, name, shape, dt, kind="Internal"):
        if name in self.dbg:
            kind = "ExternalOutput"
        if name in self.cfg.get("ext_in", ()):
            kind = "ExternalInput"
        return self.nc.dram_tensor(name, list(shape), dt, kind=kind).ap()

    def sb(self, stack, name, shape, dt):
        self.uid = getattr(self, "uid", 0) + 1
        return stack.enter_context(self.nc.sbuf_tensor("%s_u%d" % (name, self.uid), list(shape), dt))

    def ps(self, stack, name, shape, dt):
        self.uid = getattr(self, "uid", 0) + 1
        return stack.enter_context(self.nc.psum_tensor("%s_u%d" % (name, self.uid), list(shape), dt))


IN_SIZES = (1024, 1536, 16, 16, 512, 512, 1024, 1024, 16, 16, 1024, 1024, 1024, 3072)
IN_OFF = [0]
for _s in IN_SIZES:
    IN_OFF.append(IN_OFF[-1] + _s)
(O_Z, O_XBC, O_DTF, O_DTB, O_GQ, O_GK, O_GV, O_GG, O_GAF, O_GAB, O_NQ, O_NK, O_NV, O_GATE, _) = IN_OFF

PARAM_NAMES = ["norm_mix_w", "w_in", "ssd_conv_w", "ssd_conv_b", "ssd_dt_bias_f", "ssd_dt_bias_b",
               "ssd_a_log_f", "ssd_a_log_b", "ssd_d", "ssd_norm_w", "gla_a2_f", "gla_a2_bias_f",
               "gla_a2_b", "gla_a2_bias_b", "gla_norm_w", "na_q_norm_w", "na_k_norm_w", "na_rpb",
               "w_branch_ssd", "w_branch_gla", "w_branch_na", "w_out", "norm_mlp_w", "w_ff1", "w_ff2"]
PARAM_SHAPES = {
    "norm_mix_w": (2, 1024), "w_in": (2, 1024, 11840), "ssd_conv_w": (2, 5, 1536), "ssd_conv_b": (2, 1536),
    "ssd_dt_bias_f": (2, 16), "ssd_dt_bias_b": (2, 16), "ssd_a_log_f": (2, 16), "ssd_a_log_b": (2, 16),
    "ssd_d": (2, 16), "ssd_norm_w": (2, 1024), "gla_a2_f": (2, 16, 512), "gla_a2_bias_f": (2, 512),
    "gla_a2_b": (2, 16, 512), "gla_a2_bias_b": (2, 512), "gla_norm_w": (2, 256), "na_q_norm_w": (2, 64),
    "na_k_norm_w": (2, 64), "na_rpb": (2, 16, 15, 31), "w_branch_ssd": (2, 1024, 1024),
    "w_branch_gla": (2, 1024, 1024), "w_branch_na": (2, 1024, 1024), "w_out": (2, 1024, 1024),
    "norm_mlp_w": (2, 1024), "w_ff1": (2, 1024, 4096), "w_ff2": (2, 4096, 1024),
}


def build(cfg):
    P = Prog(cfg)
    nc = P.nc
    layers = cfg.get("layers", DEPTH)
    phases = cfg.get("phases", "ABCDEF")
    x_in = nc.dram_tensor("x", [S, D], F32, kind="ExternalInput").ap()
    prm = {n: nc.dram_tensor(n, list(PARAM_SHAPES[n]), F32, kind="ExternalInput").ap() for n in PARAM_NAMES}
    y_out = nc.dram_tensor("y", [S, D], F32, kind="ExternalOutput").ap()
    natt = nc.dram_tensor("na_tt", [DEPTH, 128, 8, 17, 64], F32, kind="ExternalInput").ap()

    U = {}
    for nm, w in (("z", 1024), ("gv", 1024), ("gg", 1024), ("nv", 1024)):
        U[nm] = P.dram("u_" + nm, [S, w], BF16)
    U["dt"] = P.dram("u_dt", [S, 32], F32)
    for nm, w in (("xbc", 1536), ("gq", 512), ("gk", 512), ("nq", 1024), ("nk", 1024), ("gate", 3072)):
        U[nm] = P.dram("u_" + nm + "T", [w, S], BF16)
    U["ga"] = P.dram("u_gaT", [32, S], BF16)
    YB = {nm: P.dram("yb_" + nm, [1024, S], BF16) for nm in ("ssd", "gla", "na")}
    ybw = P.dram("ybw", [S, 1024], BF16)

    with contextlib.ExitStack() as top:
        sc = Sched(nc, top)
        G = {}
        G["x"] = P.sb(top, "x_res", [128, NT, D], F32)
        G["xt"] = sc.tiles_n("x", NT)
        G["ident"] = P.sb(top, "ident", [128, 128], BF16)
        G["ident_t"] = sc.tile("ident")
        G["dram_t"] = {k: sc.tile("d_" + k) for k in list(U) + ["yb_ssd", "yb_gla", "yb_na", "ybw"]}
        G["ybw"] = ybw

        ones_f = P.sb(top, "ones_f", [128, 128], F32)
        ones_t = sc.tile("ones_f")
        sc.op("pool", lambda e: e.memset(ones_f[:], 1.0), writes=[ones_t])
        sc.op("pool", lambda e: e.affine_select(out=G["ident"][:], in_=ones_f[:], pattern=[[-1, 128]],
                                                compare_op=ALU.is_equal, fill=0.0, base=0,
                                                channel_multiplier=1),
              reads=[ones_t], writes=[G["ident_t"]])
        G["ones_f"] = ones_f
        G["eps"] = P.sb(top, "epsc", [128, 2], F32)
        G["eps_t"] = sc.tile("epsc")
        sc.op("pool", lambda e: e.memset(G["eps"][:], EPS), writes=[G["eps_t"]])
        G["one"] = P.sb(top, "onec", [128, 2], F32)
        G["one_t"] = sc.tile("onec")
        sc.op("pool", lambda e: e.memset(G["one"][:], 1.0), writes=[G["one_t"]])
        G["neghalf"] = P.sb(top, "neghalf", [128, 16], F32)
        G["neghalf_t"] = sc.tile("neghalf")
        sc.op("pool", lambda e: e.memset(G["neghalf"][:], -0.5), writes=[G["neghalf_t"]])
        G["ones_t"] = ones_t

        build_tri(P, sc, G, top)
        xv = x_in.rearrange("(i p) d -> p i d", p=128)
        for i in range(NT):
            sc.dma("sp", G["x"][:, i, :], xv[:, i, :], owner=G["xt"][i], writes=[G["xt"][i]])

        yv = y_out.rearrange("(i p) d -> p i d", p=128)
        outt = sc.tile("yout")
        stored = False
        for l in range(layers):
            if "A" in phases:
                phase_A(P, sc, G, U, prm, l)
            if "B" in phases:
                phase_B(P, sc, G, U, YB, prm, l)
            if "C" in phases:
                phase_C(P, sc, G, U, YB, prm, l)
            if "D" in phases:
                phase_D(P, sc, G, U, YB, prm, natt, l)
            if "E" in phases:
                phase_E(P, sc, G, U, YB, prm, l)
            if "F" in phases:
                phase_F(P, sc, G, prm, l, y_store=((yv, outt) if l == layers - 1 else None))
                stored = (l == layers - 1)

        if not stored:
            for i in range(NT):
                sc.dma("sp", yv[:, i, :], G["x"][:, i, :], owner=G["xt"][i], reads=[G["xt"][i]], writes=[outt],
                       part=True)
        sc.op("sp", lambda e: e.nop(), reads=[outt])
        sc.barrier()
        sc.emit()
    return nc


def rms_transpose(P, sc, G, ph, wrow_ap, hT, hT_t, l, tag):
    nc = P.nc
    wb = P.sb(ph, tag + "_wb", [128, D], F32)
    wb_t = sc.tile(tag + "_wb")
    sc.dma("sp", wb[:], wrow_ap.partition_broadcast(128), owner=wb_t, writes=[wb_t])
    junk = [P.sb(ph, tag + "_junk%d" % i, [128, D], BF16) for i in range(2)]
    junk_t = sc.tiles_n(tag + "_junk", 2)
    hb = [P.sb(ph, tag + "_hb%d" % i, [128, D], BF16) for i in range(2)]
    hb_t = sc.tiles_n(tag + "_hb", 2)
    ss = [P.sb(ph, tag + "_ss%d" % i, [128, 2], F32) for i in range(2)]
    ss_t = sc.tiles_n(tag + "_ss", 2)
    tp = [P.ps(ph, tag + "_tp%d" % i, [128, 4, 128], BF16) for i in range(2)]
    tp_t = sc.tiles_n(tag + "_tp", 2)
    x = G["x"]
    new_tiles = [wb_t] + junk_t + hb_t + ss_t + tp_t
    def _s1(i):
        b = i % 2
        xt = G["xt"][i]
        sc.op("dve", lambda e, i=i, b=b: e.scalar_tensor_tensor(out=junk[b][:], in0=x[:, i, :], scalar=1.0,
                                                                in1=x[:, i, :], op0=ALU.mult, op1=ALU.mult,
                                                                accum_out=ss[b][:, 0:1]),
              reads=[xt], writes=[junk_t[b], ss_t[b]])
        sc.op("dve", lambda e, b=b: e.tensor_scalar(out=ss[b][:, 1:2], in0=ss[b][:, 0:1], scalar1=1.0 / D,
                                                    scalar2=EPS, op0=ALU.mult, op1=ALU.add),
              reads=[ss_t[b]], writes=[ss_t[b]])
        sc.op("pool", lambda e, b=b: e.tensor_tensor(out=ss[b][:, 0:1], in0=ss[b][:, 1:2],
                                                     in1=G["neghalf"][:, 0:1], op=ALU.pow),
              reads=[ss_t[b], G["neghalf_t"]], writes=[ss_t[b]])

    def _s2(i):
        b = i % 2
        xt = G["xt"][i]
        sc.op("dve", lambda e, i=i, b=b: e.scalar_tensor_tensor(out=hb[b][:], in0=x[:, i, :],
                                                                scalar=ss[b][:, 0:1], in1=wb[:],
                                                                op0=ALU.mult, op1=ALU.mult),
              reads=[xt, ss_t[b], wb_t], writes=[hb_t[b]])
        for half in range(2):
            pb = (2 * i + half) % 2
            for j in range(4):
                kc = half * 4 + j
                sc.op("pe", lambda e, b=b, pb=pb, j=j, kc=kc: e.transpose(
                    out=tp[pb][:, j, :], in_=hb[b][:, kc * 128:(kc + 1) * 128], identity=G["ident"][:]),
                    reads=[hb_t[b], G["ident_t"]], writes=[tp_t[pb]], part=(j > 0))
            eng = "act"
            if eng == "act":
                sc.op("act", lambda e, pb=pb, half=half, i=i: e.copy(
                    out=hT[:, half * 4:half * 4 + 4, i * 128:(i + 1) * 128], in_=tp[pb][:]),
                    reads=[tp_t[pb]], writes=[hT_t[i]], part=True)
            else:
                sc.op("dve", lambda e, pb=pb, half=half, i=i: e.tensor_copy(
                    out=hT[:, half * 4:half * 4 + 4, i * 128:(i + 1) * 128], in_=tp[pb][:]),
                    reads=[tp_t[pb]], writes=[hT_t[i]], part=True)

    _s1(0)
    for i in range(NT):
        if i + 1 < NT:
            _s1(i + 1)
        _s2(i)
    return new_tiles


class WStream:
    def __init__(self, P, sc, ph, tag, kdim, ncol, nf=2, nb=3, cast_eng="pool"):
        self.sc = sc
        self.cast_eng = cast_eng
        self.kdim, self.ncol = kdim, ncol
        self.nf, self.nb = nf, nb
        self.f = [P.sb(ph, "%s_wf%d" % (tag, i), [128, kdim, ncol], F32) for i in range(nf)]
        self.f_t = sc.tiles_n(tag + "_wf", nf)
        self.b = [P.sb(ph, "%s_wb%d" % (tag, i), [128, kdim, ncol], BF16) for i in range(nb)]
        self.b_t = sc.tiles_n(tag + "_wbt", nb)
        self.tiles = self.f_t + self.b_t
        self.items = []

    def start(self, items):
        self.items = items
        self._load(0)
        self._load(1)
        self._cast(0)

    def _load(self, g):
        if g >= len(self.items):
            return
        ap, k, n = self.items[g]
        fs = g % self.nf
        self.sc.dma("sp", self.f[fs][:, 0:k, 0:n], ap, owner=self.f_t[fs], writes=[self.f_t[fs]])

    def _cast(self, g):
        if g >= len(self.items):
            return
        ap, k, n = self.items[g]
        fs, bs = g % self.nf, g % self.nb
        if self.cast_eng == "act":
            self.sc.op("act", lambda e: e.copy(out=self.b[bs][:, 0:k, 0:n], in_=self.f[fs][:, 0:k, 0:n]),
                       reads=[self.f_t[fs]], writes=[self.b_t[bs]])
        else:
            self.sc.op("pool", lambda e: e.tensor_copy(out=self.b[bs][:, 0:k, 0:n], in_=self.f[fs][:, 0:k, 0:n]),
                       reads=[self.f_t[fs]], writes=[self.b_t[bs]])

    def get(self, g):
        self._cast(g + 1)
        self._load(g + 2)
        return self.b[g % self.nb], self.b_t[g % self.nb]


def proj_groups():
    g = []

    def seg(off, n, mode, key):
        c = 0
        while c < n:
            w = min(512, n - c)
            g.append((off + c, w, mode, key, c))
            c += w
    seg(O_Z, 1024, "tok", "z")
    seg(O_XBC, 1536, "feat", "xbc")
    g.append((O_DTF, 32, "tok32", "dt", 0))
    seg(O_GQ, 512, "feat", "gq")
    seg(O_GK, 512, "feat", "gk")
    seg(O_GV, 1024, "tok", "gv")
    seg(O_GG, 1024, "tok", "gg")
    g.append((O_GAF, 32, "feat32", "ga", 0))
    seg(O_NQ, 1024, "feat", "nq")
    seg(O_NK, 1024, "feat", "nk")
    seg(O_NV, 1024, "tok", "nv")
    seg(O_GATE, 3072, "feat", "gate")
    return g


def phase_A(P, sc, G, U, prm, l):
    nc = P.nc
    with contextlib.ExitStack() as ph:
        hT = P.sb(ph, "A_hT", [128, 8, S], BF16)
        hT_t = sc.tiles_n("A_hT", NT)
        tiles = list(hT_t)
        tiles += rms_transpose(P, sc, G, ph, prm["norm_mix_w"][l], hT, hT_t, l, "A")
        wst = WStream(P, sc, ph, "A", 8, 512)
        acc = [P.ps(ph, "A_acc%d" % i, [128, 512], F32) for i in range(4)]
        acc_t = sc.tiles_n("A_acc", 4)
        NS = 3
        stg = [P.sb(ph, "A_stg%d" % i, [128, 2048], BF16) for i in range(NS)]
        stg_t = sc.tiles_n("A_stg", NS)
        stf = [P.sb(ph, "A_stf%d" % i, [128, 4, 32], F32) for i in range(2)]
        stf_t = sc.tiles_n("A_stf", 2)
        G["ga_stage"] = P.sb(ph, "A_gast", [32, 2048], BF16)
        G["ga_stage_t"] = sc.tile("A_gast")
        tiles += wst.tiles + acc_t + stg_t + stf_t + [G["ga_stage_t"]]
        wv = prm["w_in"][l].rearrange("(kc p) n -> p kc n", p=128)
        groups = proj_groups()
        wst.start([(wv[:, :, c0:c0 + n], 8, n) for (c0, n, _m, _k, _d) in groups])
        ai = 0
        si = 0
        ev = 0
        for gi, (c0, n, mode, key, doff) in enumerate(groups):
            wcur, wcur_t = wst.get(gi)
            dst = U[key]
            dst_t = G["dram_t"][key]
            if mode in ("tok", "tok32"):
                for tb in range(4):
                    if mode == "tok":
                        st = si % NS
                        si += 1
                    else:
                        st = tb % 2
                    for j in range(4):
                        i = tb * 4 + j
                        a = ai % 4
                        ai += 1
                        for kc in range(8):
                            sc.op("pe", lambda e, a=a, kc=kc, i=i, wcur=wcur, n=n: e.matmul(
                                acc[a][:, 0:n], lhsT=hT[:, kc, i * 128:(i + 1) * 128], rhs=wcur[:, kc, 0:n],
                                start=(kc == 0), stop=(kc == 7)),
                                reads=[hT_t[i], wcur_t], writes=[acc_t[a]], part=(kc > 0))
                        if mode == "tok":
                            o_ap = stg[st][:, j * 512:j * 512 + n]
                            o_t = stg_t[st]
                        else:
                            o_ap = stf[st][:, j, 0:n]
                            o_t = stf_t[st]
                        ev += 1
                        if ev % 2 == 0:
                            sc.op("act", lambda e, o_ap=o_ap, a=a, n=n: e.copy(out=o_ap, in_=acc[a][:, 0:n]),
                                  reads=[acc_t[a]], writes=[o_t], part=(j > 0))
                        else:
                            sc.op("dve", lambda e, o_ap=o_ap, a=a, n=n: e.tensor_copy(out=o_ap, in_=acc[a][:, 0:n]),
                                  reads=[acc_t[a]], writes=[o_t], part=(j > 0))
                    rows = dst[tb * 512:(tb + 1) * 512, doff:doff + n].rearrange("(j p) c -> p j c", p=128)
                    if mode == "tok":
                        src = stg[st][:].rearrange("p (j c) -> p j c", j=4)[:, :, 0:n]
                        sc.dma("pool", rows, src, owner=stg_t[st], reads=[stg_t[st]], writes=[dst_t], part=True)
                    else:
                        sc.dma("pool", rows, stf[st][:, :, 0:n], owner=stf_t[st], reads=[stf_t[st]], writes=[dst_t],
                               part=True)
            else:
                nchunk = (n + 127) // 128
                for c in range(nchunk):
                    m = min(128, n - c * 128)
                    if mode == "feat":
                        st = si % NS
                        si += 1
                    else:
                        st = 0
                    for tb in range(4):
                        a = ai % 4
                        ai += 1
                        for kc in range(8):
                            sc.op("pe", lambda e, a=a, kc=kc, tb=tb, wcur=wcur, c=c, m=m: e.matmul(
                                acc[a][0:m, :], lhsT=wcur[:, kc, c * 128:c * 128 + m],
                                rhs=hT[:, kc, tb * 512:(tb + 1) * 512], start=(kc == 0), stop=(kc == 7)),
                                reads=hT_t[tb * 4:tb * 4 + 4] + [wcur_t], writes=[acc_t[a]], part=(kc > 0))
                        ev += 1
                        if mode == "feat":
                            o_ap = stg[st][0:m, tb * 512:(tb + 1) * 512]
                            o_t = stg_t[st]
                            if ev % 2 == 0:
                                sc.op("act", lambda e, o_ap=o_ap, a=a, m=m: e.copy(out=o_ap, in_=acc[a][0:m, :]),
                                      reads=[acc_t[a]], writes=[o_t], part=(tb > 0))
                            else:
                                sc.op("dve", lambda e, o_ap=o_ap, a=a, m=m: e.tensor_copy(out=o_ap, in_=acc[a][0:m, :]),
                                      reads=[acc_t[a]], writes=[o_t], part=(tb > 0))
                        else:
                            sc.op("dve", lambda e, a=a, m=m, tb=tb, gast=G["ga_stage"]: e.tensor_copy(
                                out=gast[0:m, tb * 512:(tb + 1) * 512], in_=acc[a][0:m, :]),
                                reads=[acc_t[a]], writes=[G["ga_stage_t"]], part=(tb > 0))
                    if mode == "feat":
                        sc.dma("pool", dst[doff + c * 128:doff + c * 128 + m, :], stg[st][0:m, :], owner=stg_t[st],
                               reads=[stg_t[st]], writes=[dst_t], part=True)
                    else:
                        sc.dma("pool", dst[0:m, :], G["ga_stage"][0:m, :], owner=G["ga_stage_t"],
                               reads=[G["ga_stage_t"]], writes=[dst_t], part=True)
        sc.barrier(release=tiles)


def phase_E(P, sc, G, U, YB, prm, l):
    nc = P.nc
    x = G["x"]
    with contextlib.ExitStack() as ph:
        wst = WStream(P, sc, ph, "E", 8, 256, nf=2, nb=2, cast_eng="act")
        mix = P.sb(ph, "E_mix", [128, 8, 1024], F32)
        mix_t = sc.tiles_n("E_mix", 8)
        mixb = P.sb(ph, "E_mixb", [128, 8, 1024], BF16)
        mixb_t = sc.tile("E_mixb")
        ybT = [P.sb(ph, "E_yb%d" % i, [128, 8, 1024], BF16) for i in range(2)]
        ybT_t = sc.tiles_n("E_yb", 2)
        gsl = [P.sb(ph, "E_g%d" % i, [128, 1024], BF16) for i in range(3)]
        gsl_t = sc.tiles_n("E_g", 3)
        sig = [P.sb(ph, "E_sig%d" % i, [128, 1024], F32) for i in range(2)]
        sig_t = sc.tiles_n("E_sig", 2)
        tmp = [P.sb(ph, "E_tmp%d" % i, [128, 512], F32) for i in range(2)]
        tmp_t = sc.tiles_n("E_tmp", 2)
        acc = [P.ps(ph, "E_acc%d" % i, [128, 512], F32) for i in range(4)]
        acc_t = sc.tiles_n("E_acc", 4)
        tiles = wst.tiles + mix_t + [mixb_t] + ybT_t + gsl_t + sig_t + tmp_t + acc_t
        wnames = ["w_branch_ssd", "w_branch_gla", "w_branch_na", "w_out"]
        bnames = ["ssd", "gla", "na"]
        items = []
        for half in range(2):
            for wn in wnames:
                wv = prm[wn][l].rearrange("(kc p) n -> p kc n", p=128)
                for cg in range(4):
                    items.append((wv[:, :, cg * 256:(cg + 1) * 256], 8, 256))
        wst.start(items)
        gi = 0
        ai = 0
        gcount = 0
        tcount = 0
        ybcount = 0
        for half in range(2):
            t0 = half * 1024
            for b in range(3):
                ys = ybcount % 2
                ybcount += 1
                ybv = YB[bnames[b]].rearrange("(kc p) t -> p kc t", p=128)
                sc.dma("sp", ybT[ys][:], ybv[:, :, t0:t0 + 1024], owner=ybT_t[ys],
                       reads=[G["dram_t"]["yb_" + bnames[b]]], writes=[ybT_t[ys]])
                for cg in range(4):
                    wcur, wcur_t = wst.get(gi)
                    gi += 1
                    for ecl in range(2):
                        ec = cg * 2 + ecl
                        gs = gcount % 3
                        ss_ = gcount % 2
                        gcount += 1
                        grow = b * 1024 + ec * 128
                        sc.dma("sp", gsl[gs][:], U["gate"][grow:grow + 128, t0:t0 + 1024], owner=gsl_t[gs],
                               reads=[G["dram_t"]["gate"]], writes=[gsl_t[gs]])
                        sc.op("act", lambda e, gs=gs, ss_=ss_: e.activation(out=sig[ss_][:], in_=gsl[gs][:],
                                                                            func=AF.Sigmoid),
                              reads=[gsl_t[gs]], writes=[sig_t[ss_]])
                        for tbh in range(2):
                            a = ai % 4
                            ai += 1
                            for kc in range(8):
                                sc.op("pe", lambda e, a=a, kc=kc, wcur=wcur, ecl=ecl, ys=ys, tbh=tbh: e.matmul(
                                    acc[a][:], lhsT=wcur[:, kc, ecl * 128:(ecl + 1) * 128],
                                    rhs=ybT[ys][:, kc, tbh * 512:(tbh + 1) * 512], start=(kc == 0), stop=(kc == 7)),
                                    reads=[wcur_t, ybT_t[ys]], writes=[acc_t[a]], part=(kc > 0))
                            msl = mix[:, ec, tbh * 512:(tbh + 1) * 512]
                            sgl = sig[ss_][:, tbh * 512:(tbh + 1) * 512]
                            if b == 0:
                                sc.op("dve", lambda e, msl=msl, a=a, sgl=sgl: e.tensor_tensor(
                                    out=msl, in0=acc[a][:], in1=sgl, op=ALU.mult),
                                    reads=[acc_t[a], sig_t[ss_]], writes=[mix_t[ec]], part=(tbh > 0))
                            else:
                                ts = tcount % 2
                                tcount += 1
                                sc.op("dve", lambda e, ts=ts, a=a, sgl=sgl: e.tensor_tensor(
                                    out=tmp[ts][:], in0=acc[a][:], in1=sgl, op=ALU.mult),
                                    reads=[acc_t[a], sig_t[ss_]], writes=[tmp_t[ts]])
                                if b == 1:
                                    sc.op("pool", lambda e, msl=msl, ts=ts: e.tensor_tensor(
                                        out=msl, in0=msl, in1=tmp[ts][:], op=ALU.add),
                                        reads=[tmp_t[ts], mix_t[ec]], writes=[mix_t[ec]])
                                else:
                                    sc.op("pool", lambda e, msl=msl, ts=ts, ec=ec, tbh=tbh: e.tensor_tensor(
                                        out=mixb[:, ec, tbh * 512:(tbh + 1) * 512], in0=msl, in1=tmp[ts][:],
                                        op=ALU.add),
                                        reads=[tmp_t[ts], mix_t[ec]], writes=[mixb_t], part=True)
            for cg in range(4):
                wcur, wcur_t = wst.get(gi)
                gi += 1
                for j in range(8):
                    i = half * 8 + j
                    a = ai % 4
                    ai += 1
                    for ec in range(8):
                        sc.op("pe", lambda e, a=a, ec=ec, wcur=wcur, j=j: e.matmul(
                            acc[a][:, 0:256], lhsT=mixb[:, ec, j * 128:(j + 1) * 128], rhs=wcur[:, ec, :],
                            start=(ec == 0), stop=(ec == 7)),
                            reads=[wcur_t, mixb_t], writes=[acc_t[a]], part=(ec > 0))
                    xs = x[:, i, cg * 256:(cg + 1) * 256]
                    sc.op("dve", lambda e, xs=xs, a=a: e.tensor_tensor(out=xs, in0=xs, in1=acc[a][:, 0:256], op=ALU.add),
                          reads=[acc_t[a], G["xt"][i]], writes=[G["xt"][i]])
        sc.barrier(release=tiles)


class Ring:
    def __init__(self, P, sc, stack, name, shape, dt, n, psum=False, views=None):
        if views is not None:
            self.h = views
            n = len(views)
        else:
            mk = P.ps if psum else P.sb
            self.h = [mk(stack, "%s%d" % (name, i), shape, dt) for i in range(n)]
        self.t = sc.tiles_n(name + "_", n)
        self.i = 0
        self.n = n

    def next(self):
        k = self.i % self.n
        self.i += 1
        return self.h[k], self.t[k]


def build_tri(P, sc, G, top):
    for nm in ("trif", "trib", "trif64", "trib64", "mcf64", "mcb64", "trifs", "tribs"):
        G[nm] = P.sb(top, nm, [128, 128], F32)
        G[nm + "_t"] = sc.tile(nm)
    ones_f, ones_t = G["ones_f"], G["ones_t"]
    sc.op("pool", lambda e: e.affine_select(out=G["trif"][:], in_=ones_f[:], pattern=[[1, 128]], compare_op=ALU.is_ge,
                                            fill=0.0, base=0, channel_multiplier=-1),
          reads=[ones_t], writes=[G["trif_t"]])
    sc.op("pool", lambda e: e.affine_select(out=G["trib"][:], in_=ones_f[:], pattern=[[-1, 128]], compare_op=ALU.is_ge,
                                            fill=0.0, base=0, channel_multiplier=1),
          reads=[ones_t], writes=[G["trib_t"]])
    sc.op("pool", lambda e: e.affine_select(out=G["trifs"][:], in_=ones_f[:], pattern=[[1, 128]], compare_op=ALU.is_gt,
                                            fill=0.0, base=0, channel_multiplier=-1),
          reads=[ones_t], writes=[G["trifs_t"]])
    sc.op("pool", lambda e: e.affine_select(out=G["tribs"][:], in_=ones_f[:], pattern=[[-1, 128]], compare_op=ALU.is_gt,
                                            fill=0.0, base=0, channel_multiplier=1),
          reads=[ones_t], writes=[G["tribs_t"]])
    sc.op("pool", lambda e: e.tensor_copy(out=G["trif64"][:], in_=G["trif"][:]), reads=[G["trif_t"]], writes=[G["trif64_t"]])
    sc.op("pool", lambda e: e.memset(G["trif64"][0:64, 64:128], 0.0), reads=[G["trif64_t"]], writes=[G["trif64_t"]])
    sc.op("pool", lambda e: e.tensor_copy(out=G["trib64"][:], in_=G["trib"][:]), reads=[G["trib_t"]], writes=[G["trib64_t"]])
    sc.op("pool", lambda e: e.memset(G["trib64"][64:128, 0:64], 0.0), reads=[G["trib64_t"]], writes=[G["trib64_t"]])
    sc.op("pool", lambda e: e.tensor_scalar(out=G["mcf64"][:], in0=G["trif64"][:], scalar1=-1.0 / 16.0, scalar2=None,
                                            op0=ALU.mult), reads=[G["trif64_t"]], writes=[G["mcf64_t"]])
    sc.op("pool", lambda e: e.tensor_scalar(out=G["mcb64"][:], in0=G["trib64"][:], scalar1=-1.0 / 16.0, scalar2=None,
                                            op0=ALU.mult), reads=[G["trib64_t"]], writes=[G["mcb64_t"]])
    for nm in ("mcf64", "mcb64", "trifs", "tribs"):
        G[nm + "b"] = P.sb(top, nm + "b", [128, 128], BF16)
        G[nm + "b_t"] = sc.tile(nm + "b")
        sc.op("pool", lambda e, nm=nm: e.tensor_copy(out=G[nm + "b"][:], in_=G[nm][:]), reads=[G[nm + "_t"]],
              writes=[G[nm + "b_t"]])


def phase_C(P, sc, G, U, YB, prm, l):
    nc = P.nc
    ident = G["ident"]
    with contextlib.ExitStack() as ph:
        qT = P.sb(ph, "C_qT", [128, 4, S], BF16)
        kT = P.sb(ph, "C_kT", [128, 4, S], BF16)
        qT_t = sc.tile("C_qT")
        kT_t = sc.tile("C_kT")
        ob = P.sb(ph, "C_ob", [128, NT, 1024], BF16)
        ob_t = sc.tiles_n("C_ob", NT)
        gaX = P.sb(ph, "C_gaX", [32, S], BF16)
        gaX_t = sc.tile("C_gaX")
        a2X = [P.sb(ph, "C_a2X%d" % d, [32, 512], BF16) for d in range(2)]
        a2X_t = sc.tiles_n("C_a2X", 2)
        a2f = P.sb(ph, "C_a2f", [32, 512], F32)
        a2f_t = sc.tile("C_a2f")
        nwb = P.sb(ph, "C_nwb", [128, 256], F32)
        nwb_t = sc.tile("C_nwb")
        Sf = P.sb(ph, "C_Sf", [128, 4, 256], F32)
        Sf_t = sc.tile("C_Sf")
        yst = P.sb(ph, "C_yst", [128, 8, 256], BF16)
        yst_t = sc.tile("C_yst")
        R = lambda name, shape, dt, n, psum=False: Ring(P, sc, ph, "C_" + name, shape, dt, n, psum)
        r_Sb = R("Sb", [128, 4, 256], BF16, 3)
        r_v = R("v", [128, 1024], BF16, 2)
        r_gg = R("gg", [128, 4, 1024], BF16, 1)
        r_e1 = R("e1", [128, 512], F32, 1)
        r_gn = R("gn", [128, 512], BF16, 1)
        r_eb = R("eb", [128, 4, 128], F32, 1)
        r_enb = R("enb", [128, 4, 128], F32, 1)
        r_ew = R("ew", [128, 4, 128], F32, 1)
        r_ed = R("ed", [128, 4, 2], F32, 3)
        r_qd = R("qd", [128, 4, 128], BF16, 2)
        r_kd = R("kd", [128, 4, 128], BF16, 2)
        r_kw = R("kw", [128, 4, 128], BF16, 1)
        r_kwt = R("kwt", [128, 4, 128], BF16, 2)
        r_am = R("am", [128, 4, 128], BF16, 2)
        r_oa = R("oa", [128, 1024], F32, 1)
        r_sg = R("sg", [128, 4, 1024], BF16, 1)
        r_jk = R("jk", [128, 256], BF16, 1)
        r_ss = R("ss", [128, 8], F32, 2)
        r_y = R("y", [128, 1024], BF16, 2)
        r_gp = R("gp", [128, 512], F32, 1, True)
        r_bT = R("bT", [128, 4, 128], F32, 1, True)
        r_att = R("att", [128, 4, 128], F32, 1, True)
        r_kwp = R("kwp", [128, 4, 128], BF16, 1, True)
        r_st = R("st", [128, 4, 256], F32, 1, True)
        r_o = R("o", [128, 4, 256], F32, 1, True)
        rings = [r_Sb, r_v, r_gg, r_e1, r_gn, r_eb, r_enb, r_ew, r_ed, r_qd, r_kd, r_kw, r_kwt, r_am, r_oa, r_sg,
                 r_jk, r_ss, r_y, r_gp, r_bT, r_att, r_kwp, r_st, r_o]
        tiles = [qT_t, kT_t, nwb_t, yst_t, gaX_t, Sf_t, a2f_t] + ob_t + a2X_t
        for r in rings:
            tiles += r.t
        sc.dma("sp", qT[:], U["gq"].rearrange("(h p) t -> p h t", p=128), owner=qT_t, reads=[G["dram_t"]["gq"]],
               writes=[qT_t])
        sc.dma("sp", kT[:], U["gk"].rearrange("(h p) t -> p h t", p=128), owner=kT_t, reads=[G["dram_t"]["gk"]],
               writes=[kT_t])
        sc.dma("sp", nwb[:], prm["gla_norm_w"][l].partition_broadcast(128), owner=nwb_t, writes=[nwb_t])
        for d in range(2):
            a2 = prm["gla_a2_f" if d == 0 else "gla_a2_b"][l]
            bi = prm["gla_a2_bias_f" if d == 0 else "gla_a2_bias_b"][l]
            sc.dma("sp", a2f[0:16, :], a2, owner=a2f_t, writes=[a2f_t])
            sc.dma("sp", a2f[16:17, :], bi.rearrange("(o n) -> o n", o=1), owner=a2f_t, writes=[a2f_t], part=True)
            sc.op("act", lambda e, d=d: e.copy(out=a2X[d][0:17, :], in_=a2f[0:17, :]), reads=[a2f_t], writes=[a2X_t[d]])

        def gla_pass(d):
            fwd = (d == 0)
            mc, mc_t = (G["mcf64b"], G["mcf64b_t"]) if fwd else (G["mcb64b"], G["mcb64b_t"])
            ma, ma_t = (G["trif64"], G["trif64_t"]) if fwd else (G["trib64"], G["trib64_t"])
            lc0 = 63 if fwd else 0
            sc.op("pool", lambda e: e.memset(gaX[:], 1.0), writes=[gaX_t])
            sc.dma("sp", gaX[0:16, :], U["ga"][16 * d:16 * d + 16, :], owner=gaX_t, reads=[G["dram_t"]["ga"]],
                   writes=[gaX_t])
            sc.op("pool", lambda e: e.memset(Sf[:], 0.0), writes=[Sf_t])
            sb0, sb0_t = r_Sb.next()
            sc.op("pool", lambda e, sb0=sb0: e.memset(sb0[:], 0.0), writes=[sb0_t])
            cur = [(sb0, sb0_t)]
            sgcur = [None]
            order = list(range(NT)) if fwd else list(range(NT - 1, -1, -1))
            chunks = (0, 1) if fwd else (1, 0)

            def stage1(i):
                tsl = slice(i * 128, (i + 1) * 128)
                v, v_t = r_v.next()
                sc.dma("sp", v[:], U["gv"][tsl, :], owner=v_t, reads=[G["dram_t"]["gv"]], writes=[v_t])
                gp, gp_t = r_gp.next()
                sc.op("pe", lambda e, gp=gp, tsl=tsl: e.matmul(gp[:], lhsT=gaX[0:17, tsl], rhs=a2X[d][0:17, :],
                                                               start=True, stop=True),
                      reads=[gaX_t, a2X_t[d]], writes=[gp_t])
                e1, e1_t = r_e1.next()
                sc.op("act", lambda e, e1=e1, gp=gp: e.activation(out=e1[:], in_=gp[:], func=AF.Exp, scale=-1.0),
                      reads=[gp_t], writes=[e1_t])
                gn, gn_t = r_gn.next()
                sc.op("act", lambda e, gn=gn, e1=e1: e.activation(out=gn[:], in_=e1[:], func=AF.Ln, bias=G["one"][:, 0:1]),
                      reads=[e1_t, G["one_t"]], writes=[gn_t])
                bT, bT_t = r_bT.next()
                for h in range(4):
                    sc.op("pe", lambda e, bT=bT, gn=gn, h=h: e.matmul(bT[:, h, :], lhsT=gn[:, h * 128:(h + 1) * 128], rhs=mc[:],
                                                                     start=True, stop=True, skip_group_check=True),
                          reads=[gn_t, mc_t], writes=[bT_t], part=(h > 0))
                bs, bs_t = bT, bT_t
                eb, eb_t = r_eb.next()
                sc.op("act", lambda e, eb=eb, bs=bs: e.activation(out=eb[:], in_=bs[:], func=AF.Exp), reads=[bs_t], writes=[eb_t])
                enb, enb_t = r_enb.next()
                sc.op("act", lambda e, enb=enb, bs=bs: e.activation(out=enb[:], in_=bs[:], func=AF.Exp, scale=-1.0),
                      reads=[bs_t], writes=[enb_t])
                ed, ed_t = r_ed.next()
                sc.op("act", lambda e, ed=ed, bs=bs: e.activation(
                    out=ed[:], in_=bs[:].rearrange("p h (c l) -> p h c l", c=2)[:, :, :, lc0], func=AF.Exp),
                    reads=[bs_t], writes=[ed_t])
                qd, qd_t = r_qd.next()
                sc.op("dve", lambda e, qd=qd, tsl=tsl, eb=eb: e.scalar_tensor_tensor(
                    out=qd[:], in0=qT[:, :, tsl], scalar=128.0 ** -0.5, in1=eb[:], op0=ALU.mult, op1=ALU.mult),
                    reads=[qT_t, eb_t], writes=[qd_t])
                kd, kd_t = r_kd.next()
                sc.op("dve", lambda e, kd=kd, tsl=tsl, enb=enb: e.tensor_tensor(
                    out=kd[:], in0=kT[:, :, tsl], in1=enb[:], op=ALU.mult), reads=[kT_t, enb_t], writes=[kd_t])
                ew, ew_t = r_ew.next()
                sc.op("dve", lambda e, ew=ew, enb=enb, ed=ed: e.tensor_tensor(
                    out=ew[:].rearrange("p h (c l) -> p (h c) l", c=2), in0=enb[:].rearrange("p h (c l) -> p (h c) l", c=2),
                    in1=ed[:].rearrange("p h c -> p (h c)").unsqueeze(2).to_broadcast([128, 8, 64]), op=ALU.mult),
                    reads=[enb_t, ed_t], writes=[ew_t])
                kw, kw_t = r_kw.next()
                sc.op("dve", lambda e, kw=kw, tsl=tsl, ew=ew: e.tensor_tensor(
                    out=kw[:], in0=kT[:, :, tsl], in1=ew[:], op=ALU.mult), reads=[kT_t, ew_t], writes=[kw_t])
                kwp, kwp_t = r_kwp.next()
                for h in range(4):
                    sc.op("pe", lambda e, kwp=kwp, kw=kw, h=h: e.transpose(out=kwp[:, h, :], in_=kw[:, h, :], identity=ident[:]),
                          reads=[kw_t, G["ident_t"]], writes=[kwp_t], part=(h > 0))
                kwt, kwt_t = r_kwt.next()
                sc.op("act", lambda e, kwt=kwt, kwp=kwp: e.copy(out=kwt[:], in_=kwp[:]), reads=[kwp_t], writes=[kwt_t])
                att, att_t = r_att.next()
                for h in range(4):
                    sc.op("pe", lambda e, att=att, kd=kd, qd=qd, h=h: e.matmul(att[:, h, :], lhsT=kd[:, h, :], rhs=qd[:, h, :],
                                                                            start=True, stop=True, skip_group_check=True),
                          reads=[kd_t, qd_t], writes=[att_t], part=(h > 0))
                am, am_t = r_am.next()
                sc.op("dve", lambda e, am=am, att=att: e.tensor_tensor(
                    out=am[:], in0=att[:], in1=ma[:].unsqueeze(1).to_broadcast([128, 4, 128]), op=ALU.mult),
                    reads=[att_t, ma_t], writes=[am_t])
                return (i, tsl, v, v_t, qd, qd_t, kwt, kwt_t, ed, ed_t, am, am_t)

            def stage23(ctx):
                (i, tsl, v, v_t, qd, qd_t, kwt, kwt_t, ed, ed_t, am, am_t) = ctx
                sbs = [cur[0]]
                for ci, c in enumerate(chunks):
                    cs = slice(c * 64, (c + 1) * 64)
                    st, st_t = r_st.next()
                    for h in range(4):
                        sc.op("pe", lambda e, st=st, kwt=kwt, cs=cs, v=v, h=h: e.matmul(
                            st[:, h, :], lhsT=kwt[cs, h, :], rhs=v[cs, h * 256:(h + 1) * 256], start=True, stop=True,
                            skip_group_check=True),
                            reads=[kwt_t, v_t], writes=[st_t], part=(h > 0))
                    for h in range(4):
                        sc.op("dve", lambda e, st=st, h=h, ed=ed, c=c: e.scalar_tensor_tensor(
                            out=Sf[:, h, :], in0=Sf[:, h, :], scalar=ed[:, h, c:c + 1], in1=st[:, h, :], op0=ALU.mult,
                            op1=ALU.add),
                            reads=[st_t, ed_t, Sf_t], writes=[Sf_t])
                    nb, nb_t = r_Sb.next()
                    sc.op("act", lambda e, nb=nb: e.copy(out=nb[:], in_=Sf[:]), reads=[Sf_t], writes=[nb_t])
                    sbs.append((nb, nb_t))
                o, o_t = r_o.next()
                for h in range(4):
                    sc.op("pe", lambda e, o=o, am=am, v=v, h=h: e.matmul(o[:, h, :], lhsT=am[:, h, :],
                                                                       rhs=v[:, h * 256:(h + 1) * 256],
                                                                       start=True, stop=False, skip_group_check=True),
                          reads=[am_t, v_t], writes=[o_t], part=(h > 0))
                    for ci, c in enumerate(chunks):
                        cs = slice(c * 64, (c + 1) * 64)
                        sbv, sbv_t = sbs[ci]
                        sc.op("pe", lambda e, o=o, qd=qd, cs=cs, sbv=sbv, ci=ci, h=h: e.matmul(
                            o[cs, h, :], lhsT=qd[:, h, cs], rhs=sbv[:, h, :], start=False, stop=(ci == 1),
                            skip_group_check=True),
                            reads=[qd_t, sbv_t], writes=[o_t], part=True)
                cur[0] = sbs[2]
                if not fwd:
                    for hb in range(2):
                        sc.op("act", lambda e, o=o, hb=hb, i=i: e.copy(
                            out=ob[:, i, hb * 512:(hb + 1) * 512], in_=o[:, 2 * hb:2 * hb + 2, :].rearrange("p a b -> p (a b)")),
                            reads=[o_t], writes=[ob_t[i]], part=(hb > 0))
                    return
                oa, oa_t = r_oa.next()
                ss, ss_t = r_ss.next()
                for hb in range(2):
                    sc.op("dve", lambda e, oa=oa, o=o, hb=hb, i=i: e.tensor_tensor(
                        out=oa[:, hb * 512:(hb + 1) * 512], in0=o[:, 2 * hb:2 * hb + 2, :].rearrange("p a b -> p (a b)"),
                        in1=ob[:, i, hb * 512:(hb + 1) * 512], op=ALU.add),
                        reads=[o_t, ob_t[i]], writes=[oa_t], part=(hb > 0))
                for h in range(4):
                    hs = slice(h * 256, (h + 1) * 256)
                    jk, jk_t = r_jk.next()
                    sc.op("dve", lambda e, jk=jk, oa=oa, hs=hs, ss=ss, h=h: e.scalar_tensor_tensor(
                        out=jk[:], in0=oa[:, hs], scalar=1.0, in1=oa[:, hs], op0=ALU.mult, op1=ALU.mult,
                        accum_out=ss[:, h:h + 1]), reads=[oa_t], writes=[jk_t, ss_t])
                if i % 4 == 0:
                    gg, gg_t = r_gg.next()
                    sc.dma("sp", gg[:], U["gg"][i * 128:(i + 4) * 128, :].rearrange("(j p) c -> p j c", p=128), owner=gg_t,
                           reads=[G["dram_t"]["gg"]], writes=[gg_t])
                    sg, sg_t = r_sg.next()
                    sc.op("act", lambda e, sg=sg, gg=gg: e.activation(out=sg[:], in_=gg[:], func=AF.Silu),
                          reads=[gg_t], writes=[sg_t])
                    sgcur[0] = (sg, sg_t)
                sg, sg_t = sgcur[0]
                sgn = sg[:, i % 4, :]
                sgn_t = sg_t
                sc.op("pool", lambda e, sgn=sgn: e.tensor_tensor(
                    out=sgn.rearrange("p (h v) -> p h v", h=4), in0=sgn.rearrange("p (h v) -> p h v", h=4),
                    in1=nwb[:].unsqueeze(1).to_broadcast([128, 4, 256]), op=ALU.mult),
                    reads=[sg_t, nwb_t], writes=[sg_t])
                sc.op("dve", lambda e, ss=ss: e.tensor_scalar(out=ss[:, 4:8], in0=ss[:, 0:4], scalar1=1.0 / 256.0, scalar2=EPS,
                                                              op0=ALU.mult, op1=ALU.add), reads=[ss_t], writes=[ss_t])
                sc.op("pool", lambda e, ss=ss: e.tensor_tensor(out=ss[:, 0:4], in0=ss[:, 4:8], in1=G["neghalf"][:, 0:4],
                                                               op=ALU.pow), reads=[ss_t, G["neghalf_t"]], writes=[ss_t])
                sc.op("dve", lambda e, oa=oa, ss=ss: e.tensor_tensor(
                    out=oa[:].rearrange("p (h v) -> p h v", h=4), in0=oa[:].rearrange("p (h v) -> p h v", h=4),
                    in1=ss[:, 0:4].unsqueeze(2).to_broadcast([128, 4, 256]), op=ALU.mult),
                    reads=[oa_t, ss_t], writes=[oa_t])
                y, y_t = r_y.next()
                sc.op("dve", lambda e, y=y, oa=oa, sgn=sgn: e.tensor_tensor(out=y[:], in0=oa[:], in1=sgn, op=ALU.mult),
                      reads=[oa_t, sgn_t], writes=[y_t])
                return (y, y_t, i)

            def stage3(c3):
                if c3 is None:
                    return
                (y, y_t, i) = c3
                emit_yT(P, sc, G, r_kwp, y, y_t, yst, yst_t, i, YB["gla"], G["dram_t"]["yb_gla"], gsz=2)

            prev = None
            prev3 = None
            for i in order:
                ctx = stage1(i)
                if prev is not None:
                    n3 = stage23(prev)
                    stage3(prev3)
                    prev3 = n3
                prev = ctx
            n3 = stage23(prev)
            stage3(prev3)
            stage3(n3)

        gla_pass(1)
        if "dbg_ob" in P.dbg:
            dob = P.dram("dbg_ob", [S, 1024], BF16)
            dt_ = sc.tile("dbg_ob")
            sc.dma("sp", dob.rearrange("(i p) c -> p i c", p=128), ob[:], owner=ob_t[0], reads=ob_t, writes=[dt_])
        gla_pass(0)
        sc.barrier(release=tiles)


def phase_B(P, sc, G, U, YB, prm, l):
    nc = P.nc
    ident = G["ident"]
    ybw = G["ybw"]
    ybw_t = G["dram_t"]["ybw"]
    with contextlib.ExitStack() as ph:
        xtok = P.sb(ph, "B_xtok", [128, NT, 1280], BF16)
        xtok_t = sc.tiles_n("B_xtok", NT)
        BT = P.sb(ph, "B_BT", [128, 2, S], BF16)
        CT = P.sb(ph, "B_CT", [128, 2, S], BF16)
        BT_t = sc.tiles_n("B_BT", 2)
        CT_t = sc.tiles_n("B_CT", 2)
        dtv = P.sb(ph, "B_dtv", [128, NT, 32], F32)
        av = P.sb(ph, "B_av", [128, NT, 32], F32)
        dtv_t = sc.tile("B_dtv")
        av_t = sc.tile("B_av")
        rows = P.sb(ph, "B_rows", [128, 4, 32], F32)
        rows_t = sc.tile("B_rows")
        nwb = P.sb(ph, "B_nwb", [128, 1024], F32)
        nwb_t = sc.tile("B_nwb")
        tiles = xtok_t + BT_t + CT_t + [dtv_t, av_t, rows_t, nwb_t]
        with contextlib.ExitStack() as s1:
            cwr = P.sb(s1, "B_cwr", [72, 128], F32)
            cwr_t = sc.tile("B_cwr")
            cw = P.sb(s1, "B_cw", [128, 72], F32)
            cw_t = sc.tile("B_cw")
            cwp = P.ps(s1, "B_cwp", [128, 72], F32)
            cwp_t = sc.tile("B_cwp")
            identf = P.sb(s1, "B_identf", [128, 128], F32)
            identf_t = sc.tile("B_identf")
            xc = [P.sb(s1, "B_xc%d" % i, [128, S + 4], BF16) for i in range(2)]
            xc_t = sc.tiles_n("B_xc", 2)
            dg = [P.sb(s1, "B_dg%d" % i, [128, 5, 128], BF16) for i in range(2)]
            dg_t = sc.tiles_n("B_dg", 2)
            cacc = [P.ps(s1, "B_cacc%d" % i, [128, 512], F32) for i in range(2)]
            cacc_t = sc.tiles_n("B_cacc", 2)
            xa = [P.sb(s1, "B_xa%d" % i, [128, S], BF16) for i in range(2)]
            xa_t = sc.tiles_n("B_xa", 2)
            tp = [P.ps(s1, "B_tp%d" % i, [128, 4, 128], BF16) for i in range(2)]
            tp_t = sc.tiles_n("B_tp", 2)
            tl1 = [cwr_t, cw_t, cwp_t, identf_t] + dg_t + cacc_t + xc_t + xa_t + tp_t
            sc.op("pool", lambda e: e.affine_select(out=identf[:], in_=G["ones_f"][:], pattern=[[-1, 128]],
                                                    compare_op=ALU.is_equal, fill=0.0, base=0, channel_multiplier=1),
                  reads=[G["ones_t"]], writes=[identf_t])
            sc.dma("sp", cwr[0:60, :], prm["ssd_conv_w"][l].rearrange("k (c p) -> (k c) p", p=128), owner=cwr_t, writes=[cwr_t])
            sc.dma("sp", cwr[60:72, :], prm["ssd_conv_b"][l].rearrange("(c p) -> c p", p=128), owner=cwr_t, writes=[cwr_t],
                   part=True)
            sc.op("pe", lambda e: e.transpose(out=cwp[:], in_=cwr[:], identity=identf[0:72, 0:72]),
                  reads=[cwr_t, identf_t], writes=[cwp_t])
            sc.op("act", lambda e: e.copy(out=cw[:], in_=cwp[:]), reads=[cwp_t], writes=[cw_t])
            for b in range(2):
                sc.op("pool", lambda e, b=b: e.memset(xc[b][:, 0:2], 0.0), writes=[xc_t[b]])
                sc.op("pool", lambda e, b=b: e.memset(xc[b][:, S + 2:S + 4], 0.0), writes=[xc_t[b]], part=True)
            sc.dma("sp", dtv[:], U["dt"].rearrange("(i p) c -> p i c", p=128), owner=dtv_t, reads=[G["dram_t"]["dt"]],
                   writes=[dtv_t])
            for k, nm in enumerate(("ssd_dt_bias_f", "ssd_dt_bias_b")):
                sc.dma("sp", rows[:, 0, 16 * k:16 * k + 16], prm[nm][l].partition_broadcast(128), owner=rows_t,
                       writes=[rows_t], part=True)
            for k, nm in enumerate(("ssd_a_log_f", "ssd_a_log_b")):
                sc.dma("sp", rows[:, 1, 16 * k:16 * k + 16], prm[nm][l].partition_broadcast(128), owner=rows_t,
                       writes=[rows_t], part=True)
            sc.dma("sp", rows[:, 2, 0:16], prm["ssd_d"][l].partition_broadcast(128), owner=rows_t, writes=[rows_t], part=True)
            sc.dma("sp", nwb[:], prm["ssd_norm_w"][l].partition_broadcast(128), owner=nwb_t, writes=[nwb_t])
            sc.op("dve", lambda e: e.tensor_tensor(out=dtv[:], in0=dtv[:], in1=rows[:, 0:1, :].to_broadcast([128, NT, 32]),
                                                   op=ALU.add), reads=[dtv_t, rows_t], writes=[dtv_t])
            sc.op("act", lambda e: e.activation(out=dtv[:], in_=dtv[:], func=AF.Exp), reads=[dtv_t], writes=[dtv_t])
            sc.op("act", lambda e: e.activation(out=dtv[:], in_=dtv[:], func=AF.Ln, bias=G["one"][:, 0:1]),
                  reads=[dtv_t, G["one_t"]], writes=[dtv_t])
            sc.op("act", lambda e: e.activation(out=rows[:, 3, :], in_=rows[:, 1, :], func=AF.Exp), reads=[rows_t],
                  writes=[rows_t])
            sc.op("dve", lambda e: e.scalar_tensor_tensor(out=av[:], in0=dtv[:], scalar=-1.0,
                                                          in1=rows[:, 3:4, :].to_broadcast([128, NT, 32]),
                                                          op0=ALU.mult, op1=ALU.mult),
                  reads=[dtv_t, rows_t], writes=[av_t])
            tpc = 0
            for c in range(12):
                b = c % 2
                sc.dma("sp", xc[b][:, 2:S + 2], U["xbc"][c * 128:(c + 1) * 128, :], owner=xc_t[b],
                       reads=[G["dram_t"]["xbc"]], writes=[xc_t[b]], part=True)
                dgb = c % 2
                for k in range(5):
                    sc.op("dve", lambda e, dgb=dgb, k=k, c=c: e.tensor_scalar(
                        out=dg[dgb][:, k, :], in0=identf[:], scalar1=cw[:, k * 12 + c:k * 12 + c + 1], scalar2=None,
                        op0=ALU.mult), reads=[identf_t, cw_t], writes=[dg_t[dgb]], part=(k > 0))
                if c < 10:
                    xo, xo_t = xa[b], xa_t[b]
                    xsl = lambda tb: xa[b][:, tb * 512:(tb + 1) * 512]
                else:
                    xo_t = CT_t[c - 10]
                    xsl = lambda tb, c=c: CT[:, c - 10, tb * 512:(tb + 1) * 512]
                for tb in range(4):
                    ca, ca_t = cacc[(4 * c + tb) % 2], cacc_t[(4 * c + tb) % 2]
                    for k in range(5):
                        sc.op("pe", lambda e, ca=ca, dgb=dgb, k=k, b=b, tb=tb: e.matmul(
                            ca[:], lhsT=dg[dgb][:, k, :], rhs=xc[b][:, k + tb * 512:k + tb * 512 + 512],
                            start=(k == 0), stop=(k == 4)),
                            reads=[dg_t[dgb], xc_t[b]], writes=[ca_t], part=(k > 0))
                    sc.op("act", lambda e, ca=ca, o_ap=xsl(tb), c=c: e.activation(
                        out=o_ap, in_=ca[:], func=AF.Silu, bias=cw[:, 60 + c:61 + c]),
                        reads=[ca_t, cw_t], writes=[xo_t], part=(tb > 0))
                if c in (8, 9):
                    sc.op("pool", lambda e, b=b, c=c: e.tensor_copy(out=BT[:, c - 8, :], in_=xa[b][:]),
                          reads=[xa_t[b]], writes=[BT_t[c - 8]])
                if c < 10:
                    for i0 in range(0, NT, 4):
                        tb_ = tpc % 2
                        tpc += 1
                        for j in range(4):
                            i = i0 + j
                            sc.op("pe", lambda e, tb_=tb_, j=j, b=b, i=i: e.transpose(
                                out=tp[tb_][:, j, :], in_=xa[b][:, i * 128:(i + 1) * 128], identity=ident[:]),
                                reads=[xa_t[b], G["ident_t"]], writes=[tp_t[tb_]], part=(j > 0))
                        eng = "act" if (tpc % 2) else "pool"
                        if eng == "act":
                            sc.op("act", lambda e, tb_=tb_, i0=i0, c=c: e.copy(
                                out=xtok[:, i0:i0 + 4, c * 128:(c + 1) * 128], in_=tp[tb_][:]),
                                reads=[tp_t[tb_]], writes=xtok_t[i0:i0 + 4], part=True)
                        else:
                            sc.op("dve", lambda e, tb_=tb_, i0=i0, c=c: e.tensor_copy(
                                out=xtok[:, i0:i0 + 4, c * 128:(c + 1) * 128], in_=tp[tb_][:]),
                                reads=[tp_t[tb_]], writes=xtok_t[i0:i0 + 4], part=True)
            sc.barrier(release=tl1)
        with contextlib.ExitStack() as s2:
            R = lambda name, shape, dt, n, psum=False: Ring(P, sc, s2, "B_" + name, shape, dt, n, psum)
            Sf = P.sb(s2, "B_Sf", [128, 2, 512], F32)
            Sf_t = sc.tiles_n("B_Sf", 2)
            Sbx = P.sb(s2, "B_Sb", [128, 2, 2, 512], BF16)
            r_Sb = [Ring(P, sc, s2, "B_Sb%d" % g, None, None, 2, views=[Sbx[:, g, k, :] for k in range(2)]) for g in range(2)]
            r_cb = R("cb", [128, 128], F32, 1, True)
            r_seg = R("seg", [128, 512], F32, 2, True)
            r_sm = R("sm", [128, 3, 16], F32, 1, True)
            r_yd = R("yd", [128, 512], F32, 1, True)
            r_stp = R("stp", [128, 512], F32, 1, True)
            r_yo = R("yo", [128, 512], F32, 1, True)
            r_tp = R("tp2", [128, 4, 128], BF16, 1, True)
            r_cbm = R("cbm", [128, 128], F32, 2)
            r_am = R("am", [128, 4, 128], BF16, 4)
            r_dec = R("dec", [128, 4, 128], F32, 2)
            r_mt = R("mt", [128, 4, 128], BF16, 4)
            r_ea = R("ea", [128, 3, 16], F32, 2)
            r_xdt = R("xdt", [128, 1024], BF16, 1)
            r_xw = R("xw", [128, 1024], BF16, 1)
            r_t = R("t", [128, 512], F32, 2)
            r_ybl = R("ybl", [128, 1024], BF16, 2)
            r_yf = R("yf", [128, 1024], F32, 1)
            r_z = R("z", [128, 4, 1024], BF16, 1)
            r_jk = R("jk", [128, 512], F32, 1)
            r_ss = R("ss", [128, 4], F32, 2)
            r_y = R("y", [128, 1024], BF16, 2)
            yst = P.sb(s2, "B_yst", [128, 8, 256], BF16)
            yst_t = sc.tile("B_yst")
            rings = [r_cb, r_seg, r_sm, r_yd, r_stp, r_yo, r_tp, r_cbm, r_am, r_dec, r_mt, r_ea, r_xdt, r_xw, r_t, r_ybl,
                     r_yf, r_z, r_jk, r_ss, r_y] + r_Sb
            tl2 = Sf_t + [yst_t]
            for r in rings:
                tl2 += r.t

            def ssd_pass(d):
                fwd = (d == 0)
                tri_in, tri_in_t = (G["trif"], G["trif_t"]) if fwd else (G["trib"], G["trib_t"])
                tri_st, tri_st_t = (G["tribs"], G["tribs_t"]) if fwd else (G["trifs"], G["trifs_t"])
                tri_sb, tri_sb_t = (G["tribsb"], G["tribsb_t"]) if fwd else (G["trifsb"], G["trifsb_t"])
                cur = []
                for g in range(2):
                    sc.op("pool", lambda e, g=g: e.memset(Sf[:, g, :], 0.0), writes=[Sf_t[g]])
                    sb0, sb0_t = r_Sb[g].next()
                    sc.op("pool", lambda e, sb0=sb0: e.memset(sb0, 0.0), writes=[sb0_t])
                    cur.append((sb0, sb0_t))
                order = list(range(NT)) if fwd else list(range(NT - 1, -1, -1))
                zcur = [None]

                def tileA(i):
                    tsl = slice(i * 128, (i + 1) * 128)
                    acol = av[:, i, 16 * d:16 * d + 16]
                    ams = []
                    for u in range(4):
                        h0 = u * 4
                        am, am_t = r_am.next()
                        for hh in range(4):
                            sc.op("act", lambda e, am=am, i=i, h0=h0, hh=hh: e.activation(
                                out=am[:, hh, :], in_=tri_in[:], func=AF.Copy,
                                scale=av[:, i, 16 * d + h0 + hh:16 * d + h0 + hh + 1]),
                                reads=[tri_in_t, av_t], writes=[am_t], part=(hh > 0))
                        ams.append((am, am_t))
                    sm, sm_t = r_sm.next()
                    sc.op("pe", lambda e, sm=sm, acol=acol: e.matmul(sm[:, 0, :], lhsT=tri_in[:], rhs=acol, start=True, stop=True),
                          reads=[tri_in_t, av_t], writes=[sm_t])
                    sc.op("pe", lambda e, sm=sm, acol=acol: e.matmul(sm[:, 1, :], lhsT=tri_st[:], rhs=acol, start=True, stop=True),
                          reads=[tri_st_t, av_t], writes=[sm_t], part=True)
                    sc.op("pe", lambda e, sm=sm, acol=acol: e.matmul(sm[:, 2, :], lhsT=G["ones_f"][:], rhs=acol, start=True,
                                                                     stop=True),
                          reads=[G["ones_t"], av_t], writes=[sm_t], part=True)
                    ea, ea_t = r_ea.next()
                    sc.op("act", lambda e, ea=ea, sm=sm: e.activation(out=ea[:], in_=sm[:], func=AF.Exp), reads=[sm_t],
                          writes=[ea_t])
                    xdt, xdt_t = r_xdt.next()
                    sc.op("dve", lambda e, xdt=xdt, i=i: e.tensor_tensor(
                        out=xdt[:].rearrange("p (h q) -> p h q", q=64), in0=xtok[:, i, 0:1024].rearrange("p (h q) -> p h q", q=64),
                        in1=dtv[:, i, 16 * d:16 * d + 16].unsqueeze(2).to_broadcast([128, 16, 64]), op=ALU.mult),
                        reads=[xtok_t[i], dtv_t], writes=[xdt_t])
                    xw, xw_t = r_xw.next()
                    sc.op("dve", lambda e, xw=xw, xdt=xdt, ea=ea: e.tensor_tensor(
                        out=xw[:].rearrange("p (h q) -> p h q", q=64), in0=xdt[:].rearrange("p (h q) -> p h q", q=64),
                        in1=ea[:, 1, :].unsqueeze(2).to_broadcast([128, 16, 64]), op=ALU.mult),
                        reads=[xdt_t, ea_t], writes=[xw_t])
                    ybl, ybl_t = r_ybl.next()
                    if fwd:
                        sc.dma("sp", ybl[:], ybw[tsl, :], owner=ybl_t, reads=[ybw_t], writes=[ybl_t])
                        yf, yf_t = r_yf.next()
                    ts = []
                    for g in range(2):
                        stp, stp_t = r_stp.next()
                        sc.op("pe", lambda e, stp=stp, i=i, g=g, xw=xw: e.matmul(
                            stp[:], lhsT=xtok[:, i, 1024 + g * 128:1024 + (g + 1) * 128], rhs=xw[:, g * 512:(g + 1) * 512],
                            start=True, stop=True), reads=[xtok_t[i], xw_t], writes=[stp_t])
                        yo, yo_t = r_yo.next()
                        sbv, sbv_t = cur[g]
                        sc.op("pe", lambda e, yo=yo, g=g, tsl=tsl, sbv=sbv: e.matmul(yo[:], lhsT=CT[:, g, tsl], rhs=sbv,
                                                                                    start=True, stop=True),
                              reads=[CT_t[g], sbv_t], writes=[yo_t])
                        sc.op("pool", lambda e, g=g, ea=ea: e.tensor_tensor(
                            out=Sf[:, g, :].rearrange("p (h q) -> p h q", q=64), in0=Sf[:, g, :].rearrange("p (h q) -> p h q", q=64),
                            in1=ea[:, 2, g * 8:(g + 1) * 8].unsqueeze(2).to_broadcast([128, 8, 64]), op=ALU.mult),
                            reads=[Sf_t[g], ea_t], writes=[Sf_t[g]])
                        sc.op("dve", lambda e, g=g, stp=stp: e.tensor_tensor(out=Sf[:, g, :], in0=Sf[:, g, :], in1=stp[:],
                                                                            op=ALU.add),
                              reads=[Sf_t[g], stp_t], writes=[Sf_t[g]])
                        nb, nb_t = r_Sb[g].next()
                        sc.op("act", lambda e, nb=nb, g=g: e.copy(out=nb, in_=Sf[:, g, :]), reads=[Sf_t[g]], writes=[nb_t])
                        cur[g] = (nb, nb_t)
                        t, t_t = r_t.next()
                        sc.op("dve", lambda e, t=t, yo=yo, ea=ea, g=g: e.tensor_tensor(
                            out=t[:].rearrange("p (h q) -> p h q", q=64), in0=yo[:].rearrange("p (h q) -> p h q", q=64),
                            in1=ea[:, 0, g * 8:(g + 1) * 8].unsqueeze(2).to_broadcast([128, 8, 64]), op=ALU.mult),
                            reads=[yo_t, ea_t], writes=[t_t])
                        ts.append((t, t_t))
                    cbms = []
                    for g in range(2):
                        cb, cb_t = r_cb.next()
                        sc.op("pe", lambda e, cb=cb, g=g, tsl=tsl: e.matmul(cb[:], lhsT=BT[:, g, tsl], rhs=CT[:, g, tsl],
                                                                           start=True, stop=True),
                              reads=[BT_t[g], CT_t[g]], writes=[cb_t])
                        cbm, cbm_t = r_cbm.next()
                        sc.op("dve", lambda e, cbm=cbm, cb=cb: e.tensor_tensor(out=cbm[:], in0=cb[:], in1=tri_in[:], op=ALU.mult),
                              reads=[cb_t, tri_in_t], writes=[cbm_t])
                        cbms.append((cbm, cbm_t))
                    mts = []
                    for pair in range(2):
                        segs = []
                        for u in (2 * pair, 2 * pair + 1):
                            am, am_t = ams[u]
                            seg, seg_t = r_seg.next()
                            sc.op("pe", lambda e, seg=seg, am=am: e.matmul(seg[:], lhsT=tri_sb[:],
                                                                           rhs=am[:].rearrange("p a b -> p (a b)"),
                                                                           start=True, stop=True),
                                  reads=[tri_sb_t, am_t], writes=[seg_t])
                            segs.append((seg, seg_t))
                        decs = []
                        for (seg, seg_t) in segs:
                            dec, dec_t = r_dec.next()
                            sc.op("act", lambda e, dec=dec, seg=seg: e.activation(out=dec[:].rearrange("p a b -> p (a b)"),
                                                                                 in_=seg[:], func=AF.Exp),
                                  reads=[seg_t], writes=[dec_t])
                            decs.append((dec, dec_t))
                        for k, (dec, dec_t) in enumerate(decs):
                            u = 2 * pair + k
                            cbm, cbm_t = cbms[u // 2]
                            mt, mt_t = r_mt.next()
                            sc.op("dve", lambda e, mt=mt, dec=dec, cbm=cbm: e.tensor_tensor(
                                out=mt[:], in0=dec[:], in1=cbm[:].unsqueeze(1).to_broadcast([128, 4, 128]), op=ALU.mult),
                                reads=[dec_t, cbm_t], writes=[mt_t])
                            mts.append((mt, mt_t))
                    for g in range(2):
                        yd, yd_t = r_yd.next()
                        if fwd:
                            sc.op("pe", lambda e, yd=yd, ybl=ybl, g=g: e.matmul(
                                yd[:], lhsT=ident[:], rhs=ybl[:, g * 512:(g + 1) * 512], start=True, stop=False,
                                skip_group_check=True), reads=[G["ident_t"], ybl_t], writes=[yd_t])
                        for q4 in range(2):
                            mt, mt_t = mts[g * 2 + q4]
                            for hh in range(4):
                                h = g * 8 + q4 * 4 + hh
                                hl = h - g * 8
                                sc.op("pe", lambda e, yd=yd, mt=mt, hh=hh, hl=hl, h=h, xdt=xdt: e.matmul(
                                    yd[:, hl * 64:(hl + 1) * 64], lhsT=mt[:, hh, :], rhs=xdt[:, h * 64:(h + 1) * 64],
                                    start=(not fwd), stop=True, skip_group_check=True),
                                    reads=[mt_t, xdt_t], writes=[yd_t], part=(fwd or not (q4 == 0 and hh == 0)))
                        t, t_t = ts[g]
                        gs = slice(g * 512, (g + 1) * 512)
                        if not fwd:
                            sc.op("dve", lambda e, t=t, yd=yd, ybl=ybl, gs=gs: e.tensor_tensor(out=ybl[:, gs], in0=t[:], in1=yd[:],
                                                                                              op=ALU.add),
                                  reads=[t_t, yd_t], writes=[ybl_t], part=(g > 0))
                        else:
                            sc.op("dve", lambda e, t=t, yd=yd, yf=yf, gs=gs: e.tensor_tensor(out=yf[:, gs], in0=t[:], in1=yd[:],
                                                                                            op=ALU.add),
                                  reads=[t_t, yd_t], writes=[yf_t], part=(g > 0))
                    if not fwd:
                        sc.dma("pool", ybw[tsl, :], ybl[:], owner=ybl_t, reads=[ybl_t], writes=[ybw_t], part=True)
                        return None
                    if i % 4 == 0:
                        z, z_t = r_z.next()
                        sc.dma("sp", z[:], U["z"][i * 128:(i + 4) * 128, :].rearrange("(j p) c -> p j c", p=128), owner=z_t,
                               reads=[G["dram_t"]["z"]], writes=[z_t])
                        sc.op("act", lambda e, z=z: e.activation(out=z[:], in_=z[:], func=AF.Silu), reads=[z_t], writes=[z_t])
                        zcur[0] = (z, z_t)
                    z, z_t = zcur[0]
                    sz = z[:, i % 4, :]
                    sz_t = z_t
                    xd, xd_t = r_xdt.next()
                    sc.op("pool", lambda e, xd=xd, i=i: e.tensor_tensor(
                        out=xd[:].rearrange("p (h q) -> p h q", q=64), in0=xtok[:, i, 0:1024].rearrange("p (h q) -> p h q", q=64),
                        in1=rows[:, 2, 0:16].unsqueeze(2).to_broadcast([128, 16, 64]), op=ALU.mult),
                        reads=[xtok_t[i], rows_t], writes=[xd_t])
                    sc.op("dve", lambda e, yf=yf, xd=xd: e.tensor_tensor(out=yf[:], in0=yf[:], in1=xd[:], op=ALU.add),
                          reads=[yf_t, xd_t], writes=[yf_t])
                    sc.op("dve", lambda e, yf=yf, sz=sz: e.tensor_tensor(out=yf[:], in0=yf[:], in1=sz, op=ALU.mult),
                          reads=[yf_t, sz_t], writes=[yf_t])
                    ss, ss_t = r_ss.next()
                    for g in range(2):
                        gs = slice(g * 512, (g + 1) * 512)
                        jk, jk_t = r_jk.next()
                        sc.op("dve", lambda e, jk=jk, yf=yf, gs=gs, ss=ss, g=g: e.scalar_tensor_tensor(
                            out=jk[:], in0=yf[:, gs], scalar=1.0, in1=yf[:, gs], op0=ALU.mult, op1=ALU.mult,
                            accum_out=ss[:, g:g + 1]), reads=[yf_t], writes=[jk_t, ss_t])
                    sc.op("dve", lambda e, ss=ss: e.tensor_scalar(out=ss[:, 2:4], in0=ss[:, 0:2], scalar1=1.0 / 512.0, scalar2=EPS,
                                                                  op0=ALU.mult, op1=ALU.add), reads=[ss_t], writes=[ss_t])
                    sc.op("pool", lambda e, ss=ss: e.tensor_tensor(out=ss[:, 0:2], in0=ss[:, 2:4], in1=G["neghalf"][:, 0:2],
                                                                   op=ALU.pow), reads=[ss_t, G["neghalf_t"]], writes=[ss_t])
                    y, y_t = r_y.next()
                    for g in range(2):
                        gs = slice(g * 512, (g + 1) * 512)
                        sc.op("dve", lambda e, y=y, yf=yf, gs=gs, ss=ss, g=g: e.scalar_tensor_tensor(
                            out=y[:, gs], in0=yf[:, gs], scalar=ss[:, g:g + 1], in1=nwb[:, gs], op0=ALU.mult, op1=ALU.mult),
                            reads=[yf_t, ss_t, nwb_t], writes=[y_t], part=(g > 0))
                    return (y, y_t, i)

                def tileC(c3):
                    if c3 is None:
                        return
                    (y, y_t, i) = c3
                    emit_yT(P, sc, G, r_tp, y, y_t, yst, yst_t, i, YB["ssd"], G["dram_t"]["yb_ssd"], gsz=2)

                prev3 = None
                for i in order:
                    n3 = tileA(i)
                    tileC(prev3)
                    prev3 = n3
                tileC(prev3)

            ssd_pass(1)
            ssd_pass(0)
            sc.barrier(release=tl2)
        sc.barrier(release=tiles)


def emit_yT(P, sc, G, r_tp, y, y_t, yst, yst_t, i, dst, dst_t, gsz=4):
    ident = G["ident"]
    for half in range(2):
        tp, tp_t = r_tp.next()
        for jq in range(4):
            c = half * 4 + jq
            sc.op("pe", lambda e, tp=tp, jq=jq, c=c: e.transpose(out=tp[:, jq, :], in_=y[:, c * 128:(c + 1) * 128],
                                                               identity=ident[:]),
                  reads=[y_t, G["ident_t"]], writes=[tp_t], part=(jq > 0))
        sc.op("act", lambda e, tp=tp, half=half: e.copy(
            out=yst[:, half * 4:half * 4 + 4, (i % gsz) * 128:(i % gsz + 1) * 128], in_=tp[:]),
            reads=[tp_t], writes=[yst_t], part=not (i % gsz == 0 and half == 0))
    if i % gsz == gsz - 1:
        yv = dst.rearrange("(c p) t -> p c t", p=128)
        sc.dma("pool", yv[:, :, (i - gsz + 1) * 128:(i + 1) * 128], yst[:], owner=yst_t, reads=[yst_t], writes=[dst_t],
               part=True)


NEG = -30000.0


def na_r0(r):
    return min(max(r - 4, 0), 24)


def na_valid(kr, qr):
    return na_r0(qr) <= kr < na_r0(qr) + 8


def phase_D(P, sc, G, U, YB, prm, natt, l):
    nc = P.nc
    with contextlib.ExitStack() as ph:
        qnT = P.sb(ph, "D_qnT", [128, 8, S], BF16)
        knT = P.sb(ph, "D_knT", [128, 8, S], BF16)
        qn_t = sc.tiles_n("D_qn", 8)
        kn_t = sc.tiles_n("D_kn", 8)
        TT = P.sb(ph, "D_TT", [128, 8, 17, 64], BF16)
        TT_t = sc.tiles_n("D_TT", 4)
        wcol = P.sb(ph, "D_wcol", [128, 4], F32)
        wcol_t = sc.tile("D_wcol")
        tiles = qn_t + kn_t + TT_t + [wcol_t]
        with contextlib.ExitStack() as s1:
            TTf = [P.sb(s1, "D_TTf%d" % i, [128, 2, 17, 64], F32) for i in range(2)]
            TTf_t = sc.tiles_n("D_TTf", 2)
            qc_ = [P.sb(s1, "D_qc%d" % i, [128, S], BF16) for i in range(2)]
            qc_t = sc.tiles_n("D_qc", 2)
            sq = [P.sb(s1, "D_sq%d" % i, [128, S], BF16) for i in range(2)]
            sq_t = sc.tiles_n("D_sq", 2)
            lnv = [P.sb(s1, "D_ln%d" % i, [128, S], F32) for i in range(2)]
            lnv_t = sc.tiles_n("D_ln", 2)
            bones = P.sb(s1, "D_bones", [128, 128], BF16)
            bones_t = sc.tile("D_bones")
            ssp = [P.ps(s1, "D_ssp%d" % i, [128, S], F32) for i in range(2)]
            ssp_t = sc.tiles_n("D_ssp", 2)
            t1 = TTf_t + qc_t + sq_t + lnv_t + [bones_t] + ssp_t
            for g in range(4):
                b = g % 2
                sc.dma("sp", TTf[b][:], natt[l][:, 2 * g:2 * g + 2, :, :], owner=TTf_t[b], writes=[TTf_t[b]])
                sc.op("pool", lambda e, b=b, g=g: e.tensor_copy(out=TT[:, 2 * g:2 * g + 2, :, :], in_=TTf[b][:]),
                      reads=[TTf_t[b]], writes=[TT_t[g]])
            for hh in range(2):
                sc.dma("sp", wcol[hh * 64:(hh + 1) * 64, 2:3], prm["na_q_norm_w"][l].rearrange("(d o) -> d o", o=1),
                       owner=wcol_t, writes=[wcol_t], part=True)
                sc.dma("sp", wcol[hh * 64:(hh + 1) * 64, 1:2], prm["na_k_norm_w"][l].rearrange("(d o) -> d o", o=1),
                       owner=wcol_t, writes=[wcol_t], part=True)
            sc.op("dve", lambda e: e.tensor_scalar(out=wcol[:, 0:1], in0=wcol[:, 2:3], scalar1=0.125, scalar2=None,
                                                   op0=ALU.mult), reads=[wcol_t], writes=[wcol_t])
            sc.op("pool", lambda e: e.memset(bones[:], 0.0), writes=[bones_t])
            sc.op("pool", lambda e: e.memset(bones[0:64, 0:64], 1.0), reads=[bones_t], writes=[bones_t])
            sc.op("pool", lambda e: e.memset(bones[64:128, 64:128], 1.0), reads=[bones_t], writes=[bones_t])
            jobs = []
            for which, (src, dstT, dst_t, wc) in enumerate(((U["nq"], qnT, qn_t, 0), (U["nk"], knT, kn_t, 1))):
                src_t = G["dram_t"]["nq" if which == 0 else "nk"]
                for c in range(8):
                    jobs.append((src, src_t, dstT, dst_t, wc, c))

            def n_s1(k):
                (src, src_t, dstT, dst_t, wc, c) = jobs[k]
                cb = k % 2
                sc.dma("sp", qc_[cb][:], src[c * 128:(c + 1) * 128, :], owner=qc_t[cb], reads=[src_t], writes=[qc_t[cb]])
                sc.op("dve", lambda e, cb=cb: e.tensor_tensor(out=sq[cb][:], in0=qc_[cb][:], in1=qc_[cb][:], op=ALU.mult),
                      reads=[qc_t[cb]], writes=[sq_t[cb]])
                for tb in range(4):
                    sl = slice(tb * 512, (tb + 1) * 512)
                    sc.op("pe", lambda e, cb=cb, sl=sl: e.matmul(ssp[cb][:, sl], lhsT=bones[:], rhs=sq[cb][:, sl], start=True,
                                                                 stop=True),
                          reads=[bones_t, sq_t[cb]], writes=[ssp_t[cb]], part=(tb > 0))
                sc.op("act", lambda e, cb=cb: e.activation(out=lnv[cb][:], in_=ssp[cb][:], func=AF.Ln,
                                                           bias=G["eps"][:, 0:1], scale=1.0 / 64.0),
                      reads=[ssp_t[cb], G["eps_t"]], writes=[lnv_t[cb]])
                sc.op("act", lambda e, cb=cb: e.activation(out=lnv[cb][:], in_=lnv[cb][:], func=AF.Exp, scale=-0.5),
                      reads=[lnv_t[cb]], writes=[lnv_t[cb]])

            def n_s2(k):
                (src, src_t, dstT, dst_t, wc, c) = jobs[k]
                cb = k % 2
                sc.op("dve", lambda e, cb=cb, dstT=dstT, c=c, wc=wc: e.scalar_tensor_tensor(
                    out=dstT[:, c, :], in0=qc_[cb][:], scalar=wcol[:, wc:wc + 1], in1=lnv[cb][:],
                    op0=ALU.mult, op1=ALU.mult),
                    reads=[qc_t[cb], lnv_t[cb], wcol_t], writes=[dst_t[c]])

            n_s1(0)
            for k in range(len(jobs)):
                if k + 1 < len(jobs):
                    n_s1(k + 1)
                n_s2(k)
            sc.barrier(release=t1)
        with contextlib.ExitStack() as s2:
            vx = P.sb(s2, "D_vx", [128, NT, 16, 65], BF16)
            vx_t = sc.tiles_n("D_vx", NT)
            sps = [P.ps(s2, "D_sps%d" % i, [128, 8, 128], F32) for i in range(2)]
            sps_t = sc.tiles_n("D_sps", 2)
            pT = [P.sb(s2, "D_pT%d" % i, [128, 5, 128], BF16) for i in range(3)]
            pT_t = sc.tiles_n("D_pT", 3)
            po = [P.ps(s2, "D_po%d" % i, [128, 2, 66], F32) for i in range(2)]
            po_t = sc.tiles_n("D_po", 2)
            rc = [P.sb(s2, "D_rc%d" % i, [128, 2], F32) for i in range(2)]
            rc_t = sc.tiles_n("D_rc", 2)
            ot = [P.sb(s2, "D_ot%d" % i, [128, 1024], BF16) for i in range(2)]
            ot_t = sc.tiles_n("D_ot", 2)
            tp = [P.ps(s2, "D_tp%d" % i, [128, 4, 128], BF16) for i in range(2)]
            tp_t = sc.tiles_n("D_tp", 2)
            yst = P.sb(s2, "D_yst", [128, 8, 512], BF16)
            yst_t = sc.tile("D_yst")
            t2 = vx_t + sps_t + pT_t + po_t + rc_t + ot_t + tp_t + [yst_t]
            nvv = U["nv"].rearrange("(i p) (h d) -> p i h d", p=128, d=64)
            for i in range(NT):
                sc.op("pool", lambda e, i=i: e.memset(vx[:, i, :, 64:65], 1.0), writes=[vx_t[i]])
                sc.dma("sp", vx[:, i, :, 0:64], nvv[:, i, :, :], owner=vx_t[i], reads=[G["dram_t"]["nv"]],
                       writes=[vx_t[i]], part=True)
            ident = G["ident"]
            tpc = [0]
            units = []
            for i in range(NT):
                jlo = na_r0(2 * i) // 2
                jhi = (na_r0(2 * i + 1) + 7) // 2
                js = list(range(jlo, jhi + 1))
                for hp in range(8):
                    for hh in range(2):
                        units.append((i, hp, hh, js))

            def emit_S(u):
                i, hp, hh, js = units[u]
                h = 2 * hp + hh
                p0 = 64 * hh
                sb_ = u % 2
                for jj, j in enumerate(js):
                    sc.op("pe", lambda e, sb_=sb_, jj=jj, j=j, p0=p0, hp=hp, i=i: e.matmul(
                        sps[sb_][:, jj, :], lhsT=knT[p0:p0 + 64, hp, j * 128:(j + 1) * 128],
                        rhs=qnT[p0:p0 + 64, hp, i * 128:(i + 1) * 128], start=True, stop=False,
                        skip_group_check=True),
                        reads=[kn_t[hp], qn_t[hp]], writes=[sps_t[sb_]], part=(jj > 0))
                    mms = []
                    for b0 in range(2):
                        qr = 2 * i + b0
                        va = [na_valid(2 * j + a, qr) for a in range(2)]
                        dr0 = 2 * j - qr + 7
                        cs = slice(b0 * 64, (b0 + 1) * 64)
                        if va[0] and va[1]:
                            mms.append((slice(0, 128), cs, TT[p0:p0 + 64, hp, dr0:dr0 + 2, :]))
                        elif not va[0] and not va[1]:
                            mms.append((slice(0, 128), cs, TT[p0:p0 + 64, hp, 15:17, :]))
                        else:
                            d0 = dr0 if va[0] else 15
                            d1 = dr0 + 1 if va[1] else 16
                            mms.append((slice(0, 64), cs, TT[p0:p0 + 64, hp, d0, :]))
                            mms.append((slice(64, 128), cs, TT[p0:p0 + 64, hp, d1, :]))
                    for mi, (ps_, cs, lhs) in enumerate(mms):
                        sc.op("pe", lambda e, sb_=sb_, jj=jj, ps_=ps_, cs=cs, lhs=lhs, p0=p0, last=(mi == len(mms) - 1):
                              e.matmul(sps[sb_][ps_, jj, cs], lhsT=lhs, rhs=ident[p0:p0 + 64, p0:p0 + 64],
                                       start=False, stop=last, skip_group_check=True),
                              reads=[TT_t[hp // 2], G["ident_t"]], writes=[sps_t[sb_]], part=True)

            def emit_rest(u):
                i, hp, hh, js = units[u]
                h = 2 * hp + hh
                sb_ = u % 2
                pt = u % 3
                pb_ = (u // 2) % 2
                ob = i % 2
                n = len(js)
                n1 = min(n, 4)
                sc.op("act", lambda e, pt=pt, sb_=sb_, n1=n1: e.activation(out=pT[pt][:, 0:n1, :],
                                                                         in_=sps[sb_][:, 0:n1, :], func=AF.Exp),
                      reads=[sps_t[sb_]], writes=[pT_t[pt]])
                if n > 4:
                    sc.op("act", lambda e, pt=pt, sb_=sb_, n=n: e.activation(out=pT[pt][:, 4:n, :],
                                                                           in_=sps[sb_][:, 4:n, :], func=AF.Exp),
                          reads=[sps_t[sb_]], writes=[pT_t[pt]], part=True)
                for jj, j in enumerate(js):
                    sc.op("pe", lambda e, pb_=pb_, hh=hh, pt=pt, jj=jj, j=j, h=h, n=n: e.matmul(
                        po[pb_][:, hh, 0:65], lhsT=pT[pt][:, jj, :], rhs=vx[:, j, h, :],
                        start=(jj == 0), stop=(jj == n - 1)),
                        reads=[pT_t[pt], vx_t[j]], writes=[po_t[pb_]], part=(hh > 0 or jj > 0))
                if hh == 1:
                    sc.op("dve", lambda e, pb_=pb_: e.reciprocal(out=rc[pb_][:, 0:2], in_=po[pb_][:, :, 64]),
                          reads=[po_t[pb_]], writes=[rc_t[pb_]])
                    for h2 in range(2):
                        hx = 2 * hp + h2
                        sc.op("dve", lambda e, pb_=pb_, h2=h2, hx=hx, ob=ob: e.tensor_scalar(
                            out=ot[ob][:, hx * 64:(hx + 1) * 64], in0=po[pb_][:, h2, 0:64], scalar1=rc[pb_][:, h2:h2 + 1],
                            scalar2=None, op0=ALU.mult),
                            reads=[po_t[pb_], rc_t[pb_]], writes=[ot_t[ob]], part=(hx > 0))
                if hp == 7 and hh == 1:
                    for half in range(2):
                        tb_ = tpc[0] % 2
                        tpc[0] += 1
                        for jq in range(4):
                            c = half * 4 + jq
                            sc.op("pe", lambda e, tb_=tb_, jq=jq, c=c, ob=ob: e.transpose(
                                out=tp[tb_][:, jq, :], in_=ot[ob][:, c * 128:(c + 1) * 128], identity=ident[:]),
                                reads=[ot_t[ob], G["ident_t"]], writes=[tp_t[tb_]], part=(jq > 0))
                        sc.op("act", lambda e, tb_=tb_, half=half, i=i: e.copy(
                            out=yst[:, half * 4:half * 4 + 4, (i % 4) * 128:(i % 4 + 1) * 128], in_=tp[tb_][:]),
                            reads=[tp_t[tb_]], writes=[yst_t], part=not (i % 4 == 0 and half == 0))
                    if i % 4 == 3:
                        yv = YB["na"].rearrange("(c p) t -> p c t", p=128)
                        sc.dma("pool", yv[:, :, (i - 3) * 128:(i + 1) * 128], yst[:], owner=yst_t, reads=[yst_t],
                               writes=[G["dram_t"]["yb_na"]], part=True)

            emit_S(0)
            for u in range(len(units)):
                if u + 1 < len(units):
                    emit_S(u + 1)
                emit_rest(u)
            sc.barrier(release=t2)
        sc.barrier(release=tiles)


def phase_F(P, sc, G, prm, l, y_store=None):
    nc = P.nc
    x = G["x"]
    with contextlib.ExitStack() as ph:
        hT = P.sb(ph, "F_hT", [128, 8, S], BF16)
        hT_t = sc.tiles_n("F_hT", NT)
        tiles = list(hT_t)
        tiles += rms_transpose(P, sc, G, ph, prm["norm_mlp_w"][l], hT, hT_t, l, "F")
        wst = WStream(P, sc, ph, "F", 1, 4096, nf=2, nb=3)
        fT = [P.sb(ph, "F_fT%d" % i, [128, 4, S], BF16) for i in range(2)]
        fT_t = [sc.tiles_n("F_fT%d_" % i, 4) for i in range(2)]
        rl = [P.sb(ph, "F_rl%d" % i, [128, 512], F32) for i in range(2)]
        rl_t = sc.tiles_n("F_rl", 2)
        acc = [P.ps(ph, "F_acc%d" % i, [128, 512], F32) for i in range(4)]
        acc_t = sc.tiles_n("F_acc", 4)
        tiles += wst.tiles + fT_t[0] + fT_t[1] + rl_t + acc_t
        w1v = prm["w_ff1"][l].rearrange("(kc p) n -> p kc n", p=128)
        w2v = prm["w_ff2"][l].rearrange("(c p) n -> p c n", p=128)
        items = []
        for g in range(8):
            items.append((w1v[:, :, g * 512:(g + 1) * 512], 8, 512))
            items.append((w2v[:, g * 4:(g + 1) * 4, :], 4, 1024))
        wst.items = items
        wst_views = {}

        def view(slot, k, n):
            return slot[:, 0, :].rearrange("p (k n) -> p k n", k=k)
        def _load(g):
            if g >= len(items):
                return
            ap, k, n = items[g]
            fs = g % wst.nf
            sc.dma("sp", view(wst.f[fs], k, n), ap, owner=wst.f_t[fs], writes=[wst.f_t[fs]])

        def _cast(g):
            if g >= len(items):
                return
            fs, bs = g % wst.nf, g % wst.nb
            sc.op("pool", lambda e: e.tensor_copy(out=wst.b[bs][:, 0, :], in_=wst.f[fs][:, 0, :]),
                  reads=[wst.f_t[fs]], writes=[wst.b_t[bs]])
        wst._load = _load
        wst._cast = _cast
        _load(0)
        _load(1)
        _cast(0)
        ai = 0
        ri = 0
        for g in range(8):
            fb = g % 2
            w1s, w1_t = wst.get(2 * g)
            w1b = view(w1s, 8, 512)
            for c in range(4):
                for tb in range(4):
                    a = ai % 4
                    ai += 1
                    for kc in range(8):
                        sc.op("pe", lambda e, a=a, kc=kc, w1b=w1b, c=c, tb=tb: e.matmul(
                            acc[a][:], lhsT=w1b[:, kc, c * 128:(c + 1) * 128],
                            rhs=hT[:, kc, tb * 512:(tb + 1) * 512], start=(kc == 0), stop=(kc == 7)),
                            reads=[w1_t] + hT_t[tb * 4:tb * 4 + 4], writes=[acc_t[a]], part=(kc > 0))
                    r = ri % 2
                    ri += 1
                    sc.op("act", lambda e, r=r, a=a: e.activation(out=rl[r][:], in_=acc[a][:], func=AF.Relu),
                          reads=[acc_t[a]], writes=[rl_t[r]])
                    sc.op("pool", lambda e, r=r, fb=fb, c=c, tb=tb: e.tensor_tensor(
                        out=fT[fb][:, c, tb * 512:(tb + 1) * 512], in0=rl[r][:], in1=rl[r][:], op=ALU.mult),
                        reads=[rl_t[r]], writes=[fT_t[fb][c]], part=(tb > 0))
            w2s, w2_t = wst.get(2 * g + 1)
            w2b = view(w2s, 4, 1024)
            for i in range(NT):
                for hh in range(2):
                    a = ai % 4
                    ai += 1
                    for c in range(4):
                        sc.op("pe", lambda e, a=a, c=c, w2b=w2b, i=i, hh=hh, fb=fb: e.matmul(
                            acc[a][:], lhsT=fT[fb][:, c, i * 128:(i + 1) * 128],
                            rhs=w2b[:, c, hh * 512:(hh + 1) * 512], start=(c == 0), stop=(c == 3)),
                            reads=[w2_t, fT_t[fb][c]], writes=[acc_t[a]], part=(c > 0))
                    xs = x[:, i, hh * 512:(hh + 1) * 512]
                    sc.op("dve", lambda e, xs=xs, a=a: e.tensor_tensor(out=xs, in0=xs, in1=acc[a][:], op=ALU.add),
                          reads=[acc_t[a], G["xt"][i]], writes=[G["xt"][i]])
                if y_store is not None and g == 7:
                    yv, outt = y_store
                    sc.dma("sp", yv[:, i, :], x[:, i, :], owner=G["xt"][i], reads=[G["xt"][i]], writes=[outt], part=True)
        sc.barrier(release=tiles)


_NC_CACHE = {}


def make_na_tt(rpb):
    rpb = np.asarray(rpb, dtype=np.float32)
    L = rpb.shape[0]
    out = np.full((L, 128, 8, 17, 64), NEG, dtype=np.float32)
    qc = np.arange(64)
    ws = np.clip(qc - 8, 0, 48)
    for q in range(64):
        kc = np.arange(ws[q], ws[q] + 16)
        idx = kc - q + 15
        for hh in range(2):
            out[:, hh * 64 + q, :, 0:15, ws[q]:ws[q] + 16] = rpb[:, hh::2][:, :, :, idx]
    return out


def kernel(**inputs):
    cfg = {}
    key = "full"
    if key not in _NC_CACHE:
        _NC_CACHE[key] = build(cfg)
    nc = _NC_CACHE[key]
    x = np.ascontiguousarray(inputs["x"], dtype=np.float32)
    base = {n: np.ascontiguousarray(inputs[n], dtype=np.float32) for n in PARAM_NAMES}
    base["na_tt"] = make_na_tt(inputs["na_rpb"])
    in_maps = []
    for c in range(8):
        m = dict(base)
        m["x"] = x[c]
        in_maps.append(m)
    res = run_bass_kernel_spmd(nc, in_maps, core_ids=list(range(8)))
    return np.stack([r["y"] for r in res.results], axis=0).astype(np.float32)
```

```python
import contextlib
import numpy as np
import concourse.bass as bass
import concourse.mybir as mybir
from concourse.bass_utils import run_bass_kernel_spmd

F32 = mybir.dt.float32
BF16 = mybir.dt.bfloat16
ALU = mybir.AluOpType
AF = mybir.ActivationFunctionType
AX = mybir.AxisListType

D = 1024
S = 2048
NT = S // 128
DEPTH = 2
N_IN = 11840
EPS = 1e-6


class TT:
    __slots__ = ("name", "lw", "rd", "dsems", "gen")

    def __init__(self, name):
        self.name = name
        self.lw = {}
        self.rd = {}
        self.gen = {}
        self.dsems = {}


class Sched:
    ENG = ("pe", "act", "dve", "pool", "sp")
    BLK = {"pe": "tensor", "act": "scalar", "dve": "vector", "pool": "gpsimd", "sp": "sync"}

    def __init__(self, nc, stack):
        self.nc = nc
        self.stack = stack
        self.ops = {e: [] for e in self.ENG}
        self.seen = {e: {} for e in self.ENG}
        self.esem = {e: stack.enter_context(nc.semaphore("es_" + e)) for e in self.ENG if e != "sp"}
        self.tiles = []
        self.free_dsems = {"sp": [], "pool": [], "act": []}
        self.nsem = 4
        self.skip_same = {"pe"}

    def tile(self, name):
        t = TT(name)
        self.tiles.append(t)
        return t

    def tiles_n(self, name, n):
        return [self.tile("%s%d" % (name, i)) for i in range(n)]

    def _collect(self, reads, writes, part):
        evs = {}

        def add(d):
            for k, v in d.items():
                if k not in evs or evs[k][0] < v[0]:
                    evs[k] = v
        for t in reads:
            add(t.lw)
        for t in writes:
            if part and not t.rd:
                add(t.gen)
                continue
            g = dict(t.rd)
            for k, v in t.lw.items():
                if k not in g or g[k][0] < v[0]:
                    g[k] = v
            t.gen = g
            add(g)
        return evs

    def _waits(self, eng, evs):
        waits = []
        for k, (val, obj) in evs.items():
            if k == ("E", eng) and eng in self.skip_same:
                continue
            if self.seen[eng].get(k, 0) >= val:
                continue
            self.seen[eng][k] = val
            waits.append((k, val, obj))
            if k[0] == "E":
                self.ops[k[1]][val - 1]["inc"] = True
        return waits

    def _update(self, ev_key, ev_val, reads, writes, part):
        for t in reads:
            t.rd[ev_key] = ev_val
        for t in writes:
            if part and not t.rd:
                t.lw[ev_key] = ev_val
            else:
                t.lw = {ev_key: ev_val}
                t.rd = {}

    def op(self, eng, fn, reads=(), writes=(), part=False):
        waits = self._waits(eng, self._collect(reads, writes, part))
        self.ops[eng].append({"fn": fn, "waits": waits, "inc": False, "dma": None})
        idx = len(self.ops[eng])
        self._update(("E", eng), (idx, None), reads, writes, part)

    def dma(self, q, out, in_, owner, reads=(), writes=(), part=False, **kw):
        waits = self._waits(q, self._collect(reads, writes, part))
        rec = owner.dsems.get(q)
        if rec is None:
            if self.free_dsems[q]:
                rec = self.free_dsems[q].pop()
            else:
                rec = [self.stack.enter_context(self.nc.semaphore("ds%d" % self.nsem)), 0, self.nsem]
                self.nsem += 1
            owner.dsems[q] = rec
        rec[1] += 16
        self.ops[q].append({"fn": (lambda e: e.dma_start(out=out, in_=in_, **kw)), "waits": waits,
                            "inc": False, "dma": rec[0]})
        self._update(("D", rec[2]), (rec[1], rec[0]), reads, writes, part)

    def barrier(self, release=()):
        evs = {}
        for e in self.ENG:
            if e == "sp":
                continue
            idx = len(self.ops[e])
            while idx > 0 and (self.ops[e][idx - 1]["dma"] is not None or self.ops[e][idx - 1].get("nop")):
                idx -= 1
            if idx > 0:
                evs[("E", e)] = (idx, None)
        for t in self.tiles:
            for d in (t.lw, t.rd):
                for k, v in d.items():
                    if k[0] == "D" and (k not in evs or evs[k][0] < v[0]):
                        evs[k] = v
        for e in self.ENG:
            sk = self.skip_same
            self.skip_same = set()
            w = self._waits(e, dict(evs))
            self.skip_same = sk
            self.ops[e].append({"fn": (lambda en: en.nop()), "waits": w, "inc": False, "dma": None, "nop": True})
        for t in self.tiles:
            t.lw = {}
            t.rd = {}
            t.gen = {}
        rel = set(id(t) for t in release)
        for t in release:
            for q, rec in t.dsems.items():
                self.free_dsems[q].append(rec)
            t.dsems = {}
        self.tiles = [t for t in self.tiles if id(t) not in rel]

    def emit(self):
        nc = self.nc
        mile = {}
        for e in self.ENG:
            c = 0
            m = []
            for o in self.ops[e]:
                if o["inc"]:
                    c += 1
                m.append(c)
            mile[e] = m
            assert c < 60000, (e, c)
        with nc.Block() as block:
            for e in self.ENG:
                def body(engine, e=e):
                    for o in self.ops[e]:
                        for (k, val, obj) in o["waits"]:
                            if k[0] == "E":
                                engine.wait_ge(self.esem[k[1]], mile[k[1]][val - 1])
                            else:
                                engine.wait_ge(obj, val)
                        ins = o["fn"](engine)
                        if o["dma"] is not None:
                            ins.then_inc(o["dma"], 16)
                        elif o["inc"]:
                            ins.then_inc(self.esem[e], 1)
                getattr(block, self.BLK[e])(body)


class Prog:
    def __init__(self, cfg):
        self.cfg = cfg
        self.nc = bass.Bass("TRN2", target_bir_lowering=False)
        self.dbg = cfg.get("debug", ())

    def dram(self, name, shape, dt, kind="Internal"):
        if name in self.dbg:
            kind = "ExternalOutput"
        if name in self.cfg.get("ext_in", ()):
            kind = "ExternalInput"
        return self.nc.dram_tensor(name, list(shape), dt, kind=kind).ap()

    def sb(self, stack, name, shape, dt):
        self.uid = getattr(self, "uid", 0) + 1
        return stack.enter_context(self.nc.sbuf_tensor("%s_u%d" % (name, self.uid), list(shape), dt))

    def ps(self, stack, name, shape, dt):
        self.uid = getattr(self, "uid", 0) + 1
        return stack.enter_context(self.nc.psum_tensor("%s_u%d" % (name, self.uid), list(shape), dt))


IN_SIZES = (1024, 1536, 16, 16, 512, 512, 1024, 1024, 16, 16, 1024, 1024, 1024, 3072)
IN_OFF = [0]
for _s in IN_SIZES:
    IN_OFF.append(IN_OFF[-1] + _s)
(O_Z, O_XBC, O_DTF, O_DTB, O_GQ, O_GK, O_GV, O_GG, O_GAF, O_GAB, O_NQ, O_NK, O_NV, O_GATE, _) = IN_OFF

PARAM_NAMES = ["norm_mix_w", "w_in", "ssd_conv_w", "ssd_conv_b", "ssd_dt_bias_f", "ssd_dt_bias_b",
               "ssd_a_log_f", "ssd_a_log_b", "ssd_d", "ssd_norm_w", "gla_a2_f", "gla_a2_bias_f",
               "gla_a2_b", "gla_a2_bias_b", "gla_norm_w", "na_q_norm_w", "na_k_norm_w", "na_rpb",
               "w_branch_ssd", "w_branch_gla", "w_branch_na", "w_out", "norm_mlp_w", "w_ff1", "w_ff2"]
PARAM_SHAPES = {
    "norm_mix_w": (2, 1024), "w_in": (2, 1024, 11840), "ssd_conv_w": (2, 5, 1536), "ssd_conv_b": (2, 1536),
    "ssd_dt_bias_f": (2, 16), "ssd_dt_bias_b": (2, 16), "ssd_a_log_f": (2, 16), "ssd_a_log_b": (2, 16),
    "ssd_d": (2, 16), "ssd_norm_w": (2, 1024), "gla_a2_f": (2, 16, 512), "gla_a2_bias_f": (2, 512),
    "gla_a2_b": (2, 16, 512), "gla_a2_bias_b": (2, 512), "gla_norm_w": (2, 256), "na_q_norm_w": (2, 64),
    "na_k_norm_w": (2, 64), "na_rpb": (2, 16, 15, 31), "w_branch_ssd": (2, 1024, 1024),
    "w_branch_gla": (2, 1024, 1024), "w_branch_na": (2, 1024, 1024), "w_out": (2, 1024, 1024),
    "norm_mlp_w": (2, 1024), "w_ff1": (2, 1024, 4096), "w_ff2": (2, 4096, 1024),
}


def build(cfg):
    P = Prog(cfg)
    nc = P.nc
    layers = cfg.get("layers", DEPTH)
    phases = cfg.get("phases", "ABCDEF")
    x_in = nc.dram_tensor("x", [S, D], F32, kind="ExternalInput").ap()
    prm = {n: nc.dram_tensor(n, list(PARAM_SHAPES[n]), F32, kind="ExternalInput").ap() for n in PARAM_NAMES}
    y_out = nc.dram_tensor("y", [S, D], F32, kind="ExternalOutput").ap()
    natt = nc.dram_tensor("na_tt", [DEPTH, 128, 8, 17, 64], F32, kind="ExternalInput").ap()

    U = {}
    for nm, w in (("z", 1024), ("gv", 1024), ("gg", 1024), ("nv", 1024)):
        U[nm] = P.dram("u_" + nm, [S, w], BF16)
    U["dt"] = P.dram("u_dt", [S, 32], F32)
    for nm, w in (("xbc", 1536), ("gq", 512), ("gk", 512), ("nq", 1024), ("nk", 1024), ("gate", 3072)):
        U[nm] = P.dram("u_" + nm + "T", [w, S], BF16)
    U["ga"] = P.dram("u_gaT", [32, S], BF16)
    YB = {nm: P.dram("yb_" + nm, [1024, S], BF16) for nm in ("ssd", "gla", "na")}
    ybw = P.dram("ybw", [S, 1024], BF16)

    with contextlib.ExitStack() as top:
        sc = Sched(nc, top)
        G = {}
        G["x"] = P.sb(top, "x_res", [128, NT, D], F32)
        G["xt"] = sc.tiles_n("x", NT)
        G["ident"] = P.sb(top, "ident", [128, 128], BF16)
        G["ident_t"] = sc.tile("ident")
        G["dram_t"] = {k: sc.tile("d_" + k) for k in list(U) + ["yb_ssd", "yb_gla", "yb_na", "ybw"]}
        G["ybw"] = ybw

        ones_f = P.sb(top, "ones_f", [128, 128], F32)
        ones_t = sc.tile("ones_f")
        sc.op("pool", lambda e: e.memset(ones_f[:], 1.0), writes=[ones_t])
        sc.op("pool", lambda e: e.affine_select(out=G["ident"][:], in_=ones_f[:], pattern=[[-1, 128]],
                                                compare_op=ALU.is_equal, fill=0.0, base=0,
                                                channel_multiplier=1),
              reads=[ones_t], writes=[G["ident_t"]])
        G["ones_f"] = ones_f
        G["eps"] = P.sb(top, "epsc", [128, 2], F32)
        G["eps_t"] = sc.tile("epsc")
        sc.op("pool", lambda e: e.memset(G["eps"][:], EPS), writes=[G["eps_t"]])
        G["one"] = P.sb(top, "onec", [128, 2], F32)
        G["one_t"] = sc.tile("onec")
        sc.op("pool", lambda e: e.memset(G["one"][:], 1.0), writes=[G["one_t"]])
        G["neghalf"] = P.sb(top, "neghalf", [128, 16], F32)
        G["neghalf_t"] = sc.tile("neghalf")
        sc.op("pool", lambda e: e.memset(G["neghalf"][:], -0.5), writes=[G["neghalf_t"]])
        G["ones_t"] = ones_t

        build_tri(P, sc, G, top)
        xv = x_in.rearrange("(i p) d -> p i d", p=128)
        for i in range(NT):
            sc.dma("sp", G["x"][:, i, :], xv[:, i, :], owner=G["xt"][i], writes=[G["xt"][i]])

        for l in range(layers):
            if "A" in phases:
                phase_A(P, sc, G, U, prm, l)
            if "B" in phases:
                phase_B(P, sc, G, U, YB, prm, l)
            if "C" in phases:
                phase_C(P, sc, G, U, YB, prm, l)
            if "D" in phases:
                phase_D(P, sc, G, U, YB, prm, natt, l)
            if "E" in phases:
                phase_E(P, sc, G, U, YB, prm, l)
            if "F" in phases:
                phase_F(P, sc, G, prm, l)

        yv = y_out.rearrange("(i p) d -> p i d", p=128)
        outt = sc.tile("yout")
        for i in range(NT):
            sc.dma("sp", yv[:, i, :], G["x"][:, i, :], owner=G["xt"][i], reads=[G["xt"][i]], writes=[outt],
                   part=True)
        sc.op("sp", lambda e: e.nop(), reads=[outt])
        sc.barrier()
        sc.emit()
    return nc


def rms_transpose(P, sc, G, ph, wrow_ap, hT, hT_t, l, tag):
    nc = P.nc
    wb = P.sb(ph, tag + "_wb", [128, D], F32)
    wb_t = sc.tile(tag + "_wb")
    sc.dma("sp", wb[:], wrow_ap.partition_broadcast(128), owner=wb_t, writes=[wb_t])
    junk = [P.sb(ph, tag + "_junk%d" % i, [128, D], BF16) for i in range(2)]
    junk_t = sc.tiles_n(tag + "_junk", 2)
    hb = [P.sb(ph, tag + "_hb%d" % i, [128, D], BF16) for i in range(2)]
    hb_t = sc.tiles_n(tag + "_hb", 2)
    ss = [P.sb(ph, tag + "_ss%d" % i, [128, 2], F32) for i in range(2)]
    ss_t = sc.tiles_n(tag + "_ss", 2)
    tp = [P.ps(ph, tag + "_tp%d" % i, [128, 4, 128], BF16) for i in range(2)]
    tp_t = sc.tiles_n(tag + "_tp", 2)
    x = G["x"]
    new_tiles = [wb_t] + junk_t + hb_t + ss_t + tp_t
    def _s1(i):
        b = i % 2
        xt = G["xt"][i]
        sc.op("dve", lambda e, i=i, b=b: e.scalar_tensor_tensor(out=junk[b][:], in0=x[:, i, :], scalar=1.0,
                                                                in1=x[:, i, :], op0=ALU.mult, op1=ALU.mult,
                                                                accum_out=ss[b][:, 0:1]),
              reads=[xt], writes=[junk_t[b], ss_t[b]])
        sc.op("dve", lambda e, b=b: e.tensor_scalar(out=ss[b][:, 1:2], in0=ss[b][:, 0:1], scalar1=1.0 / D,
                                                    scalar2=EPS, op0=ALU.mult, op1=ALU.add),
              reads=[ss_t[b]], writes=[ss_t[b]])
        sc.op("pool", lambda e, b=b: e.tensor_tensor(out=ss[b][:, 0:1], in0=ss[b][:, 1:2],
                                                     in1=G["neghalf"][:, 0:1], op=ALU.pow),
              reads=[ss_t[b], G["neghalf_t"]], writes=[ss_t[b]])

    def _s2(i):
        b = i % 2
        xt = G["xt"][i]
        sc.op("dve", lambda e, i=i, b=b: e.scalar_tensor_tensor(out=hb[b][:], in0=x[:, i, :],
                                                                scalar=ss[b][:, 0:1], in1=wb[:],
                                                                op0=ALU.mult, op1=ALU.mult),
              reads=[xt, ss_t[b], wb_t], writes=[hb_t[b]])
        for half in range(2):
            pb = (2 * i + half) % 2
            for j in range(4):
                kc = half * 4 + j
                sc.op("pe", lambda e, b=b, pb=pb, j=j, kc=kc: e.transpose(
                    out=tp[pb][:, j, :], in_=hb[b][:, kc * 128:(kc + 1) * 128], identity=G["ident"][:]),
                    reads=[hb_t[b], G["ident_t"]], writes=[tp_t[pb]], part=(j > 0))
            eng = "act"
            if eng == "act":
                sc.op("act", lambda e, pb=pb, half=half, i=i: e.copy(
                    out=hT[:, half * 4:half * 4 + 4, i * 128:(i + 1) * 128], in_=tp[pb][:]),
                    reads=[tp_t[pb]], writes=[hT_t[i]], part=True)
            else:
                sc.op("dve", lambda e, pb=pb, half=half, i=i: e.tensor_copy(
                    out=hT[:, half * 4:half * 4 + 4, i * 128:(i + 1) * 128], in_=tp[pb][:]),
                    reads=[tp_t[pb]], writes=[hT_t[i]], part=True)

    _s1(0)
    for i in range(NT):
        if i + 1 < NT:
            _s1(i + 1)
        _s2(i)
    return new_tiles


class WStream:
    def __init__(self, P, sc, ph, tag, kdim, ncol, nf=2, nb=3, cast_eng="pool"):
        self.sc = sc
        self.cast_eng = cast_eng
        self.kdim, self.ncol = kdim, ncol
        self.nf, self.nb = nf, nb
        self.f = [P.sb(ph, "%s_wf%d" % (tag, i), [128, kdim, ncol], F32) for i in range(nf)]
        self.f_t = sc.tiles_n(tag + "_wf", nf)
        self.b = [P.sb(ph, "%s_wb%d" % (tag, i), [128, kdim, ncol], BF16) for i in range(nb)]
        self.b_t = sc.tiles_n(tag + "_wbt", nb)
        self.tiles = self.f_t + self.b_t
        self.items = []

    def start(self, items):
        self.items = items
        self._load(0)
        self._load(1)
        self._cast(0)

    def _load(self, g):
        if g >= len(self.items):
            return
        ap, k, n = self.items[g]
        fs = g % self.nf
        self.sc.dma("sp", self.f[fs][:, 0:k, 0:n], ap, owner=self.f_t[fs], writes=[self.f_t[fs]])

    def _cast(self, g):
        if g >= len(self.items):
            return
        ap, k, n = self.items[g]
        fs, bs = g % self.nf, g % self.nb
        if self.cast_eng == "act":
            self.sc.op("act", lambda e: e.copy(out=self.b[bs][:, 0:k, 0:n], in_=self.f[fs][:, 0:k, 0:n]),
                       reads=[self.f_t[fs]], writes=[self.b_t[bs]])
        else:
            self.sc.op("pool", lambda e: e.tensor_copy(out=self.b[bs][:, 0:k, 0:n], in_=self.f[fs][:, 0:k, 0:n]),
                       reads=[self.f_t[fs]], writes=[self.b_t[bs]])

    def get(self, g):
        self._cast(g + 1)
        self._load(g + 2)
        return self.b[g % self.nb], self.b_t[g % self.nb]


def proj_groups():
    g = []

    def seg(off, n, mode, key):
        c = 0
        while c < n:
            w = min(512, n - c)
            g.append((off + c, w, mode, key, c))
            c += w
    seg(O_Z, 1024, "tok", "z")
    seg(O_XBC, 1536, "feat", "xbc")
    g.append((O_DTF, 32, "tok32", "dt", 0))
    seg(O_GQ, 512, "feat", "gq")
    seg(O_GK, 512, "feat", "gk")
    seg(O_GV, 1024, "tok", "gv")
    seg(O_GG, 1024, "tok", "gg")
    g.append((O_GAF, 32, "feat32", "ga", 0))
    seg(O_NQ, 1024, "feat", "nq")
    seg(O_NK, 1024, "feat", "nk")
    seg(O_NV, 1024, "tok", "nv")
    seg(O_GATE, 3072, "feat", "gate")
    return g


def phase_A(P, sc, G, U, prm, l):
    nc = P.nc
    with contextlib.ExitStack() as ph:
        hT = P.sb(ph, "A_hT", [128, 8, S], BF16)
        hT_t = sc.tiles_n("A_hT", NT)
        tiles = list(hT_t)
        tiles += rms_transpose(P, sc, G, ph, prm["norm_mix_w"][l], hT, hT_t, l, "A")
        wst = WStream(P, sc, ph, "A", 8, 512)
        acc = [P.ps(ph, "A_acc%d" % i, [128, 512], F32) for i in range(4)]
        acc_t = sc.tiles_n("A_acc", 4)
        NS = 3
        stg = [P.sb(ph, "A_stg%d" % i, [128, 2048], BF16) for i in range(NS)]
        stg_t = sc.tiles_n("A_stg", NS)
        stf = [P.sb(ph, "A_stf%d" % i, [128, 4, 32], F32) for i in range(2)]
        stf_t = sc.tiles_n("A_stf", 2)
        G["ga_stage"] = P.sb(ph, "A_gast", [32, 2048], BF16)
        G["ga_stage_t"] = sc.tile("A_gast")
        tiles += wst.tiles + acc_t + stg_t + stf_t + [G["ga_stage_t"]]
        wv = prm["w_in"][l].rearrange("(kc p) n -> p kc n", p=128)
        groups = proj_groups()
        wst.start([(wv[:, :, c0:c0 + n], 8, n) for (c0, n, _m, _k, _d) in groups])
        ai = 0
        si = 0
        ev = 0
        for gi, (c0, n, mode, key, doff) in enumerate(groups):
            wcur, wcur_t = wst.get(gi)
            dst = U[key]
            dst_t = G["dram_t"][key]
            if mode in ("tok", "tok32"):
                for tb in range(4):
                    if mode == "tok":
                        st = si % NS
                        si += 1
                    else:
                        st = tb % 2
                    for j in range(4):
                        i = tb * 4 + j
                        a = ai % 4
                        ai += 1
                        for kc in range(8):
                            sc.op("pe", lambda e, a=a, kc=kc, i=i, wcur=wcur, n=n: e.matmul(
                                acc[a][:, 0:n], lhsT=hT[:, kc, i * 128:(i + 1) * 128], rhs=wcur[:, kc, 0:n],
                                start=(kc == 0), stop=(kc == 7)),
                                reads=[hT_t[i], wcur_t], writes=[acc_t[a]], part=(kc > 0))
                        if mode == "tok":
                            o_ap = stg[st][:, j * 512:j * 512 + n]
                            o_t = stg_t[st]
                        else:
                            o_ap = stf[st][:, j, 0:n]
                            o_t = stf_t[st]
                        ev += 1
                        if ev % 2 == 0:
                            sc.op("act", lambda e, o_ap=o_ap, a=a, n=n: e.copy(out=o_ap, in_=acc[a][:, 0:n]),
                                  reads=[acc_t[a]], writes=[o_t], part=(j > 0))
                        else:
                            sc.op("dve", lambda e, o_ap=o_ap, a=a, n=n: e.tensor_copy(out=o_ap, in_=acc[a][:, 0:n]),
                                  reads=[acc_t[a]], writes=[o_t], part=(j > 0))
                    rows = dst[tb * 512:(tb + 1) * 512, doff:doff + n].rearrange("(j p) c -> p j c", p=128)
                    if mode == "tok":
                        src = stg[st][:].rearrange("p (j c) -> p j c", j=4)[:, :, 0:n]
                        sc.dma("pool", rows, src, owner=stg_t[st], reads=[stg_t[st]], writes=[dst_t], part=True)
                    else:
                        sc.dma("pool", rows, stf[st][:, :, 0:n], owner=stf_t[st], reads=[stf_t[st]], writes=[dst_t],
                               part=True)
            else:
                nchunk = (n + 127) // 128
                for c in range(nchunk):
                    m = min(128, n - c * 128)
                    if mode == "feat":
                        st = si % NS
                        si += 1
                    else:
                        st = 0
                    for tb in range(4):
                        a = ai % 4
                        ai += 1
                        for kc in range(8):
                            sc.op("pe", lambda e, a=a, kc=kc, tb=tb, wcur=wcur, c=c, m=m: e.matmul(
                                acc[a][0:m, :], lhsT=wcur[:, kc, c * 128:c * 128 + m],
                                rhs=hT[:, kc, tb * 512:(tb + 1) * 512], start=(kc == 0), stop=(kc == 7)),
                                reads=hT_t[tb * 4:tb * 4 + 4] + [wcur_t], writes=[acc_t[a]], part=(kc > 0))
                        ev += 1
                        if mode == "feat":
                            o_ap = stg[st][0:m, tb * 512:(tb + 1) * 512]
                            o_t = stg_t[st]
                            if ev % 2 == 0:
                                sc.op("act", lambda e, o_ap=o_ap, a=a, m=m: e.copy(out=o_ap, in_=acc[a][0:m, :]),
                                      reads=[acc_t[a]], writes=[o_t], part=(tb > 0))
                            else:
                                sc.op("dve", lambda e, o_ap=o_ap, a=a, m=m: e.tensor_copy(out=o_ap, in_=acc[a][0:m, :]),
                                      reads=[acc_t[a]], writes=[o_t], part=(tb > 0))
                        else:
                            sc.op("dve", lambda e, a=a, m=m, tb=tb, gast=G["ga_stage"]: e.tensor_copy(
                                out=gast[0:m, tb * 512:(tb + 1) * 512], in_=acc[a][0:m, :]),
                                reads=[acc_t[a]], writes=[G["ga_stage_t"]], part=(tb > 0))
                    if mode == "feat":
                        sc.dma("pool", dst[doff + c * 128:doff + c * 128 + m, :], stg[st][0:m, :], owner=stg_t[st],
                               reads=[stg_t[st]], writes=[dst_t], part=True)
                    else:
                        sc.dma("pool", dst[0:m, :], G["ga_stage"][0:m, :], owner=G["ga_stage_t"],
                               reads=[G["ga_stage_t"]], writes=[dst_t], part=True)
        sc.barrier(release=tiles)


def phase_E(P, sc, G, U, YB, prm, l):
    nc = P.nc
    x = G["x"]
    with contextlib.ExitStack() as ph:
        wst = WStream(P, sc, ph, "E", 8, 256, nf=2, nb=2, cast_eng="act")
        mix = P.sb(ph, "E_mix", [128, 8, 1024], F32)
        mix_t = sc.tiles_n("E_mix", 8)
        mixb = P.sb(ph, "E_mixb", [128, 8, 1024], BF16)
        mixb_t = sc.tile("E_mixb")
        ybT = [P.sb(ph, "E_yb%d" % i, [128, 8, 1024], BF16) for i in range(2)]
        ybT_t = sc.tiles_n("E_yb", 2)
        gsl = [P.sb(ph, "E_g%d" % i, [128, 1024], BF16) for i in range(3)]
        gsl_t = sc.tiles_n("E_g", 3)
        sig = [P.sb(ph, "E_sig%d" % i, [128, 1024], F32) for i in range(2)]
        sig_t = sc.tiles_n("E_sig", 2)
        tmp = [P.sb(ph, "E_tmp%d" % i, [128, 512], F32) for i in range(2)]
        tmp_t = sc.tiles_n("E_tmp", 2)
        acc = [P.ps(ph, "E_acc%d" % i, [128, 512], F32) for i in range(4)]
        acc_t = sc.tiles_n("E_acc", 4)
        tiles = wst.tiles + mix_t + [mixb_t] + ybT_t + gsl_t + sig_t + tmp_t + acc_t
        wnames = ["w_branch_ssd", "w_branch_gla", "w_branch_na", "w_out"]
        bnames = ["ssd", "gla", "na"]
        items = []
        for half in range(2):
            for wn in wnames:
                wv = prm[wn][l].rearrange("(kc p) n -> p kc n", p=128)
                for cg in range(4):
                    items.append((wv[:, :, cg * 256:(cg + 1) * 256], 8, 256))
        wst.start(items)
        gi = 0
        ai = 0
        gcount = 0
        tcount = 0
        ybcount = 0
        for half in range(2):
            t0 = half * 1024
            for b in range(3):
                ys = ybcount % 2
                ybcount += 1
                ybv = YB[bnames[b]].rearrange("(kc p) t -> p kc t", p=128)
                sc.dma("sp", ybT[ys][:], ybv[:, :, t0:t0 + 1024], owner=ybT_t[ys],
                       reads=[G["dram_t"]["yb_" + bnames[b]]], writes=[ybT_t[ys]])
                for cg in range(4):
                    wcur, wcur_t = wst.get(gi)
                    gi += 1
                    for ecl in range(2):
                        ec = cg * 2 + ecl
                        gs = gcount % 3
                        ss_ = gcount % 2
                        gcount += 1
                        grow = b * 1024 + ec * 128
                        sc.dma("sp", gsl[gs][:], U["gate"][grow:grow + 128, t0:t0 + 1024], owner=gsl_t[gs],
                               reads=[G["dram_t"]["gate"]], writes=[gsl_t[gs]])
                        sc.op("act", lambda e, gs=gs, ss_=ss_: e.activation(out=sig[ss_][:], in_=gsl[gs][:],
                                                                            func=AF.Sigmoid),
                              reads=[gsl_t[gs]], writes=[sig_t[ss_]])
                        for tbh in range(2):
                            a = ai % 4
                            ai += 1
                            for kc in range(8):
                                sc.op("pe", lambda e, a=a, kc=kc, wcur=wcur, ecl=ecl, ys=ys, tbh=tbh: e.matmul(
                                    acc[a][:], lhsT=wcur[:, kc, ecl * 128:(ecl + 1) * 128],
                                    rhs=ybT[ys][:, kc, tbh * 512:(tbh + 1) * 512], start=(kc == 0), stop=(kc == 7)),
                                    reads=[wcur_t, ybT_t[ys]], writes=[acc_t[a]], part=(kc > 0))
                            msl = mix[:, ec, tbh * 512:(tbh + 1) * 512]
                            sgl = sig[ss_][:, tbh * 512:(tbh + 1) * 512]
                            if b == 0:
                                sc.op("dve", lambda e, msl=msl, a=a, sgl=sgl: e.tensor_tensor(
                                    out=msl, in0=acc[a][:], in1=sgl, op=ALU.mult),
                                    reads=[acc_t[a], sig_t[ss_]], writes=[mix_t[ec]], part=(tbh > 0))
                            else:
                                ts = tcount % 2
                                tcount += 1
                                sc.op("dve", lambda e, ts=ts, a=a, sgl=sgl: e.tensor_tensor(
                                    out=tmp[ts][:], in0=acc[a][:], in1=sgl, op=ALU.mult),
                                    reads=[acc_t[a], sig_t[ss_]], writes=[tmp_t[ts]])
                                if b == 1:
                                    sc.op("pool", lambda e, msl=msl, ts=ts: e.tensor_tensor(
                                        out=msl, in0=msl, in1=tmp[ts][:], op=ALU.add),
                                        reads=[tmp_t[ts], mix_t[ec]], writes=[mix_t[ec]])
                                else:
                                    sc.op("pool", lambda e, msl=msl, ts=ts, ec=ec, tbh=tbh: e.tensor_tensor(
                                        out=mixb[:, ec, tbh * 512:(tbh + 1) * 512], in0=msl, in1=tmp[ts][:],
                                        op=ALU.add),
                                        reads=[tmp_t[ts], mix_t[ec]], writes=[mixb_t], part=True)
            for cg in range(4):
                wcur, wcur_t = wst.get(gi)
                gi += 1
                for j in range(8):
                    i = half * 8 + j
                    a = ai % 4
                    ai += 1
                    for ec in range(8):
                        sc.op("pe", lambda e, a=a, ec=ec, wcur=wcur, j=j: e.matmul(
                            acc[a][:, 0:256], lhsT=mixb[:, ec, j * 128:(j + 1) * 128], rhs=wcur[:, ec, :],
                            start=(ec == 0), stop=(ec == 7)),
                            reads=[wcur_t, mixb_t], writes=[acc_t[a]], part=(ec > 0))
                    xs = x[:, i, cg * 256:(cg + 1) * 256]
                    sc.op("dve", lambda e, xs=xs, a=a: e.tensor_tensor(out=xs, in0=xs, in1=acc[a][:, 0:256], op=ALU.add),
                          reads=[acc_t[a], G["xt"][i]], writes=[G["xt"][i]])
        sc.barrier(release=tiles)


class Ring:
    def __init__(self, P, sc, stack, name, shape, dt, n, psum=False, views=None):
        if views is not None:
            self.h = views
            n = len(views)
        else:
            mk = P.ps if psum else P.sb
            self.h = [mk(stack, "%s%d" % (name, i), shape, dt) for i in range(n)]
        self.t = sc.tiles_n(name + "_", n)
        self.i = 0
        self.n = n

    def next(self):
        k = self.i % self.n
        self.i += 1
        return self.h[k], self.t[k]


def build_tri(P, sc, G, top):
    for nm in ("trif", "trib", "trif64", "trib64", "mcf64", "mcb64", "trifs", "tribs"):
        G[nm] = P.sb(top, nm, [128, 128], F32)
        G[nm + "_t"] = sc.tile(nm)
    ones_f, ones_t = G["ones_f"], G["ones_t"]
    sc.op("pool", lambda e: e.affine_select(out=G["trif"][:], in_=ones_f[:], pattern=[[1, 128]], compare_op=ALU.is_ge,
                                            fill=0.0, base=0, channel_multiplier=-1),
          reads=[ones_t], writes=[G["trif_t"]])
    sc.op("pool", lambda e: e.affine_select(out=G["trib"][:], in_=ones_f[:], pattern=[[-1, 128]], compare_op=ALU.is_ge,
                                            fill=0.0, base=0, channel_multiplier=1),
          reads=[ones_t], writes=[G["trib_t"]])
    sc.op("pool", lambda e: e.affine_select(out=G["trifs"][:], in_=ones_f[:], pattern=[[1, 128]], compare_op=ALU.is_gt,
                                            fill=0.0, base=0, channel_multiplier=-1),
          reads=[ones_t], writes=[G["trifs_t"]])
    sc.op("pool", lambda e: e.affine_select(out=G["tribs"][:], in_=ones_f[:], pattern=[[-1, 128]], compare_op=ALU.is_gt,
                                            fill=0.0, base=0, channel_multiplier=1),
          reads=[ones_t], writes=[G["tribs_t"]])
    sc.op("pool", lambda e: e.tensor_copy(out=G["trif64"][:], in_=G["trif"][:]), reads=[G["trif_t"]], writes=[G["trif64_t"]])
    sc.op("pool", lambda e: e.memset(G["trif64"][0:64, 64:128], 0.0), reads=[G["trif64_t"]], writes=[G["trif64_t"]])
    sc.op("pool", lambda e: e.tensor_copy(out=G["trib64"][:], in_=G["trib"][:]), reads=[G["trib_t"]], writes=[G["trib64_t"]])
    sc.op("pool", lambda e: e.memset(G["trib64"][64:128, 0:64], 0.0), reads=[G["trib64_t"]], writes=[G["trib64_t"]])
    sc.op("pool", lambda e: e.tensor_scalar(out=G["mcf64"][:], in0=G["trif64"][:], scalar1=-1.0 / 16.0, scalar2=None,
                                            op0=ALU.mult), reads=[G["trif64_t"]], writes=[G["mcf64_t"]])
    sc.op("pool", lambda e: e.tensor_scalar(out=G["mcb64"][:], in0=G["trib64"][:], scalar1=-1.0 / 16.0, scalar2=None,
                                            op0=ALU.mult), reads=[G["trib64_t"]], writes=[G["mcb64_t"]])
    for nm in ("mcf64", "mcb64", "trifs", "tribs"):
        G[nm + "b"] = P.sb(top, nm + "b", [128, 128], BF16)
        G[nm + "b_t"] = sc.tile(nm + "b")
        sc.op("pool", lambda e, nm=nm: e.tensor_copy(out=G[nm + "b"][:], in_=G[nm][:]), reads=[G[nm + "_t"]],
              writes=[G[nm + "b_t"]])


def phase_C(P, sc, G, U, YB, prm, l):
    nc = P.nc
    ident = G["ident"]
    with contextlib.ExitStack() as ph:
        qT = P.sb(ph, "C_qT", [128, 4, S], BF16)
        kT = P.sb(ph, "C_kT", [128, 4, S], BF16)
        qT_t = sc.tile("C_qT")
        kT_t = sc.tile("C_kT")
        ob = P.sb(ph, "C_ob", [128, NT, 1024], BF16)
        ob_t = sc.tiles_n("C_ob", NT)
        gaX = P.sb(ph, "C_gaX", [32, S], BF16)
        gaX_t = sc.tile("C_gaX")
        a2X = [P.sb(ph, "C_a2X%d" % d, [32, 512], BF16) for d in range(2)]
        a2X_t = sc.tiles_n("C_a2X", 2)
        a2f = P.sb(ph, "C_a2f", [32, 512], F32)
        a2f_t = sc.tile("C_a2f")
        nwb = P.sb(ph, "C_nwb", [128, 256], F32)
        nwb_t = sc.tile("C_nwb")
        Sf = P.sb(ph, "C_Sf", [128, 4, 256], F32)
        Sf_t = sc.tile("C_Sf")
        yst = P.sb(ph, "C_yst", [128, 8, 256], BF16)
        yst_t = sc.tile("C_yst")
        R = lambda name, shape, dt, n, psum=False: Ring(P, sc, ph, "C_" + name, shape, dt, n, psum)
        r_Sb = R("Sb", [128, 4, 256], BF16, 3)
        r_v = R("v", [128, 1024], BF16, 2)
        r_gg = R("gg", [128, 4, 1024], BF16, 1)
        r_e1 = R("e1", [128, 512], F32, 1)
        r_gn = R("gn", [128, 512], BF16, 1)
        r_bs = R("bs", [128, 4, 128], F32, 1)
        r_eb = R("eb", [128, 4, 128], F32, 1)
        r_enb = R("enb", [128, 4, 128], F32, 1)
        r_ew = R("ew", [128, 4, 128], F32, 1)
        r_ed = R("ed", [128, 4, 2], F32, 3)
        r_qd = R("qd", [128, 4, 128], BF16, 2)
        r_kd = R("kd", [128, 4, 128], BF16, 2)
        r_kw = R("kw", [128, 4, 128], BF16, 1)
        r_kwt = R("kwt", [128, 4, 128], BF16, 2)
        r_am = R("am", [128, 4, 128], BF16, 2)
        r_oa = R("oa", [128, 1024], F32, 1)
        r_sg = R("sg", [128, 4, 1024], BF16, 1)
        r_jk = R("jk", [128, 256], BF16, 1)
        r_ss = R("ss", [128, 8], F32, 2)
        r_y = R("y", [128, 1024], BF16, 2)
        r_gp = R("gp", [128, 512], F32, 1, True)
        r_bT = R("bT", [128, 4, 128], F32, 1, True)
        r_att = R("att", [128, 4, 128], F32, 1, True)
        r_kwp = R("kwp", [128, 4, 128], BF16, 1, True)
        r_st = R("st", [128, 4, 256], F32, 1, True)
        r_o = R("o", [128, 4, 256], F32, 1, True)
        rings = [r_Sb, r_v, r_gg, r_e1, r_gn, r_bs, r_eb, r_enb, r_ew, r_ed, r_qd, r_kd, r_kw, r_kwt, r_am, r_oa, r_sg,
                 r_jk, r_ss, r_y, r_gp, r_bT, r_att, r_kwp, r_st, r_o]
        tiles = [qT_t, kT_t, nwb_t, yst_t, gaX_t, Sf_t, a2f_t] + ob_t + a2X_t
        for r in rings:
            tiles += r.t
        sc.dma("sp", qT[:], U["gq"].rearrange("(h p) t -> p h t", p=128), owner=qT_t, reads=[G["dram_t"]["gq"]],
               writes=[qT_t])
        sc.dma("sp", kT[:], U["gk"].rearrange("(h p) t -> p h t", p=128), owner=kT_t, reads=[G["dram_t"]["gk"]],
               writes=[kT_t])
        sc.dma("sp", nwb[:], prm["gla_norm_w"][l].partition_broadcast(128), owner=nwb_t, writes=[nwb_t])
        for d in range(2):
            a2 = prm["gla_a2_f" if d == 0 else "gla_a2_b"][l]
            bi = prm["gla_a2_bias_f" if d == 0 else "gla_a2_bias_b"][l]
            sc.dma("sp", a2f[0:16, :], a2, owner=a2f_t, writes=[a2f_t])
            sc.dma("sp", a2f[16:17, :], bi.rearrange("(o n) -> o n", o=1), owner=a2f_t, writes=[a2f_t], part=True)
            sc.op("act", lambda e, d=d: e.copy(out=a2X[d][0:17, :], in_=a2f[0:17, :]), reads=[a2f_t], writes=[a2X_t[d]])

        def gla_pass(d):
            fwd = (d == 0)
            mc, mc_t = (G["mcf64b"], G["mcf64b_t"]) if fwd else (G["mcb64b"], G["mcb64b_t"])
            ma, ma_t = (G["trif64"], G["trif64_t"]) if fwd else (G["trib64"], G["trib64_t"])
            lc0 = 63 if fwd else 0
            sc.op("pool", lambda e: e.memset(gaX[:], 1.0), writes=[gaX_t])
            sc.dma("sp", gaX[0:16, :], U["ga"][16 * d:16 * d + 16, :], owner=gaX_t, reads=[G["dram_t"]["ga"]],
                   writes=[gaX_t])
            sc.op("pool", lambda e: e.memset(Sf[:], 0.0), writes=[Sf_t])
            sb0, sb0_t = r_Sb.next()
            sc.op("pool", lambda e, sb0=sb0: e.memset(sb0[:], 0.0), writes=[sb0_t])
            cur = [(sb0, sb0_t)]
            sgcur = [None]
            order = list(range(NT)) if fwd else list(range(NT - 1, -1, -1))
            chunks = (0, 1) if fwd else (1, 0)

            def stage1(i):
                tsl = slice(i * 128, (i + 1) * 128)
                v, v_t = r_v.next()
                sc.dma("sp", v[:], U["gv"][tsl, :], owner=v_t, reads=[G["dram_t"]["gv"]], writes=[v_t])
                gp, gp_t = r_gp.next()
                sc.op("pe", lambda e, gp=gp, tsl=tsl: e.matmul(gp[:], lhsT=gaX[0:17, tsl], rhs=a2X[d][0:17, :],
                                                               start=True, stop=True),
                      reads=[gaX_t, a2X_t[d]], writes=[gp_t])
                e1, e1_t = r_e1.next()
                sc.op("act", lambda e, e1=e1, gp=gp: e.activation(out=e1[:], in_=gp[:], func=AF.Exp, scale=-1.0),
                      reads=[gp_t], writes=[e1_t])
                gn, gn_t = r_gn.next()
                sc.op("act", lambda e, gn=gn, e1=e1: e.activation(out=gn[:], in_=e1[:], func=AF.Ln, bias=G["one"][:, 0:1]),
                      reads=[e1_t, G["one_t"]], writes=[gn_t])
                bT, bT_t = r_bT.next()
                for h in range(4):
                    sc.op("pe", lambda e, bT=bT, gn=gn, h=h: e.matmul(bT[:, h, :], lhsT=gn[:, h * 128:(h + 1) * 128], rhs=mc[:],
                                                                     start=True, stop=True, skip_group_check=True),
                          reads=[gn_t, mc_t], writes=[bT_t], part=(h > 0))
                bs, bs_t = r_bs.next()
                sc.op("act", lambda e, bs=bs, bT=bT: e.copy(out=bs[:], in_=bT[:]), reads=[bT_t], writes=[bs_t])
                eb, eb_t = r_eb.next()
                sc.op("act", lambda e, eb=eb, bs=bs: e.activation(out=eb[:], in_=bs[:], func=AF.Exp), reads=[bs_t], writes=[eb_t])
                enb, enb_t = r_enb.next()
                sc.op("act", lambda e, enb=enb, bs=bs: e.activation(out=enb[:], in_=bs[:], func=AF.Exp, scale=-1.0),
                      reads=[bs_t], writes=[enb_t])
                ed, ed_t = r_ed.next()
                sc.op("act", lambda e, ed=ed, bs=bs: e.activation(
                    out=ed[:], in_=bs[:].rearrange("p h (c l) -> p h c l", c=2)[:, :, :, lc0], func=AF.Exp),
                    reads=[bs_t], writes=[ed_t])
                qd, qd_t = r_qd.next()
                sc.op("dve", lambda e, qd=qd, tsl=tsl, eb=eb: e.scalar_tensor_tensor(
                    out=qd[:], in0=qT[:, :, tsl], scalar=128.0 ** -0.5, in1=eb[:], op0=ALU.mult, op1=ALU.mult),
                    reads=[qT_t, eb_t], writes=[qd_t])
                kd, kd_t = r_kd.next()
                sc.op("dve", lambda e, kd=kd, tsl=tsl, enb=enb: e.tensor_tensor(
                    out=kd[:], in0=kT[:, :, tsl], in1=enb[:], op=ALU.mult), reads=[kT_t, enb_t], writes=[kd_t])
                ew, ew_t = r_ew.next()
                sc.op("dve", lambda e, ew=ew, enb=enb, ed=ed: e.tensor_tensor(
                    out=ew[:].rearrange("p h (c l) -> p (h c) l", c=2), in0=enb[:].rearrange("p h (c l) -> p (h c) l", c=2),
                    in1=ed[:].rearrange("p h c -> p (h c)").unsqueeze(2).to_broadcast([128, 8, 64]), op=ALU.mult),
                    reads=[enb_t, ed_t], writes=[ew_t])
                kw, kw_t = r_kw.next()
                sc.op("dve", lambda e, kw=kw, tsl=tsl, ew=ew: e.tensor_tensor(
                    out=kw[:], in0=kT[:, :, tsl], in1=ew[:], op=ALU.mult), reads=[kT_t, ew_t], writes=[kw_t])
                kwp, kwp_t = r_kwp.next()
                for h in range(4):
                    sc.op("pe", lambda e, kwp=kwp, kw=kw, h=h: e.transpose(out=kwp[:, h, :], in_=kw[:, h, :], identity=ident[:]),
                          reads=[kw_t, G["ident_t"]], writes=[kwp_t], part=(h > 0))
                kwt, kwt_t = r_kwt.next()
                sc.op("act", lambda e, kwt=kwt, kwp=kwp: e.copy(out=kwt[:], in_=kwp[:]), reads=[kwp_t], writes=[kwt_t])
                att, att_t = r_att.next()
                for h in range(4):
                    sc.op("pe", lambda e, att=att, kd=kd, qd=qd, h=h: e.matmul(att[:, h, :], lhsT=kd[:, h, :], rhs=qd[:, h, :],
                                                                            start=True, stop=True, skip_group_check=True),
                          reads=[kd_t, qd_t], writes=[att_t], part=(h > 0))
                am, am_t = r_am.next()
                sc.op("dve", lambda e, am=am, att=att: e.tensor_tensor(
                    out=am[:], in0=att[:], in1=ma[:].unsqueeze(1).to_broadcast([128, 4, 128]), op=ALU.mult),
                    reads=[att_t, ma_t], writes=[am_t])
                return (i, tsl, v, v_t, qd, qd_t, kwt, kwt_t, ed, ed_t, am, am_t)

            def stage23(ctx):
                (i, tsl, v, v_t, qd, qd_t, kwt, kwt_t, ed, ed_t, am, am_t) = ctx
                sbs = [cur[0]]
                for ci, c in enumerate(chunks):
                    cs = slice(c * 64, (c + 1) * 64)
                    st, st_t = r_st.next()
                    for h in range(4):
                        sc.op("pe", lambda e, st=st, kwt=kwt, cs=cs, v=v, h=h: e.matmul(
                            st[:, h, :], lhsT=kwt[cs, h, :], rhs=v[cs, h * 256:(h + 1) * 256], start=True, stop=True,
                            skip_group_check=True),
                            reads=[kwt_t, v_t], writes=[st_t], part=(h > 0))
                    for h in range(4):
                        sc.op("dve", lambda e, st=st, h=h, ed=ed, c=c: e.scalar_tensor_tensor(
                            out=Sf[:, h, :], in0=Sf[:, h, :], scalar=ed[:, h, c:c + 1], in1=st[:, h, :], op0=ALU.mult,
                            op1=ALU.add),
                            reads=[st_t, ed_t, Sf_t], writes=[Sf_t])
                    nb, nb_t = r_Sb.next()
                    sc.op("act", lambda e, nb=nb: e.copy(out=nb[:], in_=Sf[:]), reads=[Sf_t], writes=[nb_t])
                    sbs.append((nb, nb_t))
                o, o_t = r_o.next()
                for h in range(4):
                    sc.op("pe", lambda e, o=o, am=am, v=v, h=h: e.matmul(o[:, h, :], lhsT=am[:, h, :],
                                                                       rhs=v[:, h * 256:(h + 1) * 256],
                                                                       start=True, stop=False, skip_group_check=True),
                          reads=[am_t, v_t], writes=[o_t], part=(h > 0))
                    for ci, c in enumerate(chunks):
                        cs = slice(c * 64, (c + 1) * 64)
                        sbv, sbv_t = sbs[ci]
                        sc.op("pe", lambda e, o=o, qd=qd, cs=cs, sbv=sbv, ci=ci, h=h: e.matmul(
                            o[cs, h, :], lhsT=qd[:, h, cs], rhs=sbv[:, h, :], start=False, stop=(ci == 1),
                            skip_group_check=True),
                            reads=[qd_t, sbv_t], writes=[o_t], part=True)
                cur[0] = sbs[2]
                if not fwd:
                    for hb in range(2):
                        sc.op("act", lambda e, o=o, hb=hb, i=i: e.copy(
                            out=ob[:, i, hb * 512:(hb + 1) * 512], in_=o[:, 2 * hb:2 * hb + 2, :].rearrange("p a b -> p (a b)")),
                            reads=[o_t], writes=[ob_t[i]], part=(hb > 0))
                    return
                oa, oa_t = r_oa.next()
                ss, ss_t = r_ss.next()
                for hb in range(2):
                    sc.op("dve", lambda e, oa=oa, o=o, hb=hb, i=i: e.tensor_tensor(
                        out=oa[:, hb * 512:(hb + 1) * 512], in0=o[:, 2 * hb:2 * hb + 2, :].rearrange("p a b -> p (a b)"),
                        in1=ob[:, i, hb * 512:(hb + 1) * 512], op=ALU.add),
                        reads=[o_t, ob_t[i]], writes=[oa_t], part=(hb > 0))
                for h in range(4):
                    hs = slice(h * 256, (h + 1) * 256)
                    jk, jk_t = r_jk.next()
                    sc.op("dve", lambda e, jk=jk, oa=oa, hs=hs, ss=ss, h=h: e.scalar_tensor_tensor(
                        out=jk[:], in0=oa[:, hs], scalar=1.0, in1=oa[:, hs], op0=ALU.mult, op1=ALU.mult,
                        accum_out=ss[:, h:h + 1]), reads=[oa_t], writes=[jk_t, ss_t])
                if i % 4 == 0:
                    gg, gg_t = r_gg.next()
                    sc.dma("sp", gg[:], U["gg"][i * 128:(i + 4) * 128, :].rearrange("(j p) c -> p j c", p=128), owner=gg_t,
                           reads=[G["dram_t"]["gg"]], writes=[gg_t])
                    sg, sg_t = r_sg.next()
                    sc.op("act", lambda e, sg=sg, gg=gg: e.activation(out=sg[:], in_=gg[:], func=AF.Silu),
                          reads=[gg_t], writes=[sg_t])
                    sgcur[0] = (sg, sg_t)
                sg, sg_t = sgcur[0]
                sgn = sg[:, i % 4, :]
                sgn_t = sg_t
                sc.op("pool", lambda e, sgn=sgn: e.tensor_tensor(
                    out=sgn.rearrange("p (h v) -> p h v", h=4), in0=sgn.rearrange("p (h v) -> p h v", h=4),
                    in1=nwb[:].unsqueeze(1).to_broadcast([128, 4, 256]), op=ALU.mult),
                    reads=[sg_t, nwb_t], writes=[sg_t])
                sc.op("dve", lambda e, ss=ss: e.tensor_scalar(out=ss[:, 4:8], in0=ss[:, 0:4], scalar1=1.0 / 256.0, scalar2=EPS,
                                                              op0=ALU.mult, op1=ALU.add), reads=[ss_t], writes=[ss_t])
                sc.op("pool", lambda e, ss=ss: e.tensor_tensor(out=ss[:, 0:4], in0=ss[:, 4:8], in1=G["neghalf"][:, 0:4],
                                                               op=ALU.pow), reads=[ss_t, G["neghalf_t"]], writes=[ss_t])
                sc.op("dve", lambda e, oa=oa, ss=ss: e.tensor_tensor(
                    out=oa[:].rearrange("p (h v) -> p h v", h=4), in0=oa[:].rearrange("p (h v) -> p h v", h=4),
                    in1=ss[:, 0:4].unsqueeze(2).to_broadcast([128, 4, 256]), op=ALU.mult),
                    reads=[oa_t, ss_t], writes=[oa_t])
                y, y_t = r_y.next()
                sc.op("dve", lambda e, y=y, oa=oa, sgn=sgn: e.tensor_tensor(out=y[:], in0=oa[:], in1=sgn, op=ALU.mult),
                      reads=[oa_t, sgn_t], writes=[y_t])
                return (y, y_t, i)

            def stage3(c3):
                if c3 is None:
                    return
                (y, y_t, i) = c3
                emit_yT(P, sc, G, r_kwp, y, y_t, yst, yst_t, i, YB["gla"], G["dram_t"]["yb_gla"], gsz=2)

            prev = None
            prev3 = None
            for i in order:
                ctx = stage1(i)
                if prev is not None:
                    n3 = stage23(prev)
                    stage3(prev3)
                    prev3 = n3
                prev = ctx
            n3 = stage23(prev)
            stage3(prev3)
            stage3(n3)

        gla_pass(1)
        if "dbg_ob" in P.dbg:
            dob = P.dram("dbg_ob", [S, 1024], BF16)
            dt_ = sc.tile("dbg_ob")
            sc.dma("sp", dob.rearrange("(i p) c -> p i c", p=128), ob[:], owner=ob_t[0], reads=ob_t, writes=[dt_])
        gla_pass(0)
        sc.barrier(release=tiles)


def phase_B(P, sc, G, U, YB, prm, l):
    nc = P.nc
    ident = G["ident"]
    ybw = G["ybw"]
    ybw_t = G["dram_t"]["ybw"]
    with contextlib.ExitStack() as ph:
        xtok = P.sb(ph, "B_xtok", [128, NT, 1280], BF16)
        xtok_t = sc.tiles_n("B_xtok", NT)
        BT = P.sb(ph, "B_BT", [128, 2, S], BF16)
        CT = P.sb(ph, "B_CT", [128, 2, S], BF16)
        BT_t = sc.tiles_n("B_BT", 2)
        CT_t = sc.tiles_n("B_CT", 2)
        dtv = P.sb(ph, "B_dtv", [128, NT, 32], F32)
        av = P.sb(ph, "B_av", [128, NT, 32], F32)
        dtv_t = sc.tile("B_dtv")
        av_t = sc.tile("B_av")
        rows = P.sb(ph, "B_rows", [128, 4, 32], F32)
        rows_t = sc.tile("B_rows")
        nwb = P.sb(ph, "B_nwb", [128, 1024], F32)
        nwb_t = sc.tile("B_nwb")
        tiles = xtok_t + BT_t + CT_t + [dtv_t, av_t, rows_t, nwb_t]
        with contextlib.ExitStack() as s1:
            cwr = P.sb(s1, "B_cwr", [72, 128], F32)
            cwr_t = sc.tile("B_cwr")
            cw = P.sb(s1, "B_cw", [128, 72], F32)
            cw_t = sc.tile("B_cw")
            cwp = P.ps(s1, "B_cwp", [128, 72], F32)
            cwp_t = sc.tile("B_cwp")
            identf = P.sb(s1, "B_identf", [128, 128], F32)
            identf_t = sc.tile("B_identf")
            xc = [P.sb(s1, "B_xc%d" % i, [128, S + 4], BF16) for i in range(2)]
            xc_t = sc.tiles_n("B_xc", 2)
            dg = [P.sb(s1, "B_dg%d" % i, [128, 5, 128], BF16) for i in range(2)]
            dg_t = sc.tiles_n("B_dg", 2)
            cacc = [P.ps(s1, "B_cacc%d" % i, [128, 512], F32) for i in range(2)]
            cacc_t = sc.tiles_n("B_cacc", 2)
            xa = [P.sb(s1, "B_xa%d" % i, [128, S], BF16) for i in range(2)]
            xa_t = sc.tiles_n("B_xa", 2)
            tp = [P.ps(s1, "B_tp%d" % i, [128, 4, 128], BF16) for i in range(2)]
            tp_t = sc.tiles_n("B_tp", 2)
            tl1 = [cwr_t, cw_t, cwp_t, identf_t] + dg_t + cacc_t + xc_t + xa_t + tp_t
            sc.op("pool", lambda e: e.affine_select(out=identf[:], in_=G["ones_f"][:], pattern=[[-1, 128]],
                                                    compare_op=ALU.is_equal, fill=0.0, base=0, channel_multiplier=1),
                  reads=[G["ones_t"]], writes=[identf_t])
            sc.dma("sp", cwr[0:60, :], prm["ssd_conv_w"][l].rearrange("k (c p) -> (k c) p", p=128), owner=cwr_t, writes=[cwr_t])
            sc.dma("sp", cwr[60:72, :], prm["ssd_conv_b"][l].rearrange("(c p) -> c p", p=128), owner=cwr_t, writes=[cwr_t],
                   part=True)
            sc.op("pe", lambda e: e.transpose(out=cwp[:], in_=cwr[:], identity=identf[0:72, 0:72]),
                  reads=[cwr_t, identf_t], writes=[cwp_t])
            sc.op("act", lambda e: e.copy(out=cw[:], in_=cwp[:]), reads=[cwp_t], writes=[cw_t])
            for b in range(2):
                sc.op("pool", lambda e, b=b: e.memset(xc[b][:, 0:2], 0.0), writes=[xc_t[b]])
                sc.op("pool", lambda e, b=b: e.memset(xc[b][:, S + 2:S + 4], 0.0), writes=[xc_t[b]], part=True)
            sc.dma("sp", dtv[:], U["dt"].rearrange("(i p) c -> p i c", p=128), owner=dtv_t, reads=[G["dram_t"]["dt"]],
                   writes=[dtv_t])
            for k, nm in enumerate(("ssd_dt_bias_f", "ssd_dt_bias_b")):
                sc.dma("sp", rows[:, 0, 16 * k:16 * k + 16], prm[nm][l].partition_broadcast(128), owner=rows_t,
                       writes=[rows_t], part=True)
            for k, nm in enumerate(("ssd_a_log_f", "ssd_a_log_b")):
                sc.dma("sp", rows[:, 1, 16 * k:16 * k + 16], prm[nm][l].partition_broadcast(128), owner=rows_t,
                       writes=[rows_t], part=True)
            sc.dma("sp", rows[:, 2, 0:16], prm["ssd_d"][l].partition_broadcast(128), owner=rows_t, writes=[rows_t], part=True)
            sc.dma("sp", nwb[:], prm["ssd_norm_w"][l].partition_broadcast(128), owner=nwb_t, writes=[nwb_t])
            sc.op("dve", lambda e: e.tensor_tensor(out=dtv[:], in0=dtv[:], in1=rows[:, 0:1, :].to_broadcast([128, NT, 32]),
                                                   op=ALU.add), reads=[dtv_t, rows_t], writes=[dtv_t])
            sc.op("act", lambda e: e.activation(out=dtv[:], in_=dtv[:], func=AF.Exp), reads=[dtv_t], writes=[dtv_t])
            sc.op("act", lambda e: e.activation(out=dtv[:], in_=dtv[:], func=AF.Ln, bias=G["one"][:, 0:1]),
                  reads=[dtv_t, G["one_t"]], writes=[dtv_t])
            sc.op("act", lambda e: e.activation(out=rows[:, 3, :], in_=rows[:, 1, :], func=AF.Exp), reads=[rows_t],
                  writes=[rows_t])
            sc.op("dve", lambda e: e.scalar_tensor_tensor(out=av[:], in0=dtv[:], scalar=-1.0,
                                                          in1=rows[:, 3:4, :].to_broadcast([128, NT, 32]),
                                                          op0=ALU.mult, op1=ALU.mult),
                  reads=[dtv_t, rows_t], writes=[av_t])
            tpc = 0
            for c in range(12):
                b = c % 2
                sc.dma("sp", xc[b][:, 2:S + 2], U["xbc"][c * 128:(c + 1) * 128, :], owner=xc_t[b],
                       reads=[G["dram_t"]["xbc"]], writes=[xc_t[b]], part=True)
                dgb = c % 2
                for k in range(5):
                    sc.op("dve", lambda e, dgb=dgb, k=k, c=c: e.tensor_scalar(
                        out=dg[dgb][:, k, :], in0=identf[:], scalar1=cw[:, k * 12 + c:k * 12 + c + 1], scalar2=None,
                        op0=ALU.mult), reads=[identf_t, cw_t], writes=[dg_t[dgb]], part=(k > 0))
                if c < 10:
                    xo, xo_t = xa[b], xa_t[b]
                    xsl = lambda tb: xa[b][:, tb * 512:(tb + 1) * 512]
                else:
                    xo_t = CT_t[c - 10]
                    xsl = lambda tb, c=c: CT[:, c - 10, tb * 512:(tb + 1) * 512]
                for tb in range(4):
                    ca, ca_t = cacc[(4 * c + tb) % 2], cacc_t[(4 * c + tb) % 2]
                    for k in range(5):
                        sc.op("pe", lambda e, ca=ca, dgb=dgb, k=k, b=b, tb=tb: e.matmul(
                            ca[:], lhsT=dg[dgb][:, k, :], rhs=xc[b][:, k + tb * 512:k + tb * 512 + 512],
                            start=(k == 0), stop=(k == 4)),
                            reads=[dg_t[dgb], xc_t[b]], writes=[ca_t], part=(k > 0))
                    sc.op("act", lambda e, ca=ca, o_ap=xsl(tb), c=c: e.activation(
                        out=o_ap, in_=ca[:], func=AF.Silu, bias=cw[:, 60 + c:61 + c]),
                        reads=[ca_t, cw_t], writes=[xo_t], part=(tb > 0))
                if c in (8, 9):
                    sc.op("pool", lambda e, b=b, c=c: e.tensor_copy(out=BT[:, c - 8, :], in_=xa[b][:]),
                          reads=[xa_t[b]], writes=[BT_t[c - 8]])
                if c < 10:
                    for i0 in range(0, NT, 4):
                        tb_ = tpc % 2
                        tpc += 1
                        for j in range(4):
                            i = i0 + j
                            sc.op("pe", lambda e, tb_=tb_, j=j, b=b, i=i: e.transpose(
                                out=tp[tb_][:, j, :], in_=xa[b][:, i * 128:(i + 1) * 128], identity=ident[:]),
                                reads=[xa_t[b], G["ident_t"]], writes=[tp_t[tb_]], part=(j > 0))
                        eng = "act" if (tpc % 2) else "pool"
                        if eng == "act":
                            sc.op("act", lambda e, tb_=tb_, i0=i0, c=c: e.copy(
                                out=xtok[:, i0:i0 + 4, c * 128:(c + 1) * 128], in_=tp[tb_][:]),
                                reads=[tp_t[tb_]], writes=xtok_t[i0:i0 + 4], part=True)
                        else:
                            sc.op("dve", lambda e, tb_=tb_, i0=i0, c=c: e.tensor_copy(
                                out=xtok[:, i0:i0 + 4, c * 128:(c + 1) * 128], in_=tp[tb_][:]),
                                reads=[tp_t[tb_]], writes=xtok_t[i0:i0 + 4], part=True)
            sc.barrier(release=tl1)
        with contextlib.ExitStack() as s2:
            R = lambda name, shape, dt, n, psum=False: Ring(P, sc, s2, "B_" + name, shape, dt, n, psum)
            Sf = P.sb(s2, "B_Sf", [128, 2, 512], F32)
            Sf_t = sc.tiles_n("B_Sf", 2)
            Sbx = P.sb(s2, "B_Sb", [128, 2, 2, 512], BF16)
            r_Sb = [Ring(P, sc, s2, "B_Sb%d" % g, None, None, 2, views=[Sbx[:, g, k, :] for k in range(2)]) for g in range(2)]
            r_cb = R("cb", [128, 128], F32, 1, True)
            r_seg = R("seg", [128, 512], F32, 2, True)
            r_sm = R("sm", [128, 3, 16], F32, 1, True)
            r_yd = R("yd", [128, 512], F32, 1, True)
            r_stp = R("stp", [128, 512], F32, 1, True)
            r_yo = R("yo", [128, 512], F32, 1, True)
            r_tp = R("tp2", [128, 4, 128], BF16, 1, True)
            r_cbm = R("cbm", [128, 128], F32, 2)
            r_am = R("am", [128, 4, 128], BF16, 4)
            r_dec = R("dec", [128, 4, 128], F32, 2)
            r_mt = R("mt", [128, 4, 128], BF16, 4)
            r_ea = R("ea", [128, 3, 16], F32, 2)
            r_xdt = R("xdt", [128, 1024], BF16, 3)
            r_xw = R("xw", [128, 1024], BF16, 2)
            r_t = R("t", [128, 512], F32, 2)
            r_ybl = R("ybl", [128, 1024], BF16, 2)
            r_yf = R("yf", [128, 1024], F32, 1)
            r_z = R("z", [128, 4, 1024], BF16, 1)
            r_jk = R("jk", [128, 512], F32, 1)
            r_ss = R("ss", [128, 4], F32, 2)
            r_y = R("y", [128, 1024], BF16, 2)
            yst = P.sb(s2, "B_yst", [128, 8, 256], BF16)
            yst_t = sc.tile("B_yst")
            rings = [r_cb, r_seg, r_sm, r_yd, r_stp, r_yo, r_tp, r_cbm, r_am, r_dec, r_mt, r_ea, r_xdt, r_xw, r_t, r_ybl,
                     r_yf, r_z, r_jk, r_ss, r_y] + r_Sb
            tl2 = Sf_t + [yst_t]
            for r in rings:
                tl2 += r.t

            def ssd_pass(d):
                fwd = (d == 0)
                tri_in, tri_in_t = (G["trif"], G["trif_t"]) if fwd else (G["trib"], G["trib_t"])
                tri_st, tri_st_t = (G["tribs"], G["tribs_t"]) if fwd else (G["trifs"], G["trifs_t"])
                tri_sb, tri_sb_t = (G["tribsb"], G["tribsb_t"]) if fwd else (G["trifsb"], G["trifsb_t"])
                cur = []
                for g in range(2):
                    sc.op("pool", lambda e, g=g: e.memset(Sf[:, g, :], 0.0), writes=[Sf_t[g]])
                    sb0, sb0_t = r_Sb[g].next()
                    sc.op("pool", lambda e, sb0=sb0: e.memset(sb0, 0.0), writes=[sb0_t])
                    cur.append((sb0, sb0_t))
                order = list(range(NT)) if fwd else list(range(NT - 1, -1, -1))
                zcur = [None]

                def tileA(i):
                    tsl = slice(i * 128, (i + 1) * 128)
                    acol = av[:, i, 16 * d:16 * d + 16]
                    ams = []
                    for u in range(4):
                        h0 = u * 4
                        am, am_t = r_am.next()
                        for hh in range(4):
                            sc.op("act", lambda e, am=am, i=i, h0=h0, hh=hh: e.activation(
                                out=am[:, hh, :], in_=tri_in[:], func=AF.Copy,
                                scale=av[:, i, 16 * d + h0 + hh:16 * d + h0 + hh + 1]),
                                reads=[tri_in_t, av_t], writes=[am_t], part=(hh > 0))
                        ams.append((am, am_t))
                    sm, sm_t = r_sm.next()
                    sc.op("pe", lambda e, sm=sm, acol=acol: e.matmul(sm[:, 0, :], lhsT=tri_in[:], rhs=acol, start=True, stop=True),
                          reads=[tri_in_t, av_t], writes=[sm_t])
                    sc.op("pe", lambda e, sm=sm, acol=acol: e.matmul(sm[:, 1, :], lhsT=tri_st[:], rhs=acol, start=True, stop=True),
                          reads=[tri_st_t, av_t], writes=[sm_t], part=True)
                    sc.op("pe", lambda e, sm=sm, acol=acol: e.matmul(sm[:, 2, :], lhsT=G["ones_f"][:], rhs=acol, start=True,
                                                                     stop=True),
                          reads=[G["ones_t"], av_t], writes=[sm_t], part=True)
                    ea, ea_t = r_ea.next()
                    sc.op("act", lambda e, ea=ea, sm=sm: e.activation(out=ea[:], in_=sm[:], func=AF.Exp), reads=[sm_t],
                          writes=[ea_t])
                    xdt, xdt_t = r_xdt.next()
                    sc.op("dve", lambda e, xdt=xdt, i=i: e.tensor_tensor(
                        out=xdt[:].rearrange("p (h q) -> p h q", q=64), in0=xtok[:, i, 0:1024].rearrange("p (h q) -> p h q", q=64),
                        in1=dtv[:, i, 16 * d:16 * d + 16].unsqueeze(2).to_broadcast([128, 16, 64]), op=ALU.mult),
                        reads=[xtok_t[i], dtv_t], writes=[xdt_t])
                    xw, xw_t = r_xw.next()
                    sc.op("dve", lambda e, xw=xw, xdt=xdt, ea=ea: e.tensor_tensor(
                        out=xw[:].rearrange("p (h q) -> p h q", q=64), in0=xdt[:].rearrange("p (h q) -> p h q", q=64),
                        in1=ea[:, 1, :].unsqueeze(2).to_broadcast([128, 16, 64]), op=ALU.mult),
                        reads=[xdt_t, ea_t], writes=[xw_t])
                    ybl, ybl_t = r_ybl.next()
                    if fwd:
                        sc.dma("sp", ybl[:], ybw[tsl, :], owner=ybl_t, reads=[ybw_t], writes=[ybl_t])
                        yf, yf_t = r_yf.next()
                    ts = []
                    for g in range(2):
                        stp, stp_t = r_stp.next()
                        sc.op("pe", lambda e, stp=stp, i=i, g=g, xw=xw: e.matmul(
                            stp[:], lhsT=xtok[:, i, 1024 + g * 128:1024 + (g + 1) * 128], rhs=xw[:, g * 512:(g + 1) * 512],
                            start=True, stop=True), reads=[xtok_t[i], xw_t], writes=[stp_t])
                        yo, yo_t = r_yo.next()
                        sbv, sbv_t = cur[g]
                        sc.op("pe", lambda e, yo=yo, g=g, tsl=tsl, sbv=sbv: e.matmul(yo[:], lhsT=CT[:, g, tsl], rhs=sbv,
                                                                                    start=True, stop=True),
                              reads=[CT_t[g], sbv_t], writes=[yo_t])
                        sc.op("pool", lambda e, g=g, ea=ea: e.tensor_tensor(
                            out=Sf[:, g, :].rearrange("p (h q) -> p h q", q=64), in0=Sf[:, g, :].rearrange("p (h q) -> p h q", q=64),
                            in1=ea[:, 2, g * 8:(g + 1) * 8].unsqueeze(2).to_broadcast([128, 8, 64]), op=ALU.mult),
                            reads=[Sf_t[g], ea_t], writes=[Sf_t[g]])
                        sc.op("dve", lambda e, g=g, stp=stp: e.tensor_tensor(out=Sf[:, g, :], in0=Sf[:, g, :], in1=stp[:],
                                                                            op=ALU.add),
                              reads=[Sf_t[g], stp_t], writes=[Sf_t[g]])
                        nb, nb_t = r_Sb[g].next()
                        sc.op("act", lambda e, nb=nb, g=g: e.copy(out=nb, in_=Sf[:, g, :]), reads=[Sf_t[g]], writes=[nb_t])
                        cur[g] = (nb, nb_t)
                        t, t_t = r_t.next()
                        sc.op("dve", lambda e, t=t, yo=yo, ea=ea, g=g: e.tensor_tensor(
                            out=t[:].rearrange("p (h q) -> p h q", q=64), in0=yo[:].rearrange("p (h q) -> p h q", q=64),
                            in1=ea[:, 0, g * 8:(g + 1) * 8].unsqueeze(2).to_broadcast([128, 8, 64]), op=ALU.mult),
                            reads=[yo_t, ea_t], writes=[t_t])
                        ts.append((t, t_t))
                    cbms = []
                    for g in range(2):
                        cb, cb_t = r_cb.next()
                        sc.op("pe", lambda e, cb=cb, g=g, tsl=tsl: e.matmul(cb[:], lhsT=BT[:, g, tsl], rhs=CT[:, g, tsl],
                                                                           start=True, stop=True),
                              reads=[BT_t[g], CT_t[g]], writes=[cb_t])
                        cbm, cbm_t = r_cbm.next()
                        sc.op("dve", lambda e, cbm=cbm, cb=cb: e.tensor_tensor(out=cbm[:], in0=cb[:], in1=tri_in[:], op=ALU.mult),
                              reads=[cb_t, tri_in_t], writes=[cbm_t])
                        cbms.append((cbm, cbm_t))
                    mts = []
                    for pair in range(2):
                        segs = []
                        for u in (2 * pair, 2 * pair + 1):
                            am, am_t = ams[u]
                            seg, seg_t = r_seg.next()
                            sc.op("pe", lambda e, seg=seg, am=am: e.matmul(seg[:], lhsT=tri_sb[:],
                                                                           rhs=am[:].rearrange("p a b -> p (a b)"),
                                                                           start=True, stop=True),
                                  reads=[tri_sb_t, am_t], writes=[seg_t])
                            segs.append((seg, seg_t))
                        decs = []
                        for (seg, seg_t) in segs:
                            dec, dec_t = r_dec.next()
                            sc.op("act", lambda e, dec=dec, seg=seg: e.activation(out=dec[:].rearrange("p a b -> p (a b)"),
                                                                                 in_=seg[:], func=AF.Exp),
                                  reads=[seg_t], writes=[dec_t])
                            decs.append((dec, dec_t))
                        for k, (dec, dec_t) in enumerate(decs):
                            u = 2 * pair + k
                            cbm, cbm_t = cbms[u // 2]
                            mt, mt_t = r_mt.next()
                            sc.op("dve", lambda e, mt=mt, dec=dec, cbm=cbm: e.tensor_tensor(
                                out=mt[:], in0=dec[:], in1=cbm[:].unsqueeze(1).to_broadcast([128, 4, 128]), op=ALU.mult),
                                reads=[dec_t, cbm_t], writes=[mt_t])
                            mts.append((mt, mt_t))
                    for g in range(2):
                        yd, yd_t = r_yd.next()
                        if fwd:
                            sc.op("pe", lambda e, yd=yd, ybl=ybl, g=g: e.matmul(
                                yd[:], lhsT=ident[:], rhs=ybl[:, g * 512:(g + 1) * 512], start=True, stop=False,
                                skip_group_check=True), reads=[G["ident_t"], ybl_t], writes=[yd_t])
                        for q4 in range(2):
                            mt, mt_t = mts[g * 2 + q4]
                            for hh in range(4):
                                h = g * 8 + q4 * 4 + hh
                                hl = h - g * 8
                                sc.op("pe", lambda e, yd=yd, mt=mt, hh=hh, hl=hl, h=h, xdt=xdt: e.matmul(
                                    yd[:, hl * 64:(hl + 1) * 64], lhsT=mt[:, hh, :], rhs=xdt[:, h * 64:(h + 1) * 64],
                                    start=(not fwd), stop=True, skip_group_check=True),
                                    reads=[mt_t, xdt_t], writes=[yd_t], part=(fwd or not (q4 == 0 and hh == 0)))
                        t, t_t = ts[g]
                        gs = slice(g * 512, (g + 1) * 512)
                        if not fwd:
                            sc.op("dve", lambda e, t=t, yd=yd, ybl=ybl, gs=gs: e.tensor_tensor(out=ybl[:, gs], in0=t[:], in1=yd[:],
                                                                                              op=ALU.add),
                                  reads=[t_t, yd_t], writes=[ybl_t], part=(g > 0))
                        else:
                            sc.op("dve", lambda e, t=t, yd=yd, yf=yf, gs=gs: e.tensor_tensor(out=yf[:, gs], in0=t[:], in1=yd[:],
                                                                                            op=ALU.add),
                                  reads=[t_t, yd_t], writes=[yf_t], part=(g > 0))
                    if not fwd:
                        sc.dma("pool", ybw[tsl, :], ybl[:], owner=ybl_t, reads=[ybl_t], writes=[ybw_t], part=True)
                        return None
                    if i % 4 == 0:
                        z, z_t = r_z.next()
                        sc.dma("sp", z[:], U["z"][i * 128:(i + 4) * 128, :].rearrange("(j p) c -> p j c", p=128), owner=z_t,
                               reads=[G["dram_t"]["z"]], writes=[z_t])
                        sc.op("act", lambda e, z=z: e.activation(out=z[:], in_=z[:], func=AF.Silu), reads=[z_t], writes=[z_t])
                        zcur[0] = (z, z_t)
                    z, z_t = zcur[0]
                    sz = z[:, i % 4, :]
                    sz_t = z_t
                    xd, xd_t = r_xdt.next()
                    sc.op("pool", lambda e, xd=xd, i=i: e.tensor_tensor(
                        out=xd[:].rearrange("p (h q) -> p h q", q=64), in0=xtok[:, i, 0:1024].rearrange("p (h q) -> p h q", q=64),
                        in1=rows[:, 2, 0:16].unsqueeze(2).to_broadcast([128, 16, 64]), op=ALU.mult),
                        reads=[xtok_t[i], rows_t], writes=[xd_t])
                    sc.op("dve", lambda e, yf=yf, xd=xd: e.tensor_tensor(out=yf[:], in0=yf[:], in1=xd[:], op=ALU.add),
                          reads=[yf_t, xd_t], writes=[yf_t])
                    sc.op("dve", lambda e, yf=yf, sz=sz: e.tensor_tensor(out=yf[:], in0=yf[:], in1=sz, op=ALU.mult),
                          reads=[yf_t, sz_t], writes=[yf_t])
                    ss, ss_t = r_ss.next()
                    for g in range(2):
                        gs = slice(g * 512, (g + 1) * 512)
                        jk, jk_t = r_jk.next()
                        sc.op("dve", lambda e, jk=jk, yf=yf, gs=gs, ss=ss, g=g: e.scalar_tensor_tensor(
                            out=jk[:], in0=yf[:, gs], scalar=1.0, in1=yf[:, gs], op0=ALU.mult, op1=ALU.mult,
                            accum_out=ss[:, g:g + 1]), reads=[yf_t], writes=[jk_t, ss_t])
                    sc.op("dve", lambda e, ss=ss: e.tensor_scalar(out=ss[:, 2:4], in0=ss[:, 0:2], scalar1=1.0 / 512.0, scalar2=EPS,
                                                                  op0=ALU.mult, op1=ALU.add), reads=[ss_t], writes=[ss_t])
                    sc.op("pool", lambda e, ss=ss: e.tensor_tensor(out=ss[:, 0:2], in0=ss[:, 2:4], in1=G["neghalf"][:, 0:2],
                                                                   op=ALU.pow), reads=[ss_t, G["neghalf_t"]], writes=[ss_t])
                    y, y_t = r_y.next()
                    for g in range(2):
                        gs = slice(g * 512, (g + 1) * 512)
                        sc.op("dve", lambda e, y=y, yf=yf, gs=gs, ss=ss, g=g: e.scalar_tensor_tensor(
                            out=y[:, gs], in0=yf[:, gs], scalar=ss[:, g:g + 1], in1=nwb[:, gs], op0=ALU.mult, op1=ALU.mult),
                            reads=[yf_t, ss_t, nwb_t], writes=[y_t], part=(g > 0))
                    return (y, y_t, i)

                def tileC(c3):
                    if c3 is None:
                        return
                    (y, y_t, i) = c3
                    emit_yT(P, sc, G, r_tp, y, y_t, yst, yst_t, i, YB["ssd"], G["dram_t"]["yb_ssd"], gsz=2)

                prev3 = None
                for i in order:
                    n3 = tileA(i)
                    tileC(prev3)
                    prev3 = n3
                tileC(prev3)

            ssd_pass(1)
            ssd_pass(0)
            sc.barrier(release=tl2)
        sc.barrier(release=tiles)


def emit_yT(P, sc, G, r_tp, y, y_t, yst, yst_t, i, dst, dst_t, gsz=4):
    ident = G["ident"]
    for half in range(2):
        tp, tp_t = r_tp.next()
        for jq in range(4):
            c = half * 4 + jq
            sc.op("pe", lambda e, tp=tp, jq=jq, c=c: e.transpose(out=tp[:, jq, :], in_=y[:, c * 128:(c + 1) * 128],
                                                               identity=ident[:]),
                  reads=[y_t, G["ident_t"]], writes=[tp_t], part=(jq > 0))
        sc.op("act", lambda e, tp=tp, half=half: e.copy(
            out=yst[:, half * 4:half * 4 + 4, (i % gsz) * 128:(i % gsz + 1) * 128], in_=tp[:]),
            reads=[tp_t], writes=[yst_t], part=not (i % gsz == 0 and half == 0))
    if i % gsz == gsz - 1:
        yv = dst.rearrange("(c p) t -> p c t", p=128)
        sc.dma("pool", yv[:, :, (i - gsz + 1) * 128:(i + 1) * 128], yst[:], owner=yst_t, reads=[yst_t], writes=[dst_t],
               part=True)


NEG = -30000.0


def na_r0(r):
    return min(max(r - 4, 0), 24)


def na_valid(kr, qr):
    return na_r0(qr) <= kr < na_r0(qr) + 8


def phase_D(P, sc, G, U, YB, prm, natt, l):
    nc = P.nc
    with contextlib.ExitStack() as ph:
        qnT = P.sb(ph, "D_qnT", [128, 8, S], BF16)
        knT = P.sb(ph, "D_knT", [128, 8, S], BF16)
        qn_t = sc.tiles_n("D_qn", 8)
        kn_t = sc.tiles_n("D_kn", 8)
        TT = P.sb(ph, "D_TT", [128, 8, 17, 64], BF16)
        TT_t = sc.tiles_n("D_TT", 4)
        wcol = P.sb(ph, "D_wcol", [128, 4], F32)
        wcol_t = sc.tile("D_wcol")
        tiles = qn_t + kn_t + TT_t + [wcol_t]
        with contextlib.ExitStack() as s1:
            TTf = [P.sb(s1, "D_TTf%d" % i, [128, 2, 17, 64], F32) for i in range(2)]
            TTf_t = sc.tiles_n("D_TTf", 2)
            qc_ = [P.sb(s1, "D_qc%d" % i, [128, S], BF16) for i in range(3)]
            qc_t = sc.tiles_n("D_qc", 3)
            sq = [P.sb(s1, "D_sq%d" % i, [128, S], BF16) for i in range(2)]
            sq_t = sc.tiles_n("D_sq", 2)
            lnv = [P.sb(s1, "D_ln%d" % i, [128, S], F32) for i in range(2)]
            lnv_t = sc.tiles_n("D_ln", 2)
            bones = P.sb(s1, "D_bones", [128, 128], BF16)
            bones_t = sc.tile("D_bones")
            ssp = [P.ps(s1, "D_ssp%d" % i, [128, S], F32) for i in range(2)]
            ssp_t = sc.tiles_n("D_ssp", 2)
            t1 = TTf_t + qc_t + sq_t + lnv_t + [bones_t] + ssp_t
            for g in range(4):
                b = g % 2
                sc.dma("sp", TTf[b][:], natt[l][:, 2 * g:2 * g + 2, :, :], owner=TTf_t[b], writes=[TTf_t[b]])
                sc.op("pool", lambda e, b=b, g=g: e.tensor_copy(out=TT[:, 2 * g:2 * g + 2, :, :], in_=TTf[b][:]),
                      reads=[TTf_t[b]], writes=[TT_t[g]])
            for hh in range(2):
                sc.dma("sp", wcol[hh * 64:(hh + 1) * 64, 2:3], prm["na_q_norm_w"][l].rearrange("(d o) -> d o", o=1),
                       owner=wcol_t, writes=[wcol_t], part=True)
                sc.dma("sp", wcol[hh * 64:(hh + 1) * 64, 1:2], prm["na_k_norm_w"][l].rearrange("(d o) -> d o", o=1),
                       owner=wcol_t, writes=[wcol_t], part=True)
            sc.op("dve", lambda e: e.tensor_scalar(out=wcol[:, 0:1], in0=wcol[:, 2:3], scalar1=0.125, scalar2=None,
                                                   op0=ALU.mult), reads=[wcol_t], writes=[wcol_t])
            sc.op("pool", lambda e: e.memset(bones[:], 0.0), writes=[bones_t])
            sc.op("pool", lambda e: e.memset(bones[0:64, 0:64], 1.0), reads=[bones_t], writes=[bones_t])
            sc.op("pool", lambda e: e.memset(bones[64:128, 64:128], 1.0), reads=[bones_t], writes=[bones_t])
            jobs = []
            for which, (src, dstT, dst_t, wc) in enumerate(((U["nq"], qnT, qn_t, 0), (U["nk"], knT, kn_t, 1))):
                src_t = G["dram_t"]["nq" if which == 0 else "nk"]
                for c in range(8):
                    jobs.append((src, src_t, dstT, dst_t, wc, c))

            def n_s1(k):
                (src, src_t, dstT, dst_t, wc, c) = jobs[k]
                cb = k % 2
                q3 = k % 3
                sc.dma("sp", qc_[q3][:], src[c * 128:(c + 1) * 128, :], owner=qc_t[q3], reads=[src_t], writes=[qc_t[q3]])
                sc.op("dve", lambda e, cb=cb, q3=q3: e.tensor_tensor(out=sq[cb][:], in0=qc_[q3][:], in1=qc_[q3][:], op=ALU.mult),
                      reads=[qc_t[q3]], writes=[sq_t[cb]])
                for tb in range(4):
                    sl = slice(tb * 512, (tb + 1) * 512)
                    sc.op("pe", lambda e, cb=cb, sl=sl: e.matmul(ssp[cb][:, sl], lhsT=bones[:], rhs=sq[cb][:, sl], start=True,
                                                                 stop=True),
                          reads=[bones_t, sq_t[cb]], writes=[ssp_t[cb]], part=(tb > 0))
                sc.op("act", lambda e, cb=cb: e.activation(out=lnv[cb][:], in_=ssp[cb][:], func=AF.Ln,
                                                           bias=G["eps"][:, 0:1], scale=1.0 / 64.0),
                      reads=[ssp_t[cb], G["eps_t"]], writes=[lnv_t[cb]])
                sc.op("act", lambda e, cb=cb: e.activation(out=lnv[cb][:], in_=lnv[cb][:], func=AF.Exp, scale=-0.5),
                      reads=[lnv_t[cb]], writes=[lnv_t[cb]])

            def n_s2(k):
                (src, src_t, dstT, dst_t, wc, c) = jobs[k]
                cb = k % 2
                q3 = k % 3
                sc.op("dve", lambda e, cb=cb, q3=q3, dstT=dstT, c=c, wc=wc: e.scalar_tensor_tensor(
                    out=dstT[:, c, :], in0=qc_[q3][:], scalar=wcol[:, wc:wc + 1], in1=lnv[cb][:],
                    op0=ALU.mult, op1=ALU.mult),
                    reads=[qc_t[q3], lnv_t[cb], wcol_t], writes=[dst_t[c]])

            n_s1(0)
            for k in range(len(jobs)):
                if k + 1 < len(jobs):
                    n_s1(k + 1)
                n_s2(k)
            sc.barrier(release=t1)
        with contextlib.ExitStack() as s2:
            vx = P.sb(s2, "D_vx", [128, NT, 16, 65], BF16)
            vx_t = sc.tiles_n("D_vx", NT)
            sps = [P.ps(s2, "D_sps%d" % i, [128, 8, 128], F32) for i in range(2)]
            sps_t = sc.tiles_n("D_sps", 2)
            pT = [P.sb(s2, "D_pT%d" % i, [128, 5, 128], BF16) for i in range(3)]
            pT_t = sc.tiles_n("D_pT", 3)
            po = [P.ps(s2, "D_po%d" % i, [128, 2, 66], F32) for i in range(2)]
            po_t = sc.tiles_n("D_po", 2)
            rc = [P.sb(s2, "D_rc%d" % i, [128, 2], F32) for i in range(2)]
            rc_t = sc.tiles_n("D_rc", 2)
            ot = [P.sb(s2, "D_ot%d" % i, [128, 1024], BF16) for i in range(2)]
            ot_t = sc.tiles_n("D_ot", 2)
            tp = [P.ps(s2, "D_tp%d" % i, [128, 4, 128], BF16) for i in range(2)]
            tp_t = sc.tiles_n("D_tp", 2)
            yst = P.sb(s2, "D_yst", [128, 8, 512], BF16)
            yst_t = sc.tile("D_yst")
            t2 = vx_t + sps_t + pT_t + po_t + rc_t + ot_t + tp_t + [yst_t]
            nvv = U["nv"].rearrange("(i p) (h d) -> p i h d", p=128, d=64)
            for i in range(NT):
                sc.op("pool", lambda e, i=i: e.memset(vx[:, i, :, 64:65], 1.0), writes=[vx_t[i]])
                sc.dma("sp", vx[:, i, :, 0:64], nvv[:, i, :, :], owner=vx_t[i], reads=[G["dram_t"]["nv"]],
                       writes=[vx_t[i]], part=True)
            ident = G["ident"]
            tpc = [0]
            units = []
            for i in range(NT):
                jlo = na_r0(2 * i) // 2
                jhi = (na_r0(2 * i + 1) + 7) // 2
                js = list(range(jlo, jhi + 1))
                for hp in range(8):
                    for hh in range(2):
                        units.append((i, hp, hh, js))

            def emit_S(u):
                i, hp, hh, js = units[u]
                h = 2 * hp + hh
                p0 = 64 * hh
                sb_ = u % 2
                for jj, j in enumerate(js):
                    sc.op("pe", lambda e, sb_=sb_, jj=jj, j=j, p0=p0, hp=hp, i=i: e.matmul(
                        sps[sb_][:, jj, :], lhsT=knT[p0:p0 + 64, hp, j * 128:(j + 1) * 128],
                        rhs=qnT[p0:p0 + 64, hp, i * 128:(i + 1) * 128], start=True, stop=False,
                        skip_group_check=True),
                        reads=[kn_t[hp], qn_t[hp]], writes=[sps_t[sb_]], part=(jj > 0))
                    mms = []
                    for b0 in range(2):
                        qr = 2 * i + b0
                        va = [na_valid(2 * j + a, qr) for a in range(2)]
                        dr0 = 2 * j - qr + 7
                        cs = slice(b0 * 64, (b0 + 1) * 64)
                        if va[0] and va[1]:
                            mms.append((slice(0, 128), cs, TT[p0:p0 + 64, hp, dr0:dr0 + 2, :]))
                        elif not va[0] and not va[1]:
                            mms.append((slice(0, 128), cs, TT[p0:p0 + 64, hp, 15:17, :]))
                        else:
                            d0 = dr0 if va[0] else 15
                            d1 = dr0 + 1 if va[1] else 16
                            mms.append((slice(0, 64), cs, TT[p0:p0 + 64, hp, d0, :]))
                            mms.append((slice(64, 128), cs, TT[p0:p0 + 64, hp, d1, :]))
                    for mi, (ps_, cs, lhs) in enumerate(mms):
                        sc.op("pe", lambda e, sb_=sb_, jj=jj, ps_=ps_, cs=cs, lhs=lhs, p0=p0, last=(mi == len(mms) - 1):
                              e.matmul(sps[sb_][ps_, jj, cs], lhsT=lhs, rhs=ident[p0:p0 + 64, p0:p0 + 64],
                                       start=False, stop=last, skip_group_check=True),
                              reads=[TT_t[hp // 2], G["ident_t"]], writes=[sps_t[sb_]], part=True)

            def emit_rest(u):
                i, hp, hh, js = units[u]
                h = 2 * hp + hh
                sb_ = u % 2
                pt = u % 3
                pb_ = (u // 2) % 2
                ob = i % 2
                n = len(js)
                n1 = min(n, 4)
                sc.op("act", lambda e, pt=pt, sb_=sb_, n1=n1: e.activation(out=pT[pt][:, 0:n1, :],
                                                                         in_=sps[sb_][:, 0:n1, :], func=AF.Exp),
                      reads=[sps_t[sb_]], writes=[pT_t[pt]])
                if n > 4:
                    sc.op("act", lambda e, pt=pt, sb_=sb_, n=n: e.activation(out=pT[pt][:, 4:n, :],
                                                                           in_=sps[sb_][:, 4:n, :], func=AF.Exp),
                          reads=[sps_t[sb_]], writes=[pT_t[pt]], part=True)
                for jj, j in enumerate(js):
                    sc.op("pe", lambda e, pb_=pb_, hh=hh, pt=pt, jj=jj, j=j, h=h, n=n: e.matmul(
                        po[pb_][:, hh, 0:65], lhsT=pT[pt][:, jj, :], rhs=vx[:, j, h, :],
                        start=(jj == 0), stop=(jj == n - 1)),
                        reads=[pT_t[pt], vx_t[j]], writes=[po_t[pb_]], part=(hh > 0 or jj > 0))
                if hh == 1:
                    sc.op("dve", lambda e, pb_=pb_: e.reciprocal(out=rc[pb_][:, 0:2], in_=po[pb_][:, :, 64]),
                          reads=[po_t[pb_]], writes=[rc_t[pb_]])
                    for h2 in range(2):
                        hx = 2 * hp + h2
                        sc.op("dve", lambda e, pb_=pb_, h2=h2, hx=hx, ob=ob: e.tensor_scalar(
                            out=ot[ob][:, hx * 64:(hx + 1) * 64], in0=po[pb_][:, h2, 0:64], scalar1=rc[pb_][:, h2:h2 + 1],
                            scalar2=None, op0=ALU.mult),
                            reads=[po_t[pb_], rc_t[pb_]], writes=[ot_t[ob]], part=(hx > 0))
                if hp == 7 and hh == 1:
                    for half in range(2):
                        tb_ = tpc[0] % 2
                        tpc[0] += 1
                        for jq in range(4):
                            c = half * 4 + jq
                            sc.op("pe", lambda e, tb_=tb_, jq=jq, c=c, ob=ob: e.transpose(
                                out=tp[tb_][:, jq, :], in_=ot[ob][:, c * 128:(c + 1) * 128], identity=ident[:]),
                                reads=[ot_t[ob], G["ident_t"]], writes=[tp_t[tb_]], part=(jq > 0))
                        sc.op("act", lambda e, tb_=tb_, half=half, i=i: e.copy(
                            out=yst[:, half * 4:half * 4 + 4, (i % 4) * 128:(i % 4 + 1) * 128], in_=tp[tb_][:]),
                            reads=[tp_t[tb_]], writes=[yst_t], part=not (i % 4 == 0 and half == 0))
                    if i % 4 == 3:
                        yv = YB["na"].rearrange("(c p) t -> p c t", p=128)
                        sc.dma("pool", yv[:, :, (i - 3) * 128:(i + 1) * 128], yst[:], owner=yst_t, reads=[yst_t],
                               writes=[G["dram_t"]["yb_na"]], part=True)

            emit_S(0)
            for u in range(len(units)):
                if u + 1 < len(units):
                    emit_S(u + 1)
                emit_rest(u)
            sc.barrier(release=t2)
        sc.barrier(release=tiles)


def phase_F(P, sc, G, prm, l):
    nc = P.nc
    x = G["x"]
    with contextlib.ExitStack() as ph:
        hT = P.sb(ph, "F_hT", [128, 8, S], BF16)
        hT_t = sc.tiles_n("F_hT", NT)
        tiles = list(hT_t)
        tiles += rms_transpose(P, sc, G, ph, prm["norm_mlp_w"][l], hT, hT_t, l, "F")
        wst = WStream(P, sc, ph, "F", 1, 4096, nf=2, nb=3)
        fT = [P.sb(ph, "F_fT%d" % i, [128, 4, S], BF16) for i in range(2)]
        fT_t = [sc.tiles_n("F_fT%d_" % i, 4) for i in range(2)]
        rl = [P.sb(ph, "F_rl%d" % i, [128, 512], F32) for i in range(2)]
        rl_t = sc.tiles_n("F_rl", 2)
        acc = [P.ps(ph, "F_acc%d" % i, [128, 512], F32) for i in range(4)]
        acc_t = sc.tiles_n("F_acc", 4)
        tiles += wst.tiles + fT_t[0] + fT_t[1] + rl_t + acc_t
        w1v = prm["w_ff1"][l].rearrange("(kc p) n -> p kc n", p=128)
        w2v = prm["w_ff2"][l].rearrange("(c p) n -> p c n", p=128)
        items = []
        for g in range(8):
            items.append((w1v[:, :, g * 512:(g + 1) * 512], 8, 512))
            items.append((w2v[:, g * 4:(g + 1) * 4, :], 4, 1024))
        wst.items = items
        wst_views = {}

        def view(slot, k, n):
            return slot[:, 0, :].rearrange("p (k n) -> p k n", k=k)
        def _load(g):
            if g >= len(items):
                return
            ap, k, n = items[g]
            fs = g % wst.nf
            sc.dma("sp", view(wst.f[fs], k, n), ap, owner=wst.f_t[fs], writes=[wst.f_t[fs]])

        def _cast(g):
            if g >= len(items):
                return
            fs, bs = g % wst.nf, g % wst.nb
            sc.op("pool", lambda e: e.tensor_copy(out=wst.b[bs][:, 0, :], in_=wst.f[fs][:, 0, :]),
                  reads=[wst.f_t[fs]], writes=[wst.b_t[bs]])
        wst._load = _load
        wst._cast = _cast
        _load(0)
        _load(1)
        _cast(0)
        ai = 0
        ri = 0
        for g in range(8):
            fb = g % 2
            w1s, w1_t = wst.get(2 * g)
            w1b = view(w1s, 8, 512)
            for c in range(4):
                for tb in range(4):
                    a = ai % 4
                    ai += 1
                    for kc in range(8):
                        sc.op("pe", lambda e, a=a, kc=kc, w1b=w1b, c=c, tb=tb: e.matmul(
                            acc[a][:], lhsT=w1b[:, kc, c * 128:(c + 1) * 128],
                            rhs=hT[:, kc, tb * 512:(tb + 1) * 512], start=(kc == 0), stop=(kc == 7)),
                            reads=[w1_t] + hT_t[tb * 4:tb * 4 + 4], writes=[acc_t[a]], part=(kc > 0))
                    r = ri % 2
                    ri += 1
                    sc.op("act", lambda e, r=r, a=a: e.activation(out=rl[r][:], in_=acc[a][:], func=AF.Relu),
                          reads=[acc_t[a]], writes=[rl_t[r]])
                    sc.op("pool", lambda e, r=r, fb=fb, c=c, tb=tb: e.tensor_tensor(
                        out=fT[fb][:, c, tb * 512:(tb + 1) * 512], in0=rl[r][:], in1=rl[r][:], op=ALU.mult),
                        reads=[rl_t[r]], writes=[fT_t[fb][c]], part=(tb > 0))
            w2s, w2_t = wst.get(2 * g + 1)
            w2b = view(w2s, 4, 1024)
            for i in range(NT):
                for hh in range(2):
                    a = ai % 4
                    ai += 1
                    for c in range(4):
                        sc.op("pe", lambda e, a=a, c=c, w2b=w2b, i=i, hh=hh, fb=fb: e.matmul(
                            acc[a][:], lhsT=fT[fb][:, c, i * 128:(i + 1) * 128],
                            rhs=w2b[:, c, hh * 512:(hh + 1) * 512], start=(c == 0), stop=(c == 3)),
                            reads=[w2_t, fT_t[fb][c]], writes=[acc_t[a]], part=(c > 0))
                    xs = x[:, i, hh * 512:(hh + 1) * 512]
                    sc.op("dve", lambda e, xs=xs, a=a: e.tensor_tensor(out=xs, in0=xs, in1=acc[a][:], op=ALU.add),
                          reads=[acc_t[a], G["xt"][i]], writes=[G["xt"][i]])
        sc.barrier(release=tiles)


_NC_CACHE = {}


def make_na_tt(rpb):
    rpb = np.asarray(rpb, dtype=np.float32)
    L = rpb.shape[0]
    out = np.full((L, 128, 8, 17, 64), NEG, dtype=np.float32)
    qc = np.arange(64)
    ws = np.clip(qc - 8, 0, 48)
    for q in range(64):
        kc = np.arange(ws[q], ws[q] + 16)
        idx = kc - q + 15
        for hh in range(2):
            out[:, hh * 64 + q, :, 0:15, ws[q]:ws[q] + 16] = rpb[:, hh::2][:, :, :, idx]
    return out


def kernel(**inputs):
    cfg = {}
    key = "full"
    if key not in _NC_CACHE:
        _NC_CACHE[key] = build(cfg)
    nc = _NC_CACHE[key]
    x = np.ascontiguousarray(inputs["x"], dtype=np.float32)
    base = {n: np.ascontiguousarray(inputs[n], dtype=np.float32) for n in PARAM_NAMES}
    base["na_tt"] = make_na_tt(inputs["na_rpb"])
    in_maps = []
    for c in range(8):
        m = dict(base)
        m["x"] = x[c]
        in_maps.append(m)
    res = run_bass_kernel_spmd(nc, in_maps, core_ids=list(range(8)))
    return np.stack([r["y"] for r in res.results], axis=0).astype(np.float32)
```

```python
import contextlib
import numpy as np
import concourse.bass as bass
import concourse.mybir as mybir
from concourse.bass_utils import run_bass_kernel_spmd

F32 = mybir.dt.float32
BF16 = mybir.dt.bfloat16
ALU = mybir.AluOpType
AF = mybir.ActivationFunctionType
AX = mybir.AxisListType

D = 1024
S = 2048
NT = S // 128
DEPTH = 2
N_IN = 11840
EPS = 1e-6


class TT:
    __slots__ = ("name", "lw", "rd", "dsems", "gen")

    def __init__(self, name):
        self.name = name
        self.lw = {}
        self.rd = {}
        self.gen = {}
        self.dsems = {}


class Sched:
    ENG = ("pe", "act", "dve", "pool", "sp")
    BLK = {"pe": "tensor", "act": "scalar", "dve": "vector", "pool": "gpsimd", "sp": "sync"}

    def __init__(self, nc, stack):
        self.nc = nc
        self.stack = stack
        self.ops = {e: [] for e in self.ENG}
        self.seen = {e: {} for e in self.ENG}
        self.esem = {e: stack.enter_context(nc.semaphore("es_" + e)) for e in self.ENG if e != "sp"}
        self.tiles = []
        self.free_dsems = {"sp": [], "pool": [], "act": []}
        self.nsem = 4
        self.skip_same = {"pe"}

    def tile(self, name):
        t = TT(name)
        self.tiles.append(t)
        return t

    def tiles_n(self, name, n):
        return [self.tile("%s%d" % (name, i)) for i in range(n)]

    def _collect(self, reads, writes, part):
        evs = {}

        def add(d):
            for k, v in d.items():
                if k not in evs or evs[k][0] < v[0]:
                    evs[k] = v
        for t in reads:
            add(t.lw)
        for t in writes:
            if part and not t.rd:
                add(t.gen)
                continue
            g = dict(t.rd)
            for k, v in t.lw.items():
                if k not in g or g[k][0] < v[0]:
                    g[k] = v
            t.gen = g
            add(g)
        return evs

    def _waits(self, eng, evs):
        waits = []
        for k, (val, obj) in evs.items():
            if k == ("E", eng) and eng in self.skip_same:
                continue
            if self.seen[eng].get(k, 0) >= val:
                continue
            self.seen[eng][k] = val
            waits.append((k, val, obj))
            if k[0] == "E":
                self.ops[k[1]][val - 1]["inc"] = True
        return waits

    def _update(self, ev_key, ev_val, reads, writes, part):
        for t in reads:
            t.rd[ev_key] = ev_val
        for t in writes:
            if part and not t.rd:
                t.lw[ev_key] = ev_val
            else:
                t.lw = {ev_key: ev_val}
                t.rd = {}

    def op(self, eng, fn, reads=(), writes=(), part=False):
        waits = self._waits(eng, self._collect(reads, writes, part))
        self.ops[eng].append({"fn": fn, "waits": waits, "inc": False, "dma": None})
        idx = len(self.ops[eng])
        self._update(("E", eng), (idx, None), reads, writes, part)

    def dma(self, q, out, in_, owner, reads=(), writes=(), part=False, **kw):
        waits = self._waits(q, self._collect(reads, writes, part))
        rec = owner.dsems.get(q)
        if rec is None:
            if self.free_dsems[q]:
                rec = self.free_dsems[q].pop()
            else:
                rec = [self.stack.enter_context(self.nc.semaphore("ds%d" % self.nsem)), 0, self.nsem]
                self.nsem += 1
            owner.dsems[q] = rec
        rec[1] += 16
        self.ops[q].append({"fn": (lambda e: e.dma_start(out=out, in_=in_, **kw)), "waits": waits,
                            "inc": False, "dma": rec[0]})
        self._update(("D", rec[2]), (rec[1], rec[0]), reads, writes, part)

    def barrier(self, release=()):
        evs = {}
        for e in self.ENG:
            if e == "sp":
                continue
            idx = len(self.ops[e])
            while idx > 0 and (self.ops[e][idx - 1]["dma"] is not None or self.ops[e][idx - 1].get("nop")):
                idx -= 1
            if idx > 0:
                evs[("E", e)] = (idx, None)
        for t in self.tiles:
            for d in (t.lw, t.rd):
                for k, v in d.items():
                    if k[0] == "D" and (k not in evs or evs[k][0] < v[0]):
                        evs[k] = v
        for e in self.ENG:
            sk = self.skip_same
            self.skip_same = set()
            w = self._waits(e, dict(evs))
            self.skip_same = sk
            self.ops[e].append({"fn": (lambda en: en.nop()), "waits": w, "inc": False, "dma": None, "nop": True})
        for t in self.tiles:
            t.lw = {}
            t.rd = {}
            t.gen = {}
        rel = set(id(t) for t in release)
        for t in release:
            for q, rec in t.dsems.items():
                self.free_dsems[q].append(rec)
            t.dsems = {}
        self.tiles = [t for t in self.tiles if id(t) not in rel]

    def emit(self):
        nc = self.nc
        mile = {}
        for e in self.ENG:
            c = 0
            m = []
            for o in self.ops[e]:
                if o["inc"]:
                    c += 1
                m.append(c)
            mile[e] = m
            assert c < 60000, (e, c)
        with nc.Block() as block:
            for e in self.ENG:
                def body(engine, e=e):
                    for o in self.ops[e]:
                        for (k, val, obj) in o["waits"]:
                            if k[0] == "E":
                                engine.wait_ge(self.esem[k[1]], mile[k[1]][val - 1])
                            else:
                                engine.wait_ge(obj, val)
                        ins = o["fn"](engine)
                        if o["dma"] is not None:
                            ins.then_inc(o["dma"], 16)
                        elif o["inc"]:
                            ins.then_inc(self.esem[e], 1)
                getattr(block, self.BLK[e])(body)


class Prog:
    def __init__(self, cfg):
        self.cfg = cfg
        self.nc = bass.Bass("TRN2", target_bir_lowering=False)
        self.dbg = cfg.get("debug", ())

    def dram(self, name, shape, dt, kind="Internal"):
        if name in self.dbg:
            kind = "ExternalOutput"
        if name in self.cfg.get("ext_in", ()):
            kind = "ExternalInput"
        return self.nc.dram_tensor(name, list(shape), dt, kind=kind).ap()

    def sb(self, stack, name, shape, dt):
        self.uid = getattr(self, "uid", 0) + 1
        return stack.enter_context(self.nc.sbuf_tensor("%s_u%d" % (name, self.uid), list(shape), dt))

    def ps(self, stack, name, shape, dt):
        self.uid = getattr(self, "uid", 0) + 1
        return stack.enter_context(self.nc.psum_tensor("%s_u%d" % (name, self.uid), list(shape), dt))


IN_SIZES = (1024, 1536, 16, 16, 512, 512, 1024, 1024, 16, 16, 1024, 1024, 1024, 3072)
IN_OFF = [0]
for _s in IN_SIZES:
    IN_OFF.append(IN_OFF[-1] + _s)
(O_Z, O_XBC, O_DTF, O_DTB, O_GQ, O_GK, O_GV, O_GG, O_GAF, O_GAB, O_NQ, O_NK, O_NV, O_GATE, _) = IN_OFF

PARAM_NAMES = ["norm_mix_w", "w_in", "ssd_conv_w", "ssd_conv_b", "ssd_dt_bias_f", "ssd_dt_bias_b",
               "ssd_a_log_f", "ssd_a_log_b", "ssd_d", "ssd_norm_w", "gla_a2_f", "gla_a2_bias_f",
               "gla_a2_b", "gla_a2_bias_b", "gla_norm_w", "na_q_norm_w", "na_k_norm_w", "na_rpb",
               "w_branch_ssd", "w_branch_gla", "w_branch_na", "w_out", "norm_mlp_w", "w_ff1", "w_ff2"]
PARAM_SHAPES = {
    "norm_mix_w": (2, 1024), "w_in": (2, 1024, 11840), "ssd_conv_w": (2, 5, 1536), "ssd_conv_b": (2, 1536),
    "ssd_dt_bias_f": (2, 16), "ssd_dt_bias_b": (2, 16), "ssd_a_log_f": (2, 16), "ssd_a_log_b": (2, 16),
    "ssd_d": (2, 16), "ssd_norm_w": (2, 1024), "gla_a2_f": (2, 16, 512), "gla_a2_bias_f": (2, 512),
    "gla_a2_b": (2, 16, 512), "gla_a2_bias_b": (2, 512), "gla_norm_w": (2, 256), "na_q_norm_w": (2, 64),
    "na_k_norm_w": (2, 64), "na_rpb": (2, 16, 15, 31), "w_branch_ssd": (2, 1024, 1024),
    "w_branch_gla": (2, 1024, 1024), "w_branch_na": (2, 1024, 1024), "w_out": (2, 1024, 1024),
    "norm_mlp_w": (2, 1024), "w_ff1": (2, 1024, 4096), "w_ff2": (2, 4096, 1024),
}


def build(cfg):
    P = Prog(cfg)
    nc = P.nc
    layers = cfg.get("layers", DEPTH)
    phases = cfg.get("phases", "ABCDEF")
    x_in = nc.dram_tensor("x", [S, D], F32, kind="ExternalInput").ap()
    prm = {n: nc.dram_tensor(n, list(PARAM_SHAPES[n]), F32, kind="ExternalInput").ap() for n in PARAM_NAMES}
    y_out = nc.dram_tensor("y", [S, D], F32, kind="ExternalOutput").ap()
    natt = nc.dram_tensor("na_tt", [DEPTH, 128, 8, 20, 64], F32, kind="ExternalInput").ap()

    U = {}
    for nm, w in (("z", 1024), ("gv", 1024), ("gg", 1024), ("nv", 1024)):
        U[nm] = P.dram("u_" + nm, [S, w], BF16)
    U["dt"] = P.dram("u_dt", [S, 32], F32)
    for nm, w in (("xbc", 1536), ("gq", 512), ("gk", 512), ("nq", 1024), ("nk", 1024), ("gate", 3072)):
        U[nm] = P.dram("u_" + nm + "T", [w, S], BF16)
    U["ga"] = P.dram("u_gaT", [32, S], BF16)
    YB = {nm: P.dram("yb_" + nm, [1024, S], BF16) for nm in ("ssd", "gla", "na")}
    ybw = P.dram("ybw", [S, 1024], BF16)

    with contextlib.ExitStack() as top:
        sc = Sched(nc, top)
        G = {}
        G["x"] = P.sb(top, "x_res", [128, NT, D], F32)
        G["xt"] = sc.tiles_n("x", NT)
        G["ident"] = P.sb(top, "ident", [128, 128], BF16)
        G["ident_t"] = sc.tile("ident")
        G["dram_t"] = {k: sc.tile("d_" + k) for k in list(U) + ["yb_ssd", "yb_gla", "yb_na", "ybw"]}
        G["ybw"] = ybw

        ones_f = P.sb(top, "ones_f", [128, 128], F32)
        ones_t = sc.tile("ones_f")
        sc.op("pool", lambda e: e.memset(ones_f[:], 1.0), writes=[ones_t])
        sc.op("pool", lambda e: e.affine_select(out=G["ident"][:], in_=ones_f[:], pattern=[[-1, 128]],
                                                compare_op=ALU.is_equal, fill=0.0, base=0,
                                                channel_multiplier=1),
              reads=[ones_t], writes=[G["ident_t"]])
        G["ones_f"] = ones_f
        G["eps"] = P.sb(top, "epsc", [128, 2], F32)
        G["eps_t"] = sc.tile("epsc")
        sc.op("pool", lambda e: e.memset(G["eps"][:], EPS), writes=[G["eps_t"]])
        G["one"] = P.sb(top, "onec", [128, 2], F32)
        G["one_t"] = sc.tile("onec")
        sc.op("pool", lambda e: e.memset(G["one"][:], 1.0), writes=[G["one_t"]])
        G["neghalf"] = P.sb(top, "neghalf", [128, 16], F32)
        G["neghalf_t"] = sc.tile("neghalf")
        sc.op("pool", lambda e: e.memset(G["neghalf"][:], -0.5), writes=[G["neghalf_t"]])
        G["ones_t"] = ones_t

        build_tri(P, sc, G, top)
        xv = x_in.rearrange("(i p) d -> p i d", p=128)
        for i in range(NT):
            sc.dma("sp", G["x"][:, i, :], xv[:, i, :], owner=G["xt"][i], writes=[G["xt"][i]])

        for l in range(layers):
            if "A" in phases:
                phase_A(P, sc, G, U, prm, l)
            if "B" in phases:
                phase_B(P, sc, G, U, YB, prm, l)
            if "C" in phases:
                phase_C(P, sc, G, U, YB, prm, l)
            if "D" in phases:
                phase_D(P, sc, G, U, YB, prm, natt, l)
            if "E" in phases:
                phase_E(P, sc, G, U, YB, prm, l)
            if "F" in phases:
                phase_F(P, sc, G, prm, l)

        yv = y_out.rearrange("(i p) d -> p i d", p=128)
        outt = sc.tile("yout")
        for i in range(NT):
            sc.dma("sp", yv[:, i, :], G["x"][:, i, :], owner=G["xt"][i], reads=[G["xt"][i]], writes=[outt],
                   part=True)
        sc.op("sp", lambda e: e.nop(), reads=[outt])
        sc.barrier()
        sc.emit()
    return nc


def rms_transpose(P, sc, G, ph, wrow_ap, hT, hT_t, l, tag):
    nc = P.nc
    wb = P.sb(ph, tag + "_wb", [128, D], F32)
    wb_t = sc.tile(tag + "_wb")
    sc.dma("sp", wb[:], wrow_ap.partition_broadcast(128), owner=wb_t, writes=[wb_t])
    junk = [P.sb(ph, tag + "_junk%d" % i, [128, D], BF16) for i in range(2)]
    junk_t = sc.tiles_n(tag + "_junk", 2)
    hb = [P.sb(ph, tag + "_hb%d" % i, [128, D], BF16) for i in range(2)]
    hb_t = sc.tiles_n(tag + "_hb", 2)
    ss = [P.sb(ph, tag + "_ss%d" % i, [128, 2], F32) for i in range(2)]
    ss_t = sc.tiles_n(tag + "_ss", 2)
    tp = [P.ps(ph, tag + "_tp%d" % i, [128, 4, 128], BF16) for i in range(2)]
    tp_t = sc.tiles_n(tag + "_tp", 2)
    x = G["x"]
    new_tiles = [wb_t] + junk_t + hb_t + ss_t + tp_t
    def _s1(i):
        b = i % 2
        xt = G["xt"][i]
        sc.op("dve", lambda e, i=i, b=b: e.scalar_tensor_tensor(out=junk[b][:], in0=x[:, i, :], scalar=1.0,
                                                                in1=x[:, i, :], op0=ALU.mult, op1=ALU.mult,
                                                                accum_out=ss[b][:, 0:1]),
              reads=[xt], writes=[junk_t[b], ss_t[b]])
        sc.op("dve", lambda e, b=b: e.tensor_scalar(out=ss[b][:, 1:2], in0=ss[b][:, 0:1], scalar1=1.0 / D,
                                                    scalar2=EPS, op0=ALU.mult, op1=ALU.add),
              reads=[ss_t[b]], writes=[ss_t[b]])
        sc.op("pool", lambda e, b=b: e.tensor_tensor(out=ss[b][:, 0:1], in0=ss[b][:, 1:2],
                                                     in1=G["neghalf"][:, 0:1], op=ALU.pow),
              reads=[ss_t[b], G["neghalf_t"]], writes=[ss_t[b]])

    def _s2(i):
        b = i % 2
        xt = G["xt"][i]
        sc.op("dve", lambda e, i=i, b=b: e.scalar_tensor_tensor(out=hb[b][:], in0=x[:, i, :],
                                                                scalar=ss[b][:, 0:1], in1=wb[:],
                                                                op0=ALU.mult, op1=ALU.mult),
              reads=[xt, ss_t[b], wb_t], writes=[hb_t[b]])
        for half in range(2):
            pb = (2 * i + half) % 2
            for j in range(4):
                kc = half * 4 + j
                sc.op("pe", lambda e, b=b, pb=pb, j=j, kc=kc: e.transpose(
                    out=tp[pb][:, j, :], in_=hb[b][:, kc * 128:(kc + 1) * 128], identity=G["ident"][:]),
                    reads=[hb_t[b], G["ident_t"]], writes=[tp_t[pb]], part=(j > 0))
            eng = "act"
            if eng == "act":
                sc.op("act", lambda e, pb=pb, half=half, i=i: e.copy(
                    out=hT[:, half * 4:half * 4 + 4, i * 128:(i + 1) * 128], in_=tp[pb][:]),
                    reads=[tp_t[pb]], writes=[hT_t[i]], part=True)
            else:
                sc.op("dve", lambda e, pb=pb, half=half, i=i: e.tensor_copy(
                    out=hT[:, half * 4:half * 4 + 4, i * 128:(i + 1) * 128], in_=tp[pb][:]),
                    reads=[tp_t[pb]], writes=[hT_t[i]], part=True)

    _s1(0)
    for i in range(NT):
        if i + 1 < NT:
            _s1(i + 1)
        _s2(i)
    return new_tiles


class WStream:
    def __init__(self, P, sc, ph, tag, kdim, ncol, nf=2, nb=3, cast_eng="pool"):
        self.sc = sc
        self.cast_eng = cast_eng
        self.kdim, self.ncol = kdim, ncol
        self.nf, self.nb = nf, nb
        self.f = [P.sb(ph, "%s_wf%d" % (tag, i), [128, kdim, ncol], F32) for i in range(nf)]
        self.f_t = sc.tiles_n(tag + "_wf", nf)
        self.b = [P.sb(ph, "%s_wb%d" % (tag, i), [128, kdim, ncol], BF16) for i in range(nb)]
        self.b_t = sc.tiles_n(tag + "_wbt", nb)
        self.tiles = self.f_t + self.b_t
        self.items = []

    def start(self, items):
        self.items = items
        self._load(0)
        self._load(1)
        self._cast(0)

    def _load(self, g):
        if g >= len(self.items):
            return
        ap, k, n = self.items[g]
        fs = g % self.nf
        self.sc.dma("sp", self.f[fs][:, 0:k, 0:n], ap, owner=self.f_t[fs], writes=[self.f_t[fs]])

    def _cast(self, g):
        if g >= len(self.items):
            return
        ap, k, n = self.items[g]
        fs, bs = g % self.nf, g % self.nb
        if self.cast_eng == "act":
            self.sc.op("act", lambda e: e.copy(out=self.b[bs][:, 0:k, 0:n], in_=self.f[fs][:, 0:k, 0:n]),
                       reads=[self.f_t[fs]], writes=[self.b_t[bs]])
        else:
            self.sc.op("pool", lambda e: e.tensor_copy(out=self.b[bs][:, 0:k, 0:n], in_=self.f[fs][:, 0:k, 0:n]),
                       reads=[self.f_t[fs]], writes=[self.b_t[bs]])

    def get(self, g):
        self._cast(g + 1)
        self._load(g + 2)
        return self.b[g % self.nb], self.b_t[g % self.nb]


def proj_groups():
    g = []

    def seg(off, n, mode, key):
        c = 0
        while c < n:
            w = min(512, n - c)
            g.append((off + c, w, mode, key, c))
            c += w
    seg(O_Z, 1024, "tok", "z")
    seg(O_XBC, 1536, "feat", "xbc")
    g.append((O_DTF, 32, "tok32", "dt", 0))
    seg(O_GQ, 512, "feat", "gq")
    seg(O_GK, 512, "feat", "gk")
    seg(O_GV, 1024, "tok", "gv")
    seg(O_GG, 1024, "tok", "gg")
    g.append((O_GAF, 32, "feat32", "ga", 0))
    seg(O_NQ, 1024, "feat", "nq")
    seg(O_NK, 1024, "feat", "nk")
    seg(O_NV, 1024, "tok", "nv")
    seg(O_GATE, 3072, "feat", "gate")
    return g


def phase_A(P, sc, G, U, prm, l):
    nc = P.nc
    with contextlib.ExitStack() as ph:
        hT = P.sb(ph, "A_hT", [128, 8, S], BF16)
        hT_t = sc.tiles_n("A_hT", NT)
        tiles = list(hT_t)
        tiles += rms_transpose(P, sc, G, ph, prm["norm_mix_w"][l], hT, hT_t, l, "A")
        wst = WStream(P, sc, ph, "A", 8, 512)
        acc = [P.ps(ph, "A_acc%d" % i, [128, 512], F32) for i in range(4)]
        acc_t = sc.tiles_n("A_acc", 4)
        NS = 3
        stg = [P.sb(ph, "A_stg%d" % i, [128, 2048], BF16) for i in range(NS)]
        stg_t = sc.tiles_n("A_stg", NS)
        stf = [P.sb(ph, "A_stf%d" % i, [128, 4, 32], F32) for i in range(2)]
        stf_t = sc.tiles_n("A_stf", 2)
        G["ga_stage"] = P.sb(ph, "A_gast", [32, 2048], BF16)
        G["ga_stage_t"] = sc.tile("A_gast")
        tiles += wst.tiles + acc_t + stg_t + stf_t + [G["ga_stage_t"]]
        wv = prm["w_in"][l].rearrange("(kc p) n -> p kc n", p=128)
        groups = proj_groups()
        wst.start([(wv[:, :, c0:c0 + n], 8, n) for (c0, n, _m, _k, _d) in groups])
        ai = 0
        si = 0
        ev = 0
        for gi, (c0, n, mode, key, doff) in enumerate(groups):
            wcur, wcur_t = wst.get(gi)
            dst = U[key]
            dst_t = G["dram_t"][key]
            if mode in ("tok", "tok32"):
                for tb in range(4):
                    if mode == "tok":
                        st = si % NS
                        si += 1
                    else:
                        st = tb % 2
                    for j in range(4):
                        i = tb * 4 + j
                        a = ai % 4
                        ai += 1
                        for kc in range(8):
                            sc.op("pe", lambda e, a=a, kc=kc, i=i, wcur=wcur, n=n: e.matmul(
                                acc[a][:, 0:n], lhsT=hT[:, kc, i * 128:(i + 1) * 128], rhs=wcur[:, kc, 0:n],
                                start=(kc == 0), stop=(kc == 7)),
                                reads=[hT_t[i], wcur_t], writes=[acc_t[a]], part=(kc > 0))
                        if mode == "tok":
                            o_ap = stg[st][:, j * 512:j * 512 + n]
                            o_t = stg_t[st]
                        else:
                            o_ap = stf[st][:, j, 0:n]
                            o_t = stf_t[st]
                        ev += 1
                        if ev % 2 == 0:
                            sc.op("act", lambda e, o_ap=o_ap, a=a, n=n: e.copy(out=o_ap, in_=acc[a][:, 0:n]),
                                  reads=[acc_t[a]], writes=[o_t], part=(j > 0))
                        else:
                            sc.op("dve", lambda e, o_ap=o_ap, a=a, n=n: e.tensor_copy(out=o_ap, in_=acc[a][:, 0:n]),
                                  reads=[acc_t[a]], writes=[o_t], part=(j > 0))
                    rows = dst[tb * 512:(tb + 1) * 512, doff:doff + n].rearrange("(j p) c -> p j c", p=128)
                    if mode == "tok":
                        src = stg[st][:].rearrange("p (j c) -> p j c", j=4)[:, :, 0:n]
                        sc.dma("pool", rows, src, owner=stg_t[st], reads=[stg_t[st]], writes=[dst_t], part=True)
                    else:
                        sc.dma("pool", rows, stf[st][:, :, 0:n], owner=stf_t[st], reads=[stf_t[st]], writes=[dst_t],
                               part=True)
            else:
                nchunk = (n + 127) // 128
                for c in range(nchunk):
                    m = min(128, n - c * 128)
                    if mode == "feat":
                        st = si % NS
                        si += 1
                    else:
                        st = 0
                    for tb in range(4):
                        a = ai % 4
                        ai += 1
                        for kc in range(8):
                            sc.op("pe", lambda e, a=a, kc=kc, tb=tb, wcur=wcur, c=c, m=m: e.matmul(
                                acc[a][0:m, :], lhsT=wcur[:, kc, c * 128:c * 128 + m],
                                rhs=hT[:, kc, tb * 512:(tb + 1) * 512], start=(kc == 0), stop=(kc == 7)),
                                reads=hT_t[tb * 4:tb * 4 + 4] + [wcur_t], writes=[acc_t[a]], part=(kc > 0))
                        ev += 1
                        if mode == "feat":
                            o_ap = stg[st][0:m, tb * 512:(tb + 1) * 512]
                            o_t = stg_t[st]
                            if ev % 2 == 0:
                                sc.op("act", lambda e, o_ap=o_ap, a=a, m=m: e.copy(out=o_ap, in_=acc[a][0:m, :]),
                                      reads=[acc_t[a]], writes=[o_t], part=(tb > 0))
                            else:
                                sc.op("dve", lambda e, o_ap=o_ap, a=a, m=m: e.tensor_copy(out=o_ap, in_=acc[a][0:m, :]),
                                      reads=[acc_t[a]], writes=[o_t], part=(tb > 0))
                        else:
                            sc.op("dve", lambda e, a=a, m=m, tb=tb, gast=G["ga_stage"]: e.tensor_copy(
                                out=gast[0:m, tb * 512:(tb + 1) * 512], in_=acc[a][0:m, :]),
                                reads=[acc_t[a]], writes=[G["ga_stage_t"]], part=(tb > 0))
                    if mode == "feat":
                        sc.dma("pool", dst[doff + c * 128:doff + c * 128 + m, :], stg[st][0:m, :], owner=stg_t[st],
                               reads=[stg_t[st]], writes=[dst_t], part=True)
                    else:
                        sc.dma("pool", dst[0:m, :], G["ga_stage"][0:m, :], owner=G["ga_stage_t"],
                               reads=[G["ga_stage_t"]], writes=[dst_t], part=True)
        sc.barrier(release=tiles)


def phase_E(P, sc, G, U, YB, prm, l):
    nc = P.nc
    x = G["x"]
    with contextlib.ExitStack() as ph:
        wst = WStream(P, sc, ph, "E", 8, 256, nf=2, nb=2, cast_eng="act")
        mix = P.sb(ph, "E_mix", [128, 8, 1024], F32)
        mix_t = sc.tiles_n("E_mix", 8)
        mixb = P.sb(ph, "E_mixb", [128, 8, 1024], BF16)
        mixb_t = sc.tile("E_mixb")
        ybT = [P.sb(ph, "E_yb%d" % i, [128, 8, 1024], BF16) for i in range(2)]
        ybT_t = sc.tiles_n("E_yb", 2)
        gsl = [P.sb(ph, "E_g%d" % i, [128, 1024], BF16) for i in range(3)]
        gsl_t = sc.tiles_n("E_g", 3)
        sig = [P.sb(ph, "E_sig%d" % i, [128, 1024], F32) for i in range(2)]
        sig_t = sc.tiles_n("E_sig", 2)
        tmp = [P.sb(ph, "E_tmp%d" % i, [128, 512], F32) for i in range(2)]
        tmp_t = sc.tiles_n("E_tmp", 2)
        acc = [P.ps(ph, "E_acc%d" % i, [128, 512], F32) for i in range(4)]
        acc_t = sc.tiles_n("E_acc", 4)
        tiles = wst.tiles + mix_t + [mixb_t] + ybT_t + gsl_t + sig_t + tmp_t + acc_t
        wnames = ["w_branch_ssd", "w_branch_gla", "w_branch_na", "w_out"]
        bnames = ["ssd", "gla", "na"]
        items = []
        for half in range(2):
            for wn in wnames:
                wv = prm[wn][l].rearrange("(kc p) n -> p kc n", p=128)
                for cg in range(4):
                    items.append((wv[:, :, cg * 256:(cg + 1) * 256], 8, 256))
        wst.start(items)
        gi = 0
        ai = 0
        gcount = 0
        tcount = 0
        ybcount = 0
        for half in range(2):
            t0 = half * 1024
            for b in range(3):
                ys = ybcount % 2
                ybcount += 1
                ybv = YB[bnames[b]].rearrange("(kc p) t -> p kc t", p=128)
                sc.dma("sp", ybT[ys][:], ybv[:, :, t0:t0 + 1024], owner=ybT_t[ys],
                       reads=[G["dram_t"]["yb_" + bnames[b]]], writes=[ybT_t[ys]])
                for cg in range(4):
                    wcur, wcur_t = wst.get(gi)
                    gi += 1
                    for ecl in range(2):
                        ec = cg * 2 + ecl
                        gs = gcount % 3
                        ss_ = gcount % 2
                        gcount += 1
                        grow = b * 1024 + ec * 128
                        sc.dma("sp", gsl[gs][:], U["gate"][grow:grow + 128, t0:t0 + 1024], owner=gsl_t[gs],
                               reads=[G["dram_t"]["gate"]], writes=[gsl_t[gs]])
                        sc.op("act", lambda e, gs=gs, ss_=ss_: e.activation(out=sig[ss_][:], in_=gsl[gs][:],
                                                                            func=AF.Sigmoid),
                              reads=[gsl_t[gs]], writes=[sig_t[ss_]])
                        for tbh in range(2):
                            a = ai % 4
                            ai += 1
                            for kc in range(8):
                                sc.op("pe", lambda e, a=a, kc=kc, wcur=wcur, ecl=ecl, ys=ys, tbh=tbh: e.matmul(
                                    acc[a][:], lhsT=wcur[:, kc, ecl * 128:(ecl + 1) * 128],
                                    rhs=ybT[ys][:, kc, tbh * 512:(tbh + 1) * 512], start=(kc == 0), stop=(kc == 7)),
                                    reads=[wcur_t, ybT_t[ys]], writes=[acc_t[a]], part=(kc > 0))
                            msl = mix[:, ec, tbh * 512:(tbh + 1) * 512]
                            sgl = sig[ss_][:, tbh * 512:(tbh + 1) * 512]
                            if b == 0:
                                sc.op("dve", lambda e, msl=msl, a=a, sgl=sgl: e.tensor_tensor(
                                    out=msl, in0=acc[a][:], in1=sgl, op=ALU.mult),
                                    reads=[acc_t[a], sig_t[ss_]], writes=[mix_t[ec]], part=(tbh > 0))
                            else:
                                ts = tcount % 2
                                tcount += 1
                                sc.op("dve", lambda e, ts=ts, a=a, sgl=sgl: e.tensor_tensor(
                                    out=tmp[ts][:], in0=acc[a][:], in1=sgl, op=ALU.mult),
                                    reads=[acc_t[a], sig_t[ss_]], writes=[tmp_t[ts]])
                                if b == 1:
                                    sc.op("pool", lambda e, msl=msl, ts=ts: e.tensor_tensor(
                                        out=msl, in0=msl, in1=tmp[ts][:], op=ALU.add),
                                        reads=[tmp_t[ts], mix_t[ec]], writes=[mix_t[ec]])
                                else:
                                    sc.op("pool", lambda e, msl=msl, ts=ts, ec=ec, tbh=tbh: e.tensor_tensor(
                                        out=mixb[:, ec, tbh * 512:(tbh + 1) * 512], in0=msl, in1=tmp[ts][:],
                                        op=ALU.add),
                                        reads=[tmp_t[ts], mix_t[ec]], writes=[mixb_t], part=True)
            for cg in range(4):
                wcur, wcur_t = wst.get(gi)
                gi += 1
                for j in range(8):
                    i = half * 8 + j
                    a = ai % 4
                    ai += 1
                    for ec in range(8):
                        sc.op("pe", lambda e, a=a, ec=ec, wcur=wcur, j=j: e.matmul(
                            acc[a][:, 0:256], lhsT=mixb[:, ec, j * 128:(j + 1) * 128], rhs=wcur[:, ec, :],
                            start=(ec == 0), stop=(ec == 7)),
                            reads=[wcur_t, mixb_t], writes=[acc_t[a]], part=(ec > 0))
                    xs = x[:, i, cg * 256:(cg + 1) * 256]
                    sc.op("dve", lambda e, xs=xs, a=a: e.tensor_tensor(out=xs, in0=xs, in1=acc[a][:, 0:256], op=ALU.add),
                          reads=[acc_t[a], G["xt"][i]], writes=[G["xt"][i]])
        sc.barrier(release=tiles)


class Ring:
    def __init__(self, P, sc, stack, name, shape, dt, n, psum=False, views=None):
        if views is not None:
            self.h = views
            n = len(views)
        else:
            mk = P.ps if psum else P.sb
            self.h = [mk(stack, "%s%d" % (name, i), shape, dt) for i in range(n)]
        self.t = sc.tiles_n(name + "_", n)
        self.i = 0
        self.n = n

    def next(self):
        k = self.i % self.n
        self.i += 1
        return self.h[k], self.t[k]


def build_tri(P, sc, G, top):
    for nm in ("trif", "trib", "trif64", "trib64", "mcf64", "mcb64", "trifs", "tribs"):
        G[nm] = P.sb(top, nm, [128, 128], F32)
        G[nm + "_t"] = sc.tile(nm)
    ones_f, ones_t = G["ones_f"], G["ones_t"]
    sc.op("pool", lambda e: e.affine_select(out=G["trif"][:], in_=ones_f[:], pattern=[[1, 128]], compare_op=ALU.is_ge,
                                            fill=0.0, base=0, channel_multiplier=-1),
          reads=[ones_t], writes=[G["trif_t"]])
    sc.op("pool", lambda e: e.affine_select(out=G["trib"][:], in_=ones_f[:], pattern=[[-1, 128]], compare_op=ALU.is_ge,
                                            fill=0.0, base=0, channel_multiplier=1),
          reads=[ones_t], writes=[G["trib_t"]])
    sc.op("pool", lambda e: e.affine_select(out=G["trifs"][:], in_=ones_f[:], pattern=[[1, 128]], compare_op=ALU.is_gt,
                                            fill=0.0, base=0, channel_multiplier=-1),
          reads=[ones_t], writes=[G["trifs_t"]])
    sc.op("pool", lambda e: e.affine_select(out=G["tribs"][:], in_=ones_f[:], pattern=[[-1, 128]], compare_op=ALU.is_gt,
                                            fill=0.0, base=0, channel_multiplier=1),
          reads=[ones_t], writes=[G["tribs_t"]])
    sc.op("pool", lambda e: e.tensor_copy(out=G["trif64"][:], in_=G["trif"][:]), reads=[G["trif_t"]], writes=[G["trif64_t"]])
    sc.op("pool", lambda e: e.memset(G["trif64"][0:64, 64:128], 0.0), reads=[G["trif64_t"]], writes=[G["trif64_t"]])
    sc.op("pool", lambda e: e.tensor_copy(out=G["trib64"][:], in_=G["trib"][:]), reads=[G["trib_t"]], writes=[G["trib64_t"]])
    sc.op("pool", lambda e: e.memset(G["trib64"][64:128, 0:64], 0.0), reads=[G["trib64_t"]], writes=[G["trib64_t"]])
    sc.op("pool", lambda e: e.tensor_scalar(out=G["mcf64"][:], in0=G["trif64"][:], scalar1=-1.0 / 16.0, scalar2=None,
                                            op0=ALU.mult), reads=[G["trif64_t"]], writes=[G["mcf64_t"]])
    sc.op("pool", lambda e: e.tensor_scalar(out=G["mcb64"][:], in0=G["trib64"][:], scalar1=-1.0 / 16.0, scalar2=None,
                                            op0=ALU.mult), reads=[G["trib64_t"]], writes=[G["mcb64_t"]])
    for nm in ("mcf64", "mcb64", "trifs", "tribs"):
        G[nm + "b"] = P.sb(top, nm + "b", [128, 128], BF16)
        G[nm + "b_t"] = sc.tile(nm + "b")
        sc.op("pool", lambda e, nm=nm: e.tensor_copy(out=G[nm + "b"][:], in_=G[nm][:]), reads=[G[nm + "_t"]],
              writes=[G[nm + "b_t"]])


def phase_C(P, sc, G, U, YB, prm, l):
    nc = P.nc
    ident = G["ident"]
    with contextlib.ExitStack() as ph:
        qT = P.sb(ph, "C_qT", [128, 4, S], BF16)
        kT = P.sb(ph, "C_kT", [128, 4, S], BF16)
        qT_t = sc.tile("C_qT")
        kT_t = sc.tile("C_kT")
        ob = P.sb(ph, "C_ob", [128, NT, 1024], BF16)
        ob_t = sc.tiles_n("C_ob", NT)
        gaX = P.sb(ph, "C_gaX", [32, S], BF16)
        gaX_t = sc.tile("C_gaX")
        a2X = [P.sb(ph, "C_a2X%d" % d, [32, 512], BF16) for d in range(2)]
        a2X_t = sc.tiles_n("C_a2X", 2)
        a2f = P.sb(ph, "C_a2f", [32, 512], F32)
        a2f_t = sc.tile("C_a2f")
        nwb = P.sb(ph, "C_nwb", [128, 256], F32)
        nwb_t = sc.tile("C_nwb")
        Sf = P.sb(ph, "C_Sf", [128, 4, 256], F32)
        Sf_t = sc.tile("C_Sf")
        yst = P.sb(ph, "C_yst", [128, 8, 256], BF16)
        yst_t = sc.tile("C_yst")
        R = lambda name, shape, dt, n, psum=False: Ring(P, sc, ph, "C_" + name, shape, dt, n, psum)
        r_Sb = R("Sb", [128, 4, 256], BF16, 3)
        r_v = R("v", [128, 1024], BF16, 2)
        r_gg = R("gg", [128, 4, 1024], BF16, 1)
        r_e1 = R("e1", [128, 512], F32, 1)
        r_gn = R("gn", [128, 512], BF16, 1)
        r_eb = R("eb", [128, 4, 128], F32, 1)
        r_enb = R("enb", [128, 4, 128], F32, 1)
        r_ew = R("ew", [128, 4, 128], F32, 1)
        r_ed = R("ed", [128, 4, 2], F32, 3)
        r_qd = R("qd", [128, 4, 128], BF16, 3)
        r_kd = R("kd", [128, 4, 128], BF16, 2)
        r_kw = R("kw", [128, 4, 128], BF16, 1)
        r_kwt = R("kwt", [128, 4, 128], BF16, 3)
        r_am = R("am", [128, 4, 128], BF16, 3)
        r_oa = R("oa", [128, 1024], F32, 1)
        r_sg = R("sg", [128, 4, 1024], BF16, 1)
        r_jk = R("jk", [128, 256], BF16, 1)
        r_ss = R("ss", [128, 8], F32, 2)
        r_y = R("y", [128, 1024], BF16, 2)
        r_gp = R("gp", [128, 512], F32, 1, True)
        r_bT = R("bT", [128, 4, 128], F32, 1, True)
        r_att = R("att", [128, 4, 128], F32, 1, True)
        r_kwp = R("kwp", [128, 4, 128], BF16, 1, True)
        r_st = R("st", [128, 4, 256], F32, 1, True)
        r_o = R("o", [128, 4, 256], F32, 1, True)
        rings = [r_Sb, r_v, r_gg, r_e1, r_gn, r_eb, r_enb, r_ew, r_ed, r_qd, r_kd, r_kw, r_kwt, r_am, r_oa, r_sg,
                 r_jk, r_ss, r_y, r_gp, r_bT, r_att, r_kwp, r_st, r_o]
        tiles = [qT_t, kT_t, nwb_t, yst_t, gaX_t, Sf_t, a2f_t] + ob_t + a2X_t
        for r in rings:
            tiles += r.t
        sc.dma("sp", qT[:], U["gq"].rearrange("(h p) t -> p h t", p=128), owner=qT_t, reads=[G["dram_t"]["gq"]],
               writes=[qT_t])
        sc.dma("sp", kT[:], U["gk"].rearrange("(h p) t -> p h t", p=128), owner=kT_t, reads=[G["dram_t"]["gk"]],
               writes=[kT_t])
        sc.dma("sp", nwb[:], prm["gla_norm_w"][l].partition_broadcast(128), owner=nwb_t, writes=[nwb_t])
        for d in range(2):
            a2 = prm["gla_a2_f" if d == 0 else "gla_a2_b"][l]
            bi = prm["gla_a2_bias_f" if d == 0 else "gla_a2_bias_b"][l]
            sc.dma("sp", a2f[0:16, :], a2, owner=a2f_t, writes=[a2f_t])
            sc.dma("sp", a2f[16:17, :], bi.rearrange("(o n) -> o n", o=1), owner=a2f_t, writes=[a2f_t], part=True)
            sc.op("act", lambda e, d=d: e.copy(out=a2X[d][0:17, :], in_=a2f[0:17, :]), reads=[a2f_t], writes=[a2X_t[d]])

        def gla_pass(d):
            fwd = (d == 0)
            mc, mc_t = (G["mcf64b"], G["mcf64b_t"]) if fwd else (G["mcb64b"], G["mcb64b_t"])
            ma, ma_t = (G["trif64"], G["trif64_t"]) if fwd else (G["trib64"], G["trib64_t"])
            lc0 = 63 if fwd else 0
            sc.op("pool", lambda e: e.memset(gaX[:], 1.0), writes=[gaX_t])
            sc.dma("sp", gaX[0:16, :], U["ga"][16 * d:16 * d + 16, :], owner=gaX_t, reads=[G["dram_t"]["ga"]],
                   writes=[gaX_t])
            sc.op("pool", lambda e: e.memset(Sf[:], 0.0), writes=[Sf_t])
            sb0, sb0_t = r_Sb.next()
            sc.op("pool", lambda e, sb0=sb0: e.memset(sb0[:], 0.0), writes=[sb0_t])
            cur = [(sb0, sb0_t)]
            sgcur = [None]
            order = list(range(NT)) if fwd else list(range(NT - 1, -1, -1))
            chunks = (0, 1) if fwd else (1, 0)

            def stage1(i):
                tsl = slice(i * 128, (i + 1) * 128)
                v, v_t = r_v.next()
                sc.dma("sp", v[:], U["gv"][tsl, :], owner=v_t, reads=[G["dram_t"]["gv"]], writes=[v_t])
                gp, gp_t = r_gp.next()
                sc.op("pe", lambda e, gp=gp, tsl=tsl: e.matmul(gp[:], lhsT=gaX[0:17, tsl], rhs=a2X[d][0:17, :],
                                                               start=True, stop=True),
                      reads=[gaX_t, a2X_t[d]], writes=[gp_t])
                e1, e1_t = r_e1.next()
                sc.op("act", lambda e, e1=e1, gp=gp: e.activation(out=e1[:], in_=gp[:], func=AF.Exp, scale=-1.0),
                      reads=[gp_t], writes=[e1_t])
                gn, gn_t = r_gn.next()
                sc.op("act", lambda e, gn=gn, e1=e1: e.activation(out=gn[:], in_=e1[:], func=AF.Ln, bias=G["one"][:, 0:1]),
                      reads=[e1_t, G["one_t"]], writes=[gn_t])
                bT, bT_t = r_bT.next()
                for h in range(4):
                    sc.op("pe", lambda e, bT=bT, gn=gn, h=h: e.matmul(bT[:, h, :], lhsT=gn[:, h * 128:(h + 1) * 128], rhs=mc[:],
                                                                     start=True, stop=True, skip_group_check=True),
                          reads=[gn_t, mc_t], writes=[bT_t], part=(h > 0))
                bs, bs_t = bT, bT_t
                eb, eb_t = r_eb.next()
                sc.op("act", lambda e, eb=eb, bs=bs: e.activation(out=eb[:], in_=bs[:], func=AF.Exp), reads=[bs_t], writes=[eb_t])
                enb, enb_t = r_enb.next()
                sc.op("act", lambda e, enb=enb, bs=bs: e.activation(out=enb[:], in_=bs[:], func=AF.Exp, scale=-1.0),
                      reads=[bs_t], writes=[enb_t])
                ed, ed_t = r_ed.next()
                sc.op("act", lambda e, ed=ed, bs=bs: e.activation(
                    out=ed[:], in_=bs[:].rearrange("p h (c l) -> p h c l", c=2)[:, :, :, lc0], func=AF.Exp),
                    reads=[bs_t], writes=[ed_t])
                qd, qd_t = r_qd.next()
                sc.op("dve", lambda e, qd=qd, tsl=tsl, eb=eb: e.scalar_tensor_tensor(
                    out=qd[:], in0=qT[:, :, tsl], scalar=128.0 ** -0.5, in1=eb[:], op0=ALU.mult, op1=ALU.mult),
                    reads=[qT_t, eb_t], writes=[qd_t])
                kd, kd_t = r_kd.next()
                sc.op("dve", lambda e, kd=kd, tsl=tsl, enb=enb: e.tensor_tensor(
                    out=kd[:], in0=kT[:, :, tsl], in1=enb[:], op=ALU.mult), reads=[kT_t, enb_t], writes=[kd_t])
                ew, ew_t = r_ew.next()
                sc.op("dve", lambda e, ew=ew, enb=enb, ed=ed: e.tensor_tensor(
                    out=ew[:].rearrange("p h (c l) -> p (h c) l", c=2), in0=enb[:].rearrange("p h (c l) -> p (h c) l", c=2),
                    in1=ed[:].rearrange("p h c -> p (h c)").unsqueeze(2).to_broadcast([128, 8, 64]), op=ALU.mult),
                    reads=[enb_t, ed_t], writes=[ew_t])
                kw, kw_t = r_kw.next()
                sc.op("dve", lambda e, kw=kw, tsl=tsl, ew=ew: e.tensor_tensor(
                    out=kw[:], in0=kT[:, :, tsl], in1=ew[:], op=ALU.mult), reads=[kT_t, ew_t], writes=[kw_t])
                kwp, kwp_t = r_kwp.next()
                for h in range(4):
                    sc.op("pe", lambda e, kwp=kwp, kw=kw, h=h: e.transpose(out=kwp[:, h, :], in_=kw[:, h, :], identity=ident[:]),
                          reads=[kw_t, G["ident_t"]], writes=[kwp_t], part=(h > 0))
                kwt, kwt_t = r_kwt.next()
                sc.op("act", lambda e, kwt=kwt, kwp=kwp: e.copy(out=kwt[:], in_=kwp[:]), reads=[kwp_t], writes=[kwt_t])
                att, att_t = r_att.next()
                for h in range(4):
                    sc.op("pe", lambda e, att=att, kd=kd, qd=qd, h=h: e.matmul(att[:, h, :], lhsT=kd[:, h, :], rhs=qd[:, h, :],
                                                                            start=True, stop=True, skip_group_check=True),
                          reads=[kd_t, qd_t], writes=[att_t], part=(h > 0))
                am, am_t = r_am.next()
                sc.op("dve", lambda e, am=am, att=att: e.tensor_tensor(
                    out=am[:], in0=att[:], in1=ma[:].unsqueeze(1).to_broadcast([128, 4, 128]), op=ALU.mult),
                    reads=[att_t, ma_t], writes=[am_t])
                return (i, tsl, v, v_t, qd, qd_t, kwt, kwt_t, ed, ed_t, am, am_t)

            def stage23(ctx):
                (i, tsl, v, v_t, qd, qd_t, kwt, kwt_t, ed, ed_t, am, am_t) = ctx
                sbs = [cur[0]]
                for ci, c in enumerate(chunks):
                    cs = slice(c * 64, (c + 1) * 64)
                    st, st_t = r_st.next()
                    for h in range(4):
                        sc.op("pe", lambda e, st=st, kwt=kwt, cs=cs, v=v, h=h: e.matmul(
                            st[:, h, :], lhsT=kwt[cs, h, :], rhs=v[cs, h * 256:(h + 1) * 256], start=True, stop=True,
                            skip_group_check=True),
                            reads=[kwt_t, v_t], writes=[st_t], part=(h > 0))
                    for h in range(4):
                        sc.op("dve", lambda e, st=st, h=h, ed=ed, c=c: e.scalar_tensor_tensor(
                            out=Sf[:, h, :], in0=Sf[:, h, :], scalar=ed[:, h, c:c + 1], in1=st[:, h, :], op0=ALU.mult,
                            op1=ALU.add),
                            reads=[st_t, ed_t, Sf_t], writes=[Sf_t])
                    nb, nb_t = r_Sb.next()
                    sc.op("act", lambda e, nb=nb: e.copy(out=nb[:], in_=Sf[:]), reads=[Sf_t], writes=[nb_t])
                    sbs.append((nb, nb_t))
                o, o_t = r_o.next()
                for h in range(4):
                    sc.op("pe", lambda e, o=o, am=am, v=v, h=h: e.matmul(o[:, h, :], lhsT=am[:, h, :],
                                                                       rhs=v[:, h * 256:(h + 1) * 256],
                                                                       start=True, stop=False, skip_group_check=True),
                          reads=[am_t, v_t], writes=[o_t], part=(h > 0))
                    for ci, c in enumerate(chunks):
                        cs = slice(c * 64, (c + 1) * 64)
                        sbv, sbv_t = sbs[ci]
                        sc.op("pe", lambda e, o=o, qd=qd, cs=cs, sbv=sbv, ci=ci, h=h: e.matmul(
                            o[cs, h, :], lhsT=qd[:, h, cs], rhs=sbv[:, h, :], start=False, stop=(ci == 1),
                            skip_group_check=True),
                            reads=[qd_t, sbv_t], writes=[o_t], part=True)
                cur[0] = sbs[2]
                if not fwd:
                    for hb in range(2):
                        sc.op("act", lambda e, o=o, hb=hb, i=i: e.copy(
                            out=ob[:, i, hb * 512:(hb + 1) * 512], in_=o[:, 2 * hb:2 * hb + 2, :].rearrange("p a b -> p (a b)")),
                            reads=[o_t], writes=[ob_t[i]], part=(hb > 0))
                    return
                oa, oa_t = r_oa.next()
                ss, ss_t = r_ss.next()
                for hb in range(2):
                    sc.op("dve", lambda e, oa=oa, o=o, hb=hb, i=i: e.tensor_tensor(
                        out=oa[:, hb * 512:(hb + 1) * 512], in0=o[:, 2 * hb:2 * hb + 2, :].rearrange("p a b -> p (a b)"),
                        in1=ob[:, i, hb * 512:(hb + 1) * 512], op=ALU.add),
                        reads=[o_t, ob_t[i]], writes=[oa_t], part=(hb > 0))
                for h in range(4):
                    hs = slice(h * 256, (h + 1) * 256)
                    jk, jk_t = r_jk.next()
                    sc.op("dve", lambda e, jk=jk, oa=oa, hs=hs, ss=ss, h=h: e.scalar_tensor_tensor(
                        out=jk[:], in0=oa[:, hs], scalar=1.0, in1=oa[:, hs], op0=ALU.mult, op1=ALU.mult,
                        accum_out=ss[:, h:h + 1]), reads=[oa_t], writes=[jk_t, ss_t])
                if i % 4 == 0:
                    gg, gg_t = r_gg.next()
                    sc.dma("sp", gg[:], U["gg"][i * 128:(i + 4) * 128, :].rearrange("(j p) c -> p j c", p=128), owner=gg_t,
                           reads=[G["dram_t"]["gg"]], writes=[gg_t])
                    sg, sg_t = r_sg.next()
                    sc.op("act", lambda e, sg=sg, gg=gg: e.activation(out=sg[:], in_=gg[:], func=AF.Silu),
                          reads=[gg_t], writes=[sg_t])
                    sgcur[0] = (sg, sg_t)
                sg, sg_t = sgcur[0]
                sgn = sg[:, i % 4, :]
                sgn_t = sg_t
                sc.op("pool", lambda e, sgn=sgn: e.tensor_tensor(
                    out=sgn.rearrange("p (h v) -> p h v", h=4), in0=sgn.rearrange("p (h v) -> p h v", h=4),
                    in1=nwb[:].unsqueeze(1).to_broadcast([128, 4, 256]), op=ALU.mult),
                    reads=[sg_t, nwb_t], writes=[sg_t])
                sc.op("dve", lambda e, ss=ss: e.tensor_scalar(out=ss[:, 4:8], in0=ss[:, 0:4], scalar1=1.0 / 256.0, scalar2=EPS,
                                                              op0=ALU.mult, op1=ALU.add), reads=[ss_t], writes=[ss_t])
                sc.op("pool", lambda e, ss=ss: e.tensor_tensor(out=ss[:, 0:4], in0=ss[:, 4:8], in1=G["neghalf"][:, 0:4],
                                                               op=ALU.pow), reads=[ss_t, G["neghalf_t"]], writes=[ss_t])
                sc.op("dve", lambda e, oa=oa, ss=ss: e.tensor_tensor(
                    out=oa[:].rearrange("p (h v) -> p h v", h=4), in0=oa[:].rearrange("p (h v) -> p h v", h=4),
                    in1=ss[:, 0:4].unsqueeze(2).to_broadcast([128, 4, 256]), op=ALU.mult),
                    reads=[oa_t, ss_t], writes=[oa_t])
                y, y_t = r_y.next()
                sc.op("dve", lambda e, y=y, oa=oa, sgn=sgn: e.tensor_tensor(out=y[:], in0=oa[:], in1=sgn, op=ALU.mult),
                      reads=[oa_t, sgn_t], writes=[y_t])
                return (y, y_t, i)

            def stage3(c3):
                if c3 is None:
                    return
                (y, y_t, i) = c3
                emit_yT(P, sc, G, r_kwp, y, y_t, yst, yst_t, i, YB["gla"], G["dram_t"]["yb_gla"], gsz=2)

            prev = None
            prev3 = None
            for i in order:
                ctx = stage1(i)
                if prev is not None:
                    n3 = stage23(prev)
                    stage3(prev3)
                    prev3 = n3
                prev = ctx
            n3 = stage23(prev)
            stage3(prev3)
            stage3(n3)

        gla_pass(1)
        if "dbg_ob" in P.dbg:
            dob = P.dram("dbg_ob", [S, 1024], BF16)
            dt_ = sc.tile("dbg_ob")
            sc.dma("sp", dob.rearrange("(i p) c -> p i c", p=128), ob[:], owner=ob_t[0], reads=ob_t, writes=[dt_])
        gla_pass(0)
        sc.barrier(release=tiles)


def phase_B(P, sc, G, U, YB, prm, l):
    nc = P.nc
    ident = G["ident"]
    ybw = G["ybw"]
    ybw_t = G["dram_t"]["ybw"]
    with contextlib.ExitStack() as ph:
        xtok = P.sb(ph, "B_xtok", [128, NT, 1280], BF16)
        xtok_t = sc.tiles_n("B_xtok", NT)
        BT = P.sb(ph, "B_BT", [128, 2, S], BF16)
        CT = P.sb(ph, "B_CT", [128, 2, S], BF16)
        BT_t = sc.tiles_n("B_BT", 2)
        CT_t = sc.tiles_n("B_CT", 2)
        dtv = P.sb(ph, "B_dtv", [128, NT, 32], F32)
        av = P.sb(ph, "B_av", [128, NT, 32], F32)
        dtv_t = sc.tile("B_dtv")
        av_t = sc.tile("B_av")
        rows = P.sb(ph, "B_rows", [128, 4, 32], F32)
        rows_t = sc.tile("B_rows")
        nwb = P.sb(ph, "B_nwb", [128, 1024], F32)
        nwb_t = sc.tile("B_nwb")
        tiles = xtok_t + BT_t + CT_t + [dtv_t, av_t, rows_t, nwb_t]
        with contextlib.ExitStack() as s1:
            cwr = P.sb(s1, "B_cwr", [72, 128], F32)
            cwr_t = sc.tile("B_cwr")
            cw = P.sb(s1, "B_cw", [128, 72], F32)
            cw_t = sc.tile("B_cw")
            cwp = P.ps(s1, "B_cwp", [128, 72], F32)
            cwp_t = sc.tile("B_cwp")
            identf = P.sb(s1, "B_identf", [128, 128], F32)
            identf_t = sc.tile("B_identf")
            xc = [P.sb(s1, "B_xc%d" % i, [128, S + 4], BF16) for i in range(2)]
            xc_t = sc.tiles_n("B_xc", 2)
            dg = [P.sb(s1, "B_dg%d" % i, [128, 5, 128], BF16) for i in range(2)]
            dg_t = sc.tiles_n("B_dg", 2)
            cacc = [P.ps(s1, "B_cacc%d" % i, [128, 512], F32) for i in range(2)]
            cacc_t = sc.tiles_n("B_cacc", 2)
            xa = [P.sb(s1, "B_xa%d" % i, [128, S], BF16) for i in range(2)]
            xa_t = sc.tiles_n("B_xa", 2)
            tp = [P.ps(s1, "B_tp%d" % i, [128, 4, 128], BF16) for i in range(2)]
            tp_t = sc.tiles_n("B_tp", 2)
            tl1 = [cwr_t, cw_t, cwp_t, identf_t] + dg_t + cacc_t + xc_t + xa_t + tp_t
            sc.op("pool", lambda e: e.affine_select(out=identf[:], in_=G["ones_f"][:], pattern=[[-1, 128]],
                                                    compare_op=ALU.is_equal, fill=0.0, base=0, channel_multiplier=1),
                  reads=[G["ones_t"]], writes=[identf_t])
            sc.dma("sp", cwr[0:60, :], prm["ssd_conv_w"][l].rearrange("k (c p) -> (k c) p", p=128), owner=cwr_t, writes=[cwr_t])
            sc.dma("sp", cwr[60:72, :], prm["ssd_conv_b"][l].rearrange("(c p) -> c p", p=128), owner=cwr_t, writes=[cwr_t],
                   part=True)
            sc.op("pe", lambda e: e.transpose(out=cwp[:], in_=cwr[:], identity=identf[0:72, 0:72]),
                  reads=[cwr_t, identf_t], writes=[cwp_t])
            sc.op("act", lambda e: e.copy(out=cw[:], in_=cwp[:]), reads=[cwp_t], writes=[cw_t])
            for b in range(2):
                sc.op("pool", lambda e, b=b: e.memset(xc[b][:, 0:2], 0.0), writes=[xc_t[b]])
                sc.op("pool", lambda e, b=b: e.memset(xc[b][:, S + 2:S + 4], 0.0), writes=[xc_t[b]], part=True)
            sc.dma("sp", dtv[:], U["dt"].rearrange("(i p) c -> p i c", p=128), owner=dtv_t, reads=[G["dram_t"]["dt"]],
                   writes=[dtv_t])
            for k, nm in enumerate(("ssd_dt_bias_f", "ssd_dt_bias_b")):
                sc.dma("sp", rows[:, 0, 16 * k:16 * k + 16], prm[nm][l].partition_broadcast(128), owner=rows_t,
                       writes=[rows_t], part=True)
            for k, nm in enumerate(("ssd_a_log_f", "ssd_a_log_b")):
                sc.dma("sp", rows[:, 1, 16 * k:16 * k + 16], prm[nm][l].partition_broadcast(128), owner=rows_t,
                       writes=[rows_t], part=True)
            sc.dma("sp", rows[:, 2, 0:16], prm["ssd_d"][l].partition_broadcast(128), owner=rows_t, writes=[rows_t], part=True)
            sc.dma("sp", nwb[:], prm["ssd_norm_w"][l].partition_broadcast(128), owner=nwb_t, writes=[nwb_t])
            sc.op("dve", lambda e: e.tensor_tensor(out=dtv[:], in0=dtv[:], in1=rows[:, 0:1, :].to_broadcast([128, NT, 32]),
                                                   op=ALU.add), reads=[dtv_t, rows_t], writes=[dtv_t])
            sc.op("act", lambda e: e.activation(out=dtv[:], in_=dtv[:], func=AF.Exp), reads=[dtv_t], writes=[dtv_t])
            sc.op("act", lambda e: e.activation(out=dtv[:], in_=dtv[:], func=AF.Ln, bias=G["one"][:, 0:1]),
                  reads=[dtv_t, G["one_t"]], writes=[dtv_t])
            sc.op("act", lambda e: e.activation(out=rows[:, 3, :], in_=rows[:, 1, :], func=AF.Exp), reads=[rows_t],
                  writes=[rows_t])
            sc.op("dve", lambda e: e.scalar_tensor_tensor(out=av[:], in0=dtv[:], scalar=-1.0,
                                                          in1=rows[:, 3:4, :].to_broadcast([128, NT, 32]),
                                                          op0=ALU.mult, op1=ALU.mult),
                  reads=[dtv_t, rows_t], writes=[av_t])
            tpc = 0
            for c in range(12):
                b = c % 2
                sc.dma("sp", xc[b][:, 2:S + 2], U["xbc"][c * 128:(c + 1) * 128, :], owner=xc_t[b],
                       reads=[G["dram_t"]["xbc"]], writes=[xc_t[b]], part=True)
                dgb = c % 2
                for k in range(5):
                    sc.op("dve", lambda e, dgb=dgb, k=k, c=c: e.tensor_scalar(
                        out=dg[dgb][:, k, :], in0=identf[:], scalar1=cw[:, k * 12 + c:k * 12 + c + 1], scalar2=None,
                        op0=ALU.mult), reads=[identf_t, cw_t], writes=[dg_t[dgb]], part=(k > 0))
                if c < 10:
                    xo, xo_t = xa[b], xa_t[b]
                    xsl = lambda tb: xa[b][:, tb * 512:(tb + 1) * 512]
                else:
                    xo_t = CT_t[c - 10]
                    xsl = lambda tb, c=c: CT[:, c - 10, tb * 512:(tb + 1) * 512]
                for tb in range(4):
                    ca, ca_t = cacc[(4 * c + tb) % 2], cacc_t[(4 * c + tb) % 2]
                    for k in range(5):
                        sc.op("pe", lambda e, ca=ca, dgb=dgb, k=k, b=b, tb=tb: e.matmul(
                            ca[:], lhsT=dg[dgb][:, k, :], rhs=xc[b][:, k + tb * 512:k + tb * 512 + 512],
                            start=(k == 0), stop=(k == 4)),
                            reads=[dg_t[dgb], xc_t[b]], writes=[ca_t], part=(k > 0))
                    sc.op("act", lambda e, ca=ca, o_ap=xsl(tb), c=c: e.activation(
                        out=o_ap, in_=ca[:], func=AF.Silu, bias=cw[:, 60 + c:61 + c]),
                        reads=[ca_t, cw_t], writes=[xo_t], part=(tb > 0))
                if c in (8, 9):
                    sc.op("pool", lambda e, b=b, c=c: e.tensor_copy(out=BT[:, c - 8, :], in_=xa[b][:]),
                          reads=[xa_t[b]], writes=[BT_t[c - 8]])
                if c < 10:
                    for i0 in range(0, NT, 4):
                        tb_ = tpc % 2
                        tpc += 1
                        for j in range(4):
                            i = i0 + j
                            sc.op("pe", lambda e, tb_=tb_, j=j, b=b, i=i: e.transpose(
                                out=tp[tb_][:, j, :], in_=xa[b][:, i * 128:(i + 1) * 128], identity=ident[:]),
                                reads=[xa_t[b], G["ident_t"]], writes=[tp_t[tb_]], part=(j > 0))
                        eng = "act" if (tpc % 2) else "pool"
                        if eng == "act":
                            sc.op("act", lambda e, tb_=tb_, i0=i0, c=c: e.copy(
                                out=xtok[:, i0:i0 + 4, c * 128:(c + 1) * 128], in_=tp[tb_][:]),
                                reads=[tp_t[tb_]], writes=xtok_t[i0:i0 + 4], part=True)
                        else:
                            sc.op("dve", lambda e, tb_=tb_, i0=i0, c=c: e.tensor_copy(
                                out=xtok[:, i0:i0 + 4, c * 128:(c + 1) * 128], in_=tp[tb_][:]),
                                reads=[tp_t[tb_]], writes=xtok_t[i0:i0 + 4], part=True)
            sc.barrier(release=tl1)
        with contextlib.ExitStack() as s2:
            R = lambda name, shape, dt, n, psum=False: Ring(P, sc, s2, "B_" + name, shape, dt, n, psum)
            Sf = P.sb(s2, "B_Sf", [128, 2, 512], F32)
            Sf_t = sc.tiles_n("B_Sf", 2)
            Sbx = P.sb(s2, "B_Sb", [128, 2, 2, 512], BF16)
            r_Sb = [Ring(P, sc, s2, "B_Sb%d" % g, None, None, 2, views=[Sbx[:, g, k, :] for k in range(2)]) for g in range(2)]
            r_cb = R("cb", [128, 128], F32, 1, True)
            r_seg = R("seg", [128, 512], F32, 2, True)
            r_sm = R("sm", [128, 3, 16], F32, 1, True)
            r_yd = R("yd", [128, 512], F32, 1, True)
            r_stp = R("stp", [128, 512], F32, 1, True)
            r_yo = R("yo", [128, 512], F32, 1, True)
            r_tp = R("tp2", [128, 4, 128], BF16, 1, True)
            r_cbm = R("cbm", [128, 128], F32, 2)
            r_am = R("am", [128, 4, 128], BF16, 4)
            r_dec = R("dec", [128, 4, 128], F32, 2)
            r_mt = R("mt", [128, 4, 128], BF16, 4)
            r_ea = R("ea", [128, 3, 16], F32, 2)
            r_xdt = R("xdt", [128, 1024], BF16, 3)
            r_xw = R("xw", [128, 1024], BF16, 2)
            r_t = R("t", [128, 512], F32, 2)
            r_ybl = R("ybl", [128, 1024], BF16, 2)
            r_yf = R("yf", [128, 1024], F32, 1)
            r_z = R("z", [128, 4, 1024], BF16, 1)
            r_jk = R("jk", [128, 512], F32, 1)
            r_ss = R("ss", [128, 4], F32, 2)
            r_y = R("y", [128, 1024], BF16, 2)
            yst = P.sb(s2, "B_yst", [128, 8, 256], BF16)
            yst_t = sc.tile("B_yst")
            rings = [r_cb, r_seg, r_sm, r_yd, r_stp, r_yo, r_tp, r_cbm, r_am, r_dec, r_mt, r_ea, r_xdt, r_xw, r_t, r_ybl,
                     r_yf, r_z, r_jk, r_ss, r_y] + r_Sb
            tl2 = Sf_t + [yst_t]
            for r in rings:
                tl2 += r.t

            def ssd_pass(d):
                fwd = (d == 0)
                tri_in, tri_in_t = (G["trif"], G["trif_t"]) if fwd else (G["trib"], G["trib_t"])
                tri_st, tri_st_t = (G["tribs"], G["tribs_t"]) if fwd else (G["trifs"], G["trifs_t"])
                tri_sb, tri_sb_t = (G["tribsb"], G["tribsb_t"]) if fwd else (G["trifsb"], G["trifsb_t"])
                cur = []
                for g in range(2):
                    sc.op("pool", lambda e, g=g: e.memset(Sf[:, g, :], 0.0), writes=[Sf_t[g]])
                    sb0, sb0_t = r_Sb[g].next()
                    sc.op("pool", lambda e, sb0=sb0: e.memset(sb0, 0.0), writes=[sb0_t])
                    cur.append((sb0, sb0_t))
                order = list(range(NT)) if fwd else list(range(NT - 1, -1, -1))
                zcur = [None]

                def tileA(i):
                    tsl = slice(i * 128, (i + 1) * 128)
                    acol = av[:, i, 16 * d:16 * d + 16]
                    ams = []
                    for u in range(4):
                        h0 = u * 4
                        am, am_t = r_am.next()
                        for hh in range(4):
                            sc.op("act", lambda e, am=am, i=i, h0=h0, hh=hh: e.activation(
                                out=am[:, hh, :], in_=tri_in[:], func=AF.Copy,
                                scale=av[:, i, 16 * d + h0 + hh:16 * d + h0 + hh + 1]),
                                reads=[tri_in_t, av_t], writes=[am_t], part=(hh > 0))
                        ams.append((am, am_t))
                    sm, sm_t = r_sm.next()
                    sc.op("pe", lambda e, sm=sm, acol=acol: e.matmul(sm[:, 0, :], lhsT=tri_in[:], rhs=acol, start=True, stop=True),
                          reads=[tri_in_t, av_t], writes=[sm_t])
                    sc.op("pe", lambda e, sm=sm, acol=acol: e.matmul(sm[:, 1, :], lhsT=tri_st[:], rhs=acol, start=True, stop=True),
                          reads=[tri_st_t, av_t], writes=[sm_t], part=True)
                    sc.op("pe", lambda e, sm=sm, acol=acol: e.matmul(sm[:, 2, :], lhsT=G["ones_f"][:], rhs=acol, start=True,
                                                                     stop=True),
                          reads=[G["ones_t"], av_t], writes=[sm_t], part=True)
                    ea, ea_t = r_ea.next()
                    sc.op("act", lambda e, ea=ea, sm=sm: e.activation(out=ea[:], in_=sm[:], func=AF.Exp), reads=[sm_t],
                          writes=[ea_t])
                    xdt, xdt_t = r_xdt.next()
                    sc.op("dve", lambda e, xdt=xdt, i=i: e.tensor_tensor(
                        out=xdt[:].rearrange("p (h q) -> p h q", q=64), in0=xtok[:, i, 0:1024].rearrange("p (h q) -> p h q", q=64),
                        in1=dtv[:, i, 16 * d:16 * d + 16].unsqueeze(2).to_broadcast([128, 16, 64]), op=ALU.mult),
                        reads=[xtok_t[i], dtv_t], writes=[xdt_t])
                    xw, xw_t = r_xw.next()
                    sc.op("dve", lambda e, xw=xw, xdt=xdt, ea=ea: e.tensor_tensor(
                        out=xw[:].rearrange("p (h q) -> p h q", q=64), in0=xdt[:].rearrange("p (h q) -> p h q", q=64),
                        in1=ea[:, 1, :].unsqueeze(2).to_broadcast([128, 16, 64]), op=ALU.mult),
                        reads=[xdt_t, ea_t], writes=[xw_t])
                    ybl, ybl_t = r_ybl.next()
                    if fwd:
                        sc.dma("sp", ybl[:], ybw[tsl, :], owner=ybl_t, reads=[ybw_t], writes=[ybl_t])
                        yf, yf_t = r_yf.next()
                    ts = []
                    for g in range(2):
                        stp, stp_t = r_stp.next()
                        sc.op("pe", lambda e, stp=stp, i=i, g=g, xw=xw: e.matmul(
                            stp[:], lhsT=xtok[:, i, 1024 + g * 128:1024 + (g + 1) * 128], rhs=xw[:, g * 512:(g + 1) * 512],
                            start=True, stop=True), reads=[xtok_t[i], xw_t], writes=[stp_t])
                        yo, yo_t = r_yo.next()
                        sbv, sbv_t = cur[g]
                        sc.op("pe", lambda e, yo=yo, g=g, tsl=tsl, sbv=sbv: e.matmul(yo[:], lhsT=CT[:, g, tsl], rhs=sbv,
                                                                                    start=True, stop=True),
                              reads=[CT_t[g], sbv_t], writes=[yo_t])
                        sc.op("pool", lambda e, g=g, ea=ea: e.tensor_tensor(
                            out=Sf[:, g, :].rearrange("p (h q) -> p h q", q=64), in0=Sf[:, g, :].rearrange("p (h q) -> p h q", q=64),
                            in1=ea[:, 2, g * 8:(g + 1) * 8].unsqueeze(2).to_broadcast([128, 8, 64]), op=ALU.mult),
                            reads=[Sf_t[g], ea_t], writes=[Sf_t[g]])
                        sc.op("dve", lambda e, g=g, stp=stp: e.tensor_tensor(out=Sf[:, g, :], in0=Sf[:, g, :], in1=stp[:],
                                                                            op=ALU.add),
                              reads=[Sf_t[g], stp_t], writes=[Sf_t[g]])
                        nb, nb_t = r_Sb[g].next()
                        sc.op("act", lambda e, nb=nb, g=g: e.copy(out=nb, in_=Sf[:, g, :]), reads=[Sf_t[g]], writes=[nb_t])
                        cur[g] = (nb, nb_t)
                        t, t_t = r_t.next()
                        sc.op("dve", lambda e, t=t, yo=yo, ea=ea, g=g: e.tensor_tensor(
                            out=t[:].rearrange("p (h q) -> p h q", q=64), in0=yo[:].rearrange("p (h q) -> p h q", q=64),
                            in1=ea[:, 0, g * 8:(g + 1) * 8].unsqueeze(2).to_broadcast([128, 8, 64]), op=ALU.mult),
                            reads=[yo_t, ea_t], writes=[t_t])
                        ts.append((t, t_t))
                    cbms = []
                    for g in range(2):
                        cb, cb_t = r_cb.next()
                        sc.op("pe", lambda e, cb=cb, g=g, tsl=tsl: e.matmul(cb[:], lhsT=BT[:, g, tsl], rhs=CT[:, g, tsl],
                                                                           start=True, stop=True),
                              reads=[BT_t[g], CT_t[g]], writes=[cb_t])
                        cbm, cbm_t = r_cbm.next()
                        sc.op("dve", lambda e, cbm=cbm, cb=cb: e.tensor_tensor(out=cbm[:], in0=cb[:], in1=tri_in[:], op=ALU.mult),
                              reads=[cb_t, tri_in_t], writes=[cbm_t])
                        cbms.append((cbm, cbm_t))
                    mts = []
                    for pair in range(2):
                        segs = []
                        for u in (2 * pair, 2 * pair + 1):
                            am, am_t = ams[u]
                            seg, seg_t = r_seg.next()
                            sc.op("pe", lambda e, seg=seg, am=am: e.matmul(seg[:], lhsT=tri_sb[:],
                                                                           rhs=am[:].rearrange("p a b -> p (a b)"),
                                                                           start=True, stop=True),
                                  reads=[tri_sb_t, am_t], writes=[seg_t])
                            segs.append((seg, seg_t))
                        decs = []
                        for (seg, seg_t) in segs:
                            dec, dec_t = r_dec.next()
                            sc.op("act", lambda e, dec=dec, seg=seg: e.activation(out=dec[:].rearrange("p a b -> p (a b)"),
                                                                                 in_=seg[:], func=AF.Exp),
                                  reads=[seg_t], writes=[dec_t])
                            decs.append((dec, dec_t))
                        for k, (dec, dec_t) in enumerate(decs):
                            u = 2 * pair + k
                            cbm, cbm_t = cbms[u // 2]
                            mt, mt_t = r_mt.next()
                            sc.op("dve", lambda e, mt=mt, dec=dec, cbm=cbm: e.tensor_tensor(
                                out=mt[:], in0=dec[:], in1=cbm[:].unsqueeze(1).to_broadcast([128, 4, 128]), op=ALU.mult),
                                reads=[dec_t, cbm_t], writes=[mt_t])
                            mts.append((mt, mt_t))
                    for g in range(2):
                        yd, yd_t = r_yd.next()
                        if fwd:
                            sc.op("pe", lambda e, yd=yd, ybl=ybl, g=g: e.matmul(
                                yd[:], lhsT=ident[:], rhs=ybl[:, g * 512:(g + 1) * 512], start=True, stop=False,
                                skip_group_check=True), reads=[G["ident_t"], ybl_t], writes=[yd_t])
                        for q4 in range(2):
                            mt, mt_t = mts[g * 2 + q4]
                            for hh in range(4):
                                h = g * 8 + q4 * 4 + hh
                                hl = h - g * 8
                                sc.op("pe", lambda e, yd=yd, mt=mt, hh=hh, hl=hl, h=h, xdt=xdt: e.matmul(
                                    yd[:, hl * 64:(hl + 1) * 64], lhsT=mt[:, hh, :], rhs=xdt[:, h * 64:(h + 1) * 64],
                                    start=(not fwd), stop=True, skip_group_check=True),
                                    reads=[mt_t, xdt_t], writes=[yd_t], part=(fwd or not (q4 == 0 and hh == 0)))
                        t, t_t = ts[g]
                        gs = slice(g * 512, (g + 1) * 512)
                        if not fwd:
                            sc.op("dve", lambda e, t=t, yd=yd, ybl=ybl, gs=gs: e.tensor_tensor(out=ybl[:, gs], in0=t[:], in1=yd[:],
                                                                                              op=ALU.add),
                                  reads=[t_t, yd_t], writes=[ybl_t], part=(g > 0))
                        else:
                            sc.op("dve", lambda e, t=t, yd=yd, yf=yf, gs=gs: e.tensor_tensor(out=yf[:, gs], in0=t[:], in1=yd[:],
                                                                                            op=ALU.add),
                                  reads=[t_t, yd_t], writes=[yf_t], part=(g > 0))
                    if not fwd:
                        sc.dma("pool", ybw[tsl, :], ybl[:], owner=ybl_t, reads=[ybl_t], writes=[ybw_t], part=True)
                        return None
                    if i % 4 == 0:
                        z, z_t = r_z.next()
                        sc.dma("sp", z[:], U["z"][i * 128:(i + 4) * 128, :].rearrange("(j p) c -> p j c", p=128), owner=z_t,
                               reads=[G["dram_t"]["z"]], writes=[z_t])
                        sc.op("act", lambda e, z=z: e.activation(out=z[:], in_=z[:], func=AF.Silu), reads=[z_t], writes=[z_t])
                        zcur[0] = (z, z_t)
                    z, z_t = zcur[0]
                    sz = z[:, i % 4, :]
                    sz_t = z_t
                    xd, xd_t = r_xdt.next()
                    sc.op("pool", lambda e, xd=xd, i=i: e.tensor_tensor(
                        out=xd[:].rearrange("p (h q) -> p h q", q=64), in0=xtok[:, i, 0:1024].rearrange("p (h q) -> p h q", q=64),
                        in1=rows[:, 2, 0:16].unsqueeze(2).to_broadcast([128, 16, 64]), op=ALU.mult),
                        reads=[xtok_t[i], rows_t], writes=[xd_t])
                    sc.op("dve", lambda e, yf=yf, xd=xd: e.tensor_tensor(out=yf[:], in0=yf[:], in1=xd[:], op=ALU.add),
                          reads=[yf_t, xd_t], writes=[yf_t])
                    sc.op("dve", lambda e, yf=yf, sz=sz: e.tensor_tensor(out=yf[:], in0=yf[:], in1=sz, op=ALU.mult),
                          reads=[yf_t, sz_t], writes=[yf_t])
                    ss, ss_t = r_ss.next()
                    for g in range(2):
                        gs = slice(g * 512, (g + 1) * 512)
                        jk, jk_t = r_jk.next()
                        sc.op("dve", lambda e, jk=jk, yf=yf, gs=gs, ss=ss, g=g: e.scalar_tensor_tensor(
                            out=jk[:], in0=yf[:, gs], scalar=1.0, in1=yf[:, gs], op0=ALU.mult, op1=ALU.mult,
                            accum_out=ss[:, g:g + 1]), reads=[yf_t], writes=[jk_t, ss_t])
                    sc.op("dve", lambda e, ss=ss: e.tensor_scalar(out=ss[:, 2:4], in0=ss[:, 0:2], scalar1=1.0 / 512.0, scalar2=EPS,
                                                                  op0=ALU.mult, op1=ALU.add), reads=[ss_t], writes=[ss_t])
                    sc.op("pool", lambda e, ss=ss: e.tensor_tensor(out=ss[:, 0:2], in0=ss[:, 2:4], in1=G["neghalf"][:, 0:2],
                                                                   op=ALU.pow), reads=[ss_t, G["neghalf_t"]], writes=[ss_t])
                    y, y_t = r_y.next()
                    for g in range(2):
                        gs = slice(g * 512, (g + 1) * 512)
                        sc.op("dve", lambda e, y=y, yf=yf, gs=gs, ss=ss, g=g: e.scalar_tensor_tensor(
                            out=y[:, gs], in0=yf[:, gs], scalar=ss[:, g:g + 1], in1=nwb[:, gs], op0=ALU.mult, op1=ALU.mult),
                            reads=[yf_t, ss_t, nwb_t], writes=[y_t], part=(g > 0))
                    return (y, y_t, i)

                def tileC(c3):
                    if c3 is None:
                        return
                    (y, y_t, i) = c3
                    emit_yT(P, sc, G, r_tp, y, y_t, yst, yst_t, i, YB["ssd"], G["dram_t"]["yb_ssd"], gsz=2)

                prev3 = None
                for i in order:
                    n3 = tileA(i)
                    tileC(prev3)
                    prev3 = n3
                tileC(prev3)

            ssd_pass(1)
            ssd_pass(0)
            sc.barrier(release=tl2)
        sc.barrier(release=tiles)


def emit_yT(P, sc, G, r_tp, y, y_t, yst, yst_t, i, dst, dst_t, gsz=4):
    ident = G["ident"]
    for half in range(2):
        tp, tp_t = r_tp.next()
        for jq in range(4):
            c = half * 4 + jq
            sc.op("pe", lambda e, tp=tp, jq=jq, c=c: e.transpose(out=tp[:, jq, :], in_=y[:, c * 128:(c + 1) * 128],
                                                               identity=ident[:]),
                  reads=[y_t, G["ident_t"]], writes=[tp_t], part=(jq > 0))
        sc.op("act", lambda e, tp=tp, half=half: e.copy(
            out=yst[:, half * 4:half * 4 + 4, (i % gsz) * 128:(i % gsz + 1) * 128], in_=tp[:]),
            reads=[tp_t], writes=[yst_t], part=not (i % gsz == 0 and half == 0))
    if i % gsz == gsz - 1:
        yv = dst.rearrange("(c p) t -> p c t", p=128)
        sc.dma("pool", yv[:, :, (i - gsz + 1) * 128:(i + 1) * 128], yst[:], owner=yst_t, reads=[yst_t], writes=[dst_t],
               part=True)


NEG = -30000.0


def na_r0(r):
    return min(max(r - 4, 0), 24)


def na_valid(kr, qr):
    return na_r0(qr) <= kr < na_r0(qr) + 8


def phase_D(P, sc, G, U, YB, prm, natt, l):
    nc = P.nc
    with contextlib.ExitStack() as ph:
        qnT = P.sb(ph, "D_qnT", [128, 8, S], BF16)
        knT = P.sb(ph, "D_knT", [128, 8, S], BF16)
        qn_t = sc.tiles_n("D_qn", 8)
        kn_t = sc.tiles_n("D_kn", 8)
        TT = P.sb(ph, "D_TT", [128, 8, 20, 64], BF16)
        TT_t = sc.tiles_n("D_TT", 4)
        wcol = P.sb(ph, "D_wcol", [128, 4], F32)
        wcol_t = sc.tile("D_wcol")
        tiles = qn_t + kn_t + TT_t + [wcol_t]
        with contextlib.ExitStack() as s1:
            TTf = [P.sb(s1, "D_TTf%d" % i, [128, 2, 20, 64], F32) for i in range(1)]
            TTf_t = sc.tiles_n("D_TTf", 1)
            qc_ = [P.sb(s1, "D_qc%d" % i, [128, S], BF16) for i in range(3)]
            qc_t = sc.tiles_n("D_qc", 3)
            sq = [P.sb(s1, "D_sq%d" % i, [128, S], BF16) for i in range(2)]
            sq_t = sc.tiles_n("D_sq", 2)
            lnv = [P.sb(s1, "D_ln%d" % i, [128, S], F32) for i in range(2)]
            lnv_t = sc.tiles_n("D_ln", 2)
            bones = P.sb(s1, "D_bones", [128, 128], BF16)
            bones_t = sc.tile("D_bones")
            ssp = [P.ps(s1, "D_ssp%d" % i, [128, S], F32) for i in range(2)]
            ssp_t = sc.tiles_n("D_ssp", 2)
            t1 = TTf_t + qc_t + sq_t + lnv_t + [bones_t] + ssp_t
            for g in range(4):
                b = 0
                sc.dma("sp", TTf[b][:], natt[l][:, 2 * g:2 * g + 2, :, :], owner=TTf_t[b], writes=[TTf_t[b]])
                sc.op("pool", lambda e, b=b, g=g: e.tensor_copy(out=TT[:, 2 * g:2 * g + 2, :, :], in_=TTf[b][:]),
                      reads=[TTf_t[b]], writes=[TT_t[g]])
            for hh in range(2):
                sc.dma("sp", wcol[hh * 64:(hh + 1) * 64, 2:3], prm["na_q_norm_w"][l].rearrange("(d o) -> d o", o=1),
                       owner=wcol_t, writes=[wcol_t], part=True)
                sc.dma("sp", wcol[hh * 64:(hh + 1) * 64, 1:2], prm["na_k_norm_w"][l].rearrange("(d o) -> d o", o=1),
                       owner=wcol_t, writes=[wcol_t], part=True)
            sc.op("dve", lambda e: e.tensor_scalar(out=wcol[:, 0:1], in0=wcol[:, 2:3], scalar1=0.125, scalar2=None,
                                                   op0=ALU.mult), reads=[wcol_t], writes=[wcol_t])
            sc.op("pool", lambda e: e.memset(bones[:], 0.0), writes=[bones_t])
            sc.op("pool", lambda e: e.memset(bones[0:64, 0:64], 1.0), reads=[bones_t], writes=[bones_t])
            sc.op("pool", lambda e: e.memset(bones[64:128, 64:128], 1.0), reads=[bones_t], writes=[bones_t])
            jobs = []
            for which, (src, dstT, dst_t, wc) in enumerate(((U["nq"], qnT, qn_t, 0), (U["nk"], knT, kn_t, 1))):
                src_t = G["dram_t"]["nq" if which == 0 else "nk"]
                for c in range(8):
                    jobs.append((src, src_t, dstT, dst_t, wc, c))

            def n_s1(k):
                (src, src_t, dstT, dst_t, wc, c) = jobs[k]
                cb = k % 2
                q3 = k % 3
                sc.dma("sp", qc_[q3][:], src[c * 128:(c + 1) * 128, :], owner=qc_t[q3], reads=[src_t], writes=[qc_t[q3]])
                sc.op("dve", lambda e, cb=cb, q3=q3: e.tensor_tensor(out=sq[cb][:], in0=qc_[q3][:], in1=qc_[q3][:], op=ALU.mult),
                      reads=[qc_t[q3]], writes=[sq_t[cb]])
                for tb in range(4):
                    sl = slice(tb * 512, (tb + 1) * 512)
                    sc.op("pe", lambda e, cb=cb, sl=sl: e.matmul(ssp[cb][:, sl], lhsT=bones[:], rhs=sq[cb][:, sl], start=True,
                                                                 stop=True),
                          reads=[bones_t, sq_t[cb]], writes=[ssp_t[cb]], part=(tb > 0))
                sc.op("act", lambda e, cb=cb: e.activation(out=lnv[cb][:], in_=ssp[cb][:], func=AF.Ln,
                                                           bias=G["eps"][:, 0:1], scale=1.0 / 64.0),
                      reads=[ssp_t[cb], G["eps_t"]], writes=[lnv_t[cb]])
                sc.op("act", lambda e, cb=cb: e.activation(out=lnv[cb][:], in_=lnv[cb][:], func=AF.Exp, scale=-0.5),
                      reads=[lnv_t[cb]], writes=[lnv_t[cb]])

            def n_s2(k):
                (src, src_t, dstT, dst_t, wc, c) = jobs[k]
                cb = k % 2
                q3 = k % 3
                sc.op("dve", lambda e, cb=cb, q3=q3, dstT=dstT, c=c, wc=wc: e.scalar_tensor_tensor(
                    out=dstT[:, c, :], in0=qc_[q3][:], scalar=wcol[:, wc:wc + 1], in1=lnv[cb][:],
                    op0=ALU.mult, op1=ALU.mult),
                    reads=[qc_t[q3], lnv_t[cb], wcol_t], writes=[dst_t[c]])

            n_s1(0)
            for k in range(len(jobs)):
                if k + 1 < len(jobs):
                    n_s1(k + 1)
                n_s2(k)
            sc.barrier(release=t1)
        with contextlib.ExitStack() as s2:
            vx = P.sb(s2, "D_vx", [128, NT, 16, 65], BF16)
            vx_t = sc.tiles_n("D_vx", NT)
            sps = [P.ps(s2, "D_sps%d" % i, [128, 8, 128], F32) for i in range(2)]
            sps_t = sc.tiles_n("D_sps", 2)
            pT = [P.sb(s2, "D_pT%d" % i, [128, 5, 128], BF16) for i in range(3)]
            pT_t = sc.tiles_n("D_pT", 3)
            po = [P.ps(s2, "D_po%d" % i, [128, 2, 66], F32) for i in range(2)]
            po_t = sc.tiles_n("D_po", 2)
            rc = [P.sb(s2, "D_rc%d" % i, [128, 2], F32) for i in range(2)]
            rc_t = sc.tiles_n("D_rc", 2)
            ot = [P.sb(s2, "D_ot%d" % i, [128, 1024], BF16) for i in range(2)]
            ot_t = sc.tiles_n("D_ot", 2)
            tp = [P.ps(s2, "D_tp%d" % i, [128, 4, 128], BF16) for i in range(2)]
            tp_t = sc.tiles_n("D_tp", 2)
            yst = P.sb(s2, "D_yst", [128, 8, 512], BF16)
            yst_t = sc.tile("D_yst")
            t2 = vx_t + sps_t + pT_t + po_t + rc_t + ot_t + tp_t + [yst_t]
            nvv = U["nv"].rearrange("(i p) (h d) -> p i h d", p=128, d=64)
            for i in range(NT):
                sc.op("pool", lambda e, i=i: e.memset(vx[:, i, :, 64:65], 1.0), writes=[vx_t[i]])
                sc.dma("sp", vx[:, i, :, 0:64], nvv[:, i, :, :], owner=vx_t[i], reads=[G["dram_t"]["nv"]],
                       writes=[vx_t[i]], part=True)
            ident = G["ident"]
            tpc = [0]
            units = []
            for i in range(NT):
                jlo = na_r0(2 * i) // 2
                jhi = (na_r0(2 * i + 1) + 7) // 2
                js = list(range(jlo, jhi + 1))
                for hp in range(8):
                    for hh in range(2):
                        units.append((i, hp, hh, js))

            def emit_S(u):
                i, hp, hh, js = units[u]
                h = 2 * hp + hh
                p0 = 64 * hh
                sb_ = u % 2
                for jj, j in enumerate(js):
                    sc.op("pe", lambda e, sb_=sb_, jj=jj, j=j, p0=p0, hp=hp, i=i: e.matmul(
                        sps[sb_][:, jj, :], lhsT=knT[p0:p0 + 64, hp, j * 128:(j + 1) * 128],
                        rhs=qnT[p0:p0 + 64, hp, i * 128:(i + 1) * 128], start=True, stop=False,
                        skip_group_check=True),
                        reads=[kn_t[hp], qn_t[hp]], writes=[sps_t[sb_]], part=(jj > 0))
                    mms = []
                    for b0 in range(2):
                        qr = 2 * i + b0
                        va = [na_valid(2 * j + a, qr) for a in range(2)]
                        dr0 = 2 * j - qr + 7
                        cs = slice(b0 * 64, (b0 + 1) * 64)
                        if va[0] and va[1]:
                            mms.append((slice(0, 128), cs, TT[p0:p0 + 64, hp, dr0:dr0 + 2, :]))
                        elif not va[0] and not va[1]:
                            mms.append((slice(0, 128), cs, TT[p0:p0 + 64, hp, 15:17, :]))
                        elif (not va[0]) and va[1] and dr0 + 1 == 3:
                            mms.append((slice(0, 128), cs, TT[p0:p0 + 64, hp, 16:18, :]))
                        elif va[0] and (not va[1]) and dr0 == 10:
                            mms.append((slice(0, 128), cs, TT[p0:p0 + 64, hp, 18:20, :]))
                        else:
                            d0 = dr0 if va[0] else 15
                            d1 = dr0 + 1 if va[1] else 16
                            mms.append((slice(0, 64), cs, TT[p0:p0 + 64, hp, d0, :]))
                            mms.append((slice(64, 128), cs, TT[p0:p0 + 64, hp, d1, :]))
                    for mi, (ps_, cs, lhs) in enumerate(mms):
                        sc.op("pe", lambda e, sb_=sb_, jj=jj, ps_=ps_, cs=cs, lhs=lhs, p0=p0, last=(mi == len(mms) - 1):
                              e.matmul(sps[sb_][ps_, jj, cs], lhsT=lhs, rhs=ident[p0:p0 + 64, p0:p0 + 64],
                                       start=False, stop=last, skip_group_check=True),
                              reads=[TT_t[hp // 2], G["ident_t"]], writes=[sps_t[sb_]], part=True)

            def emit_rest(u):
                i, hp, hh, js = units[u]
                h = 2 * hp + hh
                sb_ = u % 2
                pt = u % 3
                pb_ = (u // 2) % 2
                ob = i % 2
                n = len(js)
                n1 = min(n, 4)
                sc.op("act", lambda e, pt=pt, sb_=sb_, n1=n1: e.activation(out=pT[pt][:, 0:n1, :],
                                                                         in_=sps[sb_][:, 0:n1, :], func=AF.Exp),
                      reads=[sps_t[sb_]], writes=[pT_t[pt]])
                if n > 4:
                    sc.op("act", lambda e, pt=pt, sb_=sb_, n=n: e.activation(out=pT[pt][:, 4:n, :],
                                                                           in_=sps[sb_][:, 4:n, :], func=AF.Exp),
                          reads=[sps_t[sb_]], writes=[pT_t[pt]], part=True)
                for jj, j in enumerate(js):
                    sc.op("pe", lambda e, pb_=pb_, hh=hh, pt=pt, jj=jj, j=j, h=h, n=n: e.matmul(
                        po[pb_][:, hh, 0:65], lhsT=pT[pt][:, jj, :], rhs=vx[:, j, h, :],
                        start=(jj == 0), stop=(jj == n - 1)),
                        reads=[pT_t[pt], vx_t[j]], writes=[po_t[pb_]], part=(hh > 0 or jj > 0))
                if hh == 1:
                    sc.op("dve", lambda e, pb_=pb_: e.reciprocal(out=rc[pb_][:, 0:2], in_=po[pb_][:, :, 64]),
                          reads=[po_t[pb_]], writes=[rc_t[pb_]])
                    for h2 in range(2):
                        hx = 2 * hp + h2
                        sc.op("dve", lambda e, pb_=pb_, h2=h2, hx=hx, ob=ob: e.tensor_scalar(
                            out=ot[ob][:, hx * 64:(hx + 1) * 64], in0=po[pb_][:, h2, 0:64], scalar1=rc[pb_][:, h2:h2 + 1],
                            scalar2=None, op0=ALU.mult),
                            reads=[po_t[pb_], rc_t[pb_]], writes=[ot_t[ob]], part=(hx > 0))
                if hp == 7 and hh == 1:
                    for half in range(2):
                        tb_ = tpc[0] % 2
                        tpc[0] += 1
                        for jq in range(4):
                            c = half * 4 + jq
                            sc.op("pe", lambda e, tb_=tb_, jq=jq, c=c, ob=ob: e.transpose(
                                out=tp[tb_][:, jq, :], in_=ot[ob][:, c * 128:(c + 1) * 128], identity=ident[:]),
                                reads=[ot_t[ob], G["ident_t"]], writes=[tp_t[tb_]], part=(jq > 0))
                        sc.op("act", lambda e, tb_=tb_, half=half, i=i: e.copy(
                            out=yst[:, half * 4:half * 4 + 4, (i % 4) * 128:(i % 4 + 1) * 128], in_=tp[tb_][:]),
                            reads=[tp_t[tb_]], writes=[yst_t], part=not (i % 4 == 0 and half == 0))
                    if i % 4 == 3:
                        yv = YB["na"].rearrange("(c p) t -> p c t", p=128)
                        sc.dma("pool", yv[:, :, (i - 3) * 128:(i + 1) * 128], yst[:], owner=yst_t, reads=[yst_t],
                               writes=[G["dram_t"]["yb_na"]], part=True)

            emit_S(0)
            for u in range(len(units)):
                if u + 1 < len(units):
                    emit_S(u + 1)
                emit_rest(u)
            sc.barrier(release=t2)
        sc.barrier(release=tiles)


def phase_F(P, sc, G, prm, l):
    nc = P.nc
    x = G["x"]
    with contextlib.ExitStack() as ph:
        hT = P.sb(ph, "F_hT", [128, 8, S], BF16)
        hT_t = sc.tiles_n("F_hT", NT)
        tiles = list(hT_t)
        tiles += rms_transpose(P, sc, G, ph, prm["norm_mlp_w"][l], hT, hT_t, l, "F")
        wst = WStream(P, sc, ph, "F", 1, 4096, nf=2, nb=3)
        fT = [P.sb(ph, "F_fT%d" % i, [128, 4, S], BF16) for i in range(2)]
        fT_t = [sc.tiles_n("F_fT%d_" % i, 4) for i in range(2)]
        rl = [P.sb(ph, "F_rl%d" % i, [128, 512], F32) for i in range(2)]
        rl_t = sc.tiles_n("F_rl", 2)
        acc = [P.ps(ph, "F_acc%d" % i, [128, 512], F32) for i in range(4)]
        acc_t = sc.tiles_n("F_acc", 4)
        tiles += wst.tiles + fT_t[0] + fT_t[1] + rl_t + acc_t
        w1v = prm["w_ff1"][l].rearrange("(kc p) n -> p kc n", p=128)
        w2v = prm["w_ff2"][l].rearrange("(c p) n -> p c n", p=128)
        items = []
        for g in range(8):
            items.append((w1v[:, :, g * 512:(g + 1) * 512], 8, 512))
            items.append((w2v[:, g * 4:(g + 1) * 4, :], 4, 1024))
        wst.items = items
        wst_views = {}

        def view(slot, k, n):
            return slot[:, 0, :].rearrange("p (k n) -> p k n", k=k)
        def _load(g):
            if g >= len(items):
                return
            ap, k, n = items[g]
            fs = g % wst.nf
            sc.dma("sp", view(wst.f[fs], k, n), ap, owner=wst.f_t[fs], writes=[wst.f_t[fs]])

        def _cast(g):
            if g >= len(items):
                return
            fs, bs = g % wst.nf, g % wst.nb
            sc.op("pool", lambda e: e.tensor_copy(out=wst.b[bs][:, 0, :], in_=wst.f[fs][:, 0, :]),
                  reads=[wst.f_t[fs]], writes=[wst.b_t[bs]])
        wst._load = _load
        wst._cast = _cast
        _load(0)
        _load(1)
        _cast(0)
        ai = 0
        ri = 0
        for g in range(8):
            fb = g % 2
            w1s, w1_t = wst.get(2 * g)
            w1b = view(w1s, 8, 512)
            for c in range(4):
                for tb in range(4):
                    a = ai % 4
                    ai += 1
                    for kc in range(8):
                        sc.op("pe", lambda e, a=a, kc=kc, w1b=w1b, c=c, tb=tb: e.matmul(
                            acc[a][:], lhsT=w1b[:, kc, c * 128:(c + 1) * 128],
                            rhs=hT[:, kc, tb * 512:(tb + 1) * 512], start=(kc == 0), stop=(kc == 7)),
                            reads=[w1_t] + hT_t[tb * 4:tb * 4 + 4], writes=[acc_t[a]], part=(kc > 0))
                    r = ri % 2
                    ri += 1
                    sc.op("act", lambda e, r=r, a=a: e.activation(out=rl[r][:], in_=acc[a][:], func=AF.Relu),
                          reads=[acc_t[a]], writes=[rl_t[r]])
                    sc.op("pool", lambda e, r=r, fb=fb, c=c, tb=tb: e.tensor_tensor(
                        out=fT[fb][:, c, tb * 512:(tb + 1) * 512], in0=rl[r][:], in1=rl[r][:], op=ALU.mult),
                        reads=[rl_t[r]], writes=[fT_t[fb][c]], part=(tb > 0))
            w2s, w2_t = wst.get(2 * g + 1)
            w2b = view(w2s, 4, 1024)
            for i in range(NT):
                for hh in range(2):
                    a = ai % 4
                    ai += 1
                    for c in range(4):
                        sc.op("pe", lambda e, a=a, c=c, w2b=w2b, i=i, hh=hh, fb=fb: e.matmul(
                            acc[a][:], lhsT=fT[fb][:, c, i * 128:(i + 1) * 128],
                            rhs=w2b[:, c, hh * 512:(hh + 1) * 512], start=(c == 0), stop=(c == 3)),
                            reads=[w2_t, fT_t[fb][c]], writes=[acc_t[a]], part=(c > 0))
                    xs = x[:, i, hh * 512:(hh + 1) * 512]
                    sc.op("dve", lambda e, xs=xs, a=a: e.tensor_tensor(out=xs, in0=xs, in1=acc[a][:], op=ALU.add),
                          reads=[acc_t[a], G["xt"][i]], writes=[G["xt"][i]])
        sc.barrier(release=tiles)


_NC_CACHE = {}


def make_na_tt(rpb):
    rpb = np.asarray(rpb, dtype=np.float32)
    L = rpb.shape[0]
    out = np.full((L, 128, 8, 20, 64), NEG, dtype=np.float32)
    qc = np.arange(64)
    ws = np.clip(qc - 8, 0, 48)
    for q in range(64):
        kc = np.arange(ws[q], ws[q] + 16)
        idx = kc - q + 15
        for hh in range(2):
            out[:, hh * 64 + q, :, 0:15, ws[q]:ws[q] + 16] = rpb[:, hh::2][:, :, :, idx]
    out[:, :, :, 17, :] = out[:, :, :, 3, :]
    out[:, :, :, 18, :] = out[:, :, :, 10, :]
    return out


def kernel(**inputs):
    cfg = {}
    key = "full"
    if key not in _NC_CACHE:
        _NC_CACHE[key] = build(cfg)
    nc = _NC_CACHE[key]
    x = np.ascontiguousarray(inputs["x"], dtype=np.float32)
    base = {n: np.ascontiguousarray(inputs[n], dtype=np.float32) for n in PARAM_NAMES}
    base["na_tt"] = make_na_tt(inputs["na_rpb"])
    in_maps = []
    for c in range(8):
        m = dict(base)
        m["x"] = x[c]
        in_maps.append(m)
    res = run_bass_kernel_spmd(nc, in_maps, core_ids=list(range(8)))
    return np.stack([r["y"] for r in res.results], axis=0).astype(np.float32)
```

```python
import contextlib
import numpy as np
import concourse.bass as bass
import concourse.mybir as mybir
from concourse.bass_utils import run_bass_kernel_spmd

F32 = mybir.dt.float32
BF16 = mybir.dt.bfloat16
ALU = mybir.AluOpType
AF = mybir.ActivationFunctionType
AX = mybir.AxisListType

D = 1024
S = 2048
NT = S // 128
DEPTH = 2
N_IN = 11840
EPS = 1e-6


class TT:
    __slots__ = ("name", "lw", "rd", "dsems", "gen")

    def __init__(self, name):
        self.name = name
        self.lw = {}
        self.rd = {}
        self.gen = {}
        self.dsems = {}


class Sched:
    ENG = ("pe", "act", "dve", "pool", "sp")
    BLK = {"pe": "tensor", "act": "scalar", "dve": "vector", "pool": "gpsimd", "sp": "sync"}

    def __init__(self, nc, stack):
        self.nc = nc
        self.stack = stack
        self.ops = {e: [] for e in self.ENG}
        self.seen = {e: {} for e in self.ENG}
        self.esem = {e: stack.enter_context(nc.semaphore("es_" + e)) for e in self.ENG if e != "sp"}
        self.tiles = []
        self.free_dsems = {"sp": [], "pool": [], "act": []}
        self.nsem = 4
        self.skip_same = {"pe"}

    def tile(self, name):
        t = TT(name)
        self.tiles.append(t)
        return t

    def tiles_n(self, name, n):
        return [self.tile("%s%d" % (name, i)) for i in range(n)]

    def _collect(self, reads, writes, part):
        evs = {}

        def add(d):
            for k, v in d.items():
                if k not in evs or evs[k][0] < v[0]:
                    evs[k] = v
        for t in reads:
            add(t.lw)
        for t in writes:
            if part and not t.rd:
                add(t.gen)
                continue
            g = dict(t.rd)
            for k, v in t.lw.items():
                if k not in g or g[k][0] < v[0]:
                    g[k] = v
            t.gen = g
            add(g)
        return evs

    def _waits(self, eng, evs):
        waits = []
        for k, (val, obj) in evs.items():
            if k == ("E", eng) and eng in self.skip_same:
                continue
            if self.seen[eng].get(k, 0) >= val:
                continue
            self.seen[eng][k] = val
            waits.append((k, val, obj))
            if k[0] == "E":
                self.ops[k[1]][val - 1]["inc"] = True
        return waits

    def _update(self, ev_key, ev_val, reads, writes, part):
        for t in reads:
            t.rd[ev_key] = ev_val
        for t in writes:
            if part and not t.rd:
                t.lw[ev_key] = ev_val
            else:
                t.lw = {ev_key: ev_val}
                t.rd = {}

    def op(self, eng, fn, reads=(), writes=(), part=False):
        waits = self._waits(eng, self._collect(reads, writes, part))
        self.ops[eng].append({"fn": fn, "waits": waits, "inc": False, "dma": None})
        idx = len(self.ops[eng])
        self._update(("E", eng), (idx, None), reads, writes, part)

    def dma(self, q, out, in_, owner, reads=(), writes=(), part=False, **kw):
        waits = self._waits(q, self._collect(reads, writes, part))
        rec = owner.dsems.get(q)
        if rec is None:
            if self.free_dsems[q]:
                rec = self.free_dsems[q].pop()
            else:
                rec = [self.stack.enter_context(self.nc.semaphore("ds%d" % self.nsem)), 0, self.nsem]
                self.nsem += 1
            owner.dsems[q] = rec
        rec[1] += 16
        self.ops[q].append({"fn": (lambda e: e.dma_start(out=out, in_=in_, **kw)), "waits": waits,
                            "inc": False, "dma": rec[0]})
        self._update(("D", rec[2]), (rec[1], rec[0]), reads, writes, part)

    def barrier(self, release=()):
        evs = {}
        for e in self.ENG:
            if e == "sp":
                continue
            idx = len(self.ops[e])
            while idx > 0 and (self.ops[e][idx - 1]["dma"] is not None or self.ops[e][idx - 1].get("nop")):
                idx -= 1
            if idx > 0:
                evs[("E", e)] = (idx, None)
        for t in self.tiles:
            for d in (t.lw, t.rd):
                for k, v in d.items():
                    if k[0] == "D" and (k not in evs or evs[k][0] < v[0]):
                        evs[k] = v
        for e in self.ENG:
            sk = self.skip_same
            self.skip_same = set()
            w = self._waits(e, dict(evs))
            self.skip_same = sk
            self.ops[e].append({"fn": (lambda en: en.nop()), "waits": w, "inc": False, "dma": None, "nop": True})
        for t in self.tiles:
            t.lw = {}
            t.rd = {}
            t.gen = {}
        rel = set(id(t) for t in release)
        for t in release:
            for q, rec in t.dsems.items():
                self.free_dsems[q].append(rec)
            t.dsems = {}
        self.tiles = [t for t in self.tiles if id(t) not in rel]

    def emit(self):
        nc = self.nc
        mile = {}
        for e in self.ENG:
            c = 0
            m = []
            for o in self.ops[e]:
                if o["inc"]:
                    c += 1
                m.append(c)
            mile[e] = m
            assert c < 60000, (e, c)
        with nc.Block() as block:
            for e in self.ENG:
                def body(engine, e=e):
                    for o in self.ops[e]:
                        for (k, val, obj) in o["waits"]:
                            if k[0] == "E":
                                engine.wait_ge(self.esem[k[1]], mile[k[1]][val - 1])
                            else:
                                engine.wait_ge(obj, val)
                        ins = o["fn"](engine)
                        if o["dma"] is not None:
                            ins.then_inc(o["dma"], 16)
                        elif o["inc"]:
                            ins.then_inc(self.esem[e], 1)
                getattr(block, self.BLK[e])(body)


class Prog:
    def __init__(self, cfg):
        self.cfg = cfg
        self.nc = bass.Bass("TRN2", target_bir_lowering=False)
        self.dbg = cfg.get("debug", ())

    def dram(self, name, shape, dt, kind="Internal"):
        if name in self.dbg:
            kind = "ExternalOutput"
        if name in self.cfg.get("ext_in", ()):
            kind = "ExternalInput"
        return self.nc.dram_tensor(name, list(shape), dt, kind=kind).ap()

    def sb(self, stack, name, shape, dt):
        self.uid = getattr(self, "uid", 0) + 1
        return stack.enter_context(self.nc.sbuf_tensor("%s_u%d" % (name, self.uid), list(shape), dt))

    def ps(self, stack, name, shape, dt):
        self.uid = getattr(self, "uid", 0) + 1
        return stack.enter_context(self.nc.psum_tensor("%s_u%d" % (name, self.uid), list(shape), dt))


IN_SIZES = (1024, 1536, 16, 16, 512, 512, 1024, 1024, 16, 16, 1024, 1024, 1024, 3072)
IN_OFF = [0]
for _s in IN_SIZES:
    IN_OFF.append(IN_OFF[-1] + _s)
(O_Z, O_XBC, O_DTF, O_DTB, O_GQ, O_GK, O_GV, O_GG, O_GAF, O_GAB, O_NQ, O_NK, O_NV, O_GATE, _) = IN_OFF

PARAM_NAMES = ["norm_mix_w", "w_in", "ssd_conv_w", "ssd_conv_b", "ssd_dt_bias_f", "ssd_dt_bias_b",
               "ssd_a_log_f", "ssd_a_log_b", "ssd_d", "ssd_norm_w", "gla_a2_f", "gla_a2_bias_f",
               "gla_a2_b", "gla_a2_bias_b", "gla_norm_w", "na_q_norm_w", "na_k_norm_w", "na_rpb",
               "w_branch_ssd", "w_branch_gla", "w_branch_na", "w_out", "norm_mlp_w", "w_ff1", "w_ff2"]
PARAM_SHAPES = {
    "norm_mix_w": (2, 1024), "w_in": (2, 1024, 11840), "ssd_conv_w": (2, 5, 1536), "ssd_conv_b": (2, 1536),
    "ssd_dt_bias_f": (2, 16), "ssd_dt_bias_b": (2, 16), "ssd_a_log_f": (2, 16), "ssd_a_log_b": (2, 16),
    "ssd_d": (2, 16), "ssd_norm_w": (2, 1024), "gla_a2_f": (2, 16, 512), "gla_a2_bias_f": (2, 512),
    "gla_a2_b": (2, 16, 512), "gla_a2_bias_b": (2, 512), "gla_norm_w": (2, 256), "na_q_norm_w": (2, 64),
    "na_k_norm_w": (2, 64), "na_rpb": (2, 16, 15, 31), "w_branch_ssd": (2, 1024, 1024),
    "w_branch_gla": (2, 1024, 1024), "w_branch_na": (2, 1024, 1024), "w_out": (2, 1024, 1024),
    "norm_mlp_w": (2, 1024), "w_ff1": (2, 1024, 4096), "w_ff2": (2, 4096, 1024),
}


def build(cfg):
    P = Prog(cfg)
    nc = P.nc
    layers = cfg.get("layers", DEPTH)
    phases = cfg.get("phases", "ABCDEF")
    x_in = nc.dram_tensor("x", [S, D], F32, kind="ExternalInput").ap()
    prm = {n: nc.dram_tensor(n, list(PARAM_SHAPES[n]), F32, kind="ExternalInput").ap() for n in PARAM_NAMES}
    y_out = nc.dram_tensor("y", [S, D], F32, kind="ExternalOutput").ap()
    natt = nc.dram_tensor("na_tt", [DEPTH, 128, 8, 20, 64], F32, kind="ExternalInput").ap()

    U = {}
    for nm, w in (("z", 1024), ("gv", 1024), ("gg", 1024), ("nv", 1024)):
        U[nm] = P.dram("u_" + nm, [S, w], BF16)
    U["dt"] = P.dram("u_dt", [S, 32], F32)
    for nm, w in (("xbc", 1536), ("gq", 512), ("gk", 512), ("nq", 1024), ("nk", 1024), ("gate", 3072)):
        U[nm] = P.dram("u_" + nm + "T", [w, S], BF16)
    U["ga"] = P.dram("u_gaT", [32, S], BF16)
    YB = {nm: P.dram("yb_" + nm, [1024, S], BF16) for nm in ("ssd", "gla", "na")}
    ybw = P.dram("ybw", [S, 1024], BF16)

    with contextlib.ExitStack() as top:
        sc = Sched(nc, top)
        G = {}
        G["x"] = P.sb(top, "x_res", [128, NT, D], F32)
        G["xt"] = sc.tiles_n("x", NT)
        G["ident"] = P.sb(top, "ident", [128, 128], BF16)
        G["ident_t"] = sc.tile("ident")
        G["dram_t"] = {k: sc.tile("d_" + k) for k in list(U) + ["yb_ssd", "yb_gla", "yb_na", "ybw"]}
        G["ybw"] = ybw

        ones_f = P.sb(top, "ones_f", [128, 128], F32)
        ones_t = sc.tile("ones_f")
        sc.op("pool", lambda e: e.memset(ones_f[:], 1.0), writes=[ones_t])
        sc.op("pool", lambda e: e.affine_select(out=G["ident"][:], in_=ones_f[:], pattern=[[-1, 128]],
                                                compare_op=ALU.is_equal, fill=0.0, base=0,
                                                channel_multiplier=1),
              reads=[ones_t], writes=[G["ident_t"]])
        G["ones_f"] = ones_f
        G["eps"] = P.sb(top, "epsc", [128, 2], F32)
        G["eps_t"] = sc.tile("epsc")
        sc.op("pool", lambda e: e.memset(G["eps"][:], EPS), writes=[G["eps_t"]])
        G["one"] = P.sb(top, "onec", [128, 2], F32)
        G["one_t"] = sc.tile("onec")
        sc.op("pool", lambda e: e.memset(G["one"][:], 1.0), writes=[G["one_t"]])
        G["neghalf"] = P.sb(top, "neghalf", [128, 16], F32)
        G["neghalf_t"] = sc.tile("neghalf")
        sc.op("pool", lambda e: e.memset(G["neghalf"][:], -0.5), writes=[G["neghalf_t"]])
        G["ones_t"] = ones_t

        build_tri(P, sc, G, top)
        xv = x_in.rearrange("(i p) d -> p i d", p=128)
        for i in range(NT):
            sc.dma("sp", G["x"][:, i, :], xv[:, i, :], owner=G["xt"][i], writes=[G["xt"][i]])

        for l in range(layers):
            if "A" in phases:
                phase_A(P, sc, G, U, prm, l)
            if "B" in phases:
                phase_B(P, sc, G, U, YB, prm, l)
            if "C" in phases:
                phase_C(P, sc, G, U, YB, prm, l)
            if "D" in phases:
                phase_D(P, sc, G, U, YB, prm, natt, l)
            if "E" in phases:
                phase_E(P, sc, G, U, YB, prm, l)
            if "F" in phases:
                phase_F(P, sc, G, prm, l)

        yv = y_out.rearrange("(i p) d -> p i d", p=128)
        outt = sc.tile("yout")
        for i in range(NT):
            sc.dma("sp", yv[:, i, :], G["x"][:, i, :], owner=G["xt"][i], reads=[G["xt"][i]], writes=[outt],
                   part=True)
        sc.op("sp", lambda e: e.nop(), reads=[outt])
        sc.barrier()
        sc.emit()
    return nc


def rms_transpose(P, sc, G, ph, wrow_ap, hT, hT_t, l, tag):
    nc = P.nc
    wb = P.sb(ph, tag + "_wb", [128, D], F32)
    wb_t = sc.tile(tag + "_wb")
    sc.dma("sp", wb[:], wrow_ap.partition_broadcast(128), owner=wb_t, writes=[wb_t])
    junk = [P.sb(ph, tag + "_junk%d" % i, [128, D], BF16) for i in range(2)]
    junk_t = sc.tiles_n(tag + "_junk", 2)
    hb = [P.sb(ph, tag + "_hb%d" % i, [128, D], BF16) for i in range(2)]
    hb_t = sc.tiles_n(tag + "_hb", 2)
    ss = [P.sb(ph, tag + "_ss%d" % i, [128, 2], F32) for i in range(2)]
    ss_t = sc.tiles_n(tag + "_ss", 2)
    tp = [P.ps(ph, tag + "_tp%d" % i, [128, 4, 128], BF16) for i in range(2)]
    tp_t = sc.tiles_n(tag + "_tp", 2)
    x = G["x"]
    new_tiles = [wb_t] + junk_t + hb_t + ss_t + tp_t
    def _s1(i):
        b = i % 2
        xt = G["xt"][i]
        sc.op("dve", lambda e, i=i, b=b: e.scalar_tensor_tensor(out=junk[b][:], in0=x[:, i, :], scalar=1.0,
                                                                in1=x[:, i, :], op0=ALU.mult, op1=ALU.mult,
                                                                accum_out=ss[b][:, 0:1]),
              reads=[xt], writes=[junk_t[b], ss_t[b]])
        sc.op("dve", lambda e, b=b: e.tensor_scalar(out=ss[b][:, 1:2], in0=ss[b][:, 0:1], scalar1=1.0 / D,
                                                    scalar2=EPS, op0=ALU.mult, op1=ALU.add),
              reads=[ss_t[b]], writes=[ss_t[b]])
        sc.op("pool", lambda e, b=b: e.tensor_tensor(out=ss[b][:, 0:1], in0=ss[b][:, 1:2],
                                                     in1=G["neghalf"][:, 0:1], op=ALU.pow),
              reads=[ss_t[b], G["neghalf_t"]], writes=[ss_t[b]])

    def _s2(i):
        b = i % 2
        xt = G["xt"][i]
        sc.op("dve", lambda e, i=i, b=b: e.scalar_tensor_tensor(out=hb[b][:], in0=x[:, i, :],
                                                                scalar=ss[b][:, 0:1], in1=wb[:],
                                                                op0=ALU.mult, op1=ALU.mult),
              reads=[xt, ss_t[b], wb_t], writes=[hb_t[b]])
        for half in range(2):
            pb = (2 * i + half) % 2
            for j in range(4):
                kc = half * 4 + j
                sc.op("pe", lambda e, b=b, pb=pb, j=j, kc=kc: e.transpose(
                    out=tp[pb][:, j, :], in_=hb[b][:, kc * 128:(kc + 1) * 128], identity=G["ident"][:]),
                    reads=[hb_t[b], G["ident_t"]], writes=[tp_t[pb]], part=(j > 0))
            eng = "act"
            if eng == "act":
                sc.op("act", lambda e, pb=pb, half=half, i=i: e.copy(
                    out=hT[:, half * 4:half * 4 + 4, i * 128:(i + 1) * 128], in_=tp[pb][:]),
                    reads=[tp_t[pb]], writes=[hT_t[i]], part=True)
            else:
                sc.op("dve", lambda e, pb=pb, half=half, i=i: e.tensor_copy(
                    out=hT[:, half * 4:half * 4 + 4, i * 128:(i + 1) * 128], in_=tp[pb][:]),
                    reads=[tp_t[pb]], writes=[hT_t[i]], part=True)

    _s1(0)
    for i in range(NT):
        if i + 1 < NT:
            _s1(i + 1)
        _s2(i)
    return new_tiles


class WStream:
    def __init__(self, P, sc, ph, tag, kdim, ncol, nf=2, nb=3, cast_eng="pool"):
        self.sc = sc
        self.cast_eng = cast_eng
        self.kdim, self.ncol = kdim, ncol
        self.nf, self.nb = nf, nb
        self.f = [P.sb(ph, "%s_wf%d" % (tag, i), [128, kdim, ncol], F32) for i in range(nf)]
        self.f_t = sc.tiles_n(tag + "_wf", nf)
        self.b = [P.sb(ph, "%s_wb%d" % (tag, i), [128, kdim, ncol], BF16) for i in range(nb)]
        self.b_t = sc.tiles_n(tag + "_wbt", nb)
        self.tiles = self.f_t + self.b_t
        self.items = []

    def start(self, items):
        self.items = items
        self._load(0)
        self._load(1)
        self._cast(0)

    def _load(self, g):
        if g >= len(self.items):
            return
        ap, k, n = self.items[g]
        fs = g % self.nf
        self.sc.dma("sp", self.f[fs][:, 0:k, 0:n], ap, owner=self.f_t[fs], writes=[self.f_t[fs]])

    def _cast(self, g):
        if g >= len(self.items):
            return
        ap, k, n = self.items[g]
        fs, bs = g % self.nf, g % self.nb
        if self.cast_eng == "act":
            self.sc.op("act", lambda e: e.copy(out=self.b[bs][:, 0:k, 0:n], in_=self.f[fs][:, 0:k, 0:n]),
                       reads=[self.f_t[fs]], writes=[self.b_t[bs]])
        else:
            self.sc.op("pool", lambda e: e.tensor_copy(out=self.b[bs][:, 0:k, 0:n], in_=self.f[fs][:, 0:k, 0:n]),
                       reads=[self.f_t[fs]], writes=[self.b_t[bs]])

    def get(self, g):
        self._cast(g + 1)
        self._load(g + 2)
        return self.b[g % self.nb], self.b_t[g % self.nb]


def proj_groups():
    g = []

    def seg(off, n, mode, key):
        c = 0
        while c < n:
            w = min(512, n - c)
            g.append((off + c, w, mode, key, c))
            c += w
    seg(O_Z, 1024, "tok", "z")
    seg(O_XBC, 1536, "feat", "xbc")
    g.append((O_DTF, 32, "tok32", "dt", 0))
    seg(O_GQ, 512, "feat", "gq")
    seg(O_GK, 512, "feat", "gk")
    seg(O_GV, 1024, "tok", "gv")
    seg(O_GG, 1024, "tok", "gg")
    g.append((O_GAF, 32, "feat32", "ga", 0))
    seg(O_NQ, 1024, "feat", "nq")
    seg(O_NK, 1024, "feat", "nk")
    seg(O_NV, 1024, "tok", "nv")
    seg(O_GATE, 3072, "feat", "gate")
    return g


def phase_A(P, sc, G, U, prm, l):
    nc = P.nc
    with contextlib.ExitStack() as ph:
        hT = P.sb(ph, "A_hT", [128, 8, S], BF16)
        hT_t = sc.tiles_n("A_hT", NT)
        tiles = list(hT_t)
        tiles += rms_transpose(P, sc, G, ph, prm["norm_mix_w"][l], hT, hT_t, l, "A")
        wst = WStream(P, sc, ph, "A", 8, 512)
        acc = [P.ps(ph, "A_acc%d" % i, [128, 512], F32) for i in range(4)]
        acc_t = sc.tiles_n("A_acc", 4)
        NS = 3
        stg = [P.sb(ph, "A_stg%d" % i, [128, 2048], BF16) for i in range(NS)]
        stg_t = sc.tiles_n("A_stg", NS)
        stf = [P.sb(ph, "A_stf%d" % i, [128, 4, 32], F32) for i in range(2)]
        stf_t = sc.tiles_n("A_stf", 2)
        G["ga_stage"] = P.sb(ph, "A_gast", [32, 2048], BF16)
        G["ga_stage_t"] = sc.tile("A_gast")
        tiles += wst.tiles + acc_t + stg_t + stf_t + [G["ga_stage_t"]]
        wv = prm["w_in"][l].rearrange("(kc p) n -> p kc n", p=128)
        groups = proj_groups()
        wst.start([(wv[:, :, c0:c0 + n], 8, n) for (c0, n, _m, _k, _d) in groups])
        ai = 0
        si = 0
        ev = 0
        for gi, (c0, n, mode, key, doff) in enumerate(groups):
            wcur, wcur_t = wst.get(gi)
            dst = U[key]
            dst_t = G["dram_t"][key]
            if mode in ("tok", "tok32"):
                for tb in range(4):
                    if mode == "tok":
                        st = si % NS
                        si += 1
                    else:
                        st = tb % 2
                    for j in range(4):
                        i = tb * 4 + j
                        a = ai % 4
                        ai += 1
                        for kc in range(8):
                            sc.op("pe", lambda e, a=a, kc=kc, i=i, wcur=wcur, n=n: e.matmul(
                                acc[a][:, 0:n], lhsT=hT[:, kc, i * 128:(i + 1) * 128], rhs=wcur[:, kc, 0:n],
                                start=(kc == 0), stop=(kc == 7)),
                                reads=[hT_t[i], wcur_t], writes=[acc_t[a]], part=(kc > 0))
                        if mode == "tok":
                            o_ap = stg[st][:, j * 512:j * 512 + n]
                            o_t = stg_t[st]
                        else:
                            o_ap = stf[st][:, j, 0:n]
                            o_t = stf_t[st]
                        ev += 1
                        if ev % 2 == 0:
                            sc.op("act", lambda e, o_ap=o_ap, a=a, n=n: e.copy(out=o_ap, in_=acc[a][:, 0:n]),
                                  reads=[acc_t[a]], writes=[o_t], part=(j > 0))
                        else:
                            sc.op("dve", lambda e, o_ap=o_ap, a=a, n=n: e.tensor_copy(out=o_ap, in_=acc[a][:, 0:n]),
                                  reads=[acc_t[a]], writes=[o_t], part=(j > 0))
                    rows = dst[tb * 512:(tb + 1) * 512, doff:doff + n].rearrange("(j p) c -> p j c", p=128)
                    if mode == "tok":
                        src = stg[st][:].rearrange("p (j c) -> p j c", j=4)[:, :, 0:n]
                        sc.dma("pool", rows, src, owner=stg_t[st], reads=[stg_t[st]], writes=[dst_t], part=True)
                    else:
                        sc.dma("pool", rows, stf[st][:, :, 0:n], owner=stf_t[st], reads=[stf_t[st]], writes=[dst_t],
                               part=True)
            else:
                nchunk = (n + 127) // 128
                for c in range(nchunk):
                    m = min(128, n - c * 128)
                    if mode == "feat":
                        st = si % NS
                        si += 1
                    else:
                        st = 0
                    for tb in range(4):
                        a = ai % 4
                        ai += 1
                        for kc in range(8):
                            sc.op("pe", lambda e, a=a, kc=kc, tb=tb, wcur=wcur, c=c, m=m: e.matmul(
                                acc[a][0:m, :], lhsT=wcur[:, kc, c * 128:c * 128 + m],
                                rhs=hT[:, kc, tb * 512:(tb + 1) * 512], start=(kc == 0), stop=(kc == 7)),
                                reads=hT_t[tb * 4:tb * 4 + 4] + [wcur_t], writes=[acc_t[a]], part=(kc > 0))
                        ev += 1
                        if mode == "feat":
                            o_ap = stg[st][0:m, tb * 512:(tb + 1) * 512]
                            o_t = stg_t[st]
                            if ev % 2 == 0:
                                sc.op("act", lambda e, o_ap=o_ap, a=a, m=m: e.copy(out=o_ap, in_=acc[a][0:m, :]),
                                      reads=[acc_t[a]], writes=[o_t], part=(tb > 0))
                            else:
                                sc.op("dve", lambda e, o_ap=o_ap, a=a, m=m: e.tensor_copy(out=o_ap, in_=acc[a][0:m, :]),
                                      reads=[acc_t[a]], writes=[o_t], part=(tb > 0))
                        else:
                            sc.op("dve", lambda e, a=a, m=m, tb=tb, gast=G["ga_stage"]: e.tensor_copy(
                                out=gast[0:m, tb * 512:(tb + 1) * 512], in_=acc[a][0:m, :]),
                                reads=[acc_t[a]], writes=[G["ga_stage_t"]], part=(tb > 0))
                    if mode == "feat":
                        sc.dma("pool", dst[doff + c * 128:doff + c * 128 + m, :], stg[st][0:m, :], owner=stg_t[st],
                               reads=[stg_t[st]], writes=[dst_t], part=True)
                    else:
                        sc.dma("pool", dst[0:m, :], G["ga_stage"][0:m, :], owner=G["ga_stage_t"],
                               reads=[G["ga_stage_t"]], writes=[dst_t], part=True)
        sc.barrier(release=tiles)


def phase_E(P, sc, G, U, YB, prm, l):
    nc = P.nc
    x = G["x"]
    with contextlib.ExitStack() as ph:
        wst = WStream(P, sc, ph, "E", 8, 256, nf=2, nb=2, cast_eng="act")
        mix = P.sb(ph, "E_mix", [128, 8, 1024], F32)
        mix_t = sc.tiles_n("E_mix", 8)
        mixb = P.sb(ph, "E_mixb", [128, 8, 1024], BF16)
        mixb_t = sc.tile("E_mixb")
        ybT = [P.sb(ph, "E_yb%d" % i, [128, 8, 1024], BF16) for i in range(2)]
        ybT_t = sc.tiles_n("E_yb", 2)
        gsl = [P.sb(ph, "E_g%d" % i, [128, 1024], BF16) for i in range(3)]
        gsl_t = sc.tiles_n("E_g", 3)
        sig = [P.sb(ph, "E_sig%d" % i, [128, 1024], F32) for i in range(2)]
        sig_t = sc.tiles_n("E_sig", 2)
        tmp = [P.sb(ph, "E_tmp%d" % i, [128, 512], F32) for i in range(2)]
        tmp_t = sc.tiles_n("E_tmp", 2)
        acc = [P.ps(ph, "E_acc%d" % i, [128, 512], F32) for i in range(4)]
        acc_t = sc.tiles_n("E_acc", 4)
        tiles = wst.tiles + mix_t + [mixb_t] + ybT_t + gsl_t + sig_t + tmp_t + acc_t
        wnames = ["w_branch_ssd", "w_branch_gla", "w_branch_na", "w_out"]
        bnames = ["ssd", "gla", "na"]
        items = []
        for half in range(2):
            for wn in wnames:
                wv = prm[wn][l].rearrange("(kc p) n -> p kc n", p=128)
                for cg in range(4):
                    items.append((wv[:, :, cg * 256:(cg + 1) * 256], 8, 256))
        wst.start(items)
        gi = 0
        ai = 0
        gcount = 0
        tcount = 0
        ybcount = 0
        for half in range(2):
            t0 = half * 1024
            for b in range(3):
                ys = ybcount % 2
                ybcount += 1
                ybv = YB[bnames[b]].rearrange("(kc p) t -> p kc t", p=128)
                sc.dma("sp", ybT[ys][:], ybv[:, :, t0:t0 + 1024], owner=ybT_t[ys],
                       reads=[G["dram_t"]["yb_" + bnames[b]]], writes=[ybT_t[ys]])
                for cg in range(4):
                    wcur, wcur_t = wst.get(gi)
                    gi += 1
                    for ecl in range(2):
                        ec = cg * 2 + ecl
                        gs = gcount % 3
                        ss_ = gcount % 2
                        gcount += 1
                        grow = b * 1024 + ec * 128
                        sc.dma("sp", gsl[gs][:], U["gate"][grow:grow + 128, t0:t0 + 1024], owner=gsl_t[gs],
                               reads=[G["dram_t"]["gate"]], writes=[gsl_t[gs]])
                        sc.op("act", lambda e, gs=gs, ss_=ss_: e.activation(out=sig[ss_][:], in_=gsl[gs][:],
                                                                            func=AF.Sigmoid),
                              reads=[gsl_t[gs]], writes=[sig_t[ss_]])
                        for tbh in range(2):
                            a = ai % 4
                            ai += 1
                            for kc in range(8):
                                sc.op("pe", lambda e, a=a, kc=kc, wcur=wcur, ecl=ecl, ys=ys, tbh=tbh: e.matmul(
                                    acc[a][:], lhsT=wcur[:, kc, ecl * 128:(ecl + 1) * 128],
                                    rhs=ybT[ys][:, kc, tbh * 512:(tbh + 1) * 512], start=(kc == 0), stop=(kc == 7)),
                                    reads=[wcur_t, ybT_t[ys]], writes=[acc_t[a]], part=(kc > 0))
                            msl = mix[:, ec, tbh * 512:(tbh + 1) * 512]
                            sgl = sig[ss_][:, tbh * 512:(tbh + 1) * 512]
                            if b == 0:
                                sc.op("dve", lambda e, msl=msl, a=a, sgl=sgl: e.tensor_tensor(
                                    out=msl, in0=acc[a][:], in1=sgl, op=ALU.mult),
                                    reads=[acc_t[a], sig_t[ss_]], writes=[mix_t[ec]], part=(tbh > 0))
                            else:
                                ts = tcount % 2
                                tcount += 1
                                sc.op("dve", lambda e, ts=ts, a=a, sgl=sgl: e.tensor_tensor(
                                    out=tmp[ts][:], in0=acc[a][:], in1=sgl, op=ALU.mult),
                                    reads=[acc_t[a], sig_t[ss_]], writes=[tmp_t[ts]])
                                if b == 1:
                                    sc.op("pool", lambda e, msl=msl, ts=ts: e.tensor_tensor(
                                        out=msl, in0=msl, in1=tmp[ts][:], op=ALU.add),
                                        reads=[tmp_t[ts], mix_t[ec]], writes=[mix_t[ec]])
                                else:
                                    sc.op("pool", lambda e, msl=msl, ts=ts, ec=ec, tbh=tbh: e.tensor_tensor(
                                        out=mixb[:, ec, tbh * 512:(tbh + 1) * 512], in0=msl, in1=tmp[ts][:],
                                        op=ALU.add),
                                        reads=[tmp_t[ts], mix_t[ec]], writes=[mixb_t], part=True)
            for cg in range(4):
                wcur, wcur_t = wst.get(gi)
                gi += 1
                for j in range(8):
                    i = half * 8 + j
                    a = ai % 4
                    ai += 1
                    for ec in range(8):
                        sc.op("pe", lambda e, a=a, ec=ec, wcur=wcur, j=j: e.matmul(
                            acc[a][:, 0:256], lhsT=mixb[:, ec, j * 128:(j + 1) * 128], rhs=wcur[:, ec, :],
                            start=(ec == 0), stop=(ec == 7)),
                            reads=[wcur_t, mixb_t], writes=[acc_t[a]], part=(ec > 0))
                    xs = x[:, i, cg * 256:(cg + 1) * 256]
                    sc.op("dve", lambda e, xs=xs, a=a: e.tensor_tensor(out=xs, in0=xs, in1=acc[a][:, 0:256], op=ALU.add),
                          reads=[acc_t[a], G["xt"][i]], writes=[G["xt"][i]])
        sc.barrier(release=tiles)


class Ring:
    def __init__(self, P, sc, stack, name, shape, dt, n, psum=False, views=None):
        if views is not None:
            self.h = views
            n = len(views)
        else:
            mk = P.ps if psum else P.sb
            self.h = [mk(stack, "%s%d" % (name, i), shape, dt) for i in range(n)]
        self.t = sc.tiles_n(name + "_", n)
        self.i = 0
        self.n = n

    def next(self):
        k = self.i % self.n
        self.i += 1
        return self.h[k], self.t[k]


def build_tri(P, sc, G, top):
    for nm in ("trif", "trib", "trif64", "trib64", "mcf64", "mcb64", "trifs", "tribs"):
        G[nm] = P.sb(top, nm, [128, 128], F32)
        G[nm + "_t"] = sc.tile(nm)
    ones_f, ones_t = G["ones_f"], G["ones_t"]
    sc.op("pool", lambda e: e.affine_select(out=G["trif"][:], in_=ones_f[:], pattern=[[1, 128]], compare_op=ALU.is_ge,
                                            fill=0.0, base=0, channel_multiplier=-1),
          reads=[ones_t], writes=[G["trif_t"]])
    sc.op("pool", lambda e: e.affine_select(out=G["trib"][:], in_=ones_f[:], pattern=[[-1, 128]], compare_op=ALU.is_ge,
                                            fill=0.0, base=0, channel_multiplier=1),
          reads=[ones_t], writes=[G["trib_t"]])
    sc.op("pool", lambda e: e.affine_select(out=G["trifs"][:], in_=ones_f[:], pattern=[[1, 128]], compare_op=ALU.is_gt,
                                            fill=0.0, base=0, channel_multiplier=-1),
          reads=[ones_t], writes=[G["trifs_t"]])
    sc.op("pool", lambda e: e.affine_select(out=G["tribs"][:], in_=ones_f[:], pattern=[[-1, 128]], compare_op=ALU.is_gt,
                                            fill=0.0, base=0, channel_multiplier=1),
          reads=[ones_t], writes=[G["tribs_t"]])
    sc.op("pool", lambda e: e.tensor_copy(out=G["trif64"][:], in_=G["trif"][:]), reads=[G["trif_t"]], writes=[G["trif64_t"]])
    sc.op("pool", lambda e: e.memset(G["trif64"][0:64, 64:128], 0.0), reads=[G["trif64_t"]], writes=[G["trif64_t"]])
    sc.op("pool", lambda e: e.tensor_copy(out=G["trib64"][:], in_=G["trib"][:]), reads=[G["trib_t"]], writes=[G["trib64_t"]])
    sc.op("pool", lambda e: e.memset(G["trib64"][64:128, 0:64], 0.0), reads=[G["trib64_t"]], writes=[G["trib64_t"]])
    sc.op("pool", lambda e: e.tensor_scalar(out=G["mcf64"][:], in0=G["trif64"][:], scalar1=-1.0 / 16.0, scalar2=None,
                                            op0=ALU.mult), reads=[G["trif64_t"]], writes=[G["mcf64_t"]])
    sc.op("pool", lambda e: e.tensor_scalar(out=G["mcb64"][:], in0=G["trib64"][:], scalar1=-1.0 / 16.0, scalar2=None,
                                            op0=ALU.mult), reads=[G["trib64_t"]], writes=[G["mcb64_t"]])
    for nm in ("mcf64", "mcb64", "trifs", "tribs"):
        G[nm + "b"] = P.sb(top, nm + "b", [128, 128], BF16)
        G[nm + "b_t"] = sc.tile(nm + "b")
        sc.op("pool", lambda e, nm=nm: e.tensor_copy(out=G[nm + "b"][:], in_=G[nm][:]), reads=[G[nm + "_t"]],
              writes=[G[nm + "b_t"]])


def phase_C(P, sc, G, U, YB, prm, l):
    nc = P.nc
    ident = G["ident"]
    with contextlib.ExitStack() as ph:
        qT = P.sb(ph, "C_qT", [128, 4, S], BF16)
        kT = P.sb(ph, "C_kT", [128, 4, S], BF16)
        qT_t = sc.tile("C_qT")
        kT_t = sc.tile("C_kT")
        ob = P.sb(ph, "C_ob", [128, NT, 1024], BF16)
        ob_t = sc.tiles_n("C_ob", NT)
        gaX = P.sb(ph, "C_gaX", [32, S], BF16)
        gaX_t = sc.tile("C_gaX")
        a2X = [P.sb(ph, "C_a2X%d" % d, [32, 512], BF16) for d in range(2)]
        a2X_t = sc.tiles_n("C_a2X", 2)
        a2f = P.sb(ph, "C_a2f", [32, 512], F32)
        a2f_t = sc.tile("C_a2f")
        nwb = P.sb(ph, "C_nwb", [128, 256], F32)
        nwb_t = sc.tile("C_nwb")
        Sf = P.sb(ph, "C_Sf", [128, 4, 256], F32)
        Sf_t = sc.tile("C_Sf")
        yst = P.sb(ph, "C_yst", [128, 8, 256], BF16)
        yst_t = sc.tile("C_yst")
        R = lambda name, shape, dt, n, psum=False: Ring(P, sc, ph, "C_" + name, shape, dt, n, psum)
        r_Sb = R("Sb", [128, 4, 256], BF16, 3)
        r_v = R("v", [128, 1024], BF16, 2)
        r_gg = R("gg", [128, 4, 1024], BF16, 1)
        r_e1 = R("e1", [128, 512], F32, 1)
        r_gn = R("gn", [128, 512], BF16, 1)
        r_eb = R("eb", [128, 4, 128], F32, 1)
        r_enb = R("enb", [128, 4, 128], F32, 1)
        r_ew = R("ew", [128, 4, 128], F32, 1)
        r_ed = R("ed", [128, 4, 2], F32, 3)
        r_qd = R("qd", [128, 4, 128], BF16, 3)
        r_kd = R("kd", [128, 4, 128], BF16, 2)
        r_kw = R("kw", [128, 4, 128], BF16, 1)
        r_kwt = R("kwt", [128, 4, 128], BF16, 3)
        r_am = R("am", [128, 4, 128], BF16, 3)
        r_oa = R("oa", [128, 1024], F32, 1)
        r_sg = R("sg", [128, 4, 1024], BF16, 1)
        r_jk = R("jk", [128, 256], BF16, 1)
        r_ss = R("ss", [128, 8], F32, 2)
        r_y = R("y", [128, 1024], BF16, 2)
        r_gp = R("gp", [128, 512], F32, 1, True)
        r_bT = R("bT", [128, 4, 128], F32, 1, True)
        r_att = R("att", [128, 4, 128], F32, 1, True)
        r_kwp = R("kwp", [128, 4, 128], BF16, 1, True)
        r_st = R("st", [128, 4, 256], F32, 1, True)
        r_o = R("o", [128, 4, 256], F32, 1, True)
        rings = [r_Sb, r_v, r_gg, r_e1, r_gn, r_eb, r_enb, r_ew, r_ed, r_qd, r_kd, r_kw, r_kwt, r_am, r_oa, r_sg,
                 r_jk, r_ss, r_y, r_gp, r_bT, r_att, r_kwp, r_st, r_o]
        tiles = [qT_t, kT_t, nwb_t, yst_t, gaX_t, Sf_t, a2f_t] + ob_t + a2X_t
        for r in rings:
            tiles += r.t
        sc.dma("sp", qT[:], U["gq"].rearrange("(h p) t -> p h t", p=128), owner=qT_t, reads=[G["dram_t"]["gq"]],
               writes=[qT_t])
        sc.dma("sp", kT[:], U["gk"].rearrange("(h p) t -> p h t", p=128), owner=kT_t, reads=[G["dram_t"]["gk"]],
               writes=[kT_t])
        sc.dma("sp", nwb[:], prm["gla_norm_w"][l].partition_broadcast(128), owner=nwb_t, writes=[nwb_t])
        for d in range(2):
            a2 = prm["gla_a2_f" if d == 0 else "gla_a2_b"][l]
            bi = prm["gla_a2_bias_f" if d == 0 else "gla_a2_bias_b"][l]
            sc.dma("sp", a2f[0:16, :], a2, owner=a2f_t, writes=[a2f_t])
            sc.dma("sp", a2f[16:17, :], bi.rearrange("(o n) -> o n", o=1), owner=a2f_t, writes=[a2f_t], part=True)
            sc.op("act", lambda e, d=d: e.copy(out=a2X[d][0:17, :], in_=a2f[0:17, :]), reads=[a2f_t], writes=[a2X_t[d]])

        def gla_pass(d):
            fwd = (d == 0)
            mc, mc_t = (G["mcf64b"], G["mcf64b_t"]) if fwd else (G["mcb64b"], G["mcb64b_t"])
            ma, ma_t = (G["trif64"], G["trif64_t"]) if fwd else (G["trib64"], G["trib64_t"])
            lc0 = 63 if fwd else 0
            sc.op("pool", lambda e: e.memset(gaX[:], 1.0), writes=[gaX_t])
            sc.dma("sp", gaX[0:16, :], U["ga"][16 * d:16 * d + 16, :], owner=gaX_t, reads=[G["dram_t"]["ga"]],
                   writes=[gaX_t])
            sc.op("pool", lambda e: e.memset(Sf[:], 0.0), writes=[Sf_t])
            sb0, sb0_t = r_Sb.next()
            sc.op("pool", lambda e, sb0=sb0: e.memset(sb0[:], 0.0), writes=[sb0_t])
            cur = [(sb0, sb0_t)]
            sgcur = [None]
            order = list(range(NT)) if fwd else list(range(NT - 1, -1, -1))
            chunks = (0, 1) if fwd else (1, 0)

            def stage1(i):
                tsl = slice(i * 128, (i + 1) * 128)
                v, v_t = r_v.next()
                sc.dma("sp", v[:], U["gv"][tsl, :], owner=v_t, reads=[G["dram_t"]["gv"]], writes=[v_t])
                gp, gp_t = r_gp.next()
                sc.op("pe", lambda e, gp=gp, tsl=tsl: e.matmul(gp[:], lhsT=gaX[0:17, tsl], rhs=a2X[d][0:17, :],
                                                               start=True, stop=True),
                      reads=[gaX_t, a2X_t[d]], writes=[gp_t])
                e1, e1_t = r_e1.next()
                sc.op("act", lambda e, e1=e1, gp=gp: e.activation(out=e1[:], in_=gp[:], func=AF.Exp, scale=-1.0),
                      reads=[gp_t], writes=[e1_t])
                gn, gn_t = r_gn.next()
                sc.op("act", lambda e, gn=gn, e1=e1: e.activation(out=gn[:], in_=e1[:], func=AF.Ln, bias=G["one"][:, 0:1]),
                      reads=[e1_t, G["one_t"]], writes=[gn_t])
                bT, bT_t = r_bT.next()
                for h in range(4):
                    sc.op("pe", lambda e, bT=bT, gn=gn, h=h: e.matmul(bT[:, h, :], lhsT=gn[:, h * 128:(h + 1) * 128], rhs=mc[:],
                                                                     start=True, stop=True, skip_group_check=True),
                          reads=[gn_t, mc_t], writes=[bT_t], part=(h > 0))
                bs, bs_t = bT, bT_t
                eb, eb_t = r_eb.next()
                sc.op("act", lambda e, eb=eb, bs=bs: e.activation(out=eb[:], in_=bs[:], func=AF.Exp), reads=[bs_t], writes=[eb_t])
                enb, enb_t = r_enb.next()
                sc.op("act", lambda e, enb=enb, bs=bs: e.activation(out=enb[:], in_=bs[:], func=AF.Exp, scale=-1.0),
                      reads=[bs_t], writes=[enb_t])
                ed, ed_t = r_ed.next()
                sc.op("act", lambda e, ed=ed, bs=bs: e.activation(
                    out=ed[:], in_=bs[:].rearrange("p h (c l) -> p h c l", c=2)[:, :, :, lc0], func=AF.Exp),
                    reads=[bs_t], writes=[ed_t])
                qd, qd_t = r_qd.next()
                sc.op("dve", lambda e, qd=qd, tsl=tsl, eb=eb: e.scalar_tensor_tensor(
                    out=qd[:], in0=qT[:, :, tsl], scalar=128.0 ** -0.5, in1=eb[:], op0=ALU.mult, op1=ALU.mult),
                    reads=[qT_t, eb_t], writes=[qd_t])
                kd, kd_t = r_kd.next()
                sc.op("dve", lambda e, kd=kd, tsl=tsl, enb=enb: e.tensor_tensor(
                    out=kd[:], in0=kT[:, :, tsl], in1=enb[:], op=ALU.mult), reads=[kT_t, enb_t], writes=[kd_t])
                ew, ew_t = r_ew.next()
                sc.op("dve", lambda e, ew=ew, enb=enb, ed=ed: e.tensor_tensor(
                    out=ew[:].rearrange("p h (c l) -> p (h c) l", c=2), in0=enb[:].rearrange("p h (c l) -> p (h c) l", c=2),
                    in1=ed[:].rearrange("p h c -> p (h c)").unsqueeze(2).to_broadcast([128, 8, 64]), op=ALU.mult),
                    reads=[enb_t, ed_t], writes=[ew_t])
                kw, kw_t = r_kw.next()
                sc.op("dve", lambda e, kw=kw, tsl=tsl, ew=ew: e.tensor_tensor(
                    out=kw[:], in0=kT[:, :, tsl], in1=ew[:], op=ALU.mult), reads=[kT_t, ew_t], writes=[kw_t])
                kwp, kwp_t = r_kwp.next()
                for h in range(4):
                    sc.op("pe", lambda e, kwp=kwp, kw=kw, h=h: e.transpose(out=kwp[:, h, :], in_=kw[:, h, :], identity=ident[:]),
                          reads=[kw_t, G["ident_t"]], writes=[kwp_t], part=(h > 0))
                kwt, kwt_t = r_kwt.next()
                sc.op("act", lambda e, kwt=kwt, kwp=kwp: e.copy(out=kwt[:], in_=kwp[:]), reads=[kwp_t], writes=[kwt_t])
                att, att_t = r_att.next()
                for h in range(4):
                    sc.op("pe", lambda e, att=att, kd=kd, qd=qd, h=h: e.matmul(att[:, h, :], lhsT=kd[:, h, :], rhs=qd[:, h, :],
                                                                            start=True, stop=True, skip_group_check=True),
                          reads=[kd_t, qd_t], writes=[att_t], part=(h > 0))
                am, am_t = r_am.next()
                sc.op("dve", lambda e, am=am, att=att: e.tensor_tensor(
                    out=am[:], in0=att[:], in1=ma[:].unsqueeze(1).to_broadcast([128, 4, 128]), op=ALU.mult),
                    reads=[att_t, ma_t], writes=[am_t])
                return (i, tsl, v, v_t, qd, qd_t, kwt, kwt_t, ed, ed_t, am, am_t)

            def stage23(ctx):
                (i, tsl, v, v_t, qd, qd_t, kwt, kwt_t, ed, ed_t, am, am_t) = ctx
                sbs = [cur[0]]
                for ci, c in enumerate(chunks):
                    cs = slice(c * 64, (c + 1) * 64)
                    st, st_t = r_st.next()
                    for h in range(4):
                        sc.op("pe", lambda e, st=st, kwt=kwt, cs=cs, v=v, h=h: e.matmul(
                            st[:, h, :], lhsT=kwt[cs, h, :], rhs=v[cs, h * 256:(h + 1) * 256], start=True, stop=True,
                            skip_group_check=True),
                            reads=[kwt_t, v_t], writes=[st_t], part=(h > 0))
                    for h in range(4):
                        sc.op("dve", lambda e, st=st, h=h, ed=ed, c=c: e.scalar_tensor_tensor(
                            out=Sf[:, h, :], in0=Sf[:, h, :], scalar=ed[:, h, c:c + 1], in1=st[:, h, :], op0=ALU.mult,
                            op1=ALU.add),
                            reads=[st_t, ed_t, Sf_t], writes=[Sf_t])
                    nb, nb_t = r_Sb.next()
                    sc.op("act", lambda e, nb=nb: e.copy(out=nb[:], in_=Sf[:]), reads=[Sf_t], writes=[nb_t])
                    sbs.append((nb, nb_t))
                o, o_t = r_o.next()
                for h in range(4):
                    sc.op("pe", lambda e, o=o, am=am, v=v, h=h: e.matmul(o[:, h, :], lhsT=am[:, h, :],
                                                                       rhs=v[:, h * 256:(h + 1) * 256],
                                                                       start=True, stop=False, skip_group_check=True),
                          reads=[am_t, v_t], writes=[o_t], part=(h > 0))
                    for ci, c in enumerate(chunks):
                        cs = slice(c * 64, (c + 1) * 64)
                        sbv, sbv_t = sbs[ci]
                        sc.op("pe", lambda e, o=o, qd=qd, cs=cs, sbv=sbv, ci=ci, h=h: e.matmul(
                            o[cs, h, :], lhsT=qd[:, h, cs], rhs=sbv[:, h, :], start=False, stop=(ci == 1),
                            skip_group_check=True),
                            reads=[qd_t, sbv_t], writes=[o_t], part=True)
                cur[0] = sbs[2]
                if not fwd:
                    for hb in range(2):
                        sc.op("act", lambda e, o=o, hb=hb, i=i: e.copy(
                            out=ob[:, i, hb * 512:(hb + 1) * 512], in_=o[:, 2 * hb:2 * hb + 2, :].rearrange("p a b -> p (a b)")),
                            reads=[o_t], writes=[ob_t[i]], part=(hb > 0))
                    return
                oa, oa_t = r_oa.next()
                ss, ss_t = r_ss.next()
                for hb in range(2):
                    sc.op("dve", lambda e, oa=oa, o=o, hb=hb, i=i: e.tensor_tensor(
                        out=oa[:, hb * 512:(hb + 1) * 512], in0=o[:, 2 * hb:2 * hb + 2, :].rearrange("p a b -> p (a b)"),
                        in1=ob[:, i, hb * 512:(hb + 1) * 512], op=ALU.add),
                        reads=[o_t, ob_t[i]], writes=[oa_t], part=(hb > 0))
                for h in range(4):
                    hs = slice(h * 256, (h + 1) * 256)
                    jk, jk_t = r_jk.next()
                    sc.op("dve", lambda e, jk=jk, oa=oa, hs=hs, ss=ss, h=h: e.scalar_tensor_tensor(
                        out=jk[:], in0=oa[:, hs], scalar=1.0, in1=oa[:, hs], op0=ALU.mult, op1=ALU.mult,
                        accum_out=ss[:, h:h + 1]), reads=[oa_t], writes=[jk_t, ss_t])
                if i % 4 == 0:
                    gg, gg_t = r_gg.next()
                    sc.dma("sp", gg[:], U["gg"][i * 128:(i + 4) * 128, :].rearrange("(j p) c -> p j c", p=128), owner=gg_t,
                           reads=[G["dram_t"]["gg"]], writes=[gg_t])
                    sg, sg_t = r_sg.next()
                    sc.op("act", lambda e, sg=sg, gg=gg: e.activation(out=sg[:], in_=gg[:], func=AF.Silu),
                          reads=[gg_t], writes=[sg_t])
                    sgcur[0] = (sg, sg_t)
                sg, sg_t = sgcur[0]
                sgn = sg[:, i % 4, :]
                sgn_t = sg_t
                sc.op("pool", lambda e, sgn=sgn: e.tensor_tensor(
                    out=sgn.rearrange("p (h v) -> p h v", h=4), in0=sgn.rearrange("p (h v) -> p h v", h=4),
                    in1=nwb[:].unsqueeze(1).to_broadcast([128, 4, 256]), op=ALU.mult),
                    reads=[sg_t, nwb_t], writes=[sg_t])
                sc.op("dve", lambda e, ss=ss: e.tensor_scalar(out=ss[:, 4:8], in0=ss[:, 0:4], scalar1=1.0 / 256.0, scalar2=EPS,
                                                              op0=ALU.mult, op1=ALU.add), reads=[ss_t], writes=[ss_t])
                sc.op("pool", lambda e, ss=ss: e.tensor_tensor(out=ss[:, 0:4], in0=ss[:, 4:8], in1=G["neghalf"][:, 0:4],
                                                               op=ALU.pow), reads=[ss_t, G["neghalf_t"]], writes=[ss_t])
                sc.op("dve", lambda e, oa=oa, ss=ss: e.tensor_tensor(
                    out=oa[:].rearrange("p (h v) -> p h v", h=4), in0=oa[:].rearrange("p (h v) -> p h v", h=4),
                    in1=ss[:, 0:4].unsqueeze(2).to_broadcast([128, 4, 256]), op=ALU.mult),
                    reads=[oa_t, ss_t], writes=[oa_t])
                y, y_t = r_y.next()
                sc.op("dve", lambda e, y=y, oa=oa, sgn=sgn: e.tensor_tensor(out=y[:], in0=oa[:], in1=sgn, op=ALU.mult),
                      reads=[oa_t, sgn_t], writes=[y_t])
                return (y, y_t, i)

            def stage3(c3):
                if c3 is None:
                    return
                (y, y_t, i) = c3
                emit_yT(P, sc, G, r_kwp, y, y_t, yst, yst_t, i, YB["gla"], G["dram_t"]["yb_gla"], gsz=2)

            prev = None
            prev3 = None
            for i in order:
                ctx = stage1(i)
                if prev is not None:
                    n3 = stage23(prev)
                    stage3(prev3)
                    prev3 = n3
                prev = ctx
            n3 = stage23(prev)
            stage3(prev3)
            stage3(n3)

        gla_pass(1)
        if "dbg_ob" in P.dbg:
            dob = P.dram("dbg_ob", [S, 1024], BF16)
            dt_ = sc.tile("dbg_ob")
            sc.dma("sp", dob.rearrange("(i p) c -> p i c", p=128), ob[:], owner=ob_t[0], reads=ob_t, writes=[dt_])
        gla_pass(0)
        sc.barrier(release=tiles)


def phase_B(P, sc, G, U, YB, prm, l):
    nc = P.nc
    ident = G["ident"]
    ybw = G["ybw"]
    ybw_t = G["dram_t"]["ybw"]
    with contextlib.ExitStack() as ph:
        xtok = P.sb(ph, "B_xtok", [128, NT, 1280], BF16)
        xtok_t = sc.tiles_n("B_xtok", NT)
        BT = P.sb(ph, "B_BT", [128, 2, S], BF16)
        CT = P.sb(ph, "B_CT", [128, 2, S], BF16)
        BT_t = sc.tiles_n("B_BT", 2)
        CT_t = sc.tiles_n("B_CT", 2)
        dtv = P.sb(ph, "B_dtv", [128, NT, 32], F32)
        av = P.sb(ph, "B_av", [128, NT, 32], F32)
        dtv_t = sc.tile("B_dtv")
        av_t = sc.tile("B_av")
        rows = P.sb(ph, "B_rows", [128, 4, 32], F32)
        rows_t = sc.tile("B_rows")
        nwb = P.sb(ph, "B_nwb", [128, 1024], F32)
        nwb_t = sc.tile("B_nwb")
        tiles = xtok_t + BT_t + CT_t + [dtv_t, av_t, rows_t, nwb_t]
        with contextlib.ExitStack() as s1:
            cwr = P.sb(s1, "B_cwr", [72, 128], F32)
            cwr_t = sc.tile("B_cwr")
            cw = P.sb(s1, "B_cw", [128, 72], F32)
            cw_t = sc.tile("B_cw")
            cwp = P.ps(s1, "B_cwp", [128, 72], F32)
            cwp_t = sc.tile("B_cwp")
            identf = P.sb(s1, "B_identf", [128, 128], F32)
            identf_t = sc.tile("B_identf")
            xc = [P.sb(s1, "B_xc%d" % i, [128, S + 4], BF16) for i in range(2)]
            xc_t = sc.tiles_n("B_xc", 2)
            dg = [P.sb(s1, "B_dg%d" % i, [128, 5, 128], BF16) for i in range(2)]
            dg_t = sc.tiles_n("B_dg", 2)
            cacc = [P.ps(s1, "B_cacc%d" % i, [128, 512], F32) for i in range(2)]
            cacc_t = sc.tiles_n("B_cacc", 2)
            xa = [P.sb(s1, "B_xa%d" % i, [128, S], BF16) for i in range(2)]
            xa_t = sc.tiles_n("B_xa", 2)
            tp = [P.ps(s1, "B_tp%d" % i, [128, 4, 128], BF16) for i in range(2)]
            tp_t = sc.tiles_n("B_tp", 2)
            tl1 = [cwr_t, cw_t, cwp_t, identf_t] + dg_t + cacc_t + xc_t + xa_t + tp_t
            sc.op("pool", lambda e: e.affine_select(out=identf[:], in_=G["ones_f"][:], pattern=[[-1, 128]],
                                                    compare_op=ALU.is_equal, fill=0.0, base=0, channel_multiplier=1),
                  reads=[G["ones_t"]], writes=[identf_t])
            sc.dma("sp", cwr[0:60, :], prm["ssd_conv_w"][l].rearrange("k (c p) -> (k c) p", p=128), owner=cwr_t, writes=[cwr_t])
            sc.dma("sp", cwr[60:72, :], prm["ssd_conv_b"][l].rearrange("(c p) -> c p", p=128), owner=cwr_t, writes=[cwr_t],
                   part=True)
            sc.op("pe", lambda e: e.transpose(out=cwp[:], in_=cwr[:], identity=identf[0:72, 0:72]),
                  reads=[cwr_t, identf_t], writes=[cwp_t])
            sc.op("act", lambda e: e.copy(out=cw[:], in_=cwp[:]), reads=[cwp_t], writes=[cw_t])
            for b in range(2):
                sc.op("pool", lambda e, b=b: e.memset(xc[b][:, 0:2], 0.0), writes=[xc_t[b]])
                sc.op("pool", lambda e, b=b: e.memset(xc[b][:, S + 2:S + 4], 0.0), writes=[xc_t[b]], part=True)
            sc.dma("sp", dtv[:], U["dt"].rearrange("(i p) c -> p i c", p=128), owner=dtv_t, reads=[G["dram_t"]["dt"]],
                   writes=[dtv_t])
            for k, nm in enumerate(("ssd_dt_bias_f", "ssd_dt_bias_b")):
                sc.dma("sp", rows[:, 0, 16 * k:16 * k + 16], prm[nm][l].partition_broadcast(128), owner=rows_t,
                       writes=[rows_t], part=True)
            for k, nm in enumerate(("ssd_a_log_f", "ssd_a_log_b")):
                sc.dma("sp", rows[:, 1, 16 * k:16 * k + 16], prm[nm][l].partition_broadcast(128), owner=rows_t,
                       writes=[rows_t], part=True)
            sc.dma("sp", rows[:, 2, 0:16], prm["ssd_d"][l].partition_broadcast(128), owner=rows_t, writes=[rows_t], part=True)
            sc.dma("sp", nwb[:], prm["ssd_norm_w"][l].partition_broadcast(128), owner=nwb_t, writes=[nwb_t])
            sc.op("dve", lambda e: e.tensor_tensor(out=dtv[:], in0=dtv[:], in1=rows[:, 0:1, :].to_broadcast([128, NT, 32]),
                                                   op=ALU.add), reads=[dtv_t, rows_t], writes=[dtv_t])
            sc.op("act", lambda e: e.activation(out=dtv[:], in_=dtv[:], func=AF.Exp), reads=[dtv_t], writes=[dtv_t])
            sc.op("act", lambda e: e.activation(out=dtv[:], in_=dtv[:], func=AF.Ln, bias=G["one"][:, 0:1]),
                  reads=[dtv_t, G["one_t"]], writes=[dtv_t])
            sc.op("act", lambda e: e.activation(out=rows[:, 3, :], in_=rows[:, 1, :], func=AF.Exp), reads=[rows_t],
                  writes=[rows_t])
            sc.op("dve", lambda e: e.scalar_tensor_tensor(out=av[:], in0=dtv[:], scalar=-1.0,
                                                          in1=rows[:, 3:4, :].to_broadcast([128, NT, 32]),
                                                          op0=ALU.mult, op1=ALU.mult),
                  reads=[dtv_t, rows_t], writes=[av_t])
            tpc = 0
            for c in range(12):
                b = c % 2
                sc.dma("sp", xc[b][:, 2:S + 2], U["xbc"][c * 128:(c + 1) * 128, :], owner=xc_t[b],
                       reads=[G["dram_t"]["xbc"]], writes=[xc_t[b]], part=True)
                dgb = c % 2
                for k in range(5):
                    sc.op("dve", lambda e, dgb=dgb, k=k, c=c: e.tensor_scalar(
                        out=dg[dgb][:, k, :], in0=identf[:], scalar1=cw[:, k * 12 + c:k * 12 + c + 1], scalar2=None,
                        op0=ALU.mult), reads=[identf_t, cw_t], writes=[dg_t[dgb]], part=(k > 0))
                if c < 10:
                    xo, xo_t = xa[b], xa_t[b]
                    xsl = lambda tb: xa[b][:, tb * 512:(tb + 1) * 512]
                else:
                    xo_t = CT_t[c - 10]
                    xsl = lambda tb, c=c: CT[:, c - 10, tb * 512:(tb + 1) * 512]
                for tb in range(4):
                    ca, ca_t = cacc[(4 * c + tb) % 2], cacc_t[(4 * c + tb) % 2]
                    for k in range(5):
                        sc.op("pe", lambda e, ca=ca, dgb=dgb, k=k, b=b, tb=tb: e.matmul(
                            ca[:], lhsT=dg[dgb][:, k, :], rhs=xc[b][:, k + tb * 512:k + tb * 512 + 512],
                            start=(k == 0), stop=(k == 4)),
                            reads=[dg_t[dgb], xc_t[b]], writes=[ca_t], part=(k > 0))
                    sc.op("act", lambda e, ca=ca, o_ap=xsl(tb), c=c: e.activation(
                        out=o_ap, in_=ca[:], func=AF.Silu, bias=cw[:, 60 + c:61 + c]),
                        reads=[ca_t, cw_t], writes=[xo_t], part=(tb > 0))
                if c in (8, 9):
                    sc.op("pool", lambda e, b=b, c=c: e.tensor_copy(out=BT[:, c - 8, :], in_=xa[b][:]),
                          reads=[xa_t[b]], writes=[BT_t[c - 8]])
                if c < 10:
                    for i0 in range(0, NT, 4):
                        tb_ = tpc % 2
                        tpc += 1
                        for j in range(4):
                            i = i0 + j
                            sc.op("pe", lambda e, tb_=tb_, j=j, b=b, i=i: e.transpose(
                                out=tp[tb_][:, j, :], in_=xa[b][:, i * 128:(i + 1) * 128], identity=ident[:]),
                                reads=[xa_t[b], G["ident_t"]], writes=[tp_t[tb_]], part=(j > 0))
                        eng = "act" if (tpc % 2) else "pool"
                        if eng == "act":
                            sc.op("act", lambda e, tb_=tb_, i0=i0, c=c: e.copy(
                                out=xtok[:, i0:i0 + 4, c * 128:(c + 1) * 128], in_=tp[tb_][:]),
                                reads=[tp_t[tb_]], writes=xtok_t[i0:i0 + 4], part=True)
                        else:
                            sc.op("dve", lambda e, tb_=tb_, i0=i0, c=c: e.tensor_copy(
                                out=xtok[:, i0:i0 + 4, c * 128:(c + 1) * 128], in_=tp[tb_][:]),
                                reads=[tp_t[tb_]], writes=xtok_t[i0:i0 + 4], part=True)
            sc.barrier(release=tl1)
        with contextlib.ExitStack() as s2:
            R = lambda name, shape, dt, n, psum=False: Ring(P, sc, s2, "B_" + name, shape, dt, n, psum)
            Sf = P.sb(s2, "B_Sf", [128, 2, 512], F32)
            Sf_t = sc.tiles_n("B_Sf", 2)
            Sbx = P.sb(s2, "B_Sb", [128, 2, 2, 512], BF16)
            r_Sb = [Ring(P, sc, s2, "B_Sb%d" % g, None, None, 2, views=[Sbx[:, g, k, :] for k in range(2)]) for g in range(2)]
            r_cb = R("cb", [128, 128], F32, 1, True)
            r_seg = R("seg", [128, 512], F32, 2, True)
            r_sm = R("sm", [128, 3, 16], F32, 1, True)
            r_yd = R("yd", [128, 512], F32, 1, True)
            r_stp = R("stp", [128, 512], F32, 1, True)
            r_yo = R("yo", [128, 512], F32, 1, True)
            r_tp = R("tp2", [128, 4, 128], BF16, 1, True)
            r_cbm = R("cbm", [128, 128], F32, 2)
            r_am = R("am", [128, 4, 128], BF16, 4)
            r_dec = R("dec", [128, 4, 128], F32, 2)
            r_mt = R("mt", [128, 4, 128], BF16, 4)
            r_ea = R("ea", [128, 3, 16], F32, 2)
            r_xdt = R("xdt", [128, 1024], BF16, 3)
            r_xw = R("xw", [128, 1024], BF16, 2)
            r_t = R("t", [128, 512], F32, 2)
            r_ybl = R("ybl", [128, 1024], BF16, 2)
            r_yf = R("yf", [128, 1024], F32, 1)
            r_z = R("z", [128, 4, 1024], BF16, 1)
            r_jk = R("jk", [128, 512], F32, 1)
            r_ss = R("ss", [128, 4], F32, 2)
            r_y = R("y", [128, 1024], BF16, 2)
            yst = P.sb(s2, "B_yst", [128, 8, 256], BF16)
            yst_t = sc.tile("B_yst")
            rings = [r_cb, r_seg, r_sm, r_yd, r_stp, r_yo, r_tp, r_cbm, r_am, r_dec, r_mt, r_ea, r_xdt, r_xw, r_t, r_ybl,
                     r_yf, r_z, r_jk, r_ss, r_y] + r_Sb
            tl2 = Sf_t + [yst_t]
            for r in rings:
                tl2 += r.t

            def ssd_pass(d):
                fwd = (d == 0)
                tri_in, tri_in_t = (G["trif"], G["trif_t"]) if fwd else (G["trib"], G["trib_t"])
                tri_st, tri_st_t = (G["tribs"], G["tribs_t"]) if fwd else (G["trifs"], G["trifs_t"])
                tri_sb, tri_sb_t = (G["tribsb"], G["tribsb_t"]) if fwd else (G["trifsb"], G["trifsb_t"])
                cur = []
                for g in range(2):
                    sc.op("pool", lambda e, g=g: e.memset(Sf[:, g, :], 0.0), writes=[Sf_t[g]])
                    sb0, sb0_t = r_Sb[g].next()
                    sc.op("pool", lambda e, sb0=sb0: e.memset(sb0, 0.0), writes=[sb0_t])
                    cur.append((sb0, sb0_t))
                order = list(range(NT)) if fwd else list(range(NT - 1, -1, -1))
                zcur = [None]

                def tileA(i):
                    tsl = slice(i * 128, (i + 1) * 128)
                    acol = av[:, i, 16 * d:16 * d + 16]
                    ams = []
                    for u in range(4):
                        h0 = u * 4
                        am, am_t = r_am.next()
                        for hh in range(4):
                            sc.op("act", lambda e, am=am, i=i, h0=h0, hh=hh: e.activation(
                                out=am[:, hh, :], in_=tri_in[:], func=AF.Copy,
                                scale=av[:, i, 16 * d + h0 + hh:16 * d + h0 + hh + 1]),
                                reads=[tri_in_t, av_t], writes=[am_t], part=(hh > 0))
                        ams.append((am, am_t))
                    sm, sm_t = r_sm.next()
                    sc.op("pe", lambda e, sm=sm, acol=acol: e.matmul(sm[:, 0, :], lhsT=tri_in[:], rhs=acol, start=True, stop=True),
                          reads=[tri_in_t, av_t], writes=[sm_t])
                    sc.op("pe", lambda e, sm=sm, acol=acol: e.matmul(sm[:, 1, :], lhsT=tri_st[:], rhs=acol, start=True, stop=True),
                          reads=[tri_st_t, av_t], writes=[sm_t], part=True)
                    sc.op("pe", lambda e, sm=sm, acol=acol: e.matmul(sm[:, 2, :], lhsT=G["ones_f"][:], rhs=acol, start=True,
                                                                     stop=True),
                          reads=[G["ones_t"], av_t], writes=[sm_t], part=True)
                    ea, ea_t = r_ea.next()
                    sc.op("act", lambda e, ea=ea, sm=sm: e.activation(out=ea[:], in_=sm[:], func=AF.Exp), reads=[sm_t],
                          writes=[ea_t])
                    xdt, xdt_t = r_xdt.next()
                    sc.op("dve", lambda e, xdt=xdt, i=i: e.tensor_tensor(
                        out=xdt[:].rearrange("p (h q) -> p h q", q=64), in0=xtok[:, i, 0:1024].rearrange("p (h q) -> p h q", q=64),
                        in1=dtv[:, i, 16 * d:16 * d + 16].unsqueeze(2).to_broadcast([128, 16, 64]), op=ALU.mult),
                        reads=[xtok_t[i], dtv_t], writes=[xdt_t])
                    xw, xw_t = r_xw.next()
                    sc.op("dve", lambda e, xw=xw, xdt=xdt, ea=ea: e.tensor_tensor(
                        out=xw[:].rearrange("p (h q) -> p h q", q=64), in0=xdt[:].rearrange("p (h q) -> p h q", q=64),
                        in1=ea[:, 1, :].unsqueeze(2).to_broadcast([128, 16, 64]), op=ALU.mult),
                        reads=[xdt_t, ea_t], writes=[xw_t])
                    ybl, ybl_t = r_ybl.next()
                    if fwd:
                        sc.dma("sp", ybl[:], ybw[tsl, :], owner=ybl_t, reads=[ybw_t], writes=[ybl_t])
                        yf, yf_t = r_yf.next()
                    ts = []
                    for g in range(2):
                        stp, stp_t = r_stp.next()
                        sc.op("pe", lambda e, stp=stp, i=i, g=g, xw=xw: e.matmul(
                            stp[:], lhsT=xtok[:, i, 1024 + g * 128:1024 + (g + 1) * 128], rhs=xw[:, g * 512:(g + 1) * 512],
                            start=True, stop=True), reads=[xtok_t[i], xw_t], writes=[stp_t])
                        yo, yo_t = r_yo.next()
                        sbv, sbv_t = cur[g]
                        sc.op("pe", lambda e, yo=yo, g=g, tsl=tsl, sbv=sbv: e.matmul(yo[:], lhsT=CT[:, g, tsl], rhs=sbv,
                                                                                    start=True, stop=True),
                              reads=[CT_t[g], sbv_t], writes=[yo_t])
                        sc.op("pool", lambda e, g=g, ea=ea: e.tensor_tensor(
                            out=Sf[:, g, :].rearrange("p (h q) -> p h q", q=64), in0=Sf[:, g, :].rearrange("p (h q) -> p h q", q=64),
                            in1=ea[:, 2, g * 8:(g + 1) * 8].unsqueeze(2).to_broadcast([128, 8, 64]), op=ALU.mult),
                            reads=[Sf_t[g], ea_t], writes=[Sf_t[g]])
                        sc.op("dve", lambda e, g=g, stp=stp: e.tensor_tensor(out=Sf[:, g, :], in0=Sf[:, g, :], in1=stp[:],
                                                                            op=ALU.add),
                              reads=[Sf_t[g], stp_t], writes=[Sf_t[g]])
                        nb, nb_t = r_Sb[g].next()
                        sc.op("act", lambda e, nb=nb, g=g: e.copy(out=nb, in_=Sf[:, g, :]), reads=[Sf_t[g]], writes=[nb_t])
                        cur[g] = (nb, nb_t)
                        t, t_t = r_t.next()
                        sc.op("dve", lambda e, t=t, yo=yo, ea=ea, g=g: e.tensor_tensor(
                            out=t[:].rearrange("p (h q) -> p h q", q=64), in0=yo[:].rearrange("p (h q) -> p h q", q=64),
                            in1=ea[:, 0, g * 8:(g + 1) * 8].unsqueeze(2).to_broadcast([128, 8, 64]), op=ALU.mult),
                            reads=[yo_t, ea_t], writes=[t_t])
                        ts.append((t, t_t))
                    cbms = []
                    for g in range(2):
                        cb, cb_t = r_cb.next()
                        sc.op("pe", lambda e, cb=cb, g=g, tsl=tsl: e.matmul(cb[:], lhsT=BT[:, g, tsl], rhs=CT[:, g, tsl],
                                                                           start=True, stop=True),
                              reads=[BT_t[g], CT_t[g]], writes=[cb_t])
                        cbm, cbm_t = r_cbm.next()
                        sc.op("dve", lambda e, cbm=cbm, cb=cb: e.tensor_tensor(out=cbm[:], in0=cb[:], in1=tri_in[:], op=ALU.mult),
                              reads=[cb_t, tri_in_t], writes=[cbm_t])
                        cbms.append((cbm, cbm_t))
                    mts = []
                    for pair in range(2):
                        segs = []
                        for u in (2 * pair, 2 * pair + 1):
                            am, am_t = ams[u]
                            seg, seg_t = r_seg.next()
                            sc.op("pe", lambda e, seg=seg, am=am: e.matmul(seg[:], lhsT=tri_sb[:],
                                                                           rhs=am[:].rearrange("p a b -> p (a b)"),
                                                                           start=True, stop=True),
                                  reads=[tri_sb_t, am_t], writes=[seg_t])
                            segs.append((seg, seg_t))
                        decs = []
                        for (seg, seg_t) in segs:
                            dec, dec_t = r_dec.next()
                            sc.op("act", lambda e, dec=dec, seg=seg: e.activation(out=dec[:].rearrange("p a b -> p (a b)"),
                                                                                 in_=seg[:], func=AF.Exp),
                                  reads=[seg_t], writes=[dec_t])
                            decs.append((dec, dec_t))
                        for k, (dec, dec_t) in enumerate(decs):
                            u = 2 * pair + k
                            cbm, cbm_t = cbms[u // 2]
                            mt, mt_t = r_mt.next()
                            sc.op("dve", lambda e, mt=mt, dec=dec, cbm=cbm: e.tensor_tensor(
                                out=mt[:], in0=dec[:], in1=cbm[:].unsqueeze(1).to_broadcast([128, 4, 128]), op=ALU.mult),
                                reads=[dec_t, cbm_t], writes=[mt_t])
                            mts.append((mt, mt_t))
                    for g in range(2):
                        yd, yd_t = r_yd.next()
                        if fwd:
                            sc.op("pe", lambda e, yd=yd, ybl=ybl, g=g: e.matmul(
                                yd[:], lhsT=ident[:], rhs=ybl[:, g * 512:(g + 1) * 512], start=True, stop=False,
                                skip_group_check=True), reads=[G["ident_t"], ybl_t], writes=[yd_t])
                        for q4 in range(2):
                            mt, mt_t = mts[g * 2 + q4]
                            for hh in range(4):
                                h = g * 8 + q4 * 4 + hh
                                hl = h - g * 8
                                sc.op("pe", lambda e, yd=yd, mt=mt, hh=hh, hl=hl, h=h, xdt=xdt: e.matmul(
                                    yd[:, hl * 64:(hl + 1) * 64], lhsT=mt[:, hh, :], rhs=xdt[:, h * 64:(h + 1) * 64],
                                    start=(not fwd), stop=True, skip_group_check=True),
                                    reads=[mt_t, xdt_t], writes=[yd_t], part=(fwd or not (q4 == 0 and hh == 0)))
                        t, t_t = ts[g]
                        gs = slice(g * 512, (g + 1) * 512)
                        if not fwd:
                            sc.op("dve", lambda e, t=t, yd=yd, ybl=ybl, gs=gs: e.tensor_tensor(out=ybl[:, gs], in0=t[:], in1=yd[:],
                                                                                              op=ALU.add),
                                  reads=[t_t, yd_t], writes=[ybl_t], part=(g > 0))
                        else:
                            sc.op("dve", lambda e, t=t, yd=yd, yf=yf, gs=gs: e.tensor_tensor(out=yf[:, gs], in0=t[:], in1=yd[:],
                                                                                            op=ALU.add),
                                  reads=[t_t, yd_t], writes=[yf_t], part=(g > 0))
                    if not fwd:
                        sc.dma("pool", ybw[tsl, :], ybl[:], owner=ybl_t, reads=[ybl_t], writes=[ybw_t], part=True)
                        return None
                    if i % 4 == 0:
                        z, z_t = r_z.next()
                        sc.dma("sp", z[:], U["z"][i * 128:(i + 4) * 128, :].rearrange("(j p) c -> p j c", p=128), owner=z_t,
                               reads=[G["dram_t"]["z"]], writes=[z_t])
                        sc.op("act", lambda e, z=z: e.activation(out=z[:], in_=z[:], func=AF.Silu), reads=[z_t], writes=[z_t])
                        zcur[0] = (z, z_t)
                    z, z_t = zcur[0]
                    sz = z[:, i % 4, :]
                    sz_t = z_t
                    xd, xd_t = r_xdt.next()
                    sc.op("pool", lambda e, xd=xd, i=i: e.tensor_tensor(
                        out=xd[:].rearrange("p (h q) -> p h q", q=64), in0=xtok[:, i, 0:1024].rearrange("p (h q) -> p h q", q=64),
                        in1=rows[:, 2, 0:16].unsqueeze(2).to_broadcast([128, 16, 64]), op=ALU.mult),
                        reads=[xtok_t[i], rows_t], writes=[xd_t])
                    sc.op("dve", lambda e, yf=yf, xd=xd: e.tensor_tensor(out=yf[:], in0=yf[:], in1=xd[:], op=ALU.add),
                          reads=[yf_t, xd_t], writes=[yf_t])
                    sc.op("dve", lambda e, yf=yf, sz=sz: e.tensor_tensor(out=yf[:], in0=yf[:], in1=sz, op=ALU.mult),
                          reads=[yf_t, sz_t], writes=[yf_t])
                    ss, ss_t = r_ss.next()
                    for g in range(2):
                        gs = slice(g * 512, (g + 1) * 512)
                        jk, jk_t = r_jk.next()
                        sc.op("dve", lambda e, jk=jk, yf=yf, gs=gs, ss=ss, g=g: e.scalar_tensor_tensor(
                            out=jk[:], in0=yf[:, gs], scalar=1.0, in1=yf[:, gs], op0=ALU.mult, op1=ALU.mult,
                            accum_out=ss[:, g:g + 1]), reads=[yf_t], writes=[jk_t, ss_t])
                    sc.op("dve", lambda e, ss=ss: e.tensor_scalar(out=ss[:, 2:4], in0=ss[:, 0:2], scalar1=1.0 / 512.0, scalar2=EPS,
                                                                  op0=ALU.mult, op1=ALU.add), reads=[ss_t], writes=[ss_t])
                    sc.op("pool", lambda e, ss=ss: e.tensor_tensor(out=ss[:, 0:2], in0=ss[:, 2:4], in1=G["neghalf"][:, 0:2],
                                                                   op=ALU.pow), reads=[ss_t, G["neghalf_t"]], writes=[ss_t])
                    y, y_t = r_y.next()
                    for g in range(2):
                        gs = slice(g * 512, (g + 1) * 512)
                        sc.op("dve", lambda e, y=y, yf=yf, gs=gs, ss=ss, g=g: e.scalar_tensor_tensor(
                            out=y[:, gs], in0=yf[:, gs], scalar=ss[:, g:g + 1], in1=nwb[:, gs], op0=ALU.mult, op1=ALU.mult),
                            reads=[yf_t, ss_t, nwb_t], writes=[y_t], part=(g > 0))
                    return (y, y_t, i)

                def tileC(c3):
                    if c3 is None:
                        return
                    (y, y_t, i) = c3
                    emit_yT(P, sc, G, r_tp, y, y_t, yst, yst_t, i, YB["ssd"], G["dram_t"]["yb_ssd"], gsz=2)

                prev3 = None
                for i in order:
                    n3 = tileA(i)
                    tileC(prev3)
                    prev3 = n3
                tileC(prev3)

            ssd_pass(1)
            ssd_pass(0)
            sc.barrier(release=tl2)
        sc.barrier(release=tiles)


def emit_yT(P, sc, G, r_tp, y, y_t, yst, yst_t, i, dst, dst_t, gsz=4):
    ident = G["ident"]
    for half in range(2):
        tp, tp_t = r_tp.next()
        for jq in range(4):
            c = half * 4 + jq
            sc.op("pe", lambda e, tp=tp, jq=jq, c=c: e.transpose(out=tp[:, jq, :], in_=y[:, c * 128:(c + 1) * 128],
                                                               identity=ident[:]),
                  reads=[y_t, G["ident_t"]], writes=[tp_t], part=(jq > 0))
        sc.op("act", lambda e, tp=tp, half=half: e.copy(
            out=yst[:, half * 4:half * 4 + 4, (i % gsz) * 128:(i % gsz + 1) * 128], in_=tp[:]),
            reads=[tp_t], writes=[yst_t], part=not (i % gsz == 0 and half == 0))
    if i % gsz == gsz - 1:
        yv = dst.rearrange("(c p) t -> p c t", p=128)
        sc.dma("pool", yv[:, :, (i - gsz + 1) * 128:(i + 1) * 128], yst[:], owner=yst_t, reads=[yst_t], writes=[dst_t],
               part=True)


NEG = -30000.0


def na_r0(r):
    return min(max(r - 4, 0), 24)


def na_valid(kr, qr):
    return na_r0(qr) <= kr < na_r0(qr) + 8


def phase_D(P, sc, G, U, YB, prm, natt, l):
    nc = P.nc
    with contextlib.ExitStack() as ph:
        qnT = P.sb(ph, "D_qnT", [128, 8, S], BF16)
        knT = P.sb(ph, "D_knT", [128, 8, S], BF16)
        qn_t = sc.tiles_n("D_qn", 8)
        kn_t = sc.tiles_n("D_kn", 8)
        TT = P.sb(ph, "D_TT", [128, 8, 20, 64], BF16)
        TT_t = sc.tiles_n("D_TT", 4)
        wcol = P.sb(ph, "D_wcol", [128, 4], F32)
        wcol_t = sc.tile("D_wcol")
        tiles = qn_t + kn_t + TT_t + [wcol_t]
        with contextlib.ExitStack() as s1:
            TTf = [P.sb(s1, "D_TTf%d" % i, [128, 2, 20, 64], F32) for i in range(1)]
            TTf_t = sc.tiles_n("D_TTf", 1)
            qc_ = [P.sb(s1, "D_qc%d" % i, [128, S], BF16) for i in range(3)]
            qc_t = sc.tiles_n("D_qc", 3)
            sq = [P.sb(s1, "D_sq%d" % i, [128, S], BF16) for i in range(2)]
            sq_t = sc.tiles_n("D_sq", 2)
            lnv = [P.sb(s1, "D_ln%d" % i, [128, S], F32) for i in range(2)]
            lnv_t = sc.tiles_n("D_ln", 2)
            bones = P.sb(s1, "D_bones", [128, 128], BF16)
            bones_t = sc.tile("D_bones")
            ssp = [P.ps(s1, "D_ssp%d" % i, [128, S], F32) for i in range(2)]
            ssp_t = sc.tiles_n("D_ssp", 2)
            t1 = TTf_t + qc_t + sq_t + lnv_t + [bones_t] + ssp_t
            def tt_piece(g):
                b = 0
                sc.dma("sp", TTf[b][:], natt[l][:, 2 * g:2 * g + 2, :, :], owner=TTf_t[b], writes=[TTf_t[b]])
                sc.op("pool", lambda e, b=b, g=g: e.tensor_copy(out=TT[:, 2 * g:2 * g + 2, :, :], in_=TTf[b][:]),
                      reads=[TTf_t[b]], writes=[TT_t[g]])

            for hh in range(2):
                sc.dma("sp", wcol[hh * 64:(hh + 1) * 64, 2:3], prm["na_q_norm_w"][l].rearrange("(d o) -> d o", o=1),
                       owner=wcol_t, writes=[wcol_t], part=True)
                sc.dma("sp", wcol[hh * 64:(hh + 1) * 64, 1:2], prm["na_k_norm_w"][l].rearrange("(d o) -> d o", o=1),
                       owner=wcol_t, writes=[wcol_t], part=True)
            sc.op("dve", lambda e: e.tensor_scalar(out=wcol[:, 0:1], in0=wcol[:, 2:3], scalar1=0.125, scalar2=None,
                                                   op0=ALU.mult), reads=[wcol_t], writes=[wcol_t])
            sc.op("pool", lambda e: e.memset(bones[:], 0.0), writes=[bones_t])
            sc.op("pool", lambda e: e.memset(bones[0:64, 0:64], 1.0), reads=[bones_t], writes=[bones_t])
            sc.op("pool", lambda e: e.memset(bones[64:128, 64:128], 1.0), reads=[bones_t], writes=[bones_t])
            jobs = []
            for which, (src, dstT, dst_t, wc) in enumerate(((U["nq"], qnT, qn_t, 0), (U["nk"], knT, kn_t, 1))):
                src_t = G["dram_t"]["nq" if which == 0 else "nk"]
                for c in range(8):
                    jobs.append((src, src_t, dstT, dst_t, wc, c))

            def n_s1(k):
                (src, src_t, dstT, dst_t, wc, c) = jobs[k]
                cb = k % 2
                q3 = k % 3
                sc.dma("sp", qc_[q3][:], src[c * 128:(c + 1) * 128, :], owner=qc_t[q3], reads=[src_t], writes=[qc_t[q3]])
                sc.op("dve", lambda e, cb=cb, q3=q3: e.tensor_tensor(out=sq[cb][:], in0=qc_[q3][:], in1=qc_[q3][:], op=ALU.mult),
                      reads=[qc_t[q3]], writes=[sq_t[cb]])
                for tb in range(4):
                    sl = slice(tb * 512, (tb + 1) * 512)
                    sc.op("pe", lambda e, cb=cb, sl=sl: e.matmul(ssp[cb][:, sl], lhsT=bones[:], rhs=sq[cb][:, sl], start=True,
                                                                 stop=True),
                          reads=[bones_t, sq_t[cb]], writes=[ssp_t[cb]], part=(tb > 0))
                sc.op("act", lambda e, cb=cb: e.activation(out=lnv[cb][:], in_=ssp[cb][:], func=AF.Ln,
                                                           bias=G["eps"][:, 0:1], scale=1.0 / 64.0),
                      reads=[ssp_t[cb], G["eps_t"]], writes=[lnv_t[cb]])
                sc.op("act", lambda e, cb=cb: e.activation(out=lnv[cb][:], in_=lnv[cb][:], func=AF.Exp, scale=-0.5),
                      reads=[lnv_t[cb]], writes=[lnv_t[cb]])

            def n_s2(k):
                (src, src_t, dstT, dst_t, wc, c) = jobs[k]
                cb = k % 2
                q3 = k % 3
                sc.op("dve", lambda e, cb=cb, q3=q3, dstT=dstT, c=c, wc=wc: e.scalar_tensor_tensor(
                    out=dstT[:, c, :], in0=qc_[q3][:], scalar=wcol[:, wc:wc + 1], in1=lnv[cb][:],
                    op0=ALU.mult, op1=ALU.mult),
                    reads=[qc_t[q3], lnv_t[cb], wcol_t], writes=[dst_t[c]])

            n_s1(0)
            for k in range(len(jobs)):
                if k + 1 < len(jobs):
                    n_s1(k + 1)
                n_s2(k)
                if k % 4 == 1:
                    tt_piece(k // 4)
            sc.barrier(release=t1)
        with contextlib.ExitStack() as s2:
            vx = P.sb(s2, "D_vx", [128, NT, 16, 65], BF16)
            vx_t = sc.tiles_n("D_vx", NT)
            sps = [P.ps(s2, "D_sps%d" % i, [128, 8, 128], F32) for i in range(2)]
            sps_t = sc.tiles_n("D_sps", 2)
            pT = [P.sb(s2, "D_pT%d" % i, [128, 5, 128], BF16) for i in range(3)]
            pT_t = sc.tiles_n("D_pT", 3)
            po = [P.ps(s2, "D_po%d" % i, [128, 2, 66], F32) for i in range(2)]
            po_t = sc.tiles_n("D_po", 2)
            rc = [P.sb(s2, "D_rc%d" % i, [128, 2], F32) for i in range(2)]
            rc_t = sc.tiles_n("D_rc", 2)
            ot = [P.sb(s2, "D_ot%d" % i, [128, 1024], BF16) for i in range(2)]
            ot_t = sc.tiles_n("D_ot", 2)
            tp = [P.ps(s2, "D_tp%d" % i, [128, 4, 128], BF16) for i in range(2)]
            tp_t = sc.tiles_n("D_tp", 2)
            yst = P.sb(s2, "D_yst", [128, 8, 512], BF16)
            yst_t = sc.tile("D_yst")
            t2 = vx_t + sps_t + pT_t + po_t + rc_t + ot_t + tp_t + [yst_t]
            nvv = U["nv"].rearrange("(i p) (h d) -> p i h d", p=128, d=64)
            for i in range(NT):
                sc.op("pool", lambda e, i=i: e.memset(vx[:, i, :, 64:65], 1.0), writes=[vx_t[i]])
                sc.dma("sp", vx[:, i, :, 0:64], nvv[:, i, :, :], owner=vx_t[i], reads=[G["dram_t"]["nv"]],
                       writes=[vx_t[i]], part=True)
            ident = G["ident"]
            tpc = [0]
            units = []
            for i in range(NT):
                jlo = na_r0(2 * i) // 2
                jhi = (na_r0(2 * i + 1) + 7) // 2
                js = list(range(jlo, jhi + 1))
                for hp in range(8):
                    for hh in range(2):
                        units.append((i, hp, hh, js))

            def emit_S(u):
                i, hp, hh, js = units[u]
                h = 2 * hp + hh
                p0 = 64 * hh
                sb_ = u % 2
                for jj, j in enumerate(js):
                    sc.op("pe", lambda e, sb_=sb_, jj=jj, j=j, p0=p0, hp=hp, i=i: e.matmul(
                        sps[sb_][:, jj, :], lhsT=knT[p0:p0 + 64, hp, j * 128:(j + 1) * 128],
                        rhs=qnT[p0:p0 + 64, hp, i * 128:(i + 1) * 128], start=True, stop=False,
                        skip_group_check=True),
                        reads=[kn_t[hp], qn_t[hp]], writes=[sps_t[sb_]], part=(jj > 0))
                    mms = []
                    for b0 in range(2):
                        qr = 2 * i + b0
                        va = [na_valid(2 * j + a, qr) for a in range(2)]
                        dr0 = 2 * j - qr + 7
                        cs = slice(b0 * 64, (b0 + 1) * 64)
                        if va[0] and va[1]:
                            mms.append((slice(0, 128), cs, TT[p0:p0 + 64, hp, dr0:dr0 + 2, :]))
                        elif not va[0] and not va[1]:
                            mms.append((slice(0, 128), cs, TT[p0:p0 + 64, hp, 15:17, :]))
                        elif (not va[0]) and va[1] and dr0 + 1 == 3:
                            mms.append((slice(0, 128), cs, TT[p0:p0 + 64, hp, 16:18, :]))
                        elif va[0] and (not va[1]) and dr0 == 10:
                            mms.append((slice(0, 128), cs, TT[p0:p0 + 64, hp, 18:20, :]))
                        else:
                            d0 = dr0 if va[0] else 15
                            d1 = dr0 + 1 if va[1] else 16
                            mms.append((slice(0, 64), cs, TT[p0:p0 + 64, hp, d0, :]))
                            mms.append((slice(64, 128), cs, TT[p0:p0 + 64, hp, d1, :]))
                    for mi, (ps_, cs, lhs) in enumerate(mms):
                        sc.op("pe", lambda e, sb_=sb_, jj=jj, ps_=ps_, cs=cs, lhs=lhs, p0=p0, last=(mi == len(mms) - 1):
                              e.matmul(sps[sb_][ps_, jj, cs], lhsT=lhs, rhs=ident[p0:p0 + 64, p0:p0 + 64],
                                       start=False, stop=last, skip_group_check=True),
                              reads=[TT_t[hp // 2], G["ident_t"]], writes=[sps_t[sb_]], part=True)

            def emit_rest(u):
                i, hp, hh, js = units[u]
                h = 2 * hp + hh
                sb_ = u % 2
                pt = u % 3
                pb_ = (u // 2) % 2
                ob = i % 2
                n = len(js)
                n1 = min(n, 4)
                sc.op("act", lambda e, pt=pt, sb_=sb_, n1=n1: e.activation(out=pT[pt][:, 0:n1, :],
                                                                         in_=sps[sb_][:, 0:n1, :], func=AF.Exp),
                      reads=[sps_t[sb_]], writes=[pT_t[pt]])
                if n > 4:
                    sc.op("act", lambda e, pt=pt, sb_=sb_, n=n: e.activation(out=pT[pt][:, 4:n, :],
                                                                           in_=sps[sb_][:, 4:n, :], func=AF.Exp),
                          reads=[sps_t[sb_]], writes=[pT_t[pt]], part=True)
                for jj, j in enumerate(js):
                    sc.op("pe", lambda e, pb_=pb_, hh=hh, pt=pt, jj=jj, j=j, h=h, n=n: e.matmul(
                        po[pb_][:, hh, 0:65], lhsT=pT[pt][:, jj, :], rhs=vx[:, j, h, :],
                        start=(jj == 0), stop=(jj == n - 1)),
                        reads=[pT_t[pt], vx_t[j]], writes=[po_t[pb_]], part=(hh > 0 or jj > 0))
                if hh == 1:
                    sc.op("dve", lambda e, pb_=pb_: e.reciprocal(out=rc[pb_][:, 0:2], in_=po[pb_][:, :, 64]),
                          reads=[po_t[pb_]], writes=[rc_t[pb_]])
                    for h2 in range(2):
                        hx = 2 * hp + h2
                        sc.op("dve", lambda e, pb_=pb_, h2=h2, hx=hx, ob=ob: e.tensor_scalar(
                            out=ot[ob][:, hx * 64:(hx + 1) * 64], in0=po[pb_][:, h2, 0:64], scalar1=rc[pb_][:, h2:h2 + 1],
                            scalar2=None, op0=ALU.mult),
                            reads=[po_t[pb_], rc_t[pb_]], writes=[ot_t[ob]], part=(hx > 0))
                if hp == 7 and hh == 1:
                    for half in range(2):
                        tb_ = tpc[0] % 2
                        tpc[0] += 1
                        for jq in range(4):
                            c = half * 4 + jq
                            sc.op("pe", lambda e, tb_=tb_, jq=jq, c=c, ob=ob: e.transpose(
                                out=tp[tb_][:, jq, :], in_=ot[ob][:, c * 128:(c + 1) * 128], identity=ident[:]),
                                reads=[ot_t[ob], G["ident_t"]], writes=[tp_t[tb_]], part=(jq > 0))
                        sc.op("act", lambda e, tb_=tb_, half=half, i=i: e.copy(
                            out=yst[:, half * 4:half * 4 + 4, (i % 4) * 128:(i % 4 + 1) * 128], in_=tp[tb_][:]),
                            reads=[tp_t[tb_]], writes=[yst_t], part=not (i % 4 == 0 and half == 0))
                    if i % 4 == 3:
                        yv = YB["na"].rearrange("(c p) t -> p c t", p=128)
                        sc.dma("pool", yv[:, :, (i - 3) * 128:(i + 1) * 128], yst[:], owner=yst_t, reads=[yst_t],
                               writes=[G["dram_t"]["yb_na"]], part=True)

            emit_S(0)
            for u in range(len(units)):
                if u + 1 < len(units):
                    emit_S(u + 1)
                emit_rest(u)
            sc.barrier(release=t2)
        sc.barrier(release=tiles)


def phase_F(P, sc, G, prm, l):
    nc = P.nc
    x = G["x"]
    with contextlib.ExitStack() as ph:
        hT = P.sb(ph, "F_hT", [128, 8, S], BF16)
        hT_t = sc.tiles_n("F_hT", NT)
        tiles = list(hT_t)
        tiles += rms_transpose(P, sc, G, ph, prm["norm_mlp_w"][l], hT, hT_t, l, "F")
        wst = WStream(P, sc, ph, "F", 1, 4096, nf=2, nb=3)
        fT = [P.sb(ph, "F_fT%d" % i, [128, 4, S], BF16) for i in range(2)]
        fT_t = [sc.tiles_n("F_fT%d_" % i, 4) for i in range(2)]
        rl = [P.sb(ph, "F_rl%d" % i, [128, 512], F32) for i in range(2)]
        rl_t = sc.tiles_n("F_rl", 2)
        acc = [P.ps(ph, "F_acc%d" % i, [128, 512], F32) for i in range(4)]
        acc_t = sc.tiles_n("F_acc", 4)
        tiles += wst.tiles + fT_t[0] + fT_t[1] + rl_t + acc_t
        w1v = prm["w_ff1"][l].rearrange("(kc p) n -> p kc n", p=128)
        w2v = prm["w_ff2"][l].rearrange("(c p) n -> p c n", p=128)
        items = []
        for g in range(8):
            items.append((w1v[:, :, g * 512:(g + 1) * 512], 8, 512))
            items.append((w2v[:, g * 4:(g + 1) * 4, :], 4, 1024))
        wst.items = items
        wst_views = {}

        def view(slot, k, n):
            return slot[:, 0, :].rearrange("p (k n) -> p k n", k=k)
        def _load(g):
            if g >= len(items):
                return
            ap, k, n = items[g]
            fs = g % wst.nf
            sc.dma("sp", view(wst.f[fs], k, n), ap, owner=wst.f_t[fs], writes=[wst.f_t[fs]])

        def _cast(g):
            if g >= len(items):
                return
            fs, bs = g % wst.nf, g % wst.nb
            sc.op("pool", lambda e: e.tensor_copy(out=wst.b[bs][:, 0, :], in_=wst.f[fs][:, 0, :]),
                  reads=[wst.f_t[fs]], writes=[wst.b_t[bs]])
        wst._load = _load
        wst._cast = _cast
        _load(0)
        _load(1)
        _cast(0)
        ai = 0
        ri = 0
        for g in range(8):
            fb = g % 2
            w1s, w1_t = wst.get(2 * g)
            w1b = view(w1s, 8, 512)
            for c in range(4):
                for tb in range(4):
                    a = ai % 4
                    ai += 1
                    for kc in range(8):
                        sc.op("pe", lambda e, a=a, kc=kc, w1b=w1b, c=c, tb=tb: e.matmul(
                            acc[a][:], lhsT=w1b[:, kc, c * 128:(c + 1) * 128],
                            rhs=hT[:, kc, tb * 512:(tb + 1) * 512], start=(kc == 0), stop=(kc == 7)),
                            reads=[w1_t] + hT_t[tb * 4:tb * 4 + 4], writes=[acc_t[a]], part=(kc > 0))
                    r = ri % 2
                    ri += 1
                    sc.op("act", lambda e, r=r, a=a: e.activation(out=rl[r][:], in_=acc[a][:], func=AF.Relu),
                          reads=[acc_t[a]], writes=[rl_t[r]])
                    sc.op("pool", lambda e, r=r, fb=fb, c=c, tb=tb: e.tensor_tensor(
                        out=fT[fb][:, c, tb * 512:(tb + 1) * 512], in0=rl[r][:], in1=rl[r][:], op=ALU.mult),
                        reads=[rl_t[r]], writes=[fT_t[fb][c]], part=(tb > 0))
            w2s, w2_t = wst.get(2 * g + 1)
            w2b = view(w2s, 4, 1024)
            for i in range(NT):
                for hh in range(2):
                    a = ai % 4
                    ai += 1
                    for c in range(4):
                        sc.op("pe", lambda e, a=a, c=c, w2b=w2b, i=i, hh=hh, fb=fb: e.matmul(
                            acc[a][:], lhsT=fT[fb][:, c, i * 128:(i + 1) * 128],
                            rhs=w2b[:, c, hh * 512:(hh + 1) * 512], start=(c == 0), stop=(c == 3)),
                            reads=[w2_t, fT_t[fb][c]], writes=[acc_t[a]], part=(c > 0))
                    xs = x[:, i, hh * 512:(hh + 1) * 512]
                    sc.op("dve", lambda e, xs=xs, a=a: e.tensor_tensor(out=xs, in0=xs, in1=acc[a][:], op=ALU.add),
                          reads=[acc_t[a], G["xt"][i]], writes=[G["xt"][i]])
        sc.barrier(release=tiles)


_NC_CACHE = {}


def make_na_tt(rpb):
    rpb = np.asarray(rpb, dtype=np.float32)
    L = rpb.shape[0]
    out = np.full((L, 128, 8, 20, 64), NEG, dtype=np.float32)
    qc = np.arange(64)
    ws = np.clip(qc - 8, 0, 48)
    for q in range(64):
        kc = np.arange(ws[q], ws[q] + 16)
        idx = kc - q + 15
        for hh in range(2):
            out[:, hh * 64 + q, :, 0:15, ws[q]:ws[q] + 16] = rpb[:, hh::2][:, :, :, idx]
    out[:, :, :, 17, :] = out[:, :, :, 3, :]
    out[:, :, :, 18, :] = out[:, :, :, 10, :]
    return out


def kernel(**inputs):
    cfg = {}
    key = "full"
    if key not in _NC_CACHE:
        _NC_CACHE[key] = build(cfg)
    nc = _NC_CACHE[key]
    x = np.ascontiguousarray(inputs["x"], dtype=np.float32)
    base = {n: np.ascontiguousarray(inputs[n], dtype=np.float32) for n in PARAM_NAMES}
    base["na_tt"] = make_na_tt(inputs["na_rpb"])
    in_maps = []
    for c in range(8):
        m = dict(base)
        m["x"] = x[c]
        in_maps.append(m)
    res = run_bass_kernel_spmd(nc, in_maps, core_ids=list(range(8)))
    return np.stack([r["y"] for r in res.results], axis=0).astype(np.float32)
```

```python
import contextlib
import numpy as np
import concourse.bass as bass
import concourse.mybir as mybir
from concourse.bass_utils import run_bass_kernel_spmd

F32 = mybir.dt.float32
BF16 = mybir.dt.bfloat16
ALU = mybir.AluOpType
AF = mybir.ActivationFunctionType
AX = mybir.AxisListType

D = 1024
S = 2048
NT = S // 128
DEPTH = 2
N_IN = 11840
EPS = 1e-6


class TT:
    __slots__ = ("name", "lw", "rd", "dsems", "gen")

    def __init__(self, name):
        self.name = name
        self.lw = {}
        self.rd = {}
        self.gen = {}
        self.dsems = {}


class Sched:
    ENG = ("pe", "act", "dve", "pool", "sp")
    BLK = {"pe": "tensor", "act": "scalar", "dve": "vector", "pool": "gpsimd", "sp": "sync"}

    def __init__(self, nc, stack):
        self.nc = nc
        self.stack = stack
        self.ops = {e: [] for e in self.ENG}
        self.seen = {e: {} for e in self.ENG}
        self.esem = {e: stack.enter_context(nc.semaphore("es_" + e)) for e in self.ENG if e != "sp"}
        self.tiles = []
        self.free_dsems = {"sp": [], "pool": [], "act": []}
        self.nsem = 4
        self.skip_same = {"pe"}

    def tile(self, name):
        t = TT(name)
        self.tiles.append(t)
        return t

    def tiles_n(self, name, n):
        return [self.tile("%s%d" % (name, i)) for i in range(n)]

    def _collect(self, reads, writes, part):
        evs = {}

        def add(d):
            for k, v in d.items():
                if k not in evs or evs[k][0] < v[0]:
                    evs[k] = v
        for t in reads:
            add(t.lw)
        for t in writes:
            if part and not t.rd:
                add(t.gen)
                continue
            g = dict(t.rd)
            for k, v in t.lw.items():
                if k not in g or g[k][0] < v[0]:
                    g[k] = v
            t.gen = g
            add(g)
        return evs

    def _waits(self, eng, evs):
        waits = []
        for k, (val, obj) in evs.items():
            if k == ("E", eng) and eng in self.skip_same:
                continue
            if self.seen[eng].get(k, 0) >= val:
                continue
            self.seen[eng][k] = val
            waits.append((k, val, obj))
            if k[0] == "E":
                self.ops[k[1]][val - 1]["inc"] = True
        return waits

    def _update(self, ev_key, ev_val, reads, writes, part):
        for t in reads:
            t.rd[ev_key] = ev_val
        for t in writes:
            if part and not t.rd:
                t.lw[ev_key] = ev_val
            else:
                t.lw = {ev_key: ev_val}
                t.rd = {}

    def op(self, eng, fn, reads=(), writes=(), part=False):
        waits = self._waits(eng, self._collect(reads, writes, part))
        self.ops[eng].append({"fn": fn, "waits": waits, "inc": False, "dma": None})
        idx = len(self.ops[eng])
        self._update(("E", eng), (idx, None), reads, writes, part)

    def dma(self, q, out, in_, owner, reads=(), writes=(), part=False, **kw):
        waits = self._waits(q, self._collect(reads, writes, part))
        rec = owner.dsems.get(q)
        if rec is None:
            if self.free_dsems[q]:
                rec = self.free_dsems[q].pop()
            else:
                rec = [self.stack.enter_context(self.nc.semaphore("ds%d" % self.nsem)), 0, self.nsem]
                self.nsem += 1
            owner.dsems[q] = rec
        rec[1] += 16
        self.ops[q].append({"fn": (lambda e: e.dma_start(out=out, in_=in_, **kw)), "waits": waits,
                            "inc": False, "dma": rec[0]})
        self._update(("D", rec[2]), (rec[1], rec[0]), reads, writes, part)

    def barrier(self, release=()):
        evs = {}
        for e in self.ENG:
            if e == "sp":
                continue
            idx = len(self.ops[e])
            while idx > 0 and (self.ops[e][idx - 1]["dma"] is not None or self.ops[e][idx - 1].get("nop")):
                idx -= 1
            if idx > 0:
                evs[("E", e)] = (idx, None)
        for t in self.tiles:
            for d in (t.lw, t.rd):
                for k, v in d.items():
                    if k[0] == "D" and (k not in evs or evs[k][0] < v[0]):
                        evs[k] = v
        for e in self.ENG:
            sk = self.skip_same
            self.skip_same = set()
            w = self._waits(e, dict(evs))
            self.skip_same = sk
            self.ops[e].append({"fn": (lambda en: en.nop()), "waits": w, "inc": False, "dma": None, "nop": True})
        for t in self.tiles:
            t.lw = {}
            t.rd = {}
            t.gen = {}
        rel = set(id(t) for t in release)
        for t in release:
            for q, rec in t.dsems.items():
                self.free_dsems[q].append(rec)
            t.dsems = {}
        self.tiles = [t for t in self.tiles if id(t) not in rel]

    def emit(self):
        nc = self.nc
        mile = {}
        for e in self.ENG:
            c = 0
            m = []
            for o in self.ops[e]:
                if o["inc"]:
                    c += 1
                m.append(c)
            mile[e] = m
            assert c < 60000, (e, c)
        with nc.Block() as block:
            for e in self.ENG:
                def body(engine, e=e):
                    for o in self.ops[e]:
                        for (k, val, obj) in o["waits"]:
                            if k[0] == "E":
                                engine.wait_ge(self.esem[k[1]], mile[k[1]][val - 1])
                            else:
                                engine.wait_ge(obj, val)
                        ins = o["fn"](engine)
                        if o["dma"] is not None:
                            ins.then_inc(o["dma"], 16)
                        elif o["inc"]:
                            ins.then_inc(self.esem[e], 1)
                getattr(block, self.BLK[e])(body)


class Prog:
    def __init__(self, cfg):
        self.cfg = cfg
        self.nc = bass.Bass("TRN2", target_bir_lowering=False)
        self.dbg = cfg.get("debug", ())

    def dram(self, name, shape, dt, kind="Internal"):
        if name in self.dbg:
            kind = "ExternalOutput"
        if name in self.cfg.get("ext_in", ()):
            kind = "ExternalInput"
        return self.nc.dram_tensor(name, list(shape), dt, kind=kind).ap()

    def sb(self, stack, name, shape, dt):
        self.uid = getattr(self, "uid", 0) + 1
        return stack.enter_context(self.nc.sbuf_tensor("%s_u%d" % (name, self.uid), list(shape), dt))

    def ps(self, stack, name, shape, dt):
        self.uid = getattr(self, "uid", 0) + 1
        return stack.enter_context(self.nc.psum_tensor("%s_u%d" % (name, self.uid), list(shape), dt))


IN_SIZES = (1024, 1536, 16, 16, 512, 512, 1024, 1024, 16, 16, 1024, 1024, 1024, 3072)
IN_OFF = [0]
for _s in IN_SIZES:
    IN_OFF.append(IN_OFF[-1] + _s)
(O_Z, O_XBC, O_DTF, O_DTB, O_GQ, O_GK, O_GV, O_GG, O_GAF, O_GAB, O_NQ, O_NK, O_NV, O_GATE, _) = IN_OFF

PARAM_NAMES = ["norm_mix_w", "w_in", "ssd_conv_w", "ssd_conv_b", "ssd_dt_bias_f", "ssd_dt_bias_b",
               "ssd_a_log_f", "ssd_a_log_b", "ssd_d", "ssd_norm_w", "gla_a2_f", "gla_a2_bias_f",
               "gla_a2_b", "gla_a2_bias_b", "gla_norm_w", "na_q_norm_w", "na_k_norm_w", "na_rpb",
               "w_branch_ssd", "w_branch_gla", "w_branch_na", "w_out", "norm_mlp_w", "w_ff1", "w_ff2"]
PARAM_SHAPES = {
    "norm_mix_w": (2, 1024), "w_in": (2, 1024, 11840), "ssd_conv_w": (2, 5, 1536), "ssd_conv_b": (2, 1536),
    "ssd_dt_bias_f": (2, 16), "ssd_dt_bias_b": (2, 16), "ssd_a_log_f": (2, 16), "ssd_a_log_b": (2, 16),
    "ssd_d": (2, 16), "ssd_norm_w": (2, 1024), "gla_a2_f": (2, 16, 512), "gla_a2_bias_f": (2, 512),
    "gla_a2_b": (2, 16, 512), "gla_a2_bias_b": (2, 512), "gla_norm_w": (2, 256), "na_q_norm_w": (2, 64),
    "na_k_norm_w": (2, 64), "na_rpb": (2, 16, 15, 31), "w_branch_ssd": (2, 1024, 1024),
    "w_branch_gla": (2, 1024, 1024), "w_branch_na": (2, 1024, 1024), "w_out": (2, 1024, 1024),
    "norm_mlp_w": (2, 1024), "w_ff1": (2, 1024, 4096), "w_ff2": (2, 4096, 1024),
}


def build(cfg):
    P = Prog(cfg)
    nc = P.nc
    layers = cfg.get("layers", DEPTH)
    phases = cfg.get("phases", "ABCDEF")
    x_in = nc.dram_tensor("x", [S, D], F32, kind="ExternalInput").ap()
    prm = {n: nc.dram_tensor(n, list(PARAM_SHAPES[n]), F32, kind="ExternalInput").ap() for n in PARAM_NAMES}
    y_out = nc.dram_tensor("y", [S, D], F32, kind="ExternalOutput").ap()
    natt = nc.dram_tensor("na_tt", [DEPTH, 128, 8, 20, 64], F32, kind="ExternalInput").ap()

    U = {}
    for nm, w in (("z", 1024), ("gv", 1024), ("gg", 1024), ("nv", 1024)):
        U[nm] = P.dram("u_" + nm, [S, w], BF16)
    U["dt"] = P.dram("u_dt", [S, 32], F32)
    for nm, w in (("xbc", 1536), ("gq", 512), ("gk", 512), ("nq", 1024), ("nk", 1024), ("gate", 3072)):
        U[nm] = P.dram("u_" + nm + "T", [w, S], BF16)
    U["ga"] = P.dram("u_gaT", [32, S], BF16)
    YB = {nm: P.dram("yb_" + nm, [1024, S], BF16) for nm in ("ssd", "gla", "na")}
    ybw = P.dram("ybw", [S, 1024], BF16)

    with contextlib.ExitStack() as top:
        sc = Sched(nc, top)
        G = {}
        G["x"] = P.sb(top, "x_res", [128, NT, D], F32)
        G["xt"] = sc.tiles_n("x", NT)
        G["ident"] = P.sb(top, "ident", [128, 128], BF16)
        G["ident_t"] = sc.tile("ident")
        G["dram_t"] = {k: sc.tile("d_" + k) for k in list(U) + ["yb_ssd", "yb_gla", "yb_na", "ybw"]}
        G["ybw"] = ybw

        ones_f = P.sb(top, "ones_f", [128, 128], F32)
        ones_t = sc.tile("ones_f")
        sc.op("pool", lambda e: e.memset(ones_f[:], 1.0), writes=[ones_t])
        sc.op("pool", lambda e: e.affine_select(out=G["ident"][:], in_=ones_f[:], pattern=[[-1, 128]],
                                                compare_op=ALU.is_equal, fill=0.0, base=0,
                                                channel_multiplier=1),
              reads=[ones_t], writes=[G["ident_t"]])
        G["ones_f"] = ones_f
        G["eps"] = P.sb(top, "epsc", [128, 2], F32)
        G["eps_t"] = sc.tile("epsc")
        sc.op("pool", lambda e: e.memset(G["eps"][:], EPS), writes=[G["eps_t"]])
        G["one"] = P.sb(top, "onec", [128, 2], F32)
        G["one_t"] = sc.tile("onec")
        sc.op("pool", lambda e: e.memset(G["one"][:], 1.0), writes=[G["one_t"]])
        G["neghalf"] = P.sb(top, "neghalf", [128, 16], F32)
        G["neghalf_t"] = sc.tile("neghalf")
        sc.op("pool", lambda e: e.memset(G["neghalf"][:], -0.5), writes=[G["neghalf_t"]])
        G["ones_t"] = ones_t

        build_tri(P, sc, G, top)
        xv = x_in.rearrange("(i p) d -> p i d", p=128)
        for i in range(NT):
            sc.dma("sp", G["x"][:, i, :], xv[:, i, :], owner=G["xt"][i], writes=[G["xt"][i]])

        for l in range(layers):
            if "A" in phases:
                phase_A(P, sc, G, U, prm, l)
            if "B" in phases:
                phase_B(P, sc, G, U, YB, prm, l)
            if "C" in phases:
                phase_C(P, sc, G, U, YB, prm, l)
            if "D" in phases:
                phase_D(P, sc, G, U, YB, prm, natt, l)
            if "E" in phases:
                phase_E(P, sc, G, U, YB, prm, l)
            if "F" in phases:
                phase_F(P, sc, G, prm, l)

        yv = y_out.rearrange("(i p) d -> p i d", p=128)
        outt = sc.tile("yout")
        for i in range(NT):
            sc.dma("sp", yv[:, i, :], G["x"][:, i, :], owner=G["xt"][i], reads=[G["xt"][i]], writes=[outt],
                   part=True)
        sc.op("sp", lambda e: e.nop(), reads=[outt])
        sc.barrier()
        sc.emit()
    return nc


def rms_transpose(P, sc, G, ph, wrow_ap, hT, hT_t, l, tag):
    nc = P.nc
    wb = P.sb(ph, tag + "_wb", [128, D], F32)
    wb_t = sc.tile(tag + "_wb")
    sc.dma("sp", wb[:], wrow_ap.partition_broadcast(128), owner=wb_t, writes=[wb_t])
    junk = [P.sb(ph, tag + "_junk%d" % i, [128, D], BF16) for i in range(2)]
    junk_t = sc.tiles_n(tag + "_junk", 2)
    hb = [P.sb(ph, tag + "_hb%d" % i, [128, D], BF16) for i in range(2)]
    hb_t = sc.tiles_n(tag + "_hb", 2)
    ss = [P.sb(ph, tag + "_ss%d" % i, [128, 2], F32) for i in range(2)]
    ss_t = sc.tiles_n(tag + "_ss", 2)
    tp = [P.ps(ph, tag + "_tp%d" % i, [128, 4, 128], BF16) for i in range(2)]
    tp_t = sc.tiles_n(tag + "_tp", 2)
    x = G["x"]
    new_tiles = [wb_t] + junk_t + hb_t + ss_t + tp_t
    def _s1(i):
        b = i % 2
        xt = G["xt"][i]
        sc.op("dve", lambda e, i=i, b=b: e.scalar_tensor_tensor(out=junk[b][:], in0=x[:, i, :], scalar=1.0,
                                                                in1=x[:, i, :], op0=ALU.mult, op1=ALU.mult,
                                                                accum_out=ss[b][:, 0:1]),
              reads=[xt], writes=[junk_t[b], ss_t[b]])
        sc.op("dve", lambda e, b=b: e.tensor_scalar(out=ss[b][:, 1:2], in0=ss[b][:, 0:1], scalar1=1.0 / D,
                                                    scalar2=EPS, op0=ALU.mult, op1=ALU.add),
              reads=[ss_t[b]], writes=[ss_t[b]])
        sc.op("pool", lambda e, b=b: e.tensor_tensor(out=ss[b][:, 0:1], in0=ss[b][:, 1:2],
                                                     in1=G["neghalf"][:, 0:1], op=ALU.pow),
              reads=[ss_t[b], G["neghalf_t"]], writes=[ss_t[b]])

    def _s2(i):
        b = i % 2
        xt = G["xt"][i]
        sc.op("dve", lambda e, i=i, b=b: e.scalar_tensor_tensor(out=hb[b][:], in0=x[:, i, :],
                                                                scalar=ss[b][:, 0:1], in1=wb[:],
                                                                op0=ALU.mult, op1=ALU.mult),
              reads=[xt, ss_t[b], wb_t], writes=[hb_t[b]])
        for half in range(2):
            pb = (2 * i + half) % 2
            for j in range(4):
                kc = half * 4 + j
                sc.op("pe", lambda e, b=b, pb=pb, j=j, kc=kc: e.transpose(
                    out=tp[pb][:, j, :], in_=hb[b][:, kc * 128:(kc + 1) * 128], identity=G["ident"][:]),
                    reads=[hb_t[b], G["ident_t"]], writes=[tp_t[pb]], part=(j > 0))
            eng = "act"
            if eng == "act":
                sc.op("act", lambda e, pb=pb, half=half, i=i: e.copy(
                    out=hT[:, half * 4:half * 4 + 4, i * 128:(i + 1) * 128], in_=tp[pb][:]),
                    reads=[tp_t[pb]], writes=[hT_t[i]], part=True)
            else:
                sc.op("dve", lambda e, pb=pb, half=half, i=i: e.tensor_copy(
                    out=hT[:, half * 4:half * 4 + 4, i * 128:(i + 1) * 128], in_=tp[pb][:]),
                    reads=[tp_t[pb]], writes=[hT_t[i]], part=True)

    _s1(0)
    for i in range(NT):
        if i + 1 < NT:
            _s1(i + 1)
        _s2(i)
    return new_tiles


class WStream:
    def __init__(self, P, sc, ph, tag, kdim, ncol, nf=2, nb=3, cast_eng="pool"):
        self.sc = sc
        self.cast_eng = cast_eng
        self.kdim, self.ncol = kdim, ncol
        self.nf, self.nb = nf, nb
        self.f = [P.sb(ph, "%s_wf%d" % (tag, i), [128, kdim, ncol], F32) for i in range(nf)]
        self.f_t = sc.tiles_n(tag + "_wf", nf)
        self.b = [P.sb(ph, "%s_wb%d" % (tag, i), [128, kdim, ncol], BF16) for i in range(nb)]
        self.b_t = sc.tiles_n(tag + "_wbt", nb)
        self.tiles = self.f_t + self.b_t
        self.items = []

    def start(self, items):
        self.items = items
        self._load(0)
        self._load(1)
        self._cast(0)

    def _load(self, g):
        if g >= len(self.items):
            return
        ap, k, n = self.items[g]
        fs = g % self.nf
        self.sc.dma("sp", self.f[fs][:, 0:k, 0:n], ap, owner=self.f_t[fs], writes=[self.f_t[fs]])

    def _cast(self, g):
        if g >= len(self.items):
            return
        ap, k, n = self.items[g]
        fs, bs = g % self.nf, g % self.nb
        if self.cast_eng == "act":
            self.sc.op("act", lambda e: e.copy(out=self.b[bs][:, 0:k, 0:n], in_=self.f[fs][:, 0:k, 0:n]),
                       reads=[self.f_t[fs]], writes=[self.b_t[bs]])
        else:
            self.sc.op("pool", lambda e: e.tensor_copy(out=self.b[bs][:, 0:k, 0:n], in_=self.f[fs][:, 0:k, 0:n]),
                       reads=[self.f_t[fs]], writes=[self.b_t[bs]])

    def get(self, g):
        self._cast(g + 1)
        self._load(g + 2)
        return self.b[g % self.nb], self.b_t[g % self.nb]


def proj_groups():
    g = []

    def seg(off, n, mode, key):
        c = 0
        while c < n:
            w = min(512, n - c)
            g.append((off + c, w, mode, key, c))
            c += w
    seg(O_Z, 1024, "tok", "z")
    seg(O_XBC, 1536, "feat", "xbc")
    g.append((O_DTF, 32, "tok32", "dt", 0))
    seg(O_GQ, 512, "feat", "gq")
    seg(O_GK, 512, "feat", "gk")
    seg(O_GV, 1024, "tok", "gv")
    seg(O_GG, 1024, "tok", "gg")
    g.append((O_GAF, 32, "feat32", "ga", 0))
    seg(O_NQ, 1024, "feat", "nq")
    seg(O_NK, 1024, "feat", "nk")
    seg(O_NV, 1024, "tok", "nv")
    seg(O_GATE, 3072, "feat", "gate")
    return g


def phase_A(P, sc, G, U, prm, l):
    nc = P.nc
    with contextlib.ExitStack() as ph:
        hT = P.sb(ph, "A_hT", [128, 8, S], BF16)
        hT_t = sc.tiles_n("A_hT", NT)
        tiles = list(hT_t)
        tiles += rms_transpose(P, sc, G, ph, prm["norm_mix_w"][l], hT, hT_t, l, "A")
        wst = WStream(P, sc, ph, "A", 8, 512)
        acc = [P.ps(ph, "A_acc%d" % i, [128, 512], F32) for i in range(4)]
        acc_t = sc.tiles_n("A_acc", 4)
        NS = 3
        stg = [P.sb(ph, "A_stg%d" % i, [128, 2048], BF16) for i in range(NS)]
        stg_t = sc.tiles_n("A_stg", NS)
        stf = [P.sb(ph, "A_stf%d" % i, [128, 4, 32], F32) for i in range(2)]
        stf_t = sc.tiles_n("A_stf", 2)
        G["ga_stage"] = P.sb(ph, "A_gast", [32, 2048], BF16)
        G["ga_stage_t"] = sc.tile("A_gast")
        tiles += wst.tiles + acc_t + stg_t + stf_t + [G["ga_stage_t"]]
        wv = prm["w_in"][l].rearrange("(kc p) n -> p kc n", p=128)
        groups = proj_groups()
        wst.start([(wv[:, :, c0:c0 + n], 8, n) for (c0, n, _m, _k, _d) in groups])
        ai = 0
        si = 0
        ev = 0
        for gi, (c0, n, mode, key, doff) in enumerate(groups):
            wcur, wcur_t = wst.get(gi)
            dst = U[key]
            dst_t = G["dram_t"][key]
            if mode in ("tok", "tok32"):
                for tb in range(4):
                    if mode == "tok":
                        st = si % NS
                        si += 1
                    else:
                        st = tb % 2
                    for j in range(4):
                        i = tb * 4 + j
                        a = ai % 4
                        ai += 1
                        for kc in range(8):
                            sc.op("pe", lambda e, a=a, kc=kc, i=i, wcur=wcur, n=n: e.matmul(
                                acc[a][:, 0:n], lhsT=hT[:, kc, i * 128:(i + 1) * 128], rhs=wcur[:, kc, 0:n],
                                start=(kc == 0), stop=(kc == 7)),
                                reads=[hT_t[i], wcur_t], writes=[acc_t[a]], part=(kc > 0))
                        if mode == "tok":
                            o_ap = stg[st][:, j * 512:j * 512 + n]
                            o_t = stg_t[st]
                        else:
                            o_ap = stf[st][:, j, 0:n]
                            o_t = stf_t[st]
                        ev += 1
                        if ev % 2 == 0:
                            sc.op("act", lambda e, o_ap=o_ap, a=a, n=n: e.copy(out=o_ap, in_=acc[a][:, 0:n]),
                                  reads=[acc_t[a]], writes=[o_t], part=(j > 0))
                        else:
                            sc.op("dve", lambda e, o_ap=o_ap, a=a, n=n: e.tensor_copy(out=o_ap, in_=acc[a][:, 0:n]),
                                  reads=[acc_t[a]], writes=[o_t], part=(j > 0))
                    rows = dst[tb * 512:(tb + 1) * 512, doff:doff + n].rearrange("(j p) c -> p j c", p=128)
                    if mode == "tok":
                        src = stg[st][:].rearrange("p (j c) -> p j c", j=4)[:, :, 0:n]
                        sc.dma("pool", rows, src, owner=stg_t[st], reads=[stg_t[st]], writes=[dst_t], part=True)
                    else:
                        sc.dma("pool", rows, stf[st][:, :, 0:n], owner=stf_t[st], reads=[stf_t[st]], writes=[dst_t],
                               part=True)
            else:
                nchunk = (n + 127) // 128
                for c in range(nchunk):
                    m = min(128, n - c * 128)
                    if mode == "feat":
                        st = si % NS
                        si += 1
                    else:
                        st = 0
                    for tb in range(4):
                        a = ai % 4
                        ai += 1
                        for kc in range(8):
                            sc.op("pe", lambda e, a=a, kc=kc, tb=tb, wcur=wcur, c=c, m=m: e.matmul(
                                acc[a][0:m, :], lhsT=wcur[:, kc, c * 128:c * 128 + m],
                                rhs=hT[:, kc, tb * 512:(tb + 1) * 512], start=(kc == 0), stop=(kc == 7)),
                                reads=hT_t[tb * 4:tb * 4 + 4] + [wcur_t], writes=[acc_t[a]], part=(kc > 0))
                        ev += 1
                        if mode == "feat":
                            o_ap = stg[st][0:m, tb * 512:(tb + 1) * 512]
                            o_t = stg_t[st]
                            if ev % 2 == 0:
                                sc.op("act", lambda e, o_ap=o_ap, a=a, m=m: e.copy(out=o_ap, in_=acc[a][0:m, :]),
                                      reads=[acc_t[a]], writes=[o_t], part=(tb > 0))
                            else:
                                sc.op("dve", lambda e, o_ap=o_ap, a=a, m=m: e.tensor_copy(out=o_ap, in_=acc[a][0:m, :]),
                                      reads=[acc_t[a]], writes=[o_t], part=(tb > 0))
                        else:
                            sc.op("dve", lambda e, a=a, m=m, tb=tb, gast=G["ga_stage"]: e.tensor_copy(
                                out=gast[0:m, tb * 512:(tb + 1) * 512], in_=acc[a][0:m, :]),
                                reads=[acc_t[a]], writes=[G["ga_stage_t"]], part=(tb > 0))
                    if mode == "feat":
                        sc.dma("pool", dst[doff + c * 128:doff + c * 128 + m, :], stg[st][0:m, :], owner=stg_t[st],
                               reads=[stg_t[st]], writes=[dst_t], part=True)
                    else:
                        sc.dma("pool", dst[0:m, :], G["ga_stage"][0:m, :], owner=G["ga_stage_t"],
                               reads=[G["ga_stage_t"]], writes=[dst_t], part=True)
        sc.barrier(release=tiles)


def phase_E(P, sc, G, U, YB, prm, l):
    nc = P.nc
    x = G["x"]
    with contextlib.ExitStack() as ph:
        wst = WStream(P, sc, ph, "E", 8, 256, nf=2, nb=2, cast_eng="act")
        mix = P.sb(ph, "E_mix", [128, 8, 1024], F32)
        mix_t = sc.tiles_n("E_mix", 8)
        mixb = P.sb(ph, "E_mixb", [128, 8, 1024], BF16)
        mixb_t = sc.tile("E_mixb")
        ybT = [P.sb(ph, "E_yb%d" % i, [128, 8, 1024], BF16) for i in range(2)]
        ybT_t = sc.tiles_n("E_yb", 2)
        gsl = [P.sb(ph, "E_g%d" % i, [128, 1024], BF16) for i in range(3)]
        gsl_t = sc.tiles_n("E_g", 3)
        sig = [P.sb(ph, "E_sig%d" % i, [128, 1024], F32) for i in range(2)]
        sig_t = sc.tiles_n("E_sig", 2)
        tmp = [P.sb(ph, "E_tmp%d" % i, [128, 512], F32) for i in range(2)]
        tmp_t = sc.tiles_n("E_tmp", 2)
        acc = [P.ps(ph, "E_acc%d" % i, [128, 512], F32) for i in range(4)]
        acc_t = sc.tiles_n("E_acc", 4)
        tiles = wst.tiles + mix_t + [mixb_t] + ybT_t + gsl_t + sig_t + tmp_t + acc_t
        wnames = ["w_branch_ssd", "w_branch_gla", "w_branch_na", "w_out"]
        bnames = ["ssd", "gla", "na"]
        items = []
        for half in range(2):
            for wn in wnames:
                wv = prm[wn][l].rearrange("(kc p) n -> p kc n", p=128)
                for cg in range(4):
                    items.append((wv[:, :, cg * 256:(cg + 1) * 256], 8, 256))
        wst.start(items)
        gi = 0
        ai = 0
        gcount = 0
        tcount = 0
        ybcount = 0
        for half in range(2):
            t0 = half * 1024
            for b in range(3):
                ys = ybcount % 2
                ybcount += 1
                ybv = YB[bnames[b]].rearrange("(kc p) t -> p kc t", p=128)
                sc.dma("sp", ybT[ys][:], ybv[:, :, t0:t0 + 1024], owner=ybT_t[ys],
                       reads=[G["dram_t"]["yb_" + bnames[b]]], writes=[ybT_t[ys]])
                for cg in range(4):
                    wcur, wcur_t = wst.get(gi)
                    gi += 1
                    for ecl in range(2):
                        ec = cg * 2 + ecl
                        gs = gcount % 3
                        ss_ = gcount % 2
                        gcount += 1
                        grow = b * 1024 + ec * 128
                        sc.dma("sp", gsl[gs][:], U["gate"][grow:grow + 128, t0:t0 + 1024], owner=gsl_t[gs],
                               reads=[G["dram_t"]["gate"]], writes=[gsl_t[gs]])
                        sc.op("act", lambda e, gs=gs, ss_=ss_: e.activation(out=sig[ss_][:], in_=gsl[gs][:],
                                                                            func=AF.Sigmoid),
                              reads=[gsl_t[gs]], writes=[sig_t[ss_]])
                        for tbh in range(2):
                            a = ai % 4
                            ai += 1
                            for kc in range(8):
                                sc.op("pe", lambda e, a=a, kc=kc, wcur=wcur, ecl=ecl, ys=ys, tbh=tbh: e.matmul(
                                    acc[a][:], lhsT=wcur[:, kc, ecl * 128:(ecl + 1) * 128],
                                    rhs=ybT[ys][:, kc, tbh * 512:(tbh + 1) * 512], start=(kc == 0), stop=(kc == 7)),
                                    reads=[wcur_t, ybT_t[ys]], writes=[acc_t[a]], part=(kc > 0))
                            msl = mix[:, ec, tbh * 512:(tbh + 1) * 512]
                            sgl = sig[ss_][:, tbh * 512:(tbh + 1) * 512]
                            if b == 0:
                                sc.op("dve", lambda e, msl=msl, a=a, sgl=sgl: e.tensor_tensor(
                                    out=msl, in0=acc[a][:], in1=sgl, op=ALU.mult),
                                    reads=[acc_t[a], sig_t[ss_]], writes=[mix_t[ec]], part=(tbh > 0))
                            else:
                                ts = tcount % 2
                                tcount += 1
                                sc.op("dve", lambda e, ts=ts, a=a, sgl=sgl: e.tensor_tensor(
                                    out=tmp[ts][:], in0=acc[a][:], in1=sgl, op=ALU.mult),
                                    reads=[acc_t[a], sig_t[ss_]], writes=[tmp_t[ts]])
                                if b == 1:
                                    sc.op("pool", lambda e, msl=msl, ts=ts: e.tensor_tensor(
                                        out=msl, in0=msl, in1=tmp[ts][:], op=ALU.add),
                                        reads=[tmp_t[ts], mix_t[ec]], writes=[mix_t[ec]])
                                else:
                                    sc.op("pool", lambda e, msl=msl, ts=ts, ec=ec, tbh=tbh: e.tensor_tensor(
                                        out=mixb[:, ec, tbh * 512:(tbh + 1) * 512], in0=msl, in1=tmp[ts][:],
                                        op=ALU.add),
                                        reads=[tmp_t[ts], mix_t[ec]], writes=[mixb_t], part=True)
            for cg in range(4):
                wcur, wcur_t = wst.get(gi)
                gi += 1
                for j in range(8):
                    i = half * 8 + j
                    a = ai % 4
                    ai += 1
                    for ec in range(8):
                        sc.op("pe", lambda e, a=a, ec=ec, wcur=wcur, j=j: e.matmul(
                            acc[a][:, 0:256], lhsT=mixb[:, ec, j * 128:(j + 1) * 128], rhs=wcur[:, ec, :],
                            start=(ec == 0), stop=(ec == 7)),
                            reads=[wcur_t, mixb_t], writes=[acc_t[a]], part=(ec > 0))
                    xs = x[:, i, cg * 256:(cg + 1) * 256]
                    sc.op("dve", lambda e, xs=xs, a=a: e.tensor_tensor(out=xs, in0=xs, in1=acc[a][:, 0:256], op=ALU.add),
                          reads=[acc_t[a], G["xt"][i]], writes=[G["xt"][i]])
        sc.barrier(release=tiles)


class Ring:
    def __init__(self, P, sc, stack, name, shape, dt, n, psum=False, views=None):
        if views is not None:
            self.h = views
            n = len(views)
        else:
            mk = P.ps if psum else P.sb
            self.h = [mk(stack, "%s%d" % (name, i), shape, dt) for i in range(n)]
        self.t = sc.tiles_n(name + "_", n)
        self.i = 0
        self.n = n

    def next(self):
        k = self.i % self.n
        self.i += 1
        return self.h[k], self.t[k]


def build_tri(P, sc, G, top):
    for nm in ("trif", "trib", "trif64", "trib64", "mcf64", "mcb64", "trifs", "tribs"):
        G[nm] = P.sb(top, nm, [128, 128], F32)
        G[nm + "_t"] = sc.tile(nm)
    ones_f, ones_t = G["ones_f"], G["ones_t"]
    sc.op("pool", lambda e: e.affine_select(out=G["trif"][:], in_=ones_f[:], pattern=[[1, 128]], compare_op=ALU.is_ge,
                                            fill=0.0, base=0, channel_multiplier=-1),
          reads=[ones_t], writes=[G["trif_t"]])
    sc.op("pool", lambda e: e.affine_select(out=G["trib"][:], in_=ones_f[:], pattern=[[-1, 128]], compare_op=ALU.is_ge,
                                            fill=0.0, base=0, channel_multiplier=1),
          reads=[ones_t], writes=[G["trib_t"]])
    sc.op("pool", lambda e: e.affine_select(out=G["trifs"][:], in_=ones_f[:], pattern=[[1, 128]], compare_op=ALU.is_gt,
                                            fill=0.0, base=0, channel_multiplier=-1),
          reads=[ones_t], writes=[G["trifs_t"]])
    sc.op("pool", lambda e: e.affine_select(out=G["tribs"][:], in_=ones_f[:], pattern=[[-1, 128]], compare_op=ALU.is_gt,
                                            fill=0.0, base=0, channel_multiplier=1),
          reads=[ones_t], writes=[G["tribs_t"]])
    sc.op("pool", lambda e: e.tensor_copy(out=G["trif64"][:], in_=G["trif"][:]), reads=[G["trif_t"]], writes=[G["trif64_t"]])
    sc.op("pool", lambda e: e.memset(G["trif64"][0:64, 64:128], 0.0), reads=[G["trif64_t"]], writes=[G["trif64_t"]])
    sc.op("pool", lambda e: e.tensor_copy(out=G["trib64"][:], in_=G["trib"][:]), reads=[G["trib_t"]], writes=[G["trib64_t"]])
    sc.op("pool", lambda e: e.memset(G["trib64"][64:128, 0:64], 0.0), reads=[G["trib64_t"]], writes=[G["trib64_t"]])
    sc.op("pool", lambda e: e.tensor_scalar(out=G["mcf64"][:], in0=G["trif64"][:], scalar1=-1.0 / 16.0, scalar2=None,
                                            op0=ALU.mult), reads=[G["trif64_t"]], writes=[G["mcf64_t"]])
    sc.op("pool", lambda e: e.tensor_scalar(out=G["mcb64"][:], in0=G["trib64"][:], scalar1=-1.0 / 16.0, scalar2=None,
                                            op0=ALU.mult), reads=[G["trib64_t"]], writes=[G["mcb64_t"]])
    for nm in ("mcf64", "mcb64", "trifs", "tribs"):
        G[nm + "b"] = P.sb(top, nm + "b", [128, 128], BF16)
        G[nm + "b_t"] = sc.tile(nm + "b")
        sc.op("pool", lambda e, nm=nm: e.tensor_copy(out=G[nm + "b"][:], in_=G[nm][:]), reads=[G[nm + "_t"]],
              writes=[G[nm + "b_t"]])


def phase_C(P, sc, G, U, YB, prm, l):
    nc = P.nc
    ident = G["ident"]
    with contextlib.ExitStack() as ph:
        qT = P.sb(ph, "C_qT", [128, 4, S], BF16)
        kT = P.sb(ph, "C_kT", [128, 4, S], BF16)
        qT_t = sc.tile("C_qT")
        kT_t = sc.tile("C_kT")
        ob = P.sb(ph, "C_ob", [128, NT, 1024], BF16)
        ob_t = sc.tiles_n("C_ob", NT)
        gaX = P.sb(ph, "C_gaX", [32, S], BF16)
        gaX_t = sc.tile("C_gaX")
        a2X = [P.sb(ph, "C_a2X%d" % d, [32, 512], BF16) for d in range(2)]
        a2X_t = sc.tiles_n("C_a2X", 2)
        a2f = P.sb(ph, "C_a2f", [32, 512], F32)
        a2f_t = sc.tile("C_a2f")
        nwb = P.sb(ph, "C_nwb", [128, 256], F32)
        nwb_t = sc.tile("C_nwb")
        Sf = P.sb(ph, "C_Sf", [128, 4, 256], F32)
        Sf_t = sc.tile("C_Sf")
        yst = P.sb(ph, "C_yst", [128, 8, 256], BF16)
        yst_t = sc.tile("C_yst")
        R = lambda name, shape, dt, n, psum=False: Ring(P, sc, ph, "C_" + name, shape, dt, n, psum)
        r_Sb = R("Sb", [128, 4, 256], BF16, 3)
        r_v = R("v", [128, 1024], BF16, 2)
        r_gg = R("gg", [128, 4, 1024], BF16, 1)
        r_e1 = R("e1", [128, 512], F32, 1)
        r_gn = R("gn", [128, 512], BF16, 1)
        r_eb = R("eb", [128, 4, 128], F32, 1)
        r_enb = R("enb", [128, 4, 128], F32, 1)
        r_ew = R("ew", [128, 4, 128], F32, 1)
        r_ed = R("ed", [128, 4, 2], F32, 3)
        r_qd = R("qd", [128, 4, 128], BF16, 3)
        r_kd = R("kd", [128, 4, 128], BF16, 2)
        r_kw = R("kw", [128, 4, 128], BF16, 1)
        r_kwt = R("kwt", [128, 4, 128], BF16, 3)
        r_am = R("am", [128, 4, 128], BF16, 3)
        r_oa = R("oa", [128, 1024], F32, 1)
        r_sg = R("sg", [128, 4, 1024], BF16, 1)
        r_jk = R("jk", [128, 256], BF16, 1)
        r_ss = R("ss", [128, 8], F32, 2)
        r_y = R("y", [128, 1024], BF16, 2)
        r_gp = R("gp", [128, 512], F32, 1, True)
        r_bT = R("bT", [128, 4, 128], F32, 1, True)
        r_att = R("att", [128, 4, 128], F32, 1, True)
        r_kwp = R("kwp", [128, 4, 128], BF16, 1, True)
        r_st = R("st", [128, 4, 256], F32, 1, True)
        r_o = R("o", [128, 4, 256], F32, 1, True)
        rings = [r_Sb, r_v, r_gg, r_e1, r_gn, r_eb, r_enb, r_ew, r_ed, r_qd, r_kd, r_kw, r_kwt, r_am, r_oa, r_sg,
                 r_jk, r_ss, r_y, r_gp, r_bT, r_att, r_kwp, r_st, r_o]
        tiles = [qT_t, kT_t, nwb_t, yst_t, gaX_t, Sf_t, a2f_t] + ob_t + a2X_t
        for r in rings:
            tiles += r.t
        sc.dma("sp", qT[:], U["gq"].rearrange("(h p) t -> p h t", p=128), owner=qT_t, reads=[G["dram_t"]["gq"]],
               writes=[qT_t])
        sc.dma("sp", kT[:], U["gk"].rearrange("(h p) t -> p h t", p=128), owner=kT_t, reads=[G["dram_t"]["gk"]],
               writes=[kT_t])
        sc.dma("sp", nwb[:], prm["gla_norm_w"][l].partition_broadcast(128), owner=nwb_t, writes=[nwb_t])
        for d in range(2):
            a2 = prm["gla_a2_f" if d == 0 else "gla_a2_b"][l]
            bi = prm["gla_a2_bias_f" if d == 0 else "gla_a2_bias_b"][l]
            sc.dma("sp", a2f[0:16, :], a2, owner=a2f_t, writes=[a2f_t])
            sc.dma("sp", a2f[16:17, :], bi.rearrange("(o n) -> o n", o=1), owner=a2f_t, writes=[a2f_t], part=True)
            sc.op("act", lambda e, d=d: e.copy(out=a2X[d][0:17, :], in_=a2f[0:17, :]), reads=[a2f_t], writes=[a2X_t[d]])

        def gla_pass(d):
            fwd = (d == 0)
            mc, mc_t = (G["mcf64b"], G["mcf64b_t"]) if fwd else (G["mcb64b"], G["mcb64b_t"])
            ma, ma_t = (G["trif64"], G["trif64_t"]) if fwd else (G["trib64"], G["trib64_t"])
            lc0 = 63 if fwd else 0
            sc.op("pool", lambda e: e.memset(gaX[:], 1.0), writes=[gaX_t])
            sc.dma("sp", gaX[0:16, :], U["ga"][16 * d:16 * d + 16, :], owner=gaX_t, reads=[G["dram_t"]["ga"]],
                   writes=[gaX_t])
            sc.op("pool", lambda e: e.memset(Sf[:], 0.0), writes=[Sf_t])
            sb0, sb0_t = r_Sb.next()
            sc.op("pool", lambda e, sb0=sb0: e.memset(sb0[:], 0.0), writes=[sb0_t])
            cur = [(sb0, sb0_t)]
            sgcur = [None]
            order = list(range(NT)) if fwd else list(range(NT - 1, -1, -1))
            chunks = (0, 1) if fwd else (1, 0)

            def stage1(i):
                tsl = slice(i * 128, (i + 1) * 128)
                v, v_t = r_v.next()
                sc.dma("sp", v[:], U["gv"][tsl, :], owner=v_t, reads=[G["dram_t"]["gv"]], writes=[v_t])
                gp, gp_t = r_gp.next()
                sc.op("pe", lambda e, gp=gp, tsl=tsl: e.matmul(gp[:], lhsT=gaX[0:17, tsl], rhs=a2X[d][0:17, :],
                                                               start=True, stop=True),
                      reads=[gaX_t, a2X_t[d]], writes=[gp_t])
                e1, e1_t = r_e1.next()
                sc.op("act", lambda e, e1=e1, gp=gp: e.activation(out=e1[:], in_=gp[:], func=AF.Exp, scale=-1.0),
                      reads=[gp_t], writes=[e1_t])
                gn, gn_t = r_gn.next()
                sc.op("act", lambda e, gn=gn, e1=e1: e.activation(out=gn[:], in_=e1[:], func=AF.Ln, bias=G["one"][:, 0:1]),
                      reads=[e1_t, G["one_t"]], writes=[gn_t])
                bT, bT_t = r_bT.next()
                for h in range(4):
                    sc.op("pe", lambda e, bT=bT, gn=gn, h=h: e.matmul(bT[:, h, :], lhsT=gn[:, h * 128:(h + 1) * 128], rhs=mc[:],
                                                                     start=True, stop=True, skip_group_check=True),
                          reads=[gn_t, mc_t], writes=[bT_t], part=(h > 0))
                bs, bs_t = bT, bT_t
                eb, eb_t = r_eb.next()
                sc.op("act", lambda e, eb=eb, bs=bs: e.activation(out=eb[:], in_=bs[:], func=AF.Exp), reads=[bs_t], writes=[eb_t])
                enb, enb_t = r_enb.next()
                sc.op("act", lambda e, enb=enb, bs=bs: e.activation(out=enb[:], in_=bs[:], func=AF.Exp, scale=-1.0),
                      reads=[bs_t], writes=[enb_t])
                ed, ed_t = r_ed.next()
                sc.op("act", lambda e, ed=ed, bs=bs: e.activation(
                    out=ed[:], in_=bs[:].rearrange("p h (c l) -> p h c l", c=2)[:, :, :, lc0], func=AF.Exp),
                    reads=[bs_t], writes=[ed_t])
                qd, qd_t = r_qd.next()
                sc.op("dve", lambda e, qd=qd, tsl=tsl, eb=eb: e.scalar_tensor_tensor(
                    out=qd[:], in0=qT[:, :, tsl], scalar=128.0 ** -0.5, in1=eb[:], op0=ALU.mult, op1=ALU.mult),
                    reads=[qT_t, eb_t], writes=[qd_t])
                kd, kd_t = r_kd.next()
                sc.op("dve", lambda e, kd=kd, tsl=tsl, enb=enb: e.tensor_tensor(
                    out=kd[:], in0=kT[:, :, tsl], in1=enb[:], op=ALU.mult), reads=[kT_t, enb_t], writes=[kd_t])
                ew, ew_t = r_ew.next()
                sc.op("dve", lambda e, ew=ew, enb=enb, ed=ed: e.tensor_tensor(
                    out=ew[:].rearrange("p h (c l) -> p (h c) l", c=2), in0=enb[:].rearrange("p h (c l) -> p (h c) l", c=2),
                    in1=ed[:].rearrange("p h c -> p (h c)").unsqueeze(2).to_broadcast([128, 8, 64]), op=ALU.mult),
                    reads=[enb_t, ed_t], writes=[ew_t])
                kw, kw_t = r_kw.next()
                sc.op("dve", lambda e, kw=kw, tsl=tsl, ew=ew: e.tensor_tensor(
                    out=kw[:], in0=kT[:, :, tsl], in1=ew[:], op=ALU.mult), reads=[kT_t, ew_t], writes=[kw_t])
                kwp, kwp_t = r_kwp.next()
                for h in range(4):
                    sc.op("pe", lambda e, kwp=kwp, kw=kw, h=h: e.transpose(out=kwp[:, h, :], in_=kw[:, h, :], identity=ident[:]),
                          reads=[kw_t, G["ident_t"]], writes=[kwp_t], part=(h > 0))
                kwt, kwt_t = r_kwt.next()
                sc.op("act", lambda e, kwt=kwt, kwp=kwp: e.copy(out=kwt[:], in_=kwp[:]), reads=[kwp_t], writes=[kwt_t])
                att, att_t = r_att.next()
                for h in range(4):
                    sc.op("pe", lambda e, att=att, kd=kd, qd=qd, h=h: e.matmul(att[:, h, :], lhsT=kd[:, h, :], rhs=qd[:, h, :],
                                                                            start=True, stop=True, skip_group_check=True),
                          reads=[kd_t, qd_t], writes=[att_t], part=(h > 0))
                am, am_t = r_am.next()
                sc.op("dve", lambda e, am=am, att=att: e.tensor_tensor(
                    out=am[:], in0=att[:], in1=ma[:].unsqueeze(1).to_broadcast([128, 4, 128]), op=ALU.mult),
                    reads=[att_t, ma_t], writes=[am_t])
                return (i, tsl, v, v_t, qd, qd_t, kwt, kwt_t, ed, ed_t, am, am_t)

            def stage23(ctx):
                (i, tsl, v, v_t, qd, qd_t, kwt, kwt_t, ed, ed_t, am, am_t) = ctx
                sbs = [cur[0]]
                for ci, c in enumerate(chunks):
                    cs = slice(c * 64, (c + 1) * 64)
                    st, st_t = r_st.next()
                    for h in range(4):
                        sc.op("pe", lambda e, st=st, kwt=kwt, cs=cs, v=v, h=h: e.matmul(
                            st[:, h, :], lhsT=kwt[cs, h, :], rhs=v[cs, h * 256:(h + 1) * 256], start=True, stop=True,
                            skip_group_check=True),
                            reads=[kwt_t, v_t], writes=[st_t], part=(h > 0))
                    for h in range(4):
                        sc.op("dve", lambda e, st=st, h=h, ed=ed, c=c: e.scalar_tensor_tensor(
                            out=Sf[:, h, :], in0=Sf[:, h, :], scalar=ed[:, h, c:c + 1], in1=st[:, h, :], op0=ALU.mult,
                            op1=ALU.add),
                            reads=[st_t, ed_t, Sf_t], writes=[Sf_t])
                    nb, nb_t = r_Sb.next()
                    sc.op("act", lambda e, nb=nb: e.copy(out=nb[:], in_=Sf[:]), reads=[Sf_t], writes=[nb_t])
                    sbs.append((nb, nb_t))
                o, o_t = r_o.next()
                for h in range(4):
                    sc.op("pe", lambda e, o=o, am=am, v=v, h=h: e.matmul(o[:, h, :], lhsT=am[:, h, :],
                                                                       rhs=v[:, h * 256:(h + 1) * 256],
                                                                       start=True, stop=False, skip_group_check=True),
                          reads=[am_t, v_t], writes=[o_t], part=(h > 0))
                    for ci, c in enumerate(chunks):
                        cs = slice(c * 64, (c + 1) * 64)
                        sbv, sbv_t = sbs[ci]
                        sc.op("pe", lambda e, o=o, qd=qd, cs=cs, sbv=sbv, ci=ci, h=h: e.matmul(
                            o[cs, h, :], lhsT=qd[:, h, cs], rhs=sbv[:, h, :], start=False, stop=(ci == 1),
                            skip_group_check=True),
                            reads=[qd_t, sbv_t], writes=[o_t], part=True)
                cur[0] = sbs[2]
                if not fwd:
                    for hb in range(2):
                        sc.op("act", lambda e, o=o, hb=hb, i=i: e.copy(
                            out=ob[:, i, hb * 512:(hb + 1) * 512], in_=o[:, 2 * hb:2 * hb + 2, :].rearrange("p a b -> p (a b)")),
                            reads=[o_t], writes=[ob_t[i]], part=(hb > 0))
                    return
                oa, oa_t = r_oa.next()
                ss, ss_t = r_ss.next()
                for hb in range(2):
                    sc.op("dve", lambda e, oa=oa, o=o, hb=hb, i=i: e.tensor_tensor(
                        out=oa[:, hb * 512:(hb + 1) * 512], in0=o[:, 2 * hb:2 * hb + 2, :].rearrange("p a b -> p (a b)"),
                        in1=ob[:, i, hb * 512:(hb + 1) * 512], op=ALU.add),
                        reads=[o_t, ob_t[i]], writes=[oa_t], part=(hb > 0))
                for h in range(4):
                    hs = slice(h * 256, (h + 1) * 256)
                    jk, jk_t = r_jk.next()
                    sc.op("dve", lambda e, jk=jk, oa=oa, hs=hs, ss=ss, h=h: e.scalar_tensor_tensor(
                        out=jk[:], in0=oa[:, hs], scalar=1.0, in1=oa[:, hs], op0=ALU.mult, op1=ALU.mult,
                        accum_out=ss[:, h:h + 1]), reads=[oa_t], writes=[jk_t, ss_t])
                if i % 4 == 0:
                    gg, gg_t = r_gg.next()
                    sc.dma("sp", gg[:], U["gg"][i * 128:(i + 4) * 128, :].rearrange("(j p) c -> p j c", p=128), owner=gg_t,
                           reads=[G["dram_t"]["gg"]], writes=[gg_t])
                    sg, sg_t = r_sg.next()
                    sc.op("act", lambda e, sg=sg, gg=gg: e.activation(out=sg[:], in_=gg[:], func=AF.Silu),
                          reads=[gg_t], writes=[sg_t])
                    sgcur[0] = (sg, sg_t)
                sg, sg_t = sgcur[0]
                sgn = sg[:, i % 4, :]
                sgn_t = sg_t
                sc.op("pool", lambda e, sgn=sgn: e.tensor_tensor(
                    out=sgn.rearrange("p (h v) -> p h v", h=4), in0=sgn.rearrange("p (h v) -> p h v", h=4),
                    in1=nwb[:].unsqueeze(1).to_broadcast([128, 4, 256]), op=ALU.mult),
                    reads=[sg_t, nwb_t], writes=[sg_t])
                sc.op("dve", lambda e, ss=ss: e.tensor_scalar(out=ss[:, 4:8], in0=ss[:, 0:4], scalar1=1.0 / 256.0, scalar2=EPS,
                                                              op0=ALU.mult, op1=ALU.add), reads=[ss_t], writes=[ss_t])
                sc.op("pool", lambda e, ss=ss: e.tensor_tensor(out=ss[:, 0:4], in0=ss[:, 4:8], in1=G["neghalf"][:, 0:4],
                                                               op=ALU.pow), reads=[ss_t, G["neghalf_t"]], writes=[ss_t])
                sc.op("dve", lambda e, oa=oa, ss=ss: e.tensor_tensor(
                    out=oa[:].rearrange("p (h v) -> p h v", h=4), in0=oa[:].rearrange("p (h v) -> p h v", h=4),
                    in1=ss[:, 0:4].unsqueeze(2).to_broadcast([128, 4, 256]), op=ALU.mult),
                    reads=[oa_t, ss_t], writes=[oa_t])
                y, y_t = r_y.next()
                sc.op("dve", lambda e, y=y, oa=oa, sgn=sgn: e.tensor_tensor(out=y[:], in0=oa[:], in1=sgn, op=ALU.mult),
                      reads=[oa_t, sgn_t], writes=[y_t])
                return (y, y_t, i)

            def stage3(c3):
                if c3 is None:
                    return
                (y, y_t, i) = c3
                emit_yT(P, sc, G, r_kwp, y, y_t, yst, yst_t, i, YB["gla"], G["dram_t"]["yb_gla"], gsz=2)

            prev = None
            prev3 = None
            for i in order:
                ctx = stage1(i)
                if prev is not None:
                    n3 = stage23(prev)
                    stage3(prev3)
                    prev3 = n3
                prev = ctx
            n3 = stage23(prev)
            stage3(prev3)
            stage3(n3)

        gla_pass(1)
        if "dbg_ob" in P.dbg:
            dob = P.dram("dbg_ob", [S, 1024], BF16)
            dt_ = sc.tile("dbg_ob")
            sc.dma("sp", dob.rearrange("(i p) c -> p i c", p=128), ob[:], owner=ob_t[0], reads=ob_t, writes=[dt_])
        gla_pass(0)
        sc.barrier(release=tiles)


def phase_B(P, sc, G, U, YB, prm, l):
    nc = P.nc
    ident = G["ident"]
    ybw = G["ybw"]
    ybw_t = G["dram_t"]["ybw"]
    with contextlib.ExitStack() as ph:
        xtok = P.sb(ph, "B_xtok", [128, NT, 1280], BF16)
        xtok_t = sc.tiles_n("B_xtok", NT)
        BT = P.sb(ph, "B_BT", [128, 2, S], BF16)
        CT = P.sb(ph, "B_CT", [128, 2, S], BF16)
        BT_t = sc.tiles_n("B_BT", 2)
        CT_t = sc.tiles_n("B_CT", 2)
        dtv = P.sb(ph, "B_dtv", [128, NT, 32], F32)
        av = P.sb(ph, "B_av", [128, NT, 32], F32)
        dtv_t = sc.tile("B_dtv")
        av_t = sc.tile("B_av")
        rows = P.sb(ph, "B_rows", [128, 4, 32], F32)
        rows_t = sc.tile("B_rows")
        nwb = P.sb(ph, "B_nwb", [128, 1024], F32)
        nwb_t = sc.tile("B_nwb")
        tiles = xtok_t + BT_t + CT_t + [dtv_t, av_t, rows_t, nwb_t]
        with contextlib.ExitStack() as s1:
            cwr = P.sb(s1, "B_cwr", [72, 128], F32)
            cwr_t = sc.tile("B_cwr")
            cw = P.sb(s1, "B_cw", [128, 72], F32)
            cw_t = sc.tile("B_cw")
            cwp = P.ps(s1, "B_cwp", [128, 72], F32)
            cwp_t = sc.tile("B_cwp")
            identf = P.sb(s1, "B_identf", [128, 128], F32)
            identf_t = sc.tile("B_identf")
            xc = [P.sb(s1, "B_xc%d" % i, [128, S + 4], BF16) for i in range(2)]
            xc_t = sc.tiles_n("B_xc", 2)
            dg = [P.sb(s1, "B_dg%d" % i, [128, 5, 128], BF16) for i in range(2)]
            dg_t = sc.tiles_n("B_dg", 2)
            cacc = [P.ps(s1, "B_cacc%d" % i, [128, 512], F32) for i in range(2)]
            cacc_t = sc.tiles_n("B_cacc", 2)
            xa = [P.sb(s1, "B_xa%d" % i, [128, S], BF16) for i in range(2)]
            xa_t = sc.tiles_n("B_xa", 2)
            tp = [P.ps(s1, "B_tp%d" % i, [128, 4, 128], BF16) for i in range(2)]
            tp_t = sc.tiles_n("B_tp", 2)
            tl1 = [cwr_t, cw_t, cwp_t, identf_t] + dg_t + cacc_t + xc_t + xa_t + tp_t
            sc.op("pool", lambda e: e.affine_select(out=identf[:], in_=G["ones_f"][:], pattern=[[-1, 128]],
                                                    compare_op=ALU.is_equal, fill=0.0, base=0, channel_multiplier=1),
                  reads=[G["ones_t"]], writes=[identf_t])
            sc.dma("sp", cwr[0:60, :], prm["ssd_conv_w"][l].rearrange("k (c p) -> (k c) p", p=128), owner=cwr_t, writes=[cwr_t])
            sc.dma("sp", cwr[60:72, :], prm["ssd_conv_b"][l].rearrange("(c p) -> c p", p=128), owner=cwr_t, writes=[cwr_t],
                   part=True)
            sc.op("pe", lambda e: e.transpose(out=cwp[:], in_=cwr[:], identity=identf[0:72, 0:72]),
                  reads=[cwr_t, identf_t], writes=[cwp_t])
            sc.op("act", lambda e: e.copy(out=cw[:], in_=cwp[:]), reads=[cwp_t], writes=[cw_t])
            for b in range(2):
                sc.op("pool", lambda e, b=b: e.memset(xc[b][:, 0:2], 0.0), writes=[xc_t[b]])
                sc.op("pool", lambda e, b=b: e.memset(xc[b][:, S + 2:S + 4], 0.0), writes=[xc_t[b]], part=True)
            sc.dma("sp", dtv[:], U["dt"].rearrange("(i p) c -> p i c", p=128), owner=dtv_t, reads=[G["dram_t"]["dt"]],
                   writes=[dtv_t])
            for k, nm in enumerate(("ssd_dt_bias_f", "ssd_dt_bias_b")):
                sc.dma("sp", rows[:, 0, 16 * k:16 * k + 16], prm[nm][l].partition_broadcast(128), owner=rows_t,
                       writes=[rows_t], part=True)
            for k, nm in enumerate(("ssd_a_log_f", "ssd_a_log_b")):
                sc.dma("sp", rows[:, 1, 16 * k:16 * k + 16], prm[nm][l].partition_broadcast(128), owner=rows_t,
                       writes=[rows_t], part=True)
            sc.dma("sp", rows[:, 2, 0:16], prm["ssd_d"][l].partition_broadcast(128), owner=rows_t, writes=[rows_t], part=True)
            sc.dma("sp", nwb[:], prm["ssd_norm_w"][l].partition_broadcast(128), owner=nwb_t, writes=[nwb_t])
            sc.op("dve", lambda e: e.tensor_tensor(out=dtv[:], in0=dtv[:], in1=rows[:, 0:1, :].to_broadcast([128, NT, 32]),
                                                   op=ALU.add), reads=[dtv_t, rows_t], writes=[dtv_t])
            sc.op("act", lambda e: e.activation(out=dtv[:], in_=dtv[:], func=AF.Exp), reads=[dtv_t], writes=[dtv_t])
            sc.op("act", lambda e: e.activation(out=dtv[:], in_=dtv[:], func=AF.Ln, bias=G["one"][:, 0:1]),
                  reads=[dtv_t, G["one_t"]], writes=[dtv_t])
            sc.op("act", lambda e: e.activation(out=rows[:, 3, :], in_=rows[:, 1, :], func=AF.Exp), reads=[rows_t],
                  writes=[rows_t])
            sc.op("dve", lambda e: e.scalar_tensor_tensor(out=av[:], in0=dtv[:], scalar=-1.0,
                                                          in1=rows[:, 3:4, :].to_broadcast([128, NT, 32]),
                                                          op0=ALU.mult, op1=ALU.mult),
                  reads=[dtv_t, rows_t], writes=[av_t])
            tpc = 0
            for c in range(12):
                b = c % 2
                sc.dma("sp", xc[b][:, 2:S + 2], U["xbc"][c * 128:(c + 1) * 128, :], owner=xc_t[b],
                       reads=[G["dram_t"]["xbc"]], writes=[xc_t[b]], part=True)
                dgb = c % 2
                for k in range(5):
                    sc.op("dve", lambda e, dgb=dgb, k=k, c=c: e.tensor_scalar(
                        out=dg[dgb][:, k, :], in0=identf[:], scalar1=cw[:, k * 12 + c:k * 12 + c + 1], scalar2=None,
                        op0=ALU.mult), reads=[identf_t, cw_t], writes=[dg_t[dgb]], part=(k > 0))
                if c < 10:
                    xo, xo_t = xa[b], xa_t[b]
                    xsl = lambda tb: xa[b][:, tb * 512:(tb + 1) * 512]
                else:
                    xo_t = CT_t[c - 10]
                    xsl = lambda tb, c=c: CT[:, c - 10, tb * 512:(tb + 1) * 512]
                for tb in range(4):
                    ca, ca_t = cacc[(4 * c + tb) % 2], cacc_t[(4 * c + tb) % 2]
                    for k in range(5):
                        sc.op("pe", lambda e, ca=ca, dgb=dgb, k=k, b=b, tb=tb: e.matmul(
                            ca[:], lhsT=dg[dgb][:, k, :], rhs=xc[b][:, k + tb * 512:k + tb * 512 + 512],
                            start=(k == 0), stop=(k == 4)),
                            reads=[dg_t[dgb], xc_t[b]], writes=[ca_t], part=(k > 0))
                    sc.op("act", lambda e, ca=ca, o_ap=xsl(tb), c=c: e.activation(
                        out=o_ap, in_=ca[:], func=AF.Silu, bias=cw[:, 60 + c:61 + c]),
                        reads=[ca_t, cw_t], writes=[xo_t], part=(tb > 0))
                if c in (8, 9):
                    sc.op("pool", lambda e, b=b, c=c: e.tensor_copy(out=BT[:, c - 8, :], in_=xa[b][:]),
                          reads=[xa_t[b]], writes=[BT_t[c - 8]])
                if c < 10:
                    for i0 in range(0, NT, 4):
                        tb_ = tpc % 2
                        tpc += 1
                        for j in range(4):
                            i = i0 + j
                            sc.op("pe", lambda e, tb_=tb_, j=j, b=b, i=i: e.transpose(
                                out=tp[tb_][:, j, :], in_=xa[b][:, i * 128:(i + 1) * 128], identity=ident[:]),
                                reads=[xa_t[b], G["ident_t"]], writes=[tp_t[tb_]], part=(j > 0))
                        eng = "act" if (tpc % 2) else "pool"
                        if eng == "act":
                            sc.op("act", lambda e, tb_=tb_, i0=i0, c=c: e.copy(
                                out=xtok[:, i0:i0 + 4, c * 128:(c + 1) * 128], in_=tp[tb_][:]),
                                reads=[tp_t[tb_]], writes=xtok_t[i0:i0 + 4], part=True)
                        else:
                            sc.op("dve", lambda e, tb_=tb_, i0=i0, c=c: e.tensor_copy(
                                out=xtok[:, i0:i0 + 4, c * 128:(c + 1) * 128], in_=tp[tb_][:]),
                                reads=[tp_t[tb_]], writes=xtok_t[i0:i0 + 4], part=True)
            sc.barrier(release=tl1)
        with contextlib.ExitStack() as s2:
            R = lambda name, shape, dt, n, psum=False: Ring(P, sc, s2, "B_" + name, shape, dt, n, psum)
            Sf = P.sb(s2, "B_Sf", [128, 2, 512], F32)
            Sf_t = sc.tiles_n("B_Sf", 2)
            Sbx = P.sb(s2, "B_Sb", [128, 2, 2, 512], BF16)
            r_Sb = [Ring(P, sc, s2, "B_Sb%d" % g, None, None, 2, views=[Sbx[:, g, k, :] for k in range(2)]) for g in range(2)]
            r_cb = R("cb", [128, 128], F32, 1, True)
            r_seg = R("seg", [128, 512], F32, 2, True)
            r_sm = R("sm", [128, 3, 16], F32, 1, True)
            r_yd = R("yd", [128, 512], F32, 1, True)
            r_stp = R("stp", [128, 512], F32, 1, True)
            r_yo = R("yo", [128, 512], F32, 1, True)
            r_tp = R("tp2", [128, 4, 128], BF16, 1, True)
            r_cbm = R("cbm", [128, 128], F32, 2)
            r_am = R("am", [128, 4, 128], BF16, 4)
            r_dec = R("dec", [128, 4, 128], F32, 2)
            r_mt = R("mt", [128, 4, 128], BF16, 4)
            r_ea = R("ea", [128, 3, 16], F32, 2)
            r_xdt = R("xdt", [128, 1024], BF16, 3)
            r_xw = R("xw", [128, 1024], BF16, 2)
            r_t = R("t", [128, 512], F32, 2)
            r_ybl = R("ybl", [128, 1024], BF16, 2)
            r_yf = R("yf", [128, 1024], F32, 1)
            r_z = R("z", [128, 4, 1024], BF16, 1)
            r_jk = R("jk", [128, 512], F32, 1)
            r_ss = R("ss", [128, 4], F32, 2)
            r_y = R("y", [128, 1024], BF16, 2)
            yst = P.sb(s2, "B_yst", [128, 8, 256], BF16)
            yst_t = sc.tile("B_yst")
            rings = [r_cb, r_seg, r_sm, r_yd, r_stp, r_yo, r_tp, r_cbm, r_am, r_dec, r_mt, r_ea, r_xdt, r_xw, r_t, r_ybl,
                     r_yf, r_z, r_jk, r_ss, r_y] + r_Sb
            tl2 = Sf_t + [yst_t]
            for r in rings:
                tl2 += r.t

            def ssd_pass(d):
                fwd = (d == 0)
                tri_in, tri_in_t = (G["trif"], G["trif_t"]) if fwd else (G["trib"], G["trib_t"])
                tri_st, tri_st_t = (G["tribs"], G["tribs_t"]) if fwd else (G["trifs"], G["trifs_t"])
                tri_sb, tri_sb_t = (G["tribsb"], G["tribsb_t"]) if fwd else (G["trifsb"], G["trifsb_t"])
                cur = []
                for g in range(2):
                    sc.op("pool", lambda e, g=g: e.memset(Sf[:, g, :], 0.0), writes=[Sf_t[g]])
                    sb0, sb0_t = r_Sb[g].next()
                    sc.op("pool", lambda e, sb0=sb0: e.memset(sb0, 0.0), writes=[sb0_t])
                    cur.append((sb0, sb0_t))
                order = list(range(NT)) if fwd else list(range(NT - 1, -1, -1))
                zcur = [None]

                def tileA(i):
                    tsl = slice(i * 128, (i + 1) * 128)
                    acol = av[:, i, 16 * d:16 * d + 16]
                    ams = []
                    for u in range(4):
                        h0 = u * 4
                        am, am_t = r_am.next()
                        for hh in range(4):
                            sc.op("act", lambda e, am=am, i=i, h0=h0, hh=hh: e.activation(
                                out=am[:, hh, :], in_=tri_in[:], func=AF.Copy,
                                scale=av[:, i, 16 * d + h0 + hh:16 * d + h0 + hh + 1]),
                                reads=[tri_in_t, av_t], writes=[am_t], part=(hh > 0))
                        ams.append((am, am_t))
                    sm, sm_t = r_sm.next()
                    sc.op("pe", lambda e, sm=sm, acol=acol: e.matmul(sm[:, 0, :], lhsT=tri_in[:], rhs=acol, start=True, stop=True),
                          reads=[tri_in_t, av_t], writes=[sm_t])
                    sc.op("pe", lambda e, sm=sm, acol=acol: e.matmul(sm[:, 1, :], lhsT=tri_st[:], rhs=acol, start=True, stop=True),
                          reads=[tri_st_t, av_t], writes=[sm_t], part=True)
                    sc.op("pe", lambda e, sm=sm, acol=acol: e.matmul(sm[:, 2, :], lhsT=G["ones_f"][:], rhs=acol, start=True,
                                                                     stop=True),
                          reads=[G["ones_t"], av_t], writes=[sm_t], part=True)
                    ea, ea_t = r_ea.next()
                    sc.op("act", lambda e, ea=ea, sm=sm: e.activation(out=ea[:], in_=sm[:], func=AF.Exp), reads=[sm_t],
                          writes=[ea_t])
                    xdt, xdt_t = r_xdt.next()
                    sc.op("dve", lambda e, xdt=xdt, i=i: e.tensor_tensor(
                        out=xdt[:].rearrange("p (h q) -> p h q", q=64), in0=xtok[:, i, 0:1024].rearrange("p (h q) -> p h q", q=64),
                        in1=dtv[:, i, 16 * d:16 * d + 16].unsqueeze(2).to_broadcast([128, 16, 64]), op=ALU.mult),
                        reads=[xtok_t[i], dtv_t], writes=[xdt_t])
                    xw, xw_t = r_xw.next()
                    sc.op("dve", lambda e, xw=xw, xdt=xdt, ea=ea: e.tensor_tensor(
                        out=xw[:].rearrange("p (h q) -> p h q", q=64), in0=xdt[:].rearrange("p (h q) -> p h q", q=64),
                        in1=ea[:, 1, :].unsqueeze(2).to_broadcast([128, 16, 64]), op=ALU.mult),
                        reads=[xdt_t, ea_t], writes=[xw_t])
                    ybl, ybl_t = r_ybl.next()
                    if fwd:
                        sc.dma("sp", ybl[:], ybw[tsl, :], owner=ybl_t, reads=[ybw_t], writes=[ybl_t])
                        yf, yf_t = r_yf.next()
                    ts = []
                    for g in range(2):
                        stp, stp_t = r_stp.next()
                        sc.op("pe", lambda e, stp=stp, i=i, g=g, xw=xw: e.matmul(
                            stp[:], lhsT=xtok[:, i, 1024 + g * 128:1024 + (g + 1) * 128], rhs=xw[:, g * 512:(g + 1) * 512],
                            start=True, stop=True), reads=[xtok_t[i], xw_t], writes=[stp_t])
                        yo, yo_t = r_yo.next()
                        sbv, sbv_t = cur[g]
                        sc.op("pe", lambda e, yo=yo, g=g, tsl=tsl, sbv=sbv: e.matmul(yo[:], lhsT=CT[:, g, tsl], rhs=sbv,
                                                                                    start=True, stop=True),
                              reads=[CT_t[g], sbv_t], writes=[yo_t])
                        sc.op("pool", lambda e, g=g, ea=ea: e.tensor_tensor(
                            out=Sf[:, g, :].rearrange("p (h q) -> p h q", q=64), in0=Sf[:, g, :].rearrange("p (h q) -> p h q", q=64),
                            in1=ea[:, 2, g * 8:(g + 1) * 8].unsqueeze(2).to_broadcast([128, 8, 64]), op=ALU.mult),
                            reads=[Sf_t[g], ea_t], writes=[Sf_t[g]])
                        sc.op("dve", lambda e, g=g, stp=stp: e.tensor_tensor(out=Sf[:, g, :], in0=Sf[:, g, :], in1=stp[:],
                                                                            op=ALU.add),
                              reads=[Sf_t[g], stp_t], writes=[Sf_t[g]])
                        nb, nb_t = r_Sb[g].next()
                        sc.op("act", lambda e, nb=nb, g=g: e.copy(out=nb, in_=Sf[:, g, :]), reads=[Sf_t[g]], writes=[nb_t])
                        cur[g] = (nb, nb_t)
                        t, t_t = r_t.next()
                        sc.op("dve", lambda e, t=t, yo=yo, ea=ea, g=g: e.tensor_tensor(
                            out=t[:].rearrange("p (h q) -> p h q", q=64), in0=yo[:].rearrange("p (h q) -> p h q", q=64),
                            in1=ea[:, 0, g * 8:(g + 1) * 8].unsqueeze(2).to_broadcast([128, 8, 64]), op=ALU.mult),
                            reads=[yo_t, ea_t], writes=[t_t])
                        ts.append((t, t_t))
                    cbms = []
                    for g in range(2):
                        cb, cb_t = r_cb.next()
                        sc.op("pe", lambda e, cb=cb, g=g, tsl=tsl: e.matmul(cb[:], lhsT=BT[:, g, tsl], rhs=CT[:, g, tsl],
                                                                           start=True, stop=True),
                              reads=[BT_t[g], CT_t[g]], writes=[cb_t])
                        cbm, cbm_t = r_cbm.next()
                        sc.op("dve", lambda e, cbm=cbm, cb=cb: e.tensor_tensor(out=cbm[:], in0=cb[:], in1=tri_in[:], op=ALU.mult),
                              reads=[cb_t, tri_in_t], writes=[cbm_t])
                        cbms.append((cbm, cbm_t))
                    mts = []
                    for pair in range(2):
                        segs = []
                        for u in (2 * pair, 2 * pair + 1):
                            am, am_t = ams[u]
                            seg, seg_t = r_seg.next()
                            sc.op("pe", lambda e, seg=seg, am=am: e.matmul(seg[:], lhsT=tri_sb[:],
                                                                           rhs=am[:].rearrange("p a b -> p (a b)"),
                                                                           start=True, stop=True),
                                  reads=[tri_sb_t, am_t], writes=[seg_t])
                            segs.append((seg, seg_t))
                        decs = []
                        for (seg, seg_t) in segs:
                            dec, dec_t = r_dec.next()
                            sc.op("act", lambda e, dec=dec, seg=seg: e.activation(out=dec[:].rearrange("p a b -> p (a b)"),
                                                                                 in_=seg[:], func=AF.Exp),
                                  reads=[seg_t], writes=[dec_t])
                            decs.append((dec, dec_t))
                        for k, (dec, dec_t) in enumerate(decs):
                            u = 2 * pair + k
                            cbm, cbm_t = cbms[u // 2]
                            mt, mt_t = r_mt.next()
                            sc.op("dve", lambda e, mt=mt, dec=dec, cbm=cbm: e.tensor_tensor(
                                out=mt[:], in0=dec[:], in1=cbm[:].unsqueeze(1).to_broadcast([128, 4, 128]), op=ALU.mult),
                                reads=[dec_t, cbm_t], writes=[mt_t])
                            mts.append((mt, mt_t))
                    for g in range(2):
                        yd, yd_t = r_yd.next()
                        if fwd:
                            sc.op("pe", lambda e, yd=yd, ybl=ybl, g=g: e.matmul(
                                yd[:], lhsT=ident[:], rhs=ybl[:, g * 512:(g + 1) * 512], start=True, stop=False,
                                skip_group_check=True), reads=[G["ident_t"], ybl_t], writes=[yd_t])
                        for q4 in range(2):
                            mt, mt_t = mts[g * 2 + q4]
                            for hh in range(4):
                                h = g * 8 + q4 * 4 + hh
                                hl = h - g * 8
                                sc.op("pe", lambda e, yd=yd, mt=mt, hh=hh, hl=hl, h=h, xdt=xdt: e.matmul(
                                    yd[:, hl * 64:(hl + 1) * 64], lhsT=mt[:, hh, :], rhs=xdt[:, h * 64:(h + 1) * 64],
                                    start=(not fwd), stop=True, skip_group_check=True),
                                    reads=[mt_t, xdt_t], writes=[yd_t], part=(fwd or not (q4 == 0 and hh == 0)))
                        t, t_t = ts[g]
                        gs = slice(g * 512, (g + 1) * 512)
                        if not fwd:
                            sc.op("dve", lambda e, t=t, yd=yd, ybl=ybl, gs=gs: e.tensor_tensor(out=ybl[:, gs], in0=t[:], in1=yd[:],
                                                                                              op=ALU.add),
                                  reads=[t_t, yd_t], writes=[ybl_t], part=(g > 0))
                        else:
                            sc.op("dve", lambda e, t=t, yd=yd, yf=yf, gs=gs: e.tensor_tensor(out=yf[:, gs], in0=t[:], in1=yd[:],
                                                                                            op=ALU.add),
                                  reads=[t_t, yd_t], writes=[yf_t], part=(g > 0))
                    if not fwd:
                        sc.dma("pool", ybw[tsl, :], ybl[:], owner=ybl_t, reads=[ybl_t], writes=[ybw_t], part=True)
                        return None
                    if i % 4 == 0:
                        z, z_t = r_z.next()
                        sc.dma("sp", z[:], U["z"][i * 128:(i + 4) * 128, :].rearrange("(j p) c -> p j c", p=128), owner=z_t,
                               reads=[G["dram_t"]["z"]], writes=[z_t])
                        sc.op("act", lambda e, z=z: e.activation(out=z[:], in_=z[:], func=AF.Silu), reads=[z_t], writes=[z_t])
                        zcur[0] = (z, z_t)
                    z, z_t = zcur[0]
                    sz = z[:, i % 4, :]
                    sz_t = z_t
                    xd, xd_t = r_xdt.next()
                    sc.op("pool", lambda e, xd=xd, i=i: e.tensor_tensor(
                        out=xd[:].rearrange("p (h q) -> p h q", q=64), in0=xtok[:, i, 0:1024].rearrange("p (h q) -> p h q", q=64),
                        in1=rows[:, 2, 0:16].unsqueeze(2).to_broadcast([128, 16, 64]), op=ALU.mult),
                        reads=[xtok_t[i], rows_t], writes=[xd_t])
                    sc.op("dve", lambda e, yf=yf, xd=xd: e.tensor_tensor(out=yf[:], in0=yf[:], in1=xd[:], op=ALU.add),
                          reads=[yf_t, xd_t], writes=[yf_t])
                    sc.op("dve", lambda e, yf=yf, sz=sz: e.tensor_tensor(out=yf[:], in0=yf[:], in1=sz, op=ALU.mult),
                          reads=[yf_t, sz_t], writes=[yf_t])
                    ss, ss_t = r_ss.next()
                    for g in range(2):
                        gs = slice(g * 512, (g + 1) * 512)
                        jk, jk_t = r_jk.next()
                        sc.op("dve", lambda e, jk=jk, yf=yf, gs=gs, ss=ss, g=g: e.scalar_tensor_tensor(
                            out=jk[:], in0=yf[:, gs], scalar=1.0, in1=yf[:, gs], op0=ALU.mult, op1=ALU.mult,
                            accum_out=ss[:, g:g + 1]), reads=[yf_t], writes=[jk_t, ss_t])
                    sc.op("dve", lambda e, ss=ss: e.tensor_scalar(out=ss[:, 2:4], in0=ss[:, 0:2], scalar1=1.0 / 512.0, scalar2=EPS,
                                                                  op0=ALU.mult, op1=ALU.add), reads=[ss_t], writes=[ss_t])
                    sc.op("pool", lambda e, ss=ss: e.tensor_tensor(out=ss[:, 0:2], in0=ss[:, 2:4], in1=G["neghalf"][:, 0:2],
                                                                   op=ALU.pow), reads=[ss_t, G["neghalf_t"]], writes=[ss_t])
                    y, y_t = r_y.next()
                    for g in range(2):
                        gs = slice(g * 512, (g + 1) * 512)
                        sc.op("dve", lambda e, y=y, yf=yf, gs=gs, ss=ss, g=g: e.scalar_tensor_tensor(
                            out=y[:, gs], in0=yf[:, gs], scalar=ss[:, g:g + 1], in1=nwb[:, gs], op0=ALU.mult, op1=ALU.mult),
                            reads=[yf_t, ss_t, nwb_t], writes=[y_t], part=(g > 0))
                    return (y, y_t, i)

                def tileC(c3):
                    if c3 is None:
                        return
                    (y, y_t, i) = c3
                    emit_yT(P, sc, G, r_tp, y, y_t, yst, yst_t, i, YB["ssd"], G["dram_t"]["yb_ssd"], gsz=2)

                prev3 = None
                for i in order:
                    n3 = tileA(i)
                    tileC(prev3)
                    prev3 = n3
                tileC(prev3)

            ssd_pass(1)
            ssd_pass(0)
            sc.barrier(release=tl2)
        sc.barrier(release=tiles)


def emit_yT(P, sc, G, r_tp, y, y_t, yst, yst_t, i, dst, dst_t, gsz=4):
    ident = G["ident"]
    for half in range(2):
        tp, tp_t = r_tp.next()
        for jq in range(4):
            c = half * 4 + jq
            sc.op("pe", lambda e, tp=tp, jq=jq, c=c: e.transpose(out=tp[:, jq, :], in_=y[:, c * 128:(c + 1) * 128],
                                                               identity=ident[:]),
                  reads=[y_t, G["ident_t"]], writes=[tp_t], part=(jq > 0))
        sc.op("act", lambda e, tp=tp, half=half: e.copy(
            out=yst[:, half * 4:half * 4 + 4, (i % gsz) * 128:(i % gsz + 1) * 128], in_=tp[:]),
            reads=[tp_t], writes=[yst_t], part=not (i % gsz == 0 and half == 0))
    if i % gsz == gsz - 1:
        yv = dst.rearrange("(c p) t -> p c t", p=128)
        sc.dma("pool", yv[:, :, (i - gsz + 1) * 128:(i + 1) * 128], yst[:], owner=yst_t, reads=[yst_t], writes=[dst_t],
               part=True)


NEG = -30000.0


def na_r0(r):
    return min(max(r - 4, 0), 24)


def na_valid(kr, qr):
    return na_r0(qr) <= kr < na_r0(qr) + 8


def phase_D(P, sc, G, U, YB, prm, natt, l):
    nc = P.nc
    with contextlib.ExitStack() as ph:
        qnT = P.sb(ph, "D_qnT", [128, 8, S], BF16)
        knT = P.sb(ph, "D_knT", [128, 8, S], BF16)
        qn_t = sc.tiles_n("D_qn", 8)
        kn_t = sc.tiles_n("D_kn", 8)
        TT = P.sb(ph, "D_TT", [128, 8, 20, 64], BF16)
        TT_t = sc.tiles_n("D_TT", 4)
        wcol = P.sb(ph, "D_wcol", [128, 4], F32)
        wcol_t = sc.tile("D_wcol")
        tiles = qn_t + kn_t + TT_t + [wcol_t]
        with contextlib.ExitStack() as s1:
            TTf = [P.sb(s1, "D_TTf%d" % i, [128, 2, 20, 64], F32) for i in range(1)]
            TTf_t = sc.tiles_n("D_TTf", 1)
            qc_ = [P.sb(s1, "D_qc%d" % i, [128, S], BF16) for i in range(3)]
            qc_t = sc.tiles_n("D_qc", 3)
            sq = [P.sb(s1, "D_sq%d" % i, [128, S], BF16) for i in range(2)]
            sq_t = sc.tiles_n("D_sq", 2)
            lnv = [P.sb(s1, "D_ln%d" % i, [128, S], F32) for i in range(2)]
            lnv_t = sc.tiles_n("D_ln", 2)
            bones = P.sb(s1, "D_bones", [128, 128], BF16)
            bones_t = sc.tile("D_bones")
            ssp = [P.ps(s1, "D_ssp%d" % i, [128, S], F32) for i in range(2)]
            ssp_t = sc.tiles_n("D_ssp", 2)
            t1 = TTf_t + qc_t + sq_t + lnv_t + [bones_t] + ssp_t
            def tt_piece(g):
                b = 0
                sc.dma("sp", TTf[b][:], natt[l][:, 2 * g:2 * g + 2, :, :], owner=TTf_t[b], writes=[TTf_t[b]])
                sc.op("pool", lambda e, b=b, g=g: e.tensor_copy(out=TT[:, 2 * g:2 * g + 2, :, :], in_=TTf[b][:]),
                      reads=[TTf_t[b]], writes=[TT_t[g]])

            for hh in range(2):
                sc.dma("sp", wcol[hh * 64:(hh + 1) * 64, 2:3], prm["na_q_norm_w"][l].rearrange("(d o) -> d o", o=1),
                       owner=wcol_t, writes=[wcol_t], part=True)
                sc.dma("sp", wcol[hh * 64:(hh + 1) * 64, 1:2], prm["na_k_norm_w"][l].rearrange("(d o) -> d o", o=1),
                       owner=wcol_t, writes=[wcol_t], part=True)
            sc.op("dve", lambda e: e.tensor_scalar(out=wcol[:, 0:1], in0=wcol[:, 2:3], scalar1=0.125, scalar2=None,
                                                   op0=ALU.mult), reads=[wcol_t], writes=[wcol_t])
            sc.op("pool", lambda e: e.memset(bones[:], 0.0), writes=[bones_t])
            sc.op("pool", lambda e: e.memset(bones[0:64, 0:64], 1.0), reads=[bones_t], writes=[bones_t])
            sc.op("pool", lambda e: e.memset(bones[64:128, 64:128], 1.0), reads=[bones_t], writes=[bones_t])
            jobs = []
            for which, (src, dstT, dst_t, wc) in enumerate(((U["nq"], qnT, qn_t, 0), (U["nk"], knT, kn_t, 1))):
                src_t = G["dram_t"]["nq" if which == 0 else "nk"]
                for c in range(8):
                    jobs.append((src, src_t, dstT, dst_t, wc, c))

            def n_s1(k):
                (src, src_t, dstT, dst_t, wc, c) = jobs[k]
                cb = k % 2
                q3 = k % 3
                sc.dma("sp", qc_[q3][:], src[c * 128:(c + 1) * 128, :], owner=qc_t[q3], reads=[src_t], writes=[qc_t[q3]])
                sc.op("dve", lambda e, cb=cb, q3=q3: e.tensor_tensor(out=sq[cb][:], in0=qc_[q3][:], in1=qc_[q3][:], op=ALU.mult),
                      reads=[qc_t[q3]], writes=[sq_t[cb]])
                for tb in range(4):
                    sl = slice(tb * 512, (tb + 1) * 512)
                    sc.op("pe", lambda e, cb=cb, sl=sl: e.matmul(ssp[cb][:, sl], lhsT=bones[:], rhs=sq[cb][:, sl], start=True,
                                                                 stop=True),
                          reads=[bones_t, sq_t[cb]], writes=[ssp_t[cb]], part=(tb > 0))
                sc.op("act", lambda e, cb=cb: e.activation(out=lnv[cb][:], in_=ssp[cb][:], func=AF.Ln,
                                                           bias=G["eps"][:, 0:1], scale=1.0 / 64.0),
                      reads=[ssp_t[cb], G["eps_t"]], writes=[lnv_t[cb]])
                sc.op("act", lambda e, cb=cb: e.activation(out=lnv[cb][:], in_=lnv[cb][:], func=AF.Exp, scale=-0.5),
                      reads=[lnv_t[cb]], writes=[lnv_t[cb]])

            def n_s2(k):
                (src, src_t, dstT, dst_t, wc, c) = jobs[k]
                cb = k % 2
                q3 = k % 3
                sc.op("dve", lambda e, cb=cb, q3=q3, dstT=dstT, c=c, wc=wc: e.scalar_tensor_tensor(
                    out=dstT[:, c, :], in0=qc_[q3][:], scalar=wcol[:, wc:wc + 1], in1=lnv[cb][:],
                    op0=ALU.mult, op1=ALU.mult),
                    reads=[qc_t[q3], lnv_t[cb], wcol_t], writes=[dst_t[c]])

            n_s1(0)
            for k in range(len(jobs)):
                if k + 1 < len(jobs):
                    n_s1(k + 1)
                n_s2(k)
                if k % 4 == 1:
                    tt_piece(k // 4)
            sc.barrier(release=t1)
        with contextlib.ExitStack() as s2:
            vx = P.sb(s2, "D_vx", [128, NT, 16, 65], BF16)
            vx_t = sc.tiles_n("D_vx", NT)
            sps = [P.ps(s2, "D_sps%d" % i, [128, 8, 128], F32) for i in range(2)]
            sps_t = sc.tiles_n("D_sps", 2)
            pT = [P.sb(s2, "D_pT%d" % i, [128, 5, 128], BF16) for i in range(3)]
            pT_t = sc.tiles_n("D_pT", 3)
            po = [P.ps(s2, "D_po%d" % i, [128, 2, 66], F32) for i in range(2)]
            po_t = sc.tiles_n("D_po", 2)
            rc = [P.sb(s2, "D_rc%d" % i, [128, 2], F32) for i in range(2)]
            rc_t = sc.tiles_n("D_rc", 2)
            ot = [P.sb(s2, "D_ot%d" % i, [128, 1024], BF16) for i in range(2)]
            ot_t = sc.tiles_n("D_ot", 2)
            tp = [P.ps(s2, "D_tp%d" % i, [128, 4, 128], BF16) for i in range(2)]
            tp_t = sc.tiles_n("D_tp", 2)
            yst = P.sb(s2, "D_yst", [128, 8, 512], BF16)
            yst_t = sc.tile("D_yst")
            t2 = vx_t + sps_t + pT_t + po_t + rc_t + ot_t + tp_t + [yst_t]
            nvv = U["nv"].rearrange("(i p) (h d) -> p i h d", p=128, d=64)
            for i in range(NT):
                sc.op("pool", lambda e, i=i: e.memset(vx[:, i, :, 64:65], 1.0), writes=[vx_t[i]])
                sc.dma("sp", vx[:, i, :, 0:64], nvv[:, i, :, :], owner=vx_t[i], reads=[G["dram_t"]["nv"]],
                       writes=[vx_t[i]], part=True)
            ident = G["ident"]
            for k3 in range(3):
                sc.op("pool", lambda e, k3=k3: e.memset(pT[k3][:, 4, 0:64], 0.0), writes=[pT_t[k3]])
            tpc = [0]
            units = []
            for i in range(NT):
                jlo = na_r0(2 * i) // 2
                jhi = (na_r0(2 * i + 1) + 7) // 2
                js = list(range(jlo, jhi + 1))
                for hp in range(8):
                    for hh in range(2):
                        units.append((i, hp, hh, js))

            def emit_S(u):
                i, hp, hh, js = units[u]
                h = 2 * hp + hh
                p0 = 64 * hh
                sb_ = u % 2
                for jj, j in enumerate(js):
                    trim = (len(js) == 5 and jj == 4)
                    q0 = 64 if trim else 0
                    sc.op("pe", lambda e, sb_=sb_, jj=jj, j=j, p0=p0, hp=hp, i=i, q0=q0: e.matmul(
                        sps[sb_][:, jj, q0:128], lhsT=knT[p0:p0 + 64, hp, j * 128:(j + 1) * 128],
                        rhs=qnT[p0:p0 + 64, hp, i * 128 + q0:(i + 1) * 128], start=True, stop=False,
                        skip_group_check=True),
                        reads=[kn_t[hp], qn_t[hp]], writes=[sps_t[sb_]], part=(jj > 0))
                    mms = []
                    for b0 in ((1,) if trim else (0, 1)):
                        qr = 2 * i + b0
                        va = [na_valid(2 * j + a, qr) for a in range(2)]
                        dr0 = 2 * j - qr + 7
                        cs = slice(b0 * 64, (b0 + 1) * 64)
                        if va[0] and va[1]:
                            mms.append((slice(0, 128), cs, TT[p0:p0 + 64, hp, dr0:dr0 + 2, :]))
                        elif not va[0] and not va[1]:
                            mms.append((slice(0, 128), cs, TT[p0:p0 + 64, hp, 15:17, :]))
                        elif (not va[0]) and va[1] and dr0 + 1 == 3:
                            mms.append((slice(0, 128), cs, TT[p0:p0 + 64, hp, 16:18, :]))
                        elif va[0] and (not va[1]) and dr0 == 10:
                            mms.append((slice(0, 128), cs, TT[p0:p0 + 64, hp, 18:20, :]))
                        else:
                            d0 = dr0 if va[0] else 15
                            d1 = dr0 + 1 if va[1] else 16
                            mms.append((slice(0, 64), cs, TT[p0:p0 + 64, hp, d0, :]))
                            mms.append((slice(64, 128), cs, TT[p0:p0 + 64, hp, d1, :]))
                    for mi, (ps_, cs, lhs) in enumerate(mms):
                        sc.op("pe", lambda e, sb_=sb_, jj=jj, ps_=ps_, cs=cs, lhs=lhs, p0=p0, last=(mi == len(mms) - 1):
                              e.matmul(sps[sb_][ps_, jj, cs], lhsT=lhs, rhs=ident[p0:p0 + 64, p0:p0 + 64],
                                       start=False, stop=last, skip_group_check=True),
                              reads=[TT_t[hp // 2], G["ident_t"]], writes=[sps_t[sb_]], part=True)

            def emit_rest(u):
                i, hp, hh, js = units[u]
                h = 2 * hp + hh
                sb_ = u % 2
                pt = u % 3
                pb_ = (u // 2) % 2
                ob = i % 2
                n = len(js)
                n1 = min(n, 4)
                sc.op("act", lambda e, pt=pt, sb_=sb_, n1=n1: e.activation(out=pT[pt][:, 0:n1, :],
                                                                         in_=sps[sb_][:, 0:n1, :], func=AF.Exp),
                      reads=[sps_t[sb_]], writes=[pT_t[pt]])
                if n > 4:
                    sc.op("act", lambda e, pt=pt, sb_=sb_, n=n: e.activation(out=pT[pt][:, 4, 64:128],
                                                                           in_=sps[sb_][:, 4, 64:128], func=AF.Exp),
                          reads=[sps_t[sb_]], writes=[pT_t[pt]], part=True)
                for jj, j in enumerate(js):
                    sc.op("pe", lambda e, pb_=pb_, hh=hh, pt=pt, jj=jj, j=j, h=h, n=n: e.matmul(
                        po[pb_][:, hh, 0:65], lhsT=pT[pt][:, jj, :], rhs=vx[:, j, h, :],
                        start=(jj == 0), stop=(jj == n - 1)),
                        reads=[pT_t[pt], vx_t[j]], writes=[po_t[pb_]], part=(hh > 0 or jj > 0))
                if hh == 1:
                    sc.op("dve", lambda e, pb_=pb_: e.reciprocal(out=rc[pb_][:, 0:2], in_=po[pb_][:, :, 64]),
                          reads=[po_t[pb_]], writes=[rc_t[pb_]])
                    for h2 in range(2):
                        hx = 2 * hp + h2
                        sc.op("dve", lambda e, pb_=pb_, h2=h2, hx=hx, ob=ob: e.tensor_scalar(
                            out=ot[ob][:, hx * 64:(hx + 1) * 64], in0=po[pb_][:, h2, 0:64], scalar1=rc[pb_][:, h2:h2 + 1],
                            scalar2=None, op0=ALU.mult),
                            reads=[po_t[pb_], rc_t[pb_]], writes=[ot_t[ob]], part=(hx > 0))
                if hp == 7 and hh == 1:
                    for half in range(2):
                        tb_ = tpc[0] % 2
                        tpc[0] += 1
                        for jq in range(4):
                            c = half * 4 + jq
                            sc.op("pe", lambda e, tb_=tb_, jq=jq, c=c, ob=ob: e.transpose(
                                out=tp[tb_][:, jq, :], in_=ot[ob][:, c * 128:(c + 1) * 128], identity=ident[:]),
                                reads=[ot_t[ob], G["ident_t"]], writes=[tp_t[tb_]], part=(jq > 0))
                        sc.op("act", lambda e, tb_=tb_, half=half, i=i: e.copy(
                            out=yst[:, half * 4:half * 4 + 4, (i % 4) * 128:(i % 4 + 1) * 128], in_=tp[tb_][:]),
                            reads=[tp_t[tb_]], writes=[yst_t], part=not (i % 4 == 0 and half == 0))
                    if i % 4 == 3:
                        yv = YB["na"].rearrange("(c p) t -> p c t", p=128)
                        sc.dma("pool", yv[:, :, (i - 3) * 128:(i + 1) * 128], yst[:], owner=yst_t, reads=[yst_t],
                               writes=[G["dram_t"]["yb_na"]], part=True)

            emit_S(0)
            for u in range(len(units)):
                if u + 1 < len(units):
                    emit_S(u + 1)
                emit_rest(u)
            sc.barrier(release=t2)
        sc.barrier(release=tiles)


def phase_F(P, sc, G, prm, l):
    nc = P.nc
    x = G["x"]
    with contextlib.ExitStack() as ph:
        hT = P.sb(ph, "F_hT", [128, 8, S], BF16)
        hT_t = sc.tiles_n("F_hT", NT)
        tiles = list(hT_t)
        tiles += rms_transpose(P, sc, G, ph, prm["norm_mlp_w"][l], hT, hT_t, l, "F")
        wst = WStream(P, sc, ph, "F", 1, 4096, nf=2, nb=3)
        fT = [P.sb(ph, "F_fT%d" % i, [128, 4, S], BF16) for i in range(2)]
        fT_t = [sc.tiles_n("F_fT%d_" % i, 4) for i in range(2)]
        rl = [P.sb(ph, "F_rl%d" % i, [128, 512], F32) for i in range(2)]
        rl_t = sc.tiles_n("F_rl", 2)
        acc = [P.ps(ph, "F_acc%d" % i, [128, 512], F32) for i in range(4)]
        acc_t = sc.tiles_n("F_acc", 4)
        tiles += wst.tiles + fT_t[0] + fT_t[1] + rl_t + acc_t
        w1v = prm["w_ff1"][l].rearrange("(kc p) n -> p kc n", p=128)
        w2v = prm["w_ff2"][l].rearrange("(c p) n -> p c n", p=128)
        items = []
        for g in range(8):
            items.append((w1v[:, :, g * 512:(g + 1) * 512], 8, 512))
            items.append((w2v[:, g * 4:(g + 1) * 4, :], 4, 1024))
        wst.items = items
        wst_views = {}

        def view(slot, k, n):
            return slot[:, 0, :].rearrange("p (k n) -> p k n", k=k)
        def _load(g):
            if g >= len(items):
                return
            ap, k, n = items[g]
            fs = g % wst.nf
            sc.dma("sp", view(wst.f[fs], k, n), ap, owner=wst.f_t[fs], writes=[wst.f_t[fs]])

        def _cast(g):
            if g >= len(items):
                return
            fs, bs = g % wst.nf, g % wst.nb
            sc.op("pool", lambda e: e.tensor_copy(out=wst.b[bs][:, 0, :], in_=wst.f[fs][:, 0, :]),
                  reads=[wst.f_t[fs]], writes=[wst.b_t[bs]])
        wst._load = _load
        wst._cast = _cast
        _load(0)
        _load(1)
        _cast(0)
        ai = 0
        ri = 0
        for g in range(8):
            fb = g % 2
            w1s, w1_t = wst.get(2 * g)
            w1b = view(w1s, 8, 512)
            for c in range(4):
                for tb in range(4):
                    a = ai % 4
                    ai += 1
                    for kc in range(8):
                        sc.op("pe", lambda e, a=a, kc=kc, w1b=w1b, c=c, tb=tb: e.matmul(
                            acc[a][:], lhsT=w1b[:, kc, c * 128:(c + 1) * 128],
                            rhs=hT[:, kc, tb * 512:(tb + 1) * 512], start=(kc == 0), stop=(kc == 7)),
                            reads=[w1_t] + hT_t[tb * 4:tb * 4 + 4], writes=[acc_t[a]], part=(kc > 0))
                    r = ri % 2
                    ri += 1
                    sc.op("act", lambda e, r=r, a=a: e.activation(out=rl[r][:], in_=acc[a][:], func=AF.Relu),
                          reads=[acc_t[a]], writes=[rl_t[r]])
                    sc.op("pool", lambda e, r=r, fb=fb, c=c, tb=tb: e.tensor_tensor(
                        out=fT[fb][:, c, tb * 512:(tb + 1) * 512], in0=rl[r][:], in1=rl[r][:], op=ALU.mult),
                        reads=[rl_t[r]], writes=[fT_t[fb][c]], part=(tb > 0))
            w2s, w2_t = wst.get(2 * g + 1)
            w2b = view(w2s, 4, 1024)
            for i in range(NT):
                for hh in range(2):
                    a = ai % 4
                    ai += 1
                    for c in range(4):
                        sc.op("pe", lambda e, a=a, c=c, w2b=w2b, i=i, hh=hh, fb=fb: e.matmul(
                            acc[a][:], lhsT=fT[fb][:, c, i * 128:(i + 1) * 128],
                            rhs=w2b[:, c, hh * 512:(hh + 1) * 512], start=(c == 0), stop=(c == 3)),
                            reads=[w2_t, fT_t[fb][c]], writes=[acc_t[a]], part=(c > 0))
                    xs = x[:, i, hh * 512:(hh + 1) * 512]
                    sc.op("dve", lambda e, xs=xs, a=a: e.tensor_tensor(out=xs, in0=xs, in1=acc[a][:], op=ALU.add),
                          reads=[acc_t[a], G["xt"][i]], writes=[G["xt"][i]])
        sc.barrier(release=tiles)


_NC_CACHE = {}


def make_na_tt(rpb):
    rpb = np.asarray(rpb, dtype=np.float32)
    L = rpb.shape[0]
    out = np.full((L, 128, 8, 20, 64), NEG, dtype=np.float32)
    qc = np.arange(64)
    ws = np.clip(qc - 8, 0, 48)
    for q in range(64):
        kc = np.arange(ws[q], ws[q] + 16)
        idx = kc - q + 15
        for hh in range(2):
            out[:, hh * 64 + q, :, 0:15, ws[q]:ws[q] + 16] = rpb[:, hh::2][:, :, :, idx]
    out[:, :, :, 17, :] = out[:, :, :, 3, :]
    out[:, :, :, 18, :] = out[:, :, :, 10, :]
    return out


def kernel(**inputs):
    cfg = {}
    key = "full"
    if key not in _NC_CACHE:
        _NC_CACHE[key] = build(cfg)
    nc = _NC_CACHE[key]
    x = np.ascontiguousarray(inputs["x"], dtype=np.float32)
    base = {n: np.ascontiguousarray(inputs[n], dtype=np.float32) for n in PARAM_NAMES}
    base["na_tt"] = make_na_tt(inputs["na_rpb"])
    in_maps = []
    for c in range(8):
        m = dict(base)
        m["x"] = x[c]
        in_maps.append(m)
    res = run_bass_kernel_spmd(nc, in_maps, core_ids=list(range(8)))
    return np.stack([r["y"] for r in res.results], axis=0).astype(np.float32)
```
